# Optimizing a Trainium2 kernel written in Bass

```python
import jax, jax.numpy as jnp
from jax import lax
import numpy as np

D_MODEL = 1024
BATCH = 16
SEQ = 256
DEPTH = 1
DEC_BATCH = 8
DEC_SEQ = 2048
PAST_LEN = 256

GRID_W = 64
D_MIX = D_MODEL
D_A = D_MIX // 2
HEAD_A = 64
H_A = D_A // HEAD_A
D_B = D_MIX - D_A
BLK_B = 64
NB_B = D_B // BLK_B
R_W = 64
R_A = 64
R_G = 128
CONV_W = 4
CONV_PAD_L = 2
LRU_C = 8.0
D_FF = 4 * D_MODEL
N_DIR = 2
EPS = 1e-6
LNX_EPS = 64e-5
SPLIT_SIZES = (D_A, D_A, D_A, N_DIR * R_W, N_DIR * R_A, R_G, D_B, D_B)
D_IN = sum(SPLIT_SIZES)

kernel_name = "hymba_rwkv7_rglru_diffusion_step"


def rms_norm(x, g):
    xf = x.astype(jnp.float32)
    y = xf * lax.rsqrt(jnp.mean(xf * xf, axis=-1, keepdims=True) + EPS)
    return (y * g.astype(jnp.float32)).astype(x.dtype)


def sincos_1d(pos, dim):
    omega = 1.0 / (10000.0 ** (jnp.arange(dim // 2, dtype=jnp.float32) / (dim // 2)))
    ang = pos.astype(jnp.float32)[:, None] * omega[None, :]
    return jnp.concatenate([jnp.sin(ang), jnp.cos(ang)], axis=-1)


def grid_pos_embed(n_tokens):
    rows = n_tokens // GRID_W
    half = D_MODEL // 2
    e_row = sincos_1d(jnp.arange(rows), half)
    e_col = sincos_1d(jnp.arange(GRID_W), half)
    emb = jnp.concatenate([jnp.broadcast_to(e_row[:, None, :], (rows, GRID_W, half)),
                           jnp.broadcast_to(e_col[None, :, :], (rows, GRID_W, half))], axis=-1)
    return emb.reshape(rows * GRID_W, D_MODEL)


def split_proj(z):
    idx = np.cumsum(np.array(SPLIT_SIZES))[:-1].tolist()
    return jnp.split(z, idx, axis=-1)


def rwkv7_scan(r, w, k, v, kk, ka, s0, reverse):
    def step(S, inp):
        r_t, w_t, k_t, v_t, kk_t, ka_t = inp
        sa = jnp.einsum('bhvk,bhk->bhv', S, -kk_t)
        S = (S * w_t[:, :, None, :] + sa[..., :, None] * ka_t[..., None, :]
             + v_t[..., :, None] * k_t[..., None, :])
        y = jnp.einsum('bhvk,bhk->bhv', S, r_t)
        return S, y
    xs = tuple(jnp.moveaxis(a, 1, 0) for a in (r, w, k, v, kk, ka))
    s_fin, ys = lax.scan(step, s0, xs, reverse=reverse)
    return jnp.moveaxis(ys, 0, 1), s_fin


def rwkv7_group(r, k, v, xw, xa, xg, s0, lp):
    B, T, _ = r.shape
    f32 = jnp.float32
    heads = lambda t: t.reshape(B, T, H_A, HEAD_A)
    r, k, v = r.astype(f32), k.astype(f32), v.astype(f32)
    xw = xw.astype(f32).reshape(B, T, N_DIR, R_W)
    xa = xa.astype(f32).reshape(B, T, N_DIR, R_A)
    g = jnp.einsum('btr,rd->btd', jax.nn.sigmoid(xg.astype(f32)), lp['g_up'].astype(f32))
    w_log = -jax.nn.softplus(-(lp['w0'].astype(f32)
                               + jnp.einsum('btnr,nrd->btnd', jnp.tanh(xw), lp['w_up'].astype(f32)))) - 0.5
    decay = jnp.exp(-jnp.exp(w_log))
    a = jax.nn.sigmoid(lp['a0'].astype(f32) + jnp.einsum('btnr,nrd->btnd', xa, lp['a_up'].astype(f32)))
    kk = heads(k * lp['k_k'].astype(f32))
    kk = kk / jnp.maximum(jnp.sqrt(jnp.sum(kk * kk, axis=-1, keepdims=True)), 1e-12)
    r_h, v_h = heads(r), heads(v)
    y_sum = 0.0
    bonus = 0.0
    finals = []
    for d, rev in ((0, False), (1, True)):
        a_d = a[:, :, d]
        k_d = heads(k * (1.0 + (a_d - 1.0) * lp['k_a'].astype(f32)))
        y_d, s_d = rwkv7_scan(r_h, heads(decay[:, :, d]), k_d, v_h, kk, kk * heads(a_d), s0[:, d].astype(f32), rev)
        y_sum = y_sum + y_d
        bonus = bonus + jnp.sum(r_h * k_d * lp['r_k'].astype(f32), axis=-1, keepdims=True) * v_h
        finals.append(s_d)
    mu = jnp.mean(y_sum, axis=-1, keepdims=True)
    var = jnp.mean(jnp.square(y_sum - mu), axis=-1, keepdims=True)
    yn = ((y_sum - mu) * lax.rsqrt(var + LNX_EPS)).reshape(B, T, D_A)
    yn = yn * lp['lnx_g'].astype(f32) + lp['lnx_b'].astype(f32)
    out = (yn + bonus.reshape(B, T, D_A)) * g
    return out, jnp.stack(finals, axis=1)


def conv_centred(x, w, b):
    T = x.shape[1]
    xp = jnp.pad(x, ((0, 0), (CONV_PAD_L, CONV_W - 1 - CONV_PAD_L), (0, 0)))
    out = b
    for j in range(CONV_W):
        out = out + xp[:, j:j + T] * w[j]
    return out


def block_diag(x, w, b):
    B, T, _ = x.shape
    return jnp.einsum('btnc,ncd->btnd', x.reshape(B, T, NB_B, BLK_B), w).reshape(B, T, D_B) + b


def lin_scan(a, bx, h0, reverse):
    def comb(e1, e2):
        a1, b1 = e1
        a2, b2 = e2
        return a1 * a2, a2 * b1 + b2
    a_cum, b_cum = lax.associative_scan(comb, (a, bx), reverse=reverse, axis=1)
    return b_cum + a_cum * h0[:, None, :]


def rglru_group(xb, gb, h0, lp):
    f32 = jnp.float32
    xc = conv_centred(xb.astype(f32), lp['conv_w'].astype(f32), lp['conv_b'].astype(f32))
    gate = jax.nn.gelu(gb.astype(f32), approximate=True)
    y_sum = 0.0
    finals = []
    for d, rev in ((0, False), (1, True)):
        rg = jax.nn.sigmoid(block_diag(xc, lp['wa'][d].astype(f32), lp['ba'][d].astype(f32)))
        ig = jax.nn.sigmoid(block_diag(xc, lp['wx'][d].astype(f32), lp['bx'][d].astype(f32)))
        log_a = -LRU_C * rg * jax.nn.softplus(-lp['lam'][d].astype(f32))
        a_t = jnp.exp(log_a)
        bx = jnp.sqrt(-jnp.expm1(2.0 * log_a)) * (ig * xc)
        h = lin_scan(a_t, bx, h0[:, d].astype(f32), rev)
        y_sum = y_sum + h
        finals.append(h[:, 0] if rev else h[:, -1])
    return y_sum * gate, jnp.stack(finals, axis=1)


def trunk_layer(x, mod, s_rwkv0, s_lru0, lp):
    shift1, scale1, gate1, shift2, scale2, gate2 = jnp.split(mod, 6, axis=-1)
    h = rms_norm(x, lp['g_pre_mix']) * (1.0 + scale1) + shift1
    z = jnp.einsum('btd,de->bte', h, lp['w_in'])
    r, k, v, xw, xa, xg, xb, gb = split_proj(z)
    y_a, s_a = rwkv7_group(r, k, v, xw, xa, xg, s_rwkv0, lp)
    y_b, s_b = rglru_group(xb, gb, s_lru0, lp)
    y = jnp.einsum('bte,ed->btd', jnp.concatenate([y_a, y_b], axis=-1).astype(x.dtype), lp['w_out'])
    x = x + gate1 * rms_norm(y, lp['g_post_mix'])
    h = rms_norm(x, lp['g_pre_mlp']) * (1.0 + scale2) + shift2
    f = jnp.square(jax.nn.relu(jnp.einsum('btd,df->btf', h, lp['w_mlp1'])))
    f = jnp.einsum('btf,fd->btd', f, lp['w_mlp2'])
    x = x + gate2 * rms_norm(f, lp['g_post_mlp'])
    return x, s_a, s_b


def setup_inputs(seed: int = 0) -> dict:
    key = jax.random.key(seed)
    ks = jax.random.split(key, 40)
    nrm = lambda i, shape, s: jax.random.normal(ks[i], shape, jnp.float32) * s
    u = jax.random.uniform(ks[30], (DEPTH, N_DIR, D_B), jnp.float32, 0.9, 0.999)
    p = u ** (1.0 / LRU_C)
    lam = jnp.log(p) - jnp.log1p(-p)
    return {
        "x_prompt": nrm(0, (BATCH, SEQ, D_MODEL), 1.0),
        "x_sample": nrm(1, (DEC_BATCH, DEC_SEQ, D_MODEL), 1.0),
        "c": nrm(2, (DEC_BATCH, D_MODEL), 1.0),
        "state_rwkv": nrm(3, (DEC_BATCH, DEPTH, N_DIR, H_A, HEAD_A, HEAD_A), 0.3),
        "state_lru": nrm(4, (DEC_BATCH, DEPTH, N_DIR, D_B), 0.5),
        "c_ctx": nrm(5, (D_MODEL,), 1.0),
        "w_mod": nrm(6, (DEPTH, D_MODEL, 6 * D_MODEL), 0.5 * D_MODEL ** -0.5),
        "b_mod": nrm(7, (DEPTH, 6 * D_MODEL), 0.01),
        "g_pre_mix": 1.0 + nrm(8, (DEPTH, D_MODEL), 0.02),
        "g_post_mix": 1.0 + nrm(9, (DEPTH, D_MODEL), 0.02),
        "g_pre_mlp": 1.0 + nrm(10, (DEPTH, D_MODEL), 0.02),
        "g_post_mlp": 1.0 + nrm(11, (DEPTH, D_MODEL), 0.02),
        "w_in": nrm(12, (DEPTH, D_MODEL, D_IN), D_MODEL ** -0.5),
        "rwkv_w0": jax.random.uniform(ks[13], (DEPTH, N_DIR, D_A), jnp.float32, -6.0, 1.0),
        "rwkv_w_up": nrm(14, (DEPTH, N_DIR, R_W, D_A), 0.1),
        "rwkv_a0": nrm(15, (DEPTH, N_DIR, D_A), 0.1),
        "rwkv_a_up": nrm(16, (DEPTH, N_DIR, R_A, D_A), 0.1),
        "rwkv_g_up": nrm(17, (DEPTH, R_G, D_A), R_G ** -0.5),
        "rwkv_k_k": 0.85 + nrm(18, (DEPTH, D_A), 0.02),
        "rwkv_k_a": 1.0 + nrm(19, (DEPTH, D_A), 0.02),
        "rwkv_r_k": nrm(20, (DEPTH, H_A, HEAD_A), 0.1),
        "rwkv_lnx_g": 1.0 + nrm(21, (DEPTH, D_A), 0.02),
        "rwkv_lnx_b": nrm(22, (DEPTH, D_A), 0.01),
        "lru_conv_w": nrm(23, (DEPTH, CONV_W, D_B), CONV_W ** -0.5),
        "lru_conv_b": nrm(24, (DEPTH, D_B), 0.01),
        "lru_wa": nrm(25, (DEPTH, N_DIR, NB_B, BLK_B, BLK_B), BLK_B ** -0.5),
        "lru_ba": nrm(26, (DEPTH, N_DIR, D_B), 0.01),
        "lru_wx": nrm(27, (DEPTH, N_DIR, NB_B, BLK_B, BLK_B), BLK_B ** -0.5),
        "lru_bx": nrm(28, (DEPTH, N_DIR, D_B), 0.01),
        "lru_lambda": lam,
        "w_out": nrm(31, (DEPTH, D_MIX, D_MODEL), D_MIX ** -0.5),
        "w_mlp1": nrm(32, (DEPTH, D_MODEL, D_FF), D_MODEL ** -0.5),
        "w_mlp2": nrm(33, (DEPTH, D_FF, D_MODEL), D_FF ** -0.5),
    }


def reference(x_prompt, x_sample, c, state_rwkv, state_lru, c_ctx, w_mod, b_mod,
              g_pre_mix, g_post_mix, g_pre_mlp, g_post_mlp, w_in,
              rwkv_w0, rwkv_w_up, rwkv_a0, rwkv_a_up, rwkv_g_up, rwkv_k_k, rwkv_k_a, rwkv_r_k,
              rwkv_lnx_g, rwkv_lnx_b, lru_conv_w, lru_conv_b, lru_wa, lru_ba, lru_wx, lru_bx,
              lru_lambda, w_out, w_mlp1, w_mlp2):
    n_ctx = x_prompt.shape[0]
    xp = x_prompt
    xs = x_sample + grid_pos_embed(x_sample.shape[1]).astype(x_sample.dtype)[None]
    zero_rwkv = jnp.zeros((n_ctx, N_DIR, H_A, HEAD_A, HEAD_A), jnp.float32)
    zero_lru = jnp.zeros((n_ctx, N_DIR, D_B), jnp.float32)
    new_rwkv, new_lru = [], []
    for l in range(DEPTH):
        lp = {
            'g_pre_mix': g_pre_mix[l], 'g_post_mix': g_post_mix[l],
            'g_pre_mlp': g_pre_mlp[l], 'g_post_mlp': g_post_mlp[l],
            'w_in': w_in[l], 'w_out': w_out[l], 'w_mlp1': w_mlp1[l], 'w_mlp2': w_mlp2[l],
            'w0': rwkv_w0[l], 'w_up': rwkv_w_up[l], 'a0': rwkv_a0[l], 'a_up': rwkv_a_up[l],
            'g_up': rwkv_g_up[l], 'k_k': rwkv_k_k[l], 'k_a': rwkv_k_a[l], 'r_k': rwkv_r_k[l],
            'lnx_g': rwkv_lnx_g[l], 'lnx_b': rwkv_lnx_b[l],
            'conv_w': lru_conv_w[l], 'conv_b': lru_conv_b[l],
            'wa': lru_wa[l], 'ba': lru_ba[l], 'wx': lru_wx[l], 'bx': lru_bx[l], 'lam': lru_lambda[l],
        }
        mod_ctx = (jax.nn.silu(c_ctx) @ w_mod[l] + b_mod[l])[None, None, :]
        mod_lat = (jax.nn.silu(c) @ w_mod[l] + b_mod[l])[:, None, :]
        xp, s_r, s_l = trunk_layer(xp, mod_ctx, zero_rwkv, zero_lru, lp)
        new_rwkv.append(s_r.astype(x_prompt.dtype))
        new_lru.append(s_l.astype(x_prompt.dtype))
        xs, _, _ = trunk_layer(xs, mod_lat, state_rwkv[:, l], state_lru[:, l], lp)
    new_state_rwkv = jnp.stack(new_rwkv, axis=1)
    new_state_lru = jnp.stack(new_lru, axis=1)
    return (xp, xs, new_state_rwkv, new_state_lru)
```

```python
import contextlib
import numpy as np
import concourse.bass as bass
import concourse.mybir as mybir
from concourse.bass_utils import run_bass_kernel_spmd

F32 = mybir.dt.float32
BF16 = mybir.dt.bfloat16
F32R = mybir.dt.float32r
AF = mybir.ActivationFunctionType
ALU = mybir.AluOpType

D = 1024
TS = 2048
TP = 256
NTOK = TS + 2 * TP
NCH = NTOK // 64
DIN = 2944
DFF = 4096
LAM = float(np.exp(-0.5))
EPS = 1e-6
LNX_EPS = 64e-5
TT = 256
TC = 512
GELU_C = 1.5957691216057308

P_GPRE, P_GPOST, P_GPRE2, P_GPOST2 = 0, 8, 16, 24
P_BMOD = 32
P_W0, P_A0 = 80, 88
P_KK, P_KA, P_RK, P_LNG, P_LNB = 96, 100, 104, 108, 112
P_CW, P_CB = 116, 132
P_BA, P_BX, P_LAM, P_H0 = 136, 144, 152, 160
NPRM = 168
C_ID, C_OB, C_MSI, C_ML, C_IDS, C_RMF, C_RMB = 0, 128, 256, 384, 448, 512, 768
C_MSI1, C_ML1 = 1024, 1152
NCST = 1216


class Buf:
    __slots__ = ("name", "lw", "rd", "excl")

    def __init__(self, name=""):
        self.name = name
        self.lw = None
        self.rd = {}
        self.excl = False


class TL:
    def __init__(self, t, name=""):
        self.t = t
        self.b = Buf(name)

    def __getitem__(self, k):
        return self.t[k]


class Sched:
    ENGS = ("pe", "act", "dve", "pool", "sp")

    def __init__(self, nc):
        self.nc = nc
        self.streams = {e: [] for e in self.ENGS}
        self.cnt = {}
        self.waited = {e: {} for e in self.ENGS}
        self.n_ops = 0
        self.dma_n = {e: 0 for e in self.ENGS}
        self.NSLOT = {"sp": 44, "act": 44, "pool": 4, "dve": 2, "pe": 2}

    def _deps(self, eng, reads, writes):
        need = {}
        for b in reads:
            if b.lw is not None:
                s, v = b.lw
                if need.get(s, 0) < v:
                    need[s] = v
            if b.excl:
                for s, v in b.rd.items():
                    if s != eng and need.get(s, 0) < v:
                        need[s] = v
        for b in writes:
            if b.lw is not None:
                s, v = b.lw
                if need.get(s, 0) < v:
                    need[s] = v
            for s, v in b.rd.items():
                if need.get(s, 0) < v:
                    need[s] = v
        out = []
        w = self.waited[eng]
        for s, v in need.items():
            if s == "pe" and eng == "pe":
                continue
            if w.get(s, 0) >= v:
                continue
            w[s] = v
            out.append((s, v))
        return out

    def op(self, eng, fn, reads=(), writes=(), dma=False):
        reads = [r.b if isinstance(r, TL) else r for r in reads]
        writes = [r.b if isinstance(r, TL) else r for r in writes]
        waits = self._deps(eng, reads, writes)
        if dma:
            slot = self.dma_n[eng] % self.NSLOT[eng]
            self.dma_n[eng] += 1
            sem = "%s_d%d" % (eng, slot)
            prev = self.cnt.get(sem, 0)
            if prev > 0 and self.waited[eng].get(sem, 0) < prev:
                self.waited[eng][sem] = prev
                waits.append((sem, prev))
        else:
            sem = eng
        inc = 16 if dma else 1
        self.cnt[sem] = self.cnt.get(sem, 0) + inc
        val = self.cnt[sem]
        self.streams[eng].append((waits, fn, sem, inc))
        self.n_ops += 1
        for b in reads:
            if b.rd.get(sem, 0) < val:
                b.rd[sem] = val
        for b in writes:
            b.lw = (sem, val)
            b.rd = {}
        return val

    def barrier(self):
        snap = dict(self.cnt)
        for e in self.ENGS:
            waits = []
            for s, v in snap.items():
                if s == "pe" and e == "pe":
                    continue
                if self.waited[e].get(s, 0) < v:
                    self.waited[e][s] = v
                    waits.append((s, v))
            if waits:
                self.streams[e].append((waits, None, None, 0))

    def emit(self):
        nc = self.nc
        sems = {}
        with contextlib.ExitStack() as st:
            for s in self.cnt:
                sems[s] = st.enter_context(nc.semaphore(s))
            block = st.enter_context(nc.Block())
            engmap = {"pe": block.tensor, "act": block.scalar, "dve": block.vector,
                      "pool": block.gpsimd, "sp": block.sync}
            for e in self.ENGS:
                stream = self.streams[e]
                if not stream:
                    continue

                def body(eng, stream=stream):
                    for waits, fn, sem, inc in stream:
                        for s, v in waits:
                            eng.wait_ge(sems[s], v)
                        if fn is not None:
                            fn(eng).then_inc(sems[sem], inc)
                engmap[e](body)


class K:
    def __init__(self, debug=False, stop_after=None):
        self.debug = debug
        self.stop_after = stop_after
        import os
        self.cutk = int(os.environ.get("KCUT", "0"))
        self.cutm = int(os.environ.get("KCUTM", "99"))
        self.ktiles = int(os.environ.get("KTILES", "99"))
        self.kskip = os.environ.get("KSKIP", "").split(",")
        self.nc = bass.Bass("TRN2", target_bir_lowering=False)
        self.S = Sched(self.nc)
        self.es = contextlib.ExitStack()
        self.psr = 0
        self.rr = {}

    def dram(self, name, shape, dt, kind="Internal"):
        return TL(self.nc.dram_tensor(name, list(shape), dt, kind=kind).ap(), name)

    def sb(self, st, name, shape, dt):
        return TL(st.enter_context(self.nc.sbuf_tensor(name, list(shape), dt)), name)

    def sb2(self, st, name, shape, dt):
        t = st.enter_context(self.nc.sbuf_tensor(name, list(shape), dt))
        return [TL(t, name + "_lo"), TL(t, name + "_hi")]

    def ps(self):
        p = self.psum[self.psr % 8]
        self.psr += 1
        return p

    def mm(self, out, lhsT, rhs, start, stop, R, W):
        self.S.op("pe", lambda e: e.matmul(out, lhsT=lhsT, rhs=rhs, start=start, stop=stop), R, W)

    def tr(self, out, in_, ident, R, W):
        self.S.op("pe", lambda e: e.transpose(out, in_, ident), R, W)

    def act(self, out, in_, func, R, W, scale=1.0, bias=None, eng="act"):
        if bias is None:
            self.S.op("act", lambda e: e.activation(out=out, in_=in_, func=func, scale=scale), R, W)
        else:
            self.S.op("act", lambda e: e.activation(out=out, in_=in_, func=func, scale=scale, bias=bias), R, W)

    def tt(self, eng, out, in0, in1, op, R, W):
        self.S.op(eng, lambda e: e.tensor_tensor(out=out, in0=in0, in1=in1, op=op), R, W)

    def tsc(self, eng, out, in0, s1, op0, R, W, s2=None, op1=None):
        if op1 is None:
            self.S.op(eng, lambda e: e.tensor_scalar(out=out, in0=in0, scalar1=s1, scalar2=None, op0=op0), R, W)
        else:
            self.S.op(eng, lambda e: e.tensor_scalar(out=out, in0=in0, scalar1=s1, scalar2=s2, op0=op0, op1=op1), R, W)

    def stt(self, out, in0, scalar, in1, op0, op1, R, W):
        self.S.op("dve", lambda e: e.scalar_tensor_tensor(out=out, in0=in0, scalar=scalar, in1=in1, op0=op0, op1=op1), R, W)

    def cp(self, eng, out, in_, R, W):
        if eng == "act":
            self.S.op("act", lambda e: e.activation(out=out, in_=in_, func=AF.Copy), R, W)
        else:
            self.S.op(eng, lambda e: e.tensor_copy(out=out, in_=in_), R, W)

    def scan(self, out, d0, d1, init, R, W):
        self.S.op("dve", lambda e: e.tensor_tensor_scan(out=out, data0=d0, data1=d1, initial=init,
                                                        op0=ALU.mult, op1=ALU.add), R, W)

    def dma(self, out, in_, R, W, eng="sp"):
        self.S.op(eng, lambda e: e.dma_start(out=out, in_=in_), R, W, dma=True)

    def memset(self, eng, ap, val, W):
        self.S.op(eng, lambda e: e.memset(ap, val), (), W)

    def pick(self, key, engs):
        i = self.rr.get(key, 0)
        self.rr[key] = i + 1
        return engs[i % len(engs)]

    def build(self):
        nc = self.nc
        I = lambda n, s, dt=F32: self.dram(n, s, dt, "ExternalInput")
        O = lambda n, s, dt=F32: self.dram(n, s, dt, "ExternalOutput")
        self.xs = I("xs", [TS, D])
        self.xp = I("xp", [2 * TP, D])
        self.pe = I("pe", [TS, D])
        self.cT = I("cT", [128, 16])
        self.h0r = I("h0r", [128, 512])
        self.prm = I("prm", [128, NPRM])
        self.cst = I("cst", [128, NCST])
        self.w_mod = I("w_mod", [D, 6 * D])
        self.w_in = I("w_in", [D, DIN])
        self.w_out = I("w_out", [D, D])
        self.w1 = I("w1", [D, DFF])
        self.w2 = I("w2", [DFF, D])
        self.wup = I("wup", [128, 512])
        self.aup = I("aup", [128, 512])
        self.gup = I("gup", [128, 512])
        self.lwa = I("lwa", [2, 8, 64, 64])
        self.lwx = I("lwx", [2, 8, 64, 64])
        self.ys = O("ys", [TS, D])
        self.yp = O("yp", [2 * TP, D])
        self.str_o = O("str_o", [2, 128, 512])
        self.stl_o = O("stl_o", [128, 16])
        self.xT_scr = self.dram("xT_scr", [128, 8, NTOK], F32)
        self.xb_scr = self.dram("xb_scr", [128, 4, NTOK], F32)
        self.gate_scr = self.dram("gate_scr", [128, 4, NTOK], BF16)
        self.g_scr = self.dram("g_scr", [128, 4, NTOK], BF16)
        self.bon_scr = self.dram("bon_scr", [128, 4, NTOK], BF16)
        self.y_scr = self.dram("y_scr", [128, 8, NTOK], BF16)
        self.ytok_scr = self.dram("ytok_scr", [2, NTOK, 512], F32)
        for n in ("art", "rrt", "ttt", "akt", "mrbt", "mrkt", "bh", "kh"):
            setattr(self, n + "_scr", self.dram(n + "_scr", [NCH, 128, 512], BF16))
        self.vt_scr = self.dram("vt_scr", [NCH, 64, 512], BF16)
        self.pend_scr = self.dram("pend_scr", [NCH, 128, 512], F32)
        self.w1_scr = self.dram("w1_scr", [8, 128, 8, 512], BF16)
        self.w2_scr = self.dram("w2_scr", [8, 128, 32, 128], BF16)
        if self.debug:
            self.dbg = {}

        with self.es as st0:
            self.psum = [TL(st0.enter_context(nc.psum_tensor("ps%d" % i, [128, 512], F32)), "ps%d" % i)
                         for i in range(8)]
            for p_ in self.psum:
                p_.b.excl = True
            self.prm_t = self.sb(st0, "prm_t", [128, NPRM], F32)
            self.cst_t = self.sb(st0, "cst_t", [128, NCST], F32)
            self.modT = self.sb(st0, "modT", [128, 48, 2], F32)
            self.gs1 = self.sb(st0, "gs1", [128, 8, 2], F32)
            self.gs2 = self.sb(st0, "gs2", [128, 8, 2], F32)
            self.gg1 = self.sb(st0, "gg1", [128, 8, 2], F32)
            self.gg2 = self.sb(st0, "gg2", [128, 8, 2], F32)
            self.ident = self.sb(st0, "ident", [128, 128], F32)
            self.ones_bf = self.sb(st0, "ones_bf", [128, 128], BF16)
            self.oblk_bf = self.sb(st0, "oblk_bf", [128, 128], BF16)
            self.epsT = self.sb(st0, "epsT", [128, 2], F32)
            self.misc = self.sb(st0, "misc", [128, 32], F32)
            self.stl_t = self.sb(st0, "stl_t", [128, 16], F32)
            for nm, fn in (("p0", self.phase0), ("pA", self.phaseA), ("pB", self.phaseB), ("pC", self.phaseC)):
                fn()
                self.S.barrier()
                if self.stop_after == nm:
                    break
            self.S.emit()
        return nc

    def dump(self, name, src_ap, shape, dt, R):
        o = self.dram("dbg_" + name, shape, dt, "ExternalOutput")
        self.dma(o[:], src_ap, R, [o])

    def phase0(self):
        nc = self.nc
        prm, cst = self.prm_t, self.cst_t
        self.dma(prm[:], self.prm[:], [], [prm])
        self.dma(cst[:], self.cst[:], [], [cst])
        self.cp("dve", self.ident[:], cst[:, C_ID:C_ID + 128], [cst], [self.ident])
        self.cp("dve", self.oblk_bf[:], cst[:, C_OB:C_OB + 128], [cst], [self.oblk_bf])
        self.memset("dve", self.ones_bf[:], 1.0, [self.ones_bf])
        self.memset("dve", self.epsT[:, 0:1], EPS, [self.epsT])
        self.memset("dve", self.epsT[:, 1:2], LNX_EPS, [self.epsT])
        self.tsc("dve", self.misc[:, 0:4], prm[:, P_KA:P_KA + 4], -1.0, ALU.mult, [prm], [self.misc], 1.0, ALU.add)
        with contextlib.ExitStack() as st:
            scT = self.sb(st, "scT", [128, 16], F32)
            cT = self.sb(st, "cT_t", [128, 16], F32)
            wm = [self.sb(st, "wm%d" % i, [128, 8, 512], F32) for i in range(2)]
            tmp = self.sb(st, "lam_tmp", [128, 8], F32)
            self.dma(cT[:], self.cT[:], [], [cT])
            self.act(scT[:], cT[:], AF.Silu, [cT], [scT])
            self.act(tmp[:], prm[:, P_LAM:P_LAM + 8], AF.Exp, [prm], [tmp], scale=-1.0)
            self.act(tmp[:], tmp[:], AF.Ln, [tmp], [tmp], bias=1.0)
            self.tsc("dve", self.misc[:, 4:12], tmp[:], -8.0, ALU.mult, [tmp], [self.misc])
            self.tsc("dve", self.misc[:, 12:20], tmp[:], -16.0, ALU.mult, [tmp], [self.misc])
            wsrc = self.w_mod[:].rearrange("(kc p) n -> p kc n", p=128)
            sc3 = scT[:].rearrange("p (k c) -> p k c", c=2)
            for blk in range(12):
                w = wm[blk % 2]
                self.dma(w[:], wsrc[:, :, blk * 512:(blk + 1) * 512], [], [w], eng="sp" if blk % 2 == 0 else "act")
                p = self.ps()
                for m in range(4):
                    for kc in range(8):
                        self.mm(p[:, 2 * m:2 * m + 2], w[:, kc, m * 128:(m + 1) * 128], sc3[:, kc, :],
                                kc == 0, kc == 7, [w, scT], [p])
                for m in range(4):
                    mi = blk * 4 + m
                    self.tsc("dve", self.modT[:, mi, :], p[:, 2 * m:2 * m + 2], prm[:, P_BMOD + mi:P_BMOD + mi + 1],
                             ALU.add, [p, prm], [self.modT])
            m3 = self.modT
            for (dst, sc_off, g_off, one) in ((self.gs1, 8, P_GPRE, 1.0), (self.gs2, 32, P_GPRE2, 1.0),
                                              (self.gg1, 16, P_GPOST, 0.0), (self.gg2, 40, P_GPOST2, 0.0)):
                for c in range(2):
                    self.tsc("dve", dst[:, :, c], m3[:, sc_off:sc_off + 8, c], one, ALU.add, [m3], [dst])
                    self.tt("dve", dst[:, :, c], dst[:, :, c], prm[:, g_off:g_off + 8], ALU.mult, [dst, prm], [dst])
            if self.debug:
                self.dump("modT", self.modT[:], [128, 48, 2], F32, [self.modT])
                self.dump("gs1", self.gs1[:], [128, 8, 2], F32, [self.gs1])

    def load_cast(self, st, dst_ap, dst_tl, src_ap, shape, tag):
        key = "stg_" + tag
        if not hasattr(self, key):
            setattr(self, key, [self.sb(st, "%s%d" % (key, i), shape, F32) for i in range(2)])
        ring = getattr(self, key)
        s = ring[self.rr.get(key, 0) % 2]
        self.rr[key] = self.rr.get(key, 0) + 1
        self.dma(s[:], src_ap, [], [s], eng="sp")
        eng = self.pick("castE", ["act", "pool"])
        self.cp(eng, dst_ap, s[:], [s], [dst_tl])

    def phaseA(self):
        nc = self.nc
        prm, cst = self.prm_t, self.cst_t
        with contextlib.ExitStack() as st:
            win = self.sb(st, "win", [128, 8, DIN], BF16)
            wsrc = self.w_in[:].rearrange("(kc p) n -> p kc n", p=128)
            wup = self.sb(st, "wup_t", [128, 512], BF16)
            aup = self.sb(st, "aup_t", [128, 512], BF16)
            gup = self.sb(st, "gup_t", [128, 512], BF16)
            with contextlib.ExitStack() as st2:
                stgA = [self.sb(st2, "stgA%d" % i, [128, 8, 256], F32) for i in range(2)]
                nb = 0
                for c0 in range(0, DIN, 256):
                    cw = min(256, DIN - c0)
                    s_ = stgA[nb % 2]
                    self.dma(s_[:, :, 0:cw], wsrc[:, :, c0:c0 + cw], [], [s_], eng="sp" if nb % 2 == 0 else "act")
                    self.cp("act" if nb % 2 == 0 else "pool", win[:, :, c0:c0 + cw], s_[:, :, 0:cw], [s_], [win])
                    nb += 1
                for i, (src, dstt) in enumerate(((self.wup, wup), (self.aup, aup), (self.gup, gup))):
                    s_ = stgA[nb % 2]
                    nb += 1
                    s2 = s_[:].rearrange("p a b -> p (a b)")[:, 0:512]
                    self.dma(s2, src[:], [], [s_])
                    self.cp("dve", dstt[:], s2, [s_], [dstt])
            self.S.barrier()
            mSI = cst[:, C_MSI:C_MSI + 128].rearrange("p (q t) -> p q t", q=2)
            mL = cst[:, C_ML:C_ML + 64]
            idS = cst[:, C_IDS:C_IDS + 64]

            xin = self.sb(st, "xin", [128, 2, D], F32)
            xT = self.sb(st, "xT", [128, 8, TT], F32)
            sq = self.sb(st, "sq", [128, 8, TT], BF16)
            hT = self.sb(st, "hT", [128, 8, TT], BF16)
            rstd = self.sb(st, "rstd", [128, TT], F32)
            tmpA = [self.sb(st, "tmpA%d" % i, [128, TT], F32) for i in range(2)]
            rT = self.sb(st, "rT", [128, 4, TT], F32)
            kT = self.sb(st, "kT", [128, 4, TT], F32)
            vT = self.sb(st, "vT", [128, 4, TT], F32)
            xw = self.sb(st, "xw", [128, TT], BF16)
            xa = self.sb(st, "xa", [128, TT], BF16)
            xg = self.sb(st, "xg", [128, TT], BF16)
            xbT = self.sb(st, "xbT", [128, 4, TT], F32)
            gtmp = [self.sb(st, "gtmp%d" % i, [128, TT], F32) for i in range(3)]
            gate = self.sb(st, "gate", [128, 4, TT], BF16)
            gT = self.sb(st, "gT", [128, 4, TT], BF16)
            kkn = self.sb(st, "kkn", [128, 4, TT], F32)
            ksum = self.sb(st, "ksum", [128, 4, TT], F32)
            bon = self.sb(st, "bon", [128, 4, TT], BF16)
            sg = self.sb(st, "sg", [128, 4, TT], F32)
            cs = self.sb(st, "cs", [128, 4, TT], F32)
            E1 = self.sb(st, "E1", [128, 4, TT], F32)
            ad = self.sb(st, "ad", [128, 4, TT], F32)
            wk1 = self.sb(st, "wk1", [128, 4, TT], F32)
            wk2 = self.sb(st, "wk2", [128, 4, TT], F32)
            NC4 = TT // 64
            AR = [self.sb(st, "AR%d" % d, [128, 4, NC4, 2, 64], BF16) for d in range(2)]
            BK = [self.sb(st, "BK%d" % d, [128, 4, NC4, 2, 64], BF16) for d in range(2)]
            PEb1 = self.sb(st, "PEb", [128, 4, NC4, 64], F32)
            PEb = [PEb1, PEb1]
            tokB = [self.sb(st, "tokB%d" % d, [128, 2, 8, 64], BF16) for d in range(2)]
            tokK = [self.sb(st, "tokK%d" % d, [128, 2, 8, 64], BF16) for d in range(2)]
            tokV = self.sb(st, "tokV", [128, 2, 8, 64], BF16)
            MRBd = [self.sb(st, "MRBd%d" % d, [64, 8, 64], BF16) for d in range(2)]
            Lt0 = self.sb(st, "Lt0", [64, 8, 64], F32R)
            Tfin = self.sb(st, "Tfin", [64, 8, 64], BF16)
            SCk = self.sb(st, "SCk", [128, 2, 8, 64], BF16)
            Lm = [self.sb(st, "Lm%d" % i, [64, 8, 64], F32R) for i in range(2)]
            Ltm = [self.sb(st, "Ltm%d" % i, [64, 8, 64], F32R) for i in range(2)]
            ILm = self.sb(st, "ILm", [64, 8, 64], F32R)
            Ttm = [self.sb(st, "Ttm%d" % i, [64, 8, 64], F32R) for i in range(2)]

            ones_bf, oblk, ident = self.ones_bf, self.oblk_bf, self.ident
            tiles = [(0, t0, True, 0) for t0 in range(0, TS, TT)] + [(1, TS, False, 1), (2, TS + TP, False, 1)]
            for (seq, g0, is_s, mc) in tiles[:self.ktiles]:
                if is_s:
                    src = self.xs[g0:g0 + TT, :].rearrange("(s p) f -> p s f", p=128)
                else:
                    l0 = g0 - TS
                    src = self.xp[l0:l0 + TT, :].rearrange("(s p) f -> p s f", p=128)
                self.dma(xin[:], src, [], [xin])
                if is_s:
                    petv = xT[:].rearrange("p a b -> p (a b)").rearrange("p (s f) -> p s f", s=2)
                    self.dma(petv, self.pe[g0:g0 + TT, :].rearrange("(s p) f -> p s f", p=128), [], [xT], eng="act")
                    self.tt("pool", xin[:], xin[:], petv, ALU.add, [xin, xT], [xin])
                if self.cutk == 1:
                    return
                for j in range(8):
                    p = self.ps()
                    for s in range(2):
                        self.tr(p[:, s * 128:(s + 1) * 128], xin[:, s, j * 128:(j + 1) * 128], ident[:], [xin, ident], [p])
                    self.cp("act", xT[:, j, :], p[:, 0:TT], [p], [xT])
                    self.tt("pool", sq[:, j, :], xT[:, j, :], xT[:, j, :], ALU.mult, [xT], [sq])
                self.dma(self.xT_scr[:, :, g0:g0 + TT], xT[:], [xT], [self.xT_scr])
                if self.cutk == 2:
                    return
                p = self.ps()
                for j in range(8):
                    self.mm(p[:, 0:TT], ones_bf[:], sq[:, j, :], j == 0, j == 7, [ones_bf, sq], [p])
                self.act(rstd[:], p[:, 0:TT], AF.Sqrt, [p, self.epsT], [rstd], scale=1.0 / D, bias=self.epsT[:, 0:1])
                self.S.op("dve", lambda e: e.reciprocal(out=rstd[:], in_=rstd[:]), [rstd.b], [rstd.b])
                for j in range(8):
                    t = tmpA[j % 2]
                    self.tt("dve", t[:], xT[:, j, :], rstd[:], ALU.mult, [xT, rstd], [t])
                    self.act(hT[:, j, :], t[:], AF.Identity, [t, self.gs1, self.modT], [hT],
                             scale=self.gs1[:, j, mc:mc + 1], bias=self.modT[:, j, mc:mc + 1])
                if self.cutk == 3:
                    return
                for m in range(23):
                    if m >= self.cutm:
                        break
                    p = self.ps()
                    for kc in range(8):
                        self.mm(p[:, 0:TT], win[:, kc, m * 128:(m + 1) * 128], hT[:, kc, :], kc == 0, kc == 7, [win, hT], [p])
                    pz = p[:, 0:TT]
                    if m < 4:
                        self.cp("act", rT[:, m, :], pz, [p], [rT])
                    elif m < 8:
                        self.cp("act", kT[:, m - 4, :], pz, [p], [kT])
                    elif m < 12:
                        self.cp("act", vT[:, m - 8, :], pz, [p], [vT])
                    elif m == 12:
                        self.act(xw[:], pz, AF.Tanh, [p], [xw])
                    elif m == 13:
                        self.cp("act", xa[:], pz, [p], [xa])
                    elif m == 14:
                        self.act(xg[:], pz, AF.Sigmoid, [p], [xg])
                    elif m < 19:
                        self.cp("act", xbT[:, m - 15, :], pz, [p], [xbT])
                    else:
                        j = m - 19
                        g0_, g1_, g2_ = gtmp
                        self.cp("act", g0_[:], pz, [p], [g0_])
                        self.tt("pool", g1_[:], g0_[:], g0_[:], ALU.mult, [g0_], [g1_])
                        self.tsc("dve", g1_[:], g1_[:], 0.044715, ALU.mult, [g1_], [g1_], 1.0, ALU.add)
                        self.tt("dve", g1_[:], g1_[:], g0_[:], ALU.mult, [g1_, g0_], [g1_])
                        self.act(g2_[:], g1_[:], AF.Sigmoid, [g1_], [g2_], scale=GELU_C)
                        self.tt("pool", gate[:, j, :], g0_[:], g2_[:], ALU.mult, [g0_, g2_], [gate])
                self.dma(self.xb_scr[:, :, g0:g0 + TT], xbT[:], [xbT], [self.xb_scr])
                if self.debug and g0 == 0:
                    self.dump("hT", hT[:], [128, 8, TT], BF16, [hT])
                    self.dump("rT", rT[:], [128, 4, TT], F32, [rT])
                    self.dump("vT", vT[:], [128, 4, TT], F32, [vT])
                    self.dump("xbT", xbT[:], [128, 4, TT], F32, [xbT])
                    self.dump("gate", gate[:], [128, 4, TT], BF16, [gate])
                self.dma(self.gate_scr[:, :, g0:g0 + TT], gate[:], [gate], [self.gate_scr])
                if self.cutk == 4:
                    return
                for j in range(4):
                    p = self.ps()
                    self.mm(p[:, 0:TT], gup[:, j * 128:(j + 1) * 128], xg[:], True, True, [gup, xg], [p])
                    self.cp("act", gT[:, j, :], p[:, 0:TT], [p], [gT])
                self.dma(self.g_scr[:, :, g0:g0 + TT], gT[:], [gT], [self.g_scr])
                for j in range(4):
                    self.tsc("dve", kkn[:, j, :], kT[:, j, :], prm[:, P_KK + j:P_KK + j + 1], ALU.mult, [kT, prm], [kkn])
                    self.tt("pool", sq[:, j, :], kkn[:, j, :], kkn[:, j, :], ALU.mult, [kkn], [sq])
                for j in range(4):
                    p = self.ps()
                    self.mm(p[:, 0:TT], oblk[:], sq[:, j, :], True, True, [oblk, sq], [p])
                    t = tmpA[j % 2]
                    self.act(t[:], p[:, 0:TT], AF.Sqrt, [p], [t])
                    self.tsc("dve", t[:], t[:], 1e-12, ALU.max, [t], [t])
                    self.S.op("dve", lambda e, t=t: e.reciprocal(out=t[:], in_=t[:]), [t.b], [t.b])
                    self.tt("dve", kkn[:, j, :], kkn[:, j, :], t[:], ALU.mult, [kkn, t], [kkn])
                for s in range(2):
                    p = self.ps()
                    for j in range(4):
                        self.tr(p[:, j * 128:(j + 1) * 128], vT[:, j, s * 128:(s + 1) * 128], ident[:], [vT, ident], [p])
                    self.cp("act", tokV[:, s, :, :].rearrange("p h k -> p (h k)"), p[:], [p], [tokV])
                c0 = g0 // 64
                for s in range(2):
                    dst = self.vt_scr[c0 + 2 * s:c0 + 2 * s + 2, :, :].rearrange("c s f -> (c s) f")
                    self.dma(dst, tokV[:, s, :, :].rearrange("p h k -> p (h k)"), [tokV], [self.vt_scr])
                if self.cutk == 5:
                    return
                for d in range(2):
                    for j in range(4):
                        p = self.ps()
                        self.mm(p[:, 0:TT], wup[d * 64:(d + 1) * 64, j * 128:(j + 1) * 128], xw[d * 64:(d + 1) * 64, :],
                                True, True, [wup, xw], [p])
                        self.act(sg[:, j, :], p[:, 0:TT], AF.Sigmoid, [p, prm], [sg],
                                 bias=prm[:, P_W0 + 4 * d + j:P_W0 + 4 * d + j + 1])
                        if d == 0:
                            self.scan(cs[:, j, :], cst[:, C_RMF:C_RMF + TT], sg[:, j, :], 0.0, [cst, sg], [cs])
                        else:
                            self.scan(cs[:, j, ::-1], cst[:, C_RMB:C_RMB + TT][:, ::-1], sg[:, j, ::-1], 0.0, [cst, sg], [cs])
                    for j in range(4):
                        p = self.ps()
                        self.mm(p[:, 0:TT], aup[d * 64:(d + 1) * 64, j * 128:(j + 1) * 128], xa[d * 64:(d + 1) * 64, :],
                                True, True, [aup, xa], [p])
                        self.act(ad[:, j, :], p[:, 0:TT], AF.Sigmoid, [p, prm], [ad],
                                 bias=prm[:, P_A0 + 4 * d + j:P_A0 + 4 * d + j + 1])
                    self.tt("pool", sg[:], cs[:], sg[:], ALU.subtract, [cs, sg], [sg])
                    self.act(E1[:], cs[:], AF.Exp, [cs], [E1], scale=-LAM)
                    self.act(cs[:], cs[:], AF.Exp, [cs], [cs], scale=LAM)
                    self.act(sg[:], sg[:], AF.Exp, [sg], [sg], scale=-LAM)
                    E2, E3 = cs, sg
                    ar5 = AR[d]
                    bk5 = BK[d]
                    v4 = lambda tl: tl[:].rearrange("p j (c t) -> p j c t", t=64)
                    self.stt(ar5[:, :, :, 0, :], v4(kkn), -1.0, v4(E3), ALU.mult, ALU.mult, [kkn, E3], [ar5])
                    self.tt("pool", ar5[:, :, :, 1, :], v4(rT), v4(E1), ALU.mult, [rT, E1], [ar5])
                    self.tt("dve", wk1[:], kkn[:], ad[:], ALU.mult, [kkn, ad], [wk1])
                    self.tt("dve", wk1[:], wk1[:], E2[:], ALU.mult, [wk1, E2], [wk1])
                    self.cp("pool", bk5[:, :, :, 0, :], v4(wk1), [wk1], [bk5])
                    for j in range(4):
                        self.tsc("dve", wk2[:, j, :], ad[:, j, :], prm[:, P_KA + j:P_KA + j + 1], ALU.mult, [ad, prm, self.misc], [wk2],
                                 self.misc[:, j:j + 1], ALU.add)
                    self.tt("pool", wk2[:], wk2[:], kT[:], ALU.mult, [wk2, kT], [wk2])
                    if d == 0:
                        self.cp("pool", ksum[:], wk2[:], [wk2], [ksum])
                    else:
                        self.tt("pool", ksum[:], ksum[:], wk2[:], ALU.add, [ksum, wk2], [ksum])
                    self.tt("dve", wk2[:], wk2[:], E2[:], ALU.mult, [wk2, E2], [wk2])
                    self.cp("pool", bk5[:, :, :, 1, :], v4(wk2), [wk2], [bk5])
                    te = 63 if d == 0 else 0
                    pend_b = v4(E1)[:, :, :, te:te + 1].to_broadcast([128, 4, NC4, 64])
                    self.cp("pool", PEb[d][:], pend_b, [E1], [PEb[d]])
                    self.tt("dve", v4(wk1), v4(wk1), PEb[d][:], ALU.mult, [wk1, PEb[d]], [wk1])
                    self.tt("dve", v4(wk2), v4(wk2), PEb[d][:], ALU.mult, [wk2, PEb[d]], [wk2])
                    for (srcw, tokX, scr) in ((wk1, tokB[d], self.bh_scr), (wk2, tokK[d], self.kh_scr)):
                        for s in range(2):
                            p = self.ps()
                            for j in range(4):
                                self.tr(p[:, j * 128:(j + 1) * 128], srcw[:, j, s * 128:(s + 1) * 128], ident[:], [srcw, ident], [p])
                            self.cp("act", tokX[:, s, :, :].rearrange("p h k -> p (h k)"), p[:], [p], [tokX])
                            for cc in range(2):
                                self.dma(scr[c0 + 2 * s + cc, d * 64:(d + 1) * 64, :],
                                         tokX[cc * 64:(cc + 1) * 64, s, :, :].rearrange("p h k -> p (h k)"),
                                         [tokX], [scr])
                    for cl in range(NC4):
                        c = c0 + cl
                        for hp in range(2):
                            for (q, scr) in ((0, self.art_scr), (1, self.rrt_scr)):
                                dst = scr[c, d * 64:(d + 1) * 64, :].rearrange("k (j hp t) -> k j hp t", hp=2, t=64)[:, :, hp, :]
                                self.dma(dst, ar5[hp * 64:(hp + 1) * 64, :, cl, q, :], [ar5], [scr], eng="sp")
                            dst = self.pend_scr[c, d * 64:(d + 1) * 64, :].rearrange("k (j hp t) -> k j hp t", hp=2, t=64)[:, :, hp, :]
                            self.dma(dst, PEb[d][hp * 64:(hp + 1) * 64, :, cl, :], [PEb[d]], [self.pend_scr], eng="sp")
                if self.cutk == 6:
                    return
                for j in range(4):
                    self.stt(sq[:, j, :], rT[:, j, :], prm[:, P_RK + j:P_RK + j + 1], ksum[:, j, :], ALU.mult, ALU.mult,
                             [rT, prm, ksum], [sq])
                    p = self.ps()
                    self.mm(p[:, 0:TT], oblk[:], sq[:, j, :], True, True, [oblk, sq], [p])
                    self.tt("dve", bon[:, j, :], p[:, 0:TT], vT[:, j, :], ALU.mult, [p, vT], [bon])
                self.dma(self.bon_scr[:, :, g0:g0 + TT], bon[:], [bon], [self.bon_scr])
                if self.cutk == 7:
                    return
                mS1 = cst[0:64, C_MSI1:C_MSI1 + 128].rearrange("p (q t) -> p q t", q=2)
                mL1 = cst[0:64, C_ML1:C_ML1 + 64]
                id64 = idS[0:64]
                for cl in range(NC4):
                    c = c0 + cl
                    for hp in range(2):
                        p = self.ps()
                        for j in range(4):
                            for d in range(2):
                                self.mm(p[d * 64:(d + 1) * 64, j * 128:(j + 1) * 128],
                                        BK[d][hp * 64:(hp + 1) * 64, j, cl, 1, :],
                                        AR[d][hp * 64:(hp + 1) * 64, j, cl, :, :].rearrange("p q t -> p (q t)"),
                                        True, True, [BK[d], AR[d]], [p])
                        self.tt("dve", SCk[:, :, hp::2, :].rearrange("p q h t -> p h q t"),
                                p[:].rearrange("p (h q t) -> p h q t", q=2, t=64),
                                mSI.unsqueeze(1).to_broadcast([128, 4, 2, 64]), ALU.mult, [p, cst], [SCk])
                    self.dma(self.akt_scr[c], SCk[:, 0, :, :].rearrange("p h s -> p (h s)"), [SCk], [self.akt_scr], eng="act")
                    self.dma(self.mrkt_scr[c], SCk[:, 1, :, :].rearrange("p h s -> p (h s)"), [SCk], [self.mrkt_scr], eng="act")
                    for d in range(2):
                        msk = mSI[0:64] if d == 0 else mS1
                        mskL = mL[0:64] if d == 0 else mL1
                        L0 = Lm[0]
                        for hp in range(2):
                            p = self.ps()
                            for j in range(4):
                                self.mm(p[0:64, j * 128:(j + 1) * 128],
                                        BK[d][hp * 64:(hp + 1) * 64, j, cl, 0, :],
                                        AR[d][hp * 64:(hp + 1) * 64, j, cl, :, :].rearrange("p q t -> p (q t)"),
                                        True, True, [BK[d], AR[d]], [p])
                            p4 = p[0:64, :].rearrange("p (h q t) -> p h q t", q=2, t=64)
                            self.tt("dve", Lt0[:, hp::2, :], p4[:, :, 0, :], msk[:, 0, :].unsqueeze(1).to_broadcast([64, 4, 64]),
                                    ALU.mult, [p, cst], [Lt0])
                            self.tt("dve", MRBd[d][:, hp::2, :], p4[:, :, 1, :], msk[:, 1, :].unsqueeze(1).to_broadcast([64, 4, 64]),
                                    ALU.mult, [p, cst], [MRBd[d]])
                            p2 = self.ps()
                            for j in range(4):
                                self.mm(p2[0:64, j * 64:(j + 1) * 64],
                                        AR[d][hp * 64:(hp + 1) * 64, j, cl, 0, :], BK[d][hp * 64:(hp + 1) * 64, j, cl, 0, :],
                                        True, True, [AR[d], BK[d]], [p2])
                            self.tt("dve", L0[:, hp::2, :], p2[0:64, 0:256].rearrange("p (h s) -> p h s", s=64),
                                    mskL.unsqueeze(1).to_broadcast([64, 4, 64]), ALU.mult, [p2, cst], [L0])
                        self.dma(self.mrbt_scr[c, d * 64:(d + 1) * 64, :], MRBd[d][:].rearrange("p h s -> p (h s)"),
                                 [MRBd[d]], [self.mrbt_scr], eng="act")
                        T0 = Ttm[0]
                        self.tt("pool", T0[:], Lt0[:].bitcast(F32), id64.unsqueeze(1).to_broadcast([64, 8, 64]), ALU.add,
                                [Lt0, cst], [T0])
                        L_prev, Tt_prev, Lt_prev = L0, T0, Lt0
                        for lev in range(1, 6):
                            L_new, Lt_new, Tt_new = Lm[lev % 2], Ltm[lev % 2], Ttm[lev % 2]
                            pA = self.ps()
                            for h in range(8):
                                self.mm(pA[0:64, h * 64:(h + 1) * 64], Lt_prev[:, h, :], L_prev[:, h, :], True, True,
                                        [Lt_prev, L_prev], [pA])
                            if lev < 5:
                                pB = self.ps()
                                for h in range(8):
                                    self.mm(pB[0:64, h * 64:(h + 1) * 64], L_prev[:, h, :], Lt_prev[:, h, :], True, True,
                                            [Lt_prev, L_prev], [pB])
                            self.tt("dve", ILm[:], pA[0:64, :].rearrange("p (h s) -> p h s", s=64),
                                    id64.unsqueeze(1).to_broadcast([64, 8, 64]), ALU.add, [pA, cst], [ILm])
                            if lev < 5:
                                self.cp("act", L_new[:].rearrange("p h s -> p (h s)"), pA[0:64, :], [pA], [L_new])
                                self.cp("act", Lt_new[:].rearrange("p h s -> p (h s)"), pB[0:64, :], [pB], [Lt_new])
                            pC = self.ps()
                            for h in range(8):
                                self.mm(pC[0:64, h * 64:(h + 1) * 64], ILm[:, h, :], Tt_prev[:, h, :], True, True,
                                        [ILm, Tt_prev], [pC])
                            if lev < 5:
                                self.cp("act", Tt_new[:].rearrange("p h s -> p (h s)"), pC[0:64, :], [pC], [Tt_new])
                            else:
                                self.cp("act", Tfin[:].rearrange("p h s -> p (h s)"), pC[0:64, :], [pC], [Tfin])
                            L_prev, Tt_prev, Lt_prev = L_new, Tt_new, Lt_new
                        self.dma(self.ttt_scr[c, d * 64:(d + 1) * 64, :], Tfin[:].rearrange("p h s -> p (h s)"),
                                 [Tfin], [self.ttt_scr], eng="act")

    def phaseB(self):
        self.phaseB_lru()
        self.phaseB_chain()
        self.S.barrier()
        self.phaseB_post()
        if self.debug:
            self.S.barrier()
            self.dump("yscr", self.y_scr[:], [128, 8, NTOK], BF16, [self.y_scr])
            self.dump("ytok", self.ytok_scr[:], [2, NTOK, 512], F32, [self.ytok_scr])

    def phaseB_lru(self):
        prm, cst, misc = self.prm_t, self.cst_t, self.misc
        with contextlib.ExitStack() as st:
            wbd32 = self.sb(st, "wbd32", [128, 16, 128], F32)
            wbd = self.sb(st, "wbd", [128, 16, 128], BF16)
            self.memset("pool", wbd32[:], 0.0, [wbd32])
            for gi, src in enumerate((self.lwa, self.lwx)):
                for d in range(2):
                    for j in range(4):
                        for hb in range(2):
                            self.dma(wbd32[hb * 64:(hb + 1) * 64, (gi * 2 + d) * 4 + j, hb * 64:(hb + 1) * 64],
                                     src[d, 2 * j + hb], [], [wbd32])
            self.cp("dve", wbd[:], wbd32[:], [wbd32], [wbd])
            TM = TS
            xbp = self.sb(st, "xbp", [128, TM + 4], F32)
            xc = self.sb(st, "xc", [128, TM], F32)
            xcb = self.sb(st, "xcb", [128, TM], BF16)
            gt = self.sb(st, "gt_l", [128, TM], BF16)
            a_t = self.sb(st, "a_t", [128, TM], F32)
            bx_t = self.sb(st, "bx_t", [128, TM], F32)
            s_t = self.sb(st, "s_t", [128, TM], F32)
            hs = [self.sb(st, "hs%d" % d, [128, TM], F32) for d in range(2)]
            yb = self.sb(st, "yb", [128, TM], BF16)
            for (seq, g0, T) in ((0, 0, TS), (1, TS, TP), (2, TS + TP, TP)):
                for j in range(4):
                    self.memset("pool", xbp[:, 0:2], 0.0, [xbp])
                    self.memset("pool", xbp[:, T + 2:T + 4], 0.0, [xbp])
                    self.dma(xbp[:, 2:T + 2], self.xb_scr[:, j, g0:g0 + T], [self.xb_scr], [xbp])
                    self.dma(gt[:, 0:T], self.gate_scr[:, j, g0:g0 + T], [self.gate_scr], [gt], eng="act")
                    cw = lambda i: prm[:, P_CW + 4 * i + j:P_CW + 4 * i + j + 1]
                    self.act(xc[:, 0:T], xbp[:, 0:T], AF.Identity, [xbp, prm], [xc], scale=cw(0), bias=prm[:, P_CB + j:P_CB + j + 1])
                    for i in range(1, 4):
                        self.stt(xc[:, 0:T], xbp[:, i:i + T], cw(i), xc[:, 0:T], ALU.mult, ALU.add, [xbp, prm, xc], [xc])
                    self.cp("pool", xcb[:, 0:T], xc[:, 0:T], [xc], [xcb])
                    for d in range(2):
                        for t0 in range(0, T, 512):
                            tw = min(512, T - t0)
                            p = self.ps()
                            self.mm(p[:, 0:tw], wbd[:, (0 * 2 + d) * 4 + j, :], xcb[:, t0:t0 + tw], True, True, [wbd, xcb], [p])
                            self.act(s_t[:, t0:t0 + tw], p[:, 0:tw], AF.Sigmoid, [p, prm], [s_t],
                                     bias=prm[:, P_BA + 4 * d + j:P_BA + 4 * d + j + 1])
                            p2 = self.ps()
                            self.mm(p2[:, 0:tw], wbd[:, (1 * 2 + d) * 4 + j, :], xcb[:, t0:t0 + tw], True, True, [wbd, xcb], [p2])
                            self.act(bx_t[:, t0:t0 + tw], p2[:, 0:tw], AF.Sigmoid, [p2, prm], [bx_t],
                                     bias=prm[:, P_BX + 4 * d + j:P_BX + 4 * d + j + 1])
                        col = 4 + d * 4 + j
                        self.act(a_t[:, 0:T], s_t[:, 0:T], AF.Exp, [s_t, misc], [a_t], scale=misc[:, col:col + 1])
                        self.act(s_t[:, 0:T], s_t[:, 0:T], AF.Exp, [s_t, misc], [s_t], scale=misc[:, col + 8:col + 9])
                        self.act(s_t[:, 0:T], s_t[:, 0:T], AF.Sqrt, [s_t], [s_t], scale=-1.0, bias=1.0)
                        self.tt("pool", bx_t[:, 0:T], bx_t[:, 0:T], xc[:, 0:T], ALU.mult, [bx_t, xc], [bx_t])
                        self.tt("dve", bx_t[:, 0:T], bx_t[:, 0:T], s_t[:, 0:T], ALU.mult, [bx_t, s_t], [bx_t])
                        h = hs[d]
                        if seq == 0:
                            init = prm[:, P_H0 + 4 * d + j:P_H0 + 4 * d + j + 1]
                        else:
                            init = 0.0
                        if d == 0:
                            self.scan(h[:, 0:T], a_t[:, 0:T], bx_t[:, 0:T], init, [a_t, bx_t, prm], [h])
                        else:
                            self.scan(h[:, 0:T][:, ::-1], a_t[:, 0:T][:, ::-1], bx_t[:, 0:T][:, ::-1], init, [a_t, bx_t, prm], [h])
                        if seq > 0:
                            col_o = j * 4 + (seq - 1) * 2 + d
                            te = T - 1 if d == 0 else 0
                            self.cp("pool", self.stl_t[:, col_o:col_o + 1], h[:, te:te + 1], [h], [self.stl_t])
                    self.tt("pool", hs[0][:, 0:T], hs[0][:, 0:T], hs[1][:, 0:T], ALU.add, [hs[0], hs[1]], [hs[0]])
                    self.tt("dve", yb[:, 0:T], hs[0][:, 0:T], gt[:, 0:T], ALU.mult, [hs[0], gt], [yb])
                    self.dma(self.y_scr[:, 4 + j, g0:g0 + T], yb[:, 0:T], [yb], [self.y_scr])
            self.dma(self.stl_o[:], self.stl_t[:], [self.stl_t], [self.stl_o])

    def phaseB_chain(self):
        with contextlib.ExitStack() as st:
            NB = 2
            def ring(name, dt=BF16):
                return [self.sb2(st, "%s%d" % (name, i), [128, 512], dt) for i in range(NB)]
            art, rrt, ttt, akt, mrbt, mrkt, bh, kh, vt = [ring(n) for n in
                                                          ("c_art", "c_rrt", "c_ttt", "c_akt", "c_mrbt", "c_mrkt", "c_bh", "c_kh", "c_vt")]
            pend = ring("c_pend", F32)
            Hf = self.sb2(st, "Hf", [128, 512], F32)
            Hb = self.sb2(st, "Hb", [128, 512], BF16)
            Zs = self.sb2(st, "Zs", [128, 512], BF16)
            Us = self.sb2(st, "Us", [128, 512], BF16)
            Yt = [self.sb2(st, "Yt%d" % i, [128, 512], F32) for i in range(2)]
            tmpH = self.sb2(st, "tmpH", [128, 512], F32)
            step = 0
            hs_ = lambda h: slice(h * 64, (h + 1) * 64)
            for (seq, cbase, n) in ((0, 0, 32), (1, 32, 4), (2, 36, 4)):
                for d in range(2):
                    sl = slice(d * 64, (d + 1) * 64)
                    if seq == 0:
                        self.dma(Hf[d][sl, :], self.h0r[sl, :], [], [Hf[d]])
                    else:
                        self.memset("dve", Hf[d][sl, :], 0.0, [Hf[d]])
                    self.cp("dve", Hb[d][sl, :], Hf[d][sl, :], [Hf[d]], [Hb[d]])
                for i in range(n):
                    r = step % NB
                    step += 1
                    for d in range(2):
                        sl = slice(d * 64, (d + 1) * 64)
                        c = cbase + i if d == 0 else cbase + n - 1 - i
                        for (tl, scr) in ((art, self.art_scr), (rrt, self.rrt_scr), (ttt, self.ttt_scr), (akt, self.akt_scr),
                                          (mrbt, self.mrbt_scr), (mrkt, self.mrkt_scr), (bh, self.bh_scr), (kh, self.kh_scr),
                                          (pend, self.pend_scr)):
                            e = self.pick("chdma", ["sp", "act"])
                            self.dma(tl[r][d][sl, :], scr[c, sl, :], [scr], [tl[r][d]], eng=e)
                        self.dma(vt[r][d][sl, :], self.vt_scr[c], [self.vt_scr], [vt[r][d]])
                    for d in range(2):
                        sl = slice(d * 64, (d + 1) * 64)
                        c = cbase + i if d == 0 else cbase + n - 1 - i
                        A_, R_, T_, AK_, MRB_, MRK_, B_, K_, V_, PE_ = (x[r][d] for x in (art, rrt, ttt, akt, mrbt, mrkt, bh, kh, vt, pend))
                        H_, Hf_, Z_, U_ = Hb[d], Hf[d], Zs[d], Us[d]
                        pZ = self.ps()
                        for h in range(8):
                            self.mm(pZ[sl, hs_(h)], A_[sl, hs_(h)], H_[sl, hs_(h)], True, False, [A_, H_], [pZ])
                            self.mm(pZ[sl, hs_(h)], AK_[sl, hs_(h)], V_[sl, hs_(h)], False, True, [AK_, V_], [pZ])
                        self.cp("act", Z_[sl, :], pZ[sl, :], [pZ], [Z_])
                        pU = self.ps()
                        for h in range(8):
                            self.mm(pU[sl, hs_(h)], T_[sl, hs_(h)], Z_[sl, hs_(h)], True, True, [T_, Z_], [pU])
                        self.cp("act", U_[sl, :], pU[sl, :], [pU], [U_])
                        pY = self.ps()
                        for h in range(8):
                            self.mm(pY[sl, hs_(h)], R_[sl, hs_(h)], H_[sl, hs_(h)], True, False, [R_, H_], [pY])
                            self.mm(pY[sl, hs_(h)], MRB_[sl, hs_(h)], U_[sl, hs_(h)], False, False, [MRB_, U_], [pY])
                            self.mm(pY[sl, hs_(h)], MRK_[sl, hs_(h)], V_[sl, hs_(h)], False, True, [MRK_, V_], [pY])
                        y = Yt[step % 2][d]
                        self.cp("act", y[sl, :], pY[sl, :], [pY], [y])
                        self.dma(self.ytok_scr[d, c * 64:(c + 1) * 64, :], y[sl, :], [y], [self.ytok_scr])
                        pH = self.ps()
                        for h in range(8):
                            self.mm(pH[sl, hs_(h)], B_[sl, hs_(h)], U_[sl, hs_(h)], True, False, [B_, U_], [pH])
                            self.mm(pH[sl, hs_(h)], K_[sl, hs_(h)], V_[sl, hs_(h)], False, True, [K_, V_], [pH])
                        self.tt("dve", tmpH[d][sl, :], Hf_[sl, :], PE_[sl, :], ALU.mult, [Hf_, PE_], [tmpH[d]])
                        self.tt("dve", Hf_[sl, :], tmpH[d][sl, :], pH[sl, :], ALU.add, [tmpH[d], pH], [Hf_])
                        self.cp("dve", H_[sl, :], Hf_[sl, :], [Hf_], [H_])
                if seq > 0:
                    self.dma(self.str_o[seq - 1], Hf[0][:], [Hf[0], Hf[1]], [self.str_o])

    def phaseB_post(self):
        prm, cst = self.prm_t, self.cst_t
        ident = self.ident
        with contextlib.ExitStack() as st:
            yf = [self.sb(st, "yf%d" % i, [128, 512], F32) for i in range(2)]
            yb2 = [self.sb(st, "yb2%d" % i, [128, 512], F32) for i in range(2)]
            cen = self.sb(st, "cen", [128, 8, 64], F32)
            sqv = self.sb(st, "sqv", [128, 8, 64], F32)
            mean = self.sb(st, "mean", [128, 8], F32)
            var = self.sb(st, "var", [128, 8], F32)
            gl = [self.sb(st, "gl%d" % i, [128, 4, 128], BF16) for i in range(2)]
            bl = [self.sb(st, "bl%d" % i, [128, 4, 128], BF16) for i in range(2)]
            ynT = self.sb(st, "ynT", [128, 4, 128], F32)
            yo = [self.sb(st, "yo%d" % i, [128, 4, 128], BF16) for i in range(2)]
            for it in range(NTOK // 128):
                g0 = it * 128
                a, b = yf[it % 2], yb2[it % 2]
                g_, b_ = gl[it % 2], bl[it % 2]
                o = yo[it % 2]
                self.dma(a[:], self.ytok_scr[0, g0:g0 + 128, :], [self.ytok_scr], [a])
                self.dma(b[:], self.ytok_scr[1, g0:g0 + 128, :], [self.ytok_scr], [b], eng="act")
                self.dma(g_[:], self.g_scr[:, :, g0:g0 + 128], [self.g_scr], [g_])
                self.dma(b_[:], self.bon_scr[:, :, g0:g0 + 128], [self.bon_scr], [b_], eng="act")
                a3 = a[:].rearrange("p (h v) -> p h v", v=64)
                self.tt("pool", a[:], a[:], b[:], ALU.add, [a, b], [a])
                self.S.op("dve", lambda e, a3=a3: e.tensor_reduce(out=mean[:], in_=a3, op=ALU.add, axis=mybir.AxisListType.X),
                          [a.b], [mean.b])
                self.tsc("dve", mean[:], mean[:], 1.0 / 64, ALU.mult, [mean], [mean])
                self.tt("dve", cen[:], a3, mean[:].unsqueeze(2).to_broadcast([128, 8, 64]), ALU.subtract, [a, mean], [cen])
                self.tt("pool", sqv[:], cen[:], cen[:], ALU.mult, [cen], [sqv])
                self.S.op("dve", lambda e: e.tensor_reduce(out=var[:], in_=sqv[:], op=ALU.add, axis=mybir.AxisListType.X),
                          [sqv.b], [var.b])
                self.act(var[:], var[:], AF.Sqrt, [var, self.epsT], [var], scale=1.0 / 64, bias=self.epsT[:, 1:2])
                self.S.op("dve", lambda e: e.reciprocal(out=var[:], in_=var[:]), [var.b], [var.b])
                self.tt("dve", cen[:], cen[:], var[:].unsqueeze(2).to_broadcast([128, 8, 64]), ALU.mult, [cen, var], [cen])
                p = self.ps()
                cen2 = cen[:].rearrange("p h v -> p (h v)")
                for j in range(4):
                    self.tr(p[:, j * 128:(j + 1) * 128], cen2[:, j * 128:(j + 1) * 128], ident[:], [cen, ident], [p])
                for j in range(4):
                    self.act(ynT[:, j, :], p[:, j * 128:(j + 1) * 128], AF.Identity, [p, prm], [ynT],
                             scale=prm[:, P_LNG + j:P_LNG + j + 1], bias=prm[:, P_LNB + j:P_LNB + j + 1])
                self.tt("pool", ynT[:], ynT[:], b_[:], ALU.add, [ynT, b_], [ynT])
                self.tt("dve", o[:], ynT[:], g_[:], ALU.mult, [ynT, g_], [o])
                self.dma(self.y_scr[:, 0:4, g0:g0 + 128], o[:], [o], [self.y_scr])

    def phaseC(self):
        prm, cst = self.prm_t, self.cst_t
        ident, ones_bf = self.ident, self.ones_bf
        with contextlib.ExitStack() as st:
            stg = [self.sb(st, "stgC%d" % i, [128, 8, 512], F32) for i in range(2)]
            wb = [self.sb(st, "wbC%d" % i, [128, 8, 512], BF16) for i in range(2)]
            w1src = self.w1[:].rearrange("(kc p) n -> p kc n", p=128)
            for blk in range(8):
                s, o = stg[blk % 2], wb[blk % 2]
                self.dma(s[:], w1src[:, :, blk * 512:(blk + 1) * 512], [], [s])
                self.cp("act" if blk % 2 == 0 else "pool", o[:], s[:], [s], [o])
                self.dma(self.w1_scr[blk], o[:], [o], [self.w1_scr], eng="act")
            w2src = self.w2[:].rearrange("(fc p) n -> p fc n", p=128)
            for m in range(8):
                s, o = stg[m % 2], wb[m % 2]
                s4 = s[:].rearrange("p k (a b) -> p (k a) b", b=128)
                o4 = o[:].rearrange("p k (a b) -> p (k a) b", b=128)
                self.dma(s4, w2src[:, :, m * 128:(m + 1) * 128], [], [s])
                self.cp("act" if m % 2 == 0 else "pool", o[:], s[:], [s], [o])
                self.dma(self.w2_scr[m], o4, [o], [self.w2_scr], eng="act")
        self.S.barrier()
        with contextlib.ExitStack() as st:
            wout = self.sb(st, "wout", [128, 8, D], BF16)
            wsrc = self.w_out[:].rearrange("(kc p) n -> p kc n", p=128)
            stg = [self.sb(st, "stgD%d" % i, [128, 8, 256], F32) for i in range(2)]
            for cb in range(4):
                s = stg[cb % 2]
                self.dma(s[:], wsrc[:, :, cb * 256:(cb + 1) * 256], [], [s])
                self.cp("act", wout[:, :, cb * 256:(cb + 1) * 256], s[:], [s], [wout])
            yT = self.sb(st, "yT_c", [128, 8, TC], BF16)
            oT = self.sb(st, "oT", [128, 8, TC], F32)
            sq = self.sb(st, "sq_c", [128, 8, TC], BF16)
            xT = self.sb(st, "xT_c", [128, 8, TC], F32)
            h2 = self.sb(st, "h2", [128, 8, TC], BF16)
            f = self.sb(st, "f_c", [128, 32, TC], BF16)
            otok = self.sb(st, "otok", [128, 4, D], F32)
            rstd = self.sb(st, "rstd_c", [128, TC], F32)
            tmp = [self.sb(st, "tmpC%d" % i, [128, TC], F32) for i in range(2)]
            w1r = [self.sb(st, "w1r%d" % i, [128, 8, 512], BF16) for i in range(2)]
            w2r = [self.sb(st, "w2r%d" % i, [128, 32, 128], BF16) for i in range(2)]

            def rms(src, R):
                p = self.ps()
                for j in range(8):
                    self.mm(p[:], ones_bf[:], sq[:, j, :], j == 0, j == 7, [ones_bf, sq], [p])
                self.act(rstd[:], p[:], AF.Sqrt, [p, self.epsT], [rstd], scale=1.0 / D, bias=self.epsT[:, 0:1])
                self.S.op("dve", lambda e: e.reciprocal(out=rstd[:], in_=rstd[:]), [rstd.b], [rstd.b])

            def resid(gg, mc):
                for j in range(8):
                    t = tmp[j % 2]
                    self.tt("dve", t[:], oT[:, j, :], rstd[:], ALU.mult, [oT, rstd], [t])
                    self.stt(xT[:, j, :], t[:], gg[:, j, mc:mc + 1], xT[:, j, :], ALU.mult, ALU.add, [t, gg, xT], [xT])

            wi = 0
            for ti in range(NTOK // TC):
                g0 = ti * TC
                mc = 0 if g0 < TS else 1
                self.dma(yT[:], self.y_scr[:, :, g0:g0 + TC], [self.y_scr], [yT])
                self.dma(xT[:], self.xT_scr[:, :, g0:g0 + TC], [self.xT_scr], [xT], eng="act")
                for m in range(8):
                    p = self.ps()
                    for kc in range(8):
                        self.mm(p[:], wout[:, kc, m * 128:(m + 1) * 128], yT[:, kc, :], kc == 0, kc == 7, [wout, yT], [p])
                    self.cp("act", oT[:, m, :], p[:], [p], [oT])
                    self.tt("pool", sq[:, m, :], oT[:, m, :], oT[:, m, :], ALU.mult, [oT], [sq])
                rms(oT, None)
                resid(self.gg1, mc)
                for j in range(8):
                    self.tt("pool", sq[:, j, :], xT[:, j, :], xT[:, j, :], ALU.mult, [xT], [sq])
                rms(xT, None)
                for j in range(8):
                    t = tmp[j % 2]
                    self.tt("dve", t[:], xT[:, j, :], rstd[:], ALU.mult, [xT, rstd], [t])
                    self.act(h2[:, j, :], t[:], AF.Identity, [t, self.gs2, self.modT], [h2],
                             scale=self.gs2[:, j, mc:mc + 1], bias=self.modT[:, 24 + j, mc:mc + 1])
                for blk in range(8):
                    w = w1r[wi % 2]
                    wi += 1
                    self.dma(w[:], self.w1_scr[blk], [self.w1_scr], [w], eng="sp" if blk % 2 == 0 else "act")
                    for c4 in range(4):
                        fc = blk * 4 + c4
                        p = self.ps()
                        for kc in range(8):
                            self.mm(p[:], w[:, kc, c4 * 128:(c4 + 1) * 128], h2[:, kc, :], kc == 0, kc == 7, [w, h2], [p])
                        t = tmp[fc % 2]
                        self.act(t[:], p[:], AF.Relu, [p], [t])
                        self.tt("pool" if fc % 2 == 0 else "dve", f[:, fc, :], t[:], t[:], ALU.mult, [t], [f])
                for m in range(8):
                    w = w2r[m % 2]
                    self.dma(w[:], self.w2_scr[m], [self.w2_scr], [w], eng="sp" if m % 2 == 0 else "act")
                    p = self.ps()
                    for fc in range(32):
                        self.mm(p[:], w[:, fc, :], f[:, fc, :], fc == 0, fc == 31, [w, f], [p])
                    self.cp("act", oT[:, m, :], p[:], [p], [oT])
                    self.tt("pool", sq[:, m, :], oT[:, m, :], oT[:, m, :], ALU.mult, [oT], [sq])
                rms(oT, None)
                resid(self.gg2, mc)
                for s in range(4):
                    for half in range(2):
                        p = self.ps()
                        for jj in range(4):
                            j = half * 4 + jj
                            self.tr(p[:, jj * 128:(jj + 1) * 128], xT[:, j, s * 128:(s + 1) * 128], ident[:], [xT, ident], [p])
                        self.cp("act" if half == 0 else "dve", otok[:, s, half * 512:(half + 1) * 512], p[:], [p], [otok])
                if g0 < TS:
                    dst = self.ys[g0:g0 + TC, :].rearrange("(s p) f -> p s f", p=128)
                    self.dma(dst, otok[:], [otok], [self.ys])
                else:
                    dst = self.yp[:, :].rearrange("(s p) f -> p s f", p=128)
                    self.dma(dst, otok[:], [otok], [self.yp])


def _fm(v):
    v = np.asarray(v, np.float32).reshape(-1, 128)
    return np.ascontiguousarray(v.T)


def _pos_embed():
    def sincos(pos, dim):
        omega = (1.0 / (10000.0 ** (np.arange(dim // 2, dtype=np.float32) / np.float32(dim // 2)))).astype(np.float32)
        ang = pos.astype(np.float32)[:, None] * omega[None, :]
        return np.concatenate([np.sin(ang), np.cos(ang)], axis=-1).astype(np.float32)
    rows = TS // 64
    half = D // 2
    e_row = sincos(np.arange(rows), half)
    e_col = sincos(np.arange(64), half)
    emb = np.concatenate([np.broadcast_to(e_row[:, None, :], (rows, 64, half)),
                          np.broadcast_to(e_col[None, :, :], (rows, 64, half))], axis=-1)
    return np.ascontiguousarray(emb.reshape(rows * 64, D).astype(np.float32))


def _consts():
    c = np.zeros((128, NCST), np.float32)
    c[:, C_ID:C_ID + 128] = np.eye(128, dtype=np.float32)
    ob = np.zeros((128, 128), np.float32)
    ob[:64, :64] = 1.0
    ob[64:, 64:] = 1.0
    c[:, C_OB:C_OB + 128] = ob
    s = np.arange(64)[:, None]
    t = np.arange(64)[None, :]
    msi = np.zeros((128, 2, 64), np.float32)
    msi[:64, 0] = (s < t)
    msi[:64, 1] = (s <= t)
    msi[64:, 0] = (s > t)
    msi[64:, 1] = (s >= t)
    c[:, C_MSI:C_MSI + 128] = msi.reshape(128, 128)
    ml = np.zeros((128, 64), np.float32)
    ml[:64] = (t < s)
    ml[64:] = (t > s)
    c[:, C_ML:C_ML + 64] = ml
    ids = np.zeros((128, 64), np.float32)
    ids[:64] = np.eye(64)
    ids[64:] = np.eye(64)
    c[:, C_IDS:C_IDS + 64] = ids
    c[:64, C_MSI1:C_MSI1 + 128] = msi[64:].reshape(64, 128)
    c[:64, C_ML1:C_ML1 + 64] = ml[64:]
    tt_ = np.arange(TT)
    c[:, C_RMF:C_RMF + TT] = (tt_ % 64 != 0).astype(np.float32)[None, :]
    c[:, C_RMB:C_RMB + TT] = (tt_ % 64 != 63).astype(np.float32)[None, :]
    return c


_NC_CACHE = {}


def kernel(x_prompt, x_sample, c, state_rwkv, state_lru, c_ctx, w_mod, b_mod,
           g_pre_mix, g_post_mix, g_pre_mlp, g_post_mlp, w_in,
           rwkv_w0, rwkv_w_up, rwkv_a0, rwkv_a_up, rwkv_g_up, rwkv_k_k, rwkv_k_a, rwkv_r_k,
           rwkv_lnx_g, rwkv_lnx_b, lru_conv_w, lru_conv_b, lru_wa, lru_ba, lru_wx, lru_bx,
           lru_lambda, w_out, w_mlp1, w_mlp2, _debug=False):
    f = lambda a: np.ascontiguousarray(np.asarray(a, np.float32))
    x_prompt, x_sample, c, state_rwkv, state_lru, c_ctx = map(f, (x_prompt, x_sample, c, state_rwkv, state_lru, c_ctx))
    if "nc" not in _NC_CACHE:
        _NC_CACHE["nc"] = K(debug=_debug).build()
    nc = _NC_CACHE["nc"]
    pe = _pos_embed()
    cst = _consts()
    shared = {
        "pe": pe, "cst": cst,
        "w_mod": f(w_mod[0]), "w_in": f(w_in[0]), "w_out": f(w_out[0]), "w1": f(w_mlp1[0]), "w2": f(w_mlp2[0]),
        "wup": f(rwkv_w_up[0]).reshape(128, 512), "aup": f(rwkv_a_up[0]).reshape(128, 512), "gup": f(rwkv_g_up[0]),
        "lwa": f(lru_wa[0]), "lwx": f(lru_wx[0]),
    }
    prm0 = np.zeros((128, NPRM), np.float32)
    prm0[:, P_GPRE:P_GPRE + 8] = _fm(g_pre_mix[0])
    prm0[:, P_GPOST:P_GPOST + 8] = _fm(g_post_mix[0])
    prm0[:, P_GPRE2:P_GPRE2 + 8] = _fm(g_pre_mlp[0])
    prm0[:, P_GPOST2:P_GPOST2 + 8] = _fm(g_post_mlp[0])
    prm0[:, P_BMOD:P_BMOD + 48] = _fm(b_mod[0])
    for d in range(2):
        prm0[:, P_W0 + 4 * d:P_W0 + 4 * d + 4] = _fm(rwkv_w0[0, d])
        prm0[:, P_A0 + 4 * d:P_A0 + 4 * d + 4] = _fm(rwkv_a0[0, d])
        prm0[:, P_BA + 4 * d:P_BA + 4 * d + 4] = _fm(lru_ba[0, d])
        prm0[:, P_BX + 4 * d:P_BX + 4 * d + 4] = _fm(lru_bx[0, d])
        prm0[:, P_LAM + 4 * d:P_LAM + 4 * d + 4] = _fm(lru_lambda[0, d])
    prm0[:, P_KK:P_KK + 4] = _fm(rwkv_k_k[0])
    prm0[:, P_KA:P_KA + 4] = _fm(rwkv_k_a[0])
    prm0[:, P_RK:P_RK + 4] = _fm(np.asarray(rwkv_r_k[0]).reshape(-1))
    prm0[:, P_LNG:P_LNG + 4] = _fm(rwkv_lnx_g[0])
    prm0[:, P_LNB:P_LNB + 4] = _fm(rwkv_lnx_b[0])
    for i in range(4):
        prm0[:, P_CW + 4 * i:P_CW + 4 * i + 4] = _fm(lru_conv_w[0, i])
    prm0[:, P_CB:P_CB + 4] = _fm(lru_conv_b[0])
    in_maps = []
    for i in range(8):
        prm = prm0.copy()
        for d in range(2):
            prm[:, P_H0 + 4 * d:P_H0 + 4 * d + 4] = _fm(state_lru[i, 0, d])
        cT = np.zeros((128, 8, 2), np.float32)
        cT[:, :, 0] = _fm(c[i])
        cT[:, :, 1] = _fm(c_ctx)
        h0 = np.ascontiguousarray(state_rwkv[i, 0].transpose(0, 3, 1, 2)).reshape(128, 512)
        m = dict(shared)
        m.update({"xs": x_sample[i], "xp": np.ascontiguousarray(x_prompt[2 * i:2 * i + 2].reshape(2 * TP, D)),
                  "cT": cT.reshape(128, 16), "h0r": h0, "prm": prm})
        in_maps.append(m)
    res = run_bass_kernel_spmd(nc, in_maps, core_ids=list(range(8)))
    R = res.results
    y_prompt = np.zeros((16, TP, D), np.float32)
    y_sample = np.zeros((8, TS, D), np.float32)
    st_r = np.zeros((16, 1, 2, 8, 64, 64), np.float32)
    st_l = np.zeros((16, 1, 2, 512), np.float32)
    for i in range(8):
        r = R[i]
        y_sample[i] = r["ys"]
        y_prompt[2 * i:2 * i + 2] = r["yp"].reshape(2, TP, D)
        so = r["str_o"].reshape(2, 2, 64, 8, 64)
        st_r[2 * i:2 * i + 2, 0] = so.transpose(0, 1, 3, 4, 2)
        sl = r["stl_o"].reshape(128, 4, 2, 2)
        st_l[2 * i:2 * i + 2, 0] = sl.transpose(2, 3, 1, 0).reshape(2, 2, 512)
    if _debug:
        return (y_prompt, y_sample, st_r, st_l), R
    return (y_prompt, y_sample, st_r, st_l)
```

```python
import contextlib
import numpy as np
import concourse.bass as bass
import concourse.mybir as mybir
from concourse.bass_utils import run_bass_kernel_spmd

F32 = mybir.dt.float32
BF16 = mybir.dt.bfloat16
F32R = mybir.dt.float32r
AF = mybir.ActivationFunctionType
ALU = mybir.AluOpType

D = 1024
TS = 2048
TP = 256
NTOK = TS + 2 * TP
NCH = NTOK // 64
DIN = 2944
DFF = 4096
LAM = float(np.exp(-0.5))
EPS = 1e-6
LNX_EPS = 64e-5
TT = 256
TC = 512
GELU_C = 1.5957691216057308

P_GPRE, P_GPOST, P_GPRE2, P_GPOST2 = 0, 8, 16, 24
P_BMOD = 32
P_W0, P_A0 = 80, 88
P_KK, P_KA, P_RK, P_LNG, P_LNB = 96, 100, 104, 108, 112
P_CW, P_CB = 116, 132
P_BA, P_BX, P_LAM, P_H0 = 136, 144, 152, 160
NPRM = 168
C_ID, C_OB, C_MSI, C_ML, C_IDS, C_RMF, C_RMB = 0, 128, 256, 384, 448, 512, 768
C_MSI1, C_ML1 = 1024, 1152
NCST = 1216


class Buf:
    __slots__ = ("name", "lw", "rd", "excl", "multi", "ws")

    def __init__(self, name=""):
        self.name = name
        self.lw = None
        self.rd = {}
        self.excl = False
        self.multi = False
        self.ws = {}


class TL:
    def __init__(self, t, name=""):
        self.t = t
        self.b = Buf(name)

    def __getitem__(self, k):
        return self.t[k]


class Sched:
    ENGS = ("pe", "act", "dve", "pool", "sp")

    def __init__(self, nc):
        self.nc = nc
        self.streams = {e: [] for e in self.ENGS}
        self.cnt = {}
        self.waited = {e: {} for e in self.ENGS}
        self.n_ops = 0
        self.dma_n = {e: 0 for e in self.ENGS}
        self.NSLOT = {"sp": 44, "act": 44, "pool": 4, "dve": 2, "pe": 2}

    def _deps(self, eng, reads, writes):
        need = {}
        for b in reads:
            if b.multi:
                for s, v in b.ws.items():
                    if need.get(s, 0) < v:
                        need[s] = v
                continue
            if b.lw is not None:
                s, v = b.lw
                if need.get(s, 0) < v:
                    need[s] = v
            if b.excl:
                for s, v in b.rd.items():
                    if s != eng and need.get(s, 0) < v:
                        need[s] = v
        for b in writes:
            if b.multi:
                continue
            if b.lw is not None:
                s, v = b.lw
                if need.get(s, 0) < v:
                    need[s] = v
            for s, v in b.rd.items():
                if need.get(s, 0) < v:
                    need[s] = v
        out = []
        w = self.waited[eng]
        for s, v in need.items():
            if s == "pe" and eng == "pe":
                continue
            if w.get(s, 0) >= v:
                continue
            w[s] = v
            out.append((s, v))
        return out

    def op(self, eng, fn, reads=(), writes=(), dma=False):
        reads = [r.b if isinstance(r, TL) else r for r in reads]
        writes = [r.b if isinstance(r, TL) else r for r in writes]
        waits = self._deps(eng, reads, writes)
        if dma:
            slot = self.dma_n[eng] % self.NSLOT[eng]
            self.dma_n[eng] += 1
            sem = "%s_d%d" % (eng, slot)
            prev = self.cnt.get(sem, 0)
            if prev > 0 and self.waited[eng].get(sem, 0) < prev:
                self.waited[eng][sem] = prev
                waits.append((sem, prev))
        else:
            sem = eng
        inc = 16 if dma else 1
        self.cnt[sem] = self.cnt.get(sem, 0) + inc
        val = self.cnt[sem]
        self.streams[eng].append((waits, fn, sem, inc))
        self.n_ops += 1
        for b in reads:
            if b.rd.get(sem, 0) < val:
                b.rd[sem] = val
        for b in writes:
            if b.multi:
                if b.ws.get(sem, 0) < val:
                    b.ws[sem] = val
                continue
            b.lw = (sem, val)
            b.rd = {}
        return val

    def barrier_on(self, tl):
        if tl.b.lw is None:
            return
        sname, v = tl.b.lw
        for e in ("sp", "act", "pool"):
            if self.waited[e].get(sname, 0) < v:
                self.waited[e][sname] = v
                self.streams[e].append(([(sname, v)], None, None, 0))

    def barrier(self):
        snap = dict(self.cnt)
        for e in self.ENGS:
            waits = []
            for s, v in snap.items():
                if s == "pe" and e == "pe":
                    continue
                if self.waited[e].get(s, 0) < v:
                    self.waited[e][s] = v
                    waits.append((s, v))
            if waits:
                self.streams[e].append((waits, None, None, 0))

    def emit(self):
        nc = self.nc
        sems = {}
        with contextlib.ExitStack() as st:
            for s in self.cnt:
                sems[s] = st.enter_context(nc.semaphore(s))
            block = st.enter_context(nc.Block())
            engmap = {"pe": block.tensor, "act": block.scalar, "dve": block.vector,
                      "pool": block.gpsimd, "sp": block.sync}
            for e in self.ENGS:
                stream = self.streams[e]
                if not stream:
                    continue

                def body(eng, stream=stream):
                    for waits, fn, sem, inc in stream:
                        for s, v in waits:
                            eng.wait_ge(sems[s], v)
                        if fn is not None:
                            fn(eng).then_inc(sems[sem], inc)
                engmap[e](body)


class K:
    def __init__(self, debug=False, stop_after=None):
        self.debug = debug
        self.stop_after = stop_after
        import os
        self.cutk = int(os.environ.get("KCUT", "0"))
        self.cutm = int(os.environ.get("KCUTM", "99"))
        self.ktiles = int(os.environ.get("KTILES", "99"))
        self.kskip = os.environ.get("KSKIP", "").split(",")
        self.nc = bass.Bass("TRN2", target_bir_lowering=False)
        self.S = Sched(self.nc)
        self.es = contextlib.ExitStack()
        self.psr = 0
        self.rr = {}

    def dram(self, name, shape, dt, kind="Internal"):
        t = TL(self.nc.dram_tensor(name, list(shape), dt, kind=kind).ap(), name)
        t.b.multi = True
        return t

    def sb(self, st, name, shape, dt):
        return TL(st.enter_context(self.nc.sbuf_tensor(name, list(shape), dt)), name)

    def sb2(self, st, name, shape, dt):
        t = st.enter_context(self.nc.sbuf_tensor(name, list(shape), dt))
        return [TL(t, name + "_lo"), TL(t, name + "_hi")]

    def ps(self):
        p = self.psum[self.psr % 8]
        self.psr += 1
        return p

    def mm(self, out, lhsT, rhs, start, stop, R, W):
        self.S.op("pe", lambda e: e.matmul(out, lhsT=lhsT, rhs=rhs, start=start, stop=stop), R, W)

    def tr(self, out, in_, ident, R, W):
        self.S.op("pe", lambda e: e.transpose(out, in_, ident), R, W)

    def act(self, out, in_, func, R, W, scale=1.0, bias=None, eng="act"):
        if bias is None:
            self.S.op("act", lambda e: e.activation(out=out, in_=in_, func=func, scale=scale), R, W)
        else:
            self.S.op("act", lambda e: e.activation(out=out, in_=in_, func=func, scale=scale, bias=bias), R, W)

    def tt(self, eng, out, in0, in1, op, R, W):
        self.S.op(eng, lambda e: e.tensor_tensor(out=out, in0=in0, in1=in1, op=op), R, W)

    def tsc(self, eng, out, in0, s1, op0, R, W, s2=None, op1=None):
        if op1 is None:
            self.S.op(eng, lambda e: e.tensor_scalar(out=out, in0=in0, scalar1=s1, scalar2=None, op0=op0), R, W)
        else:
            self.S.op(eng, lambda e: e.tensor_scalar(out=out, in0=in0, scalar1=s1, scalar2=s2, op0=op0, op1=op1), R, W)

    def stt(self, out, in0, scalar, in1, op0, op1, R, W):
        self.S.op("dve", lambda e: e.scalar_tensor_tensor(out=out, in0=in0, scalar=scalar, in1=in1, op0=op0, op1=op1), R, W)

    def cp(self, eng, out, in_, R, W):
        if eng == "act":
            self.S.op("act", lambda e: e.activation(out=out, in_=in_, func=AF.Copy), R, W)
        else:
            self.S.op(eng, lambda e: e.tensor_copy(out=out, in_=in_), R, W)

    def scan(self, out, d0, d1, init, R, W):
        self.S.op("dve", lambda e: e.tensor_tensor_scan(out=out, data0=d0, data1=d1, initial=init,
                                                        op0=ALU.mult, op1=ALU.add), R, W)

    def dma(self, out, in_, R, W, eng="sp"):
        self.S.op(eng, lambda e: e.dma_start(out=out, in_=in_), R, W, dma=True)

    def memset(self, eng, ap, val, W):
        self.S.op(eng, lambda e: e.memset(ap, val), (), W)

    def pick(self, key, engs):
        i = self.rr.get(key, 0)
        self.rr[key] = i + 1
        return engs[i % len(engs)]

    def build(self):
        nc = self.nc
        I = lambda n, s, dt=F32: self.dram(n, s, dt, "ExternalInput")
        O = lambda n, s, dt=F32: self.dram(n, s, dt, "ExternalOutput")
        self.xs = I("xs", [TS, D])
        self.xp = I("xp", [2 * TP, D])
        self.pe = I("pe", [TS, D])
        self.cT = I("cT", [128, 16])
        self.h0r = I("h0r", [128, 512])
        self.prm = I("prm", [128, NPRM])
        self.cst = I("cst", [128, NCST])
        self.w_mod = I("w_mod", [D, 6 * D])
        self.w_in = I("w_in", [D, DIN])
        self.w_out = I("w_out", [D, D])
        self.w1 = I("w1", [D, DFF])
        self.w2 = I("w2", [DFF, D])
        self.wup = I("wup", [128, 512])
        self.aup = I("aup", [128, 512])
        self.gup = I("gup", [128, 512])
        self.lwa = I("lwa", [2, 8, 64, 64])
        self.lwx = I("lwx", [2, 8, 64, 64])
        self.ys = O("ys", [TS, D])
        self.yp = O("yp", [2 * TP, D])
        self.str_o = O("str_o", [2, 128, 512])
        self.stl_o = O("stl_o", [128, 16])
        self.xT_scr = self.dram("xT_scr", [128, 8, NTOK], F32)
        self.xb_scr = self.dram("xb_scr", [128, 4, NTOK], F32)
        self.gate_scr = self.dram("gate_scr", [128, 4, NTOK], BF16)
        self.g_scr = self.dram("g_scr", [128, 4, NTOK], BF16)
        self.bon_scr = self.dram("bon_scr", [128, 4, NTOK], BF16)
        self.y_scr = self.dram("y_scr", [128, 8, NTOK], BF16)
        self.ytok_scr = self.dram("ytok_scr", [2, NTOK, 512], F32)
        for n in ("art", "rrt", "ttt", "akt", "mrbt", "mrkt", "bh", "kh"):
            setattr(self, n + "_scr", self.dram(n + "_scr", [NCH, 128, 512], BF16))
        self.vt_scr = self.dram("vt_scr", [NCH, 64, 512], BF16)
        self.pend_scr = self.dram("pend_scr", [NCH, 128, 512], F32)
        self.w1_scr = self.dram("w1_scr", [8, 128, 8, 512], BF16)
        self.w2_scr = self.dram("w2_scr", [8, 128, 32, 128], BF16)
        if self.debug:
            self.dbg = {}

        with self.es as st0:
            self.psum = [TL(st0.enter_context(nc.psum_tensor("ps%d" % i, [128, 512], F32)), "ps%d" % i)
                         for i in range(8)]
            for p_ in self.psum:
                p_.b.excl = True
            self.prm_t = self.sb(st0, "prm_t", [128, NPRM], F32)
            self.cst_t = self.sb(st0, "cst_t", [128, NCST], F32)
            self.modT = self.sb(st0, "modT", [128, 48, 2], F32)
            self.gs1 = self.sb(st0, "gs1", [128, 8, 2], F32)
            self.gs2 = self.sb(st0, "gs2", [128, 8, 2], F32)
            self.gg1 = self.sb(st0, "gg1", [128, 8, 2], F32)
            self.gg2 = self.sb(st0, "gg2", [128, 8, 2], F32)
            self.ident = self.sb(st0, "ident", [128, 128], F32)
            self.ones_bf = self.sb(st0, "ones_bf", [128, 128], BF16)
            self.oblk_bf = self.sb(st0, "oblk_bf", [128, 128], BF16)
            self.epsT = self.sb(st0, "epsT", [128, 2], F32)
            self.misc = self.sb(st0, "misc", [128, 32], F32)
            self.stl_t = self.sb(st0, "stl_t", [128, 16], F32)
            for nm, fn in (("p0", self.phase0), ("pA", self.phaseA), ("pB", self.phaseB), ("pC", self.phaseC)):
                fn()
                self.S.barrier()
                if self.stop_after == nm:
                    break
            self.S.emit()
        return nc

    def dump(self, name, src_ap, shape, dt, R):
        o = self.dram("dbg_" + name, shape, dt, "ExternalOutput")
        self.dma(o[:], src_ap, R, [o])

    def phase0(self):
        nc = self.nc
        prm, cst = self.prm_t, self.cst_t
        self.dma(prm[:], self.prm[:], [], [prm])
        self.dma(cst[:], self.cst[:], [], [cst])
        self.cp("dve", self.ident[:], cst[:, C_ID:C_ID + 128], [cst], [self.ident])
        self.cp("dve", self.oblk_bf[:], cst[:, C_OB:C_OB + 128], [cst], [self.oblk_bf])
        self.memset("dve", self.ones_bf[:], 1.0, [self.ones_bf])
        self.memset("dve", self.epsT[:, 0:1], EPS, [self.epsT])
        self.memset("dve", self.epsT[:, 1:2], LNX_EPS, [self.epsT])
        self.tsc("dve", self.misc[:, 0:4], prm[:, P_KA:P_KA + 4], -1.0, ALU.mult, [prm], [self.misc], 1.0, ALU.add)
        with contextlib.ExitStack() as st:
            scT = self.sb(st, "scT", [128, 16], F32)
            cT = self.sb(st, "cT_t", [128, 16], F32)
            wm = [self.sb(st, "wm%d" % i, [128, 8, 512], F32) for i in range(2)]
            tmp = self.sb(st, "lam_tmp", [128, 8], F32)
            self.dma(cT[:], self.cT[:], [], [cT])
            self.act(scT[:], cT[:], AF.Silu, [cT], [scT])
            self.act(tmp[:], prm[:, P_LAM:P_LAM + 8], AF.Exp, [prm], [tmp], scale=-1.0)
            self.act(tmp[:], tmp[:], AF.Ln, [tmp], [tmp], bias=1.0)
            self.tsc("dve", self.misc[:, 4:12], tmp[:], -8.0, ALU.mult, [tmp], [self.misc])
            self.tsc("dve", self.misc[:, 12:20], tmp[:], -16.0, ALU.mult, [tmp], [self.misc])
            wsrc = self.w_mod[:].rearrange("(kc p) n -> p kc n", p=128)
            sc3 = scT[:].rearrange("p (k c) -> p k c", c=2)
            for blk in range(12):
                w = wm[blk % 2]
                self.dma(w[:], wsrc[:, :, blk * 512:(blk + 1) * 512], [], [w], eng="sp" if blk % 2 == 0 else "act")
                p = self.ps()
                for m in range(4):
                    for kc in range(8):
                        self.mm(p[:, 2 * m:2 * m + 2], w[:, kc, m * 128:(m + 1) * 128], sc3[:, kc, :],
                                kc == 0, kc == 7, [w, scT], [p])
                for m in range(4):
                    mi = blk * 4 + m
                    self.tsc("dve", self.modT[:, mi, :], p[:, 2 * m:2 * m + 2], prm[:, P_BMOD + mi:P_BMOD + mi + 1],
                             ALU.add, [p, prm], [self.modT])
            m3 = self.modT
            for (dst, sc_off, g_off, one) in ((self.gs1, 8, P_GPRE, 1.0), (self.gs2, 32, P_GPRE2, 1.0),
                                              (self.gg1, 16, P_GPOST, 0.0), (self.gg2, 40, P_GPOST2, 0.0)):
                for c in range(2):
                    self.tsc("dve", dst[:, :, c], m3[:, sc_off:sc_off + 8, c], one, ALU.add, [m3], [dst])
                    self.tt("dve", dst[:, :, c], dst[:, :, c], prm[:, g_off:g_off + 8], ALU.mult, [dst, prm], [dst])
            if self.debug:
                self.dump("modT", self.modT[:], [128, 48, 2], F32, [self.modT])
                self.dump("gs1", self.gs1[:], [128, 8, 2], F32, [self.gs1])

    def load_cast(self, st, dst_ap, dst_tl, src_ap, shape, tag):
        key = "stg_" + tag
        if not hasattr(self, key):
            setattr(self, key, [self.sb(st, "%s%d" % (key, i), shape, F32) for i in range(2)])
        ring = getattr(self, key)
        s = ring[self.rr.get(key, 0) % 2]
        self.rr[key] = self.rr.get(key, 0) + 1
        self.dma(s[:], src_ap, [], [s], eng="sp")
        eng = self.pick("castE", ["act", "pool"])
        self.cp(eng, dst_ap, s[:], [s], [dst_tl])

    def phaseA(self):
        nc = self.nc
        prm, cst = self.prm_t, self.cst_t
        with contextlib.ExitStack() as st:
            win = self.sb(st, "win", [128, 8, DIN], BF16)
            win.b.multi = True
            wsrc = self.w_in[:].rearrange("(kc p) n -> p kc n", p=128)
            wup = self.sb(st, "wup_t", [128, 512], BF16)
            aup = self.sb(st, "aup_t", [128, 512], BF16)
            gup = self.sb(st, "gup_t", [128, 512], BF16)
            with contextlib.ExitStack() as st2:
                stgA = [self.sb(st2, "stgA%d" % i, [128, 8, 256], F32) for i in range(2)]
                nb = 0
                for c0 in range(0, DIN, 256):
                    cw = min(256, DIN - c0)
                    s_ = stgA[nb % 2]
                    self.dma(s_[:, :, 0:cw], wsrc[:, :, c0:c0 + cw], [], [s_], eng="sp" if nb % 2 == 0 else "act")
                    self.cp("act" if nb % 2 == 0 else "pool", win[:, :, c0:c0 + cw], s_[:, :, 0:cw], [s_], [win])
                    nb += 1
                for i, (src, dstt) in enumerate(((self.wup, wup), (self.aup, aup), (self.gup, gup))):
                    s_ = stgA[nb % 2]
                    nb += 1
                    s2 = s_[:].rearrange("p a b -> p (a b)")[:, 0:512]
                    self.dma(s2, src[:], [], [s_])
                    self.cp("dve", dstt[:], s2, [s_], [dstt])
            self.S.barrier()
            mSI = cst[:, C_MSI:C_MSI + 128].rearrange("p (q t) -> p q t", q=2)
            mL = cst[:, C_ML:C_ML + 64]
            idS = cst[:, C_IDS:C_IDS + 64]

            xin = self.sb(st, "xin", [128, 2, D], F32)
            xT = self.sb(st, "xT", [128, 8, TT], F32)
            sq = self.sb(st, "sq", [128, 8, TT], BF16)
            hT = self.sb(st, "hT", [128, 8, TT], BF16)
            rstd = self.sb(st, "rstd", [128, TT], F32)
            tmpA = [self.sb(st, "tmpA%d" % i, [128, TT], F32) for i in range(2)]
            rT = self.sb(st, "rT", [128, 4, TT], F32)
            kT = self.sb(st, "kT", [128, 4, TT], F32)
            vT = self.sb(st, "vT", [128, 4, TT], F32)
            xw = self.sb(st, "xw", [128, TT], BF16)
            xa = self.sb(st, "xa", [128, TT], BF16)
            xg = self.sb(st, "xg", [128, TT], BF16)
            xbT = self.sb(st, "xbT", [128, 4, TT], F32)
            gtmp = [self.sb(st, "gtmp%d" % i, [128, TT], F32) for i in range(3)]
            gate = self.sb(st, "gate", [128, 4, TT], BF16)
            gT = self.sb(st, "gT", [128, 4, TT], BF16)
            kkn = self.sb(st, "kkn", [128, 4, TT], F32)
            ksum = self.sb(st, "ksum", [128, 4, TT], F32)
            bon = self.sb(st, "bon", [128, 4, TT], BF16)
            sg = self.sb(st, "sg", [128, 4, TT], F32)
            cs = self.sb(st, "cs", [128, 4, TT], F32)
            E1 = self.sb(st, "E1", [128, 4, TT], F32)
            ad = self.sb(st, "ad", [128, 4, TT], F32)
            wk1 = self.sb(st, "wk1", [128, 4, TT], F32)
            wk2 = self.sb(st, "wk2", [128, 4, TT], F32)
            NC4 = TT // 64
            AR = [self.sb(st, "AR%d" % d, [128, 4, NC4, 2, 64], BF16) for d in range(2)]
            BK = [self.sb(st, "BK%d" % d, [128, 4, NC4, 2, 64], BF16) for d in range(2)]
            PEb1 = self.sb(st, "PEb", [128, 4, NC4, 64], F32)
            PEb = [PEb1, PEb1]
            tokB = [self.sb(st, "tokB%d" % d, [128, 2, 8, 64], BF16) for d in range(2)]
            tokK = [self.sb(st, "tokK%d" % d, [128, 2, 8, 64], BF16) for d in range(2)]
            tokV = self.sb(st, "tokV", [128, 2, 8, 64], BF16)
            MRBd = [self.sb(st, "MRBd%d" % d, [64, 8, 64], BF16) for d in range(2)]
            Lt0 = self.sb(st, "Lt0", [64, 8, 64], F32R)
            Tfin = self.sb(st, "Tfin", [64, 8, 64], BF16)
            SCk = self.sb(st, "SCk", [128, 2, 8, 64], BF16)
            Lm = [self.sb(st, "Lm%d" % i, [64, 8, 64], F32R) for i in range(2)]
            Ltm = [self.sb(st, "Ltm%d" % i, [64, 8, 64], F32R) for i in range(2)]
            ILm = self.sb(st, "ILm", [64, 8, 64], F32R)
            Ttm = [self.sb(st, "Ttm%d" % i, [64, 8, 64], F32R) for i in range(2)]

            ones_bf, oblk, ident = self.ones_bf, self.oblk_bf, self.ident
            tiles = [(0, t0, True, 0) for t0 in range(0, TS, TT)] + [(1, TS, False, 1), (2, TS + TP, False, 1)]
            for (seq, g0, is_s, mc) in tiles[:self.ktiles]:
                if is_s:
                    src = self.xs[g0:g0 + TT, :].rearrange("(s p) f -> p s f", p=128)
                else:
                    l0 = g0 - TS
                    src = self.xp[l0:l0 + TT, :].rearrange("(s p) f -> p s f", p=128)
                self.dma(xin[:], src, [], [xin])
                if is_s:
                    petv = xT[:].rearrange("p a b -> p (a b)").rearrange("p (s f) -> p s f", s=2)
                    self.dma(petv, self.pe[g0:g0 + TT, :].rearrange("(s p) f -> p s f", p=128), [], [xT], eng="act")
                    self.tt("pool", xin[:], xin[:], petv, ALU.add, [xin, xT], [xin])
                if self.cutk == 1:
                    return
                for j in range(8):
                    p = self.ps()
                    for s in range(2):
                        self.tr(p[:, s * 128:(s + 1) * 128], xin[:, s, j * 128:(j + 1) * 128], ident[:], [xin, ident], [p])
                    self.cp("act", xT[:, j, :], p[:, 0:TT], [p], [xT])
                    self.tt("pool", sq[:, j, :], xT[:, j, :], xT[:, j, :], ALU.mult, [xT], [sq])
                self.dma(self.xT_scr[:, :, g0:g0 + TT], xT[:], [xT], [self.xT_scr])
                if self.cutk == 2:
                    return
                p = self.ps()
                for j in range(8):
                    self.mm(p[:, 0:TT], ones_bf[:], sq[:, j, :], j == 0, j == 7, [ones_bf, sq], [p])
                self.act(rstd[:], p[:, 0:TT], AF.Sqrt, [p, self.epsT], [rstd], scale=1.0 / D, bias=self.epsT[:, 0:1])
                self.S.op("dve", lambda e: e.reciprocal(out=rstd[:], in_=rstd[:]), [rstd.b], [rstd.b])
                for j in range(8):
                    t = tmpA[j % 2]
                    self.tt("dve", t[:], xT[:, j, :], rstd[:], ALU.mult, [xT, rstd], [t])
                    self.act(hT[:, j, :], t[:], AF.Identity, [t, self.gs1, self.modT], [hT],
                             scale=self.gs1[:, j, mc:mc + 1], bias=self.modT[:, j, mc:mc + 1])
                if self.cutk == 3:
                    return
                for m in range(23):
                    if m >= self.cutm:
                        break
                    p = self.ps()
                    for kc in range(8):
                        self.mm(p[:, 0:TT], win[:, kc, m * 128:(m + 1) * 128], hT[:, kc, :], kc == 0, kc == 7, [win, hT], [p])
                    pz = p[:, 0:TT]
                    if m < 4:
                        self.cp("act", rT[:, m, :], pz, [p], [rT])
                    elif m < 8:
                        self.cp("act", kT[:, m - 4, :], pz, [p], [kT])
                    elif m < 12:
                        self.cp("act", vT[:, m - 8, :], pz, [p], [vT])
                    elif m == 12:
                        self.act(xw[:], pz, AF.Tanh, [p], [xw])
                    elif m == 13:
                        self.cp("act", xa[:], pz, [p], [xa])
                    elif m == 14:
                        self.act(xg[:], pz, AF.Sigmoid, [p], [xg])
                    elif m < 19:
                        self.cp("act", xbT[:, m - 15, :], pz, [p], [xbT])
                    else:
                        j = m - 19
                        g0_, g1_, g2_ = gtmp
                        self.cp("act", g0_[:], pz, [p], [g0_])
                        self.tt("pool", g1_[:], g0_[:], g0_[:], ALU.mult, [g0_], [g1_])
                        self.tsc("dve", g1_[:], g1_[:], 0.044715, ALU.mult, [g1_], [g1_], 1.0, ALU.add)
                        self.tt("dve", g1_[:], g1_[:], g0_[:], ALU.mult, [g1_, g0_], [g1_])
                        self.act(g2_[:], g1_[:], AF.Sigmoid, [g1_], [g2_], scale=GELU_C)
                        self.tt("pool", gate[:, j, :], g0_[:], g2_[:], ALU.mult, [g0_, g2_], [gate])
                self.dma(self.xb_scr[:, :, g0:g0 + TT], xbT[:], [xbT], [self.xb_scr])
                if self.debug and g0 == 0:
                    self.dump("hT", hT[:], [128, 8, TT], BF16, [hT])
                    self.dump("rT", rT[:], [128, 4, TT], F32, [rT])
                    self.dump("vT", vT[:], [128, 4, TT], F32, [vT])
                    self.dump("xbT", xbT[:], [128, 4, TT], F32, [xbT])
                    self.dump("gate", gate[:], [128, 4, TT], BF16, [gate])
                self.dma(self.gate_scr[:, :, g0:g0 + TT], gate[:], [gate], [self.gate_scr])
                if self.cutk == 4:
                    return
                for j in range(4):
                    p = self.ps()
                    self.mm(p[:, 0:TT], gup[:, j * 128:(j + 1) * 128], xg[:], True, True, [gup, xg], [p])
                    self.cp("act", gT[:, j, :], p[:, 0:TT], [p], [gT])
                self.dma(self.g_scr[:, :, g0:g0 + TT], gT[:], [gT], [self.g_scr])
                for j in range(4):
                    self.tsc("dve", kkn[:, j, :], kT[:, j, :], prm[:, P_KK + j:P_KK + j + 1], ALU.mult, [kT, prm], [kkn])
                    self.tt("pool", sq[:, j, :], kkn[:, j, :], kkn[:, j, :], ALU.mult, [kkn], [sq])
                for j in range(4):
                    p = self.ps()
                    self.mm(p[:, 0:TT], oblk[:], sq[:, j, :], True, True, [oblk, sq], [p])
                    t = tmpA[j % 2]
                    self.act(t[:], p[:, 0:TT], AF.Sqrt, [p], [t])
                    self.tsc("dve", t[:], t[:], 1e-12, ALU.max, [t], [t])
                    self.S.op("dve", lambda e, t=t: e.reciprocal(out=t[:], in_=t[:]), [t.b], [t.b])
                    self.tt("dve", kkn[:, j, :], kkn[:, j, :], t[:], ALU.mult, [kkn, t], [kkn])
                for s in range(2):
                    p = self.ps()
                    for j in range(4):
                        self.tr(p[:, j * 128:(j + 1) * 128], vT[:, j, s * 128:(s + 1) * 128], ident[:], [vT, ident], [p])
                    self.cp("act", tokV[:, s, :, :].rearrange("p h k -> p (h k)"), p[:], [p], [tokV])
                c0 = g0 // 64
                for s in range(2):
                    dst = self.vt_scr[c0 + 2 * s:c0 + 2 * s + 2, :, :].rearrange("c s f -> (c s) f")
                    self.dma(dst, tokV[:, s, :, :].rearrange("p h k -> p (h k)"), [tokV], [self.vt_scr])
                if self.cutk == 5:
                    return
                for d in range(2):
                    for j in range(4):
                        p = self.ps()
                        self.mm(p[:, 0:TT], wup[d * 64:(d + 1) * 64, j * 128:(j + 1) * 128], xw[d * 64:(d + 1) * 64, :],
                                True, True, [wup, xw], [p])
                        self.act(sg[:, j, :], p[:, 0:TT], AF.Sigmoid, [p, prm], [sg],
                                 bias=prm[:, P_W0 + 4 * d + j:P_W0 + 4 * d + j + 1])
                        if d == 0:
                            self.scan(cs[:, j, :], cst[:, C_RMF:C_RMF + TT], sg[:, j, :], 0.0, [cst, sg], [cs])
                        else:
                            self.scan(cs[:, j, ::-1], cst[:, C_RMB:C_RMB + TT][:, ::-1], sg[:, j, ::-1], 0.0, [cst, sg], [cs])
                    for j in range(4):
                        p = self.ps()
                        self.mm(p[:, 0:TT], aup[d * 64:(d + 1) * 64, j * 128:(j + 1) * 128], xa[d * 64:(d + 1) * 64, :],
                                True, True, [aup, xa], [p])
                        self.act(ad[:, j, :], p[:, 0:TT], AF.Sigmoid, [p, prm], [ad],
                                 bias=prm[:, P_A0 + 4 * d + j:P_A0 + 4 * d + j + 1])
                    self.tt("pool", sg[:], cs[:], sg[:], ALU.subtract, [cs, sg], [sg])
                    self.act(E1[:], cs[:], AF.Exp, [cs], [E1], scale=-LAM)
                    self.act(cs[:], cs[:], AF.Exp, [cs], [cs], scale=LAM)
                    self.act(sg[:], sg[:], AF.Exp, [sg], [sg], scale=-LAM)
                    E2, E3 = cs, sg
                    ar5 = AR[d]
                    bk5 = BK[d]
                    v4 = lambda tl: tl[:].rearrange("p j (c t) -> p j c t", t=64)
                    self.stt(ar5[:, :, :, 0, :], v4(kkn), -1.0, v4(E3), ALU.mult, ALU.mult, [kkn, E3], [ar5])
                    self.tt("pool", ar5[:, :, :, 1, :], v4(rT), v4(E1), ALU.mult, [rT, E1], [ar5])
                    self.tt("dve", wk1[:], kkn[:], ad[:], ALU.mult, [kkn, ad], [wk1])
                    self.tt("dve", wk1[:], wk1[:], E2[:], ALU.mult, [wk1, E2], [wk1])
                    self.cp("pool", bk5[:, :, :, 0, :], v4(wk1), [wk1], [bk5])
                    for j in range(4):
                        self.tsc("dve", wk2[:, j, :], ad[:, j, :], prm[:, P_KA + j:P_KA + j + 1], ALU.mult, [ad, prm, self.misc], [wk2],
                                 self.misc[:, j:j + 1], ALU.add)
                    self.tt("pool", wk2[:], wk2[:], kT[:], ALU.mult, [wk2, kT], [wk2])
                    if d == 0:
                        self.cp("pool", ksum[:], wk2[:], [wk2], [ksum])
                    else:
                        self.tt("pool", ksum[:], ksum[:], wk2[:], ALU.add, [ksum, wk2], [ksum])
                    self.tt("dve", wk2[:], wk2[:], E2[:], ALU.mult, [wk2, E2], [wk2])
                    self.cp("pool", bk5[:, :, :, 1, :], v4(wk2), [wk2], [bk5])
                    te = 63 if d == 0 else 0
                    pend_b = v4(E1)[:, :, :, te:te + 1].to_broadcast([128, 4, NC4, 64])
                    self.cp("pool", PEb[d][:], pend_b, [E1], [PEb[d]])
                    self.tt("dve", v4(wk1), v4(wk1), PEb[d][:], ALU.mult, [wk1, PEb[d]], [wk1])
                    self.tt("dve", v4(wk2), v4(wk2), PEb[d][:], ALU.mult, [wk2, PEb[d]], [wk2])
                    for (srcw, tokX, scr) in ((wk1, tokB[d], self.bh_scr), (wk2, tokK[d], self.kh_scr)):
                        for s in range(2):
                            p = self.ps()
                            for j in range(4):
                                self.tr(p[:, j * 128:(j + 1) * 128], srcw[:, j, s * 128:(s + 1) * 128], ident[:], [srcw, ident], [p])
                            self.cp("act", tokX[:, s, :, :].rearrange("p h k -> p (h k)"), p[:], [p], [tokX])
                            for cc in range(2):
                                self.dma(scr[c0 + 2 * s + cc, d * 64:(d + 1) * 64, :],
                                         tokX[cc * 64:(cc + 1) * 64, s, :, :].rearrange("p h k -> p (h k)"),
                                         [tokX], [scr])
                    for cl in range(NC4):
                        c = c0 + cl
                        for hp in range(2):
                            for (q, scr) in ((0, self.art_scr), (1, self.rrt_scr)):
                                dst = scr[c, d * 64:(d + 1) * 64, :].rearrange("k (j hp t) -> k j hp t", hp=2, t=64)[:, :, hp, :]
                                self.dma(dst, ar5[hp * 64:(hp + 1) * 64, :, cl, q, :], [ar5], [scr], eng="sp")
                            dst = self.pend_scr[c, d * 64:(d + 1) * 64, :].rearrange("k (j hp t) -> k j hp t", hp=2, t=64)[:, :, hp, :]
                            self.dma(dst, PEb[d][hp * 64:(hp + 1) * 64, :, cl, :], [PEb[d]], [self.pend_scr], eng="sp")
                if self.cutk == 6:
                    return
                for j in range(4):
                    self.stt(sq[:, j, :], rT[:, j, :], prm[:, P_RK + j:P_RK + j + 1], ksum[:, j, :], ALU.mult, ALU.mult,
                             [rT, prm, ksum], [sq])
                    p = self.ps()
                    self.mm(p[:, 0:TT], oblk[:], sq[:, j, :], True, True, [oblk, sq], [p])
                    self.tt("dve", bon[:, j, :], p[:, 0:TT], vT[:, j, :], ALU.mult, [p, vT], [bon])
                self.dma(self.bon_scr[:, :, g0:g0 + TT], bon[:], [bon], [self.bon_scr])
                if self.cutk == 7:
                    return
                mS1 = cst[0:64, C_MSI1:C_MSI1 + 128].rearrange("p (q t) -> p q t", q=2)
                mL1 = cst[0:64, C_ML1:C_ML1 + 64]
                id64 = idS[0:64]
                for cl in range(NC4):
                    c = c0 + cl
                    for hp in range(2):
                        p = self.ps()
                        for j in range(4):
                            for d in range(2):
                                self.mm(p[d * 64:(d + 1) * 64, j * 128:(j + 1) * 128],
                                        BK[d][hp * 64:(hp + 1) * 64, j, cl, 1, :],
                                        AR[d][hp * 64:(hp + 1) * 64, j, cl, :, :].rearrange("p q t -> p (q t)"),
                                        True, True, [BK[d], AR[d]], [p])
                        self.tt("dve", SCk[:, :, hp::2, :].rearrange("p q h t -> p h q t"),
                                p[:].rearrange("p (h q t) -> p h q t", q=2, t=64),
                                mSI.unsqueeze(1).to_broadcast([128, 4, 2, 64]), ALU.mult, [p, cst], [SCk])
                    self.dma(self.akt_scr[c], SCk[:, 0, :, :].rearrange("p h s -> p (h s)"), [SCk], [self.akt_scr], eng="act")
                    self.dma(self.mrkt_scr[c], SCk[:, 1, :, :].rearrange("p h s -> p (h s)"), [SCk], [self.mrkt_scr], eng="act")
                    for d in range(2):
                        msk = mSI[0:64] if d == 0 else mS1
                        mskL = mL[0:64] if d == 0 else mL1
                        L0 = Lm[0]
                        for hp in range(2):
                            p = self.ps()
                            for j in range(4):
                                self.mm(p[0:64, j * 128:(j + 1) * 128],
                                        BK[d][hp * 64:(hp + 1) * 64, j, cl, 0, :],
                                        AR[d][hp * 64:(hp + 1) * 64, j, cl, :, :].rearrange("p q t -> p (q t)"),
                                        True, True, [BK[d], AR[d]], [p])
                            p4 = p[0:64, :].rearrange("p (h q t) -> p h q t", q=2, t=64)
                            self.tt("dve", Lt0[:, hp::2, :], p4[:, :, 0, :], msk[:, 0, :].unsqueeze(1).to_broadcast([64, 4, 64]),
                                    ALU.mult, [p, cst], [Lt0])
                            self.tt("dve", MRBd[d][:, hp::2, :], p4[:, :, 1, :], msk[:, 1, :].unsqueeze(1).to_broadcast([64, 4, 64]),
                                    ALU.mult, [p, cst], [MRBd[d]])
                            p2 = self.ps()
                            for j in range(4):
                                self.mm(p2[0:64, j * 64:(j + 1) * 64],
                                        AR[d][hp * 64:(hp + 1) * 64, j, cl, 0, :], BK[d][hp * 64:(hp + 1) * 64, j, cl, 0, :],
                                        True, True, [AR[d], BK[d]], [p2])
                            self.tt("dve", L0[:, hp::2, :], p2[0:64, 0:256].rearrange("p (h s) -> p h s", s=64),
                                    mskL.unsqueeze(1).to_broadcast([64, 4, 64]), ALU.mult, [p2, cst], [L0])
                        self.dma(self.mrbt_scr[c, d * 64:(d + 1) * 64, :], MRBd[d][:].rearrange("p h s -> p (h s)"),
                                 [MRBd[d]], [self.mrbt_scr], eng="act")
                        T0 = Ttm[0]
                        self.tt("pool", T0[:], Lt0[:].bitcast(F32), id64.unsqueeze(1).to_broadcast([64, 8, 64]), ALU.add,
                                [Lt0, cst], [T0])
                        L_prev, Tt_prev, Lt_prev = L0, T0, Lt0
                        for lev in range(1, 6):
                            L_new, Lt_new, Tt_new = Lm[lev % 2], Ltm[lev % 2], Ttm[lev % 2]
                            pA = self.ps()
                            for h in range(8):
                                self.mm(pA[0:64, h * 64:(h + 1) * 64], Lt_prev[:, h, :], L_prev[:, h, :], True, True,
                                        [Lt_prev, L_prev], [pA])
                            if lev < 5:
                                pB = self.ps()
                                for h in range(8):
                                    self.mm(pB[0:64, h * 64:(h + 1) * 64], L_prev[:, h, :], Lt_prev[:, h, :], True, True,
                                            [Lt_prev, L_prev], [pB])
                            self.tt("dve", ILm[:], pA[0:64, :].rearrange("p (h s) -> p h s", s=64),
                                    id64.unsqueeze(1).to_broadcast([64, 8, 64]), ALU.add, [pA, cst], [ILm])
                            if lev < 5:
                                self.cp("act", L_new[:].rearrange("p h s -> p (h s)"), pA[0:64, :], [pA], [L_new])
                                self.cp("act", Lt_new[:].rearrange("p h s -> p (h s)"), pB[0:64, :], [pB], [Lt_new])
                            pC = self.ps()
                            for h in range(8):
                                self.mm(pC[0:64, h * 64:(h + 1) * 64], ILm[:, h, :], Tt_prev[:, h, :], True, True,
                                        [ILm, Tt_prev], [pC])
                            if lev < 5:
                                self.cp("act", Tt_new[:].rearrange("p h s -> p (h s)"), pC[0:64, :], [pC], [Tt_new])
                            else:
                                self.cp("act", Tfin[:].rearrange("p h s -> p (h s)"), pC[0:64, :], [pC], [Tfin])
                            L_prev, Tt_prev, Lt_prev = L_new, Tt_new, Lt_new
                        self.dma(self.ttt_scr[c, d * 64:(d + 1) * 64, :], Tfin[:].rearrange("p h s -> p (h s)"),
                                 [Tfin], [self.ttt_scr], eng="act")

    def phaseB(self):
        with contextlib.ExitStack() as st:
            side = [self.gen_c0(st), self.phaseB_lru(st)]
            post = self.phaseB_post(st)
            next(post)
            done = np.zeros((2, NCH), bool)
            posted = [False] * (NTOK // 128)
            si = 0
            for info in self.phaseB_chain(st):
                for (d, c) in info:
                    done[d, c] = True
                for _ in range(2):
                    if side:
                        g = side[si % len(side)]
                        si += 1
                        try:
                            next(g)
                        except StopIteration:
                            side.remove(g)
                for b in range(NTOK // 128):
                    if not posted[b] and done[:, 2 * b:2 * b + 2].all():
                        posted[b] = True
                        post.send(b)
            for g in side:
                for _ in g:
                    pass
            for b in range(NTOK // 128):
                if not posted[b]:
                    post.send(b)
            if self.debug:
                self.S.barrier()
                self.dump("yscr", self.y_scr[:], [128, 8, NTOK], BF16, [self.y_scr])

    def gen_c0(self, st):
        stg = [self.sb(st, "stgC%d" % i, [128, 8, 512], F32) for i in range(2)]
        wb = [self.sb(st, "wbC%d" % i, [128, 8, 512], BF16) for i in range(2)]
        w1src = self.w1[:].rearrange("(kc p) n -> p kc n", p=128)
        for blk in range(8):
            s, o = stg[blk % 2], wb[blk % 2]
            self.dma(s[:], w1src[:, :, blk * 512:(blk + 1) * 512], [], [s])
            self.cp("pool", o[:], s[:], [s], [o])
            self.dma(self.w1_scr[blk], o[:], [o], [self.w1_scr], eng="act")
            yield
        w2src = self.w2[:].rearrange("(fc p) n -> p fc n", p=128)
        for m in range(8):
            s, o = stg[m % 2], wb[m % 2]
            s4 = s[:].rearrange("p k (a b) -> p (k a) b", b=128)
            o4 = o[:].rearrange("p k (a b) -> p (k a) b", b=128)
            self.dma(s4, w2src[:, :, m * 128:(m + 1) * 128], [], [s])
            self.cp("pool", o[:], s[:], [s], [o])
            self.dma(self.w2_scr[m], o4, [o], [self.w2_scr], eng="act")
            yield

    def phaseB_lru(self, st):
        prm, cst, misc = self.prm_t, self.cst_t, self.misc
        if True:
            wbd32 = self.sb(st, "wbd32", [128, 16, 128], F32)
            wbd = self.sb(st, "wbd", [128, 16, 128], BF16)
            self.memset("pool", wbd32[:], 0.0, [wbd32])
            self.S.barrier_on(wbd32)
            wbd32.b.multi = True
            for gi, src in enumerate((self.lwa, self.lwx)):
                for d in range(2):
                    for j in range(4):
                        for hb in range(2):
                            self.dma(wbd32[hb * 64:(hb + 1) * 64, (gi * 2 + d) * 4 + j, hb * 64:(hb + 1) * 64],
                                     src[d, 2 * j + hb], [], [wbd32])
            self.cp("dve", wbd[:], wbd32[:], [wbd32], [wbd])
            TM = TS
            xbp = self.sb(st, "xbp", [128, TM + 4], F32)
            xc = self.sb(st, "xc", [128, TM], F32)
            xcb = self.sb(st, "xcb", [128, TM], BF16)
            gt = self.sb(st, "gt_l", [128, TM], BF16)
            a_t = self.sb(st, "a_t", [128, TM], F32)
            bx_t = self.sb(st, "bx_t", [128, TM], F32)
            s_t = self.sb(st, "s_t", [128, TM], F32)
            hs = [self.sb(st, "hs%d" % d, [128, TM], F32) for d in range(2)]
            yb = self.sb(st, "yb", [128, TM], BF16)
            for (seq, g0, T) in ((0, 0, TS), (1, TS, TP), (2, TS + TP, TP)):
                for j in range(4):
                    self.memset("pool", xbp[:, 0:2], 0.0, [xbp])
                    self.memset("pool", xbp[:, T + 2:T + 4], 0.0, [xbp])
                    self.dma(xbp[:, 2:T + 2], self.xb_scr[:, j, g0:g0 + T], [self.xb_scr], [xbp])
                    self.dma(gt[:, 0:T], self.gate_scr[:, j, g0:g0 + T], [self.gate_scr], [gt], eng="act")
                    cw = lambda i: prm[:, P_CW + 4 * i + j:P_CW + 4 * i + j + 1]
                    self.act(xc[:, 0:T], xbp[:, 0:T], AF.Identity, [xbp, prm], [xc], scale=cw(0), bias=prm[:, P_CB + j:P_CB + j + 1])
                    for i in range(1, 4):
                        self.stt(xc[:, 0:T], xbp[:, i:i + T], cw(i), xc[:, 0:T], ALU.mult, ALU.add, [xbp, prm, xc], [xc])
                    self.cp("pool", xcb[:, 0:T], xc[:, 0:T], [xc], [xcb])
                    yield
                    for d in range(2):
                        for t0 in range(0, T, 512):
                            tw = min(512, T - t0)
                            p = self.ps()
                            self.mm(p[:, 0:tw], wbd[:, (0 * 2 + d) * 4 + j, :], xcb[:, t0:t0 + tw], True, True, [wbd, xcb], [p])
                            self.act(s_t[:, t0:t0 + tw], p[:, 0:tw], AF.Sigmoid, [p, prm], [s_t],
                                     bias=prm[:, P_BA + 4 * d + j:P_BA + 4 * d + j + 1])
                            p2 = self.ps()
                            self.mm(p2[:, 0:tw], wbd[:, (1 * 2 + d) * 4 + j, :], xcb[:, t0:t0 + tw], True, True, [wbd, xcb], [p2])
                            self.act(bx_t[:, t0:t0 + tw], p2[:, 0:tw], AF.Sigmoid, [p2, prm], [bx_t],
                                     bias=prm[:, P_BX + 4 * d + j:P_BX + 4 * d + j + 1])
                        col = 4 + d * 4 + j
                        self.act(a_t[:, 0:T], s_t[:, 0:T], AF.Exp, [s_t, misc], [a_t], scale=misc[:, col:col + 1])
                        self.act(s_t[:, 0:T], s_t[:, 0:T], AF.Exp, [s_t, misc], [s_t], scale=misc[:, col + 8:col + 9])
                        self.act(s_t[:, 0:T], s_t[:, 0:T], AF.Sqrt, [s_t], [s_t], scale=-1.0, bias=1.0)
                        self.tt("pool", bx_t[:, 0:T], bx_t[:, 0:T], xc[:, 0:T], ALU.mult, [bx_t, xc], [bx_t])
                        self.tt("dve", bx_t[:, 0:T], bx_t[:, 0:T], s_t[:, 0:T], ALU.mult, [bx_t, s_t], [bx_t])
                        h = hs[d]
                        if seq == 0:
                            init = prm[:, P_H0 + 4 * d + j:P_H0 + 4 * d + j + 1]
                        else:
                            init = 0.0
                        if d == 0:
                            self.scan(h[:, 0:T], a_t[:, 0:T], bx_t[:, 0:T], init, [a_t, bx_t, prm], [h])
                        else:
                            self.scan(h[:, 0:T][:, ::-1], a_t[:, 0:T][:, ::-1], bx_t[:, 0:T][:, ::-1], init, [a_t, bx_t, prm], [h])
                        if seq > 0:
                            col_o = j * 4 + (seq - 1) * 2 + d
                            te = T - 1 if d == 0 else 0
                            self.cp("pool", self.stl_t[:, col_o:col_o + 1], h[:, te:te + 1], [h], [self.stl_t])
                        yield
                    self.tt("pool", hs[0][:, 0:T], hs[0][:, 0:T], hs[1][:, 0:T], ALU.add, [hs[0], hs[1]], [hs[0]])
                    self.tt("dve", yb[:, 0:T], hs[0][:, 0:T], gt[:, 0:T], ALU.mult, [hs[0], gt], [yb])
                    self.dma(self.y_scr[:, 4 + j, g0:g0 + T], yb[:, 0:T], [yb], [self.y_scr])
            self.dma(self.stl_o[:], self.stl_t[:], [self.stl_t], [self.stl_o])

    def phaseB_chain(self, st):
        if True:
            NB = 3
            def ring(name, dt=BF16):
                return [self.sb2(st, "%s%d" % (name, i), [128, 512], dt) for i in range(NB)]
            art, rrt, ttt, akt, mrbt, mrkt, bh, kh, vt = [ring(n) for n in
                                                          ("c_art", "c_rrt", "c_ttt", "c_akt", "c_mrbt", "c_mrkt", "c_bh", "c_kh", "c_vt")]
            pend = ring("c_pend", F32)
            Hf = self.sb2(st, "Hf", [128, 512], F32)
            Hb = self.sb2(st, "Hb", [128, 512], BF16)
            Zs = self.sb2(st, "Zs", [128, 512], BF16)
            Us = self.sb2(st, "Us", [128, 512], BF16)
            Yt = [self.sb2(st, "Yt%d" % i, [128, 512], F32) for i in range(2)]
            tmpH = self.sb2(st, "tmpH", [128, 512], F32)
            hs_ = lambda h: slice(h * 64, (h + 1) * 64)
            steps = []
            for (seq, cbase, n) in ((0, 0, 32), (1, 32, 4), (2, 36, 4)):
                for i in range(n):
                    steps.append((seq, cbase, n, i))

            def loads(k):
                seq, cbase, n, i = steps[k]
                r = k % NB
                for d in range(2):
                    sl = slice(d * 64, (d + 1) * 64)
                    c = cbase + i if d == 0 else cbase + n - 1 - i
                    for (tl, scr) in ((art, self.art_scr), (rrt, self.rrt_scr), (ttt, self.ttt_scr), (akt, self.akt_scr),
                                      (mrbt, self.mrbt_scr), (mrkt, self.mrkt_scr), (bh, self.bh_scr), (kh, self.kh_scr),
                                      (pend, self.pend_scr)):
                        self.dma(tl[r][d][sl, :], scr[c, sl, :], [scr], [tl[r][d]], eng="sp")
                    self.dma(vt[r][d][sl, :], self.vt_scr[c], [self.vt_scr], [vt[r][d]], eng="sp")

            loads(0)
            for k in range(len(steps)):
                seq, cbase, n, i = steps[k]
                step = k + 1
                r = k % NB
                if k + 1 < len(steps):
                    loads(k + 1)
                if i == 0:
                    for d in range(2):
                        sl = slice(d * 64, (d + 1) * 64)
                        if seq == 0:
                            self.dma(Hf[d][sl, :], self.h0r[sl, :], [], [Hf[d]], eng="act")
                        else:
                            self.memset("dve", Hf[d][sl, :], 0.0, [Hf[d]])
                        self.cp("dve", Hb[d][sl, :], Hf[d][sl, :], [Hf[d]], [Hb[d]])
                if True:
                    ctx = []
                    for d in range(2):
                        sl = slice(d * 64, (d + 1) * 64)
                        c = cbase + i if d == 0 else cbase + n - 1 - i
                        ops = tuple(x[r][d] for x in (art, rrt, ttt, akt, mrbt, mrkt, bh, kh, vt, pend))
                        ctx.append((d, sl, c, ops))
                    pZs = {}
                    for (d, sl, c, (A_, R_, T_, AK_, MRB_, MRK_, B_, K_, V_, PE_)) in ctx:
                        H_, Z_ = Hb[d], Zs[d]
                        pZ = self.ps()
                        for h in range(8):
                            self.mm(pZ[sl, hs_(h)], A_[sl, hs_(h)], H_[sl, hs_(h)], True, False, [A_, H_], [pZ])
                            self.mm(pZ[sl, hs_(h)], AK_[sl, hs_(h)], V_[sl, hs_(h)], False, True, [AK_, V_], [pZ])
                        pZs[d] = pZ
                    pYs = {}
                    for (d, sl, c, (A_, R_, T_, AK_, MRB_, MRK_, B_, K_, V_, PE_)) in ctx:
                        H_ = Hb[d]
                        pY = self.ps()
                        pYs[d] = pY
                    for (d, sl, c, ops) in ctx:
                        self.cp("act", Zs[d][sl, :], pZs[d][sl, :], [pZs[d]], [Zs[d]])
                    pUs = {}
                    for (d, sl, c, (A_, R_, T_, AK_, MRB_, MRK_, B_, K_, V_, PE_)) in ctx:
                        Z_ = Zs[d]
                        pU = self.ps()
                        for h in range(8):
                            self.mm(pU[sl, hs_(h)], T_[sl, hs_(h)], Z_[sl, hs_(h)], True, True, [T_, Z_], [pU])
                        pUs[d] = pU
                    for (d, sl, c, ops) in ctx:
                        self.cp("act", Us[d][sl, :], pUs[d][sl, :], [pUs[d]], [Us[d]])
                    pHs = {}
                    for (d, sl, c, (A_, R_, T_, AK_, MRB_, MRK_, B_, K_, V_, PE_)) in ctx:
                        U_ = Us[d]
                        self.tt("dve", tmpH[d][sl, :], Hf[d][sl, :], PE_[sl, :], ALU.mult, [Hf[d], PE_], [tmpH[d]])
                        pH = self.ps()
                        for h in range(8):
                            self.mm(pH[sl, hs_(h)], B_[sl, hs_(h)], U_[sl, hs_(h)], True, False, [B_, U_], [pH])
                            self.mm(pH[sl, hs_(h)], K_[sl, hs_(h)], V_[sl, hs_(h)], False, True, [K_, V_], [pH])
                        pHs[d] = pH
                    for (d, sl, c, (A_, R_, T_, AK_, MRB_, MRK_, B_, K_, V_, PE_)) in ctx:
                        H_, U_ = Hb[d], Us[d]
                        pY = pYs[d]
                        for h in range(8):
                            self.mm(pY[sl, hs_(h)], R_[sl, hs_(h)], H_[sl, hs_(h)], True, False, [R_, H_], [pY])
                            self.mm(pY[sl, hs_(h)], MRB_[sl, hs_(h)], U_[sl, hs_(h)], False, False, [MRB_, U_], [pY])
                            self.mm(pY[sl, hs_(h)], MRK_[sl, hs_(h)], V_[sl, hs_(h)], False, True, [MRK_, V_], [pY])
                    for (d, sl, c, ops) in ctx:
                        self.tt("dve", Hf[d][sl, :], tmpH[d][sl, :], pHs[d][sl, :], ALU.add, [tmpH[d], pHs[d]], [Hf[d]])
                        self.cp("dve", Hb[d][sl, :], Hf[d][sl, :], [Hf[d]], [Hb[d]])
                    for (d, sl, c, ops) in ctx:
                        y = Yt[step % 2][d]
                        self.cp("act", y[sl, :], pYs[d][sl, :], [pYs[d]], [y])
                        self.dma(self.ytok_scr[d, c * 64:(c + 1) * 64, :], y[sl, :], [y], [self.ytok_scr], eng="act")
                if seq > 0 and i == n - 1:
                    self.dma(self.str_o[seq - 1], Hf[0][:], [Hf[0], Hf[1]], [self.str_o], eng="act")
                yield [(0, cbase + i), (1, cbase + n - 1 - i)]

    def phaseB_post(self, st):
        prm, cst = self.prm_t, self.cst_t
        ident = self.ident
        if True:
            yf = [self.sb(st, "yf%d" % i, [128, 512], F32) for i in range(2)]
            yb2 = [self.sb(st, "yb2%d" % i, [128, 512], F32) for i in range(2)]
            cen = self.sb(st, "cen", [128, 8, 64], F32)
            sqv = self.sb(st, "sqv", [128, 8, 64], F32)
            mean = self.sb(st, "mean", [128, 8], F32)
            var = self.sb(st, "var", [128, 8], F32)
            gl = [self.sb(st, "gl%d" % i, [128, 4, 128], BF16) for i in range(2)]
            bl = [self.sb(st, "bl%d" % i, [128, 4, 128], BF16) for i in range(2)]
            ynT = self.sb(st, "ynT", [128, 4, 128], F32)
            yo = [self.sb(st, "yo%d" % i, [128, 4, 128], BF16) for i in range(2)]
            it = -1
            blk = yield
            while True:
                it += 1
                g0 = blk * 128
                a, b = yf[it % 2], yb2[it % 2]
                g_, b_ = gl[it % 2], bl[it % 2]
                o = yo[it % 2]
                self.dma(a[:], self.ytok_scr[0, g0:g0 + 128, :], [self.ytok_scr], [a])
                self.dma(b[:], self.ytok_scr[1, g0:g0 + 128, :], [self.ytok_scr], [b], eng="act")
                self.dma(g_[:], self.g_scr[:, :, g0:g0 + 128], [self.g_scr], [g_])
                self.dma(b_[:], self.bon_scr[:, :, g0:g0 + 128], [self.bon_scr], [b_], eng="act")
                a3 = a[:].rearrange("p (h v) -> p h v", v=64)
                self.tt("pool", a[:], a[:], b[:], ALU.add, [a, b], [a])
                self.S.op("dve", lambda e, a3=a3: e.tensor_reduce(out=mean[:], in_=a3, op=ALU.add, axis=mybir.AxisListType.X),
                          [a.b], [mean.b])
                self.tsc("dve", mean[:], mean[:], 1.0 / 64, ALU.mult, [mean], [mean])
                self.tt("dve", cen[:], a3, mean[:].unsqueeze(2).to_broadcast([128, 8, 64]), ALU.subtract, [a, mean], [cen])
                self.tt("pool", sqv[:], cen[:], cen[:], ALU.mult, [cen], [sqv])
                self.S.op("dve", lambda e: e.tensor_reduce(out=var[:], in_=sqv[:], op=ALU.add, axis=mybir.AxisListType.X),
                          [sqv.b], [var.b])
                self.act(var[:], var[:], AF.Sqrt, [var, self.epsT], [var], scale=1.0 / 64, bias=self.epsT[:, 1:2])
                self.S.op("dve", lambda e: e.reciprocal(out=var[:], in_=var[:]), [var.b], [var.b])
                self.tt("dve", cen[:], cen[:], var[:].unsqueeze(2).to_broadcast([128, 8, 64]), ALU.mult, [cen, var], [cen])
                p = self.ps()
                cen2 = cen[:].rearrange("p h v -> p (h v)")
                for j in range(4):
                    self.tr(p[:, j * 128:(j + 1) * 128], cen2[:, j * 128:(j + 1) * 128], ident[:], [cen, ident], [p])
                for j in range(4):
                    self.act(ynT[:, j, :], p[:, j * 128:(j + 1) * 128], AF.Identity, [p, prm], [ynT],
                             scale=prm[:, P_LNG + j:P_LNG + j + 1], bias=prm[:, P_LNB + j:P_LNB + j + 1])
                self.tt("pool", ynT[:], ynT[:], b_[:], ALU.add, [ynT, b_], [ynT])
                self.tt("dve", o[:], ynT[:], g_[:], ALU.mult, [ynT, g_], [o])
                self.dma(self.y_scr[:, 0:4, g0:g0 + 128], o[:], [o], [self.y_scr])
                blk = yield

    def phaseC(self):
        prm, cst = self.prm_t, self.cst_t
        ident, ones_bf = self.ident, self.ones_bf
        with contextlib.ExitStack() as st:
            pass
        import os
        kcc = int(os.environ.get("KCC", "99"))
        if kcc == 0:
            return
        with contextlib.ExitStack() as st:
            wout = self.sb(st, "wout", [128, 8, D], BF16)
            wsrc = self.w_out[:].rearrange("(kc p) n -> p kc n", p=128)
            stg = [self.sb(st, "stgD%d" % i, [128, 8, 256], F32) for i in range(2)]
            for cb in range(4):
                s = stg[cb % 2]
                self.dma(s[:], wsrc[:, :, cb * 256:(cb + 1) * 256], [], [s])
                self.cp("act", wout[:, :, cb * 256:(cb + 1) * 256], s[:], [s], [wout])
            yT = self.sb(st, "yT_c", [128, 8, TC], BF16)
            oT = self.sb(st, "oT", [128, 8, TC], F32)
            sq = self.sb(st, "sq_c", [128, 8, TC], BF16)
            xT = self.sb(st, "xT_c", [128, 8, TC], F32)
            h2 = self.sb(st, "h2", [128, 8, TC], BF16)
            f = self.sb(st, "f_c", [128, 32, TC], BF16)
            otok = self.sb(st, "otok", [128, 4, D], F32)
            rstd = self.sb(st, "rstd_c", [128, TC], F32)
            tmp = [self.sb(st, "tmpC%d" % i, [128, TC], F32) for i in range(2)]
            w1r = [self.sb(st, "w1r%d" % i, [128, 8, 512], BF16) for i in range(2)]
            w2r = [self.sb(st, "w2r%d" % i, [128, 32, 128], BF16) for i in range(2)]

            def rms(src, R):
                p = self.ps()
                for j in range(8):
                    self.mm(p[:], ones_bf[:], sq[:, j, :], j == 0, j == 7, [ones_bf, sq], [p])
                self.act(rstd[:], p[:], AF.Sqrt, [p, self.epsT], [rstd], scale=1.0 / D, bias=self.epsT[:, 0:1])
                self.S.op("dve", lambda e: e.reciprocal(out=rstd[:], in_=rstd[:]), [rstd.b], [rstd.b])

            def resid(gg, mc):
                for j in range(8):
                    t = tmp[j % 2]
                    self.tt("dve", t[:], oT[:, j, :], rstd[:], ALU.mult, [oT, rstd], [t])
                    self.stt(xT[:, j, :], t[:], gg[:, j, mc:mc + 1], xT[:, j, :], ALU.mult, ALU.add, [t, gg, xT], [xT])

            wi = 0
            for ti in range(NTOK // TC):
                if ti >= kcc:
                    break
                g0 = ti * TC
                mc = 0 if g0 < TS else 1
                self.dma(yT[:], self.y_scr[:, :, g0:g0 + TC], [self.y_scr], [yT])
                self.dma(xT[:], self.xT_scr[:, :, g0:g0 + TC], [self.xT_scr], [xT], eng="act")
                for m in range(8):
                    p = self.ps()
                    for kc in range(8):
                        self.mm(p[:], wout[:, kc, m * 128:(m + 1) * 128], yT[:, kc, :], kc == 0, kc == 7, [wout, yT], [p])
                    self.cp("act", oT[:, m, :], p[:], [p], [oT])
                    self.tt("pool", sq[:, m, :], oT[:, m, :], oT[:, m, :], ALU.mult, [oT], [sq])
                rms(oT, None)
                resid(self.gg1, mc)
                for j in range(8):
                    self.tt("pool", sq[:, j, :], xT[:, j, :], xT[:, j, :], ALU.mult, [xT], [sq])
                rms(xT, None)
                for j in range(8):
                    t = tmp[j % 2]
                    self.tt("dve", t[:], xT[:, j, :], rstd[:], ALU.mult, [xT, rstd], [t])
                    self.act(h2[:, j, :], t[:], AF.Identity, [t, self.gs2, self.modT], [h2],
                             scale=self.gs2[:, j, mc:mc + 1], bias=self.modT[:, 24 + j, mc:mc + 1])
                for blk in range(8):
                    w = w1r[wi % 2]
                    wi += 1
                    self.dma(w[:], self.w1_scr[blk], [self.w1_scr], [w], eng="sp" if blk % 2 == 0 else "act")
                    for c4 in range(4):
                        fc = blk * 4 + c4
                        p = self.ps()
                        for kc in range(8):
                            self.mm(p[:], w[:, kc, c4 * 128:(c4 + 1) * 128], h2[:, kc, :], kc == 0, kc == 7, [w, h2], [p])
                        t = tmp[fc % 2]
                        self.act(t[:], p[:], AF.Relu, [p], [t])
                        self.tt("pool" if fc % 2 == 0 else "dve", f[:, fc, :], t[:], t[:], ALU.mult, [t], [f])
                for m in range(8):
                    w = w2r[m % 2]
                    self.dma(w[:], self.w2_scr[m], [self.w2_scr], [w], eng="sp" if m % 2 == 0 else "act")
                    p = self.ps()
                    for fc in range(32):
                        self.mm(p[:], w[:, fc, :], f[:, fc, :], fc == 0, fc == 31, [w, f], [p])
                    self.cp("act", oT[:, m, :], p[:], [p], [oT])
                    self.tt("pool", sq[:, m, :], oT[:, m, :], oT[:, m, :], ALU.mult, [oT], [sq])
                rms(oT, None)
                resid(self.gg2, mc)
                for s in range(4):
                    for half in range(2):
                        p = self.ps()
                        for jj in range(4):
                            j = half * 4 + jj
                            self.tr(p[:, jj * 128:(jj + 1) * 128], xT[:, j, s * 128:(s + 1) * 128], ident[:], [xT, ident], [p])
                        self.cp("act" if half == 0 else "dve", otok[:, s, half * 512:(half + 1) * 512], p[:], [p], [otok])
                if g0 < TS:
                    dst = self.ys[g0:g0 + TC, :].rearrange("(s p) f -> p s f", p=128)
                    self.dma(dst, otok[:], [otok], [self.ys])
                else:
                    dst = self.yp[:, :].rearrange("(s p) f -> p s f", p=128)
                    self.dma(dst, otok[:], [otok], [self.yp])


def _fm(v):
    v = np.asarray(v, np.float32).reshape(-1, 128)
    return np.ascontiguousarray(v.T)


def _pos_embed():
    def sincos(pos, dim):
        omega = (1.0 / (10000.0 ** (np.arange(dim // 2, dtype=np.float32) / np.float32(dim // 2)))).astype(np.float32)
        ang = pos.astype(np.float32)[:, None] * omega[None, :]
        return np.concatenate([np.sin(ang), np.cos(ang)], axis=-1).astype(np.float32)
    rows = TS // 64
    half = D // 2
    e_row = sincos(np.arange(rows), half)
    e_col = sincos(np.arange(64), half)
    emb = np.concatenate([np.broadcast_to(e_row[:, None, :], (rows, 64, half)),
                          np.broadcast_to(e_col[None, :, :], (rows, 64, half))], axis=-1)
    return np.ascontiguousarray(emb.reshape(rows * 64, D).astype(np.float32))


def _consts():
    c = np.zeros((128, NCST), np.float32)
    c[:, C_ID:C_ID + 128] = np.eye(128, dtype=np.float32)
    ob = np.zeros((128, 128), np.float32)
    ob[:64, :64] = 1.0
    ob[64:, 64:] = 1.0
    c[:, C_OB:C_OB + 128] = ob
    s = np.arange(64)[:, None]
    t = np.arange(64)[None, :]
    msi = np.zeros((128, 2, 64), np.float32)
    msi[:64, 0] = (s < t)
    msi[:64, 1] = (s <= t)
    msi[64:, 0] = (s > t)
    msi[64:, 1] = (s >= t)
    c[:, C_MSI:C_MSI + 128] = msi.reshape(128, 128)
    ml = np.zeros((128, 64), np.float32)
    ml[:64] = (t < s)
    ml[64:] = (t > s)
    c[:, C_ML:C_ML + 64] = ml
    ids = np.zeros((128, 64), np.float32)
    ids[:64] = np.eye(64)
    ids[64:] = np.eye(64)
    c[:, C_IDS:C_IDS + 64] = ids
    c[:64, C_MSI1:C_MSI1 + 128] = msi[64:].reshape(64, 128)
    c[:64, C_ML1:C_ML1 + 64] = ml[64:]
    tt_ = np.arange(TT)
    c[:, C_RMF:C_RMF + TT] = (tt_ % 64 != 0).astype(np.float32)[None, :]
    c[:, C_RMB:C_RMB + TT] = (tt_ % 64 != 63).astype(np.float32)[None, :]
    return c


_NC_CACHE = {}


def kernel(x_prompt, x_sample, c, state_rwkv, state_lru, c_ctx, w_mod, b_mod,
           g_pre_mix, g_post_mix, g_pre_mlp, g_post_mlp, w_in,
           rwkv_w0, rwkv_w_up, rwkv_a0, rwkv_a_up, rwkv_g_up, rwkv_k_k, rwkv_k_a, rwkv_r_k,
           rwkv_lnx_g, rwkv_lnx_b, lru_conv_w, lru_conv_b, lru_wa, lru_ba, lru_wx, lru_bx,
           lru_lambda, w_out, w_mlp1, w_mlp2, _debug=False):
    f = lambda a: np.ascontiguousarray(np.asarray(a, np.float32))
    x_prompt, x_sample, c, state_rwkv, state_lru, c_ctx = map(f, (x_prompt, x_sample, c, state_rwkv, state_lru, c_ctx))
    if "nc" not in _NC_CACHE:
        _NC_CACHE["nc"] = K(debug=_debug).build()
    nc = _NC_CACHE["nc"]
    pe = _pos_embed()
    cst = _consts()
    shared = {
        "pe": pe, "cst": cst,
        "w_mod": f(w_mod[0]), "w_in": f(w_in[0]), "w_out": f(w_out[0]), "w1": f(w_mlp1[0]), "w2": f(w_mlp2[0]),
        "wup": f(rwkv_w_up[0]).reshape(128, 512), "aup": f(rwkv_a_up[0]).reshape(128, 512), "gup": f(rwkv_g_up[0]),
        "lwa": f(lru_wa[0]), "lwx": f(lru_wx[0]),
    }
    prm0 = np.zeros((128, NPRM), np.float32)
    prm0[:, P_GPRE:P_GPRE + 8] = _fm(g_pre_mix[0])
    prm0[:, P_GPOST:P_GPOST + 8] = _fm(g_post_mix[0])
    prm0[:, P_GPRE2:P_GPRE2 + 8] = _fm(g_pre_mlp[0])
    prm0[:, P_GPOST2:P_GPOST2 + 8] = _fm(g_post_mlp[0])
    prm0[:, P_BMOD:P_BMOD + 48] = _fm(b_mod[0])
    for d in range(2):
        prm0[:, P_W0 + 4 * d:P_W0 + 4 * d + 4] = _fm(rwkv_w0[0, d])
        prm0[:, P_A0 + 4 * d:P_A0 + 4 * d + 4] = _fm(rwkv_a0[0, d])
        prm0[:, P_BA + 4 * d:P_BA + 4 * d + 4] = _fm(lru_ba[0, d])
        prm0[:, P_BX + 4 * d:P_BX + 4 * d + 4] = _fm(lru_bx[0, d])
        prm0[:, P_LAM + 4 * d:P_LAM + 4 * d + 4] = _fm(lru_lambda[0, d])
    prm0[:, P_KK:P_KK + 4] = _fm(rwkv_k_k[0])
    prm0[:, P_KA:P_KA + 4] = _fm(rwkv_k_a[0])
    prm0[:, P_RK:P_RK + 4] = _fm(np.asarray(rwkv_r_k[0]).reshape(-1))
    prm0[:, P_LNG:P_LNG + 4] = _fm(rwkv_lnx_g[0])
    prm0[:, P_LNB:P_LNB + 4] = _fm(rwkv_lnx_b[0])
    for i in range(4):
        prm0[:, P_CW + 4 * i:P_CW + 4 * i + 4] = _fm(lru_conv_w[0, i])
    prm0[:, P_CB:P_CB + 4] = _fm(lru_conv_b[0])
    in_maps = []
    for i in range(8):
        prm = prm0.copy()
        for d in range(2):
            prm[:, P_H0 + 4 * d:P_H0 + 4 * d + 4] = _fm(state_lru[i, 0, d])
        cT = np.zeros((128, 8, 2), np.float32)
        cT[:, :, 0] = _fm(c[i])
        cT[:, :, 1] = _fm(c_ctx)
        h0 = np.ascontiguousarray(state_rwkv[i, 0].transpose(0, 3, 1, 2)).reshape(128, 512)
        m = dict(shared)
        m.update({"xs": x_sample[i], "xp": np.ascontiguousarray(x_prompt[2 * i:2 * i + 2].reshape(2 * TP, D)),
                  "cT": cT.reshape(128, 16), "h0r": h0, "prm": prm})
        in_maps.append(m)
    res = run_bass_kernel_spmd(nc, in_maps, core_ids=list(range(8)))
    R = res.results
    y_prompt = np.zeros((16, TP, D), np.float32)
    y_sample = np.zeros((8, TS, D), np.float32)
    st_r = np.zeros((16, 1, 2, 8, 64, 64), np.float32)
    st_l = np.zeros((16, 1, 2, 512), np.float32)
    for i in range(8):
        r = R[i]
        y_sample[i] = r["ys"]
        y_prompt[2 * i:2 * i + 2] = r["yp"].reshape(2, TP, D)
        so = r["str_o"].reshape(2, 2, 64, 8, 64)
        st_r[2 * i:2 * i + 2, 0] = so.transpose(0, 1, 3, 4, 2)
        sl = r["stl_o"].reshape(128, 4, 2, 2)
        st_l[2 * i:2 * i + 2, 0] = sl.transpose(2, 3, 1, 0).reshape(2, 2, 512)
    if _debug:
        return (y_prompt, y_sample, st_r, st_l), R
    return (y_prompt, y_sample, st_r, st_l)
```

```python
import contextlib
import numpy as np
import concourse.bass as bass
import concourse.mybir as mybir
from concourse.bass_utils import run_bass_kernel_spmd

F32 = mybir.dt.float32
BF16 = mybir.dt.bfloat16
F32R = mybir.dt.float32r
AF = mybir.ActivationFunctionType
ALU = mybir.AluOpType

D = 1024
TS = 2048
TP = 256
NTOK = TS + 2 * TP
NCH = NTOK // 64
DIN = 2944
DFF = 4096
LAM = float(np.exp(-0.5))
EPS = 1e-6
LNX_EPS = 64e-5
TT = 256
TC = 512
GELU_C = 1.5957691216057308

P_GPRE, P_GPOST, P_GPRE2, P_GPOST2 = 0, 8, 16, 24
P_BMOD = 32
P_W0, P_A0 = 80, 88
P_KK, P_KA, P_RK, P_LNG, P_LNB = 96, 100, 104, 108, 112
P_CW, P_CB = 116, 132
P_BA, P_BX, P_LAM, P_H0 = 136, 144, 152, 160
NPRM = 168
C_ID, C_OB, C_MSI, C_ML, C_IDS, C_RMF, C_RMB = 0, 128, 256, 384, 448, 512, 768
C_MSI1, C_ML1 = 1024, 1152
NCST = 1216


class Buf:
    __slots__ = ("name", "lw", "rd", "excl", "multi", "ws")

    def __init__(self, name=""):
        self.name = name
        self.lw = None
        self.rd = {}
        self.excl = False
        self.multi = False
        self.ws = {}


class TL:
    def __init__(self, t, name=""):
        self.t = t
        self.b = Buf(name)

    def __getitem__(self, k):
        return self.t[k]


class Sched:
    ENGS = ("pe", "act", "dve", "pool", "sp")

    def __init__(self, nc):
        self.nc = nc
        self.streams = {e: [] for e in self.ENGS}
        self.cnt = {}
        self.waited = {e: {} for e in self.ENGS}
        self.n_ops = 0
        self.dma_n = {e: 0 for e in self.ENGS}
        self.NSLOT = {"sp": 44, "act": 44, "pool": 4, "dve": 2, "pe": 2}

    def _deps(self, eng, reads, writes):
        need = {}
        for b in reads:
            if b.multi:
                for s, v in b.ws.items():
                    if need.get(s, 0) < v:
                        need[s] = v
                continue
            if b.lw is not None:
                s, v = b.lw
                if need.get(s, 0) < v:
                    need[s] = v
            if b.excl:
                for s, v in b.rd.items():
                    if s != eng and need.get(s, 0) < v:
                        need[s] = v
        for b in writes:
            if b.multi:
                continue
            if b.lw is not None:
                s, v = b.lw
                if need.get(s, 0) < v:
                    need[s] = v
            for s, v in b.rd.items():
                if need.get(s, 0) < v:
                    need[s] = v
        out = []
        w = self.waited[eng]
        for s, v in need.items():
            if s == "pe" and eng == "pe":
                continue
            if w.get(s, 0) >= v:
                continue
            w[s] = v
            out.append((s, v))
        return out

    def op(self, eng, fn, reads=(), writes=(), dma=False):
        reads = [r.b if isinstance(r, TL) else r for r in reads]
        writes = [r.b if isinstance(r, TL) else r for r in writes]
        waits = self._deps(eng, reads, writes)
        if dma:
            slot = self.dma_n[eng] % self.NSLOT[eng]
            self.dma_n[eng] += 1
            sem = "%s_d%d" % (eng, slot)
            prev = self.cnt.get(sem, 0)
            if prev > 0 and self.waited[eng].get(sem, 0) < prev:
                self.waited[eng][sem] = prev
                waits.append((sem, prev))
        else:
            sem = eng
        inc = 16 if dma else 1
        self.cnt[sem] = self.cnt.get(sem, 0) + inc
        val = self.cnt[sem]
        self.streams[eng].append((waits, fn, sem, inc))
        self.n_ops += 1
        for b in reads:
            if b.rd.get(sem, 0) < val:
                b.rd[sem] = val
        for b in writes:
            if b.multi:
                if b.ws.get(sem, 0) < val:
                    b.ws[sem] = val
                continue
            b.lw = (sem, val)
            b.rd = {}
        return val

    def barrier_on(self, tl):
        if tl.b.lw is None:
            return
        sname, v = tl.b.lw
        for e in ("sp", "act", "pool"):
            if self.waited[e].get(sname, 0) < v:
                self.waited[e][sname] = v
                self.streams[e].append(([(sname, v)], None, None, 0))

    def barrier(self):
        snap = dict(self.cnt)
        for e in self.ENGS:
            waits = []
            for s, v in snap.items():
                if s == "pe" and e == "pe":
                    continue
                if self.waited[e].get(s, 0) < v:
                    self.waited[e][s] = v
                    waits.append((s, v))
            if waits:
                self.streams[e].append((waits, None, None, 0))

    def emit(self):
        nc = self.nc
        sems = {}
        with contextlib.ExitStack() as st:
            for s in self.cnt:
                sems[s] = st.enter_context(nc.semaphore(s))
            block = st.enter_context(nc.Block())
            engmap = {"pe": block.tensor, "act": block.scalar, "dve": block.vector,
                      "pool": block.gpsimd, "sp": block.sync}
            for e in self.ENGS:
                stream = self.streams[e]
                if not stream:
                    continue

                def body(eng, stream=stream):
                    for waits, fn, sem, inc in stream:
                        for s, v in waits:
                            eng.wait_ge(sems[s], v)
                        if fn is not None:
                            fn(eng).then_inc(sems[sem], inc)
                engmap[e](body)


class K:
    def __init__(self, debug=False, stop_after=None):
        self.debug = debug
        self.stop_after = stop_after
        import os
        self.cutk = int(os.environ.get("KCUT", "0"))
        self.cutm = int(os.environ.get("KCUTM", "99"))
        self.ktiles = int(os.environ.get("KTILES", "99"))
        self.kskip = os.environ.get("KSKIP", "").split(",")
        self.nc = bass.Bass("TRN2", target_bir_lowering=False)
        self.S = Sched(self.nc)
        self.es = contextlib.ExitStack()
        self.psr = 0
        self.rr = {}

    def dram(self, name, shape, dt, kind="Internal"):
        t = TL(self.nc.dram_tensor(name, list(shape), dt, kind=kind).ap(), name)
        t.b.multi = True
        return t

    def sb(self, st, name, shape, dt):
        return TL(st.enter_context(self.nc.sbuf_tensor(name, list(shape), dt)), name)

    def sb2(self, st, name, shape, dt):
        t = st.enter_context(self.nc.sbuf_tensor(name, list(shape), dt))
        return [TL(t, name + "_lo"), TL(t, name + "_hi")]

    def ps(self):
        p = self.psum[self.psr % 8]
        self.psr += 1
        return p

    def mm(self, out, lhsT, rhs, start, stop, R, W):
        self.S.op("pe", lambda e: e.matmul(out, lhsT=lhsT, rhs=rhs, start=start, stop=stop), R, W)

    def tr(self, out, in_, ident, R, W):
        self.S.op("pe", lambda e: e.transpose(out, in_, ident), R, W)

    def act(self, out, in_, func, R, W, scale=1.0, bias=None, eng="act"):
        if bias is None:
            self.S.op("act", lambda e: e.activation(out=out, in_=in_, func=func, scale=scale), R, W)
        else:
            self.S.op("act", lambda e: e.activation(out=out, in_=in_, func=func, scale=scale, bias=bias), R, W)

    def tt(self, eng, out, in0, in1, op, R, W):
        self.S.op(eng, lambda e: e.tensor_tensor(out=out, in0=in0, in1=in1, op=op), R, W)

    def tsc(self, eng, out, in0, s1, op0, R, W, s2=None, op1=None):
        if op1 is None:
            self.S.op(eng, lambda e: e.tensor_scalar(out=out, in0=in0, scalar1=s1, scalar2=None, op0=op0), R, W)
        else:
            self.S.op(eng, lambda e: e.tensor_scalar(out=out, in0=in0, scalar1=s1, scalar2=s2, op0=op0, op1=op1), R, W)

    def stt(self, out, in0, scalar, in1, op0, op1, R, W):
        self.S.op("dve", lambda e: e.scalar_tensor_tensor(out=out, in0=in0, scalar=scalar, in1=in1, op0=op0, op1=op1), R, W)

    def cp(self, eng, out, in_, R, W):
        if eng == "act":
            self.S.op("act", lambda e: e.activation(out=out, in_=in_, func=AF.Copy), R, W)
        else:
            self.S.op(eng, lambda e: e.tensor_copy(out=out, in_=in_), R, W)

    def scan(self, out, d0, d1, init, R, W):
        self.S.op("dve", lambda e: e.tensor_tensor_scan(out=out, data0=d0, data1=d1, initial=init,
                                                        op0=ALU.mult, op1=ALU.add), R, W)

    def dma(self, out, in_, R, W, eng="sp"):
        self.S.op(eng, lambda e: e.dma_start(out=out, in_=in_), R, W, dma=True)

    def memset(self, eng, ap, val, W):
        self.S.op(eng, lambda e: e.memset(ap, val), (), W)

    def pick(self, key, engs):
        i = self.rr.get(key, 0)
        self.rr[key] = i + 1
        return engs[i % len(engs)]

    def build(self):
        nc = self.nc
        I = lambda n, s, dt=F32: self.dram(n, s, dt, "ExternalInput")
        O = lambda n, s, dt=F32: self.dram(n, s, dt, "ExternalOutput")
        self.xs = I("xs", [TS, D])
        self.xp = I("xp", [2 * TP, D])
        self.pe = I("pe", [TS, D])
        self.cT = I("cT", [128, 16])
        self.h0r = I("h0r", [128, 512])
        self.prm = I("prm", [128, NPRM])
        self.cst = I("cst", [128, NCST])
        self.w_mod = I("w_mod", [D, 6 * D])
        self.w_in = I("w_in", [D, DIN])
        self.w_out = I("w_out", [D, D])
        self.w1 = I("w1", [D, DFF])
        self.w2 = I("w2", [DFF, D])
        self.wup = I("wup", [128, 512])
        self.aup = I("aup", [128, 512])
        self.gup = I("gup", [128, 512])
        self.lwa = I("lwa", [2, 8, 64, 64])
        self.lwx = I("lwx", [2, 8, 64, 64])
        self.ys = O("ys", [TS, D])
        self.yp = O("yp", [2 * TP, D])
        self.str_o = O("str_o", [2, 128, 512])
        self.stl_o = O("stl_o", [128, 16])
        self.xT_scr = self.dram("xT_scr", [128, 8, NTOK], F32)
        self.xb_scr = self.dram("xb_scr", [128, 4, NTOK], F32)
        self.gate_scr = self.dram("gate_scr", [128, 4, NTOK], BF16)
        self.g_scr = self.dram("g_scr", [128, 4, NTOK], BF16)
        self.bon_scr = self.dram("bon_scr", [128, 4, NTOK], BF16)
        self.y_scr = self.dram("y_scr", [128, 8, NTOK], BF16)
        self.ytok_scr = self.dram("ytok_scr", [2, NTOK, 512], F32)
        for n in ("art", "rrt", "ttt", "akt", "mrbt", "mrkt", "bh", "kh"):
            setattr(self, n + "_scr", self.dram(n + "_scr", [NCH, 128, 512], BF16))
        self.vt_scr = self.dram("vt_scr", [NCH, 64, 512], BF16)
        self.pend_scr = self.dram("pend_scr", [NCH, 128, 512], F32)
        self.w1_scr = self.dram("w1_scr", [8, 128, 8, 512], BF16)
        self.w2_scr = self.dram("w2_scr", [8, 128, 32, 128], BF16)
        if self.debug:
            self.dbg = {}

        with self.es as st0:
            self.psum = [TL(st0.enter_context(nc.psum_tensor("ps%d" % i, [128, 512], F32)), "ps%d" % i)
                         for i in range(8)]
            for p_ in self.psum:
                p_.b.excl = True
            self.prm_t = self.sb(st0, "prm_t", [128, NPRM], F32)
            self.cst_t = self.sb(st0, "cst_t", [128, NCST], F32)
            self.modT = self.sb(st0, "modT", [128, 48, 2], F32)
            self.gs1 = self.sb(st0, "gs1", [128, 8, 2], F32)
            self.gs2 = self.sb(st0, "gs2", [128, 8, 2], F32)
            self.gg1 = self.sb(st0, "gg1", [128, 8, 2], F32)
            self.gg2 = self.sb(st0, "gg2", [128, 8, 2], F32)
            self.ident = self.sb(st0, "ident", [128, 128], F32)
            self.ones_bf = self.sb(st0, "ones_bf", [128, 128], BF16)
            self.oblk_bf = self.sb(st0, "oblk_bf", [128, 128], BF16)
            self.epsT = self.sb(st0, "epsT", [128, 2], F32)
            self.misc = self.sb(st0, "misc", [128, 32], F32)
            self.stl_t = self.sb(st0, "stl_t", [128, 16], F32)
            for nm, fn in (("p0", self.phase0), ("pA", self.phaseA), ("pB", self.phaseB), ("pC", self.phaseC)):
                fn()
                self.S.barrier()
                if self.stop_after == nm:
                    break
            self.S.emit()
        return nc

    def dump(self, name, src_ap, shape, dt, R):
        o = self.dram("dbg_" + name, shape, dt, "ExternalOutput")
        self.dma(o[:], src_ap, R, [o])

    def phase0(self):
        nc = self.nc
        prm, cst = self.prm_t, self.cst_t
        self.dma(prm[:], self.prm[:], [], [prm])
        self.dma(cst[:], self.cst[:], [], [cst])
        self.cp("dve", self.ident[:], cst[:, C_ID:C_ID + 128], [cst], [self.ident])
        self.cp("dve", self.oblk_bf[:], cst[:, C_OB:C_OB + 128], [cst], [self.oblk_bf])
        self.memset("dve", self.ones_bf[:], 1.0, [self.ones_bf])
        self.memset("dve", self.epsT[:, 0:1], EPS, [self.epsT])
        self.memset("dve", self.epsT[:, 1:2], LNX_EPS, [self.epsT])
        self.tsc("dve", self.misc[:, 0:4], prm[:, P_KA:P_KA + 4], -1.0, ALU.mult, [prm], [self.misc], 1.0, ALU.add)
        with contextlib.ExitStack() as st:
            scT = self.sb(st, "scT", [128, 16], F32)
            cT = self.sb(st, "cT_t", [128, 16], F32)
            wm = [self.sb(st, "wm%d" % i, [128, 8, 512], F32) for i in range(2)]
            tmp = self.sb(st, "lam_tmp", [128, 8], F32)
            self.dma(cT[:], self.cT[:], [], [cT])
            self.act(scT[:], cT[:], AF.Silu, [cT], [scT])
            self.act(tmp[:], prm[:, P_LAM:P_LAM + 8], AF.Exp, [prm], [tmp], scale=-1.0)
            self.act(tmp[:], tmp[:], AF.Ln, [tmp], [tmp], bias=1.0)
            self.tsc("dve", self.misc[:, 4:12], tmp[:], -8.0, ALU.mult, [tmp], [self.misc])
            self.tsc("dve", self.misc[:, 12:20], tmp[:], -16.0, ALU.mult, [tmp], [self.misc])
            wsrc = self.w_mod[:].rearrange("(kc p) n -> p kc n", p=128)
            sc3 = scT[:].rearrange("p (k c) -> p k c", c=2)
            for blk in range(12):
                w = wm[blk % 2]
                self.dma(w[:], wsrc[:, :, blk * 512:(blk + 1) * 512], [], [w], eng="sp" if blk % 2 == 0 else "act")
                p = self.ps()
                for m in range(4):
                    for kc in range(8):
                        self.mm(p[:, 2 * m:2 * m + 2], w[:, kc, m * 128:(m + 1) * 128], sc3[:, kc, :],
                                kc == 0, kc == 7, [w, scT], [p])
                for m in range(4):
                    mi = blk * 4 + m
                    self.tsc("dve", self.modT[:, mi, :], p[:, 2 * m:2 * m + 2], prm[:, P_BMOD + mi:P_BMOD + mi + 1],
                             ALU.add, [p, prm], [self.modT])
            m3 = self.modT
            for (dst, sc_off, g_off, one) in ((self.gs1, 8, P_GPRE, 1.0), (self.gs2, 32, P_GPRE2, 1.0),
                                              (self.gg1, 16, P_GPOST, 0.0), (self.gg2, 40, P_GPOST2, 0.0)):
                for c in range(2):
                    self.tsc("dve", dst[:, :, c], m3[:, sc_off:sc_off + 8, c], one, ALU.add, [m3], [dst])
                    self.tt("dve", dst[:, :, c], dst[:, :, c], prm[:, g_off:g_off + 8], ALU.mult, [dst, prm], [dst])
            if self.debug:
                self.dump("modT", self.modT[:], [128, 48, 2], F32, [self.modT])
                self.dump("gs1", self.gs1[:], [128, 8, 2], F32, [self.gs1])

    def load_cast(self, st, dst_ap, dst_tl, src_ap, shape, tag):
        key = "stg_" + tag
        if not hasattr(self, key):
            setattr(self, key, [self.sb(st, "%s%d" % (key, i), shape, F32) for i in range(2)])
        ring = getattr(self, key)
        s = ring[self.rr.get(key, 0) % 2]
        self.rr[key] = self.rr.get(key, 0) + 1
        self.dma(s[:], src_ap, [], [s], eng="sp")
        eng = self.pick("castE", ["act", "pool"])
        self.cp(eng, dst_ap, s[:], [s], [dst_tl])

    def phaseA(self):
        nc = self.nc
        prm, cst = self.prm_t, self.cst_t
        with contextlib.ExitStack() as st:
            win = self.sb(st, "win", [128, 8, DIN], BF16)
            win.b.multi = True
            wsrc = self.w_in[:].rearrange("(kc p) n -> p kc n", p=128)
            wup = self.sb(st, "wup_t", [128, 512], BF16)
            aup = self.sb(st, "aup_t", [128, 512], BF16)
            gup = self.sb(st, "gup_t", [128, 512], BF16)
            with contextlib.ExitStack() as st2:
                stgA = [self.sb(st2, "stgA%d" % i, [128, 8, 256], F32) for i in range(2)]
                nb = 0
                for c0 in range(0, DIN, 256):
                    cw = min(256, DIN - c0)
                    s_ = stgA[nb % 2]
                    self.dma(s_[:, :, 0:cw], wsrc[:, :, c0:c0 + cw], [], [s_], eng="sp" if nb % 2 == 0 else "act")
                    self.cp("act" if nb % 2 == 0 else "pool", win[:, :, c0:c0 + cw], s_[:, :, 0:cw], [s_], [win])
                    nb += 1
                for i, (src, dstt) in enumerate(((self.wup, wup), (self.aup, aup), (self.gup, gup))):
                    s_ = stgA[nb % 2]
                    nb += 1
                    s2 = s_[:].rearrange("p a b -> p (a b)")[:, 0:512]
                    self.dma(s2, src[:], [], [s_])
                    self.cp("dve", dstt[:], s2, [s_], [dstt])
            self.S.barrier()
            mSI = cst[:, C_MSI:C_MSI + 128].rearrange("p (q t) -> p q t", q=2)
            mL = cst[:, C_ML:C_ML + 64]
            idS = cst[:, C_IDS:C_IDS + 64]

            xin = self.sb(st, "xin", [128, 2, D], F32)
            xT = self.sb(st, "xT", [128, 8, TT], F32)
            sq = self.sb(st, "sq", [128, 8, TT], BF16)
            hT = self.sb(st, "hT", [128, 8, TT], BF16)
            rstd = self.sb(st, "rstd", [128, TT], F32)
            tmpA = [self.sb(st, "tmpA%d" % i, [128, TT], F32) for i in range(2)]
            rT = self.sb(st, "rT", [128, 4, TT], F32)
            kT = self.sb(st, "kT", [128, 4, TT], F32)
            vT = self.sb(st, "vT", [128, 4, TT], F32)
            xw = self.sb(st, "xw", [128, TT], BF16)
            xa = self.sb(st, "xa", [128, TT], BF16)
            xg = self.sb(st, "xg", [128, TT], BF16)
            xbT = self.sb(st, "xbT", [128, 4, TT], F32)
            gtmp = [self.sb(st, "gtmp%d" % i, [128, TT], F32) for i in range(3)]
            gate = self.sb(st, "gate", [128, 4, TT], BF16)
            gT = self.sb(st, "gT", [128, 4, TT], BF16)
            kkn = self.sb(st, "kkn", [128, 4, TT], F32)
            ksum = self.sb(st, "ksum", [128, 4, TT], F32)
            bon = self.sb(st, "bon", [128, 4, TT], BF16)
            sg = self.sb(st, "sg", [128, 4, TT], F32)
            cs = self.sb(st, "cs", [128, 4, TT], F32)
            E1 = self.sb(st, "E1", [128, 4, TT], F32)
            ad = self.sb(st, "ad", [128, 4, TT], F32)
            wk1 = self.sb(st, "wk1", [128, 4, TT], F32)
            wk2 = self.sb(st, "wk2", [128, 4, TT], F32)
            NC4 = TT // 64
            AR = [self.sb(st, "AR%d" % d, [128, 4, NC4, 2, 64], BF16) for d in range(2)]
            BK = [self.sb(st, "BK%d" % d, [128, 4, NC4, 2, 64], BF16) for d in range(2)]
            PEb1 = self.sb(st, "PEb", [128, 4, NC4, 64], F32)
            PEb = [PEb1, PEb1]
            tokB1 = self.sb(st, "tokB", [128, 2, 8, 64], BF16)
            tokK1 = self.sb(st, "tokK", [128, 2, 8, 64], BF16)
            tokB, tokK = [tokB1, tokB1], [tokK1, tokK1]
            tokV = self.sb(st, "tokV", [128, 2, 8, 64], BF16)
            NSET = 2
            MRBs = [self.sb(st, "MRBs%d" % i, [64, 8, 64], BF16) for i in range(NSET)]
            Lt0s = [self.sb(st, "Lt0_%d" % i, [64, 8, 64], F32R) for i in range(NSET)]
            Tfins = [self.sb(st, "Tfin%d" % i, [64, 8, 64], BF16) for i in range(NSET)]
            SCk = self.sb(st, "SCk", [128, 2, 8, 64], BF16)
            Lms = [[self.sb(st, "Lm%d_%d" % (i, k), [64, 8, 64], F32R) for i in range(2)] for k in range(NSET)]
            Ltms = [[self.sb(st, "Ltm%d_%d" % (i, k), [64, 8, 64], F32R) for i in range(2)] for k in range(NSET)]
            ILms = [self.sb(st, "ILm_%d" % k, [64, 8, 64], F32R) for k in range(NSET)]
            Ttms = [[self.sb(st, "Ttm%d_%d" % (i, k), [64, 8, 64], F32R) for i in range(2)] for k in range(NSET)]

            ones_bf, oblk, ident = self.ones_bf, self.oblk_bf, self.ident
            tiles = [(0, t0, True, 0) for t0 in range(0, TS, TT)] + [(1, TS, False, 1), (2, TS + TP, False, 1)]
            mS1 = cst[0:64, C_MSI1:C_MSI1 + 128].rearrange("p (q t) -> p q t", q=2)
            mL1 = cst[0:64, C_ML1:C_ML1 + 64]
            id64 = idS[0:64]
            def front(seq, g0, is_s, mc):
                if is_s:
                    src = self.xs[g0:g0 + TT, :].rearrange("(s p) f -> p s f", p=128)
                else:
                    l0 = g0 - TS
                    src = self.xp[l0:l0 + TT, :].rearrange("(s p) f -> p s f", p=128)
                self.dma(xin[:], src, [], [xin])
                if is_s:
                    petv = xT[:].rearrange("p a b -> p (a b)").rearrange("p (s f) -> p s f", s=2)
                    self.dma(petv, self.pe[g0:g0 + TT, :].rearrange("(s p) f -> p s f", p=128), [], [xT], eng="act")
                    self.tt("pool", xin[:], xin[:], petv, ALU.add, [xin, xT], [xin])
                for j in range(8):
                    p = self.ps()
                    for s in range(2):
                        self.tr(p[:, s * 128:(s + 1) * 128], xin[:, s, j * 128:(j + 1) * 128], ident[:], [xin, ident], [p])
                    self.cp("act", xT[:, j, :], p[:, 0:TT], [p], [xT])
                    self.tt("pool", sq[:, j, :], xT[:, j, :], xT[:, j, :], ALU.mult, [xT], [sq])
                    yield
                self.dma(self.xT_scr[:, :, g0:g0 + TT], xT[:], [xT], [self.xT_scr])
                p = self.ps()
                for j in range(8):
                    self.mm(p[:, 0:TT], ones_bf[:], sq[:, j, :], j == 0, j == 7, [ones_bf, sq], [p])
                self.act(rstd[:], p[:, 0:TT], AF.Sqrt, [p, self.epsT], [rstd], scale=1.0 / D, bias=self.epsT[:, 0:1])
                self.S.op("dve", lambda e: e.reciprocal(out=rstd[:], in_=rstd[:]), [rstd.b], [rstd.b])
                for j in range(8):
                    t = tmpA[j % 2]
                    self.tt("dve", t[:], xT[:, j, :], rstd[:], ALU.mult, [xT, rstd], [t])
                    self.act(hT[:, j, :], t[:], AF.Identity, [t, self.gs1, self.modT], [hT],
                             scale=self.gs1[:, j, mc:mc + 1], bias=self.modT[:, j, mc:mc + 1])
                    yield
                for m in range(23):
                    if m >= self.cutm:
                        break
                    p = self.ps()
                    for kc in range(8):
                        self.mm(p[:, 0:TT], win[:, kc, m * 128:(m + 1) * 128], hT[:, kc, :], kc == 0, kc == 7, [win, hT], [p])
                    pz = p[:, 0:TT]
                    if m < 4:
                        self.cp("act", rT[:, m, :], pz, [p], [rT])
                    elif m < 8:
                        self.cp("act", kT[:, m - 4, :], pz, [p], [kT])
                    elif m < 12:
                        self.cp("act", vT[:, m - 8, :], pz, [p], [vT])
                    elif m == 12:
                        self.act(xw[:], pz, AF.Tanh, [p], [xw])
                    elif m == 13:
                        self.cp("act", xa[:], pz, [p], [xa])
                    elif m == 14:
                        self.act(xg[:], pz, AF.Sigmoid, [p], [xg])
                    elif m < 19:
                        self.cp("act", xbT[:, m - 15, :], pz, [p], [xbT])
                    else:
                        j = m - 19
                        g0_, g1_, g2_ = gtmp
                        self.cp("act", g0_[:], pz, [p], [g0_])
                        self.tt("pool", g1_[:], g0_[:], g0_[:], ALU.mult, [g0_], [g1_])
                        self.tsc("dve", g1_[:], g1_[:], 0.044715, ALU.mult, [g1_], [g1_], 1.0, ALU.add)
                        self.tt("dve", g1_[:], g1_[:], g0_[:], ALU.mult, [g1_, g0_], [g1_])
                        self.act(g2_[:], g1_[:], AF.Sigmoid, [g1_], [g2_], scale=GELU_C)
                        self.tt("pool", gate[:, j, :], g0_[:], g2_[:], ALU.mult, [g0_, g2_], [gate])
                    yield
                self.dma(self.xb_scr[:, :, g0:g0 + TT], xbT[:], [xbT], [self.xb_scr])
                if self.debug and g0 == 0:
                    self.dump("hT", hT[:], [128, 8, TT], BF16, [hT])
                    self.dump("rT", rT[:], [128, 4, TT], F32, [rT])
                    self.dump("vT", vT[:], [128, 4, TT], F32, [vT])
                    self.dump("xbT", xbT[:], [128, 4, TT], F32, [xbT])
                    self.dump("gate", gate[:], [128, 4, TT], BF16, [gate])
                self.dma(self.gate_scr[:, :, g0:g0 + TT], gate[:], [gate], [self.gate_scr])
                for j in range(4):
                    p = self.ps()
                    self.mm(p[:, 0:TT], gup[:, j * 128:(j + 1) * 128], xg[:], True, True, [gup, xg], [p])
                    self.cp("act", gT[:, j, :], p[:, 0:TT], [p], [gT])
                    yield
                self.dma(self.g_scr[:, :, g0:g0 + TT], gT[:], [gT], [self.g_scr])
                for j in range(4):
                    self.tsc("dve", kkn[:, j, :], kT[:, j, :], prm[:, P_KK + j:P_KK + j + 1], ALU.mult, [kT, prm], [kkn])
                    self.tt("pool", sq[:, j, :], kkn[:, j, :], kkn[:, j, :], ALU.mult, [kkn], [sq])
                for j in range(4):
                    p = self.ps()
                    self.mm(p[:, 0:TT], oblk[:], sq[:, j, :], True, True, [oblk, sq], [p])
                    t = tmpA[j % 2]
                    self.act(t[:], p[:, 0:TT], AF.Sqrt, [p], [t])
                    self.tsc("dve", t[:], t[:], 1e-12, ALU.max, [t], [t])
                    self.S.op("dve", lambda e, t=t: e.reciprocal(out=t[:], in_=t[:]), [t.b], [t.b])
                    self.tt("dve", kkn[:, j, :], kkn[:, j, :], t[:], ALU.mult, [kkn, t], [kkn])
                    yield
                for s in range(2):
                    p = self.ps()
                    for j in range(4):
                        self.tr(p[:, j * 128:(j + 1) * 128], vT[:, j, s * 128:(s + 1) * 128], ident[:], [vT, ident], [p])
                    self.cp("act", tokV[:, s, :, :].rearrange("p h k -> p (h k)"), p[:], [p], [tokV])
                c0 = g0 // 64
                for s in range(2):
                    dst = self.vt_scr[c0 + 2 * s:c0 + 2 * s + 2, :, :].rearrange("c s f -> (c s) f")
                    self.dma(dst, tokV[:, s, :, :].rearrange("p h k -> p (h k)"), [tokV], [self.vt_scr])
                yield
            def prep(g0, d):
                c0 = g0 // 64
                for j in range(4):
                    p = self.ps()
                    self.mm(p[:, 0:TT], wup[d * 64:(d + 1) * 64, j * 128:(j + 1) * 128], xw[d * 64:(d + 1) * 64, :],
                            True, True, [wup, xw], [p])
                    self.act(sg[:, j, :], p[:, 0:TT], AF.Sigmoid, [p, prm], [sg],
                             bias=prm[:, P_W0 + 4 * d + j:P_W0 + 4 * d + j + 1])
                    if d == 0:
                        self.scan(cs[:, j, :], cst[:, C_RMF:C_RMF + TT], sg[:, j, :], 0.0, [cst, sg], [cs])
                    else:
                        self.scan(cs[:, j, ::-1], cst[:, C_RMB:C_RMB + TT][:, ::-1], sg[:, j, ::-1], 0.0, [cst, sg], [cs])
                    yield
                for j in range(4):
                    p = self.ps()
                    self.mm(p[:, 0:TT], aup[d * 64:(d + 1) * 64, j * 128:(j + 1) * 128], xa[d * 64:(d + 1) * 64, :],
                            True, True, [aup, xa], [p])
                    self.act(ad[:, j, :], p[:, 0:TT], AF.Sigmoid, [p, prm], [ad],
                             bias=prm[:, P_A0 + 4 * d + j:P_A0 + 4 * d + j + 1])
                    yield
                self.tt("pool", sg[:], cs[:], sg[:], ALU.subtract, [cs, sg], [sg])
                self.act(E1[:], cs[:], AF.Exp, [cs], [E1], scale=-LAM)
                self.act(cs[:], cs[:], AF.Exp, [cs], [cs], scale=LAM)
                self.act(sg[:], sg[:], AF.Exp, [sg], [sg], scale=-LAM)
                E2, E3 = cs, sg
                ar5 = AR[d]
                bk5 = BK[d]
                v4 = lambda tl: tl[:].rearrange("p j (c t) -> p j c t", t=64)
                self.stt(ar5[:, :, :, 0, :], v4(kkn), -1.0, v4(E3), ALU.mult, ALU.mult, [kkn, E3], [ar5])
                self.tt("pool", ar5[:, :, :, 1, :], v4(rT), v4(E1), ALU.mult, [rT, E1], [ar5])
                yield
                self.tt("dve", wk1[:], kkn[:], ad[:], ALU.mult, [kkn, ad], [wk1])
                self.tt("dve", wk1[:], wk1[:], E2[:], ALU.mult, [wk1, E2], [wk1])
                self.cp("pool", bk5[:, :, :, 0, :], v4(wk1), [wk1], [bk5])
                yield
                for j in range(4):
                    self.tsc("dve", wk2[:, j, :], ad[:, j, :], prm[:, P_KA + j:P_KA + j + 1], ALU.mult, [ad, prm, self.misc], [wk2],
                             self.misc[:, j:j + 1], ALU.add)
                self.tt("pool", wk2[:], wk2[:], kT[:], ALU.mult, [wk2, kT], [wk2])
                if d == 0:
                    self.cp("pool", ksum[:], wk2[:], [wk2], [ksum])
                else:
                    self.tt("pool", ksum[:], ksum[:], wk2[:], ALU.add, [ksum, wk2], [ksum])
                self.tt("dve", wk2[:], wk2[:], E2[:], ALU.mult, [wk2, E2], [wk2])
                self.cp("pool", bk5[:, :, :, 1, :], v4(wk2), [wk2], [bk5])
                yield
                te = 63 if d == 0 else 0
                pend_b = v4(E1)[:, :, :, te:te + 1].to_broadcast([128, 4, NC4, 64])
                self.cp("pool", PEb[d][:], pend_b, [E1], [PEb[d]])
                self.tt("dve", v4(wk1), v4(wk1), PEb[d][:], ALU.mult, [wk1, PEb[d]], [wk1])
                self.tt("dve", v4(wk2), v4(wk2), PEb[d][:], ALU.mult, [wk2, PEb[d]], [wk2])
                yield
                for (srcw, tokX, scr) in ((wk1, tokB[d], self.bh_scr), (wk2, tokK[d], self.kh_scr)):
                    for s in range(2):
                        p = self.ps()
                        for j in range(4):
                            self.tr(p[:, j * 128:(j + 1) * 128], srcw[:, j, s * 128:(s + 1) * 128], ident[:], [srcw, ident], [p])
                        self.cp("act", tokX[:, s, :, :].rearrange("p h k -> p (h k)"), p[:], [p], [tokX])
                        for cc in range(2):
                            self.dma(scr[c0 + 2 * s + cc, d * 64:(d + 1) * 64, :],
                                     tokX[cc * 64:(cc + 1) * 64, s, :, :].rearrange("p h k -> p (h k)"),
                                     [tokX], [scr])
                        yield
                for cl in range(NC4):
                    c = c0 + cl
                    for hp in range(2):
                        for (q, scr) in ((0, self.art_scr), (1, self.rrt_scr)):
                            dst = scr[c, d * 64:(d + 1) * 64, :].rearrange("k (j hp t) -> k j hp t", hp=2, t=64)[:, :, hp, :]
                            self.dma(dst, ar5[hp * 64:(hp + 1) * 64, :, cl, q, :], [ar5], [scr], eng="sp")
                        dst = self.pend_scr[c, d * 64:(d + 1) * 64, :].rearrange("k (j hp t) -> k j hp t", hp=2, t=64)[:, :, hp, :]
                        self.dma(dst, PEb[d][hp * 64:(hp + 1) * 64, :, cl, :], [PEb[d]], [self.pend_scr], eng="sp")
                yield
            def bonus(g0):
                for j in range(4):
                    self.stt(sq[:, j, :], rT[:, j, :], prm[:, P_RK + j:P_RK + j + 1], ksum[:, j, :], ALU.mult, ALU.mult,
                             [rT, prm, ksum], [sq])
                    p = self.ps()
                    self.mm(p[:, 0:TT], oblk[:], sq[:, j, :], True, True, [oblk, sq], [p])
                    self.tt("dve", bon[:, j, :], p[:, 0:TT], vT[:, j, :], ALU.mult, [p, vT], [bon])
                self.dma(self.bon_scr[:, :, g0:g0 + TT], bon[:], [bon], [self.bon_scr])
                yield
            def chunk_sck(g0):
                c0 = g0 // 64
                for cl in range(NC4):
                    c = c0 + cl
                    for hp in range(2):
                        p = self.ps()
                        for j in range(4):
                            for d in range(2):
                                self.mm(p[d * 64:(d + 1) * 64, j * 128:(j + 1) * 128],
                                        BK[d][hp * 64:(hp + 1) * 64, j, cl, 1, :],
                                        AR[d][hp * 64:(hp + 1) * 64, j, cl, :, :].rearrange("p q t -> p (q t)"),
                                        True, True, [BK[d], AR[d]], [p])
                        self.tt("dve", SCk[:, :, hp::2, :].rearrange("p q h t -> p h q t"),
                                p[:].rearrange("p (h q t) -> p h q t", q=2, t=64),
                                mSI.unsqueeze(1).to_broadcast([128, 4, 2, 64]), ALU.mult, [p, cst], [SCk])
                    self.dma(self.akt_scr[c], SCk[:, 0, :, :].rearrange("p h s -> p (h s)"), [SCk], [self.akt_scr], eng="act")
                    self.dma(self.mrkt_scr[c], SCk[:, 1, :, :].rearrange("p h s -> p (h s)"), [SCk], [self.mrkt_scr], eng="act")
                    yield
            def chunk_d(g0, d, cls, k):
                c0 = g0 // 64
                Lm, Ltm, ILm, Ttm, Lt0, Tfin = Lms[k], Ltms[k], ILms[k], Ttms[k], Lt0s[k], Tfins[k]
                MRB = MRBs[k]
                for cl in cls:
                    c = c0 + cl
                    msk = mSI[0:64] if d == 0 else mS1
                    mskL = mL[0:64] if d == 0 else mL1
                    L0 = Lm[0]
                    for hp in range(2):
                        p = self.ps()
                        for j in range(4):
                            self.mm(p[0:64, j * 128:(j + 1) * 128],
                                    BK[d][hp * 64:(hp + 1) * 64, j, cl, 0, :],
                                    AR[d][hp * 64:(hp + 1) * 64, j, cl, :, :].rearrange("p q t -> p (q t)"),
                                    True, True, [BK[d], AR[d]], [p])
                        p4 = p[0:64, :].rearrange("p (h q t) -> p h q t", q=2, t=64)
                        self.tt("dve", Lt0[:, hp::2, :], p4[:, :, 0, :], msk[:, 0, :].unsqueeze(1).to_broadcast([64, 4, 64]),
                                ALU.mult, [p, cst], [Lt0])
                        self.tt("dve", MRB[:, hp::2, :], p4[:, :, 1, :], msk[:, 1, :].unsqueeze(1).to_broadcast([64, 4, 64]),
                                ALU.mult, [p, cst], [MRB])
                        p2 = self.ps()
                        for j in range(4):
                            self.mm(p2[0:64, j * 64:(j + 1) * 64],
                                    AR[d][hp * 64:(hp + 1) * 64, j, cl, 0, :], BK[d][hp * 64:(hp + 1) * 64, j, cl, 0, :],
                                    True, True, [AR[d], BK[d]], [p2])
                        self.tt("dve", L0[:, hp::2, :], p2[0:64, 0:256].rearrange("p (h s) -> p h s", s=64),
                                mskL.unsqueeze(1).to_broadcast([64, 4, 64]), ALU.mult, [p2, cst], [L0])
                    self.dma(self.mrbt_scr[c, d * 64:(d + 1) * 64, :], MRB[:].rearrange("p h s -> p (h s)"),
                             [MRB], [self.mrbt_scr], eng="act")
                    yield
                    T0 = Ttm[0]
                    self.tt("pool", T0[:], Lt0[:].bitcast(F32), id64.unsqueeze(1).to_broadcast([64, 8, 64]), ALU.add,
                            [Lt0, cst], [T0])
                    L_prev, Tt_prev, Lt_prev = L0, T0, Lt0
                    for lev in range(1, 6):
                        L_new, Lt_new, Tt_new = Lm[lev % 2], Ltm[lev % 2], Ttm[lev % 2]
                        pA = self.ps()
                        for h in range(8):
                            self.mm(pA[0:64, h * 64:(h + 1) * 64], Lt_prev[:, h, :], L_prev[:, h, :], True, True,
                                    [Lt_prev, L_prev], [pA])
                        if lev < 5:
                            pB = self.ps()
                            for h in range(8):
                                self.mm(pB[0:64, h * 64:(h + 1) * 64], L_prev[:, h, :], Lt_prev[:, h, :], True, True,
                                        [Lt_prev, L_prev], [pB])
                        self.tt("dve", ILm[:], pA[0:64, :].rearrange("p (h s) -> p h s", s=64),
                                id64.unsqueeze(1).to_broadcast([64, 8, 64]), ALU.add, [pA, cst], [ILm])
                        if lev < 5:
                            self.cp("act", L_new[:].rearrange("p h s -> p (h s)"), pA[0:64, :], [pA], [L_new])
                            self.cp("act", Lt_new[:].rearrange("p h s -> p (h s)"), pB[0:64, :], [pB], [Lt_new])
                        pC = self.ps()
                        for h in range(8):
                            self.mm(pC[0:64, h * 64:(h + 1) * 64], ILm[:, h, :], Tt_prev[:, h, :], True, True,
                                    [ILm, Tt_prev], [pC])
                        if lev < 5:
                            self.cp("act", Tt_new[:].rearrange("p h s -> p (h s)"), pC[0:64, :], [pC], [Tt_new])
                        else:
                            self.cp("act", Tfin[:].rearrange("p h s -> p (h s)"), pC[0:64, :], [pC], [Tfin])
                        L_prev, Tt_prev, Lt_prev = L_new, Tt_new, Lt_new
                        yield
                    self.dma(self.ttt_scr[c, d * 64:(d + 1) * 64, :], Tfin[:].rearrange("p h s -> p (h s)"),
                             [Tfin], [self.ttt_scr], eng="act")


            def run_all(*gens):
                gens = list(gens)
                while gens:
                    for g in list(gens):
                        try:
                            next(g)
                        except StopIteration:
                            gens.remove(g)

            def seq_(*gens):
                for g in gens:
                    yield from g

            prev = None
            for (seq, g0, is_s, mc) in tiles[:self.ktiles]:
                if prev is None:
                    run_all(front(seq, g0, is_s, mc))
                else:
                    run_all(seq_(chunk_d(prev, 1, [0, 1], 0), chunk_sck(prev)), chunk_d(prev, 1, [2, 3], 1), front(seq, g0, is_s, mc))
                run_all(prep(g0, 0))
                run_all(chunk_d(g0, 0, [0, 1], 0), chunk_d(g0, 0, [2, 3], 1), prep(g0, 1))
                run_all(bonus(g0))
                prev = g0
            run_all(seq_(chunk_d(prev, 1, [0, 1], 0), chunk_sck(prev)), chunk_d(prev, 1, [2, 3], 1))

    def phaseB(self):
        with contextlib.ExitStack() as st:
            side = [self.gen_c0(st), self.phaseB_lru(st)]
            post = self.phaseB_post(st)
            next(post)
            done = np.zeros((2, NCH), bool)
            posted = [False] * (NTOK // 128)
            si = 0
            for info in self.phaseB_chain(st):
                for (d, c) in info:
                    done[d, c] = True
                for _ in range(2):
                    if side:
                        g = side[si % len(side)]
                        si += 1
                        try:
                            next(g)
                        except StopIteration:
                            side.remove(g)
                for b in range(NTOK // 128):
                    if not posted[b] and done[:, 2 * b:2 * b + 2].all():
                        posted[b] = True
                        post.send(b)
            for g in side:
                for _ in g:
                    pass
            for b in range(NTOK // 128):
                if not posted[b]:
                    post.send(b)
            if self.debug:
                self.S.barrier()
                self.dump("yscr", self.y_scr[:], [128, 8, NTOK], BF16, [self.y_scr])

    def gen_c0(self, st):
        stg = [self.sb(st, "stgC%d" % i, [128, 8, 512], F32) for i in range(2)]
        wb = [self.sb(st, "wbC%d" % i, [128, 8, 512], BF16) for i in range(2)]
        w1src = self.w1[:].rearrange("(kc p) n -> p kc n", p=128)
        for blk in range(8):
            s, o = stg[blk % 2], wb[blk % 2]
            self.dma(s[:], w1src[:, :, blk * 512:(blk + 1) * 512], [], [s])
            self.cp("pool", o[:], s[:], [s], [o])
            self.dma(self.w1_scr[blk], o[:], [o], [self.w1_scr], eng="act")
            yield
        w2src = self.w2[:].rearrange("(fc p) n -> p fc n", p=128)
        for m in range(8):
            s, o = stg[m % 2], wb[m % 2]
            s4 = s[:].rearrange("p k (a b) -> p (k a) b", b=128)
            o4 = o[:].rearrange("p k (a b) -> p (k a) b", b=128)
            self.dma(s4, w2src[:, :, m * 128:(m + 1) * 128], [], [s])
            self.cp("pool", o[:], s[:], [s], [o])
            self.dma(self.w2_scr[m], o4, [o], [self.w2_scr], eng="act")
            yield

    def phaseB_lru(self, st):
        prm, cst, misc = self.prm_t, self.cst_t, self.misc
        if True:
            wbd32 = self.sb(st, "wbd32", [128, 16, 128], F32)
            wbd = self.sb(st, "wbd", [128, 16, 128], BF16)
            self.memset("pool", wbd32[:], 0.0, [wbd32])
            self.S.barrier_on(wbd32)
            wbd32.b.multi = True
            for gi, src in enumerate((self.lwa, self.lwx)):
                for d in range(2):
                    for j in range(4):
                        for hb in range(2):
                            self.dma(wbd32[hb * 64:(hb + 1) * 64, (gi * 2 + d) * 4 + j, hb * 64:(hb + 1) * 64],
                                     src[d, 2 * j + hb], [], [wbd32])
            self.cp("dve", wbd[:], wbd32[:], [wbd32], [wbd])
            TM = TS
            xbp = self.sb(st, "xbp", [128, TM + 4], F32)
            xc = self.sb(st, "xc", [128, TM], F32)
            xcb = self.sb(st, "xcb", [128, TM], BF16)
            gt = self.sb(st, "gt_l", [128, TM], BF16)
            a_t = self.sb(st, "a_t", [128, TM], F32)
            bx_t = self.sb(st, "bx_t", [128, TM], F32)
            s_t = self.sb(st, "s_t", [128, TM], F32)
            hs = [self.sb(st, "hs%d" % d, [128, TM], F32) for d in range(2)]
            yb = self.sb(st, "yb", [128, TM], BF16)
            for (seq, g0, T) in ((0, 0, TS), (1, TS, TP), (2, TS + TP, TP)):
                for j in range(4):
                    self.memset("pool", xbp[:, 0:2], 0.0, [xbp])
                    self.memset("pool", xbp[:, T + 2:T + 4], 0.0, [xbp])
                    self.dma(xbp[:, 2:T + 2], self.xb_scr[:, j, g0:g0 + T], [self.xb_scr], [xbp])
                    self.dma(gt[:, 0:T], self.gate_scr[:, j, g0:g0 + T], [self.gate_scr], [gt], eng="act")
                    cw = lambda i: prm[:, P_CW + 4 * i + j:P_CW + 4 * i + j + 1]
                    self.act(xc[:, 0:T], xbp[:, 0:T], AF.Identity, [xbp, prm], [xc], scale=cw(0), bias=prm[:, P_CB + j:P_CB + j + 1])
                    for i in range(1, 4):
                        self.stt(xc[:, 0:T], xbp[:, i:i + T], cw(i), xc[:, 0:T], ALU.mult, ALU.add, [xbp, prm, xc], [xc])
                    self.cp("pool", xcb[:, 0:T], xc[:, 0:T], [xc], [xcb])
                    yield
                    for d in range(2):
                        for t0 in range(0, T, 512):
                            tw = min(512, T - t0)
                            p = self.ps()
                            self.mm(p[:, 0:tw], wbd[:, (0 * 2 + d) * 4 + j, :], xcb[:, t0:t0 + tw], True, True, [wbd, xcb], [p])
                            self.act(s_t[:, t0:t0 + tw], p[:, 0:tw], AF.Sigmoid, [p, prm], [s_t],
                                     bias=prm[:, P_BA + 4 * d + j:P_BA + 4 * d + j + 1])
                            p2 = self.ps()
                            self.mm(p2[:, 0:tw], wbd[:, (1 * 2 + d) * 4 + j, :], xcb[:, t0:t0 + tw], True, True, [wbd, xcb], [p2])
                            self.act(bx_t[:, t0:t0 + tw], p2[:, 0:tw], AF.Sigmoid, [p2, prm], [bx_t],
                                     bias=prm[:, P_BX + 4 * d + j:P_BX + 4 * d + j + 1])
                        col = 4 + d * 4 + j
                        self.act(a_t[:, 0:T], s_t[:, 0:T], AF.Exp, [s_t, misc], [a_t], scale=misc[:, col:col + 1])
                        self.act(s_t[:, 0:T], s_t[:, 0:T], AF.Exp, [s_t, misc], [s_t], scale=misc[:, col + 8:col + 9])
                        self.act(s_t[:, 0:T], s_t[:, 0:T], AF.Sqrt, [s_t], [s_t], scale=-1.0, bias=1.0)
                        self.tt("pool", bx_t[:, 0:T], bx_t[:, 0:T], xc[:, 0:T], ALU.mult, [bx_t, xc], [bx_t])
                        self.tt("dve", bx_t[:, 0:T], bx_t[:, 0:T], s_t[:, 0:T], ALU.mult, [bx_t, s_t], [bx_t])
                        h = hs[d]
                        if seq == 0:
                            init = prm[:, P_H0 + 4 * d + j:P_H0 + 4 * d + j + 1]
                        else:
                            init = 0.0
                        if d == 0:
                            self.scan(h[:, 0:T], a_t[:, 0:T], bx_t[:, 0:T], init, [a_t, bx_t, prm], [h])
                        else:
                            self.scan(h[:, 0:T][:, ::-1], a_t[:, 0:T][:, ::-1], bx_t[:, 0:T][:, ::-1], init, [a_t, bx_t, prm], [h])
                        if seq > 0:
                            col_o = j * 4 + (seq - 1) * 2 + d
                            te = T - 1 if d == 0 else 0
                            self.cp("pool", self.stl_t[:, col_o:col_o + 1], h[:, te:te + 1], [h], [self.stl_t])
                        yield
                    self.tt("pool", hs[0][:, 0:T], hs[0][:, 0:T], hs[1][:, 0:T], ALU.add, [hs[0], hs[1]], [hs[0]])
                    self.tt("dve", yb[:, 0:T], hs[0][:, 0:T], gt[:, 0:T], ALU.mult, [hs[0], gt], [yb])
                    self.dma(self.y_scr[:, 4 + j, g0:g0 + T], yb[:, 0:T], [yb], [self.y_scr])
            self.dma(self.stl_o[:], self.stl_t[:], [self.stl_t], [self.stl_o])

    def phaseB_chain(self, st):
        if True:
            NB = 3
            def ring(name, dt=BF16):
                return [self.sb2(st, "%s%d" % (name, i), [128, 512], dt) for i in range(NB)]
            art, rrt, ttt, akt, mrbt, mrkt, bh, kh, vt = [ring(n) for n in
                                                          ("c_art", "c_rrt", "c_ttt", "c_akt", "c_mrbt", "c_mrkt", "c_bh", "c_kh", "c_vt")]
            pend = ring("c_pend", F32)
            Hf = self.sb2(st, "Hf", [128, 512], F32)
            Hb = self.sb2(st, "Hb", [128, 512], BF16)
            Zs = self.sb2(st, "Zs", [128, 512], BF16)
            Us = self.sb2(st, "Us", [128, 512], BF16)
            Yt = [self.sb2(st, "Yt%d" % i, [128, 512], F32) for i in range(2)]
            tmpH = self.sb2(st, "tmpH", [128, 512], F32)
            hs_ = lambda h: slice(h * 64, (h + 1) * 64)
            steps = []
            for (seq, cbase, n) in ((0, 0, 32), (1, 32, 4), (2, 36, 4)):
                for i in range(n):
                    steps.append((seq, cbase, n, i))

            def loads(k):
                seq, cbase, n, i = steps[k]
                r = k % NB
                for d in range(2):
                    sl = slice(d * 64, (d + 1) * 64)
                    c = cbase + i if d == 0 else cbase + n - 1 - i
                    for (tl, scr) in ((art, self.art_scr), (rrt, self.rrt_scr), (ttt, self.ttt_scr), (akt, self.akt_scr),
                                      (mrbt, self.mrbt_scr), (mrkt, self.mrkt_scr), (bh, self.bh_scr), (kh, self.kh_scr),
                                      (pend, self.pend_scr)):
                        self.dma(tl[r][d][sl, :], scr[c, sl, :], [scr], [tl[r][d]], eng="sp")
                    self.dma(vt[r][d][sl, :], self.vt_scr[c], [self.vt_scr], [vt[r][d]], eng="sp")

            loads(0)
            for k in range(len(steps)):
                seq, cbase, n, i = steps[k]
                step = k + 1
                r = k % NB
                if k + 1 < len(steps):
                    loads(k + 1)
                if i == 0:
                    for d in range(2):
                        sl = slice(d * 64, (d + 1) * 64)
                        if seq == 0:
                            self.dma(Hf[d][sl, :], self.h0r[sl, :], [], [Hf[d]], eng="act")
                        else:
                            self.memset("dve", Hf[d][sl, :], 0.0, [Hf[d]])
                        self.cp("dve", Hb[d][sl, :], Hf[d][sl, :], [Hf[d]], [Hb[d]])
                if True:
                    ctx = []
                    for d in range(2):
                        sl = slice(d * 64, (d + 1) * 64)
                        c = cbase + i if d == 0 else cbase + n - 1 - i
                        ops = tuple(x[r][d] for x in (art, rrt, ttt, akt, mrbt, mrkt, bh, kh, vt, pend))
                        ctx.append((d, sl, c, ops))
                    pZs = {}
                    for (d, sl, c, (A_, R_, T_, AK_, MRB_, MRK_, B_, K_, V_, PE_)) in ctx:
                        H_, Z_ = Hb[d], Zs[d]
                        pZ = self.ps()
                        for h in range(8):
                            self.mm(pZ[sl, hs_(h)], A_[sl, hs_(h)], H_[sl, hs_(h)], True, False, [A_, H_], [pZ])
                            self.mm(pZ[sl, hs_(h)], AK_[sl, hs_(h)], V_[sl, hs_(h)], False, True, [AK_, V_], [pZ])
                        pZs[d] = pZ
                    pYs = {}
                    for (d, sl, c, (A_, R_, T_, AK_, MRB_, MRK_, B_, K_, V_, PE_)) in ctx:
                        H_ = Hb[d]
                        pY = self.ps()
                        pYs[d] = pY
                    for (d, sl, c, ops) in ctx:
                        self.cp("act", Zs[d][sl, :], pZs[d][sl, :], [pZs[d]], [Zs[d]])
                    pUs = {}
                    for (d, sl, c, (A_, R_, T_, AK_, MRB_, MRK_, B_, K_, V_, PE_)) in ctx:
                        Z_ = Zs[d]
                        pU = self.ps()
                        for h in range(8):
                            self.mm(pU[sl, hs_(h)], T_[sl, hs_(h)], Z_[sl, hs_(h)], True, True, [T_, Z_], [pU])
                        pUs[d] = pU
                    for (d, sl, c, ops) in ctx:
                        self.cp("act", Us[d][sl, :], pUs[d][sl, :], [pUs[d]], [Us[d]])
                    pHs = {}
                    for (d, sl, c, (A_, R_, T_, AK_, MRB_, MRK_, B_, K_, V_, PE_)) in ctx:
                        U_ = Us[d]
                        self.tt("dve", tmpH[d][sl, :], Hf[d][sl, :], PE_[sl, :], ALU.mult, [Hf[d], PE_], [tmpH[d]])
                        pH = self.ps()
                        for h in range(8):
                            self.mm(pH[sl, hs_(h)], B_[sl, hs_(h)], U_[sl, hs_(h)], True, False, [B_, U_], [pH])
                            self.mm(pH[sl, hs_(h)], K_[sl, hs_(h)], V_[sl, hs_(h)], False, True, [K_, V_], [pH])
                        pHs[d] = pH
                    for (d, sl, c, (A_, R_, T_, AK_, MRB_, MRK_, B_, K_, V_, PE_)) in ctx:
                        H_, U_ = Hb[d], Us[d]
                        pY = pYs[d]
                        for h in range(8):
                            self.mm(pY[sl, hs_(h)], R_[sl, hs_(h)], H_[sl, hs_(h)], True, False, [R_, H_], [pY])
                            self.mm(pY[sl, hs_(h)], MRB_[sl, hs_(h)], U_[sl, hs_(h)], False, False, [MRB_, U_], [pY])
                            self.mm(pY[sl, hs_(h)], MRK_[sl, hs_(h)], V_[sl, hs_(h)], False, True, [MRK_, V_], [pY])
                    for (d, sl, c, ops) in ctx:
                        self.tt("dve", Hf[d][sl, :], tmpH[d][sl, :], pHs[d][sl, :], ALU.add, [tmpH[d], pHs[d]], [Hf[d]])
                        self.cp("dve", Hb[d][sl, :], Hf[d][sl, :], [Hf[d]], [Hb[d]])
                    for (d, sl, c, ops) in ctx:
                        y = Yt[step % 2][d]
                        self.cp("act", y[sl, :], pYs[d][sl, :], [pYs[d]], [y])
                        self.dma(self.ytok_scr[d, c * 64:(c + 1) * 64, :], y[sl, :], [y], [self.ytok_scr], eng="act")
                if seq > 0 and i == n - 1:
                    self.dma(self.str_o[seq - 1], Hf[0][:], [Hf[0], Hf[1]], [self.str_o], eng="act")
                yield [(0, cbase + i), (1, cbase + n - 1 - i)]

    def phaseB_post(self, st):
        prm, cst = self.prm_t, self.cst_t
        ident = self.ident
        if True:
            yf = [self.sb(st, "yf%d" % i, [128, 512], F32) for i in range(2)]
            yb2 = [self.sb(st, "yb2%d" % i, [128, 512], F32) for i in range(2)]
            cen = self.sb(st, "cen", [128, 8, 64], F32)
            sqv = self.sb(st, "sqv", [128, 8, 64], F32)
            mean = self.sb(st, "mean", [128, 8], F32)
            var = self.sb(st, "var", [128, 8], F32)
            gl = [self.sb(st, "gl%d" % i, [128, 4, 128], BF16) for i in range(2)]
            bl = [self.sb(st, "bl%d" % i, [128, 4, 128], BF16) for i in range(2)]
            ynT = self.sb(st, "ynT", [128, 4, 128], F32)
            yo = [self.sb(st, "yo%d" % i, [128, 4, 128], BF16) for i in range(2)]
            it = -1
            blk = yield
            while True:
                it += 1
                g0 = blk * 128
                a, b = yf[it % 2], yb2[it % 2]
                g_, b_ = gl[it % 2], bl[it % 2]
                o = yo[it % 2]
                self.dma(a[:], self.ytok_scr[0, g0:g0 + 128, :], [self.ytok_scr], [a])
                self.dma(b[:], self.ytok_scr[1, g0:g0 + 128, :], [self.ytok_scr], [b], eng="act")
                self.dma(g_[:], self.g_scr[:, :, g0:g0 + 128], [self.g_scr], [g_])
                self.dma(b_[:], self.bon_scr[:, :, g0:g0 + 128], [self.bon_scr], [b_], eng="act")
                a3 = a[:].rearrange("p (h v) -> p h v", v=64)
                self.tt("pool", a[:], a[:], b[:], ALU.add, [a, b], [a])
                self.S.op("dve", lambda e, a3=a3: e.tensor_reduce(out=mean[:], in_=a3, op=ALU.add, axis=mybir.AxisListType.X),
                          [a.b], [mean.b])
                self.tsc("dve", mean[:], mean[:], 1.0 / 64, ALU.mult, [mean], [mean])
                self.tt("dve", cen[:], a3, mean[:].unsqueeze(2).to_broadcast([128, 8, 64]), ALU.subtract, [a, mean], [cen])
                self.tt("pool", sqv[:], cen[:], cen[:], ALU.mult, [cen], [sqv])
                self.S.op("dve", lambda e: e.tensor_reduce(out=var[:], in_=sqv[:], op=ALU.add, axis=mybir.AxisListType.X),
                          [sqv.b], [var.b])
                self.act(var[:], var[:], AF.Sqrt, [var, self.epsT], [var], scale=1.0 / 64, bias=self.epsT[:, 1:2])
                self.S.op("dve", lambda e: e.reciprocal(out=var[:], in_=var[:]), [var.b], [var.b])
                self.tt("dve", cen[:], cen[:], var[:].unsqueeze(2).to_broadcast([128, 8, 64]), ALU.mult, [cen, var], [cen])
                p = self.ps()
                cen2 = cen[:].rearrange("p h v -> p (h v)")
                for j in range(4):
                    self.tr(p[:, j * 128:(j + 1) * 128], cen2[:, j * 128:(j + 1) * 128], ident[:], [cen, ident], [p])
                for j in range(4):
                    self.act(ynT[:, j, :], p[:, j * 128:(j + 1) * 128], AF.Identity, [p, prm], [ynT],
                             scale=prm[:, P_LNG + j:P_LNG + j + 1], bias=prm[:, P_LNB + j:P_LNB + j + 1])
                self.tt("pool", ynT[:], ynT[:], b_[:], ALU.add, [ynT, b_], [ynT])
                self.tt("dve", o[:], ynT[:], g_[:], ALU.mult, [ynT, g_], [o])
                self.dma(self.y_scr[:, 0:4, g0:g0 + 128], o[:], [o], [self.y_scr])
                blk = yield

    def phaseC(self):
        prm, cst = self.prm_t, self.cst_t
        ident, ones_bf = self.ident, self.ones_bf
        with contextlib.ExitStack() as st:
            pass
        import os
        kcc = int(os.environ.get("KCC", "99"))
        if kcc == 0:
            return
        with contextlib.ExitStack() as st:
            wout = self.sb(st, "wout", [128, 8, D], BF16)
            wsrc = self.w_out[:].rearrange("(kc p) n -> p kc n", p=128)
            stg = [self.sb(st, "stgD%d" % i, [128, 8, 256], F32) for i in range(2)]
            for cb in range(4):
                s = stg[cb % 2]
                self.dma(s[:], wsrc[:, :, cb * 256:(cb + 1) * 256], [], [s])
                self.cp("act", wout[:, :, cb * 256:(cb + 1) * 256], s[:], [s], [wout])
            yT = self.sb(st, "yT_c", [128, 8, TC], BF16)
            oT = self.sb(st, "oT", [128, 8, TC], F32)
            sq = self.sb(st, "sq_c", [128, 8, TC], BF16)
            xT = self.sb(st, "xT_c", [128, 8, TC], F32)
            h2 = self.sb(st, "h2", [128, 8, TC], BF16)
            f = self.sb(st, "f_c", [128, 32, TC], BF16)
            otok = self.sb(st, "otok", [128, 4, D], F32)
            rstd = self.sb(st, "rstd_c", [128, TC], F32)
            tmp = [self.sb(st, "tmpC%d" % i, [128, TC], F32) for i in range(2)]
            w1r = [self.sb(st, "w1r%d" % i, [128, 8, 512], BF16) for i in range(2)]
            w2r = [self.sb(st, "w2r%d" % i, [128, 32, 128], BF16) for i in range(2)]

            def rms(src, R):
                p = self.ps()
                for j in range(8):
                    self.mm(p[:], ones_bf[:], sq[:, j, :], j == 0, j == 7, [ones_bf, sq], [p])
                self.act(rstd[:], p[:], AF.Sqrt, [p, self.epsT], [rstd], scale=1.0 / D, bias=self.epsT[:, 0:1])
                self.S.op("dve", lambda e: e.reciprocal(out=rstd[:], in_=rstd[:]), [rstd.b], [rstd.b])

            def resid(gg, mc):
                for j in range(8):
                    t = tmp[j % 2]
                    self.tt("dve", t[:], oT[:, j, :], rstd[:], ALU.mult, [oT, rstd], [t])
                    self.stt(xT[:, j, :], t[:], gg[:, j, mc:mc + 1], xT[:, j, :], ALU.mult, ALU.add, [t, gg, xT], [xT])

            wi = 0
            for ti in range(NTOK // TC):
                if ti >= kcc:
                    break
                g0 = ti * TC
                mc = 0 if g0 < TS else 1
                self.dma(yT[:], self.y_scr[:, :, g0:g0 + TC], [self.y_scr], [yT])
                self.dma(xT[:], self.xT_scr[:, :, g0:g0 + TC], [self.xT_scr], [xT], eng="act")
                for m in range(8):
                    p = self.ps()
                    for kc in range(8):
                        self.mm(p[:], wout[:, kc, m * 128:(m + 1) * 128], yT[:, kc, :], kc == 0, kc == 7, [wout, yT], [p])
                    self.cp("act", oT[:, m, :], p[:], [p], [oT])
                    self.tt("pool", sq[:, m, :], oT[:, m, :], oT[:, m, :], ALU.mult, [oT], [sq])
                rms(oT, None)
                resid(self.gg1, mc)
                for j in range(8):
                    self.tt("pool", sq[:, j, :], xT[:, j, :], xT[:, j, :], ALU.mult, [xT], [sq])
                rms(xT, None)
                for j in range(8):
                    t = tmp[j % 2]
                    self.tt("dve", t[:], xT[:, j, :], rstd[:], ALU.mult, [xT, rstd], [t])
                    self.act(h2[:, j, :], t[:], AF.Identity, [t, self.gs2, self.modT], [h2],
                             scale=self.gs2[:, j, mc:mc + 1], bias=self.modT[:, 24 + j, mc:mc + 1])
                for blk in range(8):
                    w = w1r[wi % 2]
                    wi += 1
                    self.dma(w[:], self.w1_scr[blk], [self.w1_scr], [w], eng="sp" if blk % 2 == 0 else "act")
                    for c4 in range(4):
                        fc = blk * 4 + c4
                        p = self.ps()
                        for kc in range(8):
                            self.mm(p[:], w[:, kc, c4 * 128:(c4 + 1) * 128], h2[:, kc, :], kc == 0, kc == 7, [w, h2], [p])
                        t = tmp[fc % 2]
                        self.act(t[:], p[:], AF.Relu, [p], [t])
                        self.tt("pool" if fc % 2 == 0 else "dve", f[:, fc, :], t[:], t[:], ALU.mult, [t], [f])
                for m in range(8):
                    w = w2r[m % 2]
                    self.dma(w[:], self.w2_scr[m], [self.w2_scr], [w], eng="sp" if m % 2 == 0 else "act")
                    p = self.ps()
                    for fc in range(32):
                        self.mm(p[:], w[:, fc, :], f[:, fc, :], fc == 0, fc == 31, [w, f], [p])
                    self.cp("act", oT[:, m, :], p[:], [p], [oT])
                    self.tt("pool", sq[:, m, :], oT[:, m, :], oT[:, m, :], ALU.mult, [oT], [sq])
                rms(oT, None)
                resid(self.gg2, mc)
                for s in range(4):
                    for half in range(2):
                        p = self.ps()
                        for jj in range(4):
                            j = half * 4 + jj
                            self.tr(p[:, jj * 128:(jj + 1) * 128], xT[:, j, s * 128:(s + 1) * 128], ident[:], [xT, ident], [p])
                        self.cp("act" if half == 0 else "dve", otok[:, s, half * 512:(half + 1) * 512], p[:], [p], [otok])
                if g0 < TS:
                    dst = self.ys[g0:g0 + TC, :].rearrange("(s p) f -> p s f", p=128)
                    self.dma(dst, otok[:], [otok], [self.ys])
                else:
                    dst = self.yp[:, :].rearrange("(s p) f -> p s f", p=128)
                    self.dma(dst, otok[:], [otok], [self.yp])


def _fm(v):
    v = np.asarray(v, np.float32).reshape(-1, 128)
    return np.ascontiguousarray(v.T)


def _pos_embed():
    def sincos(pos, dim):
        omega = (1.0 / (10000.0 ** (np.arange(dim // 2, dtype=np.float32) / np.float32(dim // 2)))).astype(np.float32)
        ang = pos.astype(np.float32)[:, None] * omega[None, :]
        return np.concatenate([np.sin(ang), np.cos(ang)], axis=-1).astype(np.float32)
    rows = TS // 64
    half = D // 2
    e_row = sincos(np.arange(rows), half)
    e_col = sincos(np.arange(64), half)
    emb = np.concatenate([np.broadcast_to(e_row[:, None, :], (rows, 64, half)),
                          np.broadcast_to(e_col[None, :, :], (rows, 64, half))], axis=-1)
    return np.ascontiguousarray(emb.reshape(rows * 64, D).astype(np.float32))


def _consts():
    c = np.zeros((128, NCST), np.float32)
    c[:, C_ID:C_ID + 128] = np.eye(128, dtype=np.float32)
    ob = np.zeros((128, 128), np.float32)
    ob[:64, :64] = 1.0
    ob[64:, 64:] = 1.0
    c[:, C_OB:C_OB + 128] = ob
    s = np.arange(64)[:, None]
    t = np.arange(64)[None, :]
    msi = np.zeros((128, 2, 64), np.float32)
    msi[:64, 0] = (s < t)
    msi[:64, 1] = (s <= t)
    msi[64:, 0] = (s > t)
    msi[64:, 1] = (s >= t)
    c[:, C_MSI:C_MSI + 128] = msi.reshape(128, 128)
    ml = np.zeros((128, 64), np.float32)
    ml[:64] = (t < s)
    ml[64:] = (t > s)
    c[:, C_ML:C_ML + 64] = ml
    ids = np.zeros((128, 64), np.float32)
    ids[:64] = np.eye(64)
    ids[64:] = np.eye(64)
    c[:, C_IDS:C_IDS + 64] = ids
    c[:64, C_MSI1:C_MSI1 + 128] = msi[64:].reshape(64, 128)
    c[:64, C_ML1:C_ML1 + 64] = ml[64:]
    tt_ = np.arange(TT)
    c[:, C_RMF:C_RMF + TT] = (tt_ % 64 != 0).astype(np.float32)[None, :]
    c[:, C_RMB:C_RMB + TT] = (tt_ % 64 != 63).astype(np.float32)[None, :]
    return c


_NC_CACHE = {}


def kernel(x_prompt, x_sample, c, state_rwkv, state_lru, c_ctx, w_mod, b_mod,
           g_pre_mix, g_post_mix, g_pre_mlp, g_post_mlp, w_in,
           rwkv_w0, rwkv_w_up, rwkv_a0, rwkv_a_up, rwkv_g_up, rwkv_k_k, rwkv_k_a, rwkv_r_k,
           rwkv_lnx_g, rwkv_lnx_b, lru_conv_w, lru_conv_b, lru_wa, lru_ba, lru_wx, lru_bx,
           lru_lambda, w_out, w_mlp1, w_mlp2, _debug=False):
    f = lambda a: np.ascontiguousarray(np.asarray(a, np.float32))
    x_prompt, x_sample, c, state_rwkv, state_lru, c_ctx = map(f, (x_prompt, x_sample, c, state_rwkv, state_lru, c_ctx))
    if "nc" not in _NC_CACHE:
        _NC_CACHE["nc"] = K(debug=_debug).build()
    nc = _NC_CACHE["nc"]
    pe = _pos_embed()
    cst = _consts()
    shared = {
        "pe": pe, "cst": cst,
        "w_mod": f(w_mod[0]), "w_in": f(w_in[0]), "w_out": f(w_out[0]), "w1": f(w_mlp1[0]), "w2": f(w_mlp2[0]),
        "wup": f(rwkv_w_up[0]).reshape(128, 512), "aup": f(rwkv_a_up[0]).reshape(128, 512), "gup": f(rwkv_g_up[0]),
        "lwa": f(lru_wa[0]), "lwx": f(lru_wx[0]),
    }
    prm0 = np.zeros((128, NPRM), np.float32)
    prm0[:, P_GPRE:P_GPRE + 8] = _fm(g_pre_mix[0])
    prm0[:, P_GPOST:P_GPOST + 8] = _fm(g_post_mix[0])
    prm0[:, P_GPRE2:P_GPRE2 + 8] = _fm(g_pre_mlp[0])
    prm0[:, P_GPOST2:P_GPOST2 + 8] = _fm(g_post_mlp[0])
    prm0[:, P_BMOD:P_BMOD + 48] = _fm(b_mod[0])
    for d in range(2):
        prm0[:, P_W0 + 4 * d:P_W0 + 4 * d + 4] = _fm(rwkv_w0[0, d])
        prm0[:, P_A0 + 4 * d:P_A0 + 4 * d + 4] = _fm(rwkv_a0[0, d])
        prm0[:, P_BA + 4 * d:P_BA + 4 * d + 4] = _fm(lru_ba[0, d])
        prm0[:, P_BX + 4 * d:P_BX + 4 * d + 4] = _fm(lru_bx[0, d])
        prm0[:, P_LAM + 4 * d:P_LAM + 4 * d + 4] = _fm(lru_lambda[0, d])
    prm0[:, P_KK:P_KK + 4] = _fm(rwkv_k_k[0])
    prm0[:, P_KA:P_KA + 4] = _fm(rwkv_k_a[0])
    prm0[:, P_RK:P_RK + 4] = _fm(np.asarray(rwkv_r_k[0]).reshape(-1))
    prm0[:, P_LNG:P_LNG + 4] = _fm(rwkv_lnx_g[0])
    prm0[:, P_LNB:P_LNB + 4] = _fm(rwkv_lnx_b[0])
    for i in range(4):
        prm0[:, P_CW + 4 * i:P_CW + 4 * i + 4] = _fm(lru_conv_w[0, i])
    prm0[:, P_CB:P_CB + 4] = _fm(lru_conv_b[0])
    in_maps = []
    for i in range(8):
        prm = prm0.copy()
        for d in range(2):
            prm[:, P_H0 + 4 * d:P_H0 + 4 * d + 4] = _fm(state_lru[i, 0, d])
        cT = np.zeros((128, 8, 2), np.float32)
        cT[:, :, 0] = _fm(c[i])
        cT[:, :, 1] = _fm(c_ctx)
        h0 = np.ascontiguousarray(state_rwkv[i, 0].transpose(0, 3, 1, 2)).reshape(128, 512)
        m = dict(shared)
        m.update({"xs": x_sample[i], "xp": np.ascontiguousarray(x_prompt[2 * i:2 * i + 2].reshape(2 * TP, D)),
                  "cT": cT.reshape(128, 16), "h0r": h0, "prm": prm})
        in_maps.append(m)
    res = run_bass_kernel_spmd(nc, in_maps, core_ids=list(range(8)))
    R = res.results
    y_prompt = np.zeros((16, TP, D), np.float32)
    y_sample = np.zeros((8, TS, D), np.float32)
    st_r = np.zeros((16, 1, 2, 8, 64, 64), np.float32)
    st_l = np.zeros((16, 1, 2, 512), np.float32)
    for i in range(8):
        r = R[i]
        y_sample[i] = r["ys"]
        y_prompt[2 * i:2 * i + 2] = r["yp"].reshape(2, TP, D)
        so = r["str_o"].reshape(2, 2, 64, 8, 64)
        st_r[2 * i:2 * i + 2, 0] = so.transpose(0, 1, 3, 4, 2)
        sl = r["stl_o"].reshape(128, 4, 2, 2)
        st_l[2 * i:2 * i + 2, 0] = sl.transpose(2, 3, 1, 0).reshape(2, 2, 512)
    if _debug:
        return (y_prompt, y_sample, st_r, st_l), R
    return (y_prompt, y_sample, st_r, st_l)
```

```python
import contextlib
import numpy as np
import concourse.bass as bass
import concourse.mybir as mybir
from concourse.bass_utils import run_bass_kernel_spmd

F32 = mybir.dt.float32
BF16 = mybir.dt.bfloat16
F32R = mybir.dt.float32r
AF = mybir.ActivationFunctionType
ALU = mybir.AluOpType

D = 1024
TS = 2048
TP = 256
NTOK = TS + 2 * TP
NCH = NTOK // 64
DIN = 2944
DFF = 4096
LAM = float(np.exp(-0.5))
EPS = 1e-6
LNX_EPS = 64e-5
TT = 256
TC = 512
GELU_C = 1.5957691216057308

P_GPRE, P_GPOST, P_GPRE2, P_GPOST2 = 0, 8, 16, 24
P_BMOD = 32
P_W0, P_A0 = 80, 88
P_KK, P_KA, P_RK, P_LNG, P_LNB = 96, 100, 104, 108, 112
P_CW, P_CB = 116, 132
P_BA, P_BX, P_LAM, P_H0 = 136, 144, 152, 160
NPRM = 168
C_ID, C_OB, C_MSI, C_ML, C_IDS, C_RMF, C_RMB = 0, 128, 256, 384, 448, 512, 768
C_MSI1, C_ML1 = 1024, 1152
NCST = 1216


class Buf:
    __slots__ = ("name", "lw", "rd", "excl", "multi", "ws")

    def __init__(self, name=""):
        self.name = name
        self.lw = None
        self.rd = {}
        self.excl = False
        self.multi = False
        self.ws = {}


class TL:
    def __init__(self, t, name=""):
        self.t = t
        self.b = Buf(name)

    def __getitem__(self, k):
        return self.t[k]


class Sched:
    ENGS = ("pe", "act", "dve", "pool", "sp")

    def __init__(self, nc):
        self.nc = nc
        self.streams = {e: [] for e in self.ENGS}
        self.cnt = {}
        self.waited = {e: {} for e in self.ENGS}
        self.n_ops = 0
        self.dma_n = {e: 0 for e in self.ENGS}
        self.NSLOT = {"sp": 44, "act": 44, "pool": 4, "dve": 2, "pe": 2}

    def _deps(self, eng, reads, writes):
        need = {}
        for b in reads:
            if b.multi:
                for s, v in b.ws.items():
                    if need.get(s, 0) < v:
                        need[s] = v
                continue
            if b.lw is not None:
                s, v = b.lw
                if need.get(s, 0) < v:
                    need[s] = v
            if b.excl:
                for s, v in b.rd.items():
                    if s != eng and need.get(s, 0) < v:
                        need[s] = v
        for b in writes:
            if b.multi:
                continue
            if b.lw is not None:
                s, v = b.lw
                if need.get(s, 0) < v:
                    need[s] = v
            for s, v in b.rd.items():
                if need.get(s, 0) < v:
                    need[s] = v
        out = []
        w = self.waited[eng]
        for s, v in need.items():
            if s == "pe" and eng == "pe":
                continue
            if w.get(s, 0) >= v:
                continue
            w[s] = v
            out.append((s, v))
        return out

    def op(self, eng, fn, reads=(), writes=(), dma=False):
        reads = [r.b if isinstance(r, TL) else r for r in reads]
        writes = [r.b if isinstance(r, TL) else r for r in writes]
        waits = self._deps(eng, reads, writes)
        if dma:
            slot = self.dma_n[eng] % self.NSLOT[eng]
            self.dma_n[eng] += 1
            sem = "%s_d%d" % (eng, slot)
            prev = self.cnt.get(sem, 0)
            if prev > 0 and self.waited[eng].get(sem, 0) < prev:
                self.waited[eng][sem] = prev
                waits.append((sem, prev))
        else:
            sem = eng
        inc = 16 if dma else 1
        self.cnt[sem] = self.cnt.get(sem, 0) + inc
        val = self.cnt[sem]
        self.streams[eng].append((waits, fn, sem, inc))
        self.n_ops += 1
        for b in reads:
            if b.rd.get(sem, 0) < val:
                b.rd[sem] = val
        for b in writes:
            if b.multi:
                if b.ws.get(sem, 0) < val:
                    b.ws[sem] = val
                continue
            b.lw = (sem, val)
            b.rd = {}
        return val

    def barrier_on(self, tl):
        if tl.b.lw is None:
            return
        sname, v = tl.b.lw
        for e in ("sp", "act", "pool"):
            if self.waited[e].get(sname, 0) < v:
                self.waited[e][sname] = v
                self.streams[e].append(([(sname, v)], None, None, 0))

    def barrier(self):
        snap = dict(self.cnt)
        for e in self.ENGS:
            waits = []
            for s, v in snap.items():
                if s == "pe" and e == "pe":
                    continue
                if self.waited[e].get(s, 0) < v:
                    self.waited[e][s] = v
                    waits.append((s, v))
            if waits:
                self.streams[e].append((waits, None, None, 0))

    def emit(self):
        nc = self.nc
        sems = {}
        with contextlib.ExitStack() as st:
            for s in self.cnt:
                sems[s] = st.enter_context(nc.semaphore(s))
            block = st.enter_context(nc.Block())
            engmap = {"pe": block.tensor, "act": block.scalar, "dve": block.vector,
                      "pool": block.gpsimd, "sp": block.sync}
            for e in self.ENGS:
                stream = self.streams[e]
                if not stream:
                    continue

                def body(eng, stream=stream):
                    for waits, fn, sem, inc in stream:
                        for s, v in waits:
                            eng.wait_ge(sems[s], v)
                        if fn is not None:
                            fn(eng).then_inc(sems[sem], inc)
                engmap[e](body)


class K:
    def __init__(self, debug=False, stop_after=None):
        self.debug = debug
        self.stop_after = stop_after
        import os
        self.cutk = int(os.environ.get("KCUT", "0"))
        self.cutm = int(os.environ.get("KCUTM", "99"))
        self.ktiles = int(os.environ.get("KTILES", "99"))
        self.kskip = os.environ.get("KSKIP", "").split(",")
        self.nc = bass.Bass("TRN2", target_bir_lowering=False)
        self.S = Sched(self.nc)
        self.es = contextlib.ExitStack()
        self.psr = 0
        self.rr = {}

    def dram(self, name, shape, dt, kind="Internal"):
        t = TL(self.nc.dram_tensor(name, list(shape), dt, kind=kind).ap(), name)
        t.b.multi = True
        return t

    def sb(self, st, name, shape, dt):
        return TL(st.enter_context(self.nc.sbuf_tensor(name, list(shape), dt)), name)

    def sb2(self, st, name, shape, dt):
        t = st.enter_context(self.nc.sbuf_tensor(name, list(shape), dt))
        return [TL(t, name + "_lo"), TL(t, name + "_hi")]

    def ps(self):
        p = self.psum[self.psr % 8]
        self.psr += 1
        return p

    def mm(self, out, lhsT, rhs, start, stop, R, W):
        self.S.op("pe", lambda e: e.matmul(out, lhsT=lhsT, rhs=rhs, start=start, stop=stop), R, W)

    def tr(self, out, in_, ident, R, W):
        self.S.op("pe", lambda e: e.transpose(out, in_, ident), R, W)

    def act(self, out, in_, func, R, W, scale=1.0, bias=None, eng="act"):
        if bias is None:
            self.S.op("act", lambda e: e.activation(out=out, in_=in_, func=func, scale=scale), R, W)
        else:
            self.S.op("act", lambda e: e.activation(out=out, in_=in_, func=func, scale=scale, bias=bias), R, W)

    def tt(self, eng, out, in0, in1, op, R, W):
        self.S.op(eng, lambda e: e.tensor_tensor(out=out, in0=in0, in1=in1, op=op), R, W)

    def tsc(self, eng, out, in0, s1, op0, R, W, s2=None, op1=None):
        if op1 is None:
            self.S.op(eng, lambda e: e.tensor_scalar(out=out, in0=in0, scalar1=s1, scalar2=None, op0=op0), R, W)
        else:
            self.S.op(eng, lambda e: e.tensor_scalar(out=out, in0=in0, scalar1=s1, scalar2=s2, op0=op0, op1=op1), R, W)

    def stt(self, out, in0, scalar, in1, op0, op1, R, W):
        self.S.op("dve", lambda e: e.scalar_tensor_tensor(out=out, in0=in0, scalar=scalar, in1=in1, op0=op0, op1=op1), R, W)

    def cp(self, eng, out, in_, R, W):
        if eng == "act":
            self.S.op("act", lambda e: e.activation(out=out, in_=in_, func=AF.Copy), R, W)
        else:
            self.S.op(eng, lambda e: e.tensor_copy(out=out, in_=in_), R, W)

    def scan(self, out, d0, d1, init, R, W):
        self.S.op("dve", lambda e: e.tensor_tensor_scan(out=out, data0=d0, data1=d1, initial=init,
                                                        op0=ALU.mult, op1=ALU.add), R, W)

    def dma(self, out, in_, R, W, eng="sp"):
        self.S.op(eng, lambda e: e.dma_start(out=out, in_=in_), R, W, dma=True)

    def memset(self, eng, ap, val, W):
        self.S.op(eng, lambda e: e.memset(ap, val), (), W)

    def pick(self, key, engs):
        i = self.rr.get(key, 0)
        self.rr[key] = i + 1
        return engs[i % len(engs)]

    def build(self):
        nc = self.nc
        I = lambda n, s, dt=F32: self.dram(n, s, dt, "ExternalInput")
        O = lambda n, s, dt=F32: self.dram(n, s, dt, "ExternalOutput")
        self.xs = I("xs", [TS, D])
        self.xp = I("xp", [2 * TP, D])
        self.pe = I("pe", [TS, D])
        self.cT = I("cT", [128, 16])
        self.h0r = I("h0r", [128, 512])
        self.prm = I("prm", [128, NPRM])
        self.cst = I("cst", [128, NCST])
        self.w_mod = I("w_mod", [D, 6 * D])
        self.w_in = I("w_in", [D, DIN])
        self.w_out = I("w_out", [D, D])
        self.w1 = I("w1", [D, DFF])
        self.w2 = I("w2", [DFF, D])
        self.wup = I("wup", [128, 512])
        self.aup = I("aup", [128, 512])
        self.gup = I("gup", [128, 512])
        self.lwa = I("lwa", [2, 8, 64, 64])
        self.lwx = I("lwx", [2, 8, 64, 64])
        self.ys = O("ys", [TS, D])
        self.yp = O("yp", [2 * TP, D])
        self.str_o = O("str_o", [2, 128, 512])
        self.stl_o = O("stl_o", [128, 16])
        self.xT_scr = self.dram("xT_scr", [128, 8, NTOK], F32)
        self.xb_scr = self.dram("xb_scr", [128, 4, NTOK], F32)
        self.gate_scr = self.dram("gate_scr", [128, 4, NTOK], BF16)
        self.g_scr = self.dram("g_scr", [128, 4, NTOK], BF16)
        self.bon_scr = self.dram("bon_scr", [128, 4, NTOK], BF16)
        self.y_scr = self.dram("y_scr", [128, 8, NTOK], BF16)
        self.ytok_scr = self.dram("ytok_scr", [2, NTOK, 512], F32)
        for n in ("art", "rrt", "ttt", "akt", "mrbt", "mrkt", "bh", "kh"):
            setattr(self, n + "_scr", self.dram(n + "_scr", [NCH, 128, 512], BF16))
        self.vt_scr = self.dram("vt_scr", [NCH, 64, 512], BF16)
        self.pend_scr = self.dram("pend_scr", [NCH, 128, 512], F32)
        self.w1_scr = self.dram("w1_scr", [8, 128, 8, 512], BF16)
        self.w2_scr = self.dram("w2_scr", [8, 128, 32, 128], BF16)
        if self.debug:
            self.dbg = {}

        with self.es as st0:
            self.psum = [TL(st0.enter_context(nc.psum_tensor("ps%d" % i, [128, 512], F32)), "ps%d" % i)
                         for i in range(8)]
            for p_ in self.psum:
                p_.b.excl = True
            self.prm_t = self.sb(st0, "prm_t", [128, NPRM], F32)
            self.cst_t = self.sb(st0, "cst_t", [128, NCST], F32)
            self.modT = self.sb(st0, "modT", [128, 48, 2], F32)
            self.gs1 = self.sb(st0, "gs1", [128, 8, 2], F32)
            self.gs2 = self.sb(st0, "gs2", [128, 8, 2], F32)
            self.gg1 = self.sb(st0, "gg1", [128, 8, 2], F32)
            self.gg2 = self.sb(st0, "gg2", [128, 8, 2], F32)
            self.ident = self.sb(st0, "ident", [128, 128], F32)
            self.ones_bf = self.sb(st0, "ones_bf", [128, 128], BF16)
            self.oblk_bf = self.sb(st0, "oblk_bf", [128, 128], BF16)
            self.epsT = self.sb(st0, "epsT", [128, 2], F32)
            self.misc = self.sb(st0, "misc", [128, 32], F32)
            self.stl_t = self.sb(st0, "stl_t", [128, 16], F32)
            for nm, fn in (("p0", self.phase0), ("pA", self.phaseA), ("pB", self.phaseB), ("pC", self.phaseC)):
                fn()
                self.S.barrier()
                if self.stop_after == nm:
                    break
            self.S.emit()
        return nc

    def dump(self, name, src_ap, shape, dt, R):
        o = self.dram("dbg_" + name, shape, dt, "ExternalOutput")
        self.dma(o[:], src_ap, R, [o])

    def phase0(self):
        nc = self.nc
        prm, cst = self.prm_t, self.cst_t
        self.dma(prm[:], self.prm[:], [], [prm])
        self.dma(cst[:], self.cst[:], [], [cst])
        self.cp("dve", self.ident[:], cst[:, C_ID:C_ID + 128], [cst], [self.ident])
        self.cp("dve", self.oblk_bf[:], cst[:, C_OB:C_OB + 128], [cst], [self.oblk_bf])
        self.memset("dve", self.ones_bf[:], 1.0, [self.ones_bf])
        self.memset("dve", self.epsT[:, 0:1], EPS, [self.epsT])
        self.memset("dve", self.epsT[:, 1:2], LNX_EPS, [self.epsT])
        self.tsc("dve", self.misc[:, 0:4], prm[:, P_KA:P_KA + 4], -1.0, ALU.mult, [prm], [self.misc], 1.0, ALU.add)
        with contextlib.ExitStack() as st:
            scT = self.sb(st, "scT", [128, 16], F32)
            cT = self.sb(st, "cT_t", [128, 16], F32)
            wm = [self.sb(st, "wm%d" % i, [128, 8, 512], F32) for i in range(2)]
            tmp = self.sb(st, "lam_tmp", [128, 8], F32)
            self.dma(cT[:], self.cT[:], [], [cT])
            self.act(scT[:], cT[:], AF.Silu, [cT], [scT])
            self.act(tmp[:], prm[:, P_LAM:P_LAM + 8], AF.Exp, [prm], [tmp], scale=-1.0)
            self.act(tmp[:], tmp[:], AF.Ln, [tmp], [tmp], bias=1.0)
            self.tsc("dve", self.misc[:, 4:12], tmp[:], -8.0, ALU.mult, [tmp], [self.misc])
            self.tsc("dve", self.misc[:, 12:20], tmp[:], -16.0, ALU.mult, [tmp], [self.misc])
            wsrc = self.w_mod[:].rearrange("(kc p) n -> p kc n", p=128)
            sc3 = scT[:].rearrange("p (k c) -> p k c", c=2)
            for blk in range(12):
                w = wm[blk % 2]
                self.dma(w[:], wsrc[:, :, blk * 512:(blk + 1) * 512], [], [w], eng="sp" if blk % 2 == 0 else "act")
                p = self.ps()
                for m in range(4):
                    for kc in range(8):
                        self.mm(p[:, 2 * m:2 * m + 2], w[:, kc, m * 128:(m + 1) * 128], sc3[:, kc, :],
                                kc == 0, kc == 7, [w, scT], [p])
                for m in range(4):
                    mi = blk * 4 + m
                    self.tsc("dve", self.modT[:, mi, :], p[:, 2 * m:2 * m + 2], prm[:, P_BMOD + mi:P_BMOD + mi + 1],
                             ALU.add, [p, prm], [self.modT])
            m3 = self.modT
            for (dst, sc_off, g_off, one) in ((self.gs1, 8, P_GPRE, 1.0), (self.gs2, 32, P_GPRE2, 1.0),
                                              (self.gg1, 16, P_GPOST, 0.0), (self.gg2, 40, P_GPOST2, 0.0)):
                for c in range(2):
                    self.tsc("dve", dst[:, :, c], m3[:, sc_off:sc_off + 8, c], one, ALU.add, [m3], [dst])
                    self.tt("dve", dst[:, :, c], dst[:, :, c], prm[:, g_off:g_off + 8], ALU.mult, [dst, prm], [dst])
            if self.debug:
                self.dump("modT", self.modT[:], [128, 48, 2], F32, [self.modT])
                self.dump("gs1", self.gs1[:], [128, 8, 2], F32, [self.gs1])

    def load_cast(self, st, dst_ap, dst_tl, src_ap, shape, tag):
        key = "stg_" + tag
        if not hasattr(self, key):
            setattr(self, key, [self.sb(st, "%s%d" % (key, i), shape, F32) for i in range(2)])
        ring = getattr(self, key)
        s = ring[self.rr.get(key, 0) % 2]
        self.rr[key] = self.rr.get(key, 0) + 1
        self.dma(s[:], src_ap, [], [s], eng="sp")
        eng = self.pick("castE", ["act", "pool"])
        self.cp(eng, dst_ap, s[:], [s], [dst_tl])

    def phaseA(self):
        nc = self.nc
        prm, cst = self.prm_t, self.cst_t
        with contextlib.ExitStack() as st:
            win = self.sb(st, "win", [128, 8, DIN], BF16)
            win.b.multi = True
            wsrc = self.w_in[:].rearrange("(kc p) n -> p kc n", p=128)
            wup = self.sb(st, "wup_t", [128, 512], BF16)
            aup = self.sb(st, "aup_t", [128, 512], BF16)
            gup = self.sb(st, "gup_t", [128, 512], BF16)
            with contextlib.ExitStack() as st2:
                stgA = [self.sb(st2, "stgA%d" % i, [128, 8, 256], F32) for i in range(2)]
                nb = 0
                for c0 in range(0, DIN, 256):
                    cw = min(256, DIN - c0)
                    s_ = stgA[nb % 2]
                    self.dma(s_[:, :, 0:cw], wsrc[:, :, c0:c0 + cw], [], [s_], eng="sp" if nb % 2 == 0 else "act")
                    self.cp("act" if nb % 2 == 0 else "pool", win[:, :, c0:c0 + cw], s_[:, :, 0:cw], [s_], [win])
                    nb += 1
                for i, (src, dstt) in enumerate(((self.wup, wup), (self.aup, aup), (self.gup, gup))):
                    s_ = stgA[nb % 2]
                    nb += 1
                    s2 = s_[:].rearrange("p a b -> p (a b)")[:, 0:512]
                    self.dma(s2, src[:], [], [s_])
                    self.cp("dve", dstt[:], s2, [s_], [dstt])
            self.S.barrier()
            mSI = cst[:, C_MSI:C_MSI + 128].rearrange("p (q t) -> p q t", q=2)
            mL = cst[:, C_ML:C_ML + 64]
            idS = cst[:, C_IDS:C_IDS + 64]

            xin = self.sb(st, "xin", [128, 2, D], F32)
            xT = self.sb(st, "xT", [128, 8, TT], F32)
            sq = self.sb(st, "sq", [128, 8, TT], BF16)
            hT = self.sb(st, "hT", [128, 8, TT], BF16)
            rstd = self.sb(st, "rstd", [128, TT], F32)
            tmpA = [self.sb(st, "tmpA%d" % i, [128, TT], F32) for i in range(2)]
            rT = self.sb(st, "rT", [128, 4, TT], F32)
            kT = self.sb(st, "kT", [128, 4, TT], F32)
            vT = self.sb(st, "vT", [128, 4, TT], F32)
            xw = self.sb(st, "xw", [128, TT], BF16)
            xa = self.sb(st, "xa", [128, TT], BF16)
            xg = self.sb(st, "xg", [128, TT], BF16)
            xbT = self.sb(st, "xbT", [128, 4, TT], F32)
            gtmp = [self.sb(st, "gtmp%d" % i, [128, TT], F32) for i in range(3)]
            gate = self.sb(st, "gate", [128, 4, TT], BF16)
            gT = self.sb(st, "gT", [128, 4, TT], BF16)
            kkn = self.sb(st, "kkn", [128, 4, TT], F32)
            ksum = self.sb(st, "ksum", [128, 4, TT], F32)
            bon = self.sb(st, "bon", [128, 4, TT], BF16)
            sg = self.sb(st, "sg", [128, 4, TT], F32)
            cs = self.sb(st, "cs", [128, 4, TT], F32)
            E1 = self.sb(st, "E1", [128, 4, TT], F32)
            ad = self.sb(st, "ad", [128, 4, TT], F32)
            wk1 = self.sb(st, "wk1", [128, 4, TT], F32)
            wk2 = self.sb(st, "wk2", [128, 4, TT], F32)
            NC4 = TT // 64
            AR = [self.sb(st, "AR%d" % d, [128, 4, NC4, 2, 64], BF16) for d in range(2)]
            BK = [self.sb(st, "BK%d" % d, [128, 4, NC4, 2, 64], BF16) for d in range(2)]
            PEb1 = self.sb(st, "PEb", [128, 4, NC4, 64], F32)
            PEb = [PEb1, PEb1]
            tokB1 = self.sb(st, "tokB", [128, 2, 8, 64], BF16)
            tokK1 = self.sb(st, "tokK", [128, 2, 8, 64], BF16)
            tokB, tokK = [tokB1, tokB1], [tokK1, tokK1]
            tokV = self.sb(st, "tokV", [128, 2, 8, 64], BF16)
            NSET = 2
            MRBs = [self.sb(st, "MRBs%d" % i, [64, 8, 64], BF16) for i in range(NSET)]
            Lt0s = [self.sb(st, "Lt0_%d" % i, [64, 8, 64], F32R) for i in range(NSET)]
            Tfins = [self.sb(st, "Tfin%d" % i, [64, 8, 64], BF16) for i in range(NSET)]
            SCk = self.sb(st, "SCk", [128, 2, 8, 64], BF16)
            Lms = [[self.sb(st, "Lm%d_%d" % (i, k), [64, 8, 64], F32R) for i in range(2)] for k in range(NSET)]
            Ltms = [[self.sb(st, "Ltm%d_%d" % (i, k), [64, 8, 64], F32R) for i in range(2)] for k in range(NSET)]
            ILms = [self.sb(st, "ILm_%d" % k, [64, 8, 64], F32R) for k in range(NSET)]
            Ttms = [[self.sb(st, "Ttm%d_%d" % (i, k), [64, 8, 64], F32R) for i in range(2)] for k in range(NSET)]

            ones_bf, oblk, ident = self.ones_bf, self.oblk_bf, self.ident
            tiles = [(0, t0, True, 0) for t0 in range(0, TS, TT)] + [(1, TS, False, 1), (2, TS + TP, False, 1)]
            mS1 = cst[0:64, C_MSI1:C_MSI1 + 128].rearrange("p (q t) -> p q t", q=2)
            mL1 = cst[0:64, C_ML1:C_ML1 + 64]
            id64 = idS[0:64]
            def front(seq, g0, is_s, mc):
                if is_s:
                    src = self.xs[g0:g0 + TT, :].rearrange("(s p) f -> p s f", p=128)
                else:
                    l0 = g0 - TS
                    src = self.xp[l0:l0 + TT, :].rearrange("(s p) f -> p s f", p=128)
                self.dma(xin[:], src, [], [xin])
                if is_s:
                    petv = xT[:].rearrange("p a b -> p (a b)").rearrange("p (s f) -> p s f", s=2)
                    self.dma(petv, self.pe[g0:g0 + TT, :].rearrange("(s p) f -> p s f", p=128), [], [xT], eng="act")
                    self.tt("pool", xin[:], xin[:], petv, ALU.add, [xin, xT], [xin])
                for j in range(8):
                    p = self.ps()
                    for s in range(2):
                        self.tr(p[:, s * 128:(s + 1) * 128], xin[:, s, j * 128:(j + 1) * 128], ident[:], [xin, ident], [p])
                    self.cp("act", xT[:, j, :], p[:, 0:TT], [p], [xT])
                    self.tt("pool", sq[:, j, :], xT[:, j, :], xT[:, j, :], ALU.mult, [xT], [sq])
                    yield
                self.dma(self.xT_scr[:, :, g0:g0 + TT], xT[:], [xT], [self.xT_scr])
                p = self.ps()
                for j in range(8):
                    self.mm(p[:, 0:TT], ones_bf[:], sq[:, j, :], j == 0, j == 7, [ones_bf, sq], [p])
                self.act(rstd[:], p[:, 0:TT], AF.Sqrt, [p, self.epsT], [rstd], scale=1.0 / D, bias=self.epsT[:, 0:1])
                self.S.op("dve", lambda e: e.reciprocal(out=rstd[:], in_=rstd[:]), [rstd.b], [rstd.b])
                for j in range(8):
                    t = tmpA[j % 2]
                    self.tt("dve", t[:], xT[:, j, :], rstd[:], ALU.mult, [xT, rstd], [t])
                    self.act(hT[:, j, :], t[:], AF.Identity, [t, self.gs1, self.modT], [hT],
                             scale=self.gs1[:, j, mc:mc + 1], bias=self.modT[:, j, mc:mc + 1])
                    yield
                for m in range(23):
                    if m >= self.cutm:
                        break
                    p = self.ps()
                    for kc in range(8):
                        self.mm(p[:, 0:TT], win[:, kc, m * 128:(m + 1) * 128], hT[:, kc, :], kc == 0, kc == 7, [win, hT], [p])
                    pz = p[:, 0:TT]
                    if m < 4:
                        self.cp("act", rT[:, m, :], pz, [p], [rT])
                    elif m < 8:
                        self.cp("act", kT[:, m - 4, :], pz, [p], [kT])
                    elif m < 12:
                        self.cp("act", vT[:, m - 8, :], pz, [p], [vT])
                    elif m == 12:
                        self.act(xw[:], pz, AF.Tanh, [p], [xw])
                    elif m == 13:
                        self.cp("act", xa[:], pz, [p], [xa])
                    elif m == 14:
                        self.act(xg[:], pz, AF.Sigmoid, [p], [xg])
                    elif m < 19:
                        self.cp("act", xbT[:, m - 15, :], pz, [p], [xbT])
                    else:
                        j = m - 19
                        g0_, g1_, g2_ = gtmp
                        self.cp("act", g0_[:], pz, [p], [g0_])
                        self.tt("pool", g1_[:], g0_[:], g0_[:], ALU.mult, [g0_], [g1_])
                        self.tsc("dve", g1_[:], g1_[:], 0.044715, ALU.mult, [g1_], [g1_], 1.0, ALU.add)
                        self.tt("dve", g1_[:], g1_[:], g0_[:], ALU.mult, [g1_, g0_], [g1_])
                        self.act(g2_[:], g1_[:], AF.Sigmoid, [g1_], [g2_], scale=GELU_C)
                        self.tt("pool", gate[:, j, :], g0_[:], g2_[:], ALU.mult, [g0_, g2_], [gate])
                    yield
                self.dma(self.xb_scr[:, :, g0:g0 + TT], xbT[:], [xbT], [self.xb_scr])
                if self.debug and g0 == 0:
                    self.dump("hT", hT[:], [128, 8, TT], BF16, [hT])
                    self.dump("rT", rT[:], [128, 4, TT], F32, [rT])
                    self.dump("vT", vT[:], [128, 4, TT], F32, [vT])
                    self.dump("xbT", xbT[:], [128, 4, TT], F32, [xbT])
                    self.dump("gate", gate[:], [128, 4, TT], BF16, [gate])
                self.dma(self.gate_scr[:, :, g0:g0 + TT], gate[:], [gate], [self.gate_scr])
                for j in range(4):
                    p = self.ps()
                    self.mm(p[:, 0:TT], gup[:, j * 128:(j + 1) * 128], xg[:], True, True, [gup, xg], [p])
                    self.cp("act", gT[:, j, :], p[:, 0:TT], [p], [gT])
                    yield
                self.dma(self.g_scr[:, :, g0:g0 + TT], gT[:], [gT], [self.g_scr])
                for j in range(4):
                    self.tsc("dve", kkn[:, j, :], kT[:, j, :], prm[:, P_KK + j:P_KK + j + 1], ALU.mult, [kT, prm], [kkn])
                    self.tt("pool", sq[:, j, :], kkn[:, j, :], kkn[:, j, :], ALU.mult, [kkn], [sq])
                for j in range(4):
                    p = self.ps()
                    self.mm(p[:, 0:TT], oblk[:], sq[:, j, :], True, True, [oblk, sq], [p])
                    t = tmpA[j % 2]
                    self.act(t[:], p[:, 0:TT], AF.Sqrt, [p], [t])
                    self.tsc("dve", t[:], t[:], 1e-12, ALU.max, [t], [t])
                    self.S.op("dve", lambda e, t=t: e.reciprocal(out=t[:], in_=t[:]), [t.b], [t.b])
                    self.tt("dve", kkn[:, j, :], kkn[:, j, :], t[:], ALU.mult, [kkn, t], [kkn])
                    yield
                for s in range(2):
                    p = self.ps()
                    for j in range(4):
                        self.tr(p[:, j * 128:(j + 1) * 128], vT[:, j, s * 128:(s + 1) * 128], ident[:], [vT, ident], [p])
                    self.cp("act", tokV[:, s, :, :].rearrange("p h k -> p (h k)"), p[:], [p], [tokV])
                c0 = g0 // 64
                for s in range(2):
                    dst = self.vt_scr[c0 + 2 * s:c0 + 2 * s + 2, :, :].rearrange("c s f -> (c s) f")
                    self.dma(dst, tokV[:, s, :, :].rearrange("p h k -> p (h k)"), [tokV], [self.vt_scr])
                yield
            def prep(g0, d):
                c0 = g0 // 64
                for j in range(4):
                    p = self.ps()
                    self.mm(p[:, 0:TT], wup[d * 64:(d + 1) * 64, j * 128:(j + 1) * 128], xw[d * 64:(d + 1) * 64, :],
                            True, True, [wup, xw], [p])
                    self.act(sg[:, j, :], p[:, 0:TT], AF.Sigmoid, [p, prm], [sg],
                             bias=prm[:, P_W0 + 4 * d + j:P_W0 + 4 * d + j + 1])
                    if d == 0:
                        self.scan(cs[:, j, :], cst[:, C_RMF:C_RMF + TT], sg[:, j, :], 0.0, [cst, sg], [cs])
                    else:
                        self.scan(cs[:, j, ::-1], cst[:, C_RMB:C_RMB + TT][:, ::-1], sg[:, j, ::-1], 0.0, [cst, sg], [cs])
                    yield
                for j in range(4):
                    p = self.ps()
                    self.mm(p[:, 0:TT], aup[d * 64:(d + 1) * 64, j * 128:(j + 1) * 128], xa[d * 64:(d + 1) * 64, :],
                            True, True, [aup, xa], [p])
                    self.act(ad[:, j, :], p[:, 0:TT], AF.Sigmoid, [p, prm], [ad],
                             bias=prm[:, P_A0 + 4 * d + j:P_A0 + 4 * d + j + 1])
                    yield
                self.tt("pool", sg[:], cs[:], sg[:], ALU.subtract, [cs, sg], [sg])
                self.act(E1[:], cs[:], AF.Exp, [cs], [E1], scale=-LAM)
                self.act(cs[:], cs[:], AF.Exp, [cs], [cs], scale=LAM)
                self.act(sg[:], sg[:], AF.Exp, [sg], [sg], scale=-LAM)
                E2, E3 = cs, sg
                ar5 = AR[d]
                bk5 = BK[d]
                v4 = lambda tl: tl[:].rearrange("p j (c t) -> p j c t", t=64)
                self.stt(ar5[:, :, :, 0, :], v4(kkn), -1.0, v4(E3), ALU.mult, ALU.mult, [kkn, E3], [ar5])
                self.tt("pool", ar5[:, :, :, 1, :], v4(rT), v4(E1), ALU.mult, [rT, E1], [ar5])
                yield
                self.tt("dve", wk1[:], kkn[:], ad[:], ALU.mult, [kkn, ad], [wk1])
                self.tt("dve", wk1[:], wk1[:], E2[:], ALU.mult, [wk1, E2], [wk1])
                self.cp("pool", bk5[:, :, :, 0, :], v4(wk1), [wk1], [bk5])
                yield
                for j in range(4):
                    self.tsc("dve", wk2[:, j, :], ad[:, j, :], prm[:, P_KA + j:P_KA + j + 1], ALU.mult, [ad, prm, self.misc], [wk2],
                             self.misc[:, j:j + 1], ALU.add)
                self.tt("pool", wk2[:], wk2[:], kT[:], ALU.mult, [wk2, kT], [wk2])
                if d == 0:
                    self.cp("pool", ksum[:], wk2[:], [wk2], [ksum])
                else:
                    self.tt("pool", ksum[:], ksum[:], wk2[:], ALU.add, [ksum, wk2], [ksum])
                self.tt("dve", wk2[:], wk2[:], E2[:], ALU.mult, [wk2, E2], [wk2])
                self.cp("pool", bk5[:, :, :, 1, :], v4(wk2), [wk2], [bk5])
                yield
                te = 63 if d == 0 else 0
                pend_b = v4(E1)[:, :, :, te:te + 1].to_broadcast([128, 4, NC4, 64])
                self.cp("pool", PEb[d][:], pend_b, [E1], [PEb[d]])
                self.tt("dve", v4(wk1), v4(wk1), PEb[d][:], ALU.mult, [wk1, PEb[d]], [wk1])
                self.tt("dve", v4(wk2), v4(wk2), PEb[d][:], ALU.mult, [wk2, PEb[d]], [wk2])
                yield
                for (srcw, tokX, scr) in ((wk1, tokB[d], self.bh_scr), (wk2, tokK[d], self.kh_scr)):
                    for s in range(2):
                        p = self.ps()
                        for j in range(4):
                            self.tr(p[:, j * 128:(j + 1) * 128], srcw[:, j, s * 128:(s + 1) * 128], ident[:], [srcw, ident], [p])
                        self.cp("act", tokX[:, s, :, :].rearrange("p h k -> p (h k)"), p[:], [p], [tokX])
                        for cc in range(2):
                            self.dma(scr[c0 + 2 * s + cc, d * 64:(d + 1) * 64, :],
                                     tokX[cc * 64:(cc + 1) * 64, s, :, :].rearrange("p h k -> p (h k)"),
                                     [tokX], [scr])
                        yield
                for cl in range(NC4):
                    c = c0 + cl
                    for hp in range(2):
                        for (q, scr) in ((0, self.art_scr), (1, self.rrt_scr)):
                            dst = scr[c, d * 64:(d + 1) * 64, :].rearrange("k (j hp t) -> k j hp t", hp=2, t=64)[:, :, hp, :]
                            self.dma(dst, ar5[hp * 64:(hp + 1) * 64, :, cl, q, :], [ar5], [scr], eng="sp")
                        dst = self.pend_scr[c, d * 64:(d + 1) * 64, :].rearrange("k (j hp t) -> k j hp t", hp=2, t=64)[:, :, hp, :]
                        self.dma(dst, PEb[d][hp * 64:(hp + 1) * 64, :, cl, :], [PEb[d]], [self.pend_scr], eng="sp")
                yield
            def bonus(g0):
                for j in range(4):
                    self.stt(sq[:, j, :], rT[:, j, :], prm[:, P_RK + j:P_RK + j + 1], ksum[:, j, :], ALU.mult, ALU.mult,
                             [rT, prm, ksum], [sq])
                    p = self.ps()
                    self.mm(p[:, 0:TT], oblk[:], sq[:, j, :], True, True, [oblk, sq], [p])
                    self.tt("dve", bon[:, j, :], p[:, 0:TT], vT[:, j, :], ALU.mult, [p, vT], [bon])
                self.dma(self.bon_scr[:, :, g0:g0 + TT], bon[:], [bon], [self.bon_scr])
                yield
            def chunk_sck(g0):
                c0 = g0 // 64
                for cl in range(NC4):
                    c = c0 + cl
                    for hp in range(2):
                        p = self.ps()
                        for j in range(4):
                            for d in range(2):
                                self.mm(p[d * 64:(d + 1) * 64, j * 128:(j + 1) * 128],
                                        BK[d][hp * 64:(hp + 1) * 64, j, cl, 1, :],
                                        AR[d][hp * 64:(hp + 1) * 64, j, cl, :, :].rearrange("p q t -> p (q t)"),
                                        True, True, [BK[d], AR[d]], [p])
                        self.tt("dve", SCk[:, :, hp::2, :].rearrange("p q h t -> p h q t"),
                                p[:].rearrange("p (h q t) -> p h q t", q=2, t=64),
                                mSI.unsqueeze(1).to_broadcast([128, 4, 2, 64]), ALU.mult, [p, cst], [SCk])
                    self.dma(self.akt_scr[c], SCk[:, 0, :, :].rearrange("p h s -> p (h s)"), [SCk], [self.akt_scr], eng="act")
                    self.dma(self.mrkt_scr[c], SCk[:, 1, :, :].rearrange("p h s -> p (h s)"), [SCk], [self.mrkt_scr], eng="act")
                    yield
            def chunk_d(g0, d, cls, k):
                c0 = g0 // 64
                Lm, Ltm, ILm, Ttm, Lt0, Tfin = Lms[k], Ltms[k], ILms[k], Ttms[k], Lt0s[k], Tfins[k]
                MRB = MRBs[k]
                for cl in cls:
                    c = c0 + cl
                    msk = mSI[0:64] if d == 0 else mS1
                    mskL = mL[0:64] if d == 0 else mL1
                    L0 = Lm[0]
                    for hp in range(2):
                        p = self.ps()
                        for j in range(4):
                            self.mm(p[0:64, j * 128:(j + 1) * 128],
                                    BK[d][hp * 64:(hp + 1) * 64, j, cl, 0, :],
                                    AR[d][hp * 64:(hp + 1) * 64, j, cl, :, :].rearrange("p q t -> p (q t)"),
                                    True, True, [BK[d], AR[d]], [p])
                        p4 = p[0:64, :].rearrange("p (h q t) -> p h q t", q=2, t=64)
                        self.tt("dve", Lt0[:, hp::2, :], p4[:, :, 0, :], msk[:, 0, :].unsqueeze(1).to_broadcast([64, 4, 64]),
                                ALU.mult, [p, cst], [Lt0])
                        self.tt("dve", MRB[:, hp::2, :], p4[:, :, 1, :], msk[:, 1, :].unsqueeze(1).to_broadcast([64, 4, 64]),
                                ALU.mult, [p, cst], [MRB])
                        p2 = self.ps()
                        for j in range(4):
                            self.mm(p2[0:64, j * 64:(j + 1) * 64],
                                    AR[d][hp * 64:(hp + 1) * 64, j, cl, 0, :], BK[d][hp * 64:(hp + 1) * 64, j, cl, 0, :],
                                    True, True, [AR[d], BK[d]], [p2])
                        self.tt("dve", L0[:, hp::2, :], p2[0:64, 0:256].rearrange("p (h s) -> p h s", s=64),
                                mskL.unsqueeze(1).to_broadcast([64, 4, 64]), ALU.mult, [p2, cst], [L0])
                    self.dma(self.mrbt_scr[c, d * 64:(d + 1) * 64, :], MRB[:].rearrange("p h s -> p (h s)"),
                             [MRB], [self.mrbt_scr], eng="act")
                    yield
                    T0 = Ttm[0]
                    self.tt("pool", T0[:], Lt0[:].bitcast(F32), id64.unsqueeze(1).to_broadcast([64, 8, 64]), ALU.add,
                            [Lt0, cst], [T0])
                    L_prev, Tt_prev, Lt_prev = L0, T0, Lt0
                    for lev in range(1, 6):
                        L_new, Lt_new, Tt_new = Lm[lev % 2], Ltm[lev % 2], Ttm[lev % 2]
                        pA = self.ps()
                        for h in range(8):
                            self.mm(pA[0:64, h * 64:(h + 1) * 64], Lt_prev[:, h, :], L_prev[:, h, :], True, True,
                                    [Lt_prev, L_prev], [pA])
                        if lev < 5:
                            pB = self.ps()
                            for h in range(8):
                                self.mm(pB[0:64, h * 64:(h + 1) * 64], L_prev[:, h, :], Lt_prev[:, h, :], True, True,
                                        [Lt_prev, L_prev], [pB])
                        self.tt("dve", ILm[:], pA[0:64, :].rearrange("p (h s) -> p h s", s=64),
                                id64.unsqueeze(1).to_broadcast([64, 8, 64]), ALU.add, [pA, cst], [ILm])
                        if lev < 5:
                            self.cp("act", L_new[:].rearrange("p h s -> p (h s)"), pA[0:64, :], [pA], [L_new])
                            self.cp("act", Lt_new[:].rearrange("p h s -> p (h s)"), pB[0:64, :], [pB], [Lt_new])
                        pC = self.ps()
                        for h in range(8):
                            self.mm(pC[0:64, h * 64:(h + 1) * 64], ILm[:, h, :], Tt_prev[:, h, :], True, True,
                                    [ILm, Tt_prev], [pC])
                        if lev < 5:
                            self.cp("act", Tt_new[:].rearrange("p h s -> p (h s)"), pC[0:64, :], [pC], [Tt_new])
                        else:
                            self.cp("act", Tfin[:].rearrange("p h s -> p (h s)"), pC[0:64, :], [pC], [Tfin])
                        L_prev, Tt_prev, Lt_prev = L_new, Tt_new, Lt_new
                        yield
                    self.dma(self.ttt_scr[c, d * 64:(d + 1) * 64, :], Tfin[:].rearrange("p h s -> p (h s)"),
                             [Tfin], [self.ttt_scr], eng="act")


            def run_all(*gens):
                gens = list(gens)
                while gens:
                    for g in list(gens):
                        try:
                            next(g)
                        except StopIteration:
                            gens.remove(g)

            def seq_(*gens):
                for g in gens:
                    yield from g

            prev = None
            for (seq, g0, is_s, mc) in tiles[:self.ktiles]:
                if prev is None:
                    run_all(front(seq, g0, is_s, mc))
                else:
                    run_all(seq_(chunk_d(prev, 1, [0, 1], 0), chunk_sck(prev)), chunk_d(prev, 1, [2, 3], 1), front(seq, g0, is_s, mc))
                run_all(prep(g0, 0))
                run_all(chunk_d(g0, 0, [0, 1], 0), chunk_d(g0, 0, [2, 3], 1), prep(g0, 1))
                run_all(bonus(g0))
                prev = g0
            run_all(seq_(chunk_d(prev, 1, [0, 1], 0), chunk_sck(prev)), chunk_d(prev, 1, [2, 3], 1))

    def phaseB(self):
        with contextlib.ExitStack() as st:
            side = [self.gen_c0(st), self.phaseB_lru(st)]
            post = self.phaseB_post(st)
            next(post)
            done = np.zeros((2, NCH), bool)
            posted = [False] * (NTOK // 128)
            si = 0
            for info in self.phaseB_chain(st):
                for (d, c) in info:
                    done[d, c] = True
                for _ in range(2):
                    if side:
                        g = side[si % len(side)]
                        si += 1
                        try:
                            next(g)
                        except StopIteration:
                            side.remove(g)
                for b in range(NTOK // 128):
                    if not posted[b] and done[:, 2 * b:2 * b + 2].all():
                        posted[b] = True
                        post.send(b)
            for g in side:
                for _ in g:
                    pass
            for b in range(NTOK // 128):
                if not posted[b]:
                    post.send(b)
            if self.debug:
                self.S.barrier()
                self.dump("yscr", self.y_scr[:], [128, 8, NTOK], BF16, [self.y_scr])

    def gen_c0(self, st):
        stg = [self.sb(st, "stgC%d" % i, [128, 8, 512], F32) for i in range(2)]
        wb = [self.sb(st, "wbC%d" % i, [128, 8, 512], BF16) for i in range(2)]
        w1src = self.w1[:].rearrange("(kc p) n -> p kc n", p=128)
        for blk in range(8):
            s, o = stg[blk % 2], wb[blk % 2]
            self.dma(s[:], w1src[:, :, blk * 512:(blk + 1) * 512], [], [s])
            self.cp("pool", o[:], s[:], [s], [o])
            self.dma(self.w1_scr[blk], o[:], [o], [self.w1_scr], eng="act")
            yield
        w2src = self.w2[:].rearrange("(fc p) n -> p fc n", p=128)
        for m in range(8):
            s, o = stg[m % 2], wb[m % 2]
            s4 = s[:].rearrange("p k (a b) -> p (k a) b", b=128)
            o4 = o[:].rearrange("p k (a b) -> p (k a) b", b=128)
            self.dma(s4, w2src[:, :, m * 128:(m + 1) * 128], [], [s])
            self.cp("pool", o[:], s[:], [s], [o])
            self.dma(self.w2_scr[m], o4, [o], [self.w2_scr], eng="act")
            yield

    def phaseB_lru(self, st):
        prm, cst, misc = self.prm_t, self.cst_t, self.misc
        if True:
            wbd32 = self.sb(st, "wbd32", [128, 16, 128], F32)
            wbd = self.sb(st, "wbd", [128, 16, 128], BF16)
            self.memset("pool", wbd32[:], 0.0, [wbd32])
            self.S.barrier_on(wbd32)
            wbd32.b.multi = True
            for gi, src in enumerate((self.lwa, self.lwx)):
                for d in range(2):
                    for j in range(4):
                        for hb in range(2):
                            self.dma(wbd32[hb * 64:(hb + 1) * 64, (gi * 2 + d) * 4 + j, hb * 64:(hb + 1) * 64],
                                     src[d, 2 * j + hb], [], [wbd32])
            self.cp("dve", wbd[:], wbd32[:], [wbd32], [wbd])
            TM = TS
            xbp = self.sb(st, "xbp", [128, TM + 4], F32)
            xc = self.sb(st, "xc", [128, TM], F32)
            xcb = self.sb(st, "xcb", [128, TM], BF16)
            gt = self.sb(st, "gt_l", [128, TM], BF16)
            a_t = self.sb(st, "a_t", [128, TM], F32)
            bx_t = self.sb(st, "bx_t", [128, TM], F32)
            s_t = self.sb(st, "s_t", [128, TM], F32)
            hs = [self.sb(st, "hs%d" % d, [128, TM], F32) for d in range(2)]
            yb = self.sb(st, "yb", [128, TM], BF16)
            for (seq, g0, T) in ((0, 0, TS), (1, TS, TP), (2, TS + TP, TP)):
                for j in range(4):
                    self.memset("pool", xbp[:, 0:2], 0.0, [xbp])
                    self.memset("pool", xbp[:, T + 2:T + 4], 0.0, [xbp])
                    self.dma(xbp[:, 2:T + 2], self.xb_scr[:, j, g0:g0 + T], [self.xb_scr], [xbp])
                    self.dma(gt[:, 0:T], self.gate_scr[:, j, g0:g0 + T], [self.gate_scr], [gt], eng="act")
                    cw = lambda i: prm[:, P_CW + 4 * i + j:P_CW + 4 * i + j + 1]
                    self.act(xc[:, 0:T], xbp[:, 0:T], AF.Identity, [xbp, prm], [xc], scale=cw(0), bias=prm[:, P_CB + j:P_CB + j + 1])
                    for i in range(1, 4):
                        self.stt(xc[:, 0:T], xbp[:, i:i + T], cw(i), xc[:, 0:T], ALU.mult, ALU.add, [xbp, prm, xc], [xc])
                    self.cp("pool", xcb[:, 0:T], xc[:, 0:T], [xc], [xcb])
                    yield
                    for d in range(2):
                        for t0 in range(0, T, 512):
                            tw = min(512, T - t0)
                            p = self.ps()
                            self.mm(p[:, 0:tw], wbd[:, (0 * 2 + d) * 4 + j, :], xcb[:, t0:t0 + tw], True, True, [wbd, xcb], [p])
                            self.act(s_t[:, t0:t0 + tw], p[:, 0:tw], AF.Sigmoid, [p, prm], [s_t],
                                     bias=prm[:, P_BA + 4 * d + j:P_BA + 4 * d + j + 1])
                            p2 = self.ps()
                            self.mm(p2[:, 0:tw], wbd[:, (1 * 2 + d) * 4 + j, :], xcb[:, t0:t0 + tw], True, True, [wbd, xcb], [p2])
                            self.act(bx_t[:, t0:t0 + tw], p2[:, 0:tw], AF.Sigmoid, [p2, prm], [bx_t],
                                     bias=prm[:, P_BX + 4 * d + j:P_BX + 4 * d + j + 1])
                        col = 4 + d * 4 + j
                        self.act(a_t[:, 0:T], s_t[:, 0:T], AF.Exp, [s_t, misc], [a_t], scale=misc[:, col:col + 1])
                        self.act(s_t[:, 0:T], s_t[:, 0:T], AF.Exp, [s_t, misc], [s_t], scale=misc[:, col + 8:col + 9])
                        self.act(s_t[:, 0:T], s_t[:, 0:T], AF.Sqrt, [s_t], [s_t], scale=-1.0, bias=1.0)
                        self.tt("pool", bx_t[:, 0:T], bx_t[:, 0:T], xc[:, 0:T], ALU.mult, [bx_t, xc], [bx_t])
                        self.tt("dve", bx_t[:, 0:T], bx_t[:, 0:T], s_t[:, 0:T], ALU.mult, [bx_t, s_t], [bx_t])
                        h = hs[d]
                        if seq == 0:
                            init = prm[:, P_H0 + 4 * d + j:P_H0 + 4 * d + j + 1]
                        else:
                            init = 0.0
                        if d == 0:
                            self.scan(h[:, 0:T], a_t[:, 0:T], bx_t[:, 0:T], init, [a_t, bx_t, prm], [h])
                        else:
                            self.scan(h[:, 0:T][:, ::-1], a_t[:, 0:T][:, ::-1], bx_t[:, 0:T][:, ::-1], init, [a_t, bx_t, prm], [h])
                        if seq > 0:
                            col_o = j * 4 + (seq - 1) * 2 + d
                            te = T - 1 if d == 0 else 0
                            self.cp("pool", self.stl_t[:, col_o:col_o + 1], h[:, te:te + 1], [h], [self.stl_t])
                        yield
                    self.tt("pool", hs[0][:, 0:T], hs[0][:, 0:T], hs[1][:, 0:T], ALU.add, [hs[0], hs[1]], [hs[0]])
                    self.tt("dve", yb[:, 0:T], hs[0][:, 0:T], gt[:, 0:T], ALU.mult, [hs[0], gt], [yb])
                    self.dma(self.y_scr[:, 4 + j, g0:g0 + T], yb[:, 0:T], [yb], [self.y_scr])
            self.dma(self.stl_o[:], self.stl_t[:], [self.stl_t], [self.stl_o])

    def phaseB_chain(self, st):
        if True:
            NB = 3
            def ring(name, dt=BF16):
                return [self.sb2(st, "%s%d" % (name, i), [128, 512], dt) for i in range(NB)]
            art, rrt, ttt, akt, mrbt, mrkt, bh, kh, vt = [ring(n) for n in
                                                          ("c_art", "c_rrt", "c_ttt", "c_akt", "c_mrbt", "c_mrkt", "c_bh", "c_kh", "c_vt")]
            pend = ring("c_pend", F32)
            Hf = self.sb2(st, "Hf", [128, 512], F32)
            Hb = self.sb2(st, "Hb", [128, 512], BF16)
            Zs = self.sb2(st, "Zs", [128, 512], BF16)
            Us = self.sb2(st, "Us", [128, 512], BF16)
            Yt = [self.sb2(st, "Yt%d" % i, [128, 512], F32) for i in range(2)]
            tmpH = self.sb2(st, "tmpH", [128, 512], F32)
            hs_ = lambda h: slice(h * 64, (h + 1) * 64)
            steps = []
            for (seq, cbase, n) in ((0, 0, 32), (1, 32, 4), (2, 36, 4)):
                for i in range(n):
                    steps.append((seq, cbase, n, i))

            def loads(k):
                seq, cbase, n, i = steps[k]
                r = k % NB
                for d in range(2):
                    sl = slice(d * 64, (d + 1) * 64)
                    c = cbase + i if d == 0 else cbase + n - 1 - i
                    for (tl, scr) in ((art, self.art_scr), (rrt, self.rrt_scr), (ttt, self.ttt_scr), (akt, self.akt_scr),
                                      (mrbt, self.mrbt_scr), (mrkt, self.mrkt_scr), (bh, self.bh_scr), (kh, self.kh_scr),
                                      (pend, self.pend_scr)):
                        self.dma(tl[r][d][sl, :], scr[c, sl, :], [scr], [tl[r][d]], eng="sp")
                    self.dma(vt[r][d][sl, :], self.vt_scr[c], [self.vt_scr], [vt[r][d]], eng="sp")

            loads(0)
            for k in range(len(steps)):
                seq, cbase, n, i = steps[k]
                step = k + 1
                r = k % NB
                if k + 1 < len(steps):
                    loads(k + 1)
                if i == 0:
                    for d in range(2):
                        sl = slice(d * 64, (d + 1) * 64)
                        if seq == 0:
                            self.dma(Hf[d][sl, :], self.h0r[sl, :], [], [Hf[d]], eng="act")
                        else:
                            self.memset("dve", Hf[d][sl, :], 0.0, [Hf[d]])
                        self.cp("dve", Hb[d][sl, :], Hf[d][sl, :], [Hf[d]], [Hb[d]])
                if True:
                    ctx = []
                    for d in range(2):
                        sl = slice(d * 64, (d + 1) * 64)
                        c = cbase + i if d == 0 else cbase + n - 1 - i
                        ops = tuple(x[r][d] for x in (art, rrt, ttt, akt, mrbt, mrkt, bh, kh, vt, pend))
                        ctx.append((d, sl, c, ops))
                    pZs = {}
                    for (d, sl, c, (A_, R_, T_, AK_, MRB_, MRK_, B_, K_, V_, PE_)) in ctx:
                        H_, Z_ = Hb[d], Zs[d]
                        pZ = self.ps()
                        for h in range(8):
                            self.mm(pZ[sl, hs_(h)], A_[sl, hs_(h)], H_[sl, hs_(h)], True, False, [A_, H_], [pZ])
                            self.mm(pZ[sl, hs_(h)], AK_[sl, hs_(h)], V_[sl, hs_(h)], False, True, [AK_, V_], [pZ])
                        pZs[d] = pZ
                    pYs = {}
                    for (d, sl, c, (A_, R_, T_, AK_, MRB_, MRK_, B_, K_, V_, PE_)) in ctx:
                        H_ = Hb[d]
                        pY = self.ps()
                        pYs[d] = pY
                    for (d, sl, c, ops) in ctx:
                        self.cp("act", Zs[d][sl, :], pZs[d][sl, :], [pZs[d]], [Zs[d]])
                    pUs = {}
                    for (d, sl, c, (A_, R_, T_, AK_, MRB_, MRK_, B_, K_, V_, PE_)) in ctx:
                        Z_ = Zs[d]
                        pU = self.ps()
                        for h in range(8):
                            self.mm(pU[sl, hs_(h)], T_[sl, hs_(h)], Z_[sl, hs_(h)], True, True, [T_, Z_], [pU])
                        pUs[d] = pU
                    for (d, sl, c, ops) in ctx:
                        self.cp("act", Us[d][sl, :], pUs[d][sl, :], [pUs[d]], [Us[d]])
                    pHs = {}
                    for (d, sl, c, (A_, R_, T_, AK_, MRB_, MRK_, B_, K_, V_, PE_)) in ctx:
                        U_ = Us[d]
                        self.tt("dve", tmpH[d][sl, :], Hf[d][sl, :], PE_[sl, :], ALU.mult, [Hf[d], PE_], [tmpH[d]])
                        pH = self.ps()
                        for h in range(8):
                            self.mm(pH[sl, hs_(h)], B_[sl, hs_(h)], U_[sl, hs_(h)], True, False, [B_, U_], [pH])
                            self.mm(pH[sl, hs_(h)], K_[sl, hs_(h)], V_[sl, hs_(h)], False, True, [K_, V_], [pH])
                        pHs[d] = pH
                    for (d, sl, c, (A_, R_, T_, AK_, MRB_, MRK_, B_, K_, V_, PE_)) in ctx:
                        H_, U_ = Hb[d], Us[d]
                        pY = pYs[d]
                        for h in range(8):
                            self.mm(pY[sl, hs_(h)], R_[sl, hs_(h)], H_[sl, hs_(h)], True, False, [R_, H_], [pY])
                            self.mm(pY[sl, hs_(h)], MRB_[sl, hs_(h)], U_[sl, hs_(h)], False, False, [MRB_, U_], [pY])
                            self.mm(pY[sl, hs_(h)], MRK_[sl, hs_(h)], V_[sl, hs_(h)], False, True, [MRK_, V_], [pY])
                    for (d, sl, c, ops) in ctx:
                        self.tt("dve", Hf[d][sl, :], tmpH[d][sl, :], pHs[d][sl, :], ALU.add, [tmpH[d], pHs[d]], [Hf[d]])
                        self.cp("dve", Hb[d][sl, :], Hf[d][sl, :], [Hf[d]], [Hb[d]])
                    for (d, sl, c, ops) in ctx:
                        y = Yt[step % 2][d]
                        self.cp("act", y[sl, :], pYs[d][sl, :], [pYs[d]], [y])
                        self.dma(self.ytok_scr[d, c * 64:(c + 1) * 64, :], y[sl, :], [y], [self.ytok_scr], eng="act")
                if seq > 0 and i == n - 1:
                    self.dma(self.str_o[seq - 1], Hf[0][:], [Hf[0], Hf[1]], [self.str_o], eng="act")
                yield [(0, cbase + i), (1, cbase + n - 1 - i)]

    def phaseB_post(self, st):
        prm, cst = self.prm_t, self.cst_t
        ident = self.ident
        if True:
            yf = [self.sb(st, "yf%d" % i, [128, 512], F32) for i in range(2)]
            yb2 = [self.sb(st, "yb2%d" % i, [128, 512], F32) for i in range(2)]
            cen = self.sb(st, "cen", [128, 8, 64], F32)
            sqv = self.sb(st, "sqv", [128, 8, 64], F32)
            mean = self.sb(st, "mean", [128, 8], F32)
            var = self.sb(st, "var", [128, 8], F32)
            gl = [self.sb(st, "gl%d" % i, [128, 4, 128], BF16) for i in range(2)]
            bl = [self.sb(st, "bl%d" % i, [128, 4, 128], BF16) for i in range(2)]
            ynT = self.sb(st, "ynT", [128, 4, 128], F32)
            yo = [self.sb(st, "yo%d" % i, [128, 4, 128], BF16) for i in range(2)]
            it = -1
            blk = yield
            while True:
                it += 1
                g0 = blk * 128
                a, b = yf[it % 2], yb2[it % 2]
                g_, b_ = gl[it % 2], bl[it % 2]
                o = yo[it % 2]
                self.dma(a[:], self.ytok_scr[0, g0:g0 + 128, :], [self.ytok_scr], [a])
                self.dma(b[:], self.ytok_scr[1, g0:g0 + 128, :], [self.ytok_scr], [b], eng="act")
                self.dma(g_[:], self.g_scr[:, :, g0:g0 + 128], [self.g_scr], [g_])
                self.dma(b_[:], self.bon_scr[:, :, g0:g0 + 128], [self.bon_scr], [b_], eng="act")
                a3 = a[:].rearrange("p (h v) -> p h v", v=64)
                self.tt("pool", a[:], a[:], b[:], ALU.add, [a, b], [a])
                self.S.op("dve", lambda e, a3=a3: e.tensor_reduce(out=mean[:], in_=a3, op=ALU.add, axis=mybir.AxisListType.X),
                          [a.b], [mean.b])
                self.tsc("dve", mean[:], mean[:], 1.0 / 64, ALU.mult, [mean], [mean])
                self.tt("dve", cen[:], a3, mean[:].unsqueeze(2).to_broadcast([128, 8, 64]), ALU.subtract, [a, mean], [cen])
                self.tt("pool", sqv[:], cen[:], cen[:], ALU.mult, [cen], [sqv])
                self.S.op("dve", lambda e: e.tensor_reduce(out=var[:], in_=sqv[:], op=ALU.add, axis=mybir.AxisListType.X),
                          [sqv.b], [var.b])
                self.act(var[:], var[:], AF.Sqrt, [var, self.epsT], [var], scale=1.0 / 64, bias=self.epsT[:, 1:2])
                self.S.op("dve", lambda e: e.reciprocal(out=var[:], in_=var[:]), [var.b], [var.b])
                self.tt("dve", cen[:], cen[:], var[:].unsqueeze(2).to_broadcast([128, 8, 64]), ALU.mult, [cen, var], [cen])
                p = self.ps()
                cen2 = cen[:].rearrange("p h v -> p (h v)")
                for j in range(4):
                    self.tr(p[:, j * 128:(j + 1) * 128], cen2[:, j * 128:(j + 1) * 128], ident[:], [cen, ident], [p])
                for j in range(4):
                    self.act(ynT[:, j, :], p[:, j * 128:(j + 1) * 128], AF.Identity, [p, prm], [ynT],
                             scale=prm[:, P_LNG + j:P_LNG + j + 1], bias=prm[:, P_LNB + j:P_LNB + j + 1])
                self.tt("pool", ynT[:], ynT[:], b_[:], ALU.add, [ynT, b_], [ynT])
                self.tt("dve", o[:], ynT[:], g_[:], ALU.mult, [ynT, g_], [o])
                self.dma(self.y_scr[:, 0:4, g0:g0 + 128], o[:], [o], [self.y_scr])
                blk = yield

    def phaseC(self):
        prm, cst = self.prm_t, self.cst_t
        ident, ones_bf = self.ident, self.ones_bf
        with contextlib.ExitStack() as st:
            pass
        import os
        kcc = int(os.environ.get("KCC", "99"))
        if kcc == 0:
            return
        with contextlib.ExitStack() as st:
            wout = self.sb(st, "wout", [128, 8, D], BF16)
            wsrc = self.w_out[:].rearrange("(kc p) n -> p kc n", p=128)
            with contextlib.ExitStack() as st2:
                stg = [self.sb(st2, "stgD%d" % i, [128, 8, 256], F32) for i in range(2)]
                for cb in range(4):
                    s = stg[cb % 2]
                    self.dma(s[:], wsrc[:, :, cb * 256:(cb + 1) * 256], [], [s])
                    self.cp("act", wout[:, :, cb * 256:(cb + 1) * 256], s[:], [s], [wout])
            self.S.barrier()
            yTs = [self.sb(st, "yT_c%d" % i, [128, 8, TC], BF16) for i in range(2)]
            oT = self.sb(st, "oT", [128, 8, TC], F32)
            sq = self.sb(st, "sq_c", [128, 8, TC], BF16)
            xT = self.sb(st, "xT_c", [128, 8, TC], F32)
            h2 = self.sb(st, "h2", [128, 8, TC], BF16)
            f = self.sb(st, "f_c", [128, 32, TC], BF16)
            otok = self.sb(st, "otok", [128, 4, D], F32)
            rstd = self.sb(st, "rstd_c", [128, TC], F32)
            tmp = [self.sb(st, "tmpC%d" % i, [128, TC], F32) for i in range(2)]
            NW = 4
            w1r = [self.sb(st, "w1r%d" % i, [128, 8, 512], BF16) for i in range(NW)]
            w2r = [self.sb(st, "w2r%d" % i, [128, 32, 128], BF16) for i in range(NW)]
            ntile = min(NTOK // TC, kcc)
            wseq = []
            for ti_ in range(ntile):
                wseq += [("w1", b_) for b_ in range(8)] + [("w2", b_) for b_ in range(8)]
            wstate = {"issued": 0, "w1": 0, "w2": 0}
            wbuf = {}

            def issue_upto(n):
                while wstate["issued"] < min(n, len(wseq)):
                    k_ = wstate["issued"]
                    kind, b_ = wseq[k_]
                    ring = w1r if kind == "w1" else w2r
                    buf = ring[wstate[kind] % NW]
                    wstate[kind] += 1
                    scr = self.w1_scr if kind == "w1" else self.w2_scr
                    self.dma(buf[:], scr[b_], [scr], [buf], eng="sp" if k_ % 2 == 0 else "act")
                    wbuf[k_] = buf
                    wstate["issued"] += 1

            def rms(src, R):
                p = self.ps()
                for j in range(8):
                    self.mm(p[:], ones_bf[:], sq[:, j, :], j == 0, j == 7, [ones_bf, sq], [p])
                self.act(rstd[:], p[:], AF.Sqrt, [p, self.epsT], [rstd], scale=1.0 / D, bias=self.epsT[:, 0:1])
                self.S.op("dve", lambda e: e.reciprocal(out=rstd[:], in_=rstd[:]), [rstd.b], [rstd.b])

            def resid(gg, mc):
                for j in range(8):
                    t = tmp[j % 2]
                    self.tt("dve", t[:], oT[:, j, :], rstd[:], ALU.mult, [oT, rstd], [t])
                    self.stt(xT[:, j, :], t[:], gg[:, j, mc:mc + 1], xT[:, j, :], ALU.mult, ALU.add, [t, gg, xT], [xT])

            wi = 0
            for ti in range(NTOK // TC):
                if ti >= kcc:
                    break
                g0 = ti * TC
                mc = 0 if g0 < TS else 1
                yT = yTs[ti % 2]
                if ti == 0:
                    self.dma(yT[:], self.y_scr[:, :, g0:g0 + TC], [self.y_scr], [yT])
                self.dma(xT[:], self.xT_scr[:, :, g0:g0 + TC], [self.xT_scr], [xT], eng="act")
                if ti + 1 < ntile:
                    self.dma(yTs[(ti + 1) % 2][:], self.y_scr[:, :, g0 + TC:g0 + 2 * TC], [self.y_scr], [yTs[(ti + 1) % 2]])
                issue_upto(ti * 16 + 4)
                for m in range(8):
                    p = self.ps()
                    for kc in range(8):
                        self.mm(p[:], wout[:, kc, m * 128:(m + 1) * 128], yT[:, kc, :], kc == 0, kc == 7, [wout, yT], [p])
                    self.cp("act", oT[:, m, :], p[:], [p], [oT])
                    self.tt("pool", sq[:, m, :], oT[:, m, :], oT[:, m, :], ALU.mult, [oT], [sq])
                rms(oT, None)
                resid(self.gg1, mc)
                for j in range(8):
                    self.tt("pool", sq[:, j, :], xT[:, j, :], xT[:, j, :], ALU.mult, [xT], [sq])
                rms(xT, None)
                for j in range(8):
                    t = tmp[j % 2]
                    self.tt("dve", t[:], xT[:, j, :], rstd[:], ALU.mult, [xT, rstd], [t])
                    self.act(h2[:, j, :], t[:], AF.Identity, [t, self.gs2, self.modT], [h2],
                             scale=self.gs2[:, j, mc:mc + 1], bias=self.modT[:, 24 + j, mc:mc + 1])
                for blk in range(8):
                    issue_upto(ti * 16 + blk + 4)
                    w = wbuf[ti * 16 + blk]
                    for c4 in range(4):
                        fc = blk * 4 + c4
                        p = self.ps()
                        for kc in range(8):
                            self.mm(p[:], w[:, kc, c4 * 128:(c4 + 1) * 128], h2[:, kc, :], kc == 0, kc == 7, [w, h2], [p])
                        t = tmp[fc % 2]
                        self.act(t[:], p[:], AF.Relu, [p], [t])
                        self.tt("pool" if fc % 2 == 0 else "dve", f[:, fc, :], t[:], t[:], ALU.mult, [t], [f])
                for m in range(8):
                    issue_upto(ti * 16 + 8 + m + 4)
                    w = wbuf[ti * 16 + 8 + m]
                    p = self.ps()
                    for fc in range(32):
                        self.mm(p[:], w[:, fc, :], f[:, fc, :], fc == 0, fc == 31, [w, f], [p])
                    self.cp("act", oT[:, m, :], p[:], [p], [oT])
                    self.tt("pool", sq[:, m, :], oT[:, m, :], oT[:, m, :], ALU.mult, [oT], [sq])
                rms(oT, None)
                resid(self.gg2, mc)
                for s in range(4):
                    for half in range(2):
                        p = self.ps()
                        for jj in range(4):
                            j = half * 4 + jj
                            self.tr(p[:, jj * 128:(jj + 1) * 128], xT[:, j, s * 128:(s + 1) * 128], ident[:], [xT, ident], [p])
                        self.cp("act" if half == 0 else "dve", otok[:, s, half * 512:(half + 1) * 512], p[:], [p], [otok])
                if g0 < TS:
                    dst = self.ys[g0:g0 + TC, :].rearrange("(s p) f -> p s f", p=128)
                    self.dma(dst, otok[:], [otok], [self.ys])
                else:
                    dst = self.yp[:, :].rearrange("(s p) f -> p s f", p=128)
                    self.dma(dst, otok[:], [otok], [self.yp])


def _fm(v):
    v = np.asarray(v, np.float32).reshape(-1, 128)
    return np.ascontiguousarray(v.T)


def _pos_embed():
    def sincos(pos, dim):
        omega = (1.0 / (10000.0 ** (np.arange(dim // 2, dtype=np.float32) / np.float32(dim // 2)))).astype(np.float32)
        ang = pos.astype(np.float32)[:, None] * omega[None, :]
        return np.concatenate([np.sin(ang), np.cos(ang)], axis=-1).astype(np.float32)
    rows = TS // 64
    half = D // 2
    e_row = sincos(np.arange(rows), half)
    e_col = sincos(np.arange(64), half)
    emb = np.concatenate([np.broadcast_to(e_row[:, None, :], (rows, 64, half)),
                          np.broadcast_to(e_col[None, :, :], (rows, 64, half))], axis=-1)
    return np.ascontiguousarray(emb.reshape(rows * 64, D).astype(np.float32))


def _consts():
    c = np.zeros((128, NCST), np.float32)
    c[:, C_ID:C_ID + 128] = np.eye(128, dtype=np.float32)
    ob = np.zeros((128, 128), np.float32)
    ob[:64, :64] = 1.0
    ob[64:, 64:] = 1.0
    c[:, C_OB:C_OB + 128] = ob
    s = np.arange(64)[:, None]
    t = np.arange(64)[None, :]
    msi = np.zeros((128, 2, 64), np.float32)
    msi[:64, 0] = (s < t)
    msi[:64, 1] = (s <= t)
    msi[64:, 0] = (s > t)
    msi[64:, 1] = (s >= t)
    c[:, C_MSI:C_MSI + 128] = msi.reshape(128, 128)
    ml = np.zeros((128, 64), np.float32)
    ml[:64] = (t < s)
    ml[64:] = (t > s)
    c[:, C_ML:C_ML + 64] = ml
    ids = np.zeros((128, 64), np.float32)
    ids[:64] = np.eye(64)
    ids[64:] = np.eye(64)
    c[:, C_IDS:C_IDS + 64] = ids
    c[:64, C_MSI1:C_MSI1 + 128] = msi[64:].reshape(64, 128)
    c[:64, C_ML1:C_ML1 + 64] = ml[64:]
    tt_ = np.arange(TT)
    c[:, C_RMF:C_RMF + TT] = (tt_ % 64 != 0).astype(np.float32)[None, :]
    c[:, C_RMB:C_RMB + TT] = (tt_ % 64 != 63).astype(np.float32)[None, :]
    return c


_NC_CACHE = {}


def kernel(x_prompt, x_sample, c, state_rwkv, state_lru, c_ctx, w_mod, b_mod,
           g_pre_mix, g_post_mix, g_pre_mlp, g_post_mlp, w_in,
           rwkv_w0, rwkv_w_up, rwkv_a0, rwkv_a_up, rwkv_g_up, rwkv_k_k, rwkv_k_a, rwkv_r_k,
           rwkv_lnx_g, rwkv_lnx_b, lru_conv_w, lru_conv_b, lru_wa, lru_ba, lru_wx, lru_bx,
           lru_lambda, w_out, w_mlp1, w_mlp2, _debug=False):
    f = lambda a: np.ascontiguousarray(np.asarray(a, np.float32))
    x_prompt, x_sample, c, state_rwkv, state_lru, c_ctx = map(f, (x_prompt, x_sample, c, state_rwkv, state_lru, c_ctx))
    if "nc" not in _NC_CACHE:
        _NC_CACHE["nc"] = K(debug=_debug).build()
    nc = _NC_CACHE["nc"]
    pe = _pos_embed()
    cst = _consts()
    shared = {
        "pe": pe, "cst": cst,
        "w_mod": f(w_mod[0]), "w_in": f(w_in[0]), "w_out": f(w_out[0]), "w1": f(w_mlp1[0]), "w2": f(w_mlp2[0]),
        "wup": f(rwkv_w_up[0]).reshape(128, 512), "aup": f(rwkv_a_up[0]).reshape(128, 512), "gup": f(rwkv_g_up[0]),
        "lwa": f(lru_wa[0]), "lwx": f(lru_wx[0]),
    }
    prm0 = np.zeros((128, NPRM), np.float32)
    prm0[:, P_GPRE:P_GPRE + 8] = _fm(g_pre_mix[0])
    prm0[:, P_GPOST:P_GPOST + 8] = _fm(g_post_mix[0])
    prm0[:, P_GPRE2:P_GPRE2 + 8] = _fm(g_pre_mlp[0])
    prm0[:, P_GPOST2:P_GPOST2 + 8] = _fm(g_post_mlp[0])
    prm0[:, P_BMOD:P_BMOD + 48] = _fm(b_mod[0])
    for d in range(2):
        prm0[:, P_W0 + 4 * d:P_W0 + 4 * d + 4] = _fm(rwkv_w0[0, d])
        prm0[:, P_A0 + 4 * d:P_A0 + 4 * d + 4] = _fm(rwkv_a0[0, d])
        prm0[:, P_BA + 4 * d:P_BA + 4 * d + 4] = _fm(lru_ba[0, d])
        prm0[:, P_BX + 4 * d:P_BX + 4 * d + 4] = _fm(lru_bx[0, d])
        prm0[:, P_LAM + 4 * d:P_LAM + 4 * d + 4] = _fm(lru_lambda[0, d])
    prm0[:, P_KK:P_KK + 4] = _fm(rwkv_k_k[0])
    prm0[:, P_KA:P_KA + 4] = _fm(rwkv_k_a[0])
    prm0[:, P_RK:P_RK + 4] = _fm(np.asarray(rwkv_r_k[0]).reshape(-1))
    prm0[:, P_LNG:P_LNG + 4] = _fm(rwkv_lnx_g[0])
    prm0[:, P_LNB:P_LNB + 4] = _fm(rwkv_lnx_b[0])
    for i in range(4):
        prm0[:, P_CW + 4 * i:P_CW + 4 * i + 4] = _fm(lru_conv_w[0, i])
    prm0[:, P_CB:P_CB + 4] = _fm(lru_conv_b[0])
    in_maps = []
    for i in range(8):
        prm = prm0.copy()
        for d in range(2):
            prm[:, P_H0 + 4 * d:P_H0 + 4 * d + 4] = _fm(state_lru[i, 0, d])
        cT = np.zeros((128, 8, 2), np.float32)
        cT[:, :, 0] = _fm(c[i])
        cT[:, :, 1] = _fm(c_ctx)
        h0 = np.ascontiguousarray(state_rwkv[i, 0].transpose(0, 3, 1, 2)).reshape(128, 512)
        m = dict(shared)
        m.update({"xs": x_sample[i], "xp": np.ascontiguousarray(x_prompt[2 * i:2 * i + 2].reshape(2 * TP, D)),
                  "cT": cT.reshape(128, 16), "h0r": h0, "prm": prm})
        in_maps.append(m)
    res = run_bass_kernel_spmd(nc, in_maps, core_ids=list(range(8)))
    R = res.results
    y_prompt = np.zeros((16, TP, D), np.float32)
    y_sample = np.zeros((8, TS, D), np.float32)
    st_r = np.zeros((16, 1, 2, 8, 64, 64), np.float32)
    st_l = np.zeros((16, 1, 2, 512), np.float32)
    for i in range(8):
        r = R[i]
        y_sample[i] = r["ys"]
        y_prompt[2 * i:2 * i + 2] = r["yp"].reshape(2, TP, D)
        so = r["str_o"].reshape(2, 2, 64, 8, 64)
        st_r[2 * i:2 * i + 2, 0] = so.transpose(0, 1, 3, 4, 2)
        sl = r["stl_o"].reshape(128, 4, 2, 2)
        st_l[2 * i:2 * i + 2, 0] = sl.transpose(2, 3, 1, 0).reshape(2, 2, 512)
    if _debug:
        return (y_prompt, y_sample, st_r, st_l), R
    return (y_prompt, y_sample, st_r, st_l)
```

```python
import contextlib
import numpy as np
import concourse.bass as bass
import concourse.mybir as mybir
from concourse.bass_utils import run_bass_kernel_spmd

F32 = mybir.dt.float32
BF16 = mybir.dt.bfloat16
F32R = mybir.dt.float32r
AF = mybir.ActivationFunctionType
ALU = mybir.AluOpType

D = 1024
TS = 2048
TP = 256
NTOK = TS + 2 * TP
NCH = NTOK // 64
DIN = 2944
DFF = 4096
LAM = float(np.exp(-0.5))
EPS = 1e-6
LNX_EPS = 64e-5
TT = 256
TC = 512
GELU_C = 1.5957691216057308

P_GPRE, P_GPOST, P_GPRE2, P_GPOST2 = 0, 8, 16, 24
P_BMOD = 32
P_W0, P_A0 = 80, 88
P_KK, P_KA, P_RK, P_LNG, P_LNB = 96, 100, 104, 108, 112
P_CW, P_CB = 116, 132
P_BA, P_BX, P_LAM, P_H0 = 136, 144, 152, 160
NPRM = 168
C_ID, C_OB, C_MSI, C_ML, C_IDS, C_RMF, C_RMB = 0, 128, 256, 384, 448, 512, 768
C_MSI1, C_ML1 = 1024, 1152
NCST = 1216


class Buf:
    __slots__ = ("name", "lw", "rd", "excl", "multi", "ws")

    def __init__(self, name=""):
        self.name = name
        self.lw = None
        self.rd = {}
        self.excl = False
        self.multi = False
        self.ws = {}


class TL:
    def __init__(self, t, name=""):
        self.t = t
        self.b = Buf(name)

    def __getitem__(self, k):
        return self.t[k]


class Sched:
    ENGS = ("pe", "act", "dve", "pool", "sp")

    def __init__(self, nc):
        self.nc = nc
        self.streams = {e: [] for e in self.ENGS}
        self.cnt = {}
        self.waited = {e: {} for e in self.ENGS}
        self.n_ops = 0
        self.dma_n = {e: 0 for e in self.ENGS}
        self.NSLOT = {"sp": 44, "act": 44, "pool": 4, "dve": 2, "pe": 2}

    def _deps(self, eng, reads, writes):
        need = {}
        for b in reads:
            if b.multi:
                for s, v in b.ws.items():
                    if need.get(s, 0) < v:
                        need[s] = v
                continue
            if b.lw is not None:
                s, v = b.lw
                if need.get(s, 0) < v:
                    need[s] = v
            if b.excl:
                for s, v in b.rd.items():
                    if s != eng and need.get(s, 0) < v:
                        need[s] = v
        for b in writes:
            if b.multi:
                continue
            if b.lw is not None:
                s, v = b.lw
                if need.get(s, 0) < v:
                    need[s] = v
            for s, v in b.rd.items():
                if need.get(s, 0) < v:
                    need[s] = v
        out = []
        w = self.waited[eng]
        for s, v in need.items():
            if s == "pe" and eng == "pe":
                continue
            if w.get(s, 0) >= v:
                continue
            w[s] = v
            out.append((s, v))
        return out

    def op(self, eng, fn, reads=(), writes=(), dma=False):
        reads = [r.b if isinstance(r, TL) else r for r in reads]
        writes = [r.b if isinstance(r, TL) else r for r in writes]
        waits = self._deps(eng, reads, writes)
        if dma:
            slot = self.dma_n[eng] % self.NSLOT[eng]
            self.dma_n[eng] += 1
            sem = "%s_d%d" % (eng, slot)
            prev = self.cnt.get(sem, 0)
            if prev > 0 and self.waited[eng].get(sem, 0) < prev:
                self.waited[eng][sem] = prev
                waits.append((sem, prev))
        else:
            sem = eng
        inc = 16 if dma else 1
        self.cnt[sem] = self.cnt.get(sem, 0) + inc
        val = self.cnt[sem]
        self.streams[eng].append((waits, fn, sem, inc))
        self.n_ops += 1
        for b in reads:
            if b.rd.get(sem, 0) < val:
                b.rd[sem] = val
        for b in writes:
            if b.multi:
                if b.ws.get(sem, 0) < val:
                    b.ws[sem] = val
                continue
            b.lw = (sem, val)
            b.rd = {}
        return val

    def barrier_on(self, tl):
        if tl.b.lw is None:
            return
        sname, v = tl.b.lw
        for e in ("sp", "act", "pool"):
            if self.waited[e].get(sname, 0) < v:
                self.waited[e][sname] = v
                self.streams[e].append(([(sname, v)], None, None, 0))

    def barrier(self):
        snap = dict(self.cnt)
        for e in self.ENGS:
            waits = []
            for s, v in snap.items():
                if s == "pe" and e == "pe":
                    continue
                if self.waited[e].get(s, 0) < v:
                    self.waited[e][s] = v
                    waits.append((s, v))
            if waits:
                self.streams[e].append((waits, None, None, 0))

    def emit(self):
        nc = self.nc
        sems = {}
        with contextlib.ExitStack() as st:
            for s in self.cnt:
                sems[s] = st.enter_context(nc.semaphore(s))
            block = st.enter_context(nc.Block())
            engmap = {"pe": block.tensor, "act": block.scalar, "dve": block.vector,
                      "pool": block.gpsimd, "sp": block.sync}
            for e in self.ENGS:
                stream = self.streams[e]
                if not stream:
                    continue

                def body(eng, stream=stream):
                    for waits, fn, sem, inc in stream:
                        for s, v in waits:
                            eng.wait_ge(sems[s], v)
                        if fn is not None:
                            fn(eng).then_inc(sems[sem], inc)
                engmap[e](body)


class K:
    def __init__(self, debug=False, stop_after=None):
        self.debug = debug
        self.stop_after = stop_after
        import os
        self.cutk = int(os.environ.get("KCUT", "0"))
        self.cutm = int(os.environ.get("KCUTM", "99"))
        self.ktiles = int(os.environ.get("KTILES", "99"))
        self.kskip = os.environ.get("KSKIP", "").split(",")
        self.nc = bass.Bass("TRN2", target_bir_lowering=False)
        self.S = Sched(self.nc)
        self.es = contextlib.ExitStack()
        self.psr = 0
        self.rr = {}

    def dram(self, name, shape, dt, kind="Internal"):
        t = TL(self.nc.dram_tensor(name, list(shape), dt, kind=kind).ap(), name)
        t.b.multi = True
        return t

    def sb(self, st, name, shape, dt):
        return TL(st.enter_context(self.nc.sbuf_tensor(name, list(shape), dt)), name)

    def sb2(self, st, name, shape, dt):
        t = st.enter_context(self.nc.sbuf_tensor(name, list(shape), dt))
        return [TL(t, name + "_lo"), TL(t, name + "_hi")]

    def ps(self):
        p = self.psum[self.psr % 8]
        self.psr += 1
        return p

    def mm(self, out, lhsT, rhs, start, stop, R, W):
        self.S.op("pe", lambda e: e.matmul(out, lhsT=lhsT, rhs=rhs, start=start, stop=stop), R, W)

    def tr(self, out, in_, ident, R, W):
        self.S.op("pe", lambda e: e.transpose(out, in_, ident), R, W)

    def act(self, out, in_, func, R, W, scale=1.0, bias=None, eng="act"):
        if bias is None:
            self.S.op("act", lambda e: e.activation(out=out, in_=in_, func=func, scale=scale), R, W)
        else:
            self.S.op("act", lambda e: e.activation(out=out, in_=in_, func=func, scale=scale, bias=bias), R, W)

    def tt(self, eng, out, in0, in1, op, R, W):
        self.S.op(eng, lambda e: e.tensor_tensor(out=out, in0=in0, in1=in1, op=op), R, W)

    def tsc(self, eng, out, in0, s1, op0, R, W, s2=None, op1=None):
        if op1 is None:
            self.S.op(eng, lambda e: e.tensor_scalar(out=out, in0=in0, scalar1=s1, scalar2=None, op0=op0), R, W)
        else:
            self.S.op(eng, lambda e: e.tensor_scalar(out=out, in0=in0, scalar1=s1, scalar2=s2, op0=op0, op1=op1), R, W)

    def stt(self, out, in0, scalar, in1, op0, op1, R, W):
        self.S.op("dve", lambda e: e.scalar_tensor_tensor(out=out, in0=in0, scalar=scalar, in1=in1, op0=op0, op1=op1), R, W)

    def cp(self, eng, out, in_, R, W):
        if eng == "act":
            self.S.op("act", lambda e: e.activation(out=out, in_=in_, func=AF.Copy), R, W)
        else:
            self.S.op(eng, lambda e: e.tensor_copy(out=out, in_=in_), R, W)

    def scan(self, out, d0, d1, init, R, W):
        self.S.op("dve", lambda e: e.tensor_tensor_scan(out=out, data0=d0, data1=d1, initial=init,
                                                        op0=ALU.mult, op1=ALU.add), R, W)

    def dma(self, out, in_, R, W, eng="sp"):
        self.S.op(eng, lambda e: e.dma_start(out=out, in_=in_), R, W, dma=True)

    def memset(self, eng, ap, val, W):
        self.S.op(eng, lambda e: e.memset(ap, val), (), W)

    def pick(self, key, engs):
        i = self.rr.get(key, 0)
        self.rr[key] = i + 1
        return engs[i % len(engs)]

    def build(self):
        nc = self.nc
        I = lambda n, s, dt=F32: self.dram(n, s, dt, "ExternalInput")
        O = lambda n, s, dt=F32: self.dram(n, s, dt, "ExternalOutput")
        self.xs = I("xs", [TS, D])
        self.xp = I("xp", [2 * TP, D])
        self.pe = I("pe", [TS, D])
        self.cT = I("cT", [128, 16])
        self.h0r = I("h0r", [128, 512])
        self.prm = I("prm", [128, NPRM])
        self.cst = I("cst", [128, NCST])
        self.w_mod = I("w_mod", [D, 6 * D])
        self.w_in = I("w_in", [D, DIN])
        self.w_out = I("w_out", [D, D])
        self.w1 = I("w1", [D, DFF])
        self.w2 = I("w2", [DFF, D])
        self.wup = I("wup", [128, 512])
        self.aup = I("aup", [128, 512])
        self.gup = I("gup", [128, 512])
        self.lwa = I("lwa", [2, 8, 64, 64])
        self.lwx = I("lwx", [2, 8, 64, 64])
        self.ys = O("ys", [TS, D])
        self.yp = O("yp", [2 * TP, D])
        self.str_o = O("str_o", [2, 128, 512])
        self.stl_o = O("stl_o", [128, 16])
        self.xT_scr = self.dram("xT_scr", [128, 8, NTOK], F32)
        self.xb_scr = self.dram("xb_scr", [128, 4, NTOK], F32)
        self.gate_scr = self.dram("gate_scr", [128, 4, NTOK], BF16)
        self.g_scr = self.dram("g_scr", [128, 4, NTOK], BF16)
        self.bon_scr = self.dram("bon_scr", [128, 4, NTOK], BF16)
        self.y_scr = self.dram("y_scr", [128, 8, NTOK], BF16)
        self.ytok_scr = self.dram("ytok_scr", [2, NTOK, 512], F32)
        for n in ("art", "rrt", "ttt", "akt", "mrbt", "mrkt", "bh", "kh"):
            setattr(self, n + "_scr", self.dram(n + "_scr", [NCH, 128, 512], BF16))
        self.vt_scr = self.dram("vt_scr", [NCH, 64, 512], BF16)
        self.pend_scr = self.dram("pend_scr", [NCH, 128, 512], F32)
        self.w1_scr = self.dram("w1_scr", [8, 128, 8, 512], BF16)
        self.w2_scr = self.dram("w2_scr", [8, 128, 32, 128], BF16)
        if self.debug:
            self.dbg = {}

        with self.es as st0:
            self.psum = [TL(st0.enter_context(nc.psum_tensor("ps%d" % i, [128, 512], F32)), "ps%d" % i)
                         for i in range(8)]
            for p_ in self.psum:
                p_.b.excl = True
            self.prm_t = self.sb(st0, "prm_t", [128, NPRM], F32)
            self.cst_t = self.sb(st0, "cst_t", [128, NCST], F32)
            self.modT = self.sb(st0, "modT", [128, 48, 2], F32)
            self.gs1 = self.sb(st0, "gs1", [128, 8, 2], F32)
            self.gs2 = self.sb(st0, "gs2", [128, 8, 2], F32)
            self.gg1 = self.sb(st0, "gg1", [128, 8, 2], F32)
            self.gg2 = self.sb(st0, "gg2", [128, 8, 2], F32)
            self.ident = self.sb(st0, "ident", [128, 128], F32)
            self.ones_bf = self.sb(st0, "ones_bf", [128, 128], BF16)
            self.oblk_bf = self.sb(st0, "oblk_bf", [128, 128], BF16)
            self.epsT = self.sb(st0, "epsT", [128, 2], F32)
            self.misc = self.sb(st0, "misc", [128, 32], F32)
            self.stl_t = self.sb(st0, "stl_t", [128, 16], F32)
            for nm, fn in (("p0", self.phase0), ("pA", self.phaseA), ("pB", self.phaseB), ("pC", self.phaseC)):
                fn()
                self.S.barrier()
                if self.stop_after == nm:
                    break
            self.S.emit()
        return nc

    def dump(self, name, src_ap, shape, dt, R):
        o = self.dram("dbg_" + name, shape, dt, "ExternalOutput")
        self.dma(o[:], src_ap, R, [o])

    def phase0(self):
        nc = self.nc
        prm, cst = self.prm_t, self.cst_t
        self.dma(prm[:], self.prm[:], [], [prm])
        self.dma(cst[:], self.cst[:], [], [cst])
        self.cp("dve", self.ident[:], cst[:, C_ID:C_ID + 128], [cst], [self.ident])
        self.cp("dve", self.oblk_bf[:], cst[:, C_OB:C_OB + 128], [cst], [self.oblk_bf])
        self.memset("dve", self.ones_bf[:], 1.0, [self.ones_bf])
        self.memset("dve", self.epsT[:, 0:1], EPS, [self.epsT])
        self.memset("dve", self.epsT[:, 1:2], LNX_EPS, [self.epsT])
        self.tsc("dve", self.misc[:, 0:4], prm[:, P_KA:P_KA + 4], -1.0, ALU.mult, [prm], [self.misc], 1.0, ALU.add)
        with contextlib.ExitStack() as st:
            scT = self.sb(st, "scT", [128, 16], F32)
            cT = self.sb(st, "cT_t", [128, 16], F32)
            wm = [self.sb(st, "wm%d" % i, [128, 8, 512], F32) for i in range(2)]
            tmp = self.sb(st, "lam_tmp", [128, 8], F32)
            self.dma(cT[:], self.cT[:], [], [cT])
            self.act(scT[:], cT[:], AF.Silu, [cT], [scT])
            self.act(tmp[:], prm[:, P_LAM:P_LAM + 8], AF.Exp, [prm], [tmp], scale=-1.0)
            self.act(tmp[:], tmp[:], AF.Ln, [tmp], [tmp], bias=1.0)
            self.tsc("dve", self.misc[:, 4:12], tmp[:], -8.0, ALU.mult, [tmp], [self.misc])
            self.tsc("dve", self.misc[:, 12:20], tmp[:], -16.0, ALU.mult, [tmp], [self.misc])
            wsrc = self.w_mod[:].rearrange("(kc p) n -> p kc n", p=128)
            sc3 = scT[:].rearrange("p (k c) -> p k c", c=2)
            for blk in range(12):
                w = wm[blk % 2]
                self.dma(w[:], wsrc[:, :, blk * 512:(blk + 1) * 512], [], [w], eng="sp" if blk % 2 == 0 else "act")
                p = self.ps()
                for m in range(4):
                    for kc in range(8):
                        self.mm(p[:, 2 * m:2 * m + 2], w[:, kc, m * 128:(m + 1) * 128], sc3[:, kc, :],
                                kc == 0, kc == 7, [w, scT], [p])
                for m in range(4):
                    mi = blk * 4 + m
                    self.tsc("dve", self.modT[:, mi, :], p[:, 2 * m:2 * m + 2], prm[:, P_BMOD + mi:P_BMOD + mi + 1],
                             ALU.add, [p, prm], [self.modT])
            m3 = self.modT
            for (dst, sc_off, g_off, one) in ((self.gs1, 8, P_GPRE, 1.0), (self.gs2, 32, P_GPRE2, 1.0),
                                              (self.gg1, 16, P_GPOST, 0.0), (self.gg2, 40, P_GPOST2, 0.0)):
                for c in range(2):
                    self.tsc("dve", dst[:, :, c], m3[:, sc_off:sc_off + 8, c], one, ALU.add, [m3], [dst])
                    self.tt("dve", dst[:, :, c], dst[:, :, c], prm[:, g_off:g_off + 8], ALU.mult, [dst, prm], [dst])
            if self.debug:
                self.dump("modT", self.modT[:], [128, 48, 2], F32, [self.modT])
                self.dump("gs1", self.gs1[:], [128, 8, 2], F32, [self.gs1])

    def load_cast(self, st, dst_ap, dst_tl, src_ap, shape, tag):
        key = "stg_" + tag
        if not hasattr(self, key):
            setattr(self, key, [self.sb(st, "%s%d" % (key, i), shape, F32) for i in range(2)])
        ring = getattr(self, key)
        s = ring[self.rr.get(key, 0) % 2]
        self.rr[key] = self.rr.get(key, 0) + 1
        self.dma(s[:], src_ap, [], [s], eng="sp")
        eng = self.pick("castE", ["act", "pool"])
        self.cp(eng, dst_ap, s[:], [s], [dst_tl])

    def phaseA(self):
        nc = self.nc
        prm, cst = self.prm_t, self.cst_t
        with contextlib.ExitStack() as st:
            win = self.sb(st, "win", [128, 8, DIN], BF16)
            win.b.multi = True
            wsrc = self.w_in[:].rearrange("(kc p) n -> p kc n", p=128)
            wup = self.sb(st, "wup_t", [128, 512], BF16)
            aup = self.sb(st, "aup_t", [128, 512], BF16)
            gup = self.sb(st, "gup_t", [128, 512], BF16)
            with contextlib.ExitStack() as st2:
                stgA = [self.sb(st2, "stgA%d" % i, [128, 8, 256], F32) for i in range(2)]
                nb = 0
                for c0 in range(0, DIN, 256):
                    cw = min(256, DIN - c0)
                    s_ = stgA[nb % 2]
                    self.dma(s_[:, :, 0:cw], wsrc[:, :, c0:c0 + cw], [], [s_], eng="sp" if nb % 2 == 0 else "act")
                    self.cp("act" if nb % 2 == 0 else "pool", win[:, :, c0:c0 + cw], s_[:, :, 0:cw], [s_], [win])
                    nb += 1
                for i, (src, dstt) in enumerate(((self.wup, wup), (self.aup, aup), (self.gup, gup))):
                    s_ = stgA[nb % 2]
                    nb += 1
                    s2 = s_[:].rearrange("p a b -> p (a b)")[:, 0:512]
                    self.dma(s2, src[:], [], [s_])
                    self.cp("dve", dstt[:], s2, [s_], [dstt])
            self.S.barrier()
            mSI = cst[:, C_MSI:C_MSI + 128].rearrange("p (q t) -> p q t", q=2)
            mL = cst[:, C_ML:C_ML + 64]
            idS = cst[:, C_IDS:C_IDS + 64]

            xin = self.sb(st, "xin", [128, 2, D], F32)
            xT = self.sb(st, "xT", [128, 8, TT], F32)
            sq = self.sb(st, "sq", [128, 8, TT], BF16)
            hT = self.sb(st, "hT", [128, 8, TT], BF16)
            rstd = self.sb(st, "rstd", [128, TT], F32)
            tmpA = [self.sb(st, "tmpA%d" % i, [128, TT], F32) for i in range(2)]
            rT = self.sb(st, "rT", [128, 4, TT], F32)
            kT = self.sb(st, "kT", [128, 4, TT], F32)
            vT = self.sb(st, "vT", [128, 4, TT], F32)
            xw = self.sb(st, "xw", [128, TT], BF16)
            xa = self.sb(st, "xa", [128, TT], BF16)
            xg = self.sb(st, "xg", [128, TT], BF16)
            xbT = self.sb(st, "xbT", [128, 4, TT], F32)
            gtmp = [self.sb(st, "gtmp%d" % i, [128, TT], F32) for i in range(3)]
            gate = self.sb(st, "gate", [128, 4, TT], BF16)
            gT = self.sb(st, "gT", [128, 4, TT], BF16)
            kkn = self.sb(st, "kkn", [128, 4, TT], F32)
            ksum = self.sb(st, "ksum", [128, 4, TT], F32)
            bon = self.sb(st, "bon", [128, 4, TT], BF16)
            sg = self.sb(st, "sg", [128, 4, TT], F32)
            cs = self.sb(st, "cs", [128, 4, TT], F32)
            E1 = self.sb(st, "E1", [128, 4, TT], F32)
            ad = self.sb(st, "ad", [128, 4, TT], F32)
            wk1 = self.sb(st, "wk1", [128, 4, TT], F32)
            wk2 = self.sb(st, "wk2", [128, 4, TT], F32)
            NC4 = TT // 64
            AR = [self.sb(st, "AR%d" % d, [128, 4, NC4, 2, 64], BF16) for d in range(2)]
            BK = [self.sb(st, "BK%d" % d, [128, 4, NC4, 2, 64], BF16) for d in range(2)]
            PEb1 = self.sb(st, "PEb", [128, 4, NC4, 64], F32)
            PEb = [PEb1, PEb1]
            tokB1 = self.sb(st, "tokB", [128, 2, 8, 64], BF16)
            tokK1 = self.sb(st, "tokK", [128, 2, 8, 64], BF16)
            tokB, tokK = [tokB1, tokB1], [tokK1, tokK1]
            tokV = self.sb(st, "tokV", [128, 2, 8, 64], BF16)
            NSET = 2
            MRBs = [self.sb(st, "MRBs%d" % i, [64, 8, 64], BF16) for i in range(NSET)]
            Lt0s = [self.sb(st, "Lt0_%d" % i, [64, 8, 64], F32R) for i in range(NSET)]
            Tfins = [self.sb(st, "Tfin%d" % i, [64, 8, 64], BF16) for i in range(NSET)]
            SCk = self.sb(st, "SCk", [128, 2, 8, 64], BF16)
            Lms = [[self.sb(st, "Lm%d_%d" % (i, k), [64, 8, 64], F32R) for i in range(2)] for k in range(NSET)]
            Ltms = [[self.sb(st, "Ltm%d_%d" % (i, k), [64, 8, 64], F32R) for i in range(2)] for k in range(NSET)]
            ILms = [self.sb(st, "ILm_%d" % k, [64, 8, 64], F32R) for k in range(NSET)]
            Ttms = [[self.sb(st, "Ttm%d_%d" % (i, k), [64, 8, 64], F32R) for i in range(2)] for k in range(NSET)]

            ones_bf, oblk, ident = self.ones_bf, self.oblk_bf, self.ident
            tiles = [(0, t0, True, 0) for t0 in range(0, TS, TT)] + [(1, TS, False, 1), (2, TS + TP, False, 1)]
            mS1 = cst[0:64, C_MSI1:C_MSI1 + 128].rearrange("p (q t) -> p q t", q=2)
            mL1 = cst[0:64, C_ML1:C_ML1 + 64]
            id64 = idS[0:64]
            def front(seq, g0, is_s, mc):
                if is_s:
                    src = self.xs[g0:g0 + TT, :].rearrange("(s p) f -> p s f", p=128)
                else:
                    l0 = g0 - TS
                    src = self.xp[l0:l0 + TT, :].rearrange("(s p) f -> p s f", p=128)
                self.dma(xin[:], src, [], [xin])
                if is_s:
                    petv = xT[:].rearrange("p a b -> p (a b)").rearrange("p (s f) -> p s f", s=2)
                    self.dma(petv, self.pe[g0:g0 + TT, :].rearrange("(s p) f -> p s f", p=128), [], [xT], eng="act")
                    self.tt("pool", xin[:], xin[:], petv, ALU.add, [xin, xT], [xin])
                for j in range(8):
                    p = self.ps()
                    for s in range(2):
                        self.tr(p[:, s * 128:(s + 1) * 128], xin[:, s, j * 128:(j + 1) * 128], ident[:], [xin, ident], [p])
                    self.cp("act", xT[:, j, :], p[:, 0:TT], [p], [xT])
                    self.tt("pool", sq[:, j, :], xT[:, j, :], xT[:, j, :], ALU.mult, [xT], [sq])
                    yield
                self.dma(self.xT_scr[:, :, g0:g0 + TT], xT[:], [xT], [self.xT_scr])
                p = self.ps()
                for j in range(8):
                    self.mm(p[:, 0:TT], ones_bf[:], sq[:, j, :], j == 0, j == 7, [ones_bf, sq], [p])
                self.act(rstd[:], p[:, 0:TT], AF.Sqrt, [p, self.epsT], [rstd], scale=1.0 / D, bias=self.epsT[:, 0:1])
                self.S.op("dve", lambda e: e.reciprocal(out=rstd[:], in_=rstd[:]), [rstd.b], [rstd.b])
                for j in range(8):
                    t = tmpA[j % 2]
                    self.tt("dve", t[:], xT[:, j, :], rstd[:], ALU.mult, [xT, rstd], [t])
                    self.act(hT[:, j, :], t[:], AF.Identity, [t, self.gs1, self.modT], [hT],
                             scale=self.gs1[:, j, mc:mc + 1], bias=self.modT[:, j, mc:mc + 1])
                    yield
                for m in range(23):
                    if m >= self.cutm:
                        break
                    p = self.ps()
                    for kc in range(8):
                        self.mm(p[:, 0:TT], win[:, kc, m * 128:(m + 1) * 128], hT[:, kc, :], kc == 0, kc == 7, [win, hT], [p])
                    pz = p[:, 0:TT]
                    if m < 4:
                        self.cp("act", rT[:, m, :], pz, [p], [rT])
                    elif m < 8:
                        self.cp("act", kT[:, m - 4, :], pz, [p], [kT])
                    elif m < 12:
                        self.cp("act", vT[:, m - 8, :], pz, [p], [vT])
                    elif m == 12:
                        self.act(xw[:], pz, AF.Tanh, [p], [xw])
                    elif m == 13:
                        self.cp("act", xa[:], pz, [p], [xa])
                    elif m == 14:
                        self.act(xg[:], pz, AF.Sigmoid, [p], [xg])
                    elif m < 19:
                        self.cp("act", xbT[:, m - 15, :], pz, [p], [xbT])
                    else:
                        j = m - 19
                        g0_, g1_, g2_ = gtmp
                        self.cp("act", g0_[:], pz, [p], [g0_])
                        self.tt("pool", g1_[:], g0_[:], g0_[:], ALU.mult, [g0_], [g1_])
                        self.tsc("dve", g1_[:], g1_[:], 0.044715, ALU.mult, [g1_], [g1_], 1.0, ALU.add)
                        self.tt("dve", g1_[:], g1_[:], g0_[:], ALU.mult, [g1_, g0_], [g1_])
                        self.act(g2_[:], g1_[:], AF.Sigmoid, [g1_], [g2_], scale=GELU_C)
                        self.tt("pool", gate[:, j, :], g0_[:], g2_[:], ALU.mult, [g0_, g2_], [gate])
                    yield
                self.dma(self.xb_scr[:, :, g0:g0 + TT], xbT[:], [xbT], [self.xb_scr])
                if self.debug and g0 == 0:
                    self.dump("hT", hT[:], [128, 8, TT], BF16, [hT])
                    self.dump("rT", rT[:], [128, 4, TT], F32, [rT])
                    self.dump("vT", vT[:], [128, 4, TT], F32, [vT])
                    self.dump("xbT", xbT[:], [128, 4, TT], F32, [xbT])
                    self.dump("gate", gate[:], [128, 4, TT], BF16, [gate])
                self.dma(self.gate_scr[:, :, g0:g0 + TT], gate[:], [gate], [self.gate_scr])
                for j in range(4):
                    p = self.ps()
                    self.mm(p[:, 0:TT], gup[:, j * 128:(j + 1) * 128], xg[:], True, True, [gup, xg], [p])
                    self.cp("act", gT[:, j, :], p[:, 0:TT], [p], [gT])
                    yield
                self.dma(self.g_scr[:, :, g0:g0 + TT], gT[:], [gT], [self.g_scr])
                for j in range(4):
                    self.tsc("dve", kkn[:, j, :], kT[:, j, :], prm[:, P_KK + j:P_KK + j + 1], ALU.mult, [kT, prm], [kkn])
                    self.tt("pool", sq[:, j, :], kkn[:, j, :], kkn[:, j, :], ALU.mult, [kkn], [sq])
                for j in range(4):
                    p = self.ps()
                    self.mm(p[:, 0:TT], oblk[:], sq[:, j, :], True, True, [oblk, sq], [p])
                    t = tmpA[j % 2]
                    self.act(t[:], p[:, 0:TT], AF.Sqrt, [p], [t])
                    self.tsc("dve", t[:], t[:], 1e-12, ALU.max, [t], [t])
                    self.S.op("dve", lambda e, t=t: e.reciprocal(out=t[:], in_=t[:]), [t.b], [t.b])
                    self.tt("dve", kkn[:, j, :], kkn[:, j, :], t[:], ALU.mult, [kkn, t], [kkn])
                    yield
                for s in range(2):
                    p = self.ps()
                    for j in range(4):
                        self.tr(p[:, j * 128:(j + 1) * 128], vT[:, j, s * 128:(s + 1) * 128], ident[:], [vT, ident], [p])
                    self.cp("act", tokV[:, s, :, :].rearrange("p h k -> p (h k)"), p[:], [p], [tokV])
                c0 = g0 // 64
                for s in range(2):
                    dst = self.vt_scr[c0 + 2 * s:c0 + 2 * s + 2, :, :].rearrange("c s f -> (c s) f")
                    self.dma(dst, tokV[:, s, :, :].rearrange("p h k -> p (h k)"), [tokV], [self.vt_scr])
                yield
            def prep(g0, d):
                c0 = g0 // 64
                for j in range(4):
                    p = self.ps()
                    self.mm(p[:, 0:TT], wup[d * 64:(d + 1) * 64, j * 128:(j + 1) * 128], xw[d * 64:(d + 1) * 64, :],
                            True, True, [wup, xw], [p])
                    self.act(sg[:, j, :], p[:, 0:TT], AF.Sigmoid, [p, prm], [sg],
                             bias=prm[:, P_W0 + 4 * d + j:P_W0 + 4 * d + j + 1])
                    if d == 0:
                        self.scan(cs[:, j, :], cst[:, C_RMF:C_RMF + TT], sg[:, j, :], 0.0, [cst, sg], [cs])
                    else:
                        self.scan(cs[:, j, ::-1], cst[:, C_RMB:C_RMB + TT][:, ::-1], sg[:, j, ::-1], 0.0, [cst, sg], [cs])
                    yield
                for j in range(4):
                    p = self.ps()
                    self.mm(p[:, 0:TT], aup[d * 64:(d + 1) * 64, j * 128:(j + 1) * 128], xa[d * 64:(d + 1) * 64, :],
                            True, True, [aup, xa], [p])
                    self.act(ad[:, j, :], p[:, 0:TT], AF.Sigmoid, [p, prm], [ad],
                             bias=prm[:, P_A0 + 4 * d + j:P_A0 + 4 * d + j + 1])
                    yield
                self.tt("pool", sg[:], cs[:], sg[:], ALU.subtract, [cs, sg], [sg])
                self.act(E1[:], cs[:], AF.Exp, [cs], [E1], scale=-LAM)
                self.act(cs[:], cs[:], AF.Exp, [cs], [cs], scale=LAM)
                self.act(sg[:], sg[:], AF.Exp, [sg], [sg], scale=-LAM)
                E2, E3 = cs, sg
                ar5 = AR[d]
                bk5 = BK[d]
                v4 = lambda tl: tl[:].rearrange("p j (c t) -> p j c t", t=64)
                self.stt(ar5[:, :, :, 0, :], v4(kkn), -1.0, v4(E3), ALU.mult, ALU.mult, [kkn, E3], [ar5])
                self.tt("pool", ar5[:, :, :, 1, :], v4(rT), v4(E1), ALU.mult, [rT, E1], [ar5])
                yield
                self.tt("dve", wk1[:], kkn[:], ad[:], ALU.mult, [kkn, ad], [wk1])
                self.tt("dve", wk1[:], wk1[:], E2[:], ALU.mult, [wk1, E2], [wk1])
                self.cp("pool", bk5[:, :, :, 0, :], v4(wk1), [wk1], [bk5])
                yield
                for j in range(4):
                    self.tsc("dve", wk2[:, j, :], ad[:, j, :], prm[:, P_KA + j:P_KA + j + 1], ALU.mult, [ad, prm, self.misc], [wk2],
                             self.misc[:, j:j + 1], ALU.add)
                self.tt("pool", wk2[:], wk2[:], kT[:], ALU.mult, [wk2, kT], [wk2])
                if d == 0:
                    self.cp("pool", ksum[:], wk2[:], [wk2], [ksum])
                else:
                    self.tt("pool", ksum[:], ksum[:], wk2[:], ALU.add, [ksum, wk2], [ksum])
                self.tt("dve", wk2[:], wk2[:], E2[:], ALU.mult, [wk2, E2], [wk2])
                self.cp("pool", bk5[:, :, :, 1, :], v4(wk2), [wk2], [bk5])
                yield
                te = 63 if d == 0 else 0
                pend_b = v4(E1)[:, :, :, te:te + 1].to_broadcast([128, 4, NC4, 64])
                self.cp("pool", PEb[d][:], pend_b, [E1], [PEb[d]])
                self.tt("dve", v4(wk1), v4(wk1), PEb[d][:], ALU.mult, [wk1, PEb[d]], [wk1])
                self.tt("dve", v4(wk2), v4(wk2), PEb[d][:], ALU.mult, [wk2, PEb[d]], [wk2])
                yield
                for (srcw, tokX, scr) in ((wk1, tokB[d], self.bh_scr), (wk2, tokK[d], self.kh_scr)):
                    for s in range(2):
                        p = self.ps()
                        for j in range(4):
                            self.tr(p[:, j * 128:(j + 1) * 128], srcw[:, j, s * 128:(s + 1) * 128], ident[:], [srcw, ident], [p])
                        self.cp("act", tokX[:, s, :, :].rearrange("p h k -> p (h k)"), p[:], [p], [tokX])
                        for cc in range(2):
                            self.dma(scr[c0 + 2 * s + cc, d * 64:(d + 1) * 64, :],
                                     tokX[cc * 64:(cc + 1) * 64, s, :, :].rearrange("p h k -> p (h k)"),
                                     [tokX], [scr])
                        yield
                for cl in range(NC4):
                    c = c0 + cl
                    for hp in range(2):
                        for (q, scr) in ((0, self.art_scr), (1, self.rrt_scr)):
                            dst = scr[c, d * 64:(d + 1) * 64, :].rearrange("k (j hp t) -> k j hp t", hp=2, t=64)[:, :, hp, :]
                            self.dma(dst, ar5[hp * 64:(hp + 1) * 64, :, cl, q, :], [ar5], [scr], eng="sp")
                        dst = self.pend_scr[c, d * 64:(d + 1) * 64, :].rearrange("k (j hp t) -> k j hp t", hp=2, t=64)[:, :, hp, :]
                        self.dma(dst, PEb[d][hp * 64:(hp + 1) * 64, :, cl, :], [PEb[d]], [self.pend_scr], eng="sp")
                yield
            def bonus(g0):
                for j in range(4):
                    self.stt(sq[:, j, :], rT[:, j, :], prm[:, P_RK + j:P_RK + j + 1], ksum[:, j, :], ALU.mult, ALU.mult,
                             [rT, prm, ksum], [sq])
                    p = self.ps()
                    self.mm(p[:, 0:TT], oblk[:], sq[:, j, :], True, True, [oblk, sq], [p])
                    self.tt("dve", bon[:, j, :], p[:, 0:TT], vT[:, j, :], ALU.mult, [p, vT], [bon])
                self.dma(self.bon_scr[:, :, g0:g0 + TT], bon[:], [bon], [self.bon_scr])
                yield
            def chunk_sck(g0):
                c0 = g0 // 64
                for cl in range(NC4):
                    c = c0 + cl
                    for hp in range(2):
                        p = self.ps()
                        for j in range(4):
                            for d in range(2):
                                self.mm(p[d * 64:(d + 1) * 64, j * 128:(j + 1) * 128],
                                        BK[d][hp * 64:(hp + 1) * 64, j, cl, 1, :],
                                        AR[d][hp * 64:(hp + 1) * 64, j, cl, :, :].rearrange("p q t -> p (q t)"),
                                        True, True, [BK[d], AR[d]], [p])
                        self.tt("dve", SCk[:, :, hp::2, :].rearrange("p q h t -> p h q t"),
                                p[:].rearrange("p (h q t) -> p h q t", q=2, t=64),
                                mSI.unsqueeze(1).to_broadcast([128, 4, 2, 64]), ALU.mult, [p, cst], [SCk])
                    self.dma(self.akt_scr[c], SCk[:, 0, :, :].rearrange("p h s -> p (h s)"), [SCk], [self.akt_scr], eng="act")
                    self.dma(self.mrkt_scr[c], SCk[:, 1, :, :].rearrange("p h s -> p (h s)"), [SCk], [self.mrkt_scr], eng="act")
                    yield
            def chunk_d(g0, d, cls, k):
                c0 = g0 // 64
                Lm, Ltm, ILm, Ttm, Lt0, Tfin = Lms[k], Ltms[k], ILms[k], Ttms[k], Lt0s[k], Tfins[k]
                MRB = MRBs[k]
                for cl in cls:
                    c = c0 + cl
                    msk = mSI[0:64] if d == 0 else mS1
                    mskL = mL[0:64] if d == 0 else mL1
                    L0 = Lm[0]
                    for hp in range(2):
                        p = self.ps()
                        for j in range(4):
                            self.mm(p[0:64, j * 128:(j + 1) * 128],
                                    BK[d][hp * 64:(hp + 1) * 64, j, cl, 0, :],
                                    AR[d][hp * 64:(hp + 1) * 64, j, cl, :, :].rearrange("p q t -> p (q t)"),
                                    True, True, [BK[d], AR[d]], [p])
                        p4 = p[0:64, :].rearrange("p (h q t) -> p h q t", q=2, t=64)
                        self.tt("dve", Lt0[:, hp::2, :], p4[:, :, 0, :], msk[:, 0, :].unsqueeze(1).to_broadcast([64, 4, 64]),
                                ALU.mult, [p, cst], [Lt0])
                        self.tt("dve", MRB[:, hp::2, :], p4[:, :, 1, :], msk[:, 1, :].unsqueeze(1).to_broadcast([64, 4, 64]),
                                ALU.mult, [p, cst], [MRB])
                        p2 = self.ps()
                        for j in range(4):
                            self.mm(p2[0:64, j * 64:(j + 1) * 64],
                                    AR[d][hp * 64:(hp + 1) * 64, j, cl, 0, :], BK[d][hp * 64:(hp + 1) * 64, j, cl, 0, :],
                                    True, True, [AR[d], BK[d]], [p2])
                        self.tt("dve", L0[:, hp::2, :], p2[0:64, 0:256].rearrange("p (h s) -> p h s", s=64),
                                mskL.unsqueeze(1).to_broadcast([64, 4, 64]), ALU.mult, [p2, cst], [L0])
                    self.dma(self.mrbt_scr[c, d * 64:(d + 1) * 64, :], MRB[:].rearrange("p h s -> p (h s)"),
                             [MRB], [self.mrbt_scr], eng="act")
                    yield
                    T0 = Ttm[0]
                    self.tt("pool", T0[:], Lt0[:].bitcast(F32), id64.unsqueeze(1).to_broadcast([64, 8, 64]), ALU.add,
                            [Lt0, cst], [T0])
                    L_prev, Tt_prev, Lt_prev = L0, T0, Lt0
                    for lev in range(1, 6):
                        L_new, Lt_new, Tt_new = Lm[lev % 2], Ltm[lev % 2], Ttm[lev % 2]
                        pA = self.ps()
                        for h in range(8):
                            self.mm(pA[0:64, h * 64:(h + 1) * 64], Lt_prev[:, h, :], L_prev[:, h, :], True, True,
                                    [Lt_prev, L_prev], [pA])
                        if lev < 5:
                            pB = self.ps()
                            for h in range(8):
                                self.mm(pB[0:64, h * 64:(h + 1) * 64], L_prev[:, h, :], Lt_prev[:, h, :], True, True,
                                        [Lt_prev, L_prev], [pB])
                        self.tt("dve", ILm[:], pA[0:64, :].rearrange("p (h s) -> p h s", s=64),
                                id64.unsqueeze(1).to_broadcast([64, 8, 64]), ALU.add, [pA, cst], [ILm])
                        if lev < 5:
                            self.cp("dve", L_new[:].rearrange("p h s -> p (h s)"), pA[0:64, :], [pA], [L_new])
                            self.cp("act", Lt_new[:].rearrange("p h s -> p (h s)"), pB[0:64, :], [pB], [Lt_new])
                        yield
                        pC = self.ps()
                        for h in range(8):
                            self.mm(pC[0:64, h * 64:(h + 1) * 64], ILm[:, h, :], Tt_prev[:, h, :], True, True,
                                    [ILm, Tt_prev], [pC])
                        if lev < 5:
                            self.cp("act", Tt_new[:].rearrange("p h s -> p (h s)"), pC[0:64, :], [pC], [Tt_new])
                        else:
                            self.cp("act", Tfin[:].rearrange("p h s -> p (h s)"), pC[0:64, :], [pC], [Tfin])
                        L_prev, Tt_prev, Lt_prev = L_new, Tt_new, Lt_new
                        yield
                    self.dma(self.ttt_scr[c, d * 64:(d + 1) * 64, :], Tfin[:].rearrange("p h s -> p (h s)"),
                             [Tfin], [self.ttt_scr], eng="act")


            def run_all(*gens):
                gens = list(gens)
                while gens:
                    for g in list(gens):
                        try:
                            next(g)
                        except StopIteration:
                            gens.remove(g)

            def seq_(*gens):
                for g in gens:
                    yield from g

            prev = None
            for (seq, g0, is_s, mc) in tiles[:self.ktiles]:
                if prev is None:
                    run_all(front(seq, g0, is_s, mc))
                else:
                    run_all(seq_(chunk_d(prev, 1, [0, 1], 0), chunk_sck(prev)), chunk_d(prev, 1, [2, 3], 1), front(seq, g0, is_s, mc))
                run_all(prep(g0, 0))
                run_all(chunk_d(g0, 0, [0, 1], 0), chunk_d(g0, 0, [2, 3], 1), prep(g0, 1))
                run_all(bonus(g0))
                prev = g0
            run_all(seq_(chunk_d(prev, 1, [0, 1], 0), chunk_sck(prev)), chunk_d(prev, 1, [2, 3], 1))

    def phaseB(self):
        with contextlib.ExitStack() as st:
            side = [self.gen_c0(st), self.phaseB_lru(st)]
            post = self.phaseB_post(st)
            next(post)
            done = np.zeros((2, NCH), bool)
            posted = [False] * (NTOK // 128)
            si = 0
            for info in self.phaseB_chain(st):
                for (d, c) in info:
                    done[d, c] = True
                for _ in range(2):
                    if side:
                        g = side[si % len(side)]
                        si += 1
                        try:
                            next(g)
                        except StopIteration:
                            side.remove(g)
                for b in range(NTOK // 128):
                    if not posted[b] and done[:, 2 * b:2 * b + 2].all():
                        posted[b] = True
                        post.send(b)
            for g in side:
                for _ in g:
                    pass
            for b in range(NTOK // 128):
                if not posted[b]:
                    post.send(b)
            if self.debug:
                self.S.barrier()
                self.dump("yscr", self.y_scr[:], [128, 8, NTOK], BF16, [self.y_scr])

    def gen_c0(self, st):
        stg = [self.sb(st, "stgC%d" % i, [128, 8, 512], F32) for i in range(2)]
        wb = [self.sb(st, "wbC%d" % i, [128, 8, 512], BF16) for i in range(2)]
        w1src = self.w1[:].rearrange("(kc p) n -> p kc n", p=128)
        for blk in range(8):
            s, o = stg[blk % 2], wb[blk % 2]
            self.dma(s[:], w1src[:, :, blk * 512:(blk + 1) * 512], [], [s])
            self.cp("pool", o[:], s[:], [s], [o])
            self.dma(self.w1_scr[blk], o[:], [o], [self.w1_scr], eng="act")
            yield
        w2src = self.w2[:].rearrange("(fc p) n -> p fc n", p=128)
        for m in range(8):
            s, o = stg[m % 2], wb[m % 2]
            s4 = s[:].rearrange("p k (a b) -> p (k a) b", b=128)
            o4 = o[:].rearrange("p k (a b) -> p (k a) b", b=128)
            self.dma(s4, w2src[:, :, m * 128:(m + 1) * 128], [], [s])
            self.cp("pool", o[:], s[:], [s], [o])
            self.dma(self.w2_scr[m], o4, [o], [self.w2_scr], eng="act")
            yield

    def phaseB_lru(self, st):
        prm, cst, misc = self.prm_t, self.cst_t, self.misc
        if True:
            wbd32 = self.sb(st, "wbd32", [128, 16, 128], F32)
            wbd = self.sb(st, "wbd", [128, 16, 128], BF16)
            self.memset("pool", wbd32[:], 0.0, [wbd32])
            self.S.barrier_on(wbd32)
            wbd32.b.multi = True
            for gi, src in enumerate((self.lwa, self.lwx)):
                for d in range(2):
                    for j in range(4):
                        for hb in range(2):
                            self.dma(wbd32[hb * 64:(hb + 1) * 64, (gi * 2 + d) * 4 + j, hb * 64:(hb + 1) * 64],
                                     src[d, 2 * j + hb], [], [wbd32])
            self.cp("dve", wbd[:], wbd32[:], [wbd32], [wbd])
            TM = TS
            xbp = self.sb(st, "xbp", [128, TM + 4], F32)
            xc = self.sb(st, "xc", [128, TM], F32)
            xcb = self.sb(st, "xcb", [128, TM], BF16)
            gt = self.sb(st, "gt_l", [128, TM], BF16)
            a_t = self.sb(st, "a_t", [128, TM], F32)
            bx_t = self.sb(st, "bx_t", [128, TM], F32)
            s_t = self.sb(st, "s_t", [128, TM], F32)
            hs = [self.sb(st, "hs%d" % d, [128, TM], F32) for d in range(2)]
            yb = self.sb(st, "yb", [128, TM], BF16)
            for (seq, g0, T) in ((0, 0, TS), (1, TS, TP), (2, TS + TP, TP)):
                for j in range(4):
                    self.memset("pool", xbp[:, 0:2], 0.0, [xbp])
                    self.memset("pool", xbp[:, T + 2:T + 4], 0.0, [xbp])
                    self.dma(xbp[:, 2:T + 2], self.xb_scr[:, j, g0:g0 + T], [self.xb_scr], [xbp])
                    self.dma(gt[:, 0:T], self.gate_scr[:, j, g0:g0 + T], [self.gate_scr], [gt], eng="act")
                    cw = lambda i: prm[:, P_CW + 4 * i + j:P_CW + 4 * i + j + 1]
                    self.act(xc[:, 0:T], xbp[:, 0:T], AF.Identity, [xbp, prm], [xc], scale=cw(0), bias=prm[:, P_CB + j:P_CB + j + 1])
                    for i in range(1, 4):
                        self.stt(xc[:, 0:T], xbp[:, i:i + T], cw(i), xc[:, 0:T], ALU.mult, ALU.add, [xbp, prm, xc], [xc])
                    self.cp("pool", xcb[:, 0:T], xc[:, 0:T], [xc], [xcb])
                    yield
                    for d in range(2):
                        for t0 in range(0, T, 512):
                            tw = min(512, T - t0)
                            p = self.ps()
                            self.mm(p[:, 0:tw], wbd[:, (0 * 2 + d) * 4 + j, :], xcb[:, t0:t0 + tw], True, True, [wbd, xcb], [p])
                            self.act(s_t[:, t0:t0 + tw], p[:, 0:tw], AF.Sigmoid, [p, prm], [s_t],
                                     bias=prm[:, P_BA + 4 * d + j:P_BA + 4 * d + j + 1])
                            p2 = self.ps()
                            self.mm(p2[:, 0:tw], wbd[:, (1 * 2 + d) * 4 + j, :], xcb[:, t0:t0 + tw], True, True, [wbd, xcb], [p2])
                            self.act(bx_t[:, t0:t0 + tw], p2[:, 0:tw], AF.Sigmoid, [p2, prm], [bx_t],
                                     bias=prm[:, P_BX + 4 * d + j:P_BX + 4 * d + j + 1])
                        col = 4 + d * 4 + j
                        self.act(a_t[:, 0:T], s_t[:, 0:T], AF.Exp, [s_t, misc], [a_t], scale=misc[:, col:col + 1])
                        self.act(s_t[:, 0:T], s_t[:, 0:T], AF.Exp, [s_t, misc], [s_t], scale=misc[:, col + 8:col + 9])
                        self.act(s_t[:, 0:T], s_t[:, 0:T], AF.Sqrt, [s_t], [s_t], scale=-1.0, bias=1.0)
                        self.tt("pool", bx_t[:, 0:T], bx_t[:, 0:T], xc[:, 0:T], ALU.mult, [bx_t, xc], [bx_t])
                        self.tt("dve", bx_t[:, 0:T], bx_t[:, 0:T], s_t[:, 0:T], ALU.mult, [bx_t, s_t], [bx_t])
                        h = hs[d]
                        if seq == 0:
                            init = prm[:, P_H0 + 4 * d + j:P_H0 + 4 * d + j + 1]
                        else:
                            init = 0.0
                        if d == 0:
                            self.scan(h[:, 0:T], a_t[:, 0:T], bx_t[:, 0:T], init, [a_t, bx_t, prm], [h])
                        else:
                            self.scan(h[:, 0:T][:, ::-1], a_t[:, 0:T][:, ::-1], bx_t[:, 0:T][:, ::-1], init, [a_t, bx_t, prm], [h])
                        if seq > 0:
                            col_o = j * 4 + (seq - 1) * 2 + d
                            te = T - 1 if d == 0 else 0
                            self.cp("pool", self.stl_t[:, col_o:col_o + 1], h[:, te:te + 1], [h], [self.stl_t])
                        yield
                    self.tt("pool", hs[0][:, 0:T], hs[0][:, 0:T], hs[1][:, 0:T], ALU.add, [hs[0], hs[1]], [hs[0]])
                    self.tt("dve", yb[:, 0:T], hs[0][:, 0:T], gt[:, 0:T], ALU.mult, [hs[0], gt], [yb])
                    self.dma(self.y_scr[:, 4 + j, g0:g0 + T], yb[:, 0:T], [yb], [self.y_scr])
            self.dma(self.stl_o[:], self.stl_t[:], [self.stl_t], [self.stl_o])

    def phaseB_chain(self, st):
        if True:
            NB = 3
            def ring(name, dt=BF16):
                return [self.sb2(st, "%s%d" % (name, i), [128, 512], dt) for i in range(NB)]
            art, rrt, ttt, akt, mrbt, mrkt, bh, kh, vt = [ring(n) for n in
                                                          ("c_art", "c_rrt", "c_ttt", "c_akt", "c_mrbt", "c_mrkt", "c_bh", "c_kh", "c_vt")]
            pend = ring("c_pend", F32)
            Hf = self.sb2(st, "Hf", [128, 512], F32)
            Hb = self.sb2(st, "Hb", [128, 512], BF16)
            Zs = self.sb2(st, "Zs", [128, 512], BF16)
            Us = self.sb2(st, "Us", [128, 512], BF16)
            Yt = [self.sb2(st, "Yt%d" % i, [128, 512], F32) for i in range(2)]
            tmpH = self.sb2(st, "tmpH", [128, 512], F32)
            hs_ = lambda h: slice(h * 64, (h + 1) * 64)
            steps = []
            for (seq, cbase, n) in ((0, 0, 32), (1, 32, 4), (2, 36, 4)):
                for i in range(n):
                    steps.append((seq, cbase, n, i))

            def loads(k):
                seq, cbase, n, i = steps[k]
                r = k % NB
                for d in range(2):
                    sl = slice(d * 64, (d + 1) * 64)
                    c = cbase + i if d == 0 else cbase + n - 1 - i
                    for (tl, scr) in ((art, self.art_scr), (rrt, self.rrt_scr), (ttt, self.ttt_scr), (akt, self.akt_scr),
                                      (mrbt, self.mrbt_scr), (mrkt, self.mrkt_scr), (bh, self.bh_scr), (kh, self.kh_scr),
                                      (pend, self.pend_scr)):
                        self.dma(tl[r][d][sl, :], scr[c, sl, :], [scr], [tl[r][d]], eng="sp")
                    self.dma(vt[r][d][sl, :], self.vt_scr[c], [self.vt_scr], [vt[r][d]], eng="sp")

            loads(0)
            for k in range(len(steps)):
                seq, cbase, n, i = steps[k]
                step = k + 1
                r = k % NB
                if k + 1 < len(steps):
                    loads(k + 1)
                if i == 0:
                    for d in range(2):
                        sl = slice(d * 64, (d + 1) * 64)
                        if seq == 0:
                            self.dma(Hf[d][sl, :], self.h0r[sl, :], [], [Hf[d]], eng="act")
                        else:
                            self.memset("dve", Hf[d][sl, :], 0.0, [Hf[d]])
                        self.cp("dve", Hb[d][sl, :], Hf[d][sl, :], [Hf[d]], [Hb[d]])
                if True:
                    ctx = []
                    for d in range(2):
                        sl = slice(d * 64, (d + 1) * 64)
                        c = cbase + i if d == 0 else cbase + n - 1 - i
                        ops = tuple(x[r][d] for x in (art, rrt, ttt, akt, mrbt, mrkt, bh, kh, vt, pend))
                        ctx.append((d, sl, c, ops))
                    pZs = {}
                    for (d, sl, c, (A_, R_, T_, AK_, MRB_, MRK_, B_, K_, V_, PE_)) in ctx:
                        H_, Z_ = Hb[d], Zs[d]
                        pZ = self.ps()
                        for h in range(8):
                            self.mm(pZ[sl, hs_(h)], A_[sl, hs_(h)], H_[sl, hs_(h)], True, False, [A_, H_], [pZ])
                            self.mm(pZ[sl, hs_(h)], AK_[sl, hs_(h)], V_[sl, hs_(h)], False, True, [AK_, V_], [pZ])
                        pZs[d] = pZ
                    pYs = {}
                    for (d, sl, c, (A_, R_, T_, AK_, MRB_, MRK_, B_, K_, V_, PE_)) in ctx:
                        H_ = Hb[d]
                        pY = self.ps()
                        pYs[d] = pY
                    for (d, sl, c, ops) in ctx:
                        self.cp("act", Zs[d][sl, :], pZs[d][sl, :], [pZs[d]], [Zs[d]])
                    pUs = {}
                    for (d, sl, c, (A_, R_, T_, AK_, MRB_, MRK_, B_, K_, V_, PE_)) in ctx:
                        Z_ = Zs[d]
                        pU = self.ps()
                        for h in range(8):
                            self.mm(pU[sl, hs_(h)], T_[sl, hs_(h)], Z_[sl, hs_(h)], True, True, [T_, Z_], [pU])
                        pUs[d] = pU
                    for (d, sl, c, ops) in ctx:
                        self.cp("act", Us[d][sl, :], pUs[d][sl, :], [pUs[d]], [Us[d]])
                    pHs = {}
                    for (d, sl, c, (A_, R_, T_, AK_, MRB_, MRK_, B_, K_, V_, PE_)) in ctx:
                        U_ = Us[d]
                        self.tt("dve", tmpH[d][sl, :], Hf[d][sl, :], PE_[sl, :], ALU.mult, [Hf[d], PE_], [tmpH[d]])
                        pH = self.ps()
                        for h in range(8):
                            self.mm(pH[sl, hs_(h)], B_[sl, hs_(h)], U_[sl, hs_(h)], True, False, [B_, U_], [pH])
                            self.mm(pH[sl, hs_(h)], K_[sl, hs_(h)], V_[sl, hs_(h)], False, True, [K_, V_], [pH])
                        pHs[d] = pH
                    for (d, sl, c, (A_, R_, T_, AK_, MRB_, MRK_, B_, K_, V_, PE_)) in ctx:
                        H_, U_ = Hb[d], Us[d]
                        pY = pYs[d]
                        for h in range(8):
                            self.mm(pY[sl, hs_(h)], R_[sl, hs_(h)], H_[sl, hs_(h)], True, False, [R_, H_], [pY])
                            self.mm(pY[sl, hs_(h)], MRB_[sl, hs_(h)], U_[sl, hs_(h)], False, False, [MRB_, U_], [pY])
                            self.mm(pY[sl, hs_(h)], MRK_[sl, hs_(h)], V_[sl, hs_(h)], False, True, [MRK_, V_], [pY])
                    for (d, sl, c, ops) in ctx:
                        self.tt("dve", Hf[d][sl, :], tmpH[d][sl, :], pHs[d][sl, :], ALU.add, [tmpH[d], pHs[d]], [Hf[d]])
                        self.cp("dve", Hb[d][sl, :], Hf[d][sl, :], [Hf[d]], [Hb[d]])
                    for (d, sl, c, ops) in ctx:
                        y = Yt[step % 2][d]
                        self.cp("act", y[sl, :], pYs[d][sl, :], [pYs[d]], [y])
                        self.dma(self.ytok_scr[d, c * 64:(c + 1) * 64, :], y[sl, :], [y], [self.ytok_scr], eng="act")
                if seq > 0 and i == n - 1:
                    self.dma(self.str_o[seq - 1], Hf[0][:], [Hf[0], Hf[1]], [self.str_o], eng="act")
                yield [(0, cbase + i), (1, cbase + n - 1 - i)]

    def phaseB_post(self, st):
        prm, cst = self.prm_t, self.cst_t
        ident = self.ident
        if True:
            yf = [self.sb(st, "yf%d" % i, [128, 512], F32) for i in range(2)]
            yb2 = [self.sb(st, "yb2%d" % i, [128, 512], F32) for i in range(2)]
            cen = self.sb(st, "cen", [128, 8, 64], F32)
            sqv = self.sb(st, "sqv", [128, 8, 64], F32)
            mean = self.sb(st, "mean", [128, 8], F32)
            var = self.sb(st, "var", [128, 8], F32)
            gl = [self.sb(st, "gl%d" % i, [128, 4, 128], BF16) for i in range(2)]
            bl = [self.sb(st, "bl%d" % i, [128, 4, 128], BF16) for i in range(2)]
            ynT = self.sb(st, "ynT", [128, 4, 128], F32)
            yo = [self.sb(st, "yo%d" % i, [128, 4, 128], BF16) for i in range(2)]
            it = -1
            blk = yield
            while True:
                it += 1
                g0 = blk * 128
                a, b = yf[it % 2], yb2[it % 2]
                g_, b_ = gl[it % 2], bl[it % 2]
                o = yo[it % 2]
                self.dma(a[:], self.ytok_scr[0, g0:g0 + 128, :], [self.ytok_scr], [a])
                self.dma(b[:], self.ytok_scr[1, g0:g0 + 128, :], [self.ytok_scr], [b], eng="act")
                self.dma(g_[:], self.g_scr[:, :, g0:g0 + 128], [self.g_scr], [g_])
                self.dma(b_[:], self.bon_scr[:, :, g0:g0 + 128], [self.bon_scr], [b_], eng="act")
                a3 = a[:].rearrange("p (h v) -> p h v", v=64)
                self.tt("pool", a[:], a[:], b[:], ALU.add, [a, b], [a])
                self.S.op("dve", lambda e, a3=a3: e.tensor_reduce(out=mean[:], in_=a3, op=ALU.add, axis=mybir.AxisListType.X),
                          [a.b], [mean.b])
                self.tsc("dve", mean[:], mean[:], 1.0 / 64, ALU.mult, [mean], [mean])
                self.tt("dve", cen[:], a3, mean[:].unsqueeze(2).to_broadcast([128, 8, 64]), ALU.subtract, [a, mean], [cen])
                self.tt("pool", sqv[:], cen[:], cen[:], ALU.mult, [cen], [sqv])
                self.S.op("dve", lambda e: e.tensor_reduce(out=var[:], in_=sqv[:], op=ALU.add, axis=mybir.AxisListType.X),
                          [sqv.b], [var.b])
                self.act(var[:], var[:], AF.Sqrt, [var, self.epsT], [var], scale=1.0 / 64, bias=self.epsT[:, 1:2])
                self.S.op("dve", lambda e: e.reciprocal(out=var[:], in_=var[:]), [var.b], [var.b])
                self.tt("dve", cen[:], cen[:], var[:].unsqueeze(2).to_broadcast([128, 8, 64]), ALU.mult, [cen, var], [cen])
                p = self.ps()
                cen2 = cen[:].rearrange("p h v -> p (h v)")
                for j in range(4):
                    self.tr(p[:, j * 128:(j + 1) * 128], cen2[:, j * 128:(j + 1) * 128], ident[:], [cen, ident], [p])
                for j in range(4):
                    self.act(ynT[:, j, :], p[:, j * 128:(j + 1) * 128], AF.Identity, [p, prm], [ynT],
                             scale=prm[:, P_LNG + j:P_LNG + j + 1], bias=prm[:, P_LNB + j:P_LNB + j + 1])
                self.tt("pool", ynT[:], ynT[:], b_[:], ALU.add, [ynT, b_], [ynT])
                self.tt("dve", o[:], ynT[:], g_[:], ALU.mult, [ynT, g_], [o])
                self.dma(self.y_scr[:, 0:4, g0:g0 + 128], o[:], [o], [self.y_scr])
                blk = yield

    def phaseC(self):
        prm, cst = self.prm_t, self.cst_t
        ident, ones_bf = self.ident, self.ones_bf
        with contextlib.ExitStack() as st:
            pass
        import os
        kcc = int(os.environ.get("KCC", "99"))
        if kcc == 0:
            return
        with contextlib.ExitStack() as st:
            wout = self.sb(st, "wout", [128, 8, D], BF16)
            wsrc = self.w_out[:].rearrange("(kc p) n -> p kc n", p=128)
            with contextlib.ExitStack() as st2:
                stg = [self.sb(st2, "stgD%d" % i, [128, 8, 256], F32) for i in range(2)]
                for cb in range(4):
                    s = stg[cb % 2]
                    self.dma(s[:], wsrc[:, :, cb * 256:(cb + 1) * 256], [], [s])
                    self.cp("act", wout[:, :, cb * 256:(cb + 1) * 256], s[:], [s], [wout])
            self.S.barrier()
            yTs = [self.sb(st, "yT_c%d" % i, [128, 8, TC], BF16) for i in range(2)]
            oT = self.sb(st, "oT", [128, 8, TC], F32)
            sq = self.sb(st, "sq_c", [128, 8, TC], BF16)
            xT = self.sb(st, "xT_c", [128, 8, TC], F32)
            h2 = self.sb(st, "h2", [128, 8, TC], BF16)
            f = self.sb(st, "f_c", [128, 32, TC], BF16)
            otok = self.sb(st, "otok", [128, 4, D], F32)
            rstd = self.sb(st, "rstd_c", [128, TC], F32)
            tmp = [self.sb(st, "tmpC%d" % i, [128, TC], F32) for i in range(2)]
            NW = 4
            w1r = [self.sb(st, "w1r%d" % i, [128, 8, 512], BF16) for i in range(NW)]
            w2r = [self.sb(st, "w2r%d" % i, [128, 32, 128], BF16) for i in range(NW)]
            ntile = min(NTOK // TC, kcc)
            wseq = []
            for ti_ in range(ntile):
                wseq += [("w1", b_) for b_ in range(8)] + [("w2", b_) for b_ in range(8)]
            wstate = {"issued": 0, "w1": 0, "w2": 0}
            wbuf = {}

            def issue_upto(n):
                while wstate["issued"] < min(n, len(wseq)):
                    k_ = wstate["issued"]
                    kind, b_ = wseq[k_]
                    ring = w1r if kind == "w1" else w2r
                    buf = ring[wstate[kind] % NW]
                    wstate[kind] += 1
                    scr = self.w1_scr if kind == "w1" else self.w2_scr
                    self.dma(buf[:], scr[b_], [scr], [buf], eng="sp" if k_ % 2 == 0 else "act")
                    wbuf[k_] = buf
                    wstate["issued"] += 1

            def rms(src, R):
                p = self.ps()
                for j in range(8):
                    self.mm(p[:], ones_bf[:], sq[:, j, :], j == 0, j == 7, [ones_bf, sq], [p])
                self.act(rstd[:], p[:], AF.Sqrt, [p, self.epsT], [rstd], scale=1.0 / D, bias=self.epsT[:, 0:1])
                self.S.op("dve", lambda e: e.reciprocal(out=rstd[:], in_=rstd[:]), [rstd.b], [rstd.b])

            def resid(gg, mc):
                for j in range(8):
                    t = tmp[j % 2]
                    self.tt("dve", t[:], oT[:, j, :], rstd[:], ALU.mult, [oT, rstd], [t])
                    self.stt(xT[:, j, :], t[:], gg[:, j, mc:mc + 1], xT[:, j, :], ALU.mult, ALU.add, [t, gg, xT], [xT])

            wi = 0
            for ti in range(NTOK // TC):
                if ti >= kcc:
                    break
                g0 = ti * TC
                mc = 0 if g0 < TS else 1
                yT = yTs[ti % 2]
                if ti == 0:
                    self.dma(yT[:], self.y_scr[:, :, g0:g0 + TC], [self.y_scr], [yT])
                self.dma(xT[:], self.xT_scr[:, :, g0:g0 + TC], [self.xT_scr], [xT], eng="act")
                if ti + 1 < ntile:
                    self.dma(yTs[(ti + 1) % 2][:], self.y_scr[:, :, g0 + TC:g0 + 2 * TC], [self.y_scr], [yTs[(ti + 1) % 2]])
                issue_upto(ti * 16 + 4)
                for m in range(8):
                    p = self.ps()
                    for kc in range(8):
                        self.mm(p[:], wout[:, kc, m * 128:(m + 1) * 128], yT[:, kc, :], kc == 0, kc == 7, [wout, yT], [p])
                    self.cp("act", oT[:, m, :], p[:], [p], [oT])
                    self.tt("pool", sq[:, m, :], oT[:, m, :], oT[:, m, :], ALU.mult, [oT], [sq])
                rms(oT, None)
                resid(self.gg1, mc)
                for j in range(8):
                    self.tt("pool", sq[:, j, :], xT[:, j, :], xT[:, j, :], ALU.mult, [xT], [sq])
                rms(xT, None)
                for j in range(8):
                    t = tmp[j % 2]
                    self.tt("dve", t[:], xT[:, j, :], rstd[:], ALU.mult, [xT, rstd], [t])
                    self.act(h2[:, j, :], t[:], AF.Identity, [t, self.gs2, self.modT], [h2],
                             scale=self.gs2[:, j, mc:mc + 1], bias=self.modT[:, 24 + j, mc:mc + 1])
                for blk in range(8):
                    issue_upto(ti * 16 + blk + 4)
                    w = wbuf[ti * 16 + blk]
                    for c4 in range(4):
                        fc = blk * 4 + c4
                        p = self.ps()
                        for kc in range(8):
                            self.mm(p[:], w[:, kc, c4 * 128:(c4 + 1) * 128], h2[:, kc, :], kc == 0, kc == 7, [w, h2], [p])
                        t = tmp[fc % 2]
                        self.act(t[:], p[:], AF.Relu, [p], [t])
                        self.tt("pool" if fc % 2 == 0 else "dve", f[:, fc, :], t[:], t[:], ALU.mult, [t], [f])
                for m in range(8):
                    issue_upto(ti * 16 + 8 + m + 4)
                    w = wbuf[ti * 16 + 8 + m]
                    p = self.ps()
                    for fc in range(32):
                        self.mm(p[:], w[:, fc, :], f[:, fc, :], fc == 0, fc == 31, [w, f], [p])
                    self.cp("act", oT[:, m, :], p[:], [p], [oT])
                    self.tt("pool", sq[:, m, :], oT[:, m, :], oT[:, m, :], ALU.mult, [oT], [sq])
                rms(oT, None)
                resid(self.gg2, mc)
                for s in range(4):
                    for half in range(2):
                        p = self.ps()
                        for jj in range(4):
                            j = half * 4 + jj
                            self.tr(p[:, jj * 128:(jj + 1) * 128], xT[:, j, s * 128:(s + 1) * 128], ident[:], [xT, ident], [p])
                        self.cp("act" if half == 0 else "dve", otok[:, s, half * 512:(half + 1) * 512], p[:], [p], [otok])
                if g0 < TS:
                    dst = self.ys[g0:g0 + TC, :].rearrange("(s p) f -> p s f", p=128)
                    self.dma(dst, otok[:], [otok], [self.ys])
                else:
                    dst = self.yp[:, :].rearrange("(s p) f -> p s f", p=128)
                    self.dma(dst, otok[:], [otok], [self.yp])


def _fm(v):
    v = np.asarray(v, np.float32).reshape(-1, 128)
    return np.ascontiguousarray(v.T)


def _pos_embed():
    def sincos(pos, dim):
        omega = (1.0 / (10000.0 ** (np.arange(dim // 2, dtype=np.float32) / np.float32(dim // 2)))).astype(np.float32)
        ang = pos.astype(np.float32)[:, None] * omega[None, :]
        return np.concatenate([np.sin(ang), np.cos(ang)], axis=-1).astype(np.float32)
    rows = TS // 64
    half = D // 2
    e_row = sincos(np.arange(rows), half)
    e_col = sincos(np.arange(64), half)
    emb = np.concatenate([np.broadcast_to(e_row[:, None, :], (rows, 64, half)),
                          np.broadcast_to(e_col[None, :, :], (rows, 64, half))], axis=-1)
    return np.ascontiguousarray(emb.reshape(rows * 64, D).astype(np.float32))


def _consts():
    c = np.zeros((128, NCST), np.float32)
    c[:, C_ID:C_ID + 128] = np.eye(128, dtype=np.float32)
    ob = np.zeros((128, 128), np.float32)
    ob[:64, :64] = 1.0
    ob[64:, 64:] = 1.0
    c[:, C_OB:C_OB + 128] = ob
    s = np.arange(64)[:, None]
    t = np.arange(64)[None, :]
    msi = np.zeros((128, 2, 64), np.float32)
    msi[:64, 0] = (s < t)
    msi[:64, 1] = (s <= t)
    msi[64:, 0] = (s > t)
    msi[64:, 1] = (s >= t)
    c[:, C_MSI:C_MSI + 128] = msi.reshape(128, 128)
    ml = np.zeros((128, 64), np.float32)
    ml[:64] = (t < s)
    ml[64:] = (t > s)
    c[:, C_ML:C_ML + 64] = ml
    ids = np.zeros((128, 64), np.float32)
    ids[:64] = np.eye(64)
    ids[64:] = np.eye(64)
    c[:, C_IDS:C_IDS + 64] = ids
    c[:64, C_MSI1:C_MSI1 + 128] = msi[64:].reshape(64, 128)
    c[:64, C_ML1:C_ML1 + 64] = ml[64:]
    tt_ = np.arange(TT)
    c[:, C_RMF:C_RMF + TT] = (tt_ % 64 != 0).astype(np.float32)[None, :]
    c[:, C_RMB:C_RMB + TT] = (tt_ % 64 != 63).astype(np.float32)[None, :]
    return c


_NC_CACHE = {}


def kernel(x_prompt, x_sample, c, state_rwkv, state_lru, c_ctx, w_mod, b_mod,
           g_pre_mix, g_post_mix, g_pre_mlp, g_post_mlp, w_in,
           rwkv_w0, rwkv_w_up, rwkv_a0, rwkv_a_up, rwkv_g_up, rwkv_k_k, rwkv_k_a, rwkv_r_k,
           rwkv_lnx_g, rwkv_lnx_b, lru_conv_w, lru_conv_b, lru_wa, lru_ba, lru_wx, lru_bx,
           lru_lambda, w_out, w_mlp1, w_mlp2, _debug=False):
    f = lambda a: np.ascontiguousarray(np.asarray(a, np.float32))
    x_prompt, x_sample, c, state_rwkv, state_lru, c_ctx = map(f, (x_prompt, x_sample, c, state_rwkv, state_lru, c_ctx))
    if "nc" not in _NC_CACHE:
        _NC_CACHE["nc"] = K(debug=_debug).build()
    nc = _NC_CACHE["nc"]
    pe = _pos_embed()
    cst = _consts()
    shared = {
        "pe": pe, "cst": cst,
        "w_mod": f(w_mod[0]), "w_in": f(w_in[0]), "w_out": f(w_out[0]), "w1": f(w_mlp1[0]), "w2": f(w_mlp2[0]),
        "wup": f(rwkv_w_up[0]).reshape(128, 512), "aup": f(rwkv_a_up[0]).reshape(128, 512), "gup": f(rwkv_g_up[0]),
        "lwa": f(lru_wa[0]), "lwx": f(lru_wx[0]),
    }
    prm0 = np.zeros((128, NPRM), np.float32)
    prm0[:, P_GPRE:P_GPRE + 8] = _fm(g_pre_mix[0])
    prm0[:, P_GPOST:P_GPOST + 8] = _fm(g_post_mix[0])
    prm0[:, P_GPRE2:P_GPRE2 + 8] = _fm(g_pre_mlp[0])
    prm0[:, P_GPOST2:P_GPOST2 + 8] = _fm(g_post_mlp[0])
    prm0[:, P_BMOD:P_BMOD + 48] = _fm(b_mod[0])
    for d in range(2):
        prm0[:, P_W0 + 4 * d:P_W0 + 4 * d + 4] = _fm(rwkv_w0[0, d])
        prm0[:, P_A0 + 4 * d:P_A0 + 4 * d + 4] = _fm(rwkv_a0[0, d])
        prm0[:, P_BA + 4 * d:P_BA + 4 * d + 4] = _fm(lru_ba[0, d])
        prm0[:, P_BX + 4 * d:P_BX + 4 * d + 4] = _fm(lru_bx[0, d])
        prm0[:, P_LAM + 4 * d:P_LAM + 4 * d + 4] = _fm(lru_lambda[0, d])
    prm0[:, P_KK:P_KK + 4] = _fm(rwkv_k_k[0])
    prm0[:, P_KA:P_KA + 4] = _fm(rwkv_k_a[0])
    prm0[:, P_RK:P_RK + 4] = _fm(np.asarray(rwkv_r_k[0]).reshape(-1))
    prm0[:, P_LNG:P_LNG + 4] = _fm(rwkv_lnx_g[0])
    prm0[:, P_LNB:P_LNB + 4] = _fm(rwkv_lnx_b[0])
    for i in range(4):
        prm0[:, P_CW + 4 * i:P_CW + 4 * i + 4] = _fm(lru_conv_w[0, i])
    prm0[:, P_CB:P_CB + 4] = _fm(lru_conv_b[0])
    in_maps = []
    for i in range(8):
        prm = prm0.copy()
        for d in range(2):
            prm[:, P_H0 + 4 * d:P_H0 + 4 * d + 4] = _fm(state_lru[i, 0, d])
        cT = np.zeros((128, 8, 2), np.float32)
        cT[:, :, 0] = _fm(c[i])
        cT[:, :, 1] = _fm(c_ctx)
        h0 = np.ascontiguousarray(state_rwkv[i, 0].transpose(0, 3, 1, 2)).reshape(128, 512)
        m = dict(shared)
        m.update({"xs": x_sample[i], "xp": np.ascontiguousarray(x_prompt[2 * i:2 * i + 2].reshape(2 * TP, D)),
                  "cT": cT.reshape(128, 16), "h0r": h0, "prm": prm})
        in_maps.append(m)
    res = run_bass_kernel_spmd(nc, in_maps, core_ids=list(range(8)))
    R = res.results
    y_prompt = np.zeros((16, TP, D), np.float32)
    y_sample = np.zeros((8, TS, D), np.float32)
    st_r = np.zeros((16, 1, 2, 8, 64, 64), np.float32)
    st_l = np.zeros((16, 1, 2, 512), np.float32)
    for i in range(8):
        r = R[i]
        y_sample[i] = r["ys"]
        y_prompt[2 * i:2 * i + 2] = r["yp"].reshape(2, TP, D)
        so = r["str_o"].reshape(2, 2, 64, 8, 64)
        st_r[2 * i:2 * i + 2, 0] = so.transpose(0, 1, 3, 4, 2)
        sl = r["stl_o"].reshape(128, 4, 2, 2)
        st_l[2 * i:2 * i + 2, 0] = sl.transpose(2, 3, 1, 0).reshape(2, 2, 512)
    if _debug:
        return (y_prompt, y_sample, st_r, st_l), R
    return (y_prompt, y_sample, st_r, st_l)
```

```python
import contextlib
import numpy as np
import concourse.bass as bass
import concourse.mybir as mybir
from concourse.bass_utils import run_bass_kernel_spmd

F32 = mybir.dt.float32
BF16 = mybir.dt.bfloat16
F32R = mybir.dt.float32r
AF = mybir.ActivationFunctionType
ALU = mybir.AluOpType

D = 1024
TS = 2048
TP = 256
NTOK = TS + 2 * TP
NCH = NTOK // 64
DIN = 2944
DFF = 4096
LAM = float(np.exp(-0.5))
EPS = 1e-6
LNX_EPS = 64e-5
TT = 256
TC = 512
GELU_C = 1.5957691216057308

P_GPRE, P_GPOST, P_GPRE2, P_GPOST2 = 0, 8, 16, 24
P_BMOD = 32
P_W0, P_A0 = 80, 88
P_KK, P_KA, P_RK, P_LNG, P_LNB = 96, 100, 104, 108, 112
P_CW, P_CB = 116, 132
P_BA, P_BX, P_LAM, P_H0 = 136, 144, 152, 160
NPRM = 168
C_ID, C_OB, C_MSI, C_ML, C_IDS, C_RMF, C_RMB = 0, 128, 256, 384, 448, 512, 768
C_MSI1, C_ML1 = 1024, 1152
NCST = 1216


class Buf:
    __slots__ = ("name", "lw", "rd", "excl", "multi", "ws")

    def __init__(self, name=""):
        self.name = name
        self.lw = None
        self.rd = {}
        self.excl = False
        self.multi = False
        self.ws = {}


class TL:
    def __init__(self, t, name=""):
        self.t = t
        self.b = Buf(name)

    def __getitem__(self, k):
        return self.t[k]


class Sched:
    ENGS = ("pe", "act", "dve", "pool", "sp")

    def __init__(self, nc):
        self.nc = nc
        self.streams = {e: [] for e in self.ENGS}
        self.cnt = {}
        self.waited = {e: {} for e in self.ENGS}
        self.n_ops = 0
        self.dma_n = {e: 0 for e in self.ENGS}
        self.NSLOT = {"sp": 44, "act": 44, "pool": 4, "dve": 2, "pe": 2}

    def _deps(self, eng, reads, writes):
        need = {}
        for b in reads:
            if b.multi:
                for s, v in b.ws.items():
                    if need.get(s, 0) < v:
                        need[s] = v
                continue
            if b.lw is not None:
                s, v = b.lw
                if need.get(s, 0) < v:
                    need[s] = v
            if b.excl:
                for s, v in b.rd.items():
                    if s != eng and need.get(s, 0) < v:
                        need[s] = v
        for b in writes:
            if b.multi:
                continue
            if b.lw is not None:
                s, v = b.lw
                if need.get(s, 0) < v:
                    need[s] = v
            for s, v in b.rd.items():
                if need.get(s, 0) < v:
                    need[s] = v
        out = []
        w = self.waited[eng]
        for s, v in need.items():
            if s == "pe" and eng == "pe":
                continue
            if w.get(s, 0) >= v:
                continue
            w[s] = v
            out.append((s, v))
        return out

    def op(self, eng, fn, reads=(), writes=(), dma=False):
        reads = [r.b if isinstance(r, TL) else r for r in reads]
        writes = [r.b if isinstance(r, TL) else r for r in writes]
        waits = self._deps(eng, reads, writes)
        if dma:
            slot = self.dma_n[eng] % self.NSLOT[eng]
            self.dma_n[eng] += 1
            sem = "%s_d%d" % (eng, slot)
            prev = self.cnt.get(sem, 0)
            if prev > 0 and self.waited[eng].get(sem, 0) < prev:
                self.waited[eng][sem] = prev
                waits.append((sem, prev))
        else:
            sem = eng
        inc = 16 if dma else 1
        self.cnt[sem] = self.cnt.get(sem, 0) + inc
        val = self.cnt[sem]
        self.streams[eng].append((waits, fn, sem, inc))
        self.n_ops += 1
        for b in reads:
            if b.rd.get(sem, 0) < val:
                b.rd[sem] = val
        for b in writes:
            if b.multi:
                if b.ws.get(sem, 0) < val:
                    b.ws[sem] = val
                continue
            b.lw = (sem, val)
            b.rd = {}
        return val

    def barrier_on(self, tl):
        if tl.b.lw is None:
            return
        sname, v = tl.b.lw
        for e in ("sp", "act", "pool"):
            if self.waited[e].get(sname, 0) < v:
                self.waited[e][sname] = v
                self.streams[e].append(([(sname, v)], None, None, 0))

    def barrier(self):
        snap = dict(self.cnt)
        for e in self.ENGS:
            waits = []
            for s, v in snap.items():
                if s == "pe" and e == "pe":
                    continue
                if self.waited[e].get(s, 0) < v:
                    self.waited[e][s] = v
                    waits.append((s, v))
            if waits:
                self.streams[e].append((waits, None, None, 0))

    def emit(self):
        nc = self.nc
        sems = {}
        with contextlib.ExitStack() as st:
            for s in self.cnt:
                sems[s] = st.enter_context(nc.semaphore(s))
            block = st.enter_context(nc.Block())
            engmap = {"pe": block.tensor, "act": block.scalar, "dve": block.vector,
                      "pool": block.gpsimd, "sp": block.sync}
            for e in self.ENGS:
                stream = self.streams[e]
                if not stream:
                    continue

                def body(eng, stream=stream):
                    for waits, fn, sem, inc in stream:
                        for s, v in waits:
                            eng.wait_ge(sems[s], v)
                        if fn is not None:
                            fn(eng).then_inc(sems[sem], inc)
                engmap[e](body)


class K:
    def __init__(self, debug=False, stop_after=None):
        self.debug = debug
        self.stop_after = stop_after
        import os
        self.cutk = int(os.environ.get("KCUT", "0"))
        self.cutm = int(os.environ.get("KCUTM", "99"))
        self.ktiles = int(os.environ.get("KTILES", "99"))
        self.kskip = os.environ.get("KSKIP", "").split(",")
        self.nc = bass.Bass("TRN2", target_bir_lowering=False)
        self.S = Sched(self.nc)
        self.es = contextlib.ExitStack()
        self.psr = 0
        self.rr = {}

    def dram(self, name, shape, dt, kind="Internal"):
        t = TL(self.nc.dram_tensor(name, list(shape), dt, kind=kind).ap(), name)
        t.b.multi = True
        return t

    def sb(self, st, name, shape, dt):
        return TL(st.enter_context(self.nc.sbuf_tensor(name, list(shape), dt)), name)

    def sb2(self, st, name, shape, dt):
        t = st.enter_context(self.nc.sbuf_tensor(name, list(shape), dt))
        return [TL(t, name + "_lo"), TL(t, name + "_hi")]

    def ps(self):
        p = self.psum[self.psr % 8]
        self.psr += 1
        return p

    def mm(self, out, lhsT, rhs, start, stop, R, W):
        self.S.op("pe", lambda e: e.matmul(out, lhsT=lhsT, rhs=rhs, start=start, stop=stop), R, W)

    def tr(self, out, in_, ident, R, W):
        self.S.op("pe", lambda e: e.transpose(out, in_, ident), R, W)

    def act(self, out, in_, func, R, W, scale=1.0, bias=None, eng="act"):
        if bias is None:
            self.S.op("act", lambda e: e.activation(out=out, in_=in_, func=func, scale=scale), R, W)
        else:
            self.S.op("act", lambda e: e.activation(out=out, in_=in_, func=func, scale=scale, bias=bias), R, W)

    def tt(self, eng, out, in0, in1, op, R, W):
        self.S.op(eng, lambda e: e.tensor_tensor(out=out, in0=in0, in1=in1, op=op), R, W)

    def tsc(self, eng, out, in0, s1, op0, R, W, s2=None, op1=None):
        if op1 is None:
            self.S.op(eng, lambda e: e.tensor_scalar(out=out, in0=in0, scalar1=s1, scalar2=None, op0=op0), R, W)
        else:
            self.S.op(eng, lambda e: e.tensor_scalar(out=out, in0=in0, scalar1=s1, scalar2=s2, op0=op0, op1=op1), R, W)

    def stt(self, out, in0, scalar, in1, op0, op1, R, W):
        self.S.op("dve", lambda e: e.scalar_tensor_tensor(out=out, in0=in0, scalar=scalar, in1=in1, op0=op0, op1=op1), R, W)

    def cp(self, eng, out, in_, R, W):
        if eng == "act":
            self.S.op("act", lambda e: e.activation(out=out, in_=in_, func=AF.Copy), R, W)
        else:
            self.S.op(eng, lambda e: e.tensor_copy(out=out, in_=in_), R, W)

    def scan(self, out, d0, d1, init, R, W):
        self.S.op("dve", lambda e: e.tensor_tensor_scan(out=out, data0=d0, data1=d1, initial=init,
                                                        op0=ALU.mult, op1=ALU.add), R, W)

    def dma(self, out, in_, R, W, eng="sp"):
        self.S.op(eng, lambda e: e.dma_start(out=out, in_=in_), R, W, dma=True)

    def memset(self, eng, ap, val, W):
        self.S.op(eng, lambda e: e.memset(ap, val), (), W)

    def pick(self, key, engs):
        i = self.rr.get(key, 0)
        self.rr[key] = i + 1
        return engs[i % len(engs)]

    def build(self):
        nc = self.nc
        I = lambda n, s, dt=F32: self.dram(n, s, dt, "ExternalInput")
        O = lambda n, s, dt=F32: self.dram(n, s, dt, "ExternalOutput")
        self.xs = I("xs", [TS, D])
        self.xp = I("xp", [2 * TP, D])
        self.pe = I("pe", [TS, D])
        self.cT = I("cT", [128, 16])
        self.h0r = I("h0r", [128, 512])
        self.prm = I("prm", [128, NPRM])
        self.cst = I("cst", [128, NCST])
        self.w_mod = I("w_mod", [D, 6 * D])
        self.w_in = I("w_in", [D, DIN])
        self.w_out = I("w_out", [D, D])
        self.w1 = I("w1", [D, DFF])
        self.w2 = I("w2", [DFF, D])
        self.wup = I("wup", [128, 512])
        self.aup = I("aup", [128, 512])
        self.gup = I("gup", [128, 512])
        self.lwa = I("lwa", [2, 8, 64, 64])
        self.lwx = I("lwx", [2, 8, 64, 64])
        self.ys = O("ys", [TS, D])
        self.yp = O("yp", [2 * TP, D])
        self.str_o = O("str_o", [2, 128, 512])
        self.stl_o = O("stl_o", [128, 16])
        self.xT_scr = self.dram("xT_scr", [128, 8, NTOK], F32)
        self.xb_scr = self.dram("xb_scr", [128, 4, NTOK], F32)
        self.gate_scr = self.dram("gate_scr", [128, 4, NTOK], BF16)
        self.g_scr = self.dram("g_scr", [128, 4, NTOK], BF16)
        self.bon_scr = self.dram("bon_scr", [128, 4, NTOK], BF16)
        self.y_scr = self.dram("y_scr", [128, 8, NTOK], BF16)
        self.ytok_scr = self.dram("ytok_scr", [2, NTOK, 512], F32)
        for n in ("art", "rrt", "ttt", "akt", "mrbt", "mrkt", "bh", "kh"):
            setattr(self, n + "_scr", self.dram(n + "_scr", [NCH, 128, 512], BF16))
        self.vt_scr = self.dram("vt_scr", [NCH, 64, 512], BF16)
        self.pend_scr = self.dram("pend_scr", [NCH, 128, 512], F32)
        self.w1_scr = self.dram("w1_scr", [8, 128, 8, 512], BF16)
        self.w2_scr = self.dram("w2_scr", [8, 128, 32, 128], BF16)
        if self.debug:
            self.dbg = {}

        with self.es as st0:
            self.psum = [TL(st0.enter_context(nc.psum_tensor("ps%d" % i, [128, 512], F32)), "ps%d" % i)
                         for i in range(8)]
            for p_ in self.psum:
                p_.b.excl = True
            self.prm_t = self.sb(st0, "prm_t", [128, NPRM], F32)
            self.cst_t = self.sb(st0, "cst_t", [128, NCST], F32)
            self.modT = self.sb(st0, "modT", [128, 48, 2], F32)
            self.gs1 = self.sb(st0, "gs1", [128, 8, 2], F32)
            self.gs2 = self.sb(st0, "gs2", [128, 8, 2], F32)
            self.gg1 = self.sb(st0, "gg1", [128, 8, 2], F32)
            self.gg2 = self.sb(st0, "gg2", [128, 8, 2], F32)
            self.ident = self.sb(st0, "ident", [128, 128], F32)
            self.ones_bf = self.sb(st0, "ones_bf", [128, 128], BF16)
            self.oblk_bf = self.sb(st0, "oblk_bf", [128, 128], BF16)
            self.epsT = self.sb(st0, "epsT", [128, 2], F32)
            self.misc = self.sb(st0, "misc", [128, 32], F32)
            self.stl_t = self.sb(st0, "stl_t", [128, 16], F32)
            for nm, fn in (("p0", self.phase0), ("pA", self.phaseA), ("pB", self.phaseB), ("pC", self.phaseC)):
                fn()
                self.S.barrier()
                if self.stop_after == nm:
                    break
            self.S.emit()
        return nc

    def dump(self, name, src_ap, shape, dt, R):
        o = self.dram("dbg_" + name, shape, dt, "ExternalOutput")
        self.dma(o[:], src_ap, R, [o])

    def phase0(self):
        nc = self.nc
        prm, cst = self.prm_t, self.cst_t
        self.dma(prm[:], self.prm[:], [], [prm])
        self.dma(cst[:], self.cst[:], [], [cst])
        self.cp("dve", self.ident[:], cst[:, C_ID:C_ID + 128], [cst], [self.ident])
        self.cp("dve", self.oblk_bf[:], cst[:, C_OB:C_OB + 128], [cst], [self.oblk_bf])
        self.memset("dve", self.ones_bf[:], 1.0, [self.ones_bf])
        self.memset("dve", self.epsT[:, 0:1], EPS, [self.epsT])
        self.memset("dve", self.epsT[:, 1:2], LNX_EPS, [self.epsT])
        self.tsc("dve", self.misc[:, 0:4], prm[:, P_KA:P_KA + 4], -1.0, ALU.mult, [prm], [self.misc], 1.0, ALU.add)
        with contextlib.ExitStack() as st:
            scT = self.sb(st, "scT", [128, 16], F32)
            cT = self.sb(st, "cT_t", [128, 16], F32)
            wm = [self.sb(st, "wm%d" % i, [128, 8, 512], F32) for i in range(2)]
            tmp = self.sb(st, "lam_tmp", [128, 8], F32)
            self.dma(cT[:], self.cT[:], [], [cT])
            self.act(scT[:], cT[:], AF.Silu, [cT], [scT])
            self.act(tmp[:], prm[:, P_LAM:P_LAM + 8], AF.Exp, [prm], [tmp], scale=-1.0)
            self.act(tmp[:], tmp[:], AF.Ln, [tmp], [tmp], bias=1.0)
            self.tsc("dve", self.misc[:, 4:12], tmp[:], -8.0, ALU.mult, [tmp], [self.misc])
            self.tsc("dve", self.misc[:, 12:20], tmp[:], -16.0, ALU.mult, [tmp], [self.misc])
            wsrc = self.w_mod[:].rearrange("(kc p) n -> p kc n", p=128)
            sc3 = scT[:].rearrange("p (k c) -> p k c", c=2)
            for blk in range(12):
                w = wm[blk % 2]
                self.dma(w[:], wsrc[:, :, blk * 512:(blk + 1) * 512], [], [w], eng="sp" if blk % 2 == 0 else "act")
                p = self.ps()
                for m in range(4):
                    for kc in range(8):
                        self.mm(p[:, 2 * m:2 * m + 2], w[:, kc, m * 128:(m + 1) * 128], sc3[:, kc, :],
                                kc == 0, kc == 7, [w, scT], [p])
                for m in range(4):
                    mi = blk * 4 + m
                    self.tsc("dve", self.modT[:, mi, :], p[:, 2 * m:2 * m + 2], prm[:, P_BMOD + mi:P_BMOD + mi + 1],
                             ALU.add, [p, prm], [self.modT])
            m3 = self.modT
            for (dst, sc_off, g_off, one) in ((self.gs1, 8, P_GPRE, 1.0), (self.gs2, 32, P_GPRE2, 1.0),
                                              (self.gg1, 16, P_GPOST, 0.0), (self.gg2, 40, P_GPOST2, 0.0)):
                for c in range(2):
                    self.tsc("dve", dst[:, :, c], m3[:, sc_off:sc_off + 8, c], one, ALU.add, [m3], [dst])
                    self.tt("dve", dst[:, :, c], dst[:, :, c], prm[:, g_off:g_off + 8], ALU.mult, [dst, prm], [dst])
            if self.debug:
                self.dump("modT", self.modT[:], [128, 48, 2], F32, [self.modT])
                self.dump("gs1", self.gs1[:], [128, 8, 2], F32, [self.gs1])

    def load_cast(self, st, dst_ap, dst_tl, src_ap, shape, tag):
        key = "stg_" + tag
        if not hasattr(self, key):
            setattr(self, key, [self.sb(st, "%s%d" % (key, i), shape, F32) for i in range(2)])
        ring = getattr(self, key)
        s = ring[self.rr.get(key, 0) % 2]
        self.rr[key] = self.rr.get(key, 0) + 1
        self.dma(s[:], src_ap, [], [s], eng="sp")
        eng = self.pick("castE", ["act", "pool"])
        self.cp(eng, dst_ap, s[:], [s], [dst_tl])

    def phaseA(self):
        nc = self.nc
        prm, cst = self.prm_t, self.cst_t
        with contextlib.ExitStack() as st:
            win = self.sb(st, "win", [128, 8, DIN], BF16)
            win.b.multi = True
            wsrc = self.w_in[:].rearrange("(kc p) n -> p kc n", p=128)
            wup = self.sb(st, "wup_t", [128, 512], BF16)
            aup = self.sb(st, "aup_t", [128, 512], BF16)
            gup = self.sb(st, "gup_t", [128, 512], BF16)
            with contextlib.ExitStack() as st2:
                stgA = [self.sb(st2, "stgA%d" % i, [128, 8, 256], F32) for i in range(2)]
                nb = 0
                for c0 in range(0, DIN, 256):
                    cw = min(256, DIN - c0)
                    s_ = stgA[nb % 2]
                    self.dma(s_[:, :, 0:cw], wsrc[:, :, c0:c0 + cw], [], [s_], eng="sp" if nb % 2 == 0 else "act")
                    self.cp("act" if nb % 2 == 0 else "pool", win[:, :, c0:c0 + cw], s_[:, :, 0:cw], [s_], [win])
                    nb += 1
                for i, (src, dstt) in enumerate(((self.wup, wup), (self.aup, aup), (self.gup, gup))):
                    s_ = stgA[nb % 2]
                    nb += 1
                    s2 = s_[:].rearrange("p a b -> p (a b)")[:, 0:512]
                    self.dma(s2, src[:], [], [s_])
                    self.cp("dve", dstt[:], s2, [s_], [dstt])
            self.S.barrier()
            mSI = cst[:, C_MSI:C_MSI + 128].rearrange("p (q t) -> p q t", q=2)
            mL = cst[:, C_ML:C_ML + 64]
            idS = cst[:, C_IDS:C_IDS + 64]

            xin = self.sb(st, "xin", [128, 2, D], F32)
            xT = self.sb(st, "xT", [128, 8, TT], F32)
            sq = self.sb(st, "sq", [128, 8, TT], BF16)
            hT = self.sb(st, "hT", [128, 8, TT], BF16)
            rstd = self.sb(st, "rstd", [128, TT], F32)
            tmpA = [self.sb(st, "tmpA%d" % i, [128, TT], F32) for i in range(2)]
            rT = self.sb(st, "rT", [128, 4, TT], F32)
            kT = self.sb(st, "kT", [128, 4, TT], F32)
            vT = self.sb(st, "vT", [128, 4, TT], F32)
            xw = self.sb(st, "xw", [128, TT], BF16)
            xa = self.sb(st, "xa", [128, TT], BF16)
            xg = self.sb(st, "xg", [128, TT], BF16)
            xbT = self.sb(st, "xbT", [128, 4, TT], F32)
            gtmp = [self.sb(st, "gtmp%d" % i, [128, TT], F32) for i in range(3)]
            gate = self.sb(st, "gate", [128, 4, TT], BF16)
            gT = self.sb(st, "gT", [128, 4, TT], BF16)
            kkn = self.sb(st, "kkn", [128, 4, TT], F32)
            ksum = self.sb(st, "ksum", [128, 4, TT], F32)
            bon = self.sb(st, "bon", [128, 4, TT], BF16)
            sg = self.sb(st, "sg", [128, 4, TT], F32)
            cs = self.sb(st, "cs", [128, 4, TT], F32)
            E1 = self.sb(st, "E1", [128, 4, TT], F32)
            ad = self.sb(st, "ad", [128, 4, TT], F32)
            wk1 = self.sb(st, "wk1", [128, 4, TT], F32)
            wk2 = self.sb(st, "wk2", [128, 4, TT], F32)
            NC4 = TT // 64
            AR = [self.sb(st, "AR%d" % d, [128, 4, NC4, 2, 64], BF16) for d in range(2)]
            BK = [self.sb(st, "BK%d" % d, [128, 4, NC4, 2, 64], BF16) for d in range(2)]
            PEb1 = self.sb(st, "PEb", [128, 4, NC4, 64], F32)
            PEb = [PEb1, PEb1]
            tokB1 = self.sb(st, "tokB", [128, 2, 8, 64], BF16)
            tokK1 = self.sb(st, "tokK", [128, 2, 8, 64], BF16)
            tokB, tokK = [tokB1, tokB1], [tokK1, tokK1]
            tokV = self.sb(st, "tokV", [128, 2, 8, 64], BF16)
            NSET = 2
            MRBs = [self.sb(st, "MRBs%d" % i, [64, 8, 64], BF16) for i in range(NSET)]
            Lt0s = [self.sb(st, "Lt0_%d" % i, [64, 8, 64], F32R) for i in range(NSET)]
            Tfins = [self.sb(st, "Tfin%d" % i, [64, 8, 64], BF16) for i in range(NSET)]
            SCk = self.sb(st, "SCk", [128, 2, 8, 64], BF16)
            Lms = [[self.sb(st, "Lm%d_%d" % (i, k), [64, 8, 64], F32R) for i in range(2)] for k in range(NSET)]
            Ltms = [[self.sb(st, "Ltm%d_%d" % (i, k), [64, 8, 64], F32R) for i in range(2)] for k in range(NSET)]
            ILms = [self.sb(st, "ILm_%d" % k, [64, 8, 64], F32R) for k in range(NSET)]
            Ttms = [[self.sb(st, "Ttm%d_%d" % (i, k), [64, 8, 64], F32R) for i in range(2)] for k in range(NSET)]

            ones_bf, oblk, ident = self.ones_bf, self.oblk_bf, self.ident
            tiles = [(0, t0, True, 0) for t0 in range(0, TS, TT)] + [(1, TS, False, 1), (2, TS + TP, False, 1)]
            mS1 = cst[0:64, C_MSI1:C_MSI1 + 128].rearrange("p (q t) -> p q t", q=2)
            mL1 = cst[0:64, C_ML1:C_ML1 + 64]
            id64 = idS[0:64]
            def front(seq, g0, is_s, mc):
                if is_s:
                    src = self.xs[g0:g0 + TT, :].rearrange("(s p) f -> p s f", p=128)
                else:
                    l0 = g0 - TS
                    src = self.xp[l0:l0 + TT, :].rearrange("(s p) f -> p s f", p=128)
                self.dma(xin[:], src, [], [xin])
                if is_s:
                    petv = xT[:].rearrange("p a b -> p (a b)").rearrange("p (s f) -> p s f", s=2)
                    self.dma(petv, self.pe[g0:g0 + TT, :].rearrange("(s p) f -> p s f", p=128), [], [xT], eng="act")
                    self.tt("pool", xin[:], xin[:], petv, ALU.add, [xin, xT], [xin])
                for j in range(8):
                    p = self.ps()
                    for s in range(2):
                        self.tr(p[:, s * 128:(s + 1) * 128], xin[:, s, j * 128:(j + 1) * 128], ident[:], [xin, ident], [p])
                    self.cp("act", xT[:, j, :], p[:, 0:TT], [p], [xT])
                    self.tt("pool", sq[:, j, :], xT[:, j, :], xT[:, j, :], ALU.mult, [xT], [sq])
                    yield
                self.dma(self.xT_scr[:, :, g0:g0 + TT], xT[:], [xT], [self.xT_scr])
                p = self.ps()
                for j in range(8):
                    self.mm(p[:, 0:TT], ones_bf[:], sq[:, j, :], j == 0, j == 7, [ones_bf, sq], [p])
                self.act(rstd[:], p[:, 0:TT], AF.Sqrt, [p, self.epsT], [rstd], scale=1.0 / D, bias=self.epsT[:, 0:1])
                self.S.op("dve", lambda e: e.reciprocal(out=rstd[:], in_=rstd[:]), [rstd.b], [rstd.b])
                for j in range(8):
                    t = tmpA[j % 2]
                    self.tt("dve", t[:], xT[:, j, :], rstd[:], ALU.mult, [xT, rstd], [t])
                    self.act(hT[:, j, :], t[:], AF.Identity, [t, self.gs1, self.modT], [hT],
                             scale=self.gs1[:, j, mc:mc + 1], bias=self.modT[:, j, mc:mc + 1])
                    yield
                for m in range(23):
                    if m >= self.cutm:
                        break
                    p = self.ps()
                    for kc in range(8):
                        self.mm(p[:, 0:TT], win[:, kc, m * 128:(m + 1) * 128], hT[:, kc, :], kc == 0, kc == 7, [win, hT], [p])
                    pz = p[:, 0:TT]
                    if m < 4:
                        self.cp("act", rT[:, m, :], pz, [p], [rT])
                    elif m < 8:
                        self.cp("act", kT[:, m - 4, :], pz, [p], [kT])
                    elif m < 12:
                        self.cp("act", vT[:, m - 8, :], pz, [p], [vT])
                    elif m == 12:
                        self.act(xw[:], pz, AF.Tanh, [p], [xw])
                    elif m == 13:
                        self.cp("act", xa[:], pz, [p], [xa])
                    elif m == 14:
                        self.act(xg[:], pz, AF.Sigmoid, [p], [xg])
                    elif m < 19:
                        self.cp("act", xbT[:, m - 15, :], pz, [p], [xbT])
                    else:
                        j = m - 19
                        g0_, g1_, g2_ = gtmp
                        self.cp("act", g0_[:], pz, [p], [g0_])
                        self.tt("pool", g1_[:], g0_[:], g0_[:], ALU.mult, [g0_], [g1_])
                        self.tsc("dve", g1_[:], g1_[:], 0.044715, ALU.mult, [g1_], [g1_], 1.0, ALU.add)
                        self.tt("dve", g1_[:], g1_[:], g0_[:], ALU.mult, [g1_, g0_], [g1_])
                        self.act(g2_[:], g1_[:], AF.Sigmoid, [g1_], [g2_], scale=GELU_C)
                        self.tt("pool", gate[:, j, :], g0_[:], g2_[:], ALU.mult, [g0_, g2_], [gate])
                    yield
                self.dma(self.xb_scr[:, :, g0:g0 + TT], xbT[:], [xbT], [self.xb_scr])
                if self.debug and g0 == 0:
                    self.dump("hT", hT[:], [128, 8, TT], BF16, [hT])
                    self.dump("rT", rT[:], [128, 4, TT], F32, [rT])
                    self.dump("vT", vT[:], [128, 4, TT], F32, [vT])
                    self.dump("xbT", xbT[:], [128, 4, TT], F32, [xbT])
                    self.dump("gate", gate[:], [128, 4, TT], BF16, [gate])
                self.dma(self.gate_scr[:, :, g0:g0 + TT], gate[:], [gate], [self.gate_scr])
                for j in range(4):
                    p = self.ps()
                    self.mm(p[:, 0:TT], gup[:, j * 128:(j + 1) * 128], xg[:], True, True, [gup, xg], [p])
                    self.cp("act", gT[:, j, :], p[:, 0:TT], [p], [gT])
                    yield
                self.dma(self.g_scr[:, :, g0:g0 + TT], gT[:], [gT], [self.g_scr])
                for j in range(4):
                    self.tsc("dve", kkn[:, j, :], kT[:, j, :], prm[:, P_KK + j:P_KK + j + 1], ALU.mult, [kT, prm], [kkn])
                    self.tt("pool", sq[:, j, :], kkn[:, j, :], kkn[:, j, :], ALU.mult, [kkn], [sq])
                for j in range(4):
                    p = self.ps()
                    self.mm(p[:, 0:TT], oblk[:], sq[:, j, :], True, True, [oblk, sq], [p])
                    t = tmpA[j % 2]
                    self.act(t[:], p[:, 0:TT], AF.Sqrt, [p], [t])
                    self.tsc("dve", t[:], t[:], 1e-12, ALU.max, [t], [t])
                    self.S.op("dve", lambda e, t=t: e.reciprocal(out=t[:], in_=t[:]), [t.b], [t.b])
                    self.tt("dve", kkn[:, j, :], kkn[:, j, :], t[:], ALU.mult, [kkn, t], [kkn])
                    yield
                for s in range(2):
                    p = self.ps()
                    for j in range(4):
                        self.tr(p[:, j * 128:(j + 1) * 128], vT[:, j, s * 128:(s + 1) * 128], ident[:], [vT, ident], [p])
                    self.cp("act", tokV[:, s, :, :].rearrange("p h k -> p (h k)"), p[:], [p], [tokV])
                c0 = g0 // 64
                for s in range(2):
                    dst = self.vt_scr[c0 + 2 * s:c0 + 2 * s + 2, :, :].rearrange("c s f -> (c s) f")
                    self.dma(dst, tokV[:, s, :, :].rearrange("p h k -> p (h k)"), [tokV], [self.vt_scr])
                yield
            def prep(g0, d):
                c0 = g0 // 64
                for j in range(4):
                    p = self.ps()
                    self.mm(p[:, 0:TT], wup[d * 64:(d + 1) * 64, j * 128:(j + 1) * 128], xw[d * 64:(d + 1) * 64, :],
                            True, True, [wup, xw], [p])
                    self.act(sg[:, j, :], p[:, 0:TT], AF.Sigmoid, [p, prm], [sg],
                             bias=prm[:, P_W0 + 4 * d + j:P_W0 + 4 * d + j + 1])
                    if d == 0:
                        self.scan(cs[:, j, :], cst[:, C_RMF:C_RMF + TT], sg[:, j, :], 0.0, [cst, sg], [cs])
                    else:
                        self.scan(cs[:, j, ::-1], cst[:, C_RMB:C_RMB + TT][:, ::-1], sg[:, j, ::-1], 0.0, [cst, sg], [cs])
                    yield
                for j in range(4):
                    p = self.ps()
                    self.mm(p[:, 0:TT], aup[d * 64:(d + 1) * 64, j * 128:(j + 1) * 128], xa[d * 64:(d + 1) * 64, :],
                            True, True, [aup, xa], [p])
                    self.act(ad[:, j, :], p[:, 0:TT], AF.Sigmoid, [p, prm], [ad],
                             bias=prm[:, P_A0 + 4 * d + j:P_A0 + 4 * d + j + 1])
                    yield
                self.tt("dve", sg[:], cs[:], sg[:], ALU.subtract, [cs, sg], [sg])
                self.act(E1[:], cs[:], AF.Exp, [cs], [E1], scale=-LAM)
                self.act(cs[:], cs[:], AF.Exp, [cs], [cs], scale=LAM)
                self.act(sg[:], sg[:], AF.Exp, [sg], [sg], scale=-LAM)
                E2, E3 = cs, sg
                ar5 = AR[d]
                bk5 = BK[d]
                v4 = lambda tl: tl[:].rearrange("p j (c t) -> p j c t", t=64)
                self.stt(ar5[:, :, :, 0, :], v4(kkn), -1.0, v4(E3), ALU.mult, ALU.mult, [kkn, E3], [ar5])
                self.tt("pool", ar5[:, :, :, 1, :], v4(rT), v4(E1), ALU.mult, [rT, E1], [ar5])
                yield
                self.tt("dve", wk1[:], kkn[:], ad[:], ALU.mult, [kkn, ad], [wk1])
                self.tt("dve", wk1[:], wk1[:], E2[:], ALU.mult, [wk1, E2], [wk1])
                self.cp("act", bk5[:, :, :, 0, :], v4(wk1), [wk1], [bk5])
                yield
                for j in range(4):
                    self.tsc("dve", wk2[:, j, :], ad[:, j, :], prm[:, P_KA + j:P_KA + j + 1], ALU.mult, [ad, prm, self.misc], [wk2],
                             self.misc[:, j:j + 1], ALU.add)
                self.tt("dve", wk2[:], wk2[:], kT[:], ALU.mult, [wk2, kT], [wk2])
                if d == 0:
                    self.cp("pool", ksum[:], wk2[:], [wk2], [ksum])
                else:
                    self.tt("pool", ksum[:], ksum[:], wk2[:], ALU.add, [ksum, wk2], [ksum])
                self.tt("dve", wk2[:], wk2[:], E2[:], ALU.mult, [wk2, E2], [wk2])
                self.cp("act", bk5[:, :, :, 1, :], v4(wk2), [wk2], [bk5])
                yield
                te = 63 if d == 0 else 0
                pend_b = v4(E1)[:, :, :, te:te + 1].to_broadcast([128, 4, NC4, 64])
                self.cp("act", PEb[d][:], pend_b, [E1], [PEb[d]])
                self.tt("dve", v4(wk1), v4(wk1), PEb[d][:], ALU.mult, [wk1, PEb[d]], [wk1])
                self.tt("dve", v4(wk2), v4(wk2), PEb[d][:], ALU.mult, [wk2, PEb[d]], [wk2])
                yield
                for (srcw, tokX, scr) in ((wk1, tokB[d], self.bh_scr), (wk2, tokK[d], self.kh_scr)):
                    for s in range(2):
                        p = self.ps()
                        for j in range(4):
                            self.tr(p[:, j * 128:(j + 1) * 128], srcw[:, j, s * 128:(s + 1) * 128], ident[:], [srcw, ident], [p])
                        self.cp("act", tokX[:, s, :, :].rearrange("p h k -> p (h k)"), p[:], [p], [tokX])
                        for cc in range(2):
                            self.dma(scr[c0 + 2 * s + cc, d * 64:(d + 1) * 64, :],
                                     tokX[cc * 64:(cc + 1) * 64, s, :, :].rearrange("p h k -> p (h k)"),
                                     [tokX], [scr])
                        yield
                for cl in range(NC4):
                    c = c0 + cl
                    for hp in range(2):
                        for (q, scr) in ((0, self.art_scr), (1, self.rrt_scr)):
                            dst = scr[c, d * 64:(d + 1) * 64, :].rearrange("k (j hp t) -> k j hp t", hp=2, t=64)[:, :, hp, :]
                            self.dma(dst, ar5[hp * 64:(hp + 1) * 64, :, cl, q, :], [ar5], [scr], eng="sp")
                        dst = self.pend_scr[c, d * 64:(d + 1) * 64, :].rearrange("k (j hp t) -> k j hp t", hp=2, t=64)[:, :, hp, :]
                        self.dma(dst, PEb[d][hp * 64:(hp + 1) * 64, :, cl, :], [PEb[d]], [self.pend_scr], eng="sp")
                yield
            def bonus(g0):
                for j in range(4):
                    self.stt(sq[:, j, :], rT[:, j, :], prm[:, P_RK + j:P_RK + j + 1], ksum[:, j, :], ALU.mult, ALU.mult,
                             [rT, prm, ksum], [sq])
                    p = self.ps()
                    self.mm(p[:, 0:TT], oblk[:], sq[:, j, :], True, True, [oblk, sq], [p])
                    self.tt("dve", bon[:, j, :], p[:, 0:TT], vT[:, j, :], ALU.mult, [p, vT], [bon])
                self.dma(self.bon_scr[:, :, g0:g0 + TT], bon[:], [bon], [self.bon_scr])
                yield
            def chunk_sck(g0):
                c0 = g0 // 64
                for cl in range(NC4):
                    c = c0 + cl
                    for hp in range(2):
                        p = self.ps()
                        for j in range(4):
                            for d in range(2):
                                self.mm(p[d * 64:(d + 1) * 64, j * 128:(j + 1) * 128],
                                        BK[d][hp * 64:(hp + 1) * 64, j, cl, 1, :],
                                        AR[d][hp * 64:(hp + 1) * 64, j, cl, :, :].rearrange("p q t -> p (q t)"),
                                        True, True, [BK[d], AR[d]], [p])
                        self.tt("dve", SCk[:, :, hp::2, :].rearrange("p q h t -> p h q t"),
                                p[:].rearrange("p (h q t) -> p h q t", q=2, t=64),
                                mSI.unsqueeze(1).to_broadcast([128, 4, 2, 64]), ALU.mult, [p, cst], [SCk])
                    self.dma(self.akt_scr[c], SCk[:, 0, :, :].rearrange("p h s -> p (h s)"), [SCk], [self.akt_scr], eng="act")
                    self.dma(self.mrkt_scr[c], SCk[:, 1, :, :].rearrange("p h s -> p (h s)"), [SCk], [self.mrkt_scr], eng="act")
                    yield
            def chunk_d(g0, d, cls, k):
                c0 = g0 // 64
                Lm, Ltm, ILm, Ttm, Lt0, Tfin = Lms[k], Ltms[k], ILms[k], Ttms[k], Lt0s[k], Tfins[k]
                MRB = MRBs[k]
                for cl in cls:
                    c = c0 + cl
                    msk = mSI[0:64] if d == 0 else mS1
                    mskL = mL[0:64] if d == 0 else mL1
                    L0 = Lm[0]
                    for hp in range(2):
                        p = self.ps()
                        for j in range(4):
                            self.mm(p[0:64, j * 128:(j + 1) * 128],
                                    BK[d][hp * 64:(hp + 1) * 64, j, cl, 0, :],
                                    AR[d][hp * 64:(hp + 1) * 64, j, cl, :, :].rearrange("p q t -> p (q t)"),
                                    True, True, [BK[d], AR[d]], [p])
                        p4 = p[0:64, :].rearrange("p (h q t) -> p h q t", q=2, t=64)
                        self.tt("dve", Lt0[:, hp::2, :], p4[:, :, 0, :], msk[:, 0, :].unsqueeze(1).to_broadcast([64, 4, 64]),
                                ALU.mult, [p, cst], [Lt0])
                        self.tt("dve", MRB[:, hp::2, :], p4[:, :, 1, :], msk[:, 1, :].unsqueeze(1).to_broadcast([64, 4, 64]),
                                ALU.mult, [p, cst], [MRB])
                        p2 = self.ps()
                        for j in range(4):
                            self.mm(p2[0:64, j * 64:(j + 1) * 64],
                                    AR[d][hp * 64:(hp + 1) * 64, j, cl, 0, :], BK[d][hp * 64:(hp + 1) * 64, j, cl, 0, :],
                                    True, True, [AR[d], BK[d]], [p2])
                        self.tt("dve", L0[:, hp::2, :], p2[0:64, 0:256].rearrange("p (h s) -> p h s", s=64),
                                mskL.unsqueeze(1).to_broadcast([64, 4, 64]), ALU.mult, [p2, cst], [L0])
                    self.dma(self.mrbt_scr[c, d * 64:(d + 1) * 64, :], MRB[:].rearrange("p h s -> p (h s)"),
                             [MRB], [self.mrbt_scr], eng="act")
                    yield
                    T0 = Ttm[0]
                    self.tt("pool", T0[:], Lt0[:].bitcast(F32), id64.unsqueeze(1).to_broadcast([64, 8, 64]), ALU.add,
                            [Lt0, cst], [T0])
                    L_prev, Tt_prev, Lt_prev = L0, T0, Lt0
                    for lev in range(1, 6):
                        L_new, Lt_new, Tt_new = Lm[lev % 2], Ltm[lev % 2], Ttm[lev % 2]
                        pA = self.ps()
                        for h in range(8):
                            self.mm(pA[0:64, h * 64:(h + 1) * 64], Lt_prev[:, h, :], L_prev[:, h, :], True, True,
                                    [Lt_prev, L_prev], [pA])
                        if lev < 5:
                            pB = self.ps()
                            for h in range(8):
                                self.mm(pB[0:64, h * 64:(h + 1) * 64], L_prev[:, h, :], Lt_prev[:, h, :], True, True,
                                        [Lt_prev, L_prev], [pB])
                        self.tt("dve", ILm[:], pA[0:64, :].rearrange("p (h s) -> p h s", s=64),
                                id64.unsqueeze(1).to_broadcast([64, 8, 64]), ALU.add, [pA, cst], [ILm])
                        if lev < 5:
                            self.cp("dve", L_new[:].rearrange("p h s -> p (h s)"), pA[0:64, :], [pA], [L_new])
                            self.cp("act", Lt_new[:].rearrange("p h s -> p (h s)"), pB[0:64, :], [pB], [Lt_new])
                        yield
                        pC = self.ps()
                        for h in range(8):
                            self.mm(pC[0:64, h * 64:(h + 1) * 64], ILm[:, h, :], Tt_prev[:, h, :], True, True,
                                    [ILm, Tt_prev], [pC])
                        if lev < 5:
                            self.cp("act", Tt_new[:].rearrange("p h s -> p (h s)"), pC[0:64, :], [pC], [Tt_new])
                        else:
                            self.cp("act", Tfin[:].rearrange("p h s -> p (h s)"), pC[0:64, :], [pC], [Tfin])
                        L_prev, Tt_prev, Lt_prev = L_new, Tt_new, Lt_new
                        yield
                    self.dma(self.ttt_scr[c, d * 64:(d + 1) * 64, :], Tfin[:].rearrange("p h s -> p (h s)"),
                             [Tfin], [self.ttt_scr], eng="act")


            def run_all(*gens):
                gens = list(gens)
                while gens:
                    for g in list(gens):
                        try:
                            next(g)
                        except StopIteration:
                            gens.remove(g)

            def seq_(*gens):
                for g in gens:
                    yield from g

            prev = None
            for (seq, g0, is_s, mc) in tiles[:self.ktiles]:
                if prev is None:
                    run_all(front(seq, g0, is_s, mc))
                else:
                    run_all(seq_(chunk_d(prev, 1, [0, 1], 0), chunk_sck(prev)), chunk_d(prev, 1, [2, 3], 1), front(seq, g0, is_s, mc))
                run_all(prep(g0, 0))
                run_all(chunk_d(g0, 0, [0, 1], 0), chunk_d(g0, 0, [2, 3], 1), prep(g0, 1))
                run_all(bonus(g0))
                prev = g0
            run_all(seq_(chunk_d(prev, 1, [0, 1], 0), chunk_sck(prev)), chunk_d(prev, 1, [2, 3], 1))

    def phaseB(self):
        with contextlib.ExitStack() as st:
            side = [self.gen_c0(st), self.phaseB_lru(st)]
            post = self.phaseB_post(st)
            next(post)
            done = np.zeros((2, NCH), bool)
            posted = [False] * (NTOK // 128)
            si = 0
            for info in self.phaseB_chain(st):
                for (d, c) in info:
                    done[d, c] = True
                for _ in range(2):
                    if side:
                        g = side[si % len(side)]
                        si += 1
                        try:
                            next(g)
                        except StopIteration:
                            side.remove(g)
                for b in range(NTOK // 128):
                    if not posted[b] and done[:, 2 * b:2 * b + 2].all():
                        posted[b] = True
                        post.send(b)
            for g in side:
                for _ in g:
                    pass
            for b in range(NTOK // 128):
                if not posted[b]:
                    post.send(b)
            if self.debug:
                self.S.barrier()
                self.dump("yscr", self.y_scr[:], [128, 8, NTOK], BF16, [self.y_scr])

    def gen_c0(self, st):
        stg = [self.sb(st, "stgC%d" % i, [128, 8, 512], F32) for i in range(2)]
        wb = [self.sb(st, "wbC%d" % i, [128, 8, 512], BF16) for i in range(2)]
        w1src = self.w1[:].rearrange("(kc p) n -> p kc n", p=128)
        for blk in range(8):
            s, o = stg[blk % 2], wb[blk % 2]
            self.dma(s[:], w1src[:, :, blk * 512:(blk + 1) * 512], [], [s])
            self.cp("pool", o[:], s[:], [s], [o])
            self.dma(self.w1_scr[blk], o[:], [o], [self.w1_scr], eng="act")
            yield
        w2src = self.w2[:].rearrange("(fc p) n -> p fc n", p=128)
        for m in range(8):
            s, o = stg[m % 2], wb[m % 2]
            s4 = s[:].rearrange("p k (a b) -> p (k a) b", b=128)
            o4 = o[:].rearrange("p k (a b) -> p (k a) b", b=128)
            self.dma(s4, w2src[:, :, m * 128:(m + 1) * 128], [], [s])
            self.cp("pool", o[:], s[:], [s], [o])
            self.dma(self.w2_scr[m], o4, [o], [self.w2_scr], eng="act")
            yield

    def phaseB_lru(self, st):
        prm, cst, misc = self.prm_t, self.cst_t, self.misc
        if True:
            wbd32 = self.sb(st, "wbd32", [128, 16, 128], F32)
            wbd = self.sb(st, "wbd", [128, 16, 128], BF16)
            self.memset("pool", wbd32[:], 0.0, [wbd32])
            self.S.barrier_on(wbd32)
            wbd32.b.multi = True
            for gi, src in enumerate((self.lwa, self.lwx)):
                for d in range(2):
                    for j in range(4):
                        for hb in range(2):
                            self.dma(wbd32[hb * 64:(hb + 1) * 64, (gi * 2 + d) * 4 + j, hb * 64:(hb + 1) * 64],
                                     src[d, 2 * j + hb], [], [wbd32])
            self.cp("dve", wbd[:], wbd32[:], [wbd32], [wbd])
            TM = TS
            xbp = self.sb(st, "xbp", [128, TM + 4], F32)
            xc = self.sb(st, "xc", [128, TM], F32)
            xcb = self.sb(st, "xcb", [128, TM], BF16)
            gt = self.sb(st, "gt_l", [128, TM], BF16)
            a_t = self.sb(st, "a_t", [128, TM], F32)
            bx_t = self.sb(st, "bx_t", [128, TM], F32)
            s_t = self.sb(st, "s_t", [128, TM], F32)
            hs = [self.sb(st, "hs%d" % d, [128, TM], F32) for d in range(2)]
            yb = self.sb(st, "yb", [128, TM], BF16)
            for (seq, g0, T) in ((0, 0, TS), (1, TS, TP), (2, TS + TP, TP)):
                for j in range(4):
                    self.memset("pool", xbp[:, 0:2], 0.0, [xbp])
                    self.memset("pool", xbp[:, T + 2:T + 4], 0.0, [xbp])
                    self.dma(xbp[:, 2:T + 2], self.xb_scr[:, j, g0:g0 + T], [self.xb_scr], [xbp])
                    self.dma(gt[:, 0:T], self.gate_scr[:, j, g0:g0 + T], [self.gate_scr], [gt], eng="act")
                    cw = lambda i: prm[:, P_CW + 4 * i + j:P_CW + 4 * i + j + 1]
                    self.act(xc[:, 0:T], xbp[:, 0:T], AF.Identity, [xbp, prm], [xc], scale=cw(0), bias=prm[:, P_CB + j:P_CB + j + 1])
                    for i in range(1, 4):
                        self.stt(xc[:, 0:T], xbp[:, i:i + T], cw(i), xc[:, 0:T], ALU.mult, ALU.add, [xbp, prm, xc], [xc])
                    self.cp("pool", xcb[:, 0:T], xc[:, 0:T], [xc], [xcb])
                    yield
                    for d in range(2):
                        for t0 in range(0, T, 512):
                            tw = min(512, T - t0)
                            p = self.ps()
                            self.mm(p[:, 0:tw], wbd[:, (0 * 2 + d) * 4 + j, :], xcb[:, t0:t0 + tw], True, True, [wbd, xcb], [p])
                            self.act(s_t[:, t0:t0 + tw], p[:, 0:tw], AF.Sigmoid, [p, prm], [s_t],
                                     bias=prm[:, P_BA + 4 * d + j:P_BA + 4 * d + j + 1])
                            p2 = self.ps()
                            self.mm(p2[:, 0:tw], wbd[:, (1 * 2 + d) * 4 + j, :], xcb[:, t0:t0 + tw], True, True, [wbd, xcb], [p2])
                            self.act(bx_t[:, t0:t0 + tw], p2[:, 0:tw], AF.Sigmoid, [p2, prm], [bx_t],
                                     bias=prm[:, P_BX + 4 * d + j:P_BX + 4 * d + j + 1])
                        col = 4 + d * 4 + j
                        self.act(a_t[:, 0:T], s_t[:, 0:T], AF.Exp, [s_t, misc], [a_t], scale=misc[:, col:col + 1])
                        self.act(s_t[:, 0:T], s_t[:, 0:T], AF.Exp, [s_t, misc], [s_t], scale=misc[:, col + 8:col + 9])
                        self.act(s_t[:, 0:T], s_t[:, 0:T], AF.Sqrt, [s_t], [s_t], scale=-1.0, bias=1.0)
                        self.tt("pool", bx_t[:, 0:T], bx_t[:, 0:T], xc[:, 0:T], ALU.mult, [bx_t, xc], [bx_t])
                        self.tt("dve", bx_t[:, 0:T], bx_t[:, 0:T], s_t[:, 0:T], ALU.mult, [bx_t, s_t], [bx_t])
                        h = hs[d]
                        if seq == 0:
                            init = prm[:, P_H0 + 4 * d + j:P_H0 + 4 * d + j + 1]
                        else:
                            init = 0.0
                        if d == 0:
                            self.scan(h[:, 0:T], a_t[:, 0:T], bx_t[:, 0:T], init, [a_t, bx_t, prm], [h])
                        else:
                            self.scan(h[:, 0:T][:, ::-1], a_t[:, 0:T][:, ::-1], bx_t[:, 0:T][:, ::-1], init, [a_t, bx_t, prm], [h])
                        if seq > 0:
                            col_o = j * 4 + (seq - 1) * 2 + d
                            te = T - 1 if d == 0 else 0
                            self.cp("pool", self.stl_t[:, col_o:col_o + 1], h[:, te:te + 1], [h], [self.stl_t])
                        yield
                    self.tt("pool", hs[0][:, 0:T], hs[0][:, 0:T], hs[1][:, 0:T], ALU.add, [hs[0], hs[1]], [hs[0]])
                    self.tt("dve", yb[:, 0:T], hs[0][:, 0:T], gt[:, 0:T], ALU.mult, [hs[0], gt], [yb])
                    self.dma(self.y_scr[:, 4 + j, g0:g0 + T], yb[:, 0:T], [yb], [self.y_scr])
            self.dma(self.stl_o[:], self.stl_t[:], [self.stl_t], [self.stl_o])

    def phaseB_chain(self, st):
        if True:
            NB = 3
            def ring(name, dt=BF16):
                return [self.sb2(st, "%s%d" % (name, i), [128, 512], dt) for i in range(NB)]
            art, rrt, ttt, akt, mrbt, mrkt, bh, kh, vt = [ring(n) for n in
                                                          ("c_art", "c_rrt", "c_ttt", "c_akt", "c_mrbt", "c_mrkt", "c_bh", "c_kh", "c_vt")]
            pend = ring("c_pend", F32)
            Hf = self.sb2(st, "Hf", [128, 512], F32)
            Hb = self.sb2(st, "Hb", [128, 512], BF16)
            Zs = self.sb2(st, "Zs", [128, 512], BF16)
            Us = self.sb2(st, "Us", [128, 512], BF16)
            Yt = [self.sb2(st, "Yt%d" % i, [128, 512], F32) for i in range(2)]
            tmpH = self.sb2(st, "tmpH", [128, 512], F32)
            hs_ = lambda h: slice(h * 64, (h + 1) * 64)
            steps = []
            for (seq, cbase, n) in ((0, 0, 32), (1, 32, 4), (2, 36, 4)):
                for i in range(n):
                    steps.append((seq, cbase, n, i))

            def loads(k):
                seq, cbase, n, i = steps[k]
                r = k % NB
                for d in range(2):
                    sl = slice(d * 64, (d + 1) * 64)
                    c = cbase + i if d == 0 else cbase + n - 1 - i
                    for (tl, scr) in ((art, self.art_scr), (rrt, self.rrt_scr), (ttt, self.ttt_scr), (akt, self.akt_scr),
                                      (mrbt, self.mrbt_scr), (mrkt, self.mrkt_scr), (bh, self.bh_scr), (kh, self.kh_scr),
                                      (pend, self.pend_scr)):
                        self.dma(tl[r][d][sl, :], scr[c, sl, :], [scr], [tl[r][d]], eng="sp")
                    self.dma(vt[r][d][sl, :], self.vt_scr[c], [self.vt_scr], [vt[r][d]], eng="sp")

            loads(0)
            for k in range(len(steps)):
                seq, cbase, n, i = steps[k]
                step = k + 1
                r = k % NB
                if k + 1 < len(steps):
                    loads(k + 1)
                if i == 0:
                    for d in range(2):
                        sl = slice(d * 64, (d + 1) * 64)
                        if seq == 0:
                            self.dma(Hf[d][sl, :], self.h0r[sl, :], [], [Hf[d]], eng="act")
                        else:
                            self.memset("dve", Hf[d][sl, :], 0.0, [Hf[d]])
                        self.cp("dve", Hb[d][sl, :], Hf[d][sl, :], [Hf[d]], [Hb[d]])
                if True:
                    ctx = []
                    for d in range(2):
                        sl = slice(d * 64, (d + 1) * 64)
                        c = cbase + i if d == 0 else cbase + n - 1 - i
                        ops = tuple(x[r][d] for x in (art, rrt, ttt, akt, mrbt, mrkt, bh, kh, vt, pend))
                        ctx.append((d, sl, c, ops))
                    pZs = {}
                    for (d, sl, c, (A_, R_, T_, AK_, MRB_, MRK_, B_, K_, V_, PE_)) in ctx:
                        H_, Z_ = Hb[d], Zs[d]
                        pZ = self.ps()
                        for h in range(8):
                            self.mm(pZ[sl, hs_(h)], A_[sl, hs_(h)], H_[sl, hs_(h)], True, False, [A_, H_], [pZ])
                            self.mm(pZ[sl, hs_(h)], AK_[sl, hs_(h)], V_[sl, hs_(h)], False, True, [AK_, V_], [pZ])
                        pZs[d] = pZ
                    pYs = {}
                    for (d, sl, c, (A_, R_, T_, AK_, MRB_, MRK_, B_, K_, V_, PE_)) in ctx:
                        H_ = Hb[d]
                        pY = self.ps()
                        pYs[d] = pY
                    for (d, sl, c, ops) in ctx:
                        self.cp("act", Zs[d][sl, :], pZs[d][sl, :], [pZs[d]], [Zs[d]])
                    pUs = {}
                    for (d, sl, c, (A_, R_, T_, AK_, MRB_, MRK_, B_, K_, V_, PE_)) in ctx:
                        Z_ = Zs[d]
                        pU = self.ps()
                        for h in range(8):
                            self.mm(pU[sl, hs_(h)], T_[sl, hs_(h)], Z_[sl, hs_(h)], True, True, [T_, Z_], [pU])
                        pUs[d] = pU
                    for (d, sl, c, ops) in ctx:
                        self.cp("act", Us[d][sl, :], pUs[d][sl, :], [pUs[d]], [Us[d]])
                    pHs = {}
                    for (d, sl, c, (A_, R_, T_, AK_, MRB_, MRK_, B_, K_, V_, PE_)) in ctx:
                        U_ = Us[d]
                        self.tt("dve", tmpH[d][sl, :], Hf[d][sl, :], PE_[sl, :], ALU.mult, [Hf[d], PE_], [tmpH[d]])
                        pH = self.ps()
                        for h in range(8):
                            self.mm(pH[sl, hs_(h)], B_[sl, hs_(h)], U_[sl, hs_(h)], True, False, [B_, U_], [pH])
                            self.mm(pH[sl, hs_(h)], K_[sl, hs_(h)], V_[sl, hs_(h)], False, True, [K_, V_], [pH])
                        pHs[d] = pH
                    for (d, sl, c, (A_, R_, T_, AK_, MRB_, MRK_, B_, K_, V_, PE_)) in ctx:
                        H_, U_ = Hb[d], Us[d]
                        pY = pYs[d]
                        for h in range(8):
                            self.mm(pY[sl, hs_(h)], R_[sl, hs_(h)], H_[sl, hs_(h)], True, False, [R_, H_], [pY])
                            self.mm(pY[sl, hs_(h)], MRB_[sl, hs_(h)], U_[sl, hs_(h)], False, False, [MRB_, U_], [pY])
                            self.mm(pY[sl, hs_(h)], MRK_[sl, hs_(h)], V_[sl, hs_(h)], False, True, [MRK_, V_], [pY])
                    for (d, sl, c, ops) in ctx:
                        self.tt("dve", Hf[d][sl, :], tmpH[d][sl, :], pHs[d][sl, :], ALU.add, [tmpH[d], pHs[d]], [Hf[d]])
                        self.cp("dve", Hb[d][sl, :], Hf[d][sl, :], [Hf[d]], [Hb[d]])
                    for (d, sl, c, ops) in ctx:
                        y = Yt[step % 2][d]
                        self.cp("act", y[sl, :], pYs[d][sl, :], [pYs[d]], [y])
                        self.dma(self.ytok_scr[d, c * 64:(c + 1) * 64, :], y[sl, :], [y], [self.ytok_scr], eng="act")
                if seq > 0 and i == n - 1:
                    self.dma(self.str_o[seq - 1], Hf[0][:], [Hf[0], Hf[1]], [self.str_o], eng="act")
                yield [(0, cbase + i), (1, cbase + n - 1 - i)]

    def phaseB_post(self, st):
        prm, cst = self.prm_t, self.cst_t
        ident = self.ident
        if True:
            yf = [self.sb(st, "yf%d" % i, [128, 512], F32) for i in range(2)]
            yb2 = [self.sb(st, "yb2%d" % i, [128, 512], F32) for i in range(2)]
            cen = self.sb(st, "cen", [128, 8, 64], F32)
            sqv = self.sb(st, "sqv", [128, 8, 64], F32)
            mean = self.sb(st, "mean", [128, 8], F32)
            var = self.sb(st, "var", [128, 8], F32)
            gl = [self.sb(st, "gl%d" % i, [128, 4, 128], BF16) for i in range(2)]
            bl = [self.sb(st, "bl%d" % i, [128, 4, 128], BF16) for i in range(2)]
            ynT = self.sb(st, "ynT", [128, 4, 128], F32)
            yo = [self.sb(st, "yo%d" % i, [128, 4, 128], BF16) for i in range(2)]
            it = -1
            blk = yield
            while True:
                it += 1
                g0 = blk * 128
                a, b = yf[it % 2], yb2[it % 2]
                g_, b_ = gl[it % 2], bl[it % 2]
                o = yo[it % 2]
                self.dma(a[:], self.ytok_scr[0, g0:g0 + 128, :], [self.ytok_scr], [a])
                self.dma(b[:], self.ytok_scr[1, g0:g0 + 128, :], [self.ytok_scr], [b], eng="act")
                self.dma(g_[:], self.g_scr[:, :, g0:g0 + 128], [self.g_scr], [g_])
                self.dma(b_[:], self.bon_scr[:, :, g0:g0 + 128], [self.bon_scr], [b_], eng="act")
                a3 = a[:].rearrange("p (h v) -> p h v", v=64)
                self.tt("pool", a[:], a[:], b[:], ALU.add, [a, b], [a])
                self.S.op("dve", lambda e, a3=a3: e.tensor_reduce(out=mean[:], in_=a3, op=ALU.add, axis=mybir.AxisListType.X),
                          [a.b], [mean.b])
                self.tsc("dve", mean[:], mean[:], 1.0 / 64, ALU.mult, [mean], [mean])
                self.tt("dve", cen[:], a3, mean[:].unsqueeze(2).to_broadcast([128, 8, 64]), ALU.subtract, [a, mean], [cen])
                self.tt("pool", sqv[:], cen[:], cen[:], ALU.mult, [cen], [sqv])
                self.S.op("dve", lambda e: e.tensor_reduce(out=var[:], in_=sqv[:], op=ALU.add, axis=mybir.AxisListType.X),
                          [sqv.b], [var.b])
                self.act(var[:], var[:], AF.Sqrt, [var, self.epsT], [var], scale=1.0 / 64, bias=self.epsT[:, 1:2])
                self.S.op("dve", lambda e: e.reciprocal(out=var[:], in_=var[:]), [var.b], [var.b])
                self.tt("dve", cen[:], cen[:], var[:].unsqueeze(2).to_broadcast([128, 8, 64]), ALU.mult, [cen, var], [cen])
                p = self.ps()
                cen2 = cen[:].rearrange("p h v -> p (h v)")
                for j in range(4):
                    self.tr(p[:, j * 128:(j + 1) * 128], cen2[:, j * 128:(j + 1) * 128], ident[:], [cen, ident], [p])
                for j in range(4):
                    self.act(ynT[:, j, :], p[:, j * 128:(j + 1) * 128], AF.Identity, [p, prm], [ynT],
                             scale=prm[:, P_LNG + j:P_LNG + j + 1], bias=prm[:, P_LNB + j:P_LNB + j + 1])
                self.tt("pool", ynT[:], ynT[:], b_[:], ALU.add, [ynT, b_], [ynT])
                self.tt("dve", o[:], ynT[:], g_[:], ALU.mult, [ynT, g_], [o])
                self.dma(self.y_scr[:, 0:4, g0:g0 + 128], o[:], [o], [self.y_scr])
                blk = yield

    def phaseC(self):
        prm, cst = self.prm_t, self.cst_t
        ident, ones_bf = self.ident, self.ones_bf
        with contextlib.ExitStack() as st:
            pass
        import os
        kcc = int(os.environ.get("KCC", "99"))
        if kcc == 0:
            return
        with contextlib.ExitStack() as st:
            wout = self.sb(st, "wout", [128, 8, D], BF16)
            wsrc = self.w_out[:].rearrange("(kc p) n -> p kc n", p=128)
            with contextlib.ExitStack() as st2:
                stg = [self.sb(st2, "stgD%d" % i, [128, 8, 256], F32) for i in range(2)]
                for cb in range(4):
                    s = stg[cb % 2]
                    self.dma(s[:], wsrc[:, :, cb * 256:(cb + 1) * 256], [], [s])
                    self.cp("act", wout[:, :, cb * 256:(cb + 1) * 256], s[:], [s], [wout])
            self.S.barrier()
            yTs = [self.sb(st, "yT_c%d" % i, [128, 8, TC], BF16) for i in range(2)]
            oT = self.sb(st, "oT", [128, 8, TC], F32)
            sq = self.sb(st, "sq_c", [128, 8, TC], BF16)
            xT = self.sb(st, "xT_c", [128, 8, TC], F32)
            h2 = self.sb(st, "h2", [128, 8, TC], BF16)
            f = self.sb(st, "f_c", [128, 32, TC], BF16)
            otok = self.sb(st, "otok", [128, 4, D], F32)
            rstd = self.sb(st, "rstd_c", [128, TC], F32)
            tmp = [self.sb(st, "tmpC%d" % i, [128, TC], F32) for i in range(2)]
            NW = 4
            w1r = [self.sb(st, "w1r%d" % i, [128, 8, 512], BF16) for i in range(NW)]
            w2r = [self.sb(st, "w2r%d" % i, [128, 32, 128], BF16) for i in range(NW)]
            ntile = min(NTOK // TC, kcc)
            wseq = []
            for ti_ in range(ntile):
                wseq += [("w1", b_) for b_ in range(8)] + [("w2", b_) for b_ in range(8)]
            wstate = {"issued": 0, "w1": 0, "w2": 0}
            wbuf = {}

            def issue_upto(n):
                while wstate["issued"] < min(n, len(wseq)):
                    k_ = wstate["issued"]
                    kind, b_ = wseq[k_]
                    ring = w1r if kind == "w1" else w2r
                    buf = ring[wstate[kind] % NW]
                    wstate[kind] += 1
                    scr = self.w1_scr if kind == "w1" else self.w2_scr
                    self.dma(buf[:], scr[b_], [scr], [buf], eng="sp" if k_ % 2 == 0 else "act")
                    wbuf[k_] = buf
                    wstate["issued"] += 1

            def rms(src, R):
                p = self.ps()
                for j in range(8):
                    self.mm(p[:], ones_bf[:], sq[:, j, :], j == 0, j == 7, [ones_bf, sq], [p])
                self.act(rstd[:], p[:], AF.Sqrt, [p, self.epsT], [rstd], scale=1.0 / D, bias=self.epsT[:, 0:1])
                self.S.op("dve", lambda e: e.reciprocal(out=rstd[:], in_=rstd[:]), [rstd.b], [rstd.b])

            def resid(gg, mc):
                for j in range(8):
                    t = tmp[j % 2]
                    self.tt("dve", t[:], oT[:, j, :], rstd[:], ALU.mult, [oT, rstd], [t])
                    self.stt(xT[:, j, :], t[:], gg[:, j, mc:mc + 1], xT[:, j, :], ALU.mult, ALU.add, [t, gg, xT], [xT])

            wi = 0
            for ti in range(NTOK // TC):
                if ti >= kcc:
                    break
                g0 = ti * TC
                mc = 0 if g0 < TS else 1
                yT = yTs[ti % 2]
                if ti == 0:
                    self.dma(yT[:], self.y_scr[:, :, g0:g0 + TC], [self.y_scr], [yT])
                self.dma(xT[:], self.xT_scr[:, :, g0:g0 + TC], [self.xT_scr], [xT], eng="act")
                if ti + 1 < ntile:
                    self.dma(yTs[(ti + 1) % 2][:], self.y_scr[:, :, g0 + TC:g0 + 2 * TC], [self.y_scr], [yTs[(ti + 1) % 2]])
                issue_upto(ti * 16 + 4)
                for m in range(8):
                    p = self.ps()
                    for kc in range(8):
                        self.mm(p[:], wout[:, kc, m * 128:(m + 1) * 128], yT[:, kc, :], kc == 0, kc == 7, [wout, yT], [p])
                    self.cp("act", oT[:, m, :], p[:], [p], [oT])
                    self.tt("pool", sq[:, m, :], oT[:, m, :], oT[:, m, :], ALU.mult, [oT], [sq])
                rms(oT, None)
                resid(self.gg1, mc)
                for j in range(8):
                    self.tt("pool", sq[:, j, :], xT[:, j, :], xT[:, j, :], ALU.mult, [xT], [sq])
                rms(xT, None)
                for j in range(8):
                    t = tmp[j % 2]
                    self.tt("dve", t[:], xT[:, j, :], rstd[:], ALU.mult, [xT, rstd], [t])
                    self.act(h2[:, j, :], t[:], AF.Identity, [t, self.gs2, self.modT], [h2],
                             scale=self.gs2[:, j, mc:mc + 1], bias=self.modT[:, 24 + j, mc:mc + 1])
                for blk in range(8):
                    issue_upto(ti * 16 + blk + 4)
                    w = wbuf[ti * 16 + blk]
                    for c4 in range(4):
                        fc = blk * 4 + c4
                        p = self.ps()
                        for kc in range(8):
                            self.mm(p[:], w[:, kc, c4 * 128:(c4 + 1) * 128], h2[:, kc, :], kc == 0, kc == 7, [w, h2], [p])
                        t = tmp[fc % 2]
                        self.act(t[:], p[:], AF.Relu, [p], [t])
                        self.tt("pool" if fc % 2 == 0 else "dve", f[:, fc, :], t[:], t[:], ALU.mult, [t], [f])
                for m in range(8):
                    issue_upto(ti * 16 + 8 + m + 4)
                    w = wbuf[ti * 16 + 8 + m]
                    p = self.ps()
                    for fc in range(32):
                        self.mm(p[:], w[:, fc, :], f[:, fc, :], fc == 0, fc == 31, [w, f], [p])
                    self.cp("act", oT[:, m, :], p[:], [p], [oT])
                    self.tt("pool", sq[:, m, :], oT[:, m, :], oT[:, m, :], ALU.mult, [oT], [sq])
                rms(oT, None)
                resid(self.gg2, mc)
                for s in range(4):
                    for half in range(2):
                        p = self.ps()
                        for jj in range(4):
                            j = half * 4 + jj
                            self.tr(p[:, jj * 128:(jj + 1) * 128], xT[:, j, s * 128:(s + 1) * 128], ident[:], [xT, ident], [p])
                        self.cp("act" if half == 0 else "dve", otok[:, s, half * 512:(half + 1) * 512], p[:], [p], [otok])
                if g0 < TS:
                    dst = self.ys[g0:g0 + TC, :].rearrange("(s p) f -> p s f", p=128)
                    self.dma(dst, otok[:], [otok], [self.ys])
                else:
                    dst = self.yp[:, :].rearrange("(s p) f -> p s f", p=128)
                    self.dma(dst, otok[:], [otok], [self.yp])


def _fm(v):
    v = np.asarray(v, np.float32).reshape(-1, 128)
    return np.ascontiguousarray(v.T)


def _pos_embed():
    def sincos(pos, dim):
        omega = (1.0 / (10000.0 ** (np.arange(dim // 2, dtype=np.float32) / np.float32(dim // 2)))).astype(np.float32)
        ang = pos.astype(np.float32)[:, None] * omega[None, :]
        return np.concatenate([np.sin(ang), np.cos(ang)], axis=-1).astype(np.float32)
    rows = TS // 64
    half = D // 2
    e_row = sincos(np.arange(rows), half)
    e_col = sincos(np.arange(64), half)
    emb = np.concatenate([np.broadcast_to(e_row[:, None, :], (rows, 64, half)),
                          np.broadcast_to(e_col[None, :, :], (rows, 64, half))], axis=-1)
    return np.ascontiguousarray(emb.reshape(rows * 64, D).astype(np.float32))


def _consts():
    c = np.zeros((128, NCST), np.float32)
    c[:, C_ID:C_ID + 128] = np.eye(128, dtype=np.float32)
    ob = np.zeros((128, 128), np.float32)
    ob[:64, :64] = 1.0
    ob[64:, 64:] = 1.0
    c[:, C_OB:C_OB + 128] = ob
    s = np.arange(64)[:, None]
    t = np.arange(64)[None, :]
    msi = np.zeros((128, 2, 64), np.float32)
    msi[:64, 0] = (s < t)
    msi[:64, 1] = (s <= t)
    msi[64:, 0] = (s > t)
    msi[64:, 1] = (s >= t)
    c[:, C_MSI:C_MSI + 128] = msi.reshape(128, 128)
    ml = np.zeros((128, 64), np.float32)
    ml[:64] = (t < s)
    ml[64:] = (t > s)
    c[:, C_ML:C_ML + 64] = ml
    ids = np.zeros((128, 64), np.float32)
    ids[:64] = np.eye(64)
    ids[64:] = np.eye(64)
    c[:, C_IDS:C_IDS + 64] = ids
    c[:64, C_MSI1:C_MSI1 + 128] = msi[64:].reshape(64, 128)
    c[:64, C_ML1:C_ML1 + 64] = ml[64:]
    tt_ = np.arange(TT)
    c[:, C_RMF:C_RMF + TT] = (tt_ % 64 != 0).astype(np.float32)[None, :]
    c[:, C_RMB:C_RMB + TT] = (tt_ % 64 != 63).astype(np.float32)[None, :]
    return c


_NC_CACHE = {}


def kernel(x_prompt, x_sample, c, state_rwkv, state_lru, c_ctx, w_mod, b_mod,
           g_pre_mix, g_post_mix, g_pre_mlp, g_post_mlp, w_in,
           rwkv_w0, rwkv_w_up, rwkv_a0, rwkv_a_up, rwkv_g_up, rwkv_k_k, rwkv_k_a, rwkv_r_k,
           rwkv_lnx_g, rwkv_lnx_b, lru_conv_w, lru_conv_b, lru_wa, lru_ba, lru_wx, lru_bx,
           lru_lambda, w_out, w_mlp1, w_mlp2, _debug=False):
    f = lambda a: np.ascontiguousarray(np.asarray(a, np.float32))
    x_prompt, x_sample, c, state_rwkv, state_lru, c_ctx = map(f, (x_prompt, x_sample, c, state_rwkv, state_lru, c_ctx))
    if "nc" not in _NC_CACHE:
        _NC_CACHE["nc"] = K(debug=_debug).build()
    nc = _NC_CACHE["nc"]
    pe = _pos_embed()
    cst = _consts()
    shared = {
        "pe": pe, "cst": cst,
        "w_mod": f(w_mod[0]), "w_in": f(w_in[0]), "w_out": f(w_out[0]), "w1": f(w_mlp1[0]), "w2": f(w_mlp2[0]),
        "wup": f(rwkv_w_up[0]).reshape(128, 512), "aup": f(rwkv_a_up[0]).reshape(128, 512), "gup": f(rwkv_g_up[0]),
        "lwa": f(lru_wa[0]), "lwx": f(lru_wx[0]),
    }
    prm0 = np.zeros((128, NPRM), np.float32)
    prm0[:, P_GPRE:P_GPRE + 8] = _fm(g_pre_mix[0])
    prm0[:, P_GPOST:P_GPOST + 8] = _fm(g_post_mix[0])
    prm0[:, P_GPRE2:P_GPRE2 + 8] = _fm(g_pre_mlp[0])
    prm0[:, P_GPOST2:P_GPOST2 + 8] = _fm(g_post_mlp[0])
    prm0[:, P_BMOD:P_BMOD + 48] = _fm(b_mod[0])
    for d in range(2):
        prm0[:, P_W0 + 4 * d:P_W0 + 4 * d + 4] = _fm(rwkv_w0[0, d])
        prm0[:, P_A0 + 4 * d:P_A0 + 4 * d + 4] = _fm(rwkv_a0[0, d])
        prm0[:, P_BA + 4 * d:P_BA + 4 * d + 4] = _fm(lru_ba[0, d])
        prm0[:, P_BX + 4 * d:P_BX + 4 * d + 4] = _fm(lru_bx[0, d])
        prm0[:, P_LAM + 4 * d:P_LAM + 4 * d + 4] = _fm(lru_lambda[0, d])
    prm0[:, P_KK:P_KK + 4] = _fm(rwkv_k_k[0])
    prm0[:, P_KA:P_KA + 4] = _fm(rwkv_k_a[0])
    prm0[:, P_RK:P_RK + 4] = _fm(np.asarray(rwkv_r_k[0]).reshape(-1))
    prm0[:, P_LNG:P_LNG + 4] = _fm(rwkv_lnx_g[0])
    prm0[:, P_LNB:P_LNB + 4] = _fm(rwkv_lnx_b[0])
    for i in range(4):
        prm0[:, P_CW + 4 * i:P_CW + 4 * i + 4] = _fm(lru_conv_w[0, i])
    prm0[:, P_CB:P_CB + 4] = _fm(lru_conv_b[0])
    in_maps = []
    for i in range(8):
        prm = prm0.copy()
        for d in range(2):
            prm[:, P_H0 + 4 * d:P_H0 + 4 * d + 4] = _fm(state_lru[i, 0, d])
        cT = np.zeros((128, 8, 2), np.float32)
        cT[:, :, 0] = _fm(c[i])
        cT[:, :, 1] = _fm(c_ctx)
        h0 = np.ascontiguousarray(state_rwkv[i, 0].transpose(0, 3, 1, 2)).reshape(128, 512)
        m = dict(shared)
        m.update({"xs": x_sample[i], "xp": np.ascontiguousarray(x_prompt[2 * i:2 * i + 2].reshape(2 * TP, D)),
                  "cT": cT.reshape(128, 16), "h0r": h0, "prm": prm})
        in_maps.append(m)
    res = run_bass_kernel_spmd(nc, in_maps, core_ids=list(range(8)))
    R = res.results
    y_prompt = np.zeros((16, TP, D), np.float32)
    y_sample = np.zeros((8, TS, D), np.float32)
    st_r = np.zeros((16, 1, 2, 8, 64, 64), np.float32)
    st_l = np.zeros((16, 1, 2, 512), np.float32)
    for i in range(8):
        r = R[i]
        y_sample[i] = r["ys"]
        y_prompt[2 * i:2 * i + 2] = r["yp"].reshape(2, TP, D)
        so = r["str_o"].reshape(2, 2, 64, 8, 64)
        st_r[2 * i:2 * i + 2, 0] = so.transpose(0, 1, 3, 4, 2)
        sl = r["stl_o"].reshape(128, 4, 2, 2)
        st_l[2 * i:2 * i + 2, 0] = sl.transpose(2, 3, 1, 0).reshape(2, 2, 512)
    if _debug:
        return (y_prompt, y_sample, st_r, st_l), R
    return (y_prompt, y_sample, st_r, st_l)
```

```python
import contextlib
import numpy as np
import concourse.bass as bass
import concourse.mybir as mybir
from concourse.bass_utils import run_bass_kernel_spmd

F32 = mybir.dt.float32
BF16 = mybir.dt.bfloat16
F32R = mybir.dt.float32r
AF = mybir.ActivationFunctionType
ALU = mybir.AluOpType

D = 1024
TS = 2048
TP = 256
NTOK = TS + 2 * TP
NCH = NTOK // 64
DIN = 2944
DFF = 4096
LAM = float(np.exp(-0.5))
EPS = 1e-6
LNX_EPS = 64e-5
TT = 256
TC = 512
GELU_C = 1.5957691216057308

P_GPRE, P_GPOST, P_GPRE2, P_GPOST2 = 0, 8, 16, 24
P_BMOD = 32
P_W0, P_A0 = 80, 88
P_KK, P_KA, P_RK, P_LNG, P_LNB = 96, 100, 104, 108, 112
P_CW, P_CB = 116, 132
P_BA, P_BX, P_LAM, P_H0 = 136, 144, 152, 160
NPRM = 168
C_ID, C_OB, C_MSI, C_ML, C_IDS, C_RMF, C_RMB = 0, 128, 256, 384, 448, 512, 768
C_MSI1, C_ML1 = 1024, 1152
NCST = 1216


class Buf:
    __slots__ = ("name", "lw", "rd", "excl", "multi", "ws")

    def __init__(self, name=""):
        self.name = name
        self.lw = None
        self.rd = {}
        self.excl = False
        self.multi = False
        self.ws = {}


class TL:
    def __init__(self, t, name=""):
        self.t = t
        self.b = Buf(name)

    def __getitem__(self, k):
        return self.t[k]


class Sched:
    ENGS = ("pe", "act", "dve", "pool", "sp")

    def __init__(self, nc):
        self.nc = nc
        self.streams = {e: [] for e in self.ENGS}
        self.cnt = {}
        self.waited = {e: {} for e in self.ENGS}
        self.n_ops = 0
        self.dma_n = {e: 0 for e in self.ENGS}
        self.NSLOT = {"sp": 44, "act": 44, "pool": 4, "dve": 2, "pe": 2}

    def _deps(self, eng, reads, writes):
        need = {}
        for b in reads:
            if b.multi:
                for s, v in b.ws.items():
                    if need.get(s, 0) < v:
                        need[s] = v
                continue
            if b.lw is not None:
                s, v = b.lw
                if need.get(s, 0) < v:
                    need[s] = v
            if b.excl:
                for s, v in b.rd.items():
                    if s != eng and need.get(s, 0) < v:
                        need[s] = v
        for b in writes:
            if b.multi:
                continue
            if b.lw is not None:
                s, v = b.lw
                if need.get(s, 0) < v:
                    need[s] = v
            for s, v in b.rd.items():
                if need.get(s, 0) < v:
                    need[s] = v
        out = []
        w = self.waited[eng]
        for s, v in need.items():
            if s == "pe" and eng == "pe":
                continue
            if w.get(s, 0) >= v:
                continue
            w[s] = v
            out.append((s, v))
        return out

    def op(self, eng, fn, reads=(), writes=(), dma=False):
        reads = [r.b if isinstance(r, TL) else r for r in reads]
        writes = [r.b if isinstance(r, TL) else r for r in writes]
        waits = self._deps(eng, reads, writes)
        if dma:
            slot = self.dma_n[eng] % self.NSLOT[eng]
            self.dma_n[eng] += 1
            sem = "%s_d%d" % (eng, slot)
            prev = self.cnt.get(sem, 0)
            if prev > 0 and self.waited[eng].get(sem, 0) < prev:
                self.waited[eng][sem] = prev
                waits.append((sem, prev))
        else:
            sem = eng
        inc = 16 if dma else 1
        self.cnt[sem] = self.cnt.get(sem, 0) + inc
        val = self.cnt[sem]
        self.streams[eng].append((waits, fn, sem, inc))
        self.n_ops += 1
        for b in reads:
            if b.rd.get(sem, 0) < val:
                b.rd[sem] = val
        for b in writes:
            if b.multi:
                if b.ws.get(sem, 0) < val:
                    b.ws[sem] = val
                continue
            b.lw = (sem, val)
            b.rd = {}
        return val

    def barrier_on(self, tl):
        if tl.b.lw is None:
            return
        sname, v = tl.b.lw
        for e in ("sp", "act", "pool"):
            if self.waited[e].get(sname, 0) < v:
                self.waited[e][sname] = v
                self.streams[e].append(([(sname, v)], None, None, 0))

    def barrier(self):
        snap = dict(self.cnt)
        for e in self.ENGS:
            waits = []
            for s, v in snap.items():
                if s == "pe" and e == "pe":
                    continue
                if self.waited[e].get(s, 0) < v:
                    self.waited[e][s] = v
                    waits.append((s, v))
            if waits:
                self.streams[e].append((waits, None, None, 0))

    def emit(self):
        nc = self.nc
        sems = {}
        with contextlib.ExitStack() as st:
            for s in self.cnt:
                sems[s] = st.enter_context(nc.semaphore(s))
            block = st.enter_context(nc.Block())
            engmap = {"pe": block.tensor, "act": block.scalar, "dve": block.vector,
                      "pool": block.gpsimd, "sp": block.sync}
            for e in self.ENGS:
                stream = self.streams[e]
                if not stream:
                    continue

                def body(eng, stream=stream):
                    for waits, fn, sem, inc in stream:
                        for s, v in waits:
                            eng.wait_ge(sems[s], v)
                        if fn is not None:
                            fn(eng).then_inc(sems[sem], inc)
                engmap[e](body)


class K:
    def __init__(self, debug=False, stop_after=None):
        self.debug = debug
        self.stop_after = stop_after
        import os
        self.cutk = int(os.environ.get("KCUT", "0"))
        self.cutm = int(os.environ.get("KCUTM", "99"))
        self.ktiles = int(os.environ.get("KTILES", "99"))
        self.kskip = os.environ.get("KSKIP", "").split(",")
        self.nc = bass.Bass("TRN2", target_bir_lowering=False)
        self.S = Sched(self.nc)
        self.es = contextlib.ExitStack()
        self.psr = 0
        self.rr = {}

    def dram(self, name, shape, dt, kind="Internal"):
        t = TL(self.nc.dram_tensor(name, list(shape), dt, kind=kind).ap(), name)
        t.b.multi = True
        return t

    def sb(self, st, name, shape, dt):
        return TL(st.enter_context(self.nc.sbuf_tensor(name, list(shape), dt)), name)

    def sb2(self, st, name, shape, dt):
        t = st.enter_context(self.nc.sbuf_tensor(name, list(shape), dt))
        return [TL(t, name + "_lo"), TL(t, name + "_hi")]

    def ps(self):
        p = self.psum[self.psr % 8]
        self.psr += 1
        return p

    def mm(self, out, lhsT, rhs, start, stop, R, W):
        self.S.op("pe", lambda e: e.matmul(out, lhsT=lhsT, rhs=rhs, start=start, stop=stop), R, W)

    def tr(self, out, in_, ident, R, W):
        self.S.op("pe", lambda e: e.transpose(out, in_, ident), R, W)

    def act(self, out, in_, func, R, W, scale=1.0, bias=None, eng="act"):
        if bias is None:
            self.S.op("act", lambda e: e.activation(out=out, in_=in_, func=func, scale=scale), R, W)
        else:
            self.S.op("act", lambda e: e.activation(out=out, in_=in_, func=func, scale=scale, bias=bias), R, W)

    def tt(self, eng, out, in0, in1, op, R, W):
        self.S.op(eng, lambda e: e.tensor_tensor(out=out, in0=in0, in1=in1, op=op), R, W)

    def tsc(self, eng, out, in0, s1, op0, R, W, s2=None, op1=None):
        if op1 is None:
            self.S.op(eng, lambda e: e.tensor_scalar(out=out, in0=in0, scalar1=s1, scalar2=None, op0=op0), R, W)
        else:
            self.S.op(eng, lambda e: e.tensor_scalar(out=out, in0=in0, scalar1=s1, scalar2=s2, op0=op0, op1=op1), R, W)

    def stt(self, out, in0, scalar, in1, op0, op1, R, W):
        self.S.op("dve", lambda e: e.scalar_tensor_tensor(out=out, in0=in0, scalar=scalar, in1=in1, op0=op0, op1=op1), R, W)

    def cp(self, eng, out, in_, R, W):
        if eng == "act":
            self.S.op("act", lambda e: e.activation(out=out, in_=in_, func=AF.Copy), R, W)
        else:
            self.S.op(eng, lambda e: e.tensor_copy(out=out, in_=in_), R, W)

    def scan(self, out, d0, d1, init, R, W):
        self.S.op("dve", lambda e: e.tensor_tensor_scan(out=out, data0=d0, data1=d1, initial=init,
                                                        op0=ALU.mult, op1=ALU.add), R, W)

    def dma(self, out, in_, R, W, eng="sp"):
        self.S.op(eng, lambda e: e.dma_start(out=out, in_=in_), R, W, dma=True)

    def memset(self, eng, ap, val, W):
        self.S.op(eng, lambda e: e.memset(ap, val), (), W)

    def pick(self, key, engs):
        i = self.rr.get(key, 0)
        self.rr[key] = i + 1
        return engs[i % len(engs)]

    def build(self):
        nc = self.nc
        I = lambda n, s, dt=F32: self.dram(n, s, dt, "ExternalInput")
        O = lambda n, s, dt=F32: self.dram(n, s, dt, "ExternalOutput")
        self.xs = I("xs", [TS, D])
        self.xp = I("xp", [2 * TP, D])
        self.pe = I("pe", [TS, D])
        self.cT = I("cT", [128, 16])
        self.h0r = I("h0r", [128, 512])
        self.prm = I("prm", [128, NPRM])
        self.cst = I("cst", [128, NCST])
        self.w_mod = I("w_mod", [D, 6 * D])
        self.w_in = I("w_in", [D, DIN])
        self.w_out = I("w_out", [D, D])
        self.w1 = I("w1", [D, DFF])
        self.w2 = I("w2", [DFF, D])
        self.wup = I("wup", [128, 512])
        self.aup = I("aup", [128, 512])
        self.gup = I("gup", [128, 512])
        self.lwa = I("lwa", [2, 8, 64, 64])
        self.lwx = I("lwx", [2, 8, 64, 64])
        self.ys = O("ys", [TS, D])
        self.yp = O("yp", [2 * TP, D])
        self.str_o = O("str_o", [2, 128, 512])
        self.stl_o = O("stl_o", [128, 16])
        self.xT_scr = self.dram("xT_scr", [128, 8, NTOK], F32)
        self.xb_scr = self.dram("xb_scr", [128, 4, NTOK], F32)
        self.gate_scr = self.dram("gate_scr", [128, 4, NTOK], BF16)
        self.g_scr = self.dram("g_scr", [128, 4, NTOK], BF16)
        self.bon_scr = self.dram("bon_scr", [128, 4, NTOK], BF16)
        self.y_scr = self.dram("y_scr", [128, 8, NTOK], BF16)
        self.ytok_scr = self.dram("ytok_scr", [2, NTOK, 512], F32)
        for n in ("art", "rrt", "ttt", "akt", "mrbt", "mrkt", "bh", "kh"):
            setattr(self, n + "_scr", self.dram(n + "_scr", [NCH, 128, 512], BF16))
        self.vt_scr = self.dram("vt_scr", [NCH, 64, 512], BF16)
        self.pend_scr = self.dram("pend_scr", [NCH, 128, 512], F32)
        self.w1_scr = self.dram("w1_scr", [8, 128, 8, 512], BF16)
        self.w2_scr = self.dram("w2_scr", [8, 128, 32, 128], BF16)
        if self.debug:
            self.dbg = {}

        with self.es as st0:
            self.psum = [TL(st0.enter_context(nc.psum_tensor("ps%d" % i, [128, 512], F32)), "ps%d" % i)
                         for i in range(8)]
            for p_ in self.psum:
                p_.b.excl = True
            self.prm_t = self.sb(st0, "prm_t", [128, NPRM], F32)
            self.cst_t = self.sb(st0, "cst_t", [128, NCST], F32)
            self.modT = self.sb(st0, "modT", [128, 48, 2], F32)
            self.gs1 = self.sb(st0, "gs1", [128, 8, 2], F32)
            self.gs2 = self.sb(st0, "gs2", [128, 8, 2], F32)
            self.gg1 = self.sb(st0, "gg1", [128, 8, 2], F32)
            self.gg2 = self.sb(st0, "gg2", [128, 8, 2], F32)
            self.ident = self.sb(st0, "ident", [128, 128], F32)
            self.ones_bf = self.sb(st0, "ones_bf", [128, 128], BF16)
            self.oblk_bf = self.sb(st0, "oblk_bf", [128, 128], BF16)
            self.epsT = self.sb(st0, "epsT", [128, 2], F32)
            self.misc = self.sb(st0, "misc", [128, 32], F32)
            self.stl_t = self.sb(st0, "stl_t", [128, 16], F32)
            for nm, fn in (("p0", self.phase0), ("pA", self.phaseA), ("pB", self.phaseB), ("pC", self.phaseC)):
                fn()
                self.S.barrier()
                if self.stop_after == nm:
                    break
            self.S.emit()
        return nc

    def dump(self, name, src_ap, shape, dt, R):
        o = self.dram("dbg_" + name, shape, dt, "ExternalOutput")
        self.dma(o[:], src_ap, R, [o])

    def phase0(self):
        nc = self.nc
        prm, cst = self.prm_t, self.cst_t
        self.dma(prm[:], self.prm[:], [], [prm])
        self.dma(cst[:], self.cst[:], [], [cst])
        self.cp("dve", self.ident[:], cst[:, C_ID:C_ID + 128], [cst], [self.ident])
        self.cp("dve", self.oblk_bf[:], cst[:, C_OB:C_OB + 128], [cst], [self.oblk_bf])
        self.memset("dve", self.ones_bf[:], 1.0, [self.ones_bf])
        self.memset("dve", self.epsT[:, 0:1], EPS, [self.epsT])
        self.memset("dve", self.epsT[:, 1:2], LNX_EPS, [self.epsT])
        self.tsc("dve", self.misc[:, 0:4], prm[:, P_KA:P_KA + 4], -1.0, ALU.mult, [prm], [self.misc], 1.0, ALU.add)
        with contextlib.ExitStack() as st:
            scT = self.sb(st, "scT", [128, 16], F32)
            cT = self.sb(st, "cT_t", [128, 16], F32)
            wm = [self.sb(st, "wm%d" % i, [128, 8, 512], F32) for i in range(2)]
            tmp = self.sb(st, "lam_tmp", [128, 8], F32)
            self.dma(cT[:], self.cT[:], [], [cT])
            self.act(scT[:], cT[:], AF.Silu, [cT], [scT])
            self.act(tmp[:], prm[:, P_LAM:P_LAM + 8], AF.Exp, [prm], [tmp], scale=-1.0)
            self.act(tmp[:], tmp[:], AF.Ln, [tmp], [tmp], bias=1.0)
            self.tsc("dve", self.misc[:, 4:12], tmp[:], -8.0, ALU.mult, [tmp], [self.misc])
            self.tsc("dve", self.misc[:, 12:20], tmp[:], -16.0, ALU.mult, [tmp], [self.misc])
            wsrc = self.w_mod[:].rearrange("(kc p) n -> p kc n", p=128)
            sc3 = scT[:].rearrange("p (k c) -> p k c", c=2)
            for blk in range(12):
                w = wm[blk % 2]
                self.dma(w[:], wsrc[:, :, blk * 512:(blk + 1) * 512], [], [w], eng="sp" if blk % 2 == 0 else "act")
                p = self.ps()
                for m in range(4):
                    for kc in range(8):
                        self.mm(p[:, 2 * m:2 * m + 2], w[:, kc, m * 128:(m + 1) * 128], sc3[:, kc, :],
                                kc == 0, kc == 7, [w, scT], [p])
                for m in range(4):
                    mi = blk * 4 + m
                    self.tsc("dve", self.modT[:, mi, :], p[:, 2 * m:2 * m + 2], prm[:, P_BMOD + mi:P_BMOD + mi + 1],
                             ALU.add, [p, prm], [self.modT])
            m3 = self.modT
            for (dst, sc_off, g_off, one) in ((self.gs1, 8, P_GPRE, 1.0), (self.gs2, 32, P_GPRE2, 1.0),
                                              (self.gg1, 16, P_GPOST, 0.0), (self.gg2, 40, P_GPOST2, 0.0)):
                for c in range(2):
                    self.tsc("dve", dst[:, :, c], m3[:, sc_off:sc_off + 8, c], one, ALU.add, [m3], [dst])
                    self.tt("dve", dst[:, :, c], dst[:, :, c], prm[:, g_off:g_off + 8], ALU.mult, [dst, prm], [dst])
            if self.debug:
                self.dump("modT", self.modT[:], [128, 48, 2], F32, [self.modT])
                self.dump("gs1", self.gs1[:], [128, 8, 2], F32, [self.gs1])

    def load_cast(self, st, dst_ap, dst_tl, src_ap, shape, tag):
        key = "stg_" + tag
        if not hasattr(self, key):
            setattr(self, key, [self.sb(st, "%s%d" % (key, i), shape, F32) for i in range(2)])
        ring = getattr(self, key)
        s = ring[self.rr.get(key, 0) % 2]
        self.rr[key] = self.rr.get(key, 0) + 1
        self.dma(s[:], src_ap, [], [s], eng="sp")
        eng = self.pick("castE", ["act", "pool"])
        self.cp(eng, dst_ap, s[:], [s], [dst_tl])

    def phaseA(self):
        nc = self.nc
        prm, cst = self.prm_t, self.cst_t
        with contextlib.ExitStack() as st:
            win = self.sb(st, "win", [128, 8, DIN], BF16)
            win.b.multi = True
            wsrc = self.w_in[:].rearrange("(kc p) n -> p kc n", p=128)
            wup = self.sb(st, "wup_t", [128, 512], BF16)
            aup = self.sb(st, "aup_t", [128, 512], BF16)
            gup = self.sb(st, "gup_t", [128, 512], BF16)
            with contextlib.ExitStack() as st2:
                stgA = [self.sb(st2, "stgA%d" % i, [128, 8, 256], F32) for i in range(2)]
                nb = 0
                for c0 in range(0, DIN, 256):
                    cw = min(256, DIN - c0)
                    s_ = stgA[nb % 2]
                    self.dma(s_[:, :, 0:cw], wsrc[:, :, c0:c0 + cw], [], [s_], eng="sp" if nb % 2 == 0 else "act")
                    self.cp("act" if nb % 2 == 0 else "pool", win[:, :, c0:c0 + cw], s_[:, :, 0:cw], [s_], [win])
                    nb += 1
                for i, (src, dstt) in enumerate(((self.wup, wup), (self.aup, aup), (self.gup, gup))):
                    s_ = stgA[nb % 2]
                    nb += 1
                    s2 = s_[:].rearrange("p a b -> p (a b)")[:, 0:512]
                    self.dma(s2, src[:], [], [s_])
                    self.cp("dve", dstt[:], s2, [s_], [dstt])
            self.S.barrier()
            mSI = cst[:, C_MSI:C_MSI + 128].rearrange("p (q t) -> p q t", q=2)
            mL = cst[:, C_ML:C_ML + 64]
            idS = cst[:, C_IDS:C_IDS + 64]

            xin = self.sb(st, "xin", [128, 2, D], F32)
            xT = self.sb(st, "xT", [128, 8, TT], F32)
            sq = self.sb(st, "sq", [128, 8, TT], BF16)
            hT = self.sb(st, "hT", [128, 8, TT], BF16)
            rstd = self.sb(st, "rstd", [128, TT], F32)
            tmpA = [self.sb(st, "tmpA%d" % i, [128, TT], F32) for i in range(2)]
            rT = self.sb(st, "rT", [128, 4, TT], F32)
            kT = self.sb(st, "kT", [128, 4, TT], F32)
            vT = self.sb(st, "vT", [128, 4, TT], F32)
            xw = self.sb(st, "xw", [128, TT], BF16)
            xa = self.sb(st, "xa", [128, TT], BF16)
            xg = self.sb(st, "xg", [128, TT], BF16)
            xbT = self.sb(st, "xbT", [128, 4, TT], F32)
            gtmp = [self.sb(st, "gtmp%d" % i, [128, TT], F32) for i in range(3)]
            gate = self.sb(st, "gate", [128, 4, TT], BF16)
            gT = self.sb(st, "gT", [128, 4, TT], BF16)
            kkn = self.sb(st, "kkn", [128, 4, TT], F32)
            ksum = self.sb(st, "ksum", [128, 4, TT], F32)
            bon = self.sb(st, "bon", [128, 4, TT], BF16)
            sg = self.sb(st, "sg", [128, 4, TT], F32)
            cs = self.sb(st, "cs", [128, 4, TT], F32)
            E1 = self.sb(st, "E1", [128, 4, TT], F32)
            ad = self.sb(st, "ad", [128, 4, TT], F32)
            wk1 = self.sb(st, "wk1", [128, 4, TT], F32)
            wk2 = self.sb(st, "wk2", [128, 4, TT], F32)
            NC4 = TT // 64
            AR = [self.sb(st, "AR%d" % d, [128, 4, NC4, 2, 64], BF16) for d in range(2)]
            BK = [self.sb(st, "BK%d" % d, [128, 4, NC4, 2, 64], BF16) for d in range(2)]
            PEb1 = self.sb(st, "PEb", [128, 4, NC4, 64], F32)
            PEb = [PEb1, PEb1]
            tokB1 = self.sb(st, "tokB", [128, 2, 8, 64], BF16)
            tokK1 = self.sb(st, "tokK", [128, 2, 8, 64], BF16)
            tokB, tokK = [tokB1, tokB1], [tokK1, tokK1]
            tokV = self.sb(st, "tokV", [128, 2, 8, 64], BF16)
            NSET = 2
            MRBs = [self.sb(st, "MRBs%d" % i, [64, 8, 64], BF16) for i in range(NSET)]
            Lt0s = [self.sb(st, "Lt0_%d" % i, [64, 8, 64], F32R) for i in range(NSET)]
            Tfins = [self.sb(st, "Tfin%d" % i, [64, 8, 64], BF16) for i in range(NSET)]
            SCk = self.sb(st, "SCk", [128, 2, 8, 64], BF16)
            Lms = [[self.sb(st, "Lm%d_%d" % (i, k), [64, 8, 64], F32R) for i in range(2)] for k in range(NSET)]
            Ltms = [[self.sb(st, "Ltm%d_%d" % (i, k), [64, 8, 64], F32R) for i in range(2)] for k in range(NSET)]
            ILms = [self.sb(st, "ILm_%d" % k, [64, 8, 64], F32R) for k in range(NSET)]
            Ttms = [[self.sb(st, "Ttm%d_%d" % (i, k), [64, 8, 64], F32R) for i in range(2)] for k in range(NSET)]

            ones_bf, oblk, ident = self.ones_bf, self.oblk_bf, self.ident
            tiles = [(0, t0, True, 0) for t0 in range(0, TS, TT)] + [(1, TS, False, 1), (2, TS + TP, False, 1)]
            mS1 = cst[0:64, C_MSI1:C_MSI1 + 128].rearrange("p (q t) -> p q t", q=2)
            mL1 = cst[0:64, C_ML1:C_ML1 + 64]
            id64 = idS[0:64]
            loaded = set()

            def load_x(g0, is_s):
                if g0 in loaded:
                    return
                loaded.add(g0)
                if is_s:
                    src = self.xs[g0:g0 + TT, :].rearrange("(s p) f -> p s f", p=128)
                else:
                    l0 = g0 - TS
                    src = self.xp[l0:l0 + TT, :].rearrange("(s p) f -> p s f", p=128)
                self.dma(xin[:], src, [], [xin])
                if is_s:
                    petv = xT[:].rearrange("p a b -> p (a b)").rearrange("p (s f) -> p s f", s=2)
                    self.dma(petv, self.pe[g0:g0 + TT, :].rearrange("(s p) f -> p s f", p=128), [], [xT], eng="act")

            def front(seq, g0, is_s, mc, nxt=None):
                load_x(g0, is_s)
                if is_s:
                    petv = xT[:].rearrange("p a b -> p (a b)").rearrange("p (s f) -> p s f", s=2)
                    self.tt("pool", xin[:], xin[:], petv, ALU.add, [xin, xT], [xin])
                for j in range(8):
                    p = self.ps()
                    for s in range(2):
                        self.tr(p[:, s * 128:(s + 1) * 128], xin[:, s, j * 128:(j + 1) * 128], ident[:], [xin, ident], [p])
                    self.cp("act", xT[:, j, :], p[:, 0:TT], [p], [xT])
                    self.tt("pool", sq[:, j, :], xT[:, j, :], xT[:, j, :], ALU.mult, [xT], [sq])
                    yield
                self.dma(self.xT_scr[:, :, g0:g0 + TT], xT[:], [xT], [self.xT_scr])
                p = self.ps()
                for j in range(8):
                    self.mm(p[:, 0:TT], ones_bf[:], sq[:, j, :], j == 0, j == 7, [ones_bf, sq], [p])
                self.act(rstd[:], p[:, 0:TT], AF.Sqrt, [p, self.epsT], [rstd], scale=1.0 / D, bias=self.epsT[:, 0:1])
                self.S.op("dve", lambda e: e.reciprocal(out=rstd[:], in_=rstd[:]), [rstd.b], [rstd.b])
                for j in range(8):
                    t = tmpA[j % 2]
                    self.tt("dve", t[:], xT[:, j, :], rstd[:], ALU.mult, [xT, rstd], [t])
                    self.act(hT[:, j, :], t[:], AF.Identity, [t, self.gs1, self.modT], [hT],
                             scale=self.gs1[:, j, mc:mc + 1], bias=self.modT[:, j, mc:mc + 1])
                    yield
                for m in range(23):
                    if m >= self.cutm:
                        break
                    p = self.ps()
                    for kc in range(8):
                        self.mm(p[:, 0:TT], win[:, kc, m * 128:(m + 1) * 128], hT[:, kc, :], kc == 0, kc == 7, [win, hT], [p])
                    pz = p[:, 0:TT]
                    if m < 4:
                        self.cp("act", rT[:, m, :], pz, [p], [rT])
                    elif m < 8:
                        self.cp("act", kT[:, m - 4, :], pz, [p], [kT])
                    elif m < 12:
                        self.cp("act", vT[:, m - 8, :], pz, [p], [vT])
                    elif m == 12:
                        self.act(xw[:], pz, AF.Tanh, [p], [xw])
                    elif m == 13:
                        self.cp("act", xa[:], pz, [p], [xa])
                    elif m == 14:
                        self.act(xg[:], pz, AF.Sigmoid, [p], [xg])
                    elif m < 19:
                        self.cp("act", xbT[:, m - 15, :], pz, [p], [xbT])
                    else:
                        j = m - 19
                        g0_, g1_, g2_ = gtmp
                        self.cp("act", g0_[:], pz, [p], [g0_])
                        self.tt("pool", g1_[:], g0_[:], g0_[:], ALU.mult, [g0_], [g1_])
                        self.tsc("dve", g1_[:], g1_[:], 0.044715, ALU.mult, [g1_], [g1_], 1.0, ALU.add)
                        self.tt("dve", g1_[:], g1_[:], g0_[:], ALU.mult, [g1_, g0_], [g1_])
                        self.act(g2_[:], g1_[:], AF.Sigmoid, [g1_], [g2_], scale=GELU_C)
                        self.tt("pool", gate[:, j, :], g0_[:], g2_[:], ALU.mult, [g0_, g2_], [gate])
                    yield
                self.dma(self.xb_scr[:, :, g0:g0 + TT], xbT[:], [xbT], [self.xb_scr])
                if self.debug and g0 == 0:
                    self.dump("hT", hT[:], [128, 8, TT], BF16, [hT])
                    self.dump("rT", rT[:], [128, 4, TT], F32, [rT])
                    self.dump("vT", vT[:], [128, 4, TT], F32, [vT])
                    self.dump("xbT", xbT[:], [128, 4, TT], F32, [xbT])
                    self.dump("gate", gate[:], [128, 4, TT], BF16, [gate])
                self.dma(self.gate_scr[:, :, g0:g0 + TT], gate[:], [gate], [self.gate_scr])
                for j in range(4):
                    p = self.ps()
                    self.mm(p[:, 0:TT], gup[:, j * 128:(j + 1) * 128], xg[:], True, True, [gup, xg], [p])
                    self.cp("act", gT[:, j, :], p[:, 0:TT], [p], [gT])
                    yield
                self.dma(self.g_scr[:, :, g0:g0 + TT], gT[:], [gT], [self.g_scr])
                for j in range(4):
                    self.tsc("dve", kkn[:, j, :], kT[:, j, :], prm[:, P_KK + j:P_KK + j + 1], ALU.mult, [kT, prm], [kkn])
                    self.tt("pool", sq[:, j, :], kkn[:, j, :], kkn[:, j, :], ALU.mult, [kkn], [sq])
                for j in range(4):
                    p = self.ps()
                    self.mm(p[:, 0:TT], oblk[:], sq[:, j, :], True, True, [oblk, sq], [p])
                    t = tmpA[j % 2]
                    self.act(t[:], p[:, 0:TT], AF.Sqrt, [p], [t])
                    self.tsc("dve", t[:], t[:], 1e-12, ALU.max, [t], [t])
                    self.S.op("dve", lambda e, t=t: e.reciprocal(out=t[:], in_=t[:]), [t.b], [t.b])
                    self.tt("dve", kkn[:, j, :], kkn[:, j, :], t[:], ALU.mult, [kkn, t], [kkn])
                    yield
                for s in range(2):
                    p = self.ps()
                    for j in range(4):
                        self.tr(p[:, j * 128:(j + 1) * 128], vT[:, j, s * 128:(s + 1) * 128], ident[:], [vT, ident], [p])
                    self.cp("act", tokV[:, s, :, :].rearrange("p h k -> p (h k)"), p[:], [p], [tokV])
                c0 = g0 // 64
                for s in range(2):
                    dst = self.vt_scr[c0 + 2 * s:c0 + 2 * s + 2, :, :].rearrange("c s f -> (c s) f")
                    self.dma(dst, tokV[:, s, :, :].rearrange("p h k -> p (h k)"), [tokV], [self.vt_scr])
                if nxt is not None:
                    load_x(nxt[1], nxt[2])
                yield
            def prep(g0, d):
                c0 = g0 // 64
                for j in range(4):
                    p = self.ps()
                    self.mm(p[:, 0:TT], wup[d * 64:(d + 1) * 64, j * 128:(j + 1) * 128], xw[d * 64:(d + 1) * 64, :],
                            True, True, [wup, xw], [p])
                    self.act(sg[:, j, :], p[:, 0:TT], AF.Sigmoid, [p, prm], [sg],
                             bias=prm[:, P_W0 + 4 * d + j:P_W0 + 4 * d + j + 1])
                    if d == 0:
                        self.scan(cs[:, j, :], cst[:, C_RMF:C_RMF + TT], sg[:, j, :], 0.0, [cst, sg], [cs])
                    else:
                        self.scan(cs[:, j, ::-1], cst[:, C_RMB:C_RMB + TT][:, ::-1], sg[:, j, ::-1], 0.0, [cst, sg], [cs])
                    yield
                for j in range(4):
                    p = self.ps()
                    self.mm(p[:, 0:TT], aup[d * 64:(d + 1) * 64, j * 128:(j + 1) * 128], xa[d * 64:(d + 1) * 64, :],
                            True, True, [aup, xa], [p])
                    self.act(ad[:, j, :], p[:, 0:TT], AF.Sigmoid, [p, prm], [ad],
                             bias=prm[:, P_A0 + 4 * d + j:P_A0 + 4 * d + j + 1])
                    yield
                self.tt("dve", sg[:], cs[:], sg[:], ALU.subtract, [cs, sg], [sg])
                self.act(E1[:], cs[:], AF.Exp, [cs], [E1], scale=-LAM)
                self.act(cs[:], cs[:], AF.Exp, [cs], [cs], scale=LAM)
                self.act(sg[:], sg[:], AF.Exp, [sg], [sg], scale=-LAM)
                E2, E3 = cs, sg
                ar5 = AR[d]
                bk5 = BK[d]
                v4 = lambda tl: tl[:].rearrange("p j (c t) -> p j c t", t=64)
                self.stt(ar5[:, :, :, 0, :], v4(kkn), -1.0, v4(E3), ALU.mult, ALU.mult, [kkn, E3], [ar5])
                self.tt("pool", ar5[:, :, :, 1, :], v4(rT), v4(E1), ALU.mult, [rT, E1], [ar5])
                yield
                self.tt("dve", wk1[:], kkn[:], ad[:], ALU.mult, [kkn, ad], [wk1])
                self.tt("dve", wk1[:], wk1[:], E2[:], ALU.mult, [wk1, E2], [wk1])
                self.cp("act", bk5[:, :, :, 0, :], v4(wk1), [wk1], [bk5])
                yield
                for j in range(4):
                    self.tsc("dve", wk2[:, j, :], ad[:, j, :], prm[:, P_KA + j:P_KA + j + 1], ALU.mult, [ad, prm, self.misc], [wk2],
                             self.misc[:, j:j + 1], ALU.add)
                self.tt("dve", wk2[:], wk2[:], kT[:], ALU.mult, [wk2, kT], [wk2])
                if d == 0:
                    self.cp("pool", ksum[:], wk2[:], [wk2], [ksum])
                else:
                    self.tt("pool", ksum[:], ksum[:], wk2[:], ALU.add, [ksum, wk2], [ksum])
                self.tt("dve", wk2[:], wk2[:], E2[:], ALU.mult, [wk2, E2], [wk2])
                self.cp("act", bk5[:, :, :, 1, :], v4(wk2), [wk2], [bk5])
                yield
                te = 63 if d == 0 else 0
                pend_b = v4(E1)[:, :, :, te:te + 1].to_broadcast([128, 4, NC4, 64])
                self.cp("act", PEb[d][:], pend_b, [E1], [PEb[d]])
                self.tt("dve", v4(wk1), v4(wk1), PEb[d][:], ALU.mult, [wk1, PEb[d]], [wk1])
                self.tt("dve", v4(wk2), v4(wk2), PEb[d][:], ALU.mult, [wk2, PEb[d]], [wk2])
                yield
                for (srcw, tokX, scr) in ((wk1, tokB[d], self.bh_scr), (wk2, tokK[d], self.kh_scr)):
                    for s in range(2):
                        p = self.ps()
                        for j in range(4):
                            self.tr(p[:, j * 128:(j + 1) * 128], srcw[:, j, s * 128:(s + 1) * 128], ident[:], [srcw, ident], [p])
                        self.cp("act", tokX[:, s, :, :].rearrange("p h k -> p (h k)"), p[:], [p], [tokX])
                        for cc in range(2):
                            self.dma(scr[c0 + 2 * s + cc, d * 64:(d + 1) * 64, :],
                                     tokX[cc * 64:(cc + 1) * 64, s, :, :].rearrange("p h k -> p (h k)"),
                                     [tokX], [scr])
                        yield
                for cl in range(NC4):
                    c = c0 + cl
                    for hp in range(2):
                        for (q, scr) in ((0, self.art_scr), (1, self.rrt_scr)):
                            dst = scr[c, d * 64:(d + 1) * 64, :].rearrange("k (j hp t) -> k j hp t", hp=2, t=64)[:, :, hp, :]
                            self.dma(dst, ar5[hp * 64:(hp + 1) * 64, :, cl, q, :], [ar5], [scr], eng="sp")
                        dst = self.pend_scr[c, d * 64:(d + 1) * 64, :].rearrange("k (j hp t) -> k j hp t", hp=2, t=64)[:, :, hp, :]
                        self.dma(dst, PEb[d][hp * 64:(hp + 1) * 64, :, cl, :], [PEb[d]], [self.pend_scr], eng="sp")
                yield
            def bonus(g0):
                for j in range(4):
                    self.stt(sq[:, j, :], rT[:, j, :], prm[:, P_RK + j:P_RK + j + 1], ksum[:, j, :], ALU.mult, ALU.mult,
                             [rT, prm, ksum], [sq])
                    p = self.ps()
                    self.mm(p[:, 0:TT], oblk[:], sq[:, j, :], True, True, [oblk, sq], [p])
                    self.tt("dve", bon[:, j, :], p[:, 0:TT], vT[:, j, :], ALU.mult, [p, vT], [bon])
                self.dma(self.bon_scr[:, :, g0:g0 + TT], bon[:], [bon], [self.bon_scr])
                yield
            def chunk_sck(g0):
                c0 = g0 // 64
                for cl in range(NC4):
                    c = c0 + cl
                    for hp in range(2):
                        p = self.ps()
                        for j in range(4):
                            for d in range(2):
                                self.mm(p[d * 64:(d + 1) * 64, j * 128:(j + 1) * 128],
                                        BK[d][hp * 64:(hp + 1) * 64, j, cl, 1, :],
                                        AR[d][hp * 64:(hp + 1) * 64, j, cl, :, :].rearrange("p q t -> p (q t)"),
                                        True, True, [BK[d], AR[d]], [p])
                        self.tt("dve", SCk[:, :, hp::2, :].rearrange("p q h t -> p h q t"),
                                p[:].rearrange("p (h q t) -> p h q t", q=2, t=64),
                                mSI.unsqueeze(1).to_broadcast([128, 4, 2, 64]), ALU.mult, [p, cst], [SCk])
                    self.dma(self.akt_scr[c], SCk[:, 0, :, :].rearrange("p h s -> p (h s)"), [SCk], [self.akt_scr], eng="act")
                    self.dma(self.mrkt_scr[c], SCk[:, 1, :, :].rearrange("p h s -> p (h s)"), [SCk], [self.mrkt_scr], eng="act")
                    yield
            def chunk_d(g0, d, cls, k):
                c0 = g0 // 64
                Lm, Ltm, ILm, Ttm, Lt0, Tfin = Lms[k], Ltms[k], ILms[k], Ttms[k], Lt0s[k], Tfins[k]
                MRB = MRBs[k]
                for cl in cls:
                    c = c0 + cl
                    msk = mSI[0:64] if d == 0 else mS1
                    mskL = mL[0:64] if d == 0 else mL1
                    L0 = Lm[0]
                    for hp in range(2):
                        p = self.ps()
                        for j in range(4):
                            self.mm(p[0:64, j * 128:(j + 1) * 128],
                                    BK[d][hp * 64:(hp + 1) * 64, j, cl, 0, :],
                                    AR[d][hp * 64:(hp + 1) * 64, j, cl, :, :].rearrange("p q t -> p (q t)"),
                                    True, True, [BK[d], AR[d]], [p])
                        p4 = p[0:64, :].rearrange("p (h q t) -> p h q t", q=2, t=64)
                        self.tt("dve", Lt0[:, hp::2, :], p4[:, :, 0, :], msk[:, 0, :].unsqueeze(1).to_broadcast([64, 4, 64]),
                                ALU.mult, [p, cst], [Lt0])
                        self.tt("dve", MRB[:, hp::2, :], p4[:, :, 1, :], msk[:, 1, :].unsqueeze(1).to_broadcast([64, 4, 64]),
                                ALU.mult, [p, cst], [MRB])
                        p2 = self.ps()
                        for j in range(4):
                            self.mm(p2[0:64, j * 64:(j + 1) * 64],
                                    AR[d][hp * 64:(hp + 1) * 64, j, cl, 0, :], BK[d][hp * 64:(hp + 1) * 64, j, cl, 0, :],
                                    True, True, [AR[d], BK[d]], [p2])
                        self.tt("dve", L0[:, hp::2, :], p2[0:64, 0:256].rearrange("p (h s) -> p h s", s=64),
                                mskL.unsqueeze(1).to_broadcast([64, 4, 64]), ALU.mult, [p2, cst], [L0])
                    self.dma(self.mrbt_scr[c, d * 64:(d + 1) * 64, :], MRB[:].rearrange("p h s -> p (h s)"),
                             [MRB], [self.mrbt_scr], eng="act")
                    yield
                    T0 = Ttm[0]
                    self.tt("pool", T0[:], Lt0[:].bitcast(F32), id64.unsqueeze(1).to_broadcast([64, 8, 64]), ALU.add,
                            [Lt0, cst], [T0])
                    L_prev, Tt_prev, Lt_prev = L0, T0, Lt0
                    for lev in range(1, 6):
                        L_new, Lt_new, Tt_new = Lm[lev % 2], Ltm[lev % 2], Ttm[lev % 2]
                        pA = self.ps()
                        for h in range(8):
                            self.mm(pA[0:64, h * 64:(h + 1) * 64], Lt_prev[:, h, :], L_prev[:, h, :], True, True,
                                    [Lt_prev, L_prev], [pA])
                        if lev < 5:
                            pB = self.ps()
                            for h in range(8):
                                self.mm(pB[0:64, h * 64:(h + 1) * 64], L_prev[:, h, :], Lt_prev[:, h, :], True, True,
                                        [Lt_prev, L_prev], [pB])
                        self.tt("dve", ILm[:], pA[0:64, :].rearrange("p (h s) -> p h s", s=64),
                                id64.unsqueeze(1).to_broadcast([64, 8, 64]), ALU.add, [pA, cst], [ILm])
                        if lev < 5:
                            self.cp("dve", L_new[:].rearrange("p h s -> p (h s)"), pA[0:64, :], [pA], [L_new])
                            self.cp("act", Lt_new[:].rearrange("p h s -> p (h s)"), pB[0:64, :], [pB], [Lt_new])
                        yield
                        pC = self.ps()
                        for h in range(8):
                            self.mm(pC[0:64, h * 64:(h + 1) * 64], ILm[:, h, :], Tt_prev[:, h, :], True, True,
                                    [ILm, Tt_prev], [pC])
                        if lev < 5:
                            self.cp("act", Tt_new[:].rearrange("p h s -> p (h s)"), pC[0:64, :], [pC], [Tt_new])
                        else:
                            self.cp("act", Tfin[:].rearrange("p h s -> p (h s)"), pC[0:64, :], [pC], [Tfin])
                        L_prev, Tt_prev, Lt_prev = L_new, Tt_new, Lt_new
                        yield
                    self.dma(self.ttt_scr[c, d * 64:(d + 1) * 64, :], Tfin[:].rearrange("p h s -> p (h s)"),
                             [Tfin], [self.ttt_scr], eng="act")


            def run_all(*gens):
                gens = list(gens)
                while gens:
                    for g in list(gens):
                        try:
                            next(g)
                        except StopIteration:
                            gens.remove(g)

            def seq_(*gens):
                for g in gens:
                    yield from g

            prev = None
            tl_ = tiles[:self.ktiles]
            for ti_, (seq, g0, is_s, mc) in enumerate(tl_):
                nxt = tl_[ti_ + 1] if ti_ + 1 < len(tl_) else None
                if prev is None:
                    run_all(front(seq, g0, is_s, mc, nxt))
                else:
                    run_all(seq_(chunk_d(prev, 1, [0, 1], 0), chunk_sck(prev)), chunk_d(prev, 1, [2, 3], 1), front(seq, g0, is_s, mc, nxt))
                run_all(prep(g0, 0))
                run_all(chunk_d(g0, 0, [0, 1], 0), chunk_d(g0, 0, [2, 3], 1), prep(g0, 1))
                run_all(bonus(g0))
                prev = g0
            run_all(seq_(chunk_d(prev, 1, [0, 1], 0), chunk_sck(prev)), chunk_d(prev, 1, [2, 3], 1))

    def phaseB(self):
        with contextlib.ExitStack() as st:
            side = [self.gen_c0(st), self.phaseB_lru(st)]
            post = self.phaseB_post(st)
            next(post)
            done = np.zeros((2, NCH), bool)
            posted = [False] * (NTOK // 128)
            si = 0
            for info in self.phaseB_chain(st):
                for (d, c) in info:
                    done[d, c] = True
                for _ in range(2):
                    if side:
                        g = side[si % len(side)]
                        si += 1
                        try:
                            next(g)
                        except StopIteration:
                            side.remove(g)
                for b in range(NTOK // 128):
                    if not posted[b] and done[:, 2 * b:2 * b + 2].all():
                        posted[b] = True
                        post.send(b)
            for g in side:
                for _ in g:
                    pass
            for b in range(NTOK // 128):
                if not posted[b]:
                    post.send(b)
            if self.debug:
                self.S.barrier()
                self.dump("yscr", self.y_scr[:], [128, 8, NTOK], BF16, [self.y_scr])

    def gen_c0(self, st):
        stg = [self.sb(st, "stgC%d" % i, [128, 8, 512], F32) for i in range(2)]
        wb = [self.sb(st, "wbC%d" % i, [128, 8, 512], BF16) for i in range(2)]
        w1src = self.w1[:].rearrange("(kc p) n -> p kc n", p=128)
        for blk in range(8):
            s, o = stg[blk % 2], wb[blk % 2]
            self.dma(s[:], w1src[:, :, blk * 512:(blk + 1) * 512], [], [s])
            self.cp("pool", o[:], s[:], [s], [o])
            self.dma(self.w1_scr[blk], o[:], [o], [self.w1_scr], eng="act")
            yield
        w2src = self.w2[:].rearrange("(fc p) n -> p fc n", p=128)
        for m in range(8):
            s, o = stg[m % 2], wb[m % 2]
            s4 = s[:].rearrange("p k (a b) -> p (k a) b", b=128)
            o4 = o[:].rearrange("p k (a b) -> p (k a) b", b=128)
            self.dma(s4, w2src[:, :, m * 128:(m + 1) * 128], [], [s])
            self.cp("pool", o[:], s[:], [s], [o])
            self.dma(self.w2_scr[m], o4, [o], [self.w2_scr], eng="act")
            yield

    def phaseB_lru(self, st):
        prm, cst, misc = self.prm_t, self.cst_t, self.misc
        if True:
            wbd32 = self.sb(st, "wbd32", [128, 16, 128], F32)
            wbd = self.sb(st, "wbd", [128, 16, 128], BF16)
            self.memset("pool", wbd32[:], 0.0, [wbd32])
            self.S.barrier_on(wbd32)
            wbd32.b.multi = True
            for gi, src in enumerate((self.lwa, self.lwx)):
                for d in range(2):
                    for j in range(4):
                        for hb in range(2):
                            self.dma(wbd32[hb * 64:(hb + 1) * 64, (gi * 2 + d) * 4 + j, hb * 64:(hb + 1) * 64],
                                     src[d, 2 * j + hb], [], [wbd32])
            self.cp("dve", wbd[:], wbd32[:], [wbd32], [wbd])
            TM = TS
            xbp = self.sb(st, "xbp", [128, TM + 4], F32)
            xc = self.sb(st, "xc", [128, TM], F32)
            xcb = self.sb(st, "xcb", [128, TM], BF16)
            gt = self.sb(st, "gt_l", [128, TM], BF16)
            a_t = self.sb(st, "a_t", [128, TM], F32)
            bx_t = self.sb(st, "bx_t", [128, TM], F32)
            s_t = self.sb(st, "s_t", [128, TM], F32)
            hs = [self.sb(st, "hs%d" % d, [128, TM], F32) for d in range(2)]
            yb = self.sb(st, "yb", [128, TM], BF16)
            for (seq, g0, T) in ((0, 0, TS), (1, TS, TP), (2, TS + TP, TP)):
                for j in range(4):
                    self.memset("pool", xbp[:, 0:2], 0.0, [xbp])
                    self.memset("pool", xbp[:, T + 2:T + 4], 0.0, [xbp])
                    self.dma(xbp[:, 2:T + 2], self.xb_scr[:, j, g0:g0 + T], [self.xb_scr], [xbp])
                    self.dma(gt[:, 0:T], self.gate_scr[:, j, g0:g0 + T], [self.gate_scr], [gt], eng="act")
                    cw = lambda i: prm[:, P_CW + 4 * i + j:P_CW + 4 * i + j + 1]
                    self.act(xc[:, 0:T], xbp[:, 0:T], AF.Identity, [xbp, prm], [xc], scale=cw(0), bias=prm[:, P_CB + j:P_CB + j + 1])
                    for i in range(1, 4):
                        self.stt(xc[:, 0:T], xbp[:, i:i + T], cw(i), xc[:, 0:T], ALU.mult, ALU.add, [xbp, prm, xc], [xc])
                    self.cp("pool", xcb[:, 0:T], xc[:, 0:T], [xc], [xcb])
                    yield
                    for d in range(2):
                        for t0 in range(0, T, 512):
                            tw = min(512, T - t0)
                            p = self.ps()
                            self.mm(p[:, 0:tw], wbd[:, (0 * 2 + d) * 4 + j, :], xcb[:, t0:t0 + tw], True, True, [wbd, xcb], [p])
                            self.act(s_t[:, t0:t0 + tw], p[:, 0:tw], AF.Sigmoid, [p, prm], [s_t],
                                     bias=prm[:, P_BA + 4 * d + j:P_BA + 4 * d + j + 1])
                            p2 = self.ps()
                            self.mm(p2[:, 0:tw], wbd[:, (1 * 2 + d) * 4 + j, :], xcb[:, t0:t0 + tw], True, True, [wbd, xcb], [p2])
                            self.act(bx_t[:, t0:t0 + tw], p2[:, 0:tw], AF.Sigmoid, [p2, prm], [bx_t],
                                     bias=prm[:, P_BX + 4 * d + j:P_BX + 4 * d + j + 1])
                        col = 4 + d * 4 + j
                        self.act(a_t[:, 0:T], s_t[:, 0:T], AF.Exp, [s_t, misc], [a_t], scale=misc[:, col:col + 1])
                        self.act(s_t[:, 0:T], s_t[:, 0:T], AF.Exp, [s_t, misc], [s_t], scale=misc[:, col + 8:col + 9])
                        self.act(s_t[:, 0:T], s_t[:, 0:T], AF.Sqrt, [s_t], [s_t], scale=-1.0, bias=1.0)
                        self.tt("pool", bx_t[:, 0:T], bx_t[:, 0:T], xc[:, 0:T], ALU.mult, [bx_t, xc], [bx_t])
                        self.tt("dve", bx_t[:, 0:T], bx_t[:, 0:T], s_t[:, 0:T], ALU.mult, [bx_t, s_t], [bx_t])
                        h = hs[d]
                        if seq == 0:
                            init = prm[:, P_H0 + 4 * d + j:P_H0 + 4 * d + j + 1]
                        else:
                            init = 0.0
                        if d == 0:
                            self.scan(h[:, 0:T], a_t[:, 0:T], bx_t[:, 0:T], init, [a_t, bx_t, prm], [h])
                        else:
                            self.scan(h[:, 0:T][:, ::-1], a_t[:, 0:T][:, ::-1], bx_t[:, 0:T][:, ::-1], init, [a_t, bx_t, prm], [h])
                        if seq > 0:
                            col_o = j * 4 + (seq - 1) * 2 + d
                            te = T - 1 if d == 0 else 0
                            self.cp("pool", self.stl_t[:, col_o:col_o + 1], h[:, te:te + 1], [h], [self.stl_t])
                        yield
                    self.tt("pool", hs[0][:, 0:T], hs[0][:, 0:T], hs[1][:, 0:T], ALU.add, [hs[0], hs[1]], [hs[0]])
                    self.tt("dve", yb[:, 0:T], hs[0][:, 0:T], gt[:, 0:T], ALU.mult, [hs[0], gt], [yb])
                    self.dma(self.y_scr[:, 4 + j, g0:g0 + T], yb[:, 0:T], [yb], [self.y_scr])
            self.dma(self.stl_o[:], self.stl_t[:], [self.stl_t], [self.stl_o])

    def phaseB_chain(self, st):
        if True:
            NB = 3
            def ring(name, dt=BF16):
                return [self.sb2(st, "%s%d" % (name, i), [128, 512], dt) for i in range(NB)]
            art, rrt, ttt, akt, mrbt, mrkt, bh, kh, vt = [ring(n) for n in
                                                          ("c_art", "c_rrt", "c_ttt", "c_akt", "c_mrbt", "c_mrkt", "c_bh", "c_kh", "c_vt")]
            pend = ring("c_pend", F32)
            Hf = self.sb2(st, "Hf", [128, 512], F32)
            Hb = self.sb2(st, "Hb", [128, 512], BF16)
            Zs = self.sb2(st, "Zs", [128, 512], BF16)
            Us = self.sb2(st, "Us", [128, 512], BF16)
            Yt = [self.sb2(st, "Yt%d" % i, [128, 512], F32) for i in range(2)]
            tmpH = self.sb2(st, "tmpH", [128, 512], F32)
            hs_ = lambda h: slice(h * 64, (h + 1) * 64)
            steps = []
            for (seq, cbase, n) in ((0, 0, 32), (1, 32, 4), (2, 36, 4)):
                for i in range(n):
                    steps.append((seq, cbase, n, i))

            def loads(k):
                seq, cbase, n, i = steps[k]
                r = k % NB
                for d in range(2):
                    sl = slice(d * 64, (d + 1) * 64)
                    c = cbase + i if d == 0 else cbase + n - 1 - i
                    for (tl, scr) in ((art, self.art_scr), (rrt, self.rrt_scr), (ttt, self.ttt_scr), (akt, self.akt_scr),
                                      (mrbt, self.mrbt_scr), (mrkt, self.mrkt_scr), (bh, self.bh_scr), (kh, self.kh_scr),
                                      (pend, self.pend_scr)):
                        self.dma(tl[r][d][sl, :], scr[c, sl, :], [scr], [tl[r][d]], eng="sp")
                    self.dma(vt[r][d][sl, :], self.vt_scr[c], [self.vt_scr], [vt[r][d]], eng="sp")

            loads(0)
            for k in range(len(steps)):
                seq, cbase, n, i = steps[k]
                step = k + 1
                r = k % NB
                if k + 1 < len(steps):
                    loads(k + 1)
                if i == 0:
                    for d in range(2):
                        sl = slice(d * 64, (d + 1) * 64)
                        if seq == 0:
                            self.dma(Hf[d][sl, :], self.h0r[sl, :], [], [Hf[d]], eng="act")
                        else:
                            self.memset("dve", Hf[d][sl, :], 0.0, [Hf[d]])
                        self.cp("dve", Hb[d][sl, :], Hf[d][sl, :], [Hf[d]], [Hb[d]])
                if True:
                    ctx = []
                    for d in range(2):
                        sl = slice(d * 64, (d + 1) * 64)
                        c = cbase + i if d == 0 else cbase + n - 1 - i
                        ops = tuple(x[r][d] for x in (art, rrt, ttt, akt, mrbt, mrkt, bh, kh, vt, pend))
                        ctx.append((d, sl, c, ops))
                    pZs = {}
                    for (d, sl, c, (A_, R_, T_, AK_, MRB_, MRK_, B_, K_, V_, PE_)) in ctx:
                        H_, Z_ = Hb[d], Zs[d]
                        pZ = self.ps()
                        for h in range(8):
                            self.mm(pZ[sl, hs_(h)], A_[sl, hs_(h)], H_[sl, hs_(h)], True, False, [A_, H_], [pZ])
                            self.mm(pZ[sl, hs_(h)], AK_[sl, hs_(h)], V_[sl, hs_(h)], False, True, [AK_, V_], [pZ])
                        pZs[d] = pZ
                    pYs = {}
                    for (d, sl, c, (A_, R_, T_, AK_, MRB_, MRK_, B_, K_, V_, PE_)) in ctx:
                        H_ = Hb[d]
                        pY = self.ps()
                        pYs[d] = pY
                    for (d, sl, c, ops) in ctx:
                        self.cp("act", Zs[d][sl, :], pZs[d][sl, :], [pZs[d]], [Zs[d]])
                    pUs = {}
                    for (d, sl, c, (A_, R_, T_, AK_, MRB_, MRK_, B_, K_, V_, PE_)) in ctx:
                        Z_ = Zs[d]
                        pU = self.ps()
                        for h in range(8):
                            self.mm(pU[sl, hs_(h)], T_[sl, hs_(h)], Z_[sl, hs_(h)], True, True, [T_, Z_], [pU])
                        pUs[d] = pU
                    for (d, sl, c, ops) in ctx:
                        self.cp("act", Us[d][sl, :], pUs[d][sl, :], [pUs[d]], [Us[d]])
                    pHs = {}
                    for (d, sl, c, (A_, R_, T_, AK_, MRB_, MRK_, B_, K_, V_, PE_)) in ctx:
                        U_ = Us[d]
                        self.tt("dve", tmpH[d][sl, :], Hf[d][sl, :], PE_[sl, :], ALU.mult, [Hf[d], PE_], [tmpH[d]])
                        pH = self.ps()
                        for h in range(8):
                            self.mm(pH[sl, hs_(h)], B_[sl, hs_(h)], U_[sl, hs_(h)], True, False, [B_, U_], [pH])
                            self.mm(pH[sl, hs_(h)], K_[sl, hs_(h)], V_[sl, hs_(h)], False, True, [K_, V_], [pH])
                        pHs[d] = pH
                    for (d, sl, c, (A_, R_, T_, AK_, MRB_, MRK_, B_, K_, V_, PE_)) in ctx:
                        H_, U_ = Hb[d], Us[d]
                        pY = pYs[d]
                        for h in range(8):
                            self.mm(pY[sl, hs_(h)], R_[sl, hs_(h)], H_[sl, hs_(h)], True, False, [R_, H_], [pY])
                            self.mm(pY[sl, hs_(h)], MRB_[sl, hs_(h)], U_[sl, hs_(h)], False, False, [MRB_, U_], [pY])
                            self.mm(pY[sl, hs_(h)], MRK_[sl, hs_(h)], V_[sl, hs_(h)], False, True, [MRK_, V_], [pY])
                    for (d, sl, c, ops) in ctx:
                        self.tt("dve", Hf[d][sl, :], tmpH[d][sl, :], pHs[d][sl, :], ALU.add, [tmpH[d], pHs[d]], [Hf[d]])
                        self.cp("dve", Hb[d][sl, :], Hf[d][sl, :], [Hf[d]], [Hb[d]])
                    for (d, sl, c, ops) in ctx:
                        y = Yt[step % 2][d]
                        self.cp("act", y[sl, :], pYs[d][sl, :], [pYs[d]], [y])
                        self.dma(self.ytok_scr[d, c * 64:(c + 1) * 64, :], y[sl, :], [y], [self.ytok_scr], eng="act")
                if seq > 0 and i == n - 1:
                    self.dma(self.str_o[seq - 1], Hf[0][:], [Hf[0], Hf[1]], [self.str_o], eng="act")
                yield [(0, cbase + i), (1, cbase + n - 1 - i)]

    def phaseB_post(self, st):
        prm, cst = self.prm_t, self.cst_t
        ident = self.ident
        if True:
            yf = [self.sb(st, "yf%d" % i, [128, 512], F32) for i in range(2)]
            yb2 = [self.sb(st, "yb2%d" % i, [128, 512], F32) for i in range(2)]
            cen = self.sb(st, "cen", [128, 8, 64], F32)
            sqv = self.sb(st, "sqv", [128, 8, 64], F32)
            mean = self.sb(st, "mean", [128, 8], F32)
            var = self.sb(st, "var", [128, 8], F32)
            gl = [self.sb(st, "gl%d" % i, [128, 4, 128], BF16) for i in range(2)]
            bl = [self.sb(st, "bl%d" % i, [128, 4, 128], BF16) for i in range(2)]
            ynT = self.sb(st, "ynT", [128, 4, 128], F32)
            yo = [self.sb(st, "yo%d" % i, [128, 4, 128], BF16) for i in range(2)]
            it = -1
            blk = yield
            while True:
                it += 1
                g0 = blk * 128
                a, b = yf[it % 2], yb2[it % 2]
                g_, b_ = gl[it % 2], bl[it % 2]
                o = yo[it % 2]
                self.dma(a[:], self.ytok_scr[0, g0:g0 + 128, :], [self.ytok_scr], [a])
                self.dma(b[:], self.ytok_scr[1, g0:g0 + 128, :], [self.ytok_scr], [b], eng="act")
                self.dma(g_[:], self.g_scr[:, :, g0:g0 + 128], [self.g_scr], [g_])
                self.dma(b_[:], self.bon_scr[:, :, g0:g0 + 128], [self.bon_scr], [b_], eng="act")
                a3 = a[:].rearrange("p (h v) -> p h v", v=64)
                self.tt("pool", a[:], a[:], b[:], ALU.add, [a, b], [a])
                self.S.op("dve", lambda e, a3=a3: e.tensor_reduce(out=mean[:], in_=a3, op=ALU.add, axis=mybir.AxisListType.X),
                          [a.b], [mean.b])
                self.tsc("dve", mean[:], mean[:], 1.0 / 64, ALU.mult, [mean], [mean])
                self.tt("dve", cen[:], a3, mean[:].unsqueeze(2).to_broadcast([128, 8, 64]), ALU.subtract, [a, mean], [cen])
                self.tt("pool", sqv[:], cen[:], cen[:], ALU.mult, [cen], [sqv])
                self.S.op("dve", lambda e: e.tensor_reduce(out=var[:], in_=sqv[:], op=ALU.add, axis=mybir.AxisListType.X),
                          [sqv.b], [var.b])
                self.act(var[:], var[:], AF.Sqrt, [var, self.epsT], [var], scale=1.0 / 64, bias=self.epsT[:, 1:2])
                self.S.op("dve", lambda e: e.reciprocal(out=var[:], in_=var[:]), [var.b], [var.b])
                self.tt("dve", cen[:], cen[:], var[:].unsqueeze(2).to_broadcast([128, 8, 64]), ALU.mult, [cen, var], [cen])
                p = self.ps()
                cen2 = cen[:].rearrange("p h v -> p (h v)")
                for j in range(4):
                    self.tr(p[:, j * 128:(j + 1) * 128], cen2[:, j * 128:(j + 1) * 128], ident[:], [cen, ident], [p])
                for j in range(4):
                    self.act(ynT[:, j, :], p[:, j * 128:(j + 1) * 128], AF.Identity, [p, prm], [ynT],
                             scale=prm[:, P_LNG + j:P_LNG + j + 1], bias=prm[:, P_LNB + j:P_LNB + j + 1])
                self.tt("pool", ynT[:], ynT[:], b_[:], ALU.add, [ynT, b_], [ynT])
                self.tt("dve", o[:], ynT[:], g_[:], ALU.mult, [ynT, g_], [o])
                self.dma(self.y_scr[:, 0:4, g0:g0 + 128], o[:], [o], [self.y_scr])
                blk = yield

    def phaseC(self):
        prm, cst = self.prm_t, self.cst_t
        ident, ones_bf = self.ident, self.ones_bf
        with contextlib.ExitStack() as st:
            pass
        import os
        kcc = int(os.environ.get("KCC", "99"))
        if kcc == 0:
            return
        with contextlib.ExitStack() as st:
            wout = self.sb(st, "wout", [128, 8, D], BF16)
            wsrc = self.w_out[:].rearrange("(kc p) n -> p kc n", p=128)
            with contextlib.ExitStack() as st2:
                stg = [self.sb(st2, "stgD%d" % i, [128, 8, 256], F32) for i in range(2)]
                for cb in range(4):
                    s = stg[cb % 2]
                    self.dma(s[:], wsrc[:, :, cb * 256:(cb + 1) * 256], [], [s])
                    self.cp("act", wout[:, :, cb * 256:(cb + 1) * 256], s[:], [s], [wout])
            self.S.barrier()
            yTs = [self.sb(st, "yT_c%d" % i, [128, 8, TC], BF16) for i in range(2)]
            oT = self.sb(st, "oT", [128, 8, TC], F32)
            sq = self.sb(st, "sq_c", [128, 8, TC], BF16)
            xT = self.sb(st, "xT_c", [128, 8, TC], F32)
            h2 = self.sb(st, "h2", [128, 8, TC], BF16)
            f = self.sb(st, "f_c", [128, 32, TC], BF16)
            otok = self.sb(st, "otok", [128, 4, D], F32)
            rstd = self.sb(st, "rstd_c", [128, TC], F32)
            tmp = [self.sb(st, "tmpC%d" % i, [128, TC], F32) for i in range(2)]
            NW = 4
            w1r = [self.sb(st, "w1r%d" % i, [128, 8, 512], BF16) for i in range(NW)]
            w2r = [self.sb(st, "w2r%d" % i, [128, 32, 128], BF16) for i in range(NW)]
            ntile = min(NTOK // TC, kcc)
            wseq = []
            for ti_ in range(ntile):
                wseq += [("w1", b_) for b_ in range(8)] + [("w2", b_) for b_ in range(8)]
            wstate = {"issued": 0, "w1": 0, "w2": 0}
            wbuf = {}

            def issue_upto(n):
                while wstate["issued"] < min(n, len(wseq)):
                    k_ = wstate["issued"]
                    kind, b_ = wseq[k_]
                    ring = w1r if kind == "w1" else w2r
                    buf = ring[wstate[kind] % NW]
                    wstate[kind] += 1
                    scr = self.w1_scr if kind == "w1" else self.w2_scr
                    self.dma(buf[:], scr[b_], [scr], [buf], eng="sp" if k_ % 2 == 0 else "act")
                    wbuf[k_] = buf
                    wstate["issued"] += 1

            def rms(src, R):
                p = self.ps()
                for j in range(8):
                    self.mm(p[:], ones_bf[:], sq[:, j, :], j == 0, j == 7, [ones_bf, sq], [p])
                self.act(rstd[:], p[:], AF.Sqrt, [p, self.epsT], [rstd], scale=1.0 / D, bias=self.epsT[:, 0:1])
                self.S.op("dve", lambda e: e.reciprocal(out=rstd[:], in_=rstd[:]), [rstd.b], [rstd.b])

            def resid(gg, mc):
                for j in range(8):
                    t = tmp[j % 2]
                    self.tt("dve", t[:], oT[:, j, :], rstd[:], ALU.mult, [oT, rstd], [t])
                    self.stt(xT[:, j, :], t[:], gg[:, j, mc:mc + 1], xT[:, j, :], ALU.mult, ALU.add, [t, gg, xT], [xT])

            wi = 0
            for ti in range(NTOK // TC):
                if ti >= kcc:
                    break
                g0 = ti * TC
                mc = 0 if g0 < TS else 1
                yT = yTs[ti % 2]
                if ti == 0:
                    self.dma(yT[:], self.y_scr[:, :, g0:g0 + TC], [self.y_scr], [yT])
                self.dma(xT[:], self.xT_scr[:, :, g0:g0 + TC], [self.xT_scr], [xT], eng="act")
                if ti + 1 < ntile:
                    self.dma(yTs[(ti + 1) % 2][:], self.y_scr[:, :, g0 + TC:g0 + 2 * TC], [self.y_scr], [yTs[(ti + 1) % 2]])
                issue_upto(ti * 16 + 4)
                for m in range(8):
                    p = self.ps()
                    for kc in range(8):
                        self.mm(p[:], wout[:, kc, m * 128:(m + 1) * 128], yT[:, kc, :], kc == 0, kc == 7, [wout, yT], [p])
                    self.cp("act", oT[:, m, :], p[:], [p], [oT])
                    self.tt("pool", sq[:, m, :], oT[:, m, :], oT[:, m, :], ALU.mult, [oT], [sq])
                rms(oT, None)
                resid(self.gg1, mc)
                for j in range(8):
                    self.tt("pool", sq[:, j, :], xT[:, j, :], xT[:, j, :], ALU.mult, [xT], [sq])
                rms(xT, None)
                for j in range(8):
                    t = tmp[j % 2]
                    self.tt("dve", t[:], xT[:, j, :], rstd[:], ALU.mult, [xT, rstd], [t])
                    self.act(h2[:, j, :], t[:], AF.Identity, [t, self.gs2, self.modT], [h2],
                             scale=self.gs2[:, j, mc:mc + 1], bias=self.modT[:, 24 + j, mc:mc + 1])
                for blk in range(8):
                    issue_upto(ti * 16 + blk + 4)
                    w = wbuf[ti * 16 + blk]
                    for c4 in range(4):
                        fc = blk * 4 + c4
                        p = self.ps()
                        for kc in range(8):
                            self.mm(p[:], w[:, kc, c4 * 128:(c4 + 1) * 128], h2[:, kc, :], kc == 0, kc == 7, [w, h2], [p])
                        t = tmp[fc % 2]
                        self.act(t[:], p[:], AF.Relu, [p], [t])
                        self.tt("pool" if fc % 2 == 0 else "dve", f[:, fc, :], t[:], t[:], ALU.mult, [t], [f])
                for m in range(8):
                    issue_upto(ti * 16 + 8 + m + 4)
                    w = wbuf[ti * 16 + 8 + m]
                    p = self.ps()
                    for fc in range(32):
                        self.mm(p[:], w[:, fc, :], f[:, fc, :], fc == 0, fc == 31, [w, f], [p])
                    self.cp("act", oT[:, m, :], p[:], [p], [oT])
                    self.tt("pool", sq[:, m, :], oT[:, m, :], oT[:, m, :], ALU.mult, [oT], [sq])
                rms(oT, None)
                resid(self.gg2, mc)
                for s in range(4):
                    for half in range(2):
                        p = self.ps()
                        for jj in range(4):
                            j = half * 4 + jj
                            self.tr(p[:, jj * 128:(jj + 1) * 128], xT[:, j, s * 128:(s + 1) * 128], ident[:], [xT, ident], [p])
                        self.cp("act" if half == 0 else "dve", otok[:, s, half * 512:(half + 1) * 512], p[:], [p], [otok])
                if g0 < TS:
                    dst = self.ys[g0:g0 + TC, :].rearrange("(s p) f -> p s f", p=128)
                    self.dma(dst, otok[:], [otok], [self.ys])
                else:
                    dst = self.yp[:, :].rearrange("(s p) f -> p s f", p=128)
                    self.dma(dst, otok[:], [otok], [self.yp])


def _fm(v):
    v = np.asarray(v, np.float32).reshape(-1, 128)
    return np.ascontiguousarray(v.T)


def _pos_embed():
    def sincos(pos, dim):
        omega = (1.0 / (10000.0 ** (np.arange(dim // 2, dtype=np.float32) / np.float32(dim // 2)))).astype(np.float32)
        ang = pos.astype(np.float32)[:, None] * omega[None, :]
        return np.concatenate([np.sin(ang), np.cos(ang)], axis=-1).astype(np.float32)
    rows = TS // 64
    half = D // 2
    e_row = sincos(np.arange(rows), half)
    e_col = sincos(np.arange(64), half)
    emb = np.concatenate([np.broadcast_to(e_row[:, None, :], (rows, 64, half)),
                          np.broadcast_to(e_col[None, :, :], (rows, 64, half))], axis=-1)
    return np.ascontiguousarray(emb.reshape(rows * 64, D).astype(np.float32))


def _consts():
    c = np.zeros((128, NCST), np.float32)
    c[:, C_ID:C_ID + 128] = np.eye(128, dtype=np.float32)
    ob = np.zeros((128, 128), np.float32)
    ob[:64, :64] = 1.0
    ob[64:, 64:] = 1.0
    c[:, C_OB:C_OB + 128] = ob
    s = np.arange(64)[:, None]
    t = np.arange(64)[None, :]
    msi = np.zeros((128, 2, 64), np.float32)
    msi[:64, 0] = (s < t)
    msi[:64, 1] = (s <= t)
    msi[64:, 0] = (s > t)
    msi[64:, 1] = (s >= t)
    c[:, C_MSI:C_MSI + 128] = msi.reshape(128, 128)
    ml = np.zeros((128, 64), np.float32)
    ml[:64] = (t < s)
    ml[64:] = (t > s)
    c[:, C_ML:C_ML + 64] = ml
    ids = np.zeros((128, 64), np.float32)
    ids[:64] = np.eye(64)
    ids[64:] = np.eye(64)
    c[:, C_IDS:C_IDS + 64] = ids
    c[:64, C_MSI1:C_MSI1 + 128] = msi[64:].reshape(64, 128)
    c[:64, C_ML1:C_ML1 + 64] = ml[64:]
    tt_ = np.arange(TT)
    c[:, C_RMF:C_RMF + TT] = (tt_ % 64 != 0).astype(np.float32)[None, :]
    c[:, C_RMB:C_RMB + TT] = (tt_ % 64 != 63).astype(np.float32)[None, :]
    return c


_NC_CACHE = {}


def kernel(x_prompt, x_sample, c, state_rwkv, state_lru, c_ctx, w_mod, b_mod,
           g_pre_mix, g_post_mix, g_pre_mlp, g_post_mlp, w_in,
           rwkv_w0, rwkv_w_up, rwkv_a0, rwkv_a_up, rwkv_g_up, rwkv_k_k, rwkv_k_a, rwkv_r_k,
           rwkv_lnx_g, rwkv_lnx_b, lru_conv_w, lru_conv_b, lru_wa, lru_ba, lru_wx, lru_bx,
           lru_lambda, w_out, w_mlp1, w_mlp2, _debug=False):
    f = lambda a: np.ascontiguousarray(np.asarray(a, np.float32))
    x_prompt, x_sample, c, state_rwkv, state_lru, c_ctx = map(f, (x_prompt, x_sample, c, state_rwkv, state_lru, c_ctx))
    if "nc" not in _NC_CACHE:
        _NC_CACHE["nc"] = K(debug=_debug).build()
    nc = _NC_CACHE["nc"]
    pe = _pos_embed()
    cst = _consts()
    shared = {
        "pe": pe, "cst": cst,
        "w_mod": f(w_mod[0]), "w_in": f(w_in[0]), "w_out": f(w_out[0]), "w1": f(w_mlp1[0]), "w2": f(w_mlp2[0]),
        "wup": f(rwkv_w_up[0]).reshape(128, 512), "aup": f(rwkv_a_up[0]).reshape(128, 512), "gup": f(rwkv_g_up[0]),
        "lwa": f(lru_wa[0]), "lwx": f(lru_wx[0]),
    }
    prm0 = np.zeros((128, NPRM), np.float32)
    prm0[:, P_GPRE:P_GPRE + 8] = _fm(g_pre_mix[0])
    prm0[:, P_GPOST:P_GPOST + 8] = _fm(g_post_mix[0])
    prm0[:, P_GPRE2:P_GPRE2 + 8] = _fm(g_pre_mlp[0])
    prm0[:, P_GPOST2:P_GPOST2 + 8] = _fm(g_post_mlp[0])
    prm0[:, P_BMOD:P_BMOD + 48] = _fm(b_mod[0])
    for d in range(2):
        prm0[:, P_W0 + 4 * d:P_W0 + 4 * d + 4] = _fm(rwkv_w0[0, d])
        prm0[:, P_A0 + 4 * d:P_A0 + 4 * d + 4] = _fm(rwkv_a0[0, d])
        prm0[:, P_BA + 4 * d:P_BA + 4 * d + 4] = _fm(lru_ba[0, d])
        prm0[:, P_BX + 4 * d:P_BX + 4 * d + 4] = _fm(lru_bx[0, d])
        prm0[:, P_LAM + 4 * d:P_LAM + 4 * d + 4] = _fm(lru_lambda[0, d])
    prm0[:, P_KK:P_KK + 4] = _fm(rwkv_k_k[0])
    prm0[:, P_KA:P_KA + 4] = _fm(rwkv_k_a[0])
    prm0[:, P_RK:P_RK + 4] = _fm(np.asarray(rwkv_r_k[0]).reshape(-1))
    prm0[:, P_LNG:P_LNG + 4] = _fm(rwkv_lnx_g[0])
    prm0[:, P_LNB:P_LNB + 4] = _fm(rwkv_lnx_b[0])
    for i in range(4):
        prm0[:, P_CW + 4 * i:P_CW + 4 * i + 4] = _fm(lru_conv_w[0, i])
    prm0[:, P_CB:P_CB + 4] = _fm(lru_conv_b[0])
    in_maps = []
    for i in range(8):
        prm = prm0.copy()
        for d in range(2):
            prm[:, P_H0 + 4 * d:P_H0 + 4 * d + 4] = _fm(state_lru[i, 0, d])
        cT = np.zeros((128, 8, 2), np.float32)
        cT[:, :, 0] = _fm(c[i])
        cT[:, :, 1] = _fm(c_ctx)
        h0 = np.ascontiguousarray(state_rwkv[i, 0].transpose(0, 3, 1, 2)).reshape(128, 512)
        m = dict(shared)
        m.update({"xs": x_sample[i], "xp": np.ascontiguousarray(x_prompt[2 * i:2 * i + 2].reshape(2 * TP, D)),
                  "cT": cT.reshape(128, 16), "h0r": h0, "prm": prm})
        in_maps.append(m)
    res = run_bass_kernel_spmd(nc, in_maps, core_ids=list(range(8)))
    R = res.results
    y_prompt = np.zeros((16, TP, D), np.float32)
    y_sample = np.zeros((8, TS, D), np.float32)
    st_r = np.zeros((16, 1, 2, 8, 64, 64), np.float32)
    st_l = np.zeros((16, 1, 2, 512), np.float32)
    for i in range(8):
        r = R[i]
        y_sample[i] = r["ys"]
        y_prompt[2 * i:2 * i + 2] = r["yp"].reshape(2, TP, D)
        so = r["str_o"].reshape(2, 2, 64, 8, 64)
        st_r[2 * i:2 * i + 2, 0] = so.transpose(0, 1, 3, 4, 2)
        sl = r["stl_o"].reshape(128, 4, 2, 2)
        st_l[2 * i:2 * i + 2, 0] = sl.transpose(2, 3, 1, 0).reshape(2, 2, 512)
    if _debug:
        return (y_prompt, y_sample, st_r, st_l), R
    return (y_prompt, y_sample, st_r, st_l)
```

```python
import contextlib
import numpy as np
import concourse.bass as bass
import concourse.mybir as mybir
from concourse.bass_utils import run_bass_kernel_spmd

F32 = mybir.dt.float32
BF16 = mybir.dt.bfloat16
F32R = mybir.dt.float32r
AF = mybir.ActivationFunctionType
ALU = mybir.AluOpType

D = 1024
TS = 2048
TP = 256
NTOK = TS + 2 * TP
NCH = NTOK // 64
DIN = 2944
DFF = 4096
LAM = float(np.exp(-0.5))
EPS = 1e-6
LNX_EPS = 64e-5
TT = 256
TC = 512
GELU_C = 1.5957691216057308

P_GPRE, P_GPOST, P_GPRE2, P_GPOST2 = 0, 8, 16, 24
P_BMOD = 32
P_W0, P_A0 = 80, 88
P_KK, P_KA, P_RK, P_LNG, P_LNB = 96, 100, 104, 108, 112
P_CW, P_CB = 116, 132
P_BA, P_BX, P_LAM, P_H0 = 136, 144, 152, 160
NPRM = 168
C_ID, C_OB, C_MSI, C_ML, C_IDS, C_RMF, C_RMB = 0, 128, 256, 384, 448, 512, 768
C_MSI1, C_ML1 = 1024, 1152
NCST = 1216


class Buf:
    __slots__ = ("name", "lw", "rd", "excl", "multi", "ws")

    def __init__(self, name=""):
        self.name = name
        self.lw = None
        self.rd = {}
        self.excl = False
        self.multi = False
        self.ws = {}


class TL:
    def __init__(self, t, name=""):
        self.t = t
        self.b = Buf(name)

    def __getitem__(self, k):
        return self.t[k]


class Sched:
    ENGS = ("pe", "act", "dve", "pool", "sp")

    def __init__(self, nc):
        self.nc = nc
        self.streams = {e: [] for e in self.ENGS}
        self.cnt = {}
        self.waited = {e: {} for e in self.ENGS}
        self.n_ops = 0
        self.dma_n = {e: 0 for e in self.ENGS}
        self.NSLOT = {"sp": 44, "act": 44, "pool": 4, "dve": 2, "pe": 2}

    def _deps(self, eng, reads, writes):
        need = {}
        for b in reads:
            if b.multi:
                for s, v in b.ws.items():
                    if need.get(s, 0) < v:
                        need[s] = v
                continue
            if b.lw is not None:
                s, v = b.lw
                if need.get(s, 0) < v:
                    need[s] = v
            if b.excl:
                for s, v in b.rd.items():
                    if s != eng and need.get(s, 0) < v:
                        need[s] = v
        for b in writes:
            if b.multi:
                continue
            if b.lw is not None:
                s, v = b.lw
                if need.get(s, 0) < v:
                    need[s] = v
            for s, v in b.rd.items():
                if need.get(s, 0) < v:
                    need[s] = v
        out = []
        w = self.waited[eng]
        for s, v in need.items():
            if s == "pe" and eng == "pe":
                continue
            if w.get(s, 0) >= v:
                continue
            w[s] = v
            out.append((s, v))
        return out

    def op(self, eng, fn, reads=(), writes=(), dma=False):
        reads = [r.b if isinstance(r, TL) else r for r in reads]
        writes = [r.b if isinstance(r, TL) else r for r in writes]
        waits = self._deps(eng, reads, writes)
        if dma:
            slot = self.dma_n[eng] % self.NSLOT[eng]
            self.dma_n[eng] += 1
            sem = "%s_d%d" % (eng, slot)
            prev = self.cnt.get(sem, 0)
            if prev > 0 and self.waited[eng].get(sem, 0) < prev:
                self.waited[eng][sem] = prev
                waits.append((sem, prev))
        else:
            sem = eng
        inc = 16 if dma else 1
        self.cnt[sem] = self.cnt.get(sem, 0) + inc
        val = self.cnt[sem]
        self.streams[eng].append((waits, fn, sem, inc))
        self.n_ops += 1
        for b in reads:
            if b.rd.get(sem, 0) < val:
                b.rd[sem] = val
        for b in writes:
            if b.multi:
                if b.ws.get(sem, 0) < val:
                    b.ws[sem] = val
                continue
            b.lw = (sem, val)
            b.rd = {}
        return val

    def barrier_on(self, tl):
        if tl.b.lw is None:
            return
        sname, v = tl.b.lw
        for e in ("sp", "act", "pool"):
            if self.waited[e].get(sname, 0) < v:
                self.waited[e][sname] = v
                self.streams[e].append(([(sname, v)], None, None, 0))

    def barrier(self):
        snap = dict(self.cnt)
        for e in self.ENGS:
            waits = []
            for s, v in snap.items():
                if s == "pe" and e == "pe":
                    continue
                if self.waited[e].get(s, 0) < v:
                    self.waited[e][s] = v
                    waits.append((s, v))
            if waits:
                self.streams[e].append((waits, None, None, 0))

    def emit(self):
        nc = self.nc
        sems = {}
        with contextlib.ExitStack() as st:
            for s in self.cnt:
                sems[s] = st.enter_context(nc.semaphore(s))
            block = st.enter_context(nc.Block())
            engmap = {"pe": block.tensor, "act": block.scalar, "dve": block.vector,
                      "pool": block.gpsimd, "sp": block.sync}
            for e in self.ENGS:
                stream = self.streams[e]
                if not stream:
                    continue

                def body(eng, stream=stream):
                    for waits, fn, sem, inc in stream:
                        for s, v in waits:
                            eng.wait_ge(sems[s], v)
                        if fn is not None:
                            fn(eng).then_inc(sems[sem], inc)
                engmap[e](body)


class K:
    def __init__(self, debug=False, stop_after=None):
        self.debug = debug
        self.stop_after = stop_after
        import os
        self.cutk = int(os.environ.get("KCUT", "0"))
        self.cutm = int(os.environ.get("KCUTM", "99"))
        self.ktiles = int(os.environ.get("KTILES", "99"))
        self.kskip = os.environ.get("KSKIP", "").split(",")
        self.nc = bass.Bass("TRN2", target_bir_lowering=False)
        self.S = Sched(self.nc)
        self.es = contextlib.ExitStack()
        self.psr = 0
        self.rr = {}

    def dram(self, name, shape, dt, kind="Internal"):
        t = TL(self.nc.dram_tensor(name, list(shape), dt, kind=kind).ap(), name)
        t.b.multi = True
        return t

    def sb(self, st, name, shape, dt):
        return TL(st.enter_context(self.nc.sbuf_tensor(name, list(shape), dt)), name)

    def sb2(self, st, name, shape, dt):
        t = st.enter_context(self.nc.sbuf_tensor(name, list(shape), dt))
        return [TL(t, name + "_lo"), TL(t, name + "_hi")]

    def ps(self):
        p = self.psum[self.psr % 8]
        self.psr += 1
        return p

    def mm(self, out, lhsT, rhs, start, stop, R, W):
        self.S.op("pe", lambda e: e.matmul(out, lhsT=lhsT, rhs=rhs, start=start, stop=stop), R, W)

    def tr(self, out, in_, ident, R, W):
        self.S.op("pe", lambda e: e.transpose(out, in_, ident), R, W)

    def act(self, out, in_, func, R, W, scale=1.0, bias=None, eng="act"):
        if bias is None:
            self.S.op("act", lambda e: e.activation(out=out, in_=in_, func=func, scale=scale), R, W)
        else:
            self.S.op("act", lambda e: e.activation(out=out, in_=in_, func=func, scale=scale, bias=bias), R, W)

    def tt(self, eng, out, in0, in1, op, R, W):
        self.S.op(eng, lambda e: e.tensor_tensor(out=out, in0=in0, in1=in1, op=op), R, W)

    def tsc(self, eng, out, in0, s1, op0, R, W, s2=None, op1=None):
        if op1 is None:
            self.S.op(eng, lambda e: e.tensor_scalar(out=out, in0=in0, scalar1=s1, scalar2=None, op0=op0), R, W)
        else:
            self.S.op(eng, lambda e: e.tensor_scalar(out=out, in0=in0, scalar1=s1, scalar2=s2, op0=op0, op1=op1), R, W)

    def stt(self, out, in0, scalar, in1, op0, op1, R, W):
        self.S.op("dve", lambda e: e.scalar_tensor_tensor(out=out, in0=in0, scalar=scalar, in1=in1, op0=op0, op1=op1), R, W)

    def cp(self, eng, out, in_, R, W):
        if eng == "act":
            self.S.op("act", lambda e: e.activation(out=out, in_=in_, func=AF.Copy), R, W)
        else:
            self.S.op(eng, lambda e: e.tensor_copy(out=out, in_=in_), R, W)

    def scan(self, out, d0, d1, init, R, W):
        self.S.op("dve", lambda e: e.tensor_tensor_scan(out=out, data0=d0, data1=d1, initial=init,
                                                        op0=ALU.mult, op1=ALU.add), R, W)

    def dma(self, out, in_, R, W, eng="sp"):
        self.S.op(eng, lambda e: e.dma_start(out=out, in_=in_), R, W, dma=True)

    def memset(self, eng, ap, val, W):
        self.S.op(eng, lambda e: e.memset(ap, val), (), W)

    def pick(self, key, engs):
        i = self.rr.get(key, 0)
        self.rr[key] = i + 1
        return engs[i % len(engs)]

    def build(self):
        nc = self.nc
        I = lambda n, s, dt=F32: self.dram(n, s, dt, "ExternalInput")
        O = lambda n, s, dt=F32: self.dram(n, s, dt, "ExternalOutput")
        self.xs = I("xs", [TS, D])
        self.xp = I("xp", [2 * TP, D])
        self.pe = I("pe", [TS, D])
        self.cT = I("cT", [128, 16])
        self.h0r = I("h0r", [128, 512])
        self.prm = I("prm", [128, NPRM])
        self.cst = I("cst", [128, NCST])
        self.w_mod = I("w_mod", [D, 6 * D])
        self.w_in = I("w_in", [D, DIN])
        self.w_out = I("w_out", [D, D])
        self.w1 = I("w1", [D, DFF])
        self.w2 = I("w2", [DFF, D])
        self.wup = I("wup", [128, 512])
        self.aup = I("aup", [128, 512])
        self.gup = I("gup", [128, 512])
        self.lwa = I("lwa", [2, 8, 64, 64])
        self.lwx = I("lwx", [2, 8, 64, 64])
        self.ys = O("ys", [TS, D])
        self.yp = O("yp", [2 * TP, D])
        self.str_o = O("str_o", [2, 128, 512])
        self.stl_o = O("stl_o", [128, 16])
        self.xT_scr = self.dram("xT_scr", [128, 8, NTOK], F32)
        self.xb_scr = self.dram("xb_scr", [128, 4, NTOK], F32)
        self.gate_scr = self.dram("gate_scr", [128, 4, NTOK], BF16)
        self.g_scr = self.dram("g_scr", [128, 4, NTOK], BF16)
        self.bon_scr = self.dram("bon_scr", [128, 4, NTOK], BF16)
        self.y_scr = self.dram("y_scr", [128, 8, NTOK], BF16)
        self.ytok_scr = self.dram("ytok_scr", [2, NTOK, 512], F32)
        for n in ("art", "rrt", "ttt", "akt", "mrbt", "mrkt", "bh", "kh"):
            setattr(self, n + "_scr", self.dram(n + "_scr", [NCH, 128, 512], BF16))
        self.vt_scr = self.dram("vt_scr", [NCH, 64, 512], BF16)
        self.pend_scr = self.dram("pend_scr", [NCH, 128, 512], F32)
        self.w1_scr = self.dram("w1_scr", [8, 128, 8, 512], BF16)
        self.w2_scr = self.dram("w2_scr", [8, 128, 32, 128], BF16)
        if self.debug:
            self.dbg = {}

        with self.es as st0:
            self.psum = [TL(st0.enter_context(nc.psum_tensor("ps%d" % i, [128, 512], F32)), "ps%d" % i)
                         for i in range(8)]
            for p_ in self.psum:
                p_.b.excl = True
            self.prm_t = self.sb(st0, "prm_t", [128, NPRM], F32)
            self.cst_t = self.sb(st0, "cst_t", [128, NCST], F32)
            self.modT = self.sb(st0, "modT", [128, 48, 2], F32)
            self.gs1 = self.sb(st0, "gs1", [128, 8, 2], F32)
            self.gs2 = self.sb(st0, "gs2", [128, 8, 2], F32)
            self.gg1 = self.sb(st0, "gg1", [128, 8, 2], F32)
            self.gg2 = self.sb(st0, "gg2", [128, 8, 2], F32)
            self.ident = self.sb(st0, "ident", [128, 128], F32)
            self.ones_bf = self.sb(st0, "ones_bf", [128, 128], BF16)
            self.oblk_bf = self.sb(st0, "oblk_bf", [128, 128], BF16)
            self.epsT = self.sb(st0, "epsT", [128, 2], F32)
            self.misc = self.sb(st0, "misc", [128, 32], F32)
            self.stl_t = self.sb(st0, "stl_t", [128, 16], F32)
            for nm, fn in (("p0", self.phase0), ("pA", self.phaseA), ("pB", self.phaseB), ("pC", self.phaseC)):
                fn()
                self.S.barrier()
                if self.stop_after == nm:
                    break
            self.S.emit()
        return nc

    def dump(self, name, src_ap, shape, dt, R):
        o = self.dram("dbg_" + name, shape, dt, "ExternalOutput")
        self.dma(o[:], src_ap, R, [o])

    def phase0(self):
        nc = self.nc
        prm, cst = self.prm_t, self.cst_t
        self.dma(prm[:], self.prm[:], [], [prm])
        self.dma(cst[:], self.cst[:], [], [cst])
        self.cp("dve", self.ident[:], cst[:, C_ID:C_ID + 128], [cst], [self.ident])
        self.cp("dve", self.oblk_bf[:], cst[:, C_OB:C_OB + 128], [cst], [self.oblk_bf])
        self.memset("dve", self.ones_bf[:], 1.0, [self.ones_bf])
        self.memset("dve", self.epsT[:, 0:1], EPS, [self.epsT])
        self.memset("dve", self.epsT[:, 1:2], LNX_EPS, [self.epsT])
        self.tsc("dve", self.misc[:, 0:4], prm[:, P_KA:P_KA + 4], -1.0, ALU.mult, [prm], [self.misc], 1.0, ALU.add)
        with contextlib.ExitStack() as st:
            scT = self.sb(st, "scT", [128, 16], F32)
            cT = self.sb(st, "cT_t", [128, 16], F32)
            wm = [self.sb(st, "wm%d" % i, [128, 8, 512], F32) for i in range(2)]
            tmp = self.sb(st, "lam_tmp", [128, 8], F32)
            self.dma(cT[:], self.cT[:], [], [cT])
            self.act(scT[:], cT[:], AF.Silu, [cT], [scT])
            self.act(tmp[:], prm[:, P_LAM:P_LAM + 8], AF.Exp, [prm], [tmp], scale=-1.0)
            self.act(tmp[:], tmp[:], AF.Ln, [tmp], [tmp], bias=1.0)
            self.tsc("dve", self.misc[:, 4:12], tmp[:], -8.0, ALU.mult, [tmp], [self.misc])
            self.tsc("dve", self.misc[:, 12:20], tmp[:], -16.0, ALU.mult, [tmp], [self.misc])
            wsrc = self.w_mod[:].rearrange("(kc p) n -> p kc n", p=128)
            sc3 = scT[:].rearrange("p (k c) -> p k c", c=2)
            for blk in range(12):
                w = wm[blk % 2]
                self.dma(w[:], wsrc[:, :, blk * 512:(blk + 1) * 512], [], [w], eng="sp" if blk % 2 == 0 else "act")
                p = self.ps()
                for m in range(4):
                    for kc in range(8):
                        self.mm(p[:, 2 * m:2 * m + 2], w[:, kc, m * 128:(m + 1) * 128], sc3[:, kc, :],
                                kc == 0, kc == 7, [w, scT], [p])
                for m in range(4):
                    mi = blk * 4 + m
                    self.tsc("dve", self.modT[:, mi, :], p[:, 2 * m:2 * m + 2], prm[:, P_BMOD + mi:P_BMOD + mi + 1],
                             ALU.add, [p, prm], [self.modT])
            m3 = self.modT
            for (dst, sc_off, g_off, one) in ((self.gs1, 8, P_GPRE, 1.0), (self.gs2, 32, P_GPRE2, 1.0),
                                              (self.gg1, 16, P_GPOST, 0.0), (self.gg2, 40, P_GPOST2, 0.0)):
                for c in range(2):
                    self.tsc("dve", dst[:, :, c], m3[:, sc_off:sc_off + 8, c], one, ALU.add, [m3], [dst])
                    self.tt("dve", dst[:, :, c], dst[:, :, c], prm[:, g_off:g_off + 8], ALU.mult, [dst, prm], [dst])
            if self.debug:
                self.dump("modT", self.modT[:], [128, 48, 2], F32, [self.modT])
                self.dump("gs1", self.gs1[:], [128, 8, 2], F32, [self.gs1])

    def load_cast(self, st, dst_ap, dst_tl, src_ap, shape, tag):
        key = "stg_" + tag
        if not hasattr(self, key):
            setattr(self, key, [self.sb(st, "%s%d" % (key, i), shape, F32) for i in range(2)])
        ring = getattr(self, key)
        s = ring[self.rr.get(key, 0) % 2]
        self.rr[key] = self.rr.get(key, 0) + 1
        self.dma(s[:], src_ap, [], [s], eng="sp")
        eng = self.pick("castE", ["act", "pool"])
        self.cp(eng, dst_ap, s[:], [s], [dst_tl])

    def phaseA(self):
        nc = self.nc
        prm, cst = self.prm_t, self.cst_t
        with contextlib.ExitStack() as st:
            win = self.sb(st, "win", [128, 8, DIN], BF16)
            win.b.multi = True
            wsrc = self.w_in[:].rearrange("(kc p) n -> p kc n", p=128)
            wup = self.sb(st, "wup_t", [128, 512], BF16)
            aup = self.sb(st, "aup_t", [128, 512], BF16)
            gup = self.sb(st, "gup_t", [128, 512], BF16)
            with contextlib.ExitStack() as st2:
                stgA = [self.sb(st2, "stgA%d" % i, [128, 8, 256], F32) for i in range(2)]
                nb = 0
                for c0 in range(0, DIN, 256):
                    cw = min(256, DIN - c0)
                    s_ = stgA[nb % 2]
                    self.dma(s_[:, :, 0:cw], wsrc[:, :, c0:c0 + cw], [], [s_], eng="sp" if nb % 2 == 0 else "act")
                    self.cp("act" if nb % 2 == 0 else "pool", win[:, :, c0:c0 + cw], s_[:, :, 0:cw], [s_], [win])
                    nb += 1
                for i, (src, dstt) in enumerate(((self.wup, wup), (self.aup, aup), (self.gup, gup))):
                    s_ = stgA[nb % 2]
                    nb += 1
                    s2 = s_[:].rearrange("p a b -> p (a b)")[:, 0:512]
                    self.dma(s2, src[:], [], [s_])
                    self.cp("dve", dstt[:], s2, [s_], [dstt])
            self.S.barrier()
            mSI = cst[:, C_MSI:C_MSI + 128].rearrange("p (q t) -> p q t", q=2)
            mL = cst[:, C_ML:C_ML + 64]
            idS = cst[:, C_IDS:C_IDS + 64]

            xin = self.sb(st, "xin", [128, 2, D], F32)
            xT = self.sb(st, "xT", [128, 8, TT], F32)
            sq = self.sb(st, "sq", [128, 8, TT], BF16)
            hT = self.sb(st, "hT", [128, 8, TT], BF16)
            rstd = self.sb(st, "rstd", [128, TT], F32)
            tmpA = [self.sb(st, "tmpA%d" % i, [128, TT], F32) for i in range(2)]
            rT = self.sb(st, "rT", [128, 4, TT], F32)
            kT = self.sb(st, "kT", [128, 4, TT], F32)
            vT = self.sb(st, "vT", [128, 4, TT], F32)
            xw = self.sb(st, "xw", [128, TT], BF16)
            xa = self.sb(st, "xa", [128, TT], BF16)
            xg = self.sb(st, "xg", [128, TT], BF16)
            xbT = self.sb(st, "xbT", [128, 4, TT], F32)
            gtmp = [self.sb(st, "gtmp%d" % i, [128, TT], F32) for i in range(3)]
            gate = self.sb(st, "gate", [128, 4, TT], BF16)
            gT = self.sb(st, "gT", [128, 4, TT], BF16)
            kkn = self.sb(st, "kkn", [128, 4, TT], F32)
            ksum = self.sb(st, "ksum", [128, 4, TT], F32)
            bon = self.sb(st, "bon", [128, 4, TT], BF16)
            sg = self.sb(st, "sg", [128, 4, TT], F32)
            cs = self.sb(st, "cs", [128, 4, TT], F32)
            E1 = self.sb(st, "E1", [128, 4, TT], F32)
            ad = self.sb(st, "ad", [128, 4, TT], F32)
            wk1 = self.sb(st, "wk1", [128, 4, TT], F32)
            wk2 = self.sb(st, "wk2", [128, 4, TT], F32)
            NC4 = TT // 64
            AR = [self.sb(st, "AR%d" % d, [128, 4, NC4, 2, 64], BF16) for d in range(2)]
            BK = [self.sb(st, "BK%d" % d, [128, 4, NC4, 2, 64], BF16) for d in range(2)]
            PEb1 = self.sb(st, "PEb", [128, 4, NC4, 64], F32)
            PEb = [PEb1, PEb1]
            tokB1 = self.sb(st, "tokB", [128, 2, 8, 64], BF16)
            tokK1 = self.sb(st, "tokK", [128, 2, 8, 64], BF16)
            tokB, tokK = [tokB1, tokB1], [tokK1, tokK1]
            tokV = self.sb(st, "tokV", [128, 2, 8, 64], BF16)
            NSET = 2
            MRBs = [self.sb(st, "MRBs%d" % i, [64, 8, 64], BF16) for i in range(NSET)]
            Lt0s = [self.sb(st, "Lt0_%d" % i, [64, 8, 64], F32R) for i in range(NSET)]
            Tfins = [self.sb(st, "Tfin%d" % i, [64, 8, 64], BF16) for i in range(NSET)]
            SCk = self.sb(st, "SCk", [128, 2, 8, 64], BF16)
            Lms = [[self.sb(st, "Lm%d_%d" % (i, k), [64, 8, 64], F32R) for i in range(2)] for k in range(NSET)]
            Ltms = [[self.sb(st, "Ltm%d_%d" % (i, k), [64, 8, 64], F32R) for i in range(2)] for k in range(NSET)]
            ILms = [self.sb(st, "ILm_%d" % k, [64, 8, 64], F32R) for k in range(NSET)]
            Ttms = [[self.sb(st, "Ttm%d_%d" % (i, k), [64, 8, 64], F32R) for i in range(2)] for k in range(NSET)]

            ones_bf, oblk, ident = self.ones_bf, self.oblk_bf, self.ident
            tiles = [(0, t0, True, 0) for t0 in range(0, TS, TT)] + [(1, TS, False, 1), (2, TS + TP, False, 1)]
            mS1 = cst[0:64, C_MSI1:C_MSI1 + 128].rearrange("p (q t) -> p q t", q=2)
            mL1 = cst[0:64, C_ML1:C_ML1 + 64]
            id64 = idS[0:64]
            loaded = set()

            def load_x(g0, is_s):
                if g0 in loaded:
                    return
                loaded.add(g0)
                if is_s:
                    src = self.xs[g0:g0 + TT, :].rearrange("(s p) f -> p s f", p=128)
                else:
                    l0 = g0 - TS
                    src = self.xp[l0:l0 + TT, :].rearrange("(s p) f -> p s f", p=128)
                self.dma(xin[:], src, [], [xin])
                if is_s:
                    petv = xT[:].rearrange("p a b -> p (a b)").rearrange("p (s f) -> p s f", s=2)
                    self.dma(petv, self.pe[g0:g0 + TT, :].rearrange("(s p) f -> p s f", p=128), [], [xT], eng="act")

            def front(seq, g0, is_s, mc, nxt=None):
                load_x(g0, is_s)
                if is_s:
                    petv = xT[:].rearrange("p a b -> p (a b)").rearrange("p (s f) -> p s f", s=2)
                    self.tt("pool", xin[:], xin[:], petv, ALU.add, [xin, xT], [xin])
                for j in range(8):
                    p = self.ps()
                    for s in range(2):
                        self.tr(p[:, s * 128:(s + 1) * 128], xin[:, s, j * 128:(j + 1) * 128], ident[:], [xin, ident], [p])
                    self.cp("act", xT[:, j, :], p[:, 0:TT], [p], [xT])
                    self.tt("pool", sq[:, j, :], xT[:, j, :], xT[:, j, :], ALU.mult, [xT], [sq])
                    yield
                self.dma(self.xT_scr[:, :, g0:g0 + TT], xT[:], [xT], [self.xT_scr])
                p = self.ps()
                for j in range(8):
                    self.mm(p[:, 0:TT], ones_bf[:], sq[:, j, :], j == 0, j == 7, [ones_bf, sq], [p])
                self.act(rstd[:], p[:, 0:TT], AF.Sqrt, [p, self.epsT], [rstd], scale=1.0 / D, bias=self.epsT[:, 0:1])
                self.S.op("dve", lambda e: e.reciprocal(out=rstd[:], in_=rstd[:]), [rstd.b], [rstd.b])
                for j in range(8):
                    t = tmpA[j % 2]
                    self.tt("dve", t[:], xT[:, j, :], rstd[:], ALU.mult, [xT, rstd], [t])
                    self.act(hT[:, j, :], t[:], AF.Identity, [t, self.gs1, self.modT], [hT],
                             scale=self.gs1[:, j, mc:mc + 1], bias=self.modT[:, j, mc:mc + 1])
                    yield
                for m in range(23):
                    if m >= self.cutm:
                        break
                    p = self.ps()
                    for kc in range(8):
                        self.mm(p[:, 0:TT], win[:, kc, m * 128:(m + 1) * 128], hT[:, kc, :], kc == 0, kc == 7, [win, hT], [p])
                    pz = p[:, 0:TT]
                    if m < 4:
                        self.cp("act", rT[:, m, :], pz, [p], [rT])
                    elif m < 8:
                        self.cp("act", kT[:, m - 4, :], pz, [p], [kT])
                    elif m < 12:
                        self.cp("act", vT[:, m - 8, :], pz, [p], [vT])
                    elif m == 12:
                        self.act(xw[:], pz, AF.Tanh, [p], [xw])
                    elif m == 13:
                        self.cp("act", xa[:], pz, [p], [xa])
                    elif m == 14:
                        self.act(xg[:], pz, AF.Sigmoid, [p], [xg])
                    elif m < 19:
                        self.cp("act", xbT[:, m - 15, :], pz, [p], [xbT])
                    else:
                        j = m - 19
                        g0_, g1_, g2_ = gtmp
                        self.cp("act", g0_[:], pz, [p], [g0_])
                        self.tt("pool", g1_[:], g0_[:], g0_[:], ALU.mult, [g0_], [g1_])
                        self.tsc("dve", g1_[:], g1_[:], 0.044715, ALU.mult, [g1_], [g1_], 1.0, ALU.add)
                        self.tt("dve", g1_[:], g1_[:], g0_[:], ALU.mult, [g1_, g0_], [g1_])
                        self.act(g2_[:], g1_[:], AF.Sigmoid, [g1_], [g2_], scale=GELU_C)
                        self.tt("pool", gate[:, j, :], g0_[:], g2_[:], ALU.mult, [g0_, g2_], [gate])
                    yield
                self.dma(self.xb_scr[:, :, g0:g0 + TT], xbT[:], [xbT], [self.xb_scr])
                if self.debug and g0 == 0:
                    self.dump("hT", hT[:], [128, 8, TT], BF16, [hT])
                    self.dump("rT", rT[:], [128, 4, TT], F32, [rT])
                    self.dump("vT", vT[:], [128, 4, TT], F32, [vT])
                    self.dump("xbT", xbT[:], [128, 4, TT], F32, [xbT])
                    self.dump("gate", gate[:], [128, 4, TT], BF16, [gate])
                self.dma(self.gate_scr[:, :, g0:g0 + TT], gate[:], [gate], [self.gate_scr])
                for j in range(4):
                    p = self.ps()
                    self.mm(p[:, 0:TT], gup[:, j * 128:(j + 1) * 128], xg[:], True, True, [gup, xg], [p])
                    self.cp("act", gT[:, j, :], p[:, 0:TT], [p], [gT])
                    yield
                self.dma(self.g_scr[:, :, g0:g0 + TT], gT[:], [gT], [self.g_scr])
                for j in range(4):
                    self.tsc("dve", kkn[:, j, :], kT[:, j, :], prm[:, P_KK + j:P_KK + j + 1], ALU.mult, [kT, prm], [kkn])
                    self.tt("pool", sq[:, j, :], kkn[:, j, :], kkn[:, j, :], ALU.mult, [kkn], [sq])
                for j in range(4):
                    p = self.ps()
                    self.mm(p[:, 0:TT], oblk[:], sq[:, j, :], True, True, [oblk, sq], [p])
                    t = tmpA[j % 2]
                    self.act(t[:], p[:, 0:TT], AF.Sqrt, [p], [t])
                    self.tsc("dve", t[:], t[:], 1e-12, ALU.max, [t], [t])
                    self.S.op("dve", lambda e, t=t: e.reciprocal(out=t[:], in_=t[:]), [t.b], [t.b])
                    self.tt("dve", kkn[:, j, :], kkn[:, j, :], t[:], ALU.mult, [kkn, t], [kkn])
                    yield
                for s in range(2):
                    p = self.ps()
                    for j in range(4):
                        self.tr(p[:, j * 128:(j + 1) * 128], vT[:, j, s * 128:(s + 1) * 128], ident[:], [vT, ident], [p])
                    self.cp("act", tokV[:, s, :, :].rearrange("p h k -> p (h k)"), p[:], [p], [tokV])
                c0 = g0 // 64
                for s in range(2):
                    dst = self.vt_scr[c0 + 2 * s:c0 + 2 * s + 2, :, :].rearrange("c s f -> (c s) f")
                    self.dma(dst, tokV[:, s, :, :].rearrange("p h k -> p (h k)"), [tokV], [self.vt_scr])
                if nxt is not None:
                    load_x(nxt[1], nxt[2])
                yield
            def prep(g0, d):
                c0 = g0 // 64
                for j in range(4):
                    p = self.ps()
                    self.mm(p[:, 0:TT], wup[d * 64:(d + 1) * 64, j * 128:(j + 1) * 128], xw[d * 64:(d + 1) * 64, :],
                            True, True, [wup, xw], [p])
                    self.act(sg[:, j, :], p[:, 0:TT], AF.Sigmoid, [p, prm], [sg],
                             bias=prm[:, P_W0 + 4 * d + j:P_W0 + 4 * d + j + 1])
                    if d == 0:
                        self.scan(cs[:, j, :], cst[:, C_RMF:C_RMF + TT], sg[:, j, :], 0.0, [cst, sg], [cs])
                    else:
                        self.scan(cs[:, j, ::-1], cst[:, C_RMB:C_RMB + TT][:, ::-1], sg[:, j, ::-1], 0.0, [cst, sg], [cs])
                    yield
                for j in range(4):
                    p = self.ps()
                    self.mm(p[:, 0:TT], aup[d * 64:(d + 1) * 64, j * 128:(j + 1) * 128], xa[d * 64:(d + 1) * 64, :],
                            True, True, [aup, xa], [p])
                    self.act(ad[:, j, :], p[:, 0:TT], AF.Sigmoid, [p, prm], [ad],
                             bias=prm[:, P_A0 + 4 * d + j:P_A0 + 4 * d + j + 1])
                    yield
                self.tt("dve", sg[:], cs[:], sg[:], ALU.subtract, [cs, sg], [sg])
                self.act(E1[:], cs[:], AF.Exp, [cs], [E1], scale=-LAM)
                self.act(cs[:], cs[:], AF.Exp, [cs], [cs], scale=LAM)
                self.act(sg[:], sg[:], AF.Exp, [sg], [sg], scale=-LAM)
                E2, E3 = cs, sg
                ar5 = AR[d]
                bk5 = BK[d]
                v4 = lambda tl: tl[:].rearrange("p j (c t) -> p j c t", t=64)
                self.stt(ar5[:, :, :, 0, :], v4(kkn), -1.0, v4(E3), ALU.mult, ALU.mult, [kkn, E3], [ar5])
                self.tt("pool", ar5[:, :, :, 1, :], v4(rT), v4(E1), ALU.mult, [rT, E1], [ar5])
                yield
                self.tt("dve", wk1[:], kkn[:], ad[:], ALU.mult, [kkn, ad], [wk1])
                self.tt("dve", wk1[:], wk1[:], E2[:], ALU.mult, [wk1, E2], [wk1])
                self.cp("act", bk5[:, :, :, 0, :], v4(wk1), [wk1], [bk5])
                yield
                for j in range(4):
                    self.tsc("dve", wk2[:, j, :], ad[:, j, :], prm[:, P_KA + j:P_KA + j + 1], ALU.mult, [ad, prm, self.misc], [wk2],
                             self.misc[:, j:j + 1], ALU.add)
                self.tt("dve", wk2[:], wk2[:], kT[:], ALU.mult, [wk2, kT], [wk2])
                if d == 0:
                    self.cp("pool", ksum[:], wk2[:], [wk2], [ksum])
                else:
                    self.tt("pool", ksum[:], ksum[:], wk2[:], ALU.add, [ksum, wk2], [ksum])
                self.tt("dve", wk2[:], wk2[:], E2[:], ALU.mult, [wk2, E2], [wk2])
                self.cp("act", bk5[:, :, :, 1, :], v4(wk2), [wk2], [bk5])
                yield
                te = 63 if d == 0 else 0
                pend_b = v4(E1)[:, :, :, te:te + 1].to_broadcast([128, 4, NC4, 64])
                self.cp("act", PEb[d][:], pend_b, [E1], [PEb[d]])
                self.tt("dve", v4(wk1), v4(wk1), PEb[d][:], ALU.mult, [wk1, PEb[d]], [wk1])
                self.tt("dve", v4(wk2), v4(wk2), PEb[d][:], ALU.mult, [wk2, PEb[d]], [wk2])
                yield
                for (srcw, tokX, scr) in ((wk1, tokB[d], self.bh_scr), (wk2, tokK[d], self.kh_scr)):
                    for s in range(2):
                        p = self.ps()
                        for j in range(4):
                            self.tr(p[:, j * 128:(j + 1) * 128], srcw[:, j, s * 128:(s + 1) * 128], ident[:], [srcw, ident], [p])
                        self.cp("act", tokX[:, s, :, :].rearrange("p h k -> p (h k)"), p[:], [p], [tokX])
                        for cc in range(2):
                            self.dma(scr[c0 + 2 * s + cc, d * 64:(d + 1) * 64, :],
                                     tokX[cc * 64:(cc + 1) * 64, s, :, :].rearrange("p h k -> p (h k)"),
                                     [tokX], [scr])
                        yield
                for cl in range(NC4):
                    c = c0 + cl
                    for hp in range(2):
                        for (q, scr) in ((0, self.art_scr), (1, self.rrt_scr)):
                            dst = scr[c, d * 64:(d + 1) * 64, :].rearrange("k (j hp t) -> k j hp t", hp=2, t=64)[:, :, hp, :]
                            self.dma(dst, ar5[hp * 64:(hp + 1) * 64, :, cl, q, :], [ar5], [scr], eng="sp")
                        dst = self.pend_scr[c, d * 64:(d + 1) * 64, :].rearrange("k (j hp t) -> k j hp t", hp=2, t=64)[:, :, hp, :]
                        self.dma(dst, PEb[d][hp * 64:(hp + 1) * 64, :, cl, :], [PEb[d]], [self.pend_scr], eng="sp")
                yield
            def bonus(g0):
                for j in range(4):
                    self.stt(sq[:, j, :], rT[:, j, :], prm[:, P_RK + j:P_RK + j + 1], ksum[:, j, :], ALU.mult, ALU.mult,
                             [rT, prm, ksum], [sq])
                    p = self.ps()
                    self.mm(p[:, 0:TT], oblk[:], sq[:, j, :], True, True, [oblk, sq], [p])
                    self.tt("dve", bon[:, j, :], p[:, 0:TT], vT[:, j, :], ALU.mult, [p, vT], [bon])
                self.dma(self.bon_scr[:, :, g0:g0 + TT], bon[:], [bon], [self.bon_scr])
                yield
            def chunk_sck(g0):
                c0 = g0 // 64
                for cl in range(NC4):
                    c = c0 + cl
                    for hp in range(2):
                        p = self.ps()
                        for j in range(4):
                            for d in range(2):
                                self.mm(p[d * 64:(d + 1) * 64, j * 128:(j + 1) * 128],
                                        BK[d][hp * 64:(hp + 1) * 64, j, cl, 1, :],
                                        AR[d][hp * 64:(hp + 1) * 64, j, cl, :, :].rearrange("p q t -> p (q t)"),
                                        True, True, [BK[d], AR[d]], [p])
                        self.tt("dve", SCk[:, :, hp::2, :].rearrange("p q h t -> p h q t"),
                                p[:].rearrange("p (h q t) -> p h q t", q=2, t=64),
                                mSI.unsqueeze(1).to_broadcast([128, 4, 2, 64]), ALU.mult, [p, cst], [SCk])
                    self.dma(self.akt_scr[c], SCk[:, 0, :, :].rearrange("p h s -> p (h s)"), [SCk], [self.akt_scr], eng="act")
                    self.dma(self.mrkt_scr[c], SCk[:, 1, :, :].rearrange("p h s -> p (h s)"), [SCk], [self.mrkt_scr], eng="act")
                    yield
            def chunk_d(g0, d, cls, k):
                c0 = g0 // 64
                Lm, Ltm, ILm, Ttm, Lt0, Tfin = Lms[k], Ltms[k], ILms[k], Ttms[k], Lt0s[k], Tfins[k]
                MRB = MRBs[k]
                for cl in cls:
                    c = c0 + cl
                    msk = mSI[0:64] if d == 0 else mS1
                    mskL = mL[0:64] if d == 0 else mL1
                    L0 = Lm[0]
                    for hp in range(2):
                        p = self.ps()
                        for j in range(4):
                            self.mm(p[0:64, j * 128:(j + 1) * 128],
                                    BK[d][hp * 64:(hp + 1) * 64, j, cl, 0, :],
                                    AR[d][hp * 64:(hp + 1) * 64, j, cl, :, :].rearrange("p q t -> p (q t)"),
                                    True, True, [BK[d], AR[d]], [p])
                        p4 = p[0:64, :].rearrange("p (h q t) -> p h q t", q=2, t=64)
                        self.tt("dve", Lt0[:, hp::2, :], p4[:, :, 0, :], msk[:, 0, :].unsqueeze(1).to_broadcast([64, 4, 64]),
                                ALU.mult, [p, cst], [Lt0])
                        self.tt("dve", MRB[:, hp::2, :], p4[:, :, 1, :], msk[:, 1, :].unsqueeze(1).to_broadcast([64, 4, 64]),
                                ALU.mult, [p, cst], [MRB])
                        p2 = self.ps()
                        for j in range(4):
                            self.mm(p2[0:64, j * 64:(j + 1) * 64],
                                    AR[d][hp * 64:(hp + 1) * 64, j, cl, 0, :], BK[d][hp * 64:(hp + 1) * 64, j, cl, 0, :],
                                    True, True, [AR[d], BK[d]], [p2])
                        self.tt("dve", L0[:, hp::2, :], p2[0:64, 0:256].rearrange("p (h s) -> p h s", s=64),
                                mskL.unsqueeze(1).to_broadcast([64, 4, 64]), ALU.mult, [p2, cst], [L0])
                    self.dma(self.mrbt_scr[c, d * 64:(d + 1) * 64, :], MRB[:].rearrange("p h s -> p (h s)"),
                             [MRB], [self.mrbt_scr], eng="act")
                    yield
                    T0 = Ttm[0]
                    self.tt("pool", T0[:], Lt0[:].bitcast(F32), id64.unsqueeze(1).to_broadcast([64, 8, 64]), ALU.add,
                            [Lt0, cst], [T0])
                    L_prev, Tt_prev, Lt_prev = L0, T0, Lt0
                    for lev in range(1, 6):
                        L_new, Lt_new, Tt_new = Lm[lev % 2], Ltm[lev % 2], Ttm[lev % 2]
                        pA = self.ps()
                        for h in range(8):
                            self.mm(pA[0:64, h * 64:(h + 1) * 64], Lt_prev[:, h, :], L_prev[:, h, :], True, True,
                                    [Lt_prev, L_prev], [pA])
                        if lev < 5:
                            pB = self.ps()
                            for h in range(8):
                                self.mm(pB[0:64, h * 64:(h + 1) * 64], L_prev[:, h, :], Lt_prev[:, h, :], True, True,
                                        [Lt_prev, L_prev], [pB])
                        self.tt("dve", ILm[:], pA[0:64, :].rearrange("p (h s) -> p h s", s=64),
                                id64.unsqueeze(1).to_broadcast([64, 8, 64]), ALU.add, [pA, cst], [ILm])
                        if lev < 5:
                            self.cp("dve", L_new[:].rearrange("p h s -> p (h s)"), pA[0:64, :], [pA], [L_new])
                            self.cp("act", Lt_new[:].rearrange("p h s -> p (h s)"), pB[0:64, :], [pB], [Lt_new])
                        yield
                        pC = self.ps()
                        for h in range(8):
                            self.mm(pC[0:64, h * 64:(h + 1) * 64], ILm[:, h, :], Tt_prev[:, h, :], True, True,
                                    [ILm, Tt_prev], [pC])
                        if lev < 5:
                            self.cp("act", Tt_new[:].rearrange("p h s -> p (h s)"), pC[0:64, :], [pC], [Tt_new])
                        else:
                            self.cp("act", Tfin[:].rearrange("p h s -> p (h s)"), pC[0:64, :], [pC], [Tfin])
                        L_prev, Tt_prev, Lt_prev = L_new, Tt_new, Lt_new
                        yield
                    self.dma(self.ttt_scr[c, d * 64:(d + 1) * 64, :], Tfin[:].rearrange("p h s -> p (h s)"),
                             [Tfin], [self.ttt_scr], eng="act")


            def run_all(*gens):
                gens = list(gens)
                while gens:
                    for g in list(gens):
                        try:
                            next(g)
                        except StopIteration:
                            gens.remove(g)

            def seq_(*gens):
                for g in gens:
                    yield from g

            prev = None
            tl_ = tiles[:self.ktiles]
            for ti_, (seq, g0, is_s, mc) in enumerate(tl_):
                nxt = tl_[ti_ + 1] if ti_ + 1 < len(tl_) else None
                if prev is None:
                    run_all(front(seq, g0, is_s, mc, nxt))
                else:
                    run_all(seq_(chunk_d(prev, 1, [0, 1], 0), chunk_sck(prev)), chunk_d(prev, 1, [2, 3], 1), front(seq, g0, is_s, mc, nxt))
                run_all(prep(g0, 0))
                run_all(chunk_d(g0, 0, [0, 1], 0), chunk_d(g0, 0, [2, 3], 1), prep(g0, 1))
                run_all(bonus(g0))
                prev = g0
            run_all(seq_(chunk_d(prev, 1, [0, 1], 0), chunk_sck(prev)), chunk_d(prev, 1, [2, 3], 1))

    def phaseB(self):
        with contextlib.ExitStack() as st:
            side = [self.gen_c0(st), self.phaseB_lru(st)]
            post = self.phaseB_post(st)
            next(post)
            done = np.zeros((2, NCH), bool)
            posted = [False] * (NTOK // 128)
            si = 0
            for info in self.phaseB_chain(st):
                for (d, c) in info:
                    done[d, c] = True
                for _ in range(2):
                    if side:
                        g = side[si % len(side)]
                        si += 1
                        try:
                            next(g)
                        except StopIteration:
                            side.remove(g)
                for b in range(NTOK // 128):
                    if not posted[b] and done[:, 2 * b:2 * b + 2].all():
                        posted[b] = True
                        post.send(b)
            for g in side:
                for _ in g:
                    pass
            for b in range(NTOK // 128):
                if not posted[b]:
                    post.send(b)
            if self.debug:
                self.S.barrier()
                self.dump("yscr", self.y_scr[:], [128, 8, NTOK], BF16, [self.y_scr])

    def gen_c0(self, st):
        stg = [self.sb(st, "stgC%d" % i, [128, 8, 512], F32) for i in range(2)]
        wb = [self.sb(st, "wbC%d" % i, [128, 8, 512], BF16) for i in range(2)]
        w1src = self.w1[:].rearrange("(kc p) n -> p kc n", p=128)
        for blk in range(8):
            s, o = stg[blk % 2], wb[blk % 2]
            self.dma(s[:], w1src[:, :, blk * 512:(blk + 1) * 512], [], [s])
            self.cp("pool", o[:], s[:], [s], [o])
            self.dma(self.w1_scr[blk], o[:], [o], [self.w1_scr], eng="act")
            yield
        w2src = self.w2[:].rearrange("(fc p) n -> p fc n", p=128)
        for m in range(8):
            s, o = stg[m % 2], wb[m % 2]
            s4 = s[:].rearrange("p k (a b) -> p (k a) b", b=128)
            o4 = o[:].rearrange("p k (a b) -> p (k a) b", b=128)
            self.dma(s4, w2src[:, :, m * 128:(m + 1) * 128], [], [s])
            self.cp("pool", o[:], s[:], [s], [o])
            self.dma(self.w2_scr[m], o4, [o], [self.w2_scr], eng="act")
            yield

    def phaseB_lru(self, st):
        prm, cst, misc = self.prm_t, self.cst_t, self.misc
        if True:
            wbd32 = self.sb(st, "wbd32", [128, 16, 128], F32)
            wbd = self.sb(st, "wbd", [128, 16, 128], BF16)
            self.memset("pool", wbd32[:], 0.0, [wbd32])
            self.S.barrier_on(wbd32)
            wbd32.b.multi = True
            for gi, src in enumerate((self.lwa, self.lwx)):
                for d in range(2):
                    for j in range(4):
                        for hb in range(2):
                            self.dma(wbd32[hb * 64:(hb + 1) * 64, (gi * 2 + d) * 4 + j, hb * 64:(hb + 1) * 64],
                                     src[d, 2 * j + hb], [], [wbd32])
            self.cp("dve", wbd[:], wbd32[:], [wbd32], [wbd])
            TM = TS
            xbp = self.sb(st, "xbp", [128, TM + 4], F32)
            xc = self.sb(st, "xc", [128, TM], F32)
            xcb = self.sb(st, "xcb", [128, TM], BF16)
            gt = self.sb(st, "gt_l", [128, TM], BF16)
            a_t = self.sb(st, "a_t", [128, TM], F32)
            bx_t = self.sb(st, "bx_t", [128, TM], F32)
            s_t = self.sb(st, "s_t", [128, TM], F32)
            hs = [self.sb(st, "hs%d" % d, [128, TM], F32) for d in range(2)]
            yb = self.sb(st, "yb", [128, TM], BF16)
            for (seq, g0, T) in ((0, 0, TS), (1, TS, TP), (2, TS + TP, TP)):
                for j in range(4):
                    self.memset("pool", xbp[:, 0:2], 0.0, [xbp])
                    self.memset("pool", xbp[:, T + 2:T + 4], 0.0, [xbp])
                    self.dma(xbp[:, 2:T + 2], self.xb_scr[:, j, g0:g0 + T], [self.xb_scr], [xbp])
                    self.dma(gt[:, 0:T], self.gate_scr[:, j, g0:g0 + T], [self.gate_scr], [gt], eng="act")
                    cw = lambda i: prm[:, P_CW + 4 * i + j:P_CW + 4 * i + j + 1]
                    self.act(xc[:, 0:T], xbp[:, 0:T], AF.Identity, [xbp, prm], [xc], scale=cw(0), bias=prm[:, P_CB + j:P_CB + j + 1])
                    for i in range(1, 4):
                        self.stt(xc[:, 0:T], xbp[:, i:i + T], cw(i), xc[:, 0:T], ALU.mult, ALU.add, [xbp, prm, xc], [xc])
                    self.cp("pool", xcb[:, 0:T], xc[:, 0:T], [xc], [xcb])
                    yield
                    for d in range(2):
                        for t0 in range(0, T, 512):
                            tw = min(512, T - t0)
                            p = self.ps()
                            self.mm(p[:, 0:tw], wbd[:, (0 * 2 + d) * 4 + j, :], xcb[:, t0:t0 + tw], True, True, [wbd, xcb], [p])
                            self.act(s_t[:, t0:t0 + tw], p[:, 0:tw], AF.Sigmoid, [p, prm], [s_t],
                                     bias=prm[:, P_BA + 4 * d + j:P_BA + 4 * d + j + 1])
                            p2 = self.ps()
                            self.mm(p2[:, 0:tw], wbd[:, (1 * 2 + d) * 4 + j, :], xcb[:, t0:t0 + tw], True, True, [wbd, xcb], [p2])
                            self.act(bx_t[:, t0:t0 + tw], p2[:, 0:tw], AF.Sigmoid, [p2, prm], [bx_t],
                                     bias=prm[:, P_BX + 4 * d + j:P_BX + 4 * d + j + 1])
                        col = 4 + d * 4 + j
                        self.act(a_t[:, 0:T], s_t[:, 0:T], AF.Exp, [s_t, misc], [a_t], scale=misc[:, col:col + 1])
                        self.act(s_t[:, 0:T], s_t[:, 0:T], AF.Exp, [s_t, misc], [s_t], scale=misc[:, col + 8:col + 9])
                        self.act(s_t[:, 0:T], s_t[:, 0:T], AF.Sqrt, [s_t], [s_t], scale=-1.0, bias=1.0)
                        self.tt("pool", bx_t[:, 0:T], bx_t[:, 0:T], xc[:, 0:T], ALU.mult, [bx_t, xc], [bx_t])
                        self.tt("dve", bx_t[:, 0:T], bx_t[:, 0:T], s_t[:, 0:T], ALU.mult, [bx_t, s_t], [bx_t])
                        h = hs[d]
                        if seq == 0:
                            init = prm[:, P_H0 + 4 * d + j:P_H0 + 4 * d + j + 1]
                        else:
                            init = 0.0
                        if d == 0:
                            self.scan(h[:, 0:T], a_t[:, 0:T], bx_t[:, 0:T], init, [a_t, bx_t, prm], [h])
                        else:
                            self.scan(h[:, 0:T][:, ::-1], a_t[:, 0:T][:, ::-1], bx_t[:, 0:T][:, ::-1], init, [a_t, bx_t, prm], [h])
                        if seq > 0:
                            col_o = j * 4 + (seq - 1) * 2 + d
                            te = T - 1 if d == 0 else 0
                            self.cp("pool", self.stl_t[:, col_o:col_o + 1], h[:, te:te + 1], [h], [self.stl_t])
                        yield
                    self.tt("pool", hs[0][:, 0:T], hs[0][:, 0:T], hs[1][:, 0:T], ALU.add, [hs[0], hs[1]], [hs[0]])
                    self.tt("dve", yb[:, 0:T], hs[0][:, 0:T], gt[:, 0:T], ALU.mult, [hs[0], gt], [yb])
                    self.dma(self.y_scr[:, 4 + j, g0:g0 + T], yb[:, 0:T], [yb], [self.y_scr])
            self.dma(self.stl_o[:], self.stl_t[:], [self.stl_t], [self.stl_o])

    def phaseB_chain(self, st):
        if True:
            NB = 3
            def ring(name, dt=BF16):
                return [self.sb2(st, "%s%d" % (name, i), [128, 512], dt) for i in range(NB)]
            art, rrt, ttt, akt, mrbt, mrkt, bh, kh, vt = [ring(n) for n in
                                                          ("c_art", "c_rrt", "c_ttt", "c_akt", "c_mrbt", "c_mrkt", "c_bh", "c_kh", "c_vt")]
            pend = ring("c_pend", F32)
            Hf = self.sb2(st, "Hf", [128, 512], F32)
            Hb = self.sb2(st, "Hb", [128, 512], BF16)
            Zs = self.sb2(st, "Zs", [128, 512], BF16)
            Us = self.sb2(st, "Us", [128, 512], BF16)
            Yt = [self.sb2(st, "Yt%d" % i, [128, 512], F32) for i in range(2)]
            tmpH = self.sb2(st, "tmpH", [128, 512], F32)
            hs_ = lambda h: slice(h * 64, (h + 1) * 64)
            steps = []
            for (seq, cbase, n) in ((0, 0, 32), (1, 32, 4), (2, 36, 4)):
                for i in range(n):
                    steps.append((seq, cbase, n, i))

            def loads(k):
                seq, cbase, n, i = steps[k]
                r = k % NB
                for d in range(2):
                    sl = slice(d * 64, (d + 1) * 64)
                    c = cbase + i if d == 0 else cbase + n - 1 - i
                    for (tl, scr) in ((art, self.art_scr), (rrt, self.rrt_scr), (ttt, self.ttt_scr), (akt, self.akt_scr),
                                      (mrbt, self.mrbt_scr), (mrkt, self.mrkt_scr), (bh, self.bh_scr), (kh, self.kh_scr),
                                      (pend, self.pend_scr)):
                        self.dma(tl[r][d][sl, :], scr[c, sl, :], [scr], [tl[r][d]], eng="sp")
                    self.dma(vt[r][d][sl, :], self.vt_scr[c], [self.vt_scr], [vt[r][d]], eng="sp")

            loads(0)
            for k in range(len(steps)):
                seq, cbase, n, i = steps[k]
                step = k + 1
                r = k % NB
                if k + 1 < len(steps):
                    loads(k + 1)
                if i == 0:
                    for d in range(2):
                        sl = slice(d * 64, (d + 1) * 64)
                        if seq == 0:
                            self.dma(Hf[d][sl, :], self.h0r[sl, :], [], [Hf[d]], eng="act")
                        else:
                            self.memset("dve", Hf[d][sl, :], 0.0, [Hf[d]])
                        self.cp("dve", Hb[d][sl, :], Hf[d][sl, :], [Hf[d]], [Hb[d]])
                if True:
                    ctx = []
                    for d in range(2):
                        sl = slice(d * 64, (d + 1) * 64)
                        c = cbase + i if d == 0 else cbase + n - 1 - i
                        ops = tuple(x[r][d] for x in (art, rrt, ttt, akt, mrbt, mrkt, bh, kh, vt, pend))
                        ctx.append((d, sl, c, ops))
                    pZs = {}
                    for (d, sl, c, (A_, R_, T_, AK_, MRB_, MRK_, B_, K_, V_, PE_)) in ctx:
                        H_, Z_ = Hb[d], Zs[d]
                        pZ = self.ps()
                        for h in range(8):
                            self.mm(pZ[sl, hs_(h)], A_[sl, hs_(h)], H_[sl, hs_(h)], True, False, [A_, H_], [pZ])
                            self.mm(pZ[sl, hs_(h)], AK_[sl, hs_(h)], V_[sl, hs_(h)], False, True, [AK_, V_], [pZ])
                        pZs[d] = pZ
                    pYs = {}
                    for (d, sl, c, (A_, R_, T_, AK_, MRB_, MRK_, B_, K_, V_, PE_)) in ctx:
                        H_ = Hb[d]
                        pY = self.ps()
                        pYs[d] = pY
                    for (d, sl, c, ops) in ctx:
                        self.cp("act", Zs[d][sl, :], pZs[d][sl, :], [pZs[d]], [Zs[d]])
                    pUs = {}
                    for (d, sl, c, (A_, R_, T_, AK_, MRB_, MRK_, B_, K_, V_, PE_)) in ctx:
                        Z_ = Zs[d]
                        pU = self.ps()
                        for h in range(8):
                            self.mm(pU[sl, hs_(h)], T_[sl, hs_(h)], Z_[sl, hs_(h)], True, True, [T_, Z_], [pU])
                        pUs[d] = pU
                    for (d, sl, c, ops) in ctx:
                        self.cp("act", Us[d][sl, :], pUs[d][sl, :], [pUs[d]], [Us[d]])
                    pHs = {}
                    for (d, sl, c, (A_, R_, T_, AK_, MRB_, MRK_, B_, K_, V_, PE_)) in ctx:
                        U_ = Us[d]
                        self.tt("dve", tmpH[d][sl, :], Hf[d][sl, :], PE_[sl, :], ALU.mult, [Hf[d], PE_], [tmpH[d]])
                        pH = self.ps()
                        for h in range(8):
                            self.mm(pH[sl, hs_(h)], B_[sl, hs_(h)], U_[sl, hs_(h)], True, False, [B_, U_], [pH])
                            self.mm(pH[sl, hs_(h)], K_[sl, hs_(h)], V_[sl, hs_(h)], False, True, [K_, V_], [pH])
                        pHs[d] = pH
                    for (d, sl, c, (A_, R_, T_, AK_, MRB_, MRK_, B_, K_, V_, PE_)) in ctx:
                        H_, U_ = Hb[d], Us[d]
                        pY = pYs[d]
                        for h in range(8):
                            self.mm(pY[sl, hs_(h)], R_[sl, hs_(h)], H_[sl, hs_(h)], True, False, [R_, H_], [pY])
                            self.mm(pY[sl, hs_(h)], MRB_[sl, hs_(h)], U_[sl, hs_(h)], False, False, [MRB_, U_], [pY])
                            self.mm(pY[sl, hs_(h)], MRK_[sl, hs_(h)], V_[sl, hs_(h)], False, True, [MRK_, V_], [pY])
                    for (d, sl, c, ops) in ctx:
                        self.tt("dve", Hf[d][sl, :], tmpH[d][sl, :], pHs[d][sl, :], ALU.add, [tmpH[d], pHs[d]], [Hf[d]])
                        self.cp("dve", Hb[d][sl, :], Hf[d][sl, :], [Hf[d]], [Hb[d]])
                    for (d, sl, c, ops) in ctx:
                        y = Yt[step % 2][d]
                        self.cp("act", y[sl, :], pYs[d][sl, :], [pYs[d]], [y])
                        self.dma(self.ytok_scr[d, c * 64:(c + 1) * 64, :], y[sl, :], [y], [self.ytok_scr], eng="act")
                if seq > 0 and i == n - 1:
                    self.dma(self.str_o[seq - 1], Hf[0][:], [Hf[0], Hf[1]], [self.str_o], eng="act")
                yield [(0, cbase + i), (1, cbase + n - 1 - i)]

    def phaseB_post(self, st):
        prm, cst = self.prm_t, self.cst_t
        ident = self.ident
        if True:
            yf = [self.sb(st, "yf%d" % i, [128, 512], F32) for i in range(2)]
            yb2 = [self.sb(st, "yb2%d" % i, [128, 512], F32) for i in range(2)]
            cen = self.sb(st, "cen", [128, 8, 64], F32)
            sqv = self.sb(st, "sqv", [128, 8, 64], F32)
            mean = self.sb(st, "mean", [128, 8], F32)
            var = self.sb(st, "var", [128, 8], F32)
            gl = [self.sb(st, "gl%d" % i, [128, 4, 128], BF16) for i in range(2)]
            bl = [self.sb(st, "bl%d" % i, [128, 4, 128], BF16) for i in range(2)]
            ynT = self.sb(st, "ynT", [128, 4, 128], F32)
            yo = [self.sb(st, "yo%d" % i, [128, 4, 128], BF16) for i in range(2)]
            it = -1
            blk = yield
            while True:
                it += 1
                g0 = blk * 128
                a, b = yf[it % 2], yb2[it % 2]
                g_, b_ = gl[it % 2], bl[it % 2]
                o = yo[it % 2]
                self.dma(a[:], self.ytok_scr[0, g0:g0 + 128, :], [self.ytok_scr], [a])
                self.dma(b[:], self.ytok_scr[1, g0:g0 + 128, :], [self.ytok_scr], [b], eng="act")
                self.dma(g_[:], self.g_scr[:, :, g0:g0 + 128], [self.g_scr], [g_])
                self.dma(b_[:], self.bon_scr[:, :, g0:g0 + 128], [self.bon_scr], [b_], eng="act")
                a3 = a[:].rearrange("p (h v) -> p h v", v=64)
                self.tt("pool", a[:], a[:], b[:], ALU.add, [a, b], [a])
                self.S.op("dve", lambda e, a3=a3: e.tensor_reduce(out=mean[:], in_=a3, op=ALU.add, axis=mybir.AxisListType.X),
                          [a.b], [mean.b])
                self.tsc("dve", mean[:], mean[:], 1.0 / 64, ALU.mult, [mean], [mean])
                self.tt("dve", cen[:], a3, mean[:].unsqueeze(2).to_broadcast([128, 8, 64]), ALU.subtract, [a, mean], [cen])
                self.tt("pool", sqv[:], cen[:], cen[:], ALU.mult, [cen], [sqv])
                self.S.op("dve", lambda e: e.tensor_reduce(out=var[:], in_=sqv[:], op=ALU.add, axis=mybir.AxisListType.X),
                          [sqv.b], [var.b])
                self.act(var[:], var[:], AF.Sqrt, [var, self.epsT], [var], scale=1.0 / 64, bias=self.epsT[:, 1:2])
                self.S.op("dve", lambda e: e.reciprocal(out=var[:], in_=var[:]), [var.b], [var.b])
                self.tt("dve", cen[:], cen[:], var[:].unsqueeze(2).to_broadcast([128, 8, 64]), ALU.mult, [cen, var], [cen])
                p = self.ps()
                cen2 = cen[:].rearrange("p h v -> p (h v)")
                for j in range(4):
                    self.tr(p[:, j * 128:(j + 1) * 128], cen2[:, j * 128:(j + 1) * 128], ident[:], [cen, ident], [p])
                for j in range(4):
                    self.act(ynT[:, j, :], p[:, j * 128:(j + 1) * 128], AF.Identity, [p, prm], [ynT],
                             scale=prm[:, P_LNG + j:P_LNG + j + 1], bias=prm[:, P_LNB + j:P_LNB + j + 1])
                self.tt("pool", ynT[:], ynT[:], b_[:], ALU.add, [ynT, b_], [ynT])
                self.tt("dve", o[:], ynT[:], g_[:], ALU.mult, [ynT, g_], [o])
                self.dma(self.y_scr[:, 0:4, g0:g0 + 128], o[:], [o], [self.y_scr])
                blk = yield

    def phaseC(self):
        prm, cst = self.prm_t, self.cst_t
        ident, ones_bf = self.ident, self.ones_bf
        with contextlib.ExitStack() as st:
            pass
        import os
        kcc = int(os.environ.get("KCC", "99"))
        if kcc == 0:
            return
        with contextlib.ExitStack() as st:
            wout = self.sb(st, "wout", [128, 8, D], BF16)
            wsrc = self.w_out[:].rearrange("(kc p) n -> p kc n", p=128)
            with contextlib.ExitStack() as st2:
                stg = [self.sb(st2, "stgD%d" % i, [128, 8, 256], F32) for i in range(2)]
                for cb in range(4):
                    s = stg[cb % 2]
                    self.dma(s[:], wsrc[:, :, cb * 256:(cb + 1) * 256], [], [s])
                    self.cp("act", wout[:, :, cb * 256:(cb + 1) * 256], s[:], [s], [wout])
            self.S.barrier()
            yT = self.sb(st, "yT_c", [128, 8, TC], BF16)
            o1T = self.sb(st, "o1T", [128, 8, TC], F32)
            sq1 = self.sb(st, "sq1_c", [128, 8, TC], BF16)
            oT = self.sb(st, "oT", [128, 8, TC], F32)
            sq = self.sb(st, "sq_c", [128, 8, TC], BF16)
            xT = self.sb(st, "xT_c", [128, 8, TC], F32)
            h2 = self.sb(st, "h2", [128, 8, TC], BF16)
            f = self.sb(st, "f_c", [128, 32, TC], BF16)
            otok = self.sb(st, "otok", [128, 4, D], F32)
            rstd = self.sb(st, "rstd_c", [128, TC], F32)
            tmp = [self.sb(st, "tmpC%d" % i, [128, TC], F32) for i in range(2)]
            NW = 3
            w1r = [self.sb(st, "w1r%d" % i, [128, 8, 512], BF16) for i in range(NW)]
            w2r = [self.sb(st, "w2r%d" % i, [128, 32, 128], BF16) for i in range(NW)]
            ntile = min(NTOK // TC, kcc)
            wseq = []
            for ti_ in range(ntile):
                wseq += [("w1", b_) for b_ in range(8)] + [("w2", b_) for b_ in range(8)]
            wstate = {"issued": 0, "w1": 0, "w2": 0}
            wbuf = {}

            def issue_upto(n):
                while wstate["issued"] < min(n, len(wseq)):
                    k_ = wstate["issued"]
                    kind, b_ = wseq[k_]
                    ring = w1r if kind == "w1" else w2r
                    buf = ring[wstate[kind] % NW]
                    wstate[kind] += 1
                    scr = self.w1_scr if kind == "w1" else self.w2_scr
                    self.dma(buf[:], scr[b_], [scr], [buf], eng="sp" if k_ % 2 == 0 else "act")
                    wbuf[k_] = buf
                    wstate["issued"] += 1

            def rms(sqt, R):
                p = self.ps()
                for j in range(8):
                    self.mm(p[:], ones_bf[:], sqt[:, j, :], j == 0, j == 7, [ones_bf, sqt], [p])
                self.act(rstd[:], p[:], AF.Sqrt, [p, self.epsT], [rstd], scale=1.0 / D, bias=self.epsT[:, 0:1])
                self.S.op("dve", lambda e: e.reciprocal(out=rstd[:], in_=rstd[:]), [rstd.b], [rstd.b])

            def resid(gg, mc, oT):
                for j in range(8):
                    t = tmp[j % 2]
                    self.tt("dve", t[:], oT[:, j, :], rstd[:], ALU.mult, [oT, rstd], [t])
                    self.stt(xT[:, j, :], t[:], gg[:, j, mc:mc + 1], xT[:, j, :], ALU.mult, ALU.add, [t, gg, xT], [xT])

            def wout_stage(tj):
                gj = tj * TC
                self.dma(yT[:], self.y_scr[:, :, gj:gj + TC], [self.y_scr], [yT])
                for m in range(8):
                    p = self.ps()
                    for kc in range(8):
                        self.mm(p[:], wout[:, kc, m * 128:(m + 1) * 128], yT[:, kc, :], kc == 0, kc == 7, [wout, yT], [p])
                    self.cp("act", o1T[:, m, :], p[:], [p], [o1T])
                    self.act(sq1[:, m, :], p[:], AF.Square, [p], [sq1])

            wi = 0
            for ti in range(NTOK // TC):
                if ti >= kcc:
                    break
                g0 = ti * TC
                mc = 0 if g0 < TS else 1
                if ti == 0:
                    wout_stage(0)
                self.dma(xT[:], self.xT_scr[:, :, g0:g0 + TC], [self.xT_scr], [xT], eng="act")
                issue_upto(ti * 16 + 3)
                rms(sq1, None)
                resid(self.gg1, mc, o1T)
                for j in range(8):
                    self.act(sq[:, j, :], xT[:, j, :], AF.Square, [xT], [sq])
                rms(sq, None)
                for j in range(8):
                    t = tmp[j % 2]
                    self.tt("dve", t[:], xT[:, j, :], rstd[:], ALU.mult, [xT, rstd], [t])
                    self.act(h2[:, j, :], t[:], AF.Identity, [t, self.gs2, self.modT], [h2],
                             scale=self.gs2[:, j, mc:mc + 1], bias=self.modT[:, 24 + j, mc:mc + 1])
                for blk in range(8):
                    issue_upto(ti * 16 + blk + 3)
                    w = wbuf[ti * 16 + blk]
                    for c4 in range(4):
                        fc = blk * 4 + c4
                        p = self.ps()
                        for kc in range(8):
                            self.mm(p[:], w[:, kc, c4 * 128:(c4 + 1) * 128], h2[:, kc, :], kc == 0, kc == 7, [w, h2], [p])
                        t = tmp[fc % 2]
                        self.act(t[:], p[:], AF.Relu, [p], [t])
                        self.tt("pool" if fc % 2 == 0 else "dve", f[:, fc, :], t[:], t[:], ALU.mult, [t], [f])
                for m in range(8):
                    issue_upto(ti * 16 + 8 + m + 3)
                    w = wbuf[ti * 16 + 8 + m]
                    p = self.ps()
                    for fc in range(32):
                        self.mm(p[:], w[:, fc, :], f[:, fc, :], fc == 0, fc == 31, [w, f], [p])
                    self.cp("act", oT[:, m, :], p[:], [p], [oT])
                    self.act(sq[:, m, :], p[:], AF.Square, [p], [sq])
                if ti + 1 < ntile:
                    wout_stage(ti + 1)
                rms(sq, None)
                resid(self.gg2, mc, oT)
                for s in range(4):
                    for half in range(2):
                        p = self.ps()
                        for jj in range(4):
                            j = half * 4 + jj
                            self.tr(p[:, jj * 128:(jj + 1) * 128], xT[:, j, s * 128:(s + 1) * 128], ident[:], [xT, ident], [p])
                        self.cp("act" if half == 0 else "dve", otok[:, s, half * 512:(half + 1) * 512], p[:], [p], [otok])
                if g0 < TS:
                    dst = self.ys[g0:g0 + TC, :].rearrange("(s p) f -> p s f", p=128)
                    self.dma(dst, otok[:], [otok], [self.ys])
                else:
                    dst = self.yp[:, :].rearrange("(s p) f -> p s f", p=128)
                    self.dma(dst, otok[:], [otok], [self.yp])


def _fm(v):
    v = np.asarray(v, np.float32).reshape(-1, 128)
    return np.ascontiguousarray(v.T)


def _pos_embed():
    def sincos(pos, dim):
        omega = (1.0 / (10000.0 ** (np.arange(dim // 2, dtype=np.float32) / np.float32(dim // 2)))).astype(np.float32)
        ang = pos.astype(np.float32)[:, None] * omega[None, :]
        return np.concatenate([np.sin(ang), np.cos(ang)], axis=-1).astype(np.float32)
    rows = TS // 64
    half = D // 2
    e_row = sincos(np.arange(rows), half)
    e_col = sincos(np.arange(64), half)
    emb = np.concatenate([np.broadcast_to(e_row[:, None, :], (rows, 64, half)),
                          np.broadcast_to(e_col[None, :, :], (rows, 64, half))], axis=-1)
    return np.ascontiguousarray(emb.reshape(rows * 64, D).astype(np.float32))


def _consts():
    c = np.zeros((128, NCST), np.float32)
    c[:, C_ID:C_ID + 128] = np.eye(128, dtype=np.float32)
    ob = np.zeros((128, 128), np.float32)
    ob[:64, :64] = 1.0
    ob[64:, 64:] = 1.0
    c[:, C_OB:C_OB + 128] = ob
    s = np.arange(64)[:, None]
    t = np.arange(64)[None, :]
    msi = np.zeros((128, 2, 64), np.float32)
    msi[:64, 0] = (s < t)
    msi[:64, 1] = (s <= t)
    msi[64:, 0] = (s > t)
    msi[64:, 1] = (s >= t)
    c[:, C_MSI:C_MSI + 128] = msi.reshape(128, 128)
    ml = np.zeros((128, 64), np.float32)
    ml[:64] = (t < s)
    ml[64:] = (t > s)
    c[:, C_ML:C_ML + 64] = ml
    ids = np.zeros((128, 64), np.float32)
    ids[:64] = np.eye(64)
    ids[64:] = np.eye(64)
    c[:, C_IDS:C_IDS + 64] = ids
    c[:64, C_MSI1:C_MSI1 + 128] = msi[64:].reshape(64, 128)
    c[:64, C_ML1:C_ML1 + 64] = ml[64:]
    tt_ = np.arange(TT)
    c[:, C_RMF:C_RMF + TT] = (tt_ % 64 != 0).astype(np.float32)[None, :]
    c[:, C_RMB:C_RMB + TT] = (tt_ % 64 != 63).astype(np.float32)[None, :]
    return c


_NC_CACHE = {}


def kernel(x_prompt, x_sample, c, state_rwkv, state_lru, c_ctx, w_mod, b_mod,
           g_pre_mix, g_post_mix, g_pre_mlp, g_post_mlp, w_in,
           rwkv_w0, rwkv_w_up, rwkv_a0, rwkv_a_up, rwkv_g_up, rwkv_k_k, rwkv_k_a, rwkv_r_k,
           rwkv_lnx_g, rwkv_lnx_b, lru_conv_w, lru_conv_b, lru_wa, lru_ba, lru_wx, lru_bx,
           lru_lambda, w_out, w_mlp1, w_mlp2, _debug=False):
    f = lambda a: np.ascontiguousarray(np.asarray(a, np.float32))
    x_prompt, x_sample, c, state_rwkv, state_lru, c_ctx = map(f, (x_prompt, x_sample, c, state_rwkv, state_lru, c_ctx))
    nc = K(debug=_debug).build()
    pe = _pos_embed()
    cst = _consts()
    shared = {
        "pe": pe, "cst": cst,
        "w_mod": f(w_mod[0]), "w_in": f(w_in[0]), "w_out": f(w_out[0]), "w1": f(w_mlp1[0]), "w2": f(w_mlp2[0]),
        "wup": f(rwkv_w_up[0]).reshape(128, 512), "aup": f(rwkv_a_up[0]).reshape(128, 512), "gup": f(rwkv_g_up[0]),
        "lwa": f(lru_wa[0]), "lwx": f(lru_wx[0]),
    }
    prm0 = np.zeros((128, NPRM), np.float32)
    prm0[:, P_GPRE:P_GPRE + 8] = _fm(g_pre_mix[0])
    prm0[:, P_GPOST:P_GPOST + 8] = _fm(g_post_mix[0])
    prm0[:, P_GPRE2:P_GPRE2 + 8] = _fm(g_pre_mlp[0])
    prm0[:, P_GPOST2:P_GPOST2 + 8] = _fm(g_post_mlp[0])
    prm0[:, P_BMOD:P_BMOD + 48] = _fm(b_mod[0])
    for d in range(2):
        prm0[:, P_W0 + 4 * d:P_W0 + 4 * d + 4] = _fm(rwkv_w0[0, d])
        prm0[:, P_A0 + 4 * d:P_A0 + 4 * d + 4] = _fm(rwkv_a0[0, d])
        prm0[:, P_BA + 4 * d:P_BA + 4 * d + 4] = _fm(lru_ba[0, d])
        prm0[:, P_BX + 4 * d:P_BX + 4 * d + 4] = _fm(lru_bx[0, d])
        prm0[:, P_LAM + 4 * d:P_LAM + 4 * d + 4] = _fm(lru_lambda[0, d])
    prm0[:, P_KK:P_KK + 4] = _fm(rwkv_k_k[0])
    prm0[:, P_KA:P_KA + 4] = _fm(rwkv_k_a[0])
    prm0[:, P_RK:P_RK + 4] = _fm(np.asarray(rwkv_r_k[0]).reshape(-1))
    prm0[:, P_LNG:P_LNG + 4] = _fm(rwkv_lnx_g[0])
    prm0[:, P_LNB:P_LNB + 4] = _fm(rwkv_lnx_b[0])
    for i in range(4):
        prm0[:, P_CW + 4 * i:P_CW + 4 * i + 4] = _fm(lru_conv_w[0, i])
    prm0[:, P_CB:P_CB + 4] = _fm(lru_conv_b[0])
    in_maps = []
    for i in range(8):
        prm = prm0.copy()
        for d in range(2):
            prm[:, P_H0 + 4 * d:P_H0 + 4 * d + 4] = _fm(state_lru[i, 0, d])
        cT = np.zeros((128, 8, 2), np.float32)
        cT[:, :, 0] = _fm(c[i])
        cT[:, :, 1] = _fm(c_ctx)
        h0 = np.ascontiguousarray(state_rwkv[i, 0].transpose(0, 3, 1, 2)).reshape(128, 512)
        m = dict(shared)
        m.update({"xs": x_sample[i], "xp": np.ascontiguousarray(x_prompt[2 * i:2 * i + 2].reshape(2 * TP, D)),
                  "cT": cT.reshape(128, 16), "h0r": h0, "prm": prm})
        in_maps.append(m)
    res = run_bass_kernel_spmd(nc, in_maps, core_ids=list(range(8)))
    R = res.results
    y_prompt = np.zeros((16, TP, D), np.float32)
    y_sample = np.zeros((8, TS, D), np.float32)
    st_r = np.zeros((16, 1, 2, 8, 64, 64), np.float32)
    st_l = np.zeros((16, 1, 2, 512), np.float32)
    for i in range(8):
        r = R[i]
        y_sample[i] = r["ys"]
        y_prompt[2 * i:2 * i + 2] = r["yp"].reshape(2, TP, D)
        so = r["str_o"].reshape(2, 2, 64, 8, 64)
        st_r[2 * i:2 * i + 2, 0] = so.transpose(0, 1, 3, 4, 2)
        sl = r["stl_o"].reshape(128, 4, 2, 2)
        st_l[2 * i:2 * i + 2, 0] = sl.transpose(2, 3, 1, 0).reshape(2, 2, 512)
    if _debug:
        return (y_prompt, y_sample, st_r, st_l), R
    return (y_prompt, y_sample, st_r, st_l)
```

```python
import contextlib
import numpy as np
import concourse.bass as bass
import concourse.mybir as mybir
from concourse.bass_utils import run_bass_kernel_spmd

F32 = mybir.dt.float32
BF16 = mybir.dt.bfloat16
F32R = mybir.dt.float32r
AF = mybir.ActivationFunctionType
ALU = mybir.AluOpType

D = 1024
TS = 2048
TP = 256
NTOK = TS + 2 * TP
NCH = NTOK // 64
DIN = 2944
DFF = 4096
LAM = float(np.exp(-0.5))
EPS = 1e-6
LNX_EPS = 64e-5
TT = 256
TC = 512
GELU_C = 1.5957691216057308

P_GPRE, P_GPOST, P_GPRE2, P_GPOST2 = 0, 8, 16, 24
P_BMOD = 32
P_W0, P_A0 = 80, 88
P_KK, P_KA, P_RK, P_LNG, P_LNB = 96, 100, 104, 108, 112
P_CW, P_CB = 116, 132
P_BA, P_BX, P_LAM, P_H0 = 136, 144, 152, 160
NPRM = 168
C_ID, C_OB, C_MSI, C_ML, C_IDS, C_RMF, C_RMB = 0, 128, 256, 384, 448, 512, 768
C_MSI1, C_ML1 = 1024, 1152
NCST = 1216


class Buf:
    __slots__ = ("name", "lw", "rd", "excl", "multi", "ws")

    def __init__(self, name=""):
        self.name = name
        self.lw = None
        self.rd = {}
        self.excl = False
        self.multi = False
        self.ws = {}


class TL:
    def __init__(self, t, name=""):
        self.t = t
        self.b = Buf(name)

    def __getitem__(self, k):
        return self.t[k]


class Sched:
    ENGS = ("pe", "act", "dve", "pool", "sp")

    def __init__(self, nc):
        self.nc = nc
        self.streams = {e: [] for e in self.ENGS}
        self.cnt = {}
        self.waited = {e: {} for e in self.ENGS}
        self.n_ops = 0
        self.dma_n = {e: 0 for e in self.ENGS}
        self.NSLOT = {"sp": 44, "act": 44, "pool": 4, "dve": 2, "pe": 2}

    def _deps(self, eng, reads, writes):
        need = {}
        for b in reads:
            if b.multi:
                for s, v in b.ws.items():
                    if need.get(s, 0) < v:
                        need[s] = v
                continue
            if b.lw is not None:
                s, v = b.lw
                if need.get(s, 0) < v:
                    need[s] = v
            if b.excl:
                for s, v in b.rd.items():
                    if s != eng and need.get(s, 0) < v:
                        need[s] = v
        for b in writes:
            if b.multi:
                continue
            if b.lw is not None:
                s, v = b.lw
                if need.get(s, 0) < v:
                    need[s] = v
            for s, v in b.rd.items():
                if need.get(s, 0) < v:
                    need[s] = v
        out = []
        w = self.waited[eng]
        for s, v in need.items():
            if s == "pe" and eng == "pe":
                continue
            if w.get(s, 0) >= v:
                continue
            w[s] = v
            out.append((s, v))
        return out

    def op(self, eng, fn, reads=(), writes=(), dma=False):
        reads = [r.b if isinstance(r, TL) else r for r in reads]
        writes = [r.b if isinstance(r, TL) else r for r in writes]
        waits = self._deps(eng, reads, writes)
        if dma:
            slot = self.dma_n[eng] % self.NSLOT[eng]
            self.dma_n[eng] += 1
            sem = "%s_d%d" % (eng, slot)
            prev = self.cnt.get(sem, 0)
            if prev > 0 and self.waited[eng].get(sem, 0) < prev:
                self.waited[eng][sem] = prev
                waits.append((sem, prev))
        else:
            sem = eng
        inc = 16 if dma else 1
        self.cnt[sem] = self.cnt.get(sem, 0) + inc
        val = self.cnt[sem]
        self.streams[eng].append((waits, fn, sem, inc))
        self.n_ops += 1
        for b in reads:
            if b.rd.get(sem, 0) < val:
                b.rd[sem] = val
        for b in writes:
            if b.multi:
                if b.ws.get(sem, 0) < val:
                    b.ws[sem] = val
                continue
            b.lw = (sem, val)
            b.rd = {}
        return val

    def barrier_on(self, tl):
        if tl.b.lw is None:
            return
        sname, v = tl.b.lw
        for e in ("sp", "act", "pool"):
            if self.waited[e].get(sname, 0) < v:
                self.waited[e][sname] = v
                self.streams[e].append(([(sname, v)], None, None, 0))

    def barrier(self):
        snap = dict(self.cnt)
        for e in self.ENGS:
            waits = []
            for s, v in snap.items():
                if s == "pe" and e == "pe":
                    continue
                if self.waited[e].get(s, 0) < v:
                    self.waited[e][s] = v
                    waits.append((s, v))
            if waits:
                self.streams[e].append((waits, None, None, 0))

    def emit(self):
        nc = self.nc
        sems = {}
        with contextlib.ExitStack() as st:
            for s in self.cnt:
                sems[s] = st.enter_context(nc.semaphore(s))
            block = st.enter_context(nc.Block())
            engmap = {"pe": block.tensor, "act": block.scalar, "dve": block.vector,
                      "pool": block.gpsimd, "sp": block.sync}
            for e in self.ENGS:
                stream = self.streams[e]
                if not stream:
                    continue

                def body(eng, stream=stream):
                    for waits, fn, sem, inc in stream:
                        for s, v in waits:
                            eng.wait_ge(sems[s], v)
                        if fn is not None:
                            fn(eng).then_inc(sems[sem], inc)
                engmap[e](body)


class K:
    def __init__(self, debug=False, stop_after=None):
        self.debug = debug
        self.stop_after = stop_after
        import os
        self.cutk = int(os.environ.get("KCUT", "0"))
        self.cutm = int(os.environ.get("KCUTM", "99"))
        self.ktiles = int(os.environ.get("KTILES", "99"))
        self.kskip = os.environ.get("KSKIP", "").split(",")
        self.nc = bass.Bass("TRN2", target_bir_lowering=False)
        self.S = Sched(self.nc)
        self.es = contextlib.ExitStack()
        self.psr = 0
        self.rr = {}

    def dram(self, name, shape, dt, kind="Internal"):
        t = TL(self.nc.dram_tensor(name, list(shape), dt, kind=kind).ap(), name)
        t.b.multi = True
        return t

    def sb(self, st, name, shape, dt):
        return TL(st.enter_context(self.nc.sbuf_tensor(name, list(shape), dt)), name)

    def sb2(self, st, name, shape, dt):
        t = st.enter_context(self.nc.sbuf_tensor(name, list(shape), dt))
        return [TL(t, name + "_lo"), TL(t, name + "_hi")]

    def ps(self):
        p = self.psum[self.psr % 8]
        self.psr += 1
        return p

    def mm(self, out, lhsT, rhs, start, stop, R, W):
        self.S.op("pe", lambda e: e.matmul(out, lhsT=lhsT, rhs=rhs, start=start, stop=stop), R, W)

    def tr(self, out, in_, ident, R, W):
        self.S.op("pe", lambda e: e.transpose(out, in_, ident), R, W)

    def act(self, out, in_, func, R, W, scale=1.0, bias=None, eng="act"):
        if bias is None:
            self.S.op("act", lambda e: e.activation(out=out, in_=in_, func=func, scale=scale), R, W)
        else:
            self.S.op("act", lambda e: e.activation(out=out, in_=in_, func=func, scale=scale, bias=bias), R, W)

    def tt(self, eng, out, in0, in1, op, R, W):
        self.S.op(eng, lambda e: e.tensor_tensor(out=out, in0=in0, in1=in1, op=op), R, W)

    def tsc(self, eng, out, in0, s1, op0, R, W, s2=None, op1=None):
        if op1 is None:
            self.S.op(eng, lambda e: e.tensor_scalar(out=out, in0=in0, scalar1=s1, scalar2=None, op0=op0), R, W)
        else:
            self.S.op(eng, lambda e: e.tensor_scalar(out=out, in0=in0, scalar1=s1, scalar2=s2, op0=op0, op1=op1), R, W)

    def stt(self, out, in0, scalar, in1, op0, op1, R, W):
        self.S.op("dve", lambda e: e.scalar_tensor_tensor(out=out, in0=in0, scalar=scalar, in1=in1, op0=op0, op1=op1), R, W)

    def cp(self, eng, out, in_, R, W):
        if eng == "act":
            self.S.op("act", lambda e: e.activation(out=out, in_=in_, func=AF.Copy), R, W)
        else:
            self.S.op(eng, lambda e: e.tensor_copy(out=out, in_=in_), R, W)

    def scan(self, out, d0, d1, init, R, W):
        self.S.op("dve", lambda e: e.tensor_tensor_scan(out=out, data0=d0, data1=d1, initial=init,
                                                        op0=ALU.mult, op1=ALU.add), R, W)

    def dma(self, out, in_, R, W, eng="sp"):
        self.S.op(eng, lambda e: e.dma_start(out=out, in_=in_), R, W, dma=True)

    def memset(self, eng, ap, val, W):
        self.S.op(eng, lambda e: e.memset(ap, val), (), W)

    def pick(self, key, engs):
        i = self.rr.get(key, 0)
        self.rr[key] = i + 1
        return engs[i % len(engs)]

    def build(self):
        nc = self.nc
        I = lambda n, s, dt=F32: self.dram(n, s, dt, "ExternalInput")
        O = lambda n, s, dt=F32: self.dram(n, s, dt, "ExternalOutput")
        self.xs = I("xs", [TS, D])
        self.xp = I("xp", [2 * TP, D])
        self.pe = I("pe", [TS, D])
        self.cT = I("cT", [128, 16])
        self.h0r = I("h0r", [128, 512])
        self.prm = I("prm", [128, NPRM])
        self.cst = I("cst", [128, NCST])
        self.w_mod = I("w_mod", [D, 6 * D])
        self.w_in = I("w_in", [D, DIN])
        self.w_out = I("w_out", [D, D])
        self.w1 = I("w1", [D, DFF])
        self.w2 = I("w2", [DFF, D])
        self.wup = I("wup", [128, 512])
        self.aup = I("aup", [128, 512])
        self.gup = I("gup", [128, 512])
        self.lwa = I("lwa", [2, 8, 64, 64])
        self.lwx = I("lwx", [2, 8, 64, 64])
        self.ys = O("ys", [TS, D])
        self.yp = O("yp", [2 * TP, D])
        self.str_o = O("str_o", [2, 128, 512])
        self.stl_o = O("stl_o", [128, 16])
        self.xT_scr = self.dram("xT_scr", [128, 8, NTOK], F32)
        self.xb_scr = self.dram("xb_scr", [128, 4, NTOK], F32)
        self.gate_scr = self.dram("gate_scr", [128, 4, NTOK], BF16)
        self.g_scr = self.dram("g_scr", [128, 4, NTOK], BF16)
        self.bon_scr = self.dram("bon_scr", [128, 4, NTOK], BF16)
        self.y_scr = self.dram("y_scr", [128, 8, NTOK], BF16)
        self.ytok_scr = self.dram("ytok_scr", [2, NTOK, 512], F32)
        for n in ("art", "rrt", "ttt", "akt", "mrbt", "mrkt", "bh", "kh"):
            setattr(self, n + "_scr", self.dram(n + "_scr", [NCH, 128, 512], BF16))
        self.vt_scr = self.dram("vt_scr", [NCH, 64, 512], BF16)
        self.pend_scr = self.dram("pend_scr", [NCH, 128, 512], F32)
        self.w1_scr = self.dram("w1_scr", [8, 128, 8, 512], BF16)
        self.w2_scr = self.dram("w2_scr", [8, 128, 32, 128], BF16)
        if self.debug:
            self.dbg = {}

        with self.es as st0:
            self.psum = [TL(st0.enter_context(nc.psum_tensor("ps%d" % i, [128, 512], F32)), "ps%d" % i)
                         for i in range(8)]
            for p_ in self.psum:
                p_.b.excl = True
            self.prm_t = self.sb(st0, "prm_t", [128, NPRM], F32)
            self.cst_t = self.sb(st0, "cst_t", [128, NCST], F32)
            self.modT = self.sb(st0, "modT", [128, 48, 2], F32)
            self.gs1 = self.sb(st0, "gs1", [128, 8, 2], F32)
            self.gs2 = self.sb(st0, "gs2", [128, 8, 2], F32)
            self.gg1 = self.sb(st0, "gg1", [128, 8, 2], F32)
            self.gg2 = self.sb(st0, "gg2", [128, 8, 2], F32)
            self.ident = self.sb(st0, "ident", [128, 128], F32)
            self.ones_bf = self.sb(st0, "ones_bf", [128, 128], BF16)
            self.oblk_bf = self.sb(st0, "oblk_bf", [128, 128], BF16)
            self.epsT = self.sb(st0, "epsT", [128, 2], F32)
            self.misc = self.sb(st0, "misc", [128, 32], F32)
            self.stl_t = self.sb(st0, "stl_t", [128, 16], F32)
            stop = False
            with contextlib.ExitStack() as stA:
                self.win = self.sb(stA, "win", [128, 8, DIN], BF16)
                self.win.b.multi = True
                self.wup_t = self.sb(stA, "wup_t", [128, 512], BF16)
                self.aup_t = self.sb(stA, "aup_t", [128, 512], BF16)
                self.gup_t = self.sb(stA, "gup_t", [128, 512], BF16)
                for nm, fn in (("p0", self.phase0), ("pA", self.phaseA)):
                    fn()
                    self.S.barrier()
                    if self.stop_after == nm:
                        stop = True
                        break
            if not stop:
                for nm, fn in (("pB", self.phaseB), ("pC", self.phaseC)):
                    fn()
                    self.S.barrier()
                    if self.stop_after == nm:
                        break
            self.S.emit()
        return nc

    def dump(self, name, src_ap, shape, dt, R):
        o = self.dram("dbg_" + name, shape, dt, "ExternalOutput")
        self.dma(o[:], src_ap, R, [o])

    def phase0(self):
        nc = self.nc
        prm, cst = self.prm_t, self.cst_t
        self.dma(prm[:], self.prm[:], [], [prm])
        self.dma(cst[:], self.cst[:], [], [cst])
        self.cp("dve", self.ident[:], cst[:, C_ID:C_ID + 128], [cst], [self.ident])
        self.cp("dve", self.oblk_bf[:], cst[:, C_OB:C_OB + 128], [cst], [self.oblk_bf])
        self.memset("dve", self.ones_bf[:], 1.0, [self.ones_bf])
        self.memset("dve", self.epsT[:, 0:1], EPS, [self.epsT])
        self.memset("dve", self.epsT[:, 1:2], LNX_EPS, [self.epsT])
        self.tsc("dve", self.misc[:, 0:4], prm[:, P_KA:P_KA + 4], -1.0, ALU.mult, [prm], [self.misc], 1.0, ALU.add)
        with contextlib.ExitStack() as st:
            scT = self.sb(st, "scT", [128, 16], F32)
            cT = self.sb(st, "cT_t", [128, 16], F32)
            wm = [self.sb(st, "wm%d" % i, [128, 8, 512], F32) for i in range(2)]
            tmp = self.sb(st, "lam_tmp", [128, 8], F32)
            self.dma(cT[:], self.cT[:], [], [cT])
            self.act(scT[:], cT[:], AF.Silu, [cT], [scT])
            self.act(tmp[:], prm[:, P_LAM:P_LAM + 8], AF.Exp, [prm], [tmp], scale=-1.0)
            self.act(tmp[:], tmp[:], AF.Ln, [tmp], [tmp], bias=1.0)
            self.tsc("dve", self.misc[:, 4:12], tmp[:], -8.0, ALU.mult, [tmp], [self.misc])
            self.tsc("dve", self.misc[:, 12:20], tmp[:], -16.0, ALU.mult, [tmp], [self.misc])
            wsrc = self.w_mod[:].rearrange("(kc p) n -> p kc n", p=128)
            scb = self.sb(st, "scb", [128, 16], BF16)
            self.cp("dve", scb[:], scT[:], [scT], [scb])
            sc3 = scb[:].rearrange("p (k c) -> p k c", c=2)
            wmb = [self.sb(st, "wmb%d" % i, [128, 8, 512], BF16) for i in range(2)]
            stgA = [self.sb(st, "stgA%d" % i, [128, 8, 256], F32) for i in range(2)]
            wisrc = self.w_in[:].rearrange("(kc p) n -> p kc n", p=128)
            wi_blocks = [(c0, min(256, DIN - c0)) for c0 in range(0, DIN, 256)]
            sm_list = [(self.wup, self.wup_t), (self.aup, self.aup_t), (self.gup, self.gup_t)]
            nb = [0]

            def win_step():
                if wi_blocks:
                    c0, cw = wi_blocks.pop(0)
                    s_ = stgA[nb[0] % 2]
                    self.dma(s_[:, :, 0:cw], wisrc[:, :, c0:c0 + cw], [], [s_], eng="act")
                    self.cp("act" if nb[0] % 2 == 0 else "pool", self.win[:, :, c0:c0 + cw], s_[:, :, 0:cw], [s_], [self.win])
                    nb[0] += 1
                elif sm_list:
                    src, dstt = sm_list.pop(0)
                    s_ = stgA[nb[0] % 2]
                    nb[0] += 1
                    s2 = s_[:].rearrange("p a b -> p (a b)")[:, 0:512]
                    self.dma(s2, src[:], [], [s_], eng="act")
                    self.cp("pool", dstt[:], s2, [s_], [dstt])

            for blk in range(12):
                w = wm[blk % 2]
                wb_ = wmb[blk % 2]
                self.dma(w[:], wsrc[:, :, blk * 512:(blk + 1) * 512], [], [w], eng="sp")
                self.cp("dve", wb_[:], w[:], [w], [wb_])
                win_step()
                p = self.ps()
                for m in range(4):
                    for kc in range(8):
                        self.mm(p[:, 2 * m:2 * m + 2], wb_[:, kc, m * 128:(m + 1) * 128], sc3[:, kc, :],
                                kc == 0, kc == 7, [wb_, scb], [p])
                for m in range(4):
                    mi = blk * 4 + m
                    self.tsc("dve", self.modT[:, mi, :], p[:, 2 * m:2 * m + 2], prm[:, P_BMOD + mi:P_BMOD + mi + 1],
                             ALU.add, [p, prm], [self.modT])
            while wi_blocks or sm_list:
                win_step()
            m3 = self.modT
            for (dst, sc_off, g_off, one) in ((self.gs1, 8, P_GPRE, 1.0), (self.gs2, 32, P_GPRE2, 1.0),
                                              (self.gg1, 16, P_GPOST, 0.0), (self.gg2, 40, P_GPOST2, 0.0)):
                for c in range(2):
                    self.tsc("dve", dst[:, :, c], m3[:, sc_off:sc_off + 8, c], one, ALU.add, [m3], [dst])
                    self.tt("dve", dst[:, :, c], dst[:, :, c], prm[:, g_off:g_off + 8], ALU.mult, [dst, prm], [dst])
            if self.debug:
                self.dump("modT", self.modT[:], [128, 48, 2], F32, [self.modT])
                self.dump("gs1", self.gs1[:], [128, 8, 2], F32, [self.gs1])

    def load_cast(self, st, dst_ap, dst_tl, src_ap, shape, tag):
        key = "stg_" + tag
        if not hasattr(self, key):
            setattr(self, key, [self.sb(st, "%s%d" % (key, i), shape, F32) for i in range(2)])
        ring = getattr(self, key)
        s = ring[self.rr.get(key, 0) % 2]
        self.rr[key] = self.rr.get(key, 0) + 1
        self.dma(s[:], src_ap, [], [s], eng="sp")
        eng = self.pick("castE", ["act", "pool"])
        self.cp(eng, dst_ap, s[:], [s], [dst_tl])

    def phaseA(self):
        nc = self.nc
        prm, cst = self.prm_t, self.cst_t
        with contextlib.ExitStack() as st:
            win, wup, aup, gup = self.win, self.wup_t, self.aup_t, self.gup_t
            mSI = cst[:, C_MSI:C_MSI + 128].rearrange("p (q t) -> p q t", q=2)
            mL = cst[:, C_ML:C_ML + 64]
            idS = cst[:, C_IDS:C_IDS + 64]

            xin = self.sb(st, "xin", [128, 2, D], F32)
            xT = self.sb(st, "xT", [128, 8, TT], F32)
            sq = self.sb(st, "sq", [128, 8, TT], BF16)
            hT = self.sb(st, "hT", [128, 8, TT], BF16)
            rstd = self.sb(st, "rstd", [128, TT], F32)
            tmpA = [self.sb(st, "tmpA%d" % i, [128, TT], F32) for i in range(2)]
            rT = self.sb(st, "rT", [128, 4, TT], F32)
            kT = self.sb(st, "kT", [128, 4, TT], F32)
            vT = self.sb(st, "vT", [128, 4, TT], F32)
            xw = self.sb(st, "xw", [128, TT], BF16)
            xa = self.sb(st, "xa", [128, TT], BF16)
            xg = self.sb(st, "xg", [128, TT], BF16)
            xbT = self.sb(st, "xbT", [128, 4, TT], F32)
            gtmp = [self.sb(st, "gtmp%d" % i, [128, TT], F32) for i in range(3)]
            gate = self.sb(st, "gate", [128, 4, TT], BF16)
            gT = self.sb(st, "gT", [128, 4, TT], BF16)
            kkn = self.sb(st, "kkn", [128, 4, TT], F32)
            ksum = self.sb(st, "ksum", [128, 4, TT], F32)
            bon = self.sb(st, "bon", [128, 4, TT], BF16)
            sg = self.sb(st, "sg", [128, 4, TT], F32)
            cs = self.sb(st, "cs", [128, 4, TT], F32)
            E1 = self.sb(st, "E1", [128, 4, TT], F32)
            ad = self.sb(st, "ad", [128, 4, TT], F32)
            wk1 = self.sb(st, "wk1", [128, 4, TT], F32)
            wk2 = self.sb(st, "wk2", [128, 4, TT], F32)
            NC4 = TT // 64
            AR = [self.sb(st, "AR%d" % d, [128, 4, NC4, 2, 64], BF16) for d in range(2)]
            BK = [self.sb(st, "BK%d" % d, [128, 4, NC4, 2, 64], BF16) for d in range(2)]
            PEb1 = self.sb(st, "PEb", [128, 4, NC4, 64], F32)
            PEb = [PEb1, PEb1]
            tokB1 = self.sb(st, "tokB", [128, 2, 8, 64], BF16)
            tokK1 = self.sb(st, "tokK", [128, 2, 8, 64], BF16)
            tokB, tokK = [tokB1, tokB1], [tokK1, tokK1]
            tokV = self.sb(st, "tokV", [128, 2, 8, 64], BF16)
            NSET = 2
            MRBs = [self.sb(st, "MRBs%d" % i, [64, 8, 64], BF16) for i in range(NSET)]
            Lt0s = [self.sb(st, "Lt0_%d" % i, [64, 8, 64], F32R) for i in range(NSET)]
            Tfins = [self.sb(st, "Tfin%d" % i, [64, 8, 64], BF16) for i in range(NSET)]
            SCk = self.sb(st, "SCk", [128, 2, 8, 64], BF16)
            Lms = [[self.sb(st, "Lm%d_%d" % (i, k), [64, 8, 64], F32R) for i in range(2)] for k in range(NSET)]
            Ltms = [[self.sb(st, "Ltm%d_%d" % (i, k), [64, 8, 64], F32R) for i in range(2)] for k in range(NSET)]
            ILms = [self.sb(st, "ILm_%d" % k, [64, 8, 64], F32R) for k in range(NSET)]
            Ttms = [[self.sb(st, "Ttm%d_%d" % (i, k), [64, 8, 64], F32R) for i in range(2)] for k in range(NSET)]

            ones_bf, oblk, ident = self.ones_bf, self.oblk_bf, self.ident
            tiles = [(0, t0, True, 0) for t0 in range(0, TS, TT)] + [(1, TS, False, 1), (2, TS + TP, False, 1)]
            mS1 = cst[0:64, C_MSI1:C_MSI1 + 128].rearrange("p (q t) -> p q t", q=2)
            mL1 = cst[0:64, C_ML1:C_ML1 + 64]
            id64 = idS[0:64]
            loaded = set()

            def load_x(g0, is_s):
                if g0 in loaded:
                    return
                loaded.add(g0)
                if is_s:
                    src = self.xs[g0:g0 + TT, :].rearrange("(s p) f -> p s f", p=128)
                else:
                    l0 = g0 - TS
                    src = self.xp[l0:l0 + TT, :].rearrange("(s p) f -> p s f", p=128)
                self.dma(xin[:], src, [], [xin])
                if is_s:
                    petv = xT[:].rearrange("p a b -> p (a b)").rearrange("p (s f) -> p s f", s=2)
                    self.dma(petv, self.pe[g0:g0 + TT, :].rearrange("(s p) f -> p s f", p=128), [], [xT], eng="act")

            def front(seq, g0, is_s, mc, nxt=None):
                load_x(g0, is_s)
                if is_s:
                    petv = xT[:].rearrange("p a b -> p (a b)").rearrange("p (s f) -> p s f", s=2)
                    self.tt("pool", xin[:], xin[:], petv, ALU.add, [xin, xT], [xin])
                for j in range(8):
                    p = self.ps()
                    for s in range(2):
                        self.tr(p[:, s * 128:(s + 1) * 128], xin[:, s, j * 128:(j + 1) * 128], ident[:], [xin, ident], [p])
                    self.cp("act", xT[:, j, :], p[:, 0:TT], [p], [xT])
                    self.tt("pool", sq[:, j, :], xT[:, j, :], xT[:, j, :], ALU.mult, [xT], [sq])
                    yield
                self.dma(self.xT_scr[:, :, g0:g0 + TT], xT[:], [xT], [self.xT_scr])
                p = self.ps()
                for j in range(8):
                    self.mm(p[:, 0:TT], ones_bf[:], sq[:, j, :], j == 0, j == 7, [ones_bf, sq], [p])
                self.act(rstd[:], p[:, 0:TT], AF.Sqrt, [p, self.epsT], [rstd], scale=1.0 / D, bias=self.epsT[:, 0:1])
                self.S.op("dve", lambda e: e.reciprocal(out=rstd[:], in_=rstd[:]), [rstd.b], [rstd.b])
                for j in range(8):
                    t = tmpA[j % 2]
                    self.tt("dve", t[:], xT[:, j, :], rstd[:], ALU.mult, [xT, rstd], [t])
                    self.act(hT[:, j, :], t[:], AF.Identity, [t, self.gs1, self.modT], [hT],
                             scale=self.gs1[:, j, mc:mc + 1], bias=self.modT[:, j, mc:mc + 1])
                    yield
                for m in range(23):
                    if m >= self.cutm:
                        break
                    p = self.ps()
                    for kc in range(8):
                        self.mm(p[:, 0:TT], win[:, kc, m * 128:(m + 1) * 128], hT[:, kc, :], kc == 0, kc == 7, [win, hT], [p])
                    pz = p[:, 0:TT]
                    if m < 4:
                        self.cp("act", rT[:, m, :], pz, [p], [rT])
                    elif m < 8:
                        self.cp("act", kT[:, m - 4, :], pz, [p], [kT])
                    elif m < 12:
                        self.cp("act", vT[:, m - 8, :], pz, [p], [vT])
                    elif m == 12:
                        self.act(xw[:], pz, AF.Tanh, [p], [xw])
                    elif m == 13:
                        self.cp("act", xa[:], pz, [p], [xa])
                    elif m == 14:
                        self.act(xg[:], pz, AF.Sigmoid, [p], [xg])
                    elif m < 19:
                        self.cp("act", xbT[:, m - 15, :], pz, [p], [xbT])
                    else:
                        j = m - 19
                        g0_, g1_, g2_ = gtmp
                        self.cp("act", g0_[:], pz, [p], [g0_])
                        self.tt("pool", g1_[:], g0_[:], g0_[:], ALU.mult, [g0_], [g1_])
                        self.tsc("dve", g1_[:], g1_[:], 0.044715, ALU.mult, [g1_], [g1_], 1.0, ALU.add)
                        self.tt("dve", g1_[:], g1_[:], g0_[:], ALU.mult, [g1_, g0_], [g1_])
                        self.act(g2_[:], g1_[:], AF.Sigmoid, [g1_], [g2_], scale=GELU_C)
                        self.tt("pool", gate[:, j, :], g0_[:], g2_[:], ALU.mult, [g0_, g2_], [gate])
                    yield
                self.dma(self.xb_scr[:, :, g0:g0 + TT], xbT[:], [xbT], [self.xb_scr])
                if self.debug and g0 == 0:
                    self.dump("hT", hT[:], [128, 8, TT], BF16, [hT])
                    self.dump("rT", rT[:], [128, 4, TT], F32, [rT])
                    self.dump("vT", vT[:], [128, 4, TT], F32, [vT])
                    self.dump("xbT", xbT[:], [128, 4, TT], F32, [xbT])
                    self.dump("gate", gate[:], [128, 4, TT], BF16, [gate])
                self.dma(self.gate_scr[:, :, g0:g0 + TT], gate[:], [gate], [self.gate_scr])
                for j in range(4):
                    p = self.ps()
                    self.mm(p[:, 0:TT], gup[:, j * 128:(j + 1) * 128], xg[:], True, True, [gup, xg], [p])
                    self.cp("act", gT[:, j, :], p[:, 0:TT], [p], [gT])
                    yield
                self.dma(self.g_scr[:, :, g0:g0 + TT], gT[:], [gT], [self.g_scr])
                for j in range(4):
                    self.tsc("dve", kkn[:, j, :], kT[:, j, :], prm[:, P_KK + j:P_KK + j + 1], ALU.mult, [kT, prm], [kkn])
                    self.tt("pool", sq[:, j, :], kkn[:, j, :], kkn[:, j, :], ALU.mult, [kkn], [sq])
                for j in range(4):
                    p = self.ps()
                    self.mm(p[:, 0:TT], oblk[:], sq[:, j, :], True, True, [oblk, sq], [p])
                    t = tmpA[j % 2]
                    self.act(t[:], p[:, 0:TT], AF.Sqrt, [p], [t])
                    self.tsc("dve", t[:], t[:], 1e-12, ALU.max, [t], [t])
                    self.S.op("dve", lambda e, t=t: e.reciprocal(out=t[:], in_=t[:]), [t.b], [t.b])
                    self.tt("dve", kkn[:, j, :], kkn[:, j, :], t[:], ALU.mult, [kkn, t], [kkn])
                    yield
                for s in range(2):
                    p = self.ps()
                    for j in range(4):
                        self.tr(p[:, j * 128:(j + 1) * 128], vT[:, j, s * 128:(s + 1) * 128], ident[:], [vT, ident], [p])
                    self.cp("act", tokV[:, s, :, :].rearrange("p h k -> p (h k)"), p[:], [p], [tokV])
                c0 = g0 // 64
                for s in range(2):
                    dst = self.vt_scr[c0 + 2 * s:c0 + 2 * s + 2, :, :].rearrange("c s f -> (c s) f")
                    self.dma(dst, tokV[:, s, :, :].rearrange("p h k -> p (h k)"), [tokV], [self.vt_scr])
                if nxt is not None:
                    load_x(nxt[1], nxt[2])
                yield
            def prep(g0, d):
                c0 = g0 // 64
                for j in range(4):
                    p = self.ps()
                    self.mm(p[:, 0:TT], wup[d * 64:(d + 1) * 64, j * 128:(j + 1) * 128], xw[d * 64:(d + 1) * 64, :],
                            True, True, [wup, xw], [p])
                    self.act(sg[:, j, :], p[:, 0:TT], AF.Sigmoid, [p, prm], [sg],
                             bias=prm[:, P_W0 + 4 * d + j:P_W0 + 4 * d + j + 1])
                    if d == 0:
                        self.scan(cs[:, j, :], cst[:, C_RMF:C_RMF + TT], sg[:, j, :], 0.0, [cst, sg], [cs])
                    else:
                        self.scan(cs[:, j, ::-1], cst[:, C_RMB:C_RMB + TT][:, ::-1], sg[:, j, ::-1], 0.0, [cst, sg], [cs])
                    yield
                for j in range(4):
                    p = self.ps()
                    self.mm(p[:, 0:TT], aup[d * 64:(d + 1) * 64, j * 128:(j + 1) * 128], xa[d * 64:(d + 1) * 64, :],
                            True, True, [aup, xa], [p])
                    self.act(ad[:, j, :], p[:, 0:TT], AF.Sigmoid, [p, prm], [ad],
                             bias=prm[:, P_A0 + 4 * d + j:P_A0 + 4 * d + j + 1])
                    yield
                self.tt("dve", sg[:], cs[:], sg[:], ALU.subtract, [cs, sg], [sg])
                self.act(E1[:], cs[:], AF.Exp, [cs], [E1], scale=-LAM)
                self.act(cs[:], cs[:], AF.Exp, [cs], [cs], scale=LAM)
                self.act(sg[:], sg[:], AF.Exp, [sg], [sg], scale=-LAM)
                E2, E3 = cs, sg
                ar5 = AR[d]
                bk5 = BK[d]
                v4 = lambda tl: tl[:].rearrange("p j (c t) -> p j c t", t=64)
                self.stt(ar5[:, :, :, 0, :], v4(kkn), -1.0, v4(E3), ALU.mult, ALU.mult, [kkn, E3], [ar5])
                self.tt("pool", ar5[:, :, :, 1, :], v4(rT), v4(E1), ALU.mult, [rT, E1], [ar5])
                yield
                self.tt("dve", wk1[:], kkn[:], ad[:], ALU.mult, [kkn, ad], [wk1])
                self.tt("dve", wk1[:], wk1[:], E2[:], ALU.mult, [wk1, E2], [wk1])
                self.cp("act", bk5[:, :, :, 0, :], v4(wk1), [wk1], [bk5])
                yield
                for j in range(4):
                    self.tsc("dve", wk2[:, j, :], ad[:, j, :], prm[:, P_KA + j:P_KA + j + 1], ALU.mult, [ad, prm, self.misc], [wk2],
                             self.misc[:, j:j + 1], ALU.add)
                self.tt("dve", wk2[:], wk2[:], kT[:], ALU.mult, [wk2, kT], [wk2])
                if d == 0:
                    self.cp("pool", ksum[:], wk2[:], [wk2], [ksum])
                else:
                    self.tt("pool", ksum[:], ksum[:], wk2[:], ALU.add, [ksum, wk2], [ksum])
                self.tt("dve", wk2[:], wk2[:], E2[:], ALU.mult, [wk2, E2], [wk2])
                self.cp("act", bk5[:, :, :, 1, :], v4(wk2), [wk2], [bk5])
                yield
                te = 63 if d == 0 else 0
                pend_b = v4(E1)[:, :, :, te:te + 1].to_broadcast([128, 4, NC4, 64])
                self.cp("act", PEb[d][:], pend_b, [E1], [PEb[d]])
                self.tt("dve", v4(wk1), v4(wk1), PEb[d][:], ALU.mult, [wk1, PEb[d]], [wk1])
                self.tt("dve", v4(wk2), v4(wk2), PEb[d][:], ALU.mult, [wk2, PEb[d]], [wk2])
                yield
                for (srcw, tokX, scr) in ((wk1, tokB[d], self.bh_scr), (wk2, tokK[d], self.kh_scr)):
                    for s in range(2):
                        p = self.ps()
                        for j in range(4):
                            self.tr(p[:, j * 128:(j + 1) * 128], srcw[:, j, s * 128:(s + 1) * 128], ident[:], [srcw, ident], [p])
                        self.cp("act", tokX[:, s, :, :].rearrange("p h k -> p (h k)"), p[:], [p], [tokX])
                        for cc in range(2):
                            self.dma(scr[c0 + 2 * s + cc, d * 64:(d + 1) * 64, :],
                                     tokX[cc * 64:(cc + 1) * 64, s, :, :].rearrange("p h k -> p (h k)"),
                                     [tokX], [scr])
                        yield
                for cl in range(NC4):
                    c = c0 + cl
                    for hp in range(2):
                        for (q, scr) in ((0, self.art_scr), (1, self.rrt_scr)):
                            dst = scr[c, d * 64:(d + 1) * 64, :].rearrange("k (j hp t) -> k j hp t", hp=2, t=64)[:, :, hp, :]
                            self.dma(dst, ar5[hp * 64:(hp + 1) * 64, :, cl, q, :], [ar5], [scr], eng="sp")
                        dst = self.pend_scr[c, d * 64:(d + 1) * 64, :].rearrange("k (j hp t) -> k j hp t", hp=2, t=64)[:, :, hp, :]
                        self.dma(dst, PEb[d][hp * 64:(hp + 1) * 64, :, cl, :], [PEb[d]], [self.pend_scr], eng="sp")
                yield
            def bonus(g0):
                for j in range(4):
                    self.stt(sq[:, j, :], rT[:, j, :], prm[:, P_RK + j:P_RK + j + 1], ksum[:, j, :], ALU.mult, ALU.mult,
                             [rT, prm, ksum], [sq])
                    p = self.ps()
                    self.mm(p[:, 0:TT], oblk[:], sq[:, j, :], True, True, [oblk, sq], [p])
                    self.tt("dve", bon[:, j, :], p[:, 0:TT], vT[:, j, :], ALU.mult, [p, vT], [bon])
                self.dma(self.bon_scr[:, :, g0:g0 + TT], bon[:], [bon], [self.bon_scr])
                yield
            def chunk_sck(g0):
                c0 = g0 // 64
                for cl in range(NC4):
                    c = c0 + cl
                    for hp in range(2):
                        p = self.ps()
                        for j in range(4):
                            for d in range(2):
                                self.mm(p[d * 64:(d + 1) * 64, j * 128:(j + 1) * 128],
                                        BK[d][hp * 64:(hp + 1) * 64, j, cl, 1, :],
                                        AR[d][hp * 64:(hp + 1) * 64, j, cl, :, :].rearrange("p q t -> p (q t)"),
                                        True, True, [BK[d], AR[d]], [p])
                        self.tt("dve", SCk[:, :, hp::2, :].rearrange("p q h t -> p h q t"),
                                p[:].rearrange("p (h q t) -> p h q t", q=2, t=64),
                                mSI.unsqueeze(1).to_broadcast([128, 4, 2, 64]), ALU.mult, [p, cst], [SCk])
                    self.dma(self.akt_scr[c], SCk[:, 0, :, :].rearrange("p h s -> p (h s)"), [SCk], [self.akt_scr], eng="act")
                    self.dma(self.mrkt_scr[c], SCk[:, 1, :, :].rearrange("p h s -> p (h s)"), [SCk], [self.mrkt_scr], eng="act")
                    yield
            def chunk_d(g0, d, cls, k):
                c0 = g0 // 64
                Lm, Ltm, ILm, Ttm, Lt0, Tfin = Lms[k], Ltms[k], ILms[k], Ttms[k], Lt0s[k], Tfins[k]
                MRB = MRBs[k]
                for cl in cls:
                    c = c0 + cl
                    msk = mSI[0:64] if d == 0 else mS1
                    mskL = mL[0:64] if d == 0 else mL1
                    L0 = Lm[0]
                    for hp in range(2):
                        p = self.ps()
                        for j in range(4):
                            self.mm(p[0:64, j * 128:(j + 1) * 128],
                                    BK[d][hp * 64:(hp + 1) * 64, j, cl, 0, :],
                                    AR[d][hp * 64:(hp + 1) * 64, j, cl, :, :].rearrange("p q t -> p (q t)"),
                                    True, True, [BK[d], AR[d]], [p])
                        p4 = p[0:64, :].rearrange("p (h q t) -> p h q t", q=2, t=64)
                        self.tt("dve", Lt0[:, hp::2, :], p4[:, :, 0, :], msk[:, 0, :].unsqueeze(1).to_broadcast([64, 4, 64]),
                                ALU.mult, [p, cst], [Lt0])
                        self.tt("dve", MRB[:, hp::2, :], p4[:, :, 1, :], msk[:, 1, :].unsqueeze(1).to_broadcast([64, 4, 64]),
                                ALU.mult, [p, cst], [MRB])
                        p2 = self.ps()
                        for j in range(4):
                            self.mm(p2[0:64, j * 64:(j + 1) * 64],
                                    AR[d][hp * 64:(hp + 1) * 64, j, cl, 0, :], BK[d][hp * 64:(hp + 1) * 64, j, cl, 0, :],
                                    True, True, [AR[d], BK[d]], [p2])
                        self.tt("dve", L0[:, hp::2, :], p2[0:64, 0:256].rearrange("p (h s) -> p h s", s=64),
                                mskL.unsqueeze(1).to_broadcast([64, 4, 64]), ALU.mult, [p2, cst], [L0])
                    self.dma(self.mrbt_scr[c, d * 64:(d + 1) * 64, :], MRB[:].rearrange("p h s -> p (h s)"),
                             [MRB], [self.mrbt_scr], eng="act")
                    yield
                    T0 = Ttm[0]
                    self.tt("pool", T0[:], Lt0[:].bitcast(F32), id64.unsqueeze(1).to_broadcast([64, 8, 64]), ALU.add,
                            [Lt0, cst], [T0])
                    L_prev, Tt_prev, Lt_prev = L0, T0, Lt0
                    for lev in range(1, 6):
                        L_new, Lt_new, Tt_new = Lm[lev % 2], Ltm[lev % 2], Ttm[lev % 2]
                        pA = self.ps()
                        for h in range(8):
                            self.mm(pA[0:64, h * 64:(h + 1) * 64], Lt_prev[:, h, :], L_prev[:, h, :], True, True,
                                    [Lt_prev, L_prev], [pA])
                        if lev < 5:
                            pB = self.ps()
                            for h in range(8):
                                self.mm(pB[0:64, h * 64:(h + 1) * 64], L_prev[:, h, :], Lt_prev[:, h, :], True, True,
                                        [Lt_prev, L_prev], [pB])
                        self.tt("dve", ILm[:], pA[0:64, :].rearrange("p (h s) -> p h s", s=64),
                                id64.unsqueeze(1).to_broadcast([64, 8, 64]), ALU.add, [pA, cst], [ILm])
                        if lev < 5:
                            self.cp("dve", L_new[:].rearrange("p h s -> p (h s)"), pA[0:64, :], [pA], [L_new])
                            self.cp("act", Lt_new[:].rearrange("p h s -> p (h s)"), pB[0:64, :], [pB], [Lt_new])
                        yield
                        pC = self.ps()
                        for h in range(8):
                            self.mm(pC[0:64, h * 64:(h + 1) * 64], ILm[:, h, :], Tt_prev[:, h, :], True, True,
                                    [ILm, Tt_prev], [pC])
                        if lev < 5:
                            self.cp("act", Tt_new[:].rearrange("p h s -> p (h s)"), pC[0:64, :], [pC], [Tt_new])
                        else:
                            self.cp("act", Tfin[:].rearrange("p h s -> p (h s)"), pC[0:64, :], [pC], [Tfin])
                        L_prev, Tt_prev, Lt_prev = L_new, Tt_new, Lt_new
                        yield
                    self.dma(self.ttt_scr[c, d * 64:(d + 1) * 64, :], Tfin[:].rearrange("p h s -> p (h s)"),
                             [Tfin], [self.ttt_scr], eng="act")


            def run_all(*gens):
                gens = list(gens)
                while gens:
                    for g in list(gens):
                        try:
                            next(g)
                        except StopIteration:
                            gens.remove(g)

            def seq_(*gens):
                for g in gens:
                    yield from g

            prev = None
            tl_ = tiles[:self.ktiles]
            for ti_, (seq, g0, is_s, mc) in enumerate(tl_):
                nxt = tl_[ti_ + 1] if ti_ + 1 < len(tl_) else None
                if prev is None:
                    run_all(front(seq, g0, is_s, mc, nxt))
                else:
                    run_all(seq_(chunk_d(prev, 1, [0, 1], 0), chunk_sck(prev)), chunk_d(prev, 1, [2, 3], 1), front(seq, g0, is_s, mc, nxt))
                run_all(prep(g0, 0))
                run_all(chunk_d(g0, 0, [0, 1], 0), chunk_d(g0, 0, [2, 3], 1), prep(g0, 1))
                run_all(bonus(g0))
                prev = g0
            run_all(seq_(chunk_d(prev, 1, [0, 1], 0), chunk_sck(prev)), chunk_d(prev, 1, [2, 3], 1))

    def phaseB(self):
        with contextlib.ExitStack() as st:
            side = [self.gen_c0(st), self.phaseB_lru(st)]
            post = self.phaseB_post(st)
            next(post)
            done = np.zeros((2, NCH), bool)
            posted = [False] * (NTOK // 128)
            si = 0
            for info in self.phaseB_chain(st):
                for (d, c) in info:
                    done[d, c] = True
                for _ in range(2):
                    if side:
                        g = side[si % len(side)]
                        si += 1
                        try:
                            next(g)
                        except StopIteration:
                            side.remove(g)
                for b in range(NTOK // 128):
                    if not posted[b] and done[:, 2 * b:2 * b + 2].all():
                        posted[b] = True
                        post.send(b)
            for g in side:
                for _ in g:
                    pass
            for b in range(NTOK // 128):
                if not posted[b]:
                    post.send(b)
            if self.debug:
                self.S.barrier()
                self.dump("yscr", self.y_scr[:], [128, 8, NTOK], BF16, [self.y_scr])

    def gen_c0(self, st):
        stg = [self.sb(st, "stgC%d" % i, [128, 8, 512], F32) for i in range(2)]
        wb = [self.sb(st, "wbC%d" % i, [128, 8, 512], BF16) for i in range(2)]
        w1src = self.w1[:].rearrange("(kc p) n -> p kc n", p=128)
        for blk in range(8):
            s, o = stg[blk % 2], wb[blk % 2]
            self.dma(s[:], w1src[:, :, blk * 512:(blk + 1) * 512], [], [s])
            self.cp("pool", o[:], s[:], [s], [o])
            self.dma(self.w1_scr[blk], o[:], [o], [self.w1_scr], eng="act")
            yield
        w2src = self.w2[:].rearrange("(fc p) n -> p fc n", p=128)
        for m in range(8):
            s, o = stg[m % 2], wb[m % 2]
            s4 = s[:].rearrange("p k (a b) -> p (k a) b", b=128)
            o4 = o[:].rearrange("p k (a b) -> p (k a) b", b=128)
            self.dma(s4, w2src[:, :, m * 128:(m + 1) * 128], [], [s])
            self.cp("pool", o[:], s[:], [s], [o])
            self.dma(self.w2_scr[m], o4, [o], [self.w2_scr], eng="act")
            yield

    def phaseB_lru(self, st):
        prm, cst, misc = self.prm_t, self.cst_t, self.misc
        if True:
            wbd32 = self.sb(st, "wbd32", [128, 16, 128], F32)
            wbd = self.sb(st, "wbd", [128, 16, 128], BF16)
            self.memset("pool", wbd32[:], 0.0, [wbd32])
            self.S.barrier_on(wbd32)
            wbd32.b.multi = True
            for gi, src in enumerate((self.lwa, self.lwx)):
                for d in range(2):
                    for j in range(4):
                        for hb in range(2):
                            self.dma(wbd32[hb * 64:(hb + 1) * 64, (gi * 2 + d) * 4 + j, hb * 64:(hb + 1) * 64],
                                     src[d, 2 * j + hb], [], [wbd32])
            self.cp("dve", wbd[:], wbd32[:], [wbd32], [wbd])
            TM = TS
            xbp = self.sb(st, "xbp", [128, TM + 4], F32)
            xc = self.sb(st, "xc", [128, TM], F32)
            xcb = self.sb(st, "xcb", [128, TM], BF16)
            gt = self.sb(st, "gt_l", [128, TM], BF16)
            a_t = self.sb(st, "a_t", [128, TM], F32)
            bx_t = self.sb(st, "bx_t", [128, TM], F32)
            s_t = self.sb(st, "s_t", [128, TM], F32)
            hs = [self.sb(st, "hs%d" % d, [128, TM], F32) for d in range(2)]
            yb = self.sb(st, "yb", [128, TM], BF16)
            for (seq, g0, T) in ((0, 0, TS), (1, TS, TP), (2, TS + TP, TP)):
                for j in range(4):
                    self.memset("pool", xbp[:, 0:2], 0.0, [xbp])
                    self.memset("pool", xbp[:, T + 2:T + 4], 0.0, [xbp])
                    self.dma(xbp[:, 2:T + 2], self.xb_scr[:, j, g0:g0 + T], [self.xb_scr], [xbp])
                    self.dma(gt[:, 0:T], self.gate_scr[:, j, g0:g0 + T], [self.gate_scr], [gt], eng="act")
                    cw = lambda i: prm[:, P_CW + 4 * i + j:P_CW + 4 * i + j + 1]
                    self.act(xc[:, 0:T], xbp[:, 0:T], AF.Identity, [xbp, prm], [xc], scale=cw(0), bias=prm[:, P_CB + j:P_CB + j + 1])
                    for i in range(1, 4):
                        self.stt(xc[:, 0:T], xbp[:, i:i + T], cw(i), xc[:, 0:T], ALU.mult, ALU.add, [xbp, prm, xc], [xc])
                    self.cp("pool", xcb[:, 0:T], xc[:, 0:T], [xc], [xcb])
                    yield
                    for d in range(2):
                        for t0 in range(0, T, 512):
                            tw = min(512, T - t0)
                            p = self.ps()
                            self.mm(p[:, 0:tw], wbd[:, (0 * 2 + d) * 4 + j, :], xcb[:, t0:t0 + tw], True, True, [wbd, xcb], [p])
                            self.act(s_t[:, t0:t0 + tw], p[:, 0:tw], AF.Sigmoid, [p, prm], [s_t],
                                     bias=prm[:, P_BA + 4 * d + j:P_BA + 4 * d + j + 1])
                            p2 = self.ps()
                            self.mm(p2[:, 0:tw], wbd[:, (1 * 2 + d) * 4 + j, :], xcb[:, t0:t0 + tw], True, True, [wbd, xcb], [p2])
                            self.act(bx_t[:, t0:t0 + tw], p2[:, 0:tw], AF.Sigmoid, [p2, prm], [bx_t],
                                     bias=prm[:, P_BX + 4 * d + j:P_BX + 4 * d + j + 1])
                        col = 4 + d * 4 + j
                        self.act(a_t[:, 0:T], s_t[:, 0:T], AF.Exp, [s_t, misc], [a_t], scale=misc[:, col:col + 1])
                        self.act(s_t[:, 0:T], s_t[:, 0:T], AF.Exp, [s_t, misc], [s_t], scale=misc[:, col + 8:col + 9])
                        self.act(s_t[:, 0:T], s_t[:, 0:T], AF.Sqrt, [s_t], [s_t], scale=-1.0, bias=1.0)
                        self.tt("pool", bx_t[:, 0:T], bx_t[:, 0:T], xc[:, 0:T], ALU.mult, [bx_t, xc], [bx_t])
                        self.tt("dve", bx_t[:, 0:T], bx_t[:, 0:T], s_t[:, 0:T], ALU.mult, [bx_t, s_t], [bx_t])
                        h = hs[d]
                        if seq == 0:
                            init = prm[:, P_H0 + 4 * d + j:P_H0 + 4 * d + j + 1]
                        else:
                            init = 0.0
                        if d == 0:
                            self.scan(h[:, 0:T], a_t[:, 0:T], bx_t[:, 0:T], init, [a_t, bx_t, prm], [h])
                        else:
                            self.scan(h[:, 0:T][:, ::-1], a_t[:, 0:T][:, ::-1], bx_t[:, 0:T][:, ::-1], init, [a_t, bx_t, prm], [h])
                        if seq > 0:
                            col_o = j * 4 + (seq - 1) * 2 + d
                            te = T - 1 if d == 0 else 0
                            self.cp("pool", self.stl_t[:, col_o:col_o + 1], h[:, te:te + 1], [h], [self.stl_t])
                        yield
                    self.tt("pool", hs[0][:, 0:T], hs[0][:, 0:T], hs[1][:, 0:T], ALU.add, [hs[0], hs[1]], [hs[0]])
                    self.tt("dve", yb[:, 0:T], hs[0][:, 0:T], gt[:, 0:T], ALU.mult, [hs[0], gt], [yb])
                    self.dma(self.y_scr[:, 4 + j, g0:g0 + T], yb[:, 0:T], [yb], [self.y_scr])
            self.dma(self.stl_o[:], self.stl_t[:], [self.stl_t], [self.stl_o])

    def phaseB_chain(self, st):
        if True:
            NB = 3
            def ring(name, dt=BF16):
                return [self.sb2(st, "%s%d" % (name, i), [128, 512], dt) for i in range(NB)]
            art, rrt, ttt, akt, mrbt, mrkt, bh, kh, vt = [ring(n) for n in
                                                          ("c_art", "c_rrt", "c_ttt", "c_akt", "c_mrbt", "c_mrkt", "c_bh", "c_kh", "c_vt")]
            pend = ring("c_pend", F32)
            Hf = self.sb2(st, "Hf", [128, 512], F32)
            Hb = self.sb2(st, "Hb", [128, 512], BF16)
            Zs = self.sb2(st, "Zs", [128, 512], BF16)
            Us = self.sb2(st, "Us", [128, 512], BF16)
            Yt = [self.sb2(st, "Yt%d" % i, [128, 512], F32) for i in range(2)]
            tmpH = self.sb2(st, "tmpH", [128, 512], F32)
            hs_ = lambda h: slice(h * 64, (h + 1) * 64)
            steps = []
            for (seq, cbase, n) in ((0, 0, 32), (1, 32, 4), (2, 36, 4)):
                for i in range(n):
                    steps.append((seq, cbase, n, i))

            def loads(k):
                seq, cbase, n, i = steps[k]
                r = k % NB
                for d in range(2):
                    sl = slice(d * 64, (d + 1) * 64)
                    c = cbase + i if d == 0 else cbase + n - 1 - i
                    for (tl, scr) in ((art, self.art_scr), (rrt, self.rrt_scr), (ttt, self.ttt_scr), (akt, self.akt_scr),
                                      (mrbt, self.mrbt_scr), (mrkt, self.mrkt_scr), (bh, self.bh_scr), (kh, self.kh_scr),
                                      (pend, self.pend_scr)):
                        self.dma(tl[r][d][sl, :], scr[c, sl, :], [scr], [tl[r][d]], eng="sp")
                    self.dma(vt[r][d][sl, :], self.vt_scr[c], [self.vt_scr], [vt[r][d]], eng="sp")

            loads(0)
            for k in range(len(steps)):
                seq, cbase, n, i = steps[k]
                step = k + 1
                r = k % NB
                if k + 1 < len(steps):
                    loads(k + 1)
                if i == 0:
                    for d in range(2):
                        sl = slice(d * 64, (d + 1) * 64)
                        if seq == 0:
                            self.dma(Hf[d][sl, :], self.h0r[sl, :], [], [Hf[d]], eng="act")
                        else:
                            self.memset("dve", Hf[d][sl, :], 0.0, [Hf[d]])
                        self.cp("dve", Hb[d][sl, :], Hf[d][sl, :], [Hf[d]], [Hb[d]])
                if True:
                    ctx = []
                    for d in range(2):
                        sl = slice(d * 64, (d + 1) * 64)
                        c = cbase + i if d == 0 else cbase + n - 1 - i
                        ops = tuple(x[r][d] for x in (art, rrt, ttt, akt, mrbt, mrkt, bh, kh, vt, pend))
                        ctx.append((d, sl, c, ops))
                    pZs = {}
                    for (d, sl, c, (A_, R_, T_, AK_, MRB_, MRK_, B_, K_, V_, PE_)) in ctx:
                        H_, Z_ = Hb[d], Zs[d]
                        pZ = self.ps()
                        for h in range(8):
                            self.mm(pZ[sl, hs_(h)], A_[sl, hs_(h)], H_[sl, hs_(h)], True, False, [A_, H_], [pZ])
                            self.mm(pZ[sl, hs_(h)], AK_[sl, hs_(h)], V_[sl, hs_(h)], False, True, [AK_, V_], [pZ])
                        pZs[d] = pZ
                    pYs = {}
                    for (d, sl, c, (A_, R_, T_, AK_, MRB_, MRK_, B_, K_, V_, PE_)) in ctx:
                        H_ = Hb[d]
                        pY = self.ps()
                        pYs[d] = pY
                    for (d, sl, c, ops) in ctx:
                        self.cp("act", Zs[d][sl, :], pZs[d][sl, :], [pZs[d]], [Zs[d]])
                    pUs = {}
                    for (d, sl, c, (A_, R_, T_, AK_, MRB_, MRK_, B_, K_, V_, PE_)) in ctx:
                        Z_ = Zs[d]
                        pU = self.ps()
                        for h in range(8):
                            self.mm(pU[sl, hs_(h)], T_[sl, hs_(h)], Z_[sl, hs_(h)], True, True, [T_, Z_], [pU])
                        pUs[d] = pU
                    for (d, sl, c, ops) in ctx:
                        self.cp("act", Us[d][sl, :], pUs[d][sl, :], [pUs[d]], [Us[d]])
                    pHs = {}
                    for (d, sl, c, (A_, R_, T_, AK_, MRB_, MRK_, B_, K_, V_, PE_)) in ctx:
                        U_ = Us[d]
                        self.tt("dve", tmpH[d][sl, :], Hf[d][sl, :], PE_[sl, :], ALU.mult, [Hf[d], PE_], [tmpH[d]])
                        pH = self.ps()
                        for h in range(8):
                            self.mm(pH[sl, hs_(h)], B_[sl, hs_(h)], U_[sl, hs_(h)], True, False, [B_, U_], [pH])
                            self.mm(pH[sl, hs_(h)], K_[sl, hs_(h)], V_[sl, hs_(h)], False, True, [K_, V_], [pH])
                        pHs[d] = pH
                    for (d, sl, c, (A_, R_, T_, AK_, MRB_, MRK_, B_, K_, V_, PE_)) in ctx:
                        H_, U_ = Hb[d], Us[d]
                        pY = pYs[d]
                        for h in range(8):
                            self.mm(pY[sl, hs_(h)], R_[sl, hs_(h)], H_[sl, hs_(h)], True, False, [R_, H_], [pY])
                            self.mm(pY[sl, hs_(h)], MRB_[sl, hs_(h)], U_[sl, hs_(h)], False, False, [MRB_, U_], [pY])
                            self.mm(pY[sl, hs_(h)], MRK_[sl, hs_(h)], V_[sl, hs_(h)], False, True, [MRK_, V_], [pY])
                    for (d, sl, c, ops) in ctx:
                        self.tt("dve", Hf[d][sl, :], tmpH[d][sl, :], pHs[d][sl, :], ALU.add, [tmpH[d], pHs[d]], [Hf[d]])
                        self.cp("dve", Hb[d][sl, :], Hf[d][sl, :], [Hf[d]], [Hb[d]])
                    for (d, sl, c, ops) in ctx:
                        y = Yt[step % 2][d]
                        self.cp("act", y[sl, :], pYs[d][sl, :], [pYs[d]], [y])
                        self.dma(self.ytok_scr[d, c * 64:(c + 1) * 64, :], y[sl, :], [y], [self.ytok_scr], eng="act")
                if seq > 0 and i == n - 1:
                    self.dma(self.str_o[seq - 1], Hf[0][:], [Hf[0], Hf[1]], [self.str_o], eng="act")
                yield [(0, cbase + i), (1, cbase + n - 1 - i)]

    def phaseB_post(self, st):
        prm, cst = self.prm_t, self.cst_t
        ident = self.ident
        if True:
            yf = [self.sb(st, "yf%d" % i, [128, 512], F32) for i in range(2)]
            yb2 = [self.sb(st, "yb2%d" % i, [128, 512], F32) for i in range(2)]
            cen = self.sb(st, "cen", [128, 8, 64], F32)
            sqv = self.sb(st, "sqv", [128, 8, 64], F32)
            mean = self.sb(st, "mean", [128, 8], F32)
            var = self.sb(st, "var", [128, 8], F32)
            gl = [self.sb(st, "gl%d" % i, [128, 4, 128], BF16) for i in range(2)]
            bl = [self.sb(st, "bl%d" % i, [128, 4, 128], BF16) for i in range(2)]
            ynT = self.sb(st, "ynT", [128, 4, 128], F32)
            yo = [self.sb(st, "yo%d" % i, [128, 4, 128], BF16) for i in range(2)]
            it = -1
            blk = yield
            while True:
                it += 1
                g0 = blk * 128
                a, b = yf[it % 2], yb2[it % 2]
                g_, b_ = gl[it % 2], bl[it % 2]
                o = yo[it % 2]
                self.dma(a[:], self.ytok_scr[0, g0:g0 + 128, :], [self.ytok_scr], [a])
                self.dma(b[:], self.ytok_scr[1, g0:g0 + 128, :], [self.ytok_scr], [b], eng="act")
                self.dma(g_[:], self.g_scr[:, :, g0:g0 + 128], [self.g_scr], [g_])
                self.dma(b_[:], self.bon_scr[:, :, g0:g0 + 128], [self.bon_scr], [b_], eng="act")
                a3 = a[:].rearrange("p (h v) -> p h v", v=64)
                self.tt("pool", a[:], a[:], b[:], ALU.add, [a, b], [a])
                self.S.op("dve", lambda e, a3=a3: e.tensor_reduce(out=mean[:], in_=a3, op=ALU.add, axis=mybir.AxisListType.X),
                          [a.b], [mean.b])
                self.tsc("dve", mean[:], mean[:], 1.0 / 64, ALU.mult, [mean], [mean])
                self.tt("dve", cen[:], a3, mean[:].unsqueeze(2).to_broadcast([128, 8, 64]), ALU.subtract, [a, mean], [cen])
                self.tt("pool", sqv[:], cen[:], cen[:], ALU.mult, [cen], [sqv])
                self.S.op("dve", lambda e: e.tensor_reduce(out=var[:], in_=sqv[:], op=ALU.add, axis=mybir.AxisListType.X),
                          [sqv.b], [var.b])
                self.act(var[:], var[:], AF.Sqrt, [var, self.epsT], [var], scale=1.0 / 64, bias=self.epsT[:, 1:2])
                self.S.op("dve", lambda e: e.reciprocal(out=var[:], in_=var[:]), [var.b], [var.b])
                self.tt("dve", cen[:], cen[:], var[:].unsqueeze(2).to_broadcast([128, 8, 64]), ALU.mult, [cen, var], [cen])
                p = self.ps()
                cen2 = cen[:].rearrange("p h v -> p (h v)")
                for j in range(4):
                    self.tr(p[:, j * 128:(j + 1) * 128], cen2[:, j * 128:(j + 1) * 128], ident[:], [cen, ident], [p])
                for j in range(4):
                    self.act(ynT[:, j, :], p[:, j * 128:(j + 1) * 128], AF.Identity, [p, prm], [ynT],
                             scale=prm[:, P_LNG + j:P_LNG + j + 1], bias=prm[:, P_LNB + j:P_LNB + j + 1])
                self.tt("pool", ynT[:], ynT[:], b_[:], ALU.add, [ynT, b_], [ynT])
                self.tt("dve", o[:], ynT[:], g_[:], ALU.mult, [ynT, g_], [o])
                self.dma(self.y_scr[:, 0:4, g0:g0 + 128], o[:], [o], [self.y_scr])
                blk = yield

    def phaseC(self):
        prm, cst = self.prm_t, self.cst_t
        ident, ones_bf = self.ident, self.ones_bf
        with contextlib.ExitStack() as st:
            pass
        import os
        kcc = int(os.environ.get("KCC", "99"))
        if kcc == 0:
            return
        with contextlib.ExitStack() as st:
            wout = self.sb(st, "wout", [128, 8, D], BF16)
            wsrc = self.w_out[:].rearrange("(kc p) n -> p kc n", p=128)
            with contextlib.ExitStack() as st2:
                stg = [self.sb(st2, "stgD%d" % i, [128, 8, 256], F32) for i in range(2)]
                for cb in range(4):
                    s = stg[cb % 2]
                    self.dma(s[:], wsrc[:, :, cb * 256:(cb + 1) * 256], [], [s])
                    self.cp("act", wout[:, :, cb * 256:(cb + 1) * 256], s[:], [s], [wout])
            self.S.barrier()
            yT = self.sb(st, "yT_c", [128, 8, TC], BF16)
            o1T = self.sb(st, "o1T", [128, 8, TC], F32)
            sq1 = self.sb(st, "sq1_c", [128, 8, TC], BF16)
            oT = self.sb(st, "oT", [128, 8, TC], F32)
            sq = self.sb(st, "sq_c", [128, 8, TC], BF16)
            xT = self.sb(st, "xT_c", [128, 8, TC], F32)
            h2 = self.sb(st, "h2", [128, 8, TC], BF16)
            f = self.sb(st, "f_c", [128, 32, TC], BF16)
            otok = self.sb(st, "otok", [128, 4, D], F32)
            rstd = self.sb(st, "rstd_c", [128, TC], F32)
            tmp = [self.sb(st, "tmpC%d" % i, [128, TC], F32) for i in range(2)]
            NW = 3
            w1r = [self.sb(st, "w1r%d" % i, [128, 8, 512], BF16) for i in range(NW)]
            w2r = [self.sb(st, "w2r%d" % i, [128, 32, 128], BF16) for i in range(NW)]
            ntile = min(NTOK // TC, kcc)
            wseq = []
            for ti_ in range(ntile):
                wseq += [("w1", b_) for b_ in range(8)] + [("w2", b_) for b_ in range(8)]
            wstate = {"issued": 0, "w1": 0, "w2": 0}
            wbuf = {}

            def issue_upto(n):
                while wstate["issued"] < min(n, len(wseq)):
                    k_ = wstate["issued"]
                    kind, b_ = wseq[k_]
                    ring = w1r if kind == "w1" else w2r
                    buf = ring[wstate[kind] % NW]
                    wstate[kind] += 1
                    scr = self.w1_scr if kind == "w1" else self.w2_scr
                    self.dma(buf[:], scr[b_], [scr], [buf], eng="sp" if k_ % 2 == 0 else "act")
                    wbuf[k_] = buf
                    wstate["issued"] += 1

            def rms(sqt, R):
                p = self.ps()
                for j in range(8):
                    self.mm(p[:], ones_bf[:], sqt[:, j, :], j == 0, j == 7, [ones_bf, sqt], [p])
                self.act(rstd[:], p[:], AF.Sqrt, [p, self.epsT], [rstd], scale=1.0 / D, bias=self.epsT[:, 0:1])
                self.S.op("dve", lambda e: e.reciprocal(out=rstd[:], in_=rstd[:]), [rstd.b], [rstd.b])

            def resid(gg, mc, oT):
                for j in range(8):
                    t = tmp[j % 2]
                    self.tt("dve", t[:], oT[:, j, :], rstd[:], ALU.mult, [oT, rstd], [t])
                    self.stt(xT[:, j, :], t[:], gg[:, j, mc:mc + 1], xT[:, j, :], ALU.mult, ALU.add, [t, gg, xT], [xT])

            def wout_stage(tj):
                gj = tj * TC
                self.dma(yT[:], self.y_scr[:, :, gj:gj + TC], [self.y_scr], [yT])
                for m in range(8):
                    p = self.ps()
                    for kc in range(8):
                        self.mm(p[:], wout[:, kc, m * 128:(m + 1) * 128], yT[:, kc, :], kc == 0, kc == 7, [wout, yT], [p])
                    self.cp("act", o1T[:, m, :], p[:], [p], [o1T])
                    self.act(sq1[:, m, :], p[:], AF.Square, [p], [sq1])

            wi = 0
            for ti in range(NTOK // TC):
                if ti >= kcc:
                    break
                g0 = ti * TC
                mc = 0 if g0 < TS else 1
                if ti == 0:
                    wout_stage(0)
                self.dma(xT[:], self.xT_scr[:, :, g0:g0 + TC], [self.xT_scr], [xT], eng="act")
                issue_upto(ti * 16 + 3)
                rms(sq1, None)
                resid(self.gg1, mc, o1T)
                for j in range(8):
                    self.act(sq[:, j, :], xT[:, j, :], AF.Square, [xT], [sq])
                rms(sq, None)
                for j in range(8):
                    t = tmp[j % 2]
                    self.tt("dve", t[:], xT[:, j, :], rstd[:], ALU.mult, [xT, rstd], [t])
                    self.act(h2[:, j, :], t[:], AF.Identity, [t, self.gs2, self.modT], [h2],
                             scale=self.gs2[:, j, mc:mc + 1], bias=self.modT[:, 24 + j, mc:mc + 1])
                for blk in range(8):
                    issue_upto(ti * 16 + blk + 3)
                    w = wbuf[ti * 16 + blk]
                    for c4 in range(4):
                        fc = blk * 4 + c4
                        p = self.ps()
                        for kc in range(8):
                            self.mm(p[:], w[:, kc, c4 * 128:(c4 + 1) * 128], h2[:, kc, :], kc == 0, kc == 7, [w, h2], [p])
                        t = tmp[fc % 2]
                        self.act(t[:], p[:], AF.Relu, [p], [t])
                        self.tt("pool" if fc % 2 == 0 else "dve", f[:, fc, :], t[:], t[:], ALU.mult, [t], [f])
                for m in range(8):
                    issue_upto(ti * 16 + 8 + m + 3)
                    w = wbuf[ti * 16 + 8 + m]
                    p = self.ps()
                    for fc in range(32):
                        self.mm(p[:], w[:, fc, :], f[:, fc, :], fc == 0, fc == 31, [w, f], [p])
                    self.cp("act", oT[:, m, :], p[:], [p], [oT])
                    self.act(sq[:, m, :], p[:], AF.Square, [p], [sq])
                if ti + 1 < ntile:
                    wout_stage(ti + 1)
                rms(sq, None)
                resid(self.gg2, mc, oT)
                for s in range(4):
                    for half in range(2):
                        p = self.ps()
                        for jj in range(4):
                            j = half * 4 + jj
                            self.tr(p[:, jj * 128:(jj + 1) * 128], xT[:, j, s * 128:(s + 1) * 128], ident[:], [xT, ident], [p])
                        self.cp("act" if half == 0 else "dve", otok[:, s, half * 512:(half + 1) * 512], p[:], [p], [otok])
                if g0 < TS:
                    dst = self.ys[g0:g0 + TC, :].rearrange("(s p) f -> p s f", p=128)
                    self.dma(dst, otok[:], [otok], [self.ys])
                else:
                    dst = self.yp[:, :].rearrange("(s p) f -> p s f", p=128)
                    self.dma(dst, otok[:], [otok], [self.yp])


def _fm(v):
    v = np.asarray(v, np.float32).reshape(-1, 128)
    return np.ascontiguousarray(v.T)


def _pos_embed():
    def sincos(pos, dim):
        omega = (1.0 / (10000.0 ** (np.arange(dim // 2, dtype=np.float32) / np.float32(dim // 2)))).astype(np.float32)
        ang = pos.astype(np.float32)[:, None] * omega[None, :]
        return np.concatenate([np.sin(ang), np.cos(ang)], axis=-1).astype(np.float32)
    rows = TS // 64
    half = D // 2
    e_row = sincos(np.arange(rows), half)
    e_col = sincos(np.arange(64), half)
    emb = np.concatenate([np.broadcast_to(e_row[:, None, :], (rows, 64, half)),
                          np.broadcast_to(e_col[None, :, :], (rows, 64, half))], axis=-1)
    return np.ascontiguousarray(emb.reshape(rows * 64, D).astype(np.float32))


def _consts():
    c = np.zeros((128, NCST), np.float32)
    c[:, C_ID:C_ID + 128] = np.eye(128, dtype=np.float32)
    ob = np.zeros((128, 128), np.float32)
    ob[:64, :64] = 1.0
    ob[64:, 64:] = 1.0
    c[:, C_OB:C_OB + 128] = ob
    s = np.arange(64)[:, None]
    t = np.arange(64)[None, :]
    msi = np.zeros((128, 2, 64), np.float32)
    msi[:64, 0] = (s < t)
    msi[:64, 1] = (s <= t)
    msi[64:, 0] = (s > t)
    msi[64:, 1] = (s >= t)
    c[:, C_MSI:C_MSI + 128] = msi.reshape(128, 128)
    ml = np.zeros((128, 64), np.float32)
    ml[:64] = (t < s)
    ml[64:] = (t > s)
    c[:, C_ML:C_ML + 64] = ml
    ids = np.zeros((128, 64), np.float32)
    ids[:64] = np.eye(64)
    ids[64:] = np.eye(64)
    c[:, C_IDS:C_IDS + 64] = ids
    c[:64, C_MSI1:C_MSI1 + 128] = msi[64:].reshape(64, 128)
    c[:64, C_ML1:C_ML1 + 64] = ml[64:]
    tt_ = np.arange(TT)
    c[:, C_RMF:C_RMF + TT] = (tt_ % 64 != 0).astype(np.float32)[None, :]
    c[:, C_RMB:C_RMB + TT] = (tt_ % 64 != 63).astype(np.float32)[None, :]
    return c


_NC_CACHE = {}


def kernel(x_prompt, x_sample, c, state_rwkv, state_lru, c_ctx, w_mod, b_mod,
           g_pre_mix, g_post_mix, g_pre_mlp, g_post_mlp, w_in,
           rwkv_w0, rwkv_w_up, rwkv_a0, rwkv_a_up, rwkv_g_up, rwkv_k_k, rwkv_k_a, rwkv_r_k,
           rwkv_lnx_g, rwkv_lnx_b, lru_conv_w, lru_conv_b, lru_wa, lru_ba, lru_wx, lru_bx,
           lru_lambda, w_out, w_mlp1, w_mlp2, _debug=False):
    f = lambda a: np.ascontiguousarray(np.asarray(a, np.float32))
    x_prompt, x_sample, c, state_rwkv, state_lru, c_ctx = map(f, (x_prompt, x_sample, c, state_rwkv, state_lru, c_ctx))
    nc = K(debug=_debug).build()
    pe = _pos_embed()
    cst = _consts()
    shared = {
        "pe": pe, "cst": cst,
        "w_mod": f(w_mod[0]), "w_in": f(w_in[0]), "w_out": f(w_out[0]), "w1": f(w_mlp1[0]), "w2": f(w_mlp2[0]),
        "wup": f(rwkv_w_up[0]).reshape(128, 512), "aup": f(rwkv_a_up[0]).reshape(128, 512), "gup": f(rwkv_g_up[0]),
        "lwa": f(lru_wa[0]), "lwx": f(lru_wx[0]),
    }
    prm0 = np.zeros((128, NPRM), np.float32)
    prm0[:, P_GPRE:P_GPRE + 8] = _fm(g_pre_mix[0])
    prm0[:, P_GPOST:P_GPOST + 8] = _fm(g_post_mix[0])
    prm0[:, P_GPRE2:P_GPRE2 + 8] = _fm(g_pre_mlp[0])
    prm0[:, P_GPOST2:P_GPOST2 + 8] = _fm(g_post_mlp[0])
    prm0[:, P_BMOD:P_BMOD + 48] = _fm(b_mod[0])
    for d in range(2):
        prm0[:, P_W0 + 4 * d:P_W0 + 4 * d + 4] = _fm(rwkv_w0[0, d])
        prm0[:, P_A0 + 4 * d:P_A0 + 4 * d + 4] = _fm(rwkv_a0[0, d])
        prm0[:, P_BA + 4 * d:P_BA + 4 * d + 4] = _fm(lru_ba[0, d])
        prm0[:, P_BX + 4 * d:P_BX + 4 * d + 4] = _fm(lru_bx[0, d])
        prm0[:, P_LAM + 4 * d:P_LAM + 4 * d + 4] = _fm(lru_lambda[0, d])
    prm0[:, P_KK:P_KK + 4] = _fm(rwkv_k_k[0])
    prm0[:, P_KA:P_KA + 4] = _fm(rwkv_k_a[0])
    prm0[:, P_RK:P_RK + 4] = _fm(np.asarray(rwkv_r_k[0]).reshape(-1))
    prm0[:, P_LNG:P_LNG + 4] = _fm(rwkv_lnx_g[0])
    prm0[:, P_LNB:P_LNB + 4] = _fm(rwkv_lnx_b[0])
    for i in range(4):
        prm0[:, P_CW + 4 * i:P_CW + 4 * i + 4] = _fm(lru_conv_w[0, i])
    prm0[:, P_CB:P_CB + 4] = _fm(lru_conv_b[0])
    in_maps = []
    for i in range(8):
        prm = prm0.copy()
        for d in range(2):
            prm[:, P_H0 + 4 * d:P_H0 + 4 * d + 4] = _fm(state_lru[i, 0, d])
        cT = np.zeros((128, 8, 2), np.float32)
        cT[:, :, 0] = _fm(c[i])
        cT[:, :, 1] = _fm(c_ctx)
        h0 = np.ascontiguousarray(state_rwkv[i, 0].transpose(0, 3, 1, 2)).reshape(128, 512)
        m = dict(shared)
        m.update({"xs": x_sample[i], "xp": np.ascontiguousarray(x_prompt[2 * i:2 * i + 2].reshape(2 * TP, D)),
                  "cT": cT.reshape(128, 16), "h0r": h0, "prm": prm})
        in_maps.append(m)
    res = run_bass_kernel_spmd(nc, in_maps, core_ids=list(range(8)))
    R = res.results
    y_prompt = np.zeros((16, TP, D), np.float32)
    y_sample = np.zeros((8, TS, D), np.float32)
    st_r = np.zeros((16, 1, 2, 8, 64, 64), np.float32)
    st_l = np.zeros((16, 1, 2, 512), np.float32)
    for i in range(8):
        r = R[i]
        y_sample[i] = r["ys"]
        y_prompt[2 * i:2 * i + 2] = r["yp"].reshape(2, TP, D)
        so = r["str_o"].reshape(2, 2, 64, 8, 64)
        st_r[2 * i:2 * i + 2, 0] = so.transpose(0, 1, 3, 4, 2)
        sl = r["stl_o"].reshape(128, 4, 2, 2)
        st_l[2 * i:2 * i + 2, 0] = sl.transpose(2, 3, 1, 0).reshape(2, 2, 512)
    if _debug:
        return (y_prompt, y_sample, st_r, st_l), R
    return (y_prompt, y_sample, st_r, st_l)
```

```python
import contextlib
import numpy as np
import concourse.bass as bass
import concourse.mybir as mybir
from concourse.bass_utils import run_bass_kernel_spmd

F32 = mybir.dt.float32
BF16 = mybir.dt.bfloat16
F32R = mybir.dt.float32r
AF = mybir.ActivationFunctionType
ALU = mybir.AluOpType

D = 1024
TS = 2048
TP = 256
NTOK = TS + 2 * TP
NCH = NTOK // 64
DIN = 2944
DFF = 4096
LAM = float(np.exp(-0.5))
EPS = 1e-6
LNX_EPS = 64e-5
TT = 256
TC = 512
GELU_C = 1.5957691216057308

P_GPRE, P_GPOST, P_GPRE2, P_GPOST2 = 0, 8, 16, 24
P_BMOD = 32
P_W0, P_A0 = 80, 88
P_KK, P_KA, P_RK, P_LNG, P_LNB = 96, 100, 104, 108, 112
P_CW, P_CB = 116, 132
P_BA, P_BX, P_LAM, P_H0 = 136, 144, 152, 160
NPRM = 168
C_ID, C_OB, C_MSI, C_ML, C_IDS, C_RMF, C_RMB = 0, 128, 256, 384, 448, 512, 768
C_MSI1, C_ML1 = 1024, 1152
NCST = 1216


class Buf:
    __slots__ = ("name", "lw", "rd", "excl", "multi", "ws")

    def __init__(self, name=""):
        self.name = name
        self.lw = None
        self.rd = {}
        self.excl = False
        self.multi = False
        self.ws = {}


class TL:
    def __init__(self, t, name=""):
        self.t = t
        self.b = Buf(name)

    def __getitem__(self, k):
        return self.t[k]


class Sched:
    ENGS = ("pe", "act", "dve", "pool", "sp")

    def __init__(self, nc):
        self.nc = nc
        self.streams = {e: [] for e in self.ENGS}
        self.cnt = {}
        self.waited = {e: {} for e in self.ENGS}
        self.n_ops = 0
        self.dma_n = {e: 0 for e in self.ENGS}
        self.NSLOT = {"sp": 44, "act": 44, "pool": 4, "dve": 2, "pe": 2}

    def _deps(self, eng, reads, writes):
        need = {}
        for b in reads:
            if b.multi:
                for s, v in b.ws.items():
                    if need.get(s, 0) < v:
                        need[s] = v
                continue
            if b.lw is not None:
                s, v = b.lw
                if need.get(s, 0) < v:
                    need[s] = v
            if b.excl:
                for s, v in b.rd.items():
                    if s != eng and need.get(s, 0) < v:
                        need[s] = v
        for b in writes:
            if b.multi:
                continue
            if b.lw is not None:
                s, v = b.lw
                if need.get(s, 0) < v:
                    need[s] = v
            for s, v in b.rd.items():
                if need.get(s, 0) < v:
                    need[s] = v
        out = []
        w = self.waited[eng]
        for s, v in need.items():
            if s == "pe" and eng == "pe":
                continue
            if w.get(s, 0) >= v:
                continue
            w[s] = v
            out.append((s, v))
        return out

    def op(self, eng, fn, reads=(), writes=(), dma=False):
        reads = [r.b if isinstance(r, TL) else r for r in reads]
        writes = [r.b if isinstance(r, TL) else r for r in writes]
        waits = self._deps(eng, reads, writes)
        if dma:
            slot = self.dma_n[eng] % self.NSLOT[eng]
            self.dma_n[eng] += 1
            sem = "%s_d%d" % (eng, slot)
            prev = self.cnt.get(sem, 0)
            if prev > 0 and self.waited[eng].get(sem, 0) < prev:
                self.waited[eng][sem] = prev
                waits.append((sem, prev))
        else:
            sem = eng
        inc = 16 if dma else 1
        self.cnt[sem] = self.cnt.get(sem, 0) + inc
        val = self.cnt[sem]
        self.streams[eng].append((waits, fn, sem, inc))
        self.n_ops += 1
        for b in reads:
            if b.rd.get(sem, 0) < val:
                b.rd[sem] = val
        for b in writes:
            if b.multi:
                if b.ws.get(sem, 0) < val:
                    b.ws[sem] = val
                continue
            b.lw = (sem, val)
            b.rd = {}
        return val

    def barrier_on(self, tl):
        if tl.b.lw is None:
            return
        sname, v = tl.b.lw
        for e in ("sp", "act", "pool"):
            if self.waited[e].get(sname, 0) < v:
                self.waited[e][sname] = v
                self.streams[e].append(([(sname, v)], None, None, 0))

    def barrier(self):
        snap = dict(self.cnt)
        for e in self.ENGS:
            waits = []
            for s, v in snap.items():
                if s == "pe" and e == "pe":
                    continue
                if self.waited[e].get(s, 0) < v:
                    self.waited[e][s] = v
                    waits.append((s, v))
            if waits:
                self.streams[e].append((waits, None, None, 0))

    def emit(self):
        nc = self.nc
        sems = {}
        with contextlib.ExitStack() as st:
            for s in self.cnt:
                sems[s] = st.enter_context(nc.semaphore(s))
            block = st.enter_context(nc.Block())
            engmap = {"pe": block.tensor, "act": block.scalar, "dve": block.vector,
                      "pool": block.gpsimd, "sp": block.sync}
            for e in self.ENGS:
                stream = self.streams[e]
                if not stream:
                    continue

                def body(eng, stream=stream):
                    for waits, fn, sem, inc in stream:
                        for s, v in waits:
                            eng.wait_ge(sems[s], v)
                        if fn is not None:
                            fn(eng).then_inc(sems[sem], inc)
                engmap[e](body)


class K:
    def __init__(self, debug=False, stop_after=None):
        self.debug = debug
        self.stop_after = stop_after
        import os
        self.cutk = int(os.environ.get("KCUT", "0"))
        self.cutm = int(os.environ.get("KCUTM", "99"))
        self.ktiles = int(os.environ.get("KTILES", "99"))
        self.kskip = os.environ.get("KSKIP", "").split(",")
        self.nc = bass.Bass("TRN2", target_bir_lowering=False)
        self.S = Sched(self.nc)
        self.es = contextlib.ExitStack()
        self.psr = 0
        self.rr = {}

    def dram(self, name, shape, dt, kind="Internal"):
        t = TL(self.nc.dram_tensor(name, list(shape), dt, kind=kind).ap(), name)
        t.b.multi = True
        return t

    def sb(self, st, name, shape, dt):
        return TL(st.enter_context(self.nc.sbuf_tensor(name, list(shape), dt)), name)

    def sb2(self, st, name, shape, dt):
        t = st.enter_context(self.nc.sbuf_tensor(name, list(shape), dt))
        return [TL(t, name + "_lo"), TL(t, name + "_hi")]

    def ps(self):
        p = self.psum[self.psr % 8]
        self.psr += 1
        return p

    def mm(self, out, lhsT, rhs, start, stop, R, W):
        self.S.op("pe", lambda e: e.matmul(out, lhsT=lhsT, rhs=rhs, start=start, stop=stop), R, W)

    def tr(self, out, in_, ident, R, W):
        self.S.op("pe", lambda e: e.transpose(out, in_, ident), R, W)

    def act(self, out, in_, func, R, W, scale=1.0, bias=None, eng="act"):
        if bias is None:
            self.S.op("act", lambda e: e.activation(out=out, in_=in_, func=func, scale=scale), R, W)
        else:
            self.S.op("act", lambda e: e.activation(out=out, in_=in_, func=func, scale=scale, bias=bias), R, W)

    def tt(self, eng, out, in0, in1, op, R, W):
        self.S.op(eng, lambda e: e.tensor_tensor(out=out, in0=in0, in1=in1, op=op), R, W)

    def tsc(self, eng, out, in0, s1, op0, R, W, s2=None, op1=None):
        if op1 is None:
            self.S.op(eng, lambda e: e.tensor_scalar(out=out, in0=in0, scalar1=s1, scalar2=None, op0=op0), R, W)
        else:
            self.S.op(eng, lambda e: e.tensor_scalar(out=out, in0=in0, scalar1=s1, scalar2=s2, op0=op0, op1=op1), R, W)

    def stt(self, out, in0, scalar, in1, op0, op1, R, W):
        self.S.op("dve", lambda e: e.scalar_tensor_tensor(out=out, in0=in0, scalar=scalar, in1=in1, op0=op0, op1=op1), R, W)

    def cp(self, eng, out, in_, R, W):
        if eng == "act":
            self.S.op("act", lambda e: e.activation(out=out, in_=in_, func=AF.Copy), R, W)
        else:
            self.S.op(eng, lambda e: e.tensor_copy(out=out, in_=in_), R, W)

    def scan(self, out, d0, d1, init, R, W):
        self.S.op("dve", lambda e: e.tensor_tensor_scan(out=out, data0=d0, data1=d1, initial=init,
                                                        op0=ALU.mult, op1=ALU.add), R, W)

    def dma(self, out, in_, R, W, eng="sp"):
        self.S.op(eng, lambda e: e.dma_start(out=out, in_=in_), R, W, dma=True)

    def memset(self, eng, ap, val, W):
        self.S.op(eng, lambda e: e.memset(ap, val), (), W)

    def pick(self, key, engs):
        i = self.rr.get(key, 0)
        self.rr[key] = i + 1
        return engs[i % len(engs)]

    def build(self):
        nc = self.nc
        I = lambda n, s, dt=F32: self.dram(n, s, dt, "ExternalInput")
        O = lambda n, s, dt=F32: self.dram(n, s, dt, "ExternalOutput")
        self.xs = I("xs", [TS, D])
        self.xp = I("xp", [2 * TP, D])
        self.pe = I("pe", [TS, D])
        self.cT = I("cT", [128, 16])
        self.h0r = I("h0r", [128, 512])
        self.prm = I("prm", [128, NPRM])
        self.cst = I("cst", [128, NCST])
        self.w_mod = I("w_mod", [D, 6 * D])
        self.w_in = I("w_in", [D, DIN])
        self.w_out = I("w_out", [D, D])
        self.w1 = I("w1", [D, DFF])
        self.w2 = I("w2", [DFF, D])
        self.wup = I("wup", [128, 512])
        self.aup = I("aup", [128, 512])
        self.gup = I("gup", [128, 512])
        self.lwa = I("lwa", [2, 8, 64, 64])
        self.lwx = I("lwx", [2, 8, 64, 64])
        self.ys = O("ys", [TS, D])
        self.yp = O("yp", [2 * TP, D])
        self.str_o = O("str_o", [2, 128, 512])
        self.stl_o = O("stl_o", [128, 16])
        self.xT_scr = self.dram("xT_scr", [128, 8, NTOK], F32)
        self.xb_scr = self.dram("xb_scr", [128, 4, NTOK], F32)
        self.gate_scr = self.dram("gate_scr", [128, 4, NTOK], BF16)
        self.g_scr = self.dram("g_scr", [128, 4, NTOK], BF16)
        self.bon_scr = self.dram("bon_scr", [128, 4, NTOK], BF16)
        self.y_scr = self.dram("y_scr", [128, 8, NTOK], BF16)
        self.ytok_scr = self.dram("ytok_scr", [2, NTOK, 512], F32)
        for n in ("art", "rrt", "ttt", "akt", "mrbt", "mrkt", "bh", "kh"):
            setattr(self, n + "_scr", self.dram(n + "_scr", [NCH, 128, 512], BF16))
        self.vt_scr = self.dram("vt_scr", [NCH, 64, 512], BF16)
        self.pend_scr = self.dram("pend_scr", [NCH, 128, 512], F32)
        self.w1_scr = self.dram("w1_scr", [8, 128, 8, 512], BF16)
        self.w2_scr = self.dram("w2_scr", [8, 128, 32, 128], BF16)
        if self.debug:
            self.dbg = {}

        with self.es as st0:
            self.psum = [TL(st0.enter_context(nc.psum_tensor("ps%d" % i, [128, 512], F32)), "ps%d" % i)
                         for i in range(8)]
            for p_ in self.psum:
                p_.b.excl = True
            self.prm_t = self.sb(st0, "prm_t", [128, NPRM], F32)
            self.cst_t = self.sb(st0, "cst_t", [128, NCST], F32)
            self.modT = self.sb(st0, "modT", [128, 48, 2], F32)
            self.gs1 = self.sb(st0, "gs1", [128, 8, 2], F32)
            self.gs2 = self.sb(st0, "gs2", [128, 8, 2], F32)
            self.gg1 = self.sb(st0, "gg1", [128, 8, 2], F32)
            self.gg2 = self.sb(st0, "gg2", [128, 8, 2], F32)
            self.ident = self.sb(st0, "ident", [128, 128], F32)
            self.ones_bf = self.sb(st0, "ones_bf", [128, 128], BF16)
            self.oblk_bf = self.sb(st0, "oblk_bf", [128, 128], BF16)
            self.epsT = self.sb(st0, "epsT", [128, 2], F32)
            self.misc = self.sb(st0, "misc", [128, 32], F32)
            self.stl_t = self.sb(st0, "stl_t", [128, 16], F32)
            stop = False
            with contextlib.ExitStack() as stA:
                self.win = self.sb(stA, "win", [128, 8, DIN], BF16)
                self.win.b.multi = True
                self.wup_t = self.sb(stA, "wup_t", [128, 512], BF16)
                self.aup_t = self.sb(stA, "aup_t", [128, 512], BF16)
                self.gup_t = self.sb(stA, "gup_t", [128, 512], BF16)
                for nm, fn in (("p0", self.phase0), ("pA", self.phaseA)):
                    fn()
                    self.S.barrier()
                    if self.stop_after == nm:
                        stop = True
                        break
            if not stop:
                for nm, fn in (("pB", self.phaseB), ("pC", self.phaseC)):
                    fn()
                    self.S.barrier()
                    if self.stop_after == nm:
                        break
            self.S.emit()
        return nc

    def dump(self, name, src_ap, shape, dt, R):
        o = self.dram("dbg_" + name, shape, dt, "ExternalOutput")
        self.dma(o[:], src_ap, R, [o])

    def phase0(self):
        nc = self.nc
        prm, cst = self.prm_t, self.cst_t
        self.dma(prm[:], self.prm[:], [], [prm])
        self.dma(cst[:], self.cst[:], [], [cst])
        self.cp("dve", self.ident[:], cst[:, C_ID:C_ID + 128], [cst], [self.ident])
        self.cp("dve", self.oblk_bf[:], cst[:, C_OB:C_OB + 128], [cst], [self.oblk_bf])
        self.memset("dve", self.ones_bf[:], 1.0, [self.ones_bf])
        self.memset("dve", self.epsT[:, 0:1], EPS, [self.epsT])
        self.memset("dve", self.epsT[:, 1:2], LNX_EPS, [self.epsT])
        self.tsc("dve", self.misc[:, 0:4], prm[:, P_KA:P_KA + 4], -1.0, ALU.mult, [prm], [self.misc], 1.0, ALU.add)
        with contextlib.ExitStack() as st:
            scT = self.sb(st, "scT", [128, 16], F32)
            cT = self.sb(st, "cT_t", [128, 16], F32)
            wm = [self.sb(st, "wm%d" % i, [128, 8, 512], F32) for i in range(2)]
            tmp = self.sb(st, "lam_tmp", [128, 8], F32)
            self.dma(cT[:], self.cT[:], [], [cT])
            self.act(scT[:], cT[:], AF.Silu, [cT], [scT])
            self.act(tmp[:], prm[:, P_LAM:P_LAM + 8], AF.Exp, [prm], [tmp], scale=-1.0)
            self.act(tmp[:], tmp[:], AF.Ln, [tmp], [tmp], bias=1.0)
            self.tsc("dve", self.misc[:, 4:12], tmp[:], -8.0, ALU.mult, [tmp], [self.misc])
            self.tsc("dve", self.misc[:, 12:20], tmp[:], -16.0, ALU.mult, [tmp], [self.misc])
            wsrc = self.w_mod[:].rearrange("(kc p) n -> p kc n", p=128)
            scb = self.sb(st, "scb", [128, 16], BF16)
            self.cp("dve", scb[:], scT[:], [scT], [scb])
            sc3 = scb[:].rearrange("p (k c) -> p k c", c=2)
            wmb = [self.sb(st, "wmb%d" % i, [128, 8, 512], BF16) for i in range(2)]
            stgA = [self.sb(st, "stgA%d" % i, [128, 8, 256], F32) for i in range(2)]
            wisrc = self.w_in[:].rearrange("(kc p) n -> p kc n", p=128)
            wi_blocks = [(c0, min(256, DIN - c0)) for c0 in range(0, DIN, 256)]
            sm_list = [(self.wup, self.wup_t), (self.aup, self.aup_t), (self.gup, self.gup_t)]
            nb = [0]

            def win_step():
                if wi_blocks:
                    c0, cw = wi_blocks.pop(0)
                    s_ = stgA[nb[0] % 2]
                    self.dma(s_[:, :, 0:cw], wisrc[:, :, c0:c0 + cw], [], [s_], eng="act")
                    self.cp("act" if nb[0] % 2 == 0 else "pool", self.win[:, :, c0:c0 + cw], s_[:, :, 0:cw], [s_], [self.win])
                    nb[0] += 1
                elif sm_list:
                    src, dstt = sm_list.pop(0)
                    s_ = stgA[nb[0] % 2]
                    nb[0] += 1
                    s2 = s_[:].rearrange("p a b -> p (a b)")[:, 0:512]
                    self.dma(s2, src[:], [], [s_], eng="act")
                    self.cp("pool", dstt[:], s2, [s_], [dstt])

            for blk in range(12):
                w = wm[blk % 2]
                wb_ = wmb[blk % 2]
                self.dma(w[:], wsrc[:, :, blk * 512:(blk + 1) * 512], [], [w], eng="sp")
                self.cp("dve", wb_[:], w[:], [w], [wb_])
                win_step()
                p = self.ps()
                for m in range(4):
                    for kc in range(8):
                        self.mm(p[:, 2 * m:2 * m + 2], wb_[:, kc, m * 128:(m + 1) * 128], sc3[:, kc, :],
                                kc == 0, kc == 7, [wb_, scb], [p])
                for m in range(4):
                    mi = blk * 4 + m
                    self.tsc("dve", self.modT[:, mi, :], p[:, 2 * m:2 * m + 2], prm[:, P_BMOD + mi:P_BMOD + mi + 1],
                             ALU.add, [p, prm], [self.modT])
            while wi_blocks or sm_list:
                win_step()
            m3 = self.modT
            for (dst, sc_off, g_off, one) in ((self.gs1, 8, P_GPRE, 1.0), (self.gs2, 32, P_GPRE2, 1.0),
                                              (self.gg1, 16, P_GPOST, 0.0), (self.gg2, 40, P_GPOST2, 0.0)):
                for c in range(2):
                    self.tsc("dve", dst[:, :, c], m3[:, sc_off:sc_off + 8, c], one, ALU.add, [m3], [dst])
                    self.tt("dve", dst[:, :, c], dst[:, :, c], prm[:, g_off:g_off + 8], ALU.mult, [dst, prm], [dst])
            if self.debug:
                self.dump("modT", self.modT[:], [128, 48, 2], F32, [self.modT])
                self.dump("gs1", self.gs1[:], [128, 8, 2], F32, [self.gs1])

    def load_cast(self, st, dst_ap, dst_tl, src_ap, shape, tag):
        key = "stg_" + tag
        if not hasattr(self, key):
            setattr(self, key, [self.sb(st, "%s%d" % (key, i), shape, F32) for i in range(2)])
        ring = getattr(self, key)
        s = ring[self.rr.get(key, 0) % 2]
        self.rr[key] = self.rr.get(key, 0) + 1
        self.dma(s[:], src_ap, [], [s], eng="sp")
        eng = self.pick("castE", ["act", "pool"])
        self.cp(eng, dst_ap, s[:], [s], [dst_tl])

    def phaseA(self):
        nc = self.nc
        prm, cst = self.prm_t, self.cst_t
        with contextlib.ExitStack() as st:
            win, wup, aup, gup = self.win, self.wup_t, self.aup_t, self.gup_t
            mSI = cst[:, C_MSI:C_MSI + 128].rearrange("p (q t) -> p q t", q=2)
            mL = cst[:, C_ML:C_ML + 64]
            idS = cst[:, C_IDS:C_IDS + 64]

            xin = self.sb(st, "xin", [128, 2, D], F32)
            xT = self.sb(st, "xT", [128, 8, TT], F32)
            sq = self.sb(st, "sq", [128, 8, TT], BF16)
            hT = self.sb(st, "hT", [128, 8, TT], BF16)
            rstd = self.sb(st, "rstd", [128, TT], F32)
            tmpA = [self.sb(st, "tmpA%d" % i, [128, TT], F32) for i in range(2)]
            rT = self.sb(st, "rT", [128, 4, TT], F32)
            kT = self.sb(st, "kT", [128, 4, TT], F32)
            vT = self.sb(st, "vT", [128, 4, TT], F32)
            xw = self.sb(st, "xw", [128, TT], BF16)
            xa = self.sb(st, "xa", [128, TT], BF16)
            xg = self.sb(st, "xg", [128, TT], BF16)
            xbT = self.sb(st, "xbT", [128, 4, TT], F32)
            gtmp = [self.sb(st, "gtmp%d" % i, [128, TT], F32) for i in range(3)]
            gate = self.sb(st, "gate", [128, 4, TT], BF16)
            gT = self.sb(st, "gT", [128, 4, TT], BF16)
            kkn = self.sb(st, "kkn", [128, 4, TT], F32)
            ksum = self.sb(st, "ksum", [128, 4, TT], F32)
            bon = self.sb(st, "bon", [128, 4, TT], BF16)
            sg = self.sb(st, "sg", [128, 4, TT], F32)
            cs = self.sb(st, "cs", [128, 4, TT], F32)
            E1 = self.sb(st, "E1", [128, 4, TT], F32)
            ad = self.sb(st, "ad", [128, 4, TT], F32)
            wk1 = self.sb(st, "wk1", [128, 4, TT], F32)
            wk2 = self.sb(st, "wk2", [128, 4, TT], F32)
            NC4 = TT // 64
            AR = [self.sb(st, "AR%d" % d, [128, 4, NC4, 2, 64], BF16) for d in range(2)]
            BK = [self.sb(st, "BK%d" % d, [128, 4, NC4, 2, 64], BF16) for d in range(2)]
            PEb1 = self.sb(st, "PEb", [128, 4, NC4, 64], F32)
            PEb = [PEb1, PEb1]
            tokB1 = self.sb(st, "tokB", [128, 2, 8, 64], BF16)
            tokK1 = self.sb(st, "tokK", [128, 2, 8, 64], BF16)
            tokB, tokK = [tokB1, tokB1], [tokK1, tokK1]
            tokV = self.sb(st, "tokV", [128, 2, 8, 64], BF16)
            NSET = 2
            MRBs = [self.sb(st, "MRBs%d" % i, [64, 8, 64], BF16) for i in range(NSET)]
            Lt0s = [self.sb(st, "Lt0_%d" % i, [64, 8, 64], F32R) for i in range(NSET)]
            Tfins = [self.sb(st, "Tfin%d" % i, [64, 8, 64], BF16) for i in range(NSET)]
            SCk = self.sb(st, "SCk", [128, 2, 8, 64], BF16)
            Lms = [[self.sb(st, "Lm%d_%d" % (i, k), [64, 8, 64], F32R) for i in range(2)] for k in range(NSET)]
            Ltms = [[self.sb(st, "Ltm%d_%d" % (i, k), [64, 8, 64], F32R) for i in range(2)] for k in range(NSET)]
            ILms = [self.sb(st, "ILm_%d" % k, [64, 8, 64], F32R) for k in range(NSET)]
            Ttms = [[self.sb(st, "Ttm%d_%d" % (i, k), [64, 8, 64], F32R) for i in range(2)] for k in range(NSET)]

            ones_bf, oblk, ident = self.ones_bf, self.oblk_bf, self.ident
            tiles = [(0, t0, True, 0) for t0 in range(0, TS, TT)] + [(1, TS, False, 1), (2, TS + TP, False, 1)]
            mS1 = cst[0:64, C_MSI1:C_MSI1 + 128].rearrange("p (q t) -> p q t", q=2)
            mL1 = cst[0:64, C_ML1:C_ML1 + 64]
            id64 = idS[0:64]
            loaded = set()

            def load_x(g0, is_s):
                if g0 in loaded:
                    return
                loaded.add(g0)
                if is_s:
                    src = self.xs[g0:g0 + TT, :].rearrange("(s p) f -> p s f", p=128)
                else:
                    l0 = g0 - TS
                    src = self.xp[l0:l0 + TT, :].rearrange("(s p) f -> p s f", p=128)
                self.dma(xin[:], src, [], [xin])
                if is_s:
                    petv = xT[:].rearrange("p a b -> p (a b)").rearrange("p (s f) -> p s f", s=2)
                    self.dma(petv, self.pe[g0:g0 + TT, :].rearrange("(s p) f -> p s f", p=128), [], [xT], eng="act")

            def front(seq, g0, is_s, mc, nxt=None):
                load_x(g0, is_s)
                if is_s:
                    petv = xT[:].rearrange("p a b -> p (a b)").rearrange("p (s f) -> p s f", s=2)
                    self.tt("pool", xin[:], xin[:], petv, ALU.add, [xin, xT], [xin])
                for j in range(8):
                    p = self.ps()
                    for s in range(2):
                        self.tr(p[:, s * 128:(s + 1) * 128], xin[:, s, j * 128:(j + 1) * 128], ident[:], [xin, ident], [p])
                    self.cp("act", xT[:, j, :], p[:, 0:TT], [p], [xT])
                    self.tt("pool", sq[:, j, :], xT[:, j, :], xT[:, j, :], ALU.mult, [xT], [sq])
                    yield
                self.dma(self.xT_scr[:, :, g0:g0 + TT], xT[:], [xT], [self.xT_scr])
                p = self.ps()
                for j in range(8):
                    self.mm(p[:, 0:TT], ones_bf[:], sq[:, j, :], j == 0, j == 7, [ones_bf, sq], [p])
                self.act(rstd[:], p[:, 0:TT], AF.Sqrt, [p, self.epsT], [rstd], scale=1.0 / D, bias=self.epsT[:, 0:1])
                self.S.op("dve", lambda e: e.reciprocal(out=rstd[:], in_=rstd[:]), [rstd.b], [rstd.b])
                for j in range(8):
                    t = tmpA[j % 2]
                    self.tt("dve", t[:], xT[:, j, :], rstd[:], ALU.mult, [xT, rstd], [t])
                    self.act(hT[:, j, :], t[:], AF.Identity, [t, self.gs1, self.modT], [hT],
                             scale=self.gs1[:, j, mc:mc + 1], bias=self.modT[:, j, mc:mc + 1])
                    yield
                for m in range(23):
                    if m >= self.cutm:
                        break
                    p = self.ps()
                    for kc in range(8):
                        self.mm(p[:, 0:TT], win[:, kc, m * 128:(m + 1) * 128], hT[:, kc, :], kc == 0, kc == 7, [win, hT], [p])
                    pz = p[:, 0:TT]
                    if m < 4:
                        self.cp("act", rT[:, m, :], pz, [p], [rT])
                    elif m < 8:
                        self.cp("act", kT[:, m - 4, :], pz, [p], [kT])
                    elif m < 12:
                        self.cp("act", vT[:, m - 8, :], pz, [p], [vT])
                    elif m == 12:
                        self.act(xw[:], pz, AF.Tanh, [p], [xw])
                    elif m == 13:
                        self.cp("act", xa[:], pz, [p], [xa])
                    elif m == 14:
                        self.act(xg[:], pz, AF.Sigmoid, [p], [xg])
                    elif m < 19:
                        self.cp("act", xbT[:, m - 15, :], pz, [p], [xbT])
                    else:
                        j = m - 19
                        g0_, g1_, g2_ = gtmp
                        self.cp("act", g0_[:], pz, [p], [g0_])
                        self.tt("pool", g1_[:], g0_[:], g0_[:], ALU.mult, [g0_], [g1_])
                        self.tsc("dve", g1_[:], g1_[:], 0.044715, ALU.mult, [g1_], [g1_], 1.0, ALU.add)
                        self.tt("dve", g1_[:], g1_[:], g0_[:], ALU.mult, [g1_, g0_], [g1_])
                        self.act(g2_[:], g1_[:], AF.Sigmoid, [g1_], [g2_], scale=GELU_C)
                        self.tt("pool", gate[:, j, :], g0_[:], g2_[:], ALU.mult, [g0_, g2_], [gate])
                    yield
                self.dma(self.xb_scr[:, :, g0:g0 + TT], xbT[:], [xbT], [self.xb_scr])
                if self.debug and g0 == 0:
                    self.dump("hT", hT[:], [128, 8, TT], BF16, [hT])
                    self.dump("rT", rT[:], [128, 4, TT], F32, [rT])
                    self.dump("vT", vT[:], [128, 4, TT], F32, [vT])
                    self.dump("xbT", xbT[:], [128, 4, TT], F32, [xbT])
                    self.dump("gate", gate[:], [128, 4, TT], BF16, [gate])
                self.dma(self.gate_scr[:, :, g0:g0 + TT], gate[:], [gate], [self.gate_scr])
                for j in range(4):
                    p = self.ps()
                    self.mm(p[:, 0:TT], gup[:, j * 128:(j + 1) * 128], xg[:], True, True, [gup, xg], [p])
                    self.cp("act", gT[:, j, :], p[:, 0:TT], [p], [gT])
                    yield
                self.dma(self.g_scr[:, :, g0:g0 + TT], gT[:], [gT], [self.g_scr])
                for j in range(4):
                    self.tsc("dve", kkn[:, j, :], kT[:, j, :], prm[:, P_KK + j:P_KK + j + 1], ALU.mult, [kT, prm], [kkn])
                    self.tt("pool", sq[:, j, :], kkn[:, j, :], kkn[:, j, :], ALU.mult, [kkn], [sq])
                for j in range(4):
                    p = self.ps()
                    self.mm(p[:, 0:TT], oblk[:], sq[:, j, :], True, True, [oblk, sq], [p])
                    t = tmpA[j % 2]
                    self.act(t[:], p[:, 0:TT], AF.Sqrt, [p], [t])
                    self.tsc("dve", t[:], t[:], 1e-12, ALU.max, [t], [t])
                    self.S.op("dve", lambda e, t=t: e.reciprocal(out=t[:], in_=t[:]), [t.b], [t.b])
                    self.tt("dve", kkn[:, j, :], kkn[:, j, :], t[:], ALU.mult, [kkn, t], [kkn])
                    yield
                for s in range(2):
                    p = self.ps()
                    for j in range(4):
                        self.tr(p[:, j * 128:(j + 1) * 128], vT[:, j, s * 128:(s + 1) * 128], ident[:], [vT, ident], [p])
                    self.cp("act", tokV[:, s, :, :].rearrange("p h k -> p (h k)"), p[:], [p], [tokV])
                c0 = g0 // 64
                for s in range(2):
                    dst = self.vt_scr[c0 + 2 * s:c0 + 2 * s + 2, :, :].rearrange("c s f -> (c s) f")
                    self.dma(dst, tokV[:, s, :, :].rearrange("p h k -> p (h k)"), [tokV], [self.vt_scr])
                if nxt is not None:
                    load_x(nxt[1], nxt[2])
                yield
            def prep(g0, d):
                c0 = g0 // 64
                for j in range(4):
                    p = self.ps()
                    self.mm(p[:, 0:TT], wup[d * 64:(d + 1) * 64, j * 128:(j + 1) * 128], xw[d * 64:(d + 1) * 64, :],
                            True, True, [wup, xw], [p])
                    self.act(sg[:, j, :], p[:, 0:TT], AF.Sigmoid, [p, prm], [sg],
                             bias=prm[:, P_W0 + 4 * d + j:P_W0 + 4 * d + j + 1])
                    if d == 0:
                        self.scan(cs[:, j, :], cst[:, C_RMF:C_RMF + TT], sg[:, j, :], 0.0, [cst, sg], [cs])
                    else:
                        self.scan(cs[:, j, ::-1], cst[:, C_RMB:C_RMB + TT][:, ::-1], sg[:, j, ::-1], 0.0, [cst, sg], [cs])
                    yield
                for j in range(4):
                    p = self.ps()
                    self.mm(p[:, 0:TT], aup[d * 64:(d + 1) * 64, j * 128:(j + 1) * 128], xa[d * 64:(d + 1) * 64, :],
                            True, True, [aup, xa], [p])
                    self.act(ad[:, j, :], p[:, 0:TT], AF.Sigmoid, [p, prm], [ad],
                             bias=prm[:, P_A0 + 4 * d + j:P_A0 + 4 * d + j + 1])
                    yield
                self.tt("dve", sg[:], cs[:], sg[:], ALU.subtract, [cs, sg], [sg])
                self.act(E1[:], cs[:], AF.Exp, [cs], [E1], scale=-LAM)
                self.act(cs[:], cs[:], AF.Exp, [cs], [cs], scale=LAM)
                self.act(sg[:], sg[:], AF.Exp, [sg], [sg], scale=-LAM)
                E2, E3 = cs, sg
                ar5 = AR[d]
                bk5 = BK[d]
                v4 = lambda tl: tl[:].rearrange("p j (c t) -> p j c t", t=64)
                self.stt(ar5[:, :, :, 0, :], v4(kkn), -1.0, v4(E3), ALU.mult, ALU.mult, [kkn, E3], [ar5])
                self.tt("pool", ar5[:, :, :, 1, :], v4(rT), v4(E1), ALU.mult, [rT, E1], [ar5])
                yield
                self.tt("dve", wk1[:], kkn[:], ad[:], ALU.mult, [kkn, ad], [wk1])
                self.tt("dve", wk1[:], wk1[:], E2[:], ALU.mult, [wk1, E2], [wk1])
                self.cp("act", bk5[:, :, :, 0, :], v4(wk1), [wk1], [bk5])
                yield
                for j in range(4):
                    self.tsc("dve", wk2[:, j, :], ad[:, j, :], prm[:, P_KA + j:P_KA + j + 1], ALU.mult, [ad, prm, self.misc], [wk2],
                             self.misc[:, j:j + 1], ALU.add)
                self.tt("dve", wk2[:], wk2[:], kT[:], ALU.mult, [wk2, kT], [wk2])
                if d == 0:
                    self.cp("pool", ksum[:], wk2[:], [wk2], [ksum])
                else:
                    self.tt("pool", ksum[:], ksum[:], wk2[:], ALU.add, [ksum, wk2], [ksum])
                self.tt("dve", wk2[:], wk2[:], E2[:], ALU.mult, [wk2, E2], [wk2])
                self.cp("act", bk5[:, :, :, 1, :], v4(wk2), [wk2], [bk5])
                yield
                te = 63 if d == 0 else 0
                pend_b = v4(E1)[:, :, :, te:te + 1].to_broadcast([128, 4, NC4, 64])
                self.cp("act", PEb[d][:], pend_b, [E1], [PEb[d]])
                self.tt("dve", v4(wk1), v4(wk1), PEb[d][:], ALU.mult, [wk1, PEb[d]], [wk1])
                self.tt("dve", v4(wk2), v4(wk2), PEb[d][:], ALU.mult, [wk2, PEb[d]], [wk2])
                yield
                for (srcw, tokX, scr) in ((wk1, tokB[d], self.bh_scr), (wk2, tokK[d], self.kh_scr)):
                    for s in range(2):
                        p = self.ps()
                        for j in range(4):
                            self.tr(p[:, j * 128:(j + 1) * 128], srcw[:, j, s * 128:(s + 1) * 128], ident[:], [srcw, ident], [p])
                        self.cp("act", tokX[:, s, :, :].rearrange("p h k -> p (h k)"), p[:], [p], [tokX])
                        for cc in range(2):
                            self.dma(scr[c0 + 2 * s + cc, d * 64:(d + 1) * 64, :],
                                     tokX[cc * 64:(cc + 1) * 64, s, :, :].rearrange("p h k -> p (h k)"),
                                     [tokX], [scr])
                        yield
                for cl in range(NC4):
                    c = c0 + cl
                    for hp in range(2):
                        for (q, scr) in ((0, self.art_scr), (1, self.rrt_scr)):
                            dst = scr[c, d * 64:(d + 1) * 64, :].rearrange("k (j hp t) -> k j hp t", hp=2, t=64)[:, :, hp, :]
                            self.dma(dst, ar5[hp * 64:(hp + 1) * 64, :, cl, q, :], [ar5], [scr], eng="sp")
                        dst = self.pend_scr[c, d * 64:(d + 1) * 64, :].rearrange("k (j hp t) -> k j hp t", hp=2, t=64)[:, :, hp, :]
                        self.dma(dst, PEb[d][hp * 64:(hp + 1) * 64, :, cl, :], [PEb[d]], [self.pend_scr], eng="sp")
                yield
            def bonus(g0):
                for j in range(4):
                    self.stt(sq[:, j, :], rT[:, j, :], prm[:, P_RK + j:P_RK + j + 1], ksum[:, j, :], ALU.mult, ALU.mult,
                             [rT, prm, ksum], [sq])
                    p = self.ps()
                    self.mm(p[:, 0:TT], oblk[:], sq[:, j, :], True, True, [oblk, sq], [p])
                    self.tt("dve", bon[:, j, :], p[:, 0:TT], vT[:, j, :], ALU.mult, [p, vT], [bon])
                self.dma(self.bon_scr[:, :, g0:g0 + TT], bon[:], [bon], [self.bon_scr])
                yield
            def chunk_sck(g0):
                c0 = g0 // 64
                for cl in range(NC4):
                    c = c0 + cl
                    for hp in range(2):
                        p = self.ps()
                        for j in range(4):
                            for d in range(2):
                                self.mm(p[d * 64:(d + 1) * 64, j * 128:(j + 1) * 128],
                                        BK[d][hp * 64:(hp + 1) * 64, j, cl, 1, :],
                                        AR[d][hp * 64:(hp + 1) * 64, j, cl, :, :].rearrange("p q t -> p (q t)"),
                                        True, True, [BK[d], AR[d]], [p])
                        self.tt("dve", SCk[:, :, hp::2, :].rearrange("p q h t -> p h q t"),
                                p[:].rearrange("p (h q t) -> p h q t", q=2, t=64),
                                mSI.unsqueeze(1).to_broadcast([128, 4, 2, 64]), ALU.mult, [p, cst], [SCk])
                    self.dma(self.akt_scr[c], SCk[:, 0, :, :].rearrange("p h s -> p (h s)"), [SCk], [self.akt_scr], eng="act")
                    self.dma(self.mrkt_scr[c], SCk[:, 1, :, :].rearrange("p h s -> p (h s)"), [SCk], [self.mrkt_scr], eng="act")
                    yield
            def chunk_d(g0, d, cls, k):
                c0 = g0 // 64
                Lm, Ltm, ILm, Ttm, Lt0, Tfin = Lms[k], Ltms[k], ILms[k], Ttms[k], Lt0s[k], Tfins[k]
                MRB = MRBs[k]
                for cl in cls:
                    c = c0 + cl
                    msk = mSI[0:64] if d == 0 else mS1
                    mskL = mL[0:64] if d == 0 else mL1
                    L0 = Lm[0]
                    for hp in range(2):
                        p = self.ps()
                        for j in range(4):
                            self.mm(p[0:64, j * 128:(j + 1) * 128],
                                    BK[d][hp * 64:(hp + 1) * 64, j, cl, 0, :],
                                    AR[d][hp * 64:(hp + 1) * 64, j, cl, :, :].rearrange("p q t -> p (q t)"),
                                    True, True, [BK[d], AR[d]], [p])
                        p4 = p[0:64, :].rearrange("p (h q t) -> p h q t", q=2, t=64)
                        self.tt("dve", Lt0[:, hp::2, :], p4[:, :, 0, :], msk[:, 0, :].unsqueeze(1).to_broadcast([64, 4, 64]),
                                ALU.mult, [p, cst], [Lt0])
                        self.tt("dve", MRB[:, hp::2, :], p4[:, :, 1, :], msk[:, 1, :].unsqueeze(1).to_broadcast([64, 4, 64]),
                                ALU.mult, [p, cst], [MRB])
                        p2 = self.ps()
                        for j in range(4):
                            self.mm(p2[0:64, j * 64:(j + 1) * 64],
                                    AR[d][hp * 64:(hp + 1) * 64, j, cl, 0, :], BK[d][hp * 64:(hp + 1) * 64, j, cl, 0, :],
                                    True, True, [AR[d], BK[d]], [p2])
                        self.tt("dve", L0[:, hp::2, :], p2[0:64, 0:256].rearrange("p (h s) -> p h s", s=64),
                                mskL.unsqueeze(1).to_broadcast([64, 4, 64]), ALU.mult, [p2, cst], [L0])
                    self.dma(self.mrbt_scr[c, d * 64:(d + 1) * 64, :], MRB[:].rearrange("p h s -> p (h s)"),
                             [MRB], [self.mrbt_scr], eng="act")
                    yield
                    T0 = Ttm[0]
                    self.tt("pool", T0[:], Lt0[:].bitcast(F32), id64.unsqueeze(1).to_broadcast([64, 8, 64]), ALU.add,
                            [Lt0, cst], [T0])
                    L_prev, Tt_prev, Lt_prev = L0, T0, Lt0
                    for lev in range(1, 6):
                        L_new, Lt_new, Tt_new = Lm[lev % 2], Ltm[lev % 2], Ttm[lev % 2]
                        pA = self.ps()
                        for h in range(8):
                            self.mm(pA[0:64, h * 64:(h + 1) * 64], Lt_prev[:, h, :], L_prev[:, h, :], True, True,
                                    [Lt_prev, L_prev], [pA])
                        if lev < 5:
                            pB = self.ps()
                            for h in range(8):
                                self.mm(pB[0:64, h * 64:(h + 1) * 64], L_prev[:, h, :], Lt_prev[:, h, :], True, True,
                                        [Lt_prev, L_prev], [pB])
                        self.tt("dve", ILm[:], pA[0:64, :].rearrange("p (h s) -> p h s", s=64),
                                id64.unsqueeze(1).to_broadcast([64, 8, 64]), ALU.add, [pA, cst], [ILm])
                        if lev < 5:
                            self.cp("dve", L_new[:].rearrange("p h s -> p (h s)"), pA[0:64, :], [pA], [L_new])
                            self.cp("act", Lt_new[:].rearrange("p h s -> p (h s)"), pB[0:64, :], [pB], [Lt_new])
                        yield
                        pC = self.ps()
                        for h in range(8):
                            self.mm(pC[0:64, h * 64:(h + 1) * 64], ILm[:, h, :], Tt_prev[:, h, :], True, True,
                                    [ILm, Tt_prev], [pC])
                        if lev < 5:
                            self.cp("act", Tt_new[:].rearrange("p h s -> p (h s)"), pC[0:64, :], [pC], [Tt_new])
                        else:
                            self.cp("act", Tfin[:].rearrange("p h s -> p (h s)"), pC[0:64, :], [pC], [Tfin])
                        L_prev, Tt_prev, Lt_prev = L_new, Tt_new, Lt_new
                        yield
                    self.dma(self.ttt_scr[c, d * 64:(d + 1) * 64, :], Tfin[:].rearrange("p h s -> p (h s)"),
                             [Tfin], [self.ttt_scr], eng="act")


            def run_all(*gens):
                gens = list(gens)
                while gens:
                    for g in list(gens):
                        try:
                            next(g)
                        except StopIteration:
                            gens.remove(g)

            def seq_(*gens):
                for g in gens:
                    yield from g

            prev = None
            tl_ = tiles[:self.ktiles]
            for ti_, (seq, g0, is_s, mc) in enumerate(tl_):
                nxt = tl_[ti_ + 1] if ti_ + 1 < len(tl_) else None
                if prev is None:
                    run_all(front(seq, g0, is_s, mc, nxt))
                else:
                    run_all(seq_(chunk_d(prev, 1, [0, 1], 0), chunk_sck(prev)), chunk_d(prev, 1, [2, 3], 1), front(seq, g0, is_s, mc, nxt))
                run_all(prep(g0, 0))
                run_all(chunk_d(g0, 0, [0, 1], 0), chunk_d(g0, 0, [2, 3], 1), prep(g0, 1))
                run_all(bonus(g0))
                prev = g0
            run_all(seq_(chunk_d(prev, 1, [0, 1], 0), chunk_sck(prev)), chunk_d(prev, 1, [2, 3], 1))

    def phaseB(self):
        with contextlib.ExitStack() as st:
            side = [self.gen_c0(st), self.phaseB_lru(st)]
            post = self.phaseB_post(st)
            next(post)
            done = np.zeros((2, NCH), bool)
            posted = [False] * (NTOK // 128)
            si = 0
            for info in self.phaseB_chain(st):
                for (d, c) in info:
                    done[d, c] = True
                for _ in range(2):
                    if side:
                        g = side[si % len(side)]
                        si += 1
                        try:
                            next(g)
                        except StopIteration:
                            side.remove(g)
                for b in range(NTOK // 128):
                    if not posted[b] and done[:, 2 * b:2 * b + 2].all():
                        posted[b] = True
                        post.send(b)
            for g in side:
                for _ in g:
                    pass
            for b in range(NTOK // 128):
                if not posted[b]:
                    post.send(b)
            if self.debug:
                self.S.barrier()
                self.dump("yscr", self.y_scr[:], [128, 8, NTOK], BF16, [self.y_scr])

    def gen_c0(self, st):
        stg = [self.sb(st, "stgC%d" % i, [128, 8, 512], F32) for i in range(2)]
        wb = [self.sb(st, "wbC%d" % i, [128, 8, 512], BF16) for i in range(2)]
        w1src = self.w1[:].rearrange("(kc p) n -> p kc n", p=128)
        for blk in range(8):
            s, o = stg[blk % 2], wb[blk % 2]
            self.dma(s[:], w1src[:, :, blk * 512:(blk + 1) * 512], [], [s])
            self.cp("pool", o[:], s[:], [s], [o])
            self.dma(self.w1_scr[blk], o[:], [o], [self.w1_scr], eng="act")
            yield
        w2src = self.w2[:].rearrange("(fc p) n -> p fc n", p=128)
        for m in range(8):
            s, o = stg[m % 2], wb[m % 2]
            s4 = s[:].rearrange("p k (a b) -> p (k a) b", b=128)
            o4 = o[:].rearrange("p k (a b) -> p (k a) b", b=128)
            self.dma(s4, w2src[:, :, m * 128:(m + 1) * 128], [], [s])
            self.cp("pool", o[:], s[:], [s], [o])
            self.dma(self.w2_scr[m], o4, [o], [self.w2_scr], eng="act")
            yield

    def phaseB_lru(self, st):
        prm, cst, misc = self.prm_t, self.cst_t, self.misc
        if True:
            wbd32 = self.sb(st, "wbd32", [128, 16, 128], F32)
            wbd = self.sb(st, "wbd", [128, 16, 128], BF16)
            self.memset("pool", wbd32[:], 0.0, [wbd32])
            self.S.barrier_on(wbd32)
            wbd32.b.multi = True
            for gi, src in enumerate((self.lwa, self.lwx)):
                for d in range(2):
                    for j in range(4):
                        for hb in range(2):
                            self.dma(wbd32[hb * 64:(hb + 1) * 64, (gi * 2 + d) * 4 + j, hb * 64:(hb + 1) * 64],
                                     src[d, 2 * j + hb], [], [wbd32])
            self.cp("dve", wbd[:], wbd32[:], [wbd32], [wbd])
            TM = TS
            xbp = self.sb(st, "xbp", [128, TM + 4], F32)
            xc = self.sb(st, "xc", [128, TM], F32)
            xcb = self.sb(st, "xcb", [128, TM], BF16)
            gt = self.sb(st, "gt_l", [128, TM], BF16)
            a_t = self.sb(st, "a_t", [128, TM], F32)
            bx_t = self.sb(st, "bx_t", [128, TM], F32)
            s_t = self.sb(st, "s_t", [128, TM], F32)
            hs = [self.sb(st, "hs%d" % d, [128, TM], F32) for d in range(2)]
            yb = self.sb(st, "yb", [128, TM], BF16)
            for (seq, g0, T) in ((0, 0, TS), (1, TS, TP), (2, TS + TP, TP)):
                for j in range(4):
                    self.memset("pool", xbp[:, 0:2], 0.0, [xbp])
                    self.memset("pool", xbp[:, T + 2:T + 4], 0.0, [xbp])
                    self.dma(xbp[:, 2:T + 2], self.xb_scr[:, j, g0:g0 + T], [self.xb_scr], [xbp])
                    self.dma(gt[:, 0:T], self.gate_scr[:, j, g0:g0 + T], [self.gate_scr], [gt], eng="act")
                    cw = lambda i: prm[:, P_CW + 4 * i + j:P_CW + 4 * i + j + 1]
                    self.act(xc[:, 0:T], xbp[:, 0:T], AF.Identity, [xbp, prm], [xc], scale=cw(0), bias=prm[:, P_CB + j:P_CB + j + 1])
                    for i in range(1, 4):
                        self.stt(xc[:, 0:T], xbp[:, i:i + T], cw(i), xc[:, 0:T], ALU.mult, ALU.add, [xbp, prm, xc], [xc])
                    self.cp("pool", xcb[:, 0:T], xc[:, 0:T], [xc], [xcb])
                    yield
                    for d in range(2):
                        for t0 in range(0, T, 512):
                            tw = min(512, T - t0)
                            p = self.ps()
                            self.mm(p[:, 0:tw], wbd[:, (0 * 2 + d) * 4 + j, :], xcb[:, t0:t0 + tw], True, True, [wbd, xcb], [p])
                            self.act(s_t[:, t0:t0 + tw], p[:, 0:tw], AF.Sigmoid, [p, prm], [s_t],
                                     bias=prm[:, P_BA + 4 * d + j:P_BA + 4 * d + j + 1])
                            p2 = self.ps()
                            self.mm(p2[:, 0:tw], wbd[:, (1 * 2 + d) * 4 + j, :], xcb[:, t0:t0 + tw], True, True, [wbd, xcb], [p2])
                            self.act(bx_t[:, t0:t0 + tw], p2[:, 0:tw], AF.Sigmoid, [p2, prm], [bx_t],
                                     bias=prm[:, P_BX + 4 * d + j:P_BX + 4 * d + j + 1])
                        col = 4 + d * 4 + j
                        self.act(a_t[:, 0:T], s_t[:, 0:T], AF.Exp, [s_t, misc], [a_t], scale=misc[:, col:col + 1])
                        self.act(s_t[:, 0:T], s_t[:, 0:T], AF.Exp, [s_t, misc], [s_t], scale=misc[:, col + 8:col + 9])
                        self.act(s_t[:, 0:T], s_t[:, 0:T], AF.Sqrt, [s_t], [s_t], scale=-1.0, bias=1.0)
                        self.tt("pool", bx_t[:, 0:T], bx_t[:, 0:T], xc[:, 0:T], ALU.mult, [bx_t, xc], [bx_t])
                        self.tt("dve", bx_t[:, 0:T], bx_t[:, 0:T], s_t[:, 0:T], ALU.mult, [bx_t, s_t], [bx_t])
                        h = hs[d]
                        if seq == 0:
                            init = prm[:, P_H0 + 4 * d + j:P_H0 + 4 * d + j + 1]
                        else:
                            init = 0.0
                        if d == 0:
                            self.scan(h[:, 0:T], a_t[:, 0:T], bx_t[:, 0:T], init, [a_t, bx_t, prm], [h])
                        else:
                            self.scan(h[:, 0:T][:, ::-1], a_t[:, 0:T][:, ::-1], bx_t[:, 0:T][:, ::-1], init, [a_t, bx_t, prm], [h])
                        if seq > 0:
                            col_o = j * 4 + (seq - 1) * 2 + d
                            te = T - 1 if d == 0 else 0
                            self.cp("pool", self.stl_t[:, col_o:col_o + 1], h[:, te:te + 1], [h], [self.stl_t])
                        yield
                    self.tt("pool", hs[0][:, 0:T], hs[0][:, 0:T], hs[1][:, 0:T], ALU.add, [hs[0], hs[1]], [hs[0]])
                    self.tt("dve", yb[:, 0:T], hs[0][:, 0:T], gt[:, 0:T], ALU.mult, [hs[0], gt], [yb])
                    self.dma(self.y_scr[:, 4 + j, g0:g0 + T], yb[:, 0:T], [yb], [self.y_scr])
            self.dma(self.stl_o[:], self.stl_t[:], [self.stl_t], [self.stl_o])

    def phaseB_chain(self, st):
        if True:
            NB = 3
            def ring(name, dt=BF16):
                return [self.sb2(st, "%s%d" % (name, i), [128, 512], dt) for i in range(NB)]
            art, rrt, ttt, akt, mrbt, mrkt, bh, kh, vt = [ring(n) for n in
                                                          ("c_art", "c_rrt", "c_ttt", "c_akt", "c_mrbt", "c_mrkt", "c_bh", "c_kh", "c_vt")]
            pend = ring("c_pend", F32)
            Hf = self.sb2(st, "Hf", [128, 512], F32)
            Hb = self.sb2(st, "Hb", [128, 512], BF16)
            Zs = self.sb2(st, "Zs", [128, 512], BF16)
            Us = self.sb2(st, "Us", [128, 512], BF16)
            Yt = [self.sb2(st, "Yt%d" % i, [128, 512], F32) for i in range(2)]
            tmpH = self.sb2(st, "tmpH", [128, 512], F32)
            hs_ = lambda h: slice(h * 64, (h + 1) * 64)
            steps = []
            for (seq, cbase, n) in ((0, 0, 32), (1, 32, 4), (2, 36, 4)):
                for i in range(n):
                    steps.append((seq, cbase, n, i))

            def loads(k):
                seq, cbase, n, i = steps[k]
                r = k % NB
                for d in range(2):
                    sl = slice(d * 64, (d + 1) * 64)
                    c = cbase + i if d == 0 else cbase + n - 1 - i
                    for (tl, scr) in ((art, self.art_scr), (rrt, self.rrt_scr), (ttt, self.ttt_scr), (akt, self.akt_scr),
                                      (mrbt, self.mrbt_scr), (mrkt, self.mrkt_scr), (bh, self.bh_scr), (kh, self.kh_scr),
                                      (pend, self.pend_scr)):
                        self.dma(tl[r][d][sl, :], scr[c, sl, :], [scr], [tl[r][d]], eng="sp")
                    self.dma(vt[r][d][sl, :], self.vt_scr[c], [self.vt_scr], [vt[r][d]], eng="sp")

            loads(0)
            for k in range(len(steps)):
                seq, cbase, n, i = steps[k]
                step = k + 1
                r = k % NB
                if k + 1 < len(steps):
                    loads(k + 1)
                if i == 0:
                    for d in range(2):
                        sl = slice(d * 64, (d + 1) * 64)
                        if seq == 0:
                            self.dma(Hf[d][sl, :], self.h0r[sl, :], [], [Hf[d]], eng="act")
                        else:
                            self.memset("dve", Hf[d][sl, :], 0.0, [Hf[d]])
                        self.cp("dve", Hb[d][sl, :], Hf[d][sl, :], [Hf[d]], [Hb[d]])
                if True:
                    ctx = []
                    for d in range(2):
                        sl = slice(d * 64, (d + 1) * 64)
                        c = cbase + i if d == 0 else cbase + n - 1 - i
                        ops = tuple(x[r][d] for x in (art, rrt, ttt, akt, mrbt, mrkt, bh, kh, vt, pend))
                        ctx.append((d, sl, c, ops))
                    pZs = {}
                    for (d, sl, c, (A_, R_, T_, AK_, MRB_, MRK_, B_, K_, V_, PE_)) in ctx:
                        H_, Z_ = Hb[d], Zs[d]
                        pZ = self.ps()
                        for h in range(8):
                            self.mm(pZ[sl, hs_(h)], A_[sl, hs_(h)], H_[sl, hs_(h)], True, False, [A_, H_], [pZ])
                            self.mm(pZ[sl, hs_(h)], AK_[sl, hs_(h)], V_[sl, hs_(h)], False, True, [AK_, V_], [pZ])
                        pZs[d] = pZ
                    pYs = {}
                    for (d, sl, c, (A_, R_, T_, AK_, MRB_, MRK_, B_, K_, V_, PE_)) in ctx:
                        H_ = Hb[d]
                        pY = self.ps()
                        pYs[d] = pY
                    for (d, sl, c, ops) in ctx:
                        self.cp("act", Zs[d][sl, :], pZs[d][sl, :], [pZs[d]], [Zs[d]])
                    pUs = {}
                    for (d, sl, c, (A_, R_, T_, AK_, MRB_, MRK_, B_, K_, V_, PE_)) in ctx:
                        Z_ = Zs[d]
                        pU = self.ps()
                        for h in range(8):
                            self.mm(pU[sl, hs_(h)], T_[sl, hs_(h)], Z_[sl, hs_(h)], True, True, [T_, Z_], [pU])
                        pUs[d] = pU
                    for (d, sl, c, ops) in ctx:
                        self.cp("act", Us[d][sl, :], pUs[d][sl, :], [pUs[d]], [Us[d]])
                    pHs = {}
                    for (d, sl, c, (A_, R_, T_, AK_, MRB_, MRK_, B_, K_, V_, PE_)) in ctx:
                        U_ = Us[d]
                        self.tt("dve", tmpH[d][sl, :], Hf[d][sl, :], PE_[sl, :], ALU.mult, [Hf[d], PE_], [tmpH[d]])
                        pH = self.ps()
                        for h in range(8):
                            self.mm(pH[sl, hs_(h)], B_[sl, hs_(h)], U_[sl, hs_(h)], True, False, [B_, U_], [pH])
                            self.mm(pH[sl, hs_(h)], K_[sl, hs_(h)], V_[sl, hs_(h)], False, True, [K_, V_], [pH])
                        pHs[d] = pH
                    for (d, sl, c, (A_, R_, T_, AK_, MRB_, MRK_, B_, K_, V_, PE_)) in ctx:
                        H_, U_ = Hb[d], Us[d]
                        pY = pYs[d]
                        for h in range(8):
                            self.mm(pY[sl, hs_(h)], R_[sl, hs_(h)], H_[sl, hs_(h)], True, False, [R_, H_], [pY])
                            self.mm(pY[sl, hs_(h)], MRB_[sl, hs_(h)], U_[sl, hs_(h)], False, False, [MRB_, U_], [pY])
                            self.mm(pY[sl, hs_(h)], MRK_[sl, hs_(h)], V_[sl, hs_(h)], False, True, [MRK_, V_], [pY])
                    for (d, sl, c, ops) in ctx:
                        self.tt("dve", Hf[d][sl, :], tmpH[d][sl, :], pHs[d][sl, :], ALU.add, [tmpH[d], pHs[d]], [Hf[d]])
                        self.cp("dve", Hb[d][sl, :], Hf[d][sl, :], [Hf[d]], [Hb[d]])
                    for (d, sl, c, ops) in ctx:
                        y = Yt[step % 2][d]
                        self.cp("act", y[sl, :], pYs[d][sl, :], [pYs[d]], [y])
                        self.dma(self.ytok_scr[d, c * 64:(c + 1) * 64, :], y[sl, :], [y], [self.ytok_scr], eng="act")
                if seq > 0 and i == n - 1:
                    self.dma(self.str_o[seq - 1], Hf[0][:], [Hf[0], Hf[1]], [self.str_o], eng="act")
                yield [(0, cbase + i), (1, cbase + n - 1 - i)]

    def phaseB_post(self, st):
        prm, cst = self.prm_t, self.cst_t
        ident = self.ident
        if True:
            yf = [self.sb(st, "yf%d" % i, [128, 512], F32) for i in range(2)]
            yb2 = [self.sb(st, "yb2%d" % i, [128, 512], F32) for i in range(2)]
            cen = self.sb(st, "cen", [128, 8, 64], F32)
            sqv = self.sb(st, "sqv", [128, 8, 64], F32)
            mean = self.sb(st, "mean", [128, 8], F32)
            var = self.sb(st, "var", [128, 8], F32)
            gl = [self.sb(st, "gl%d" % i, [128, 4, 128], BF16) for i in range(2)]
            bl = [self.sb(st, "bl%d" % i, [128, 4, 128], BF16) for i in range(2)]
            ynT = self.sb(st, "ynT", [128, 4, 128], F32)
            yo = [self.sb(st, "yo%d" % i, [128, 4, 128], BF16) for i in range(2)]
            it = -1
            blk = yield
            while True:
                it += 1
                g0 = blk * 128
                a, b = yf[it % 2], yb2[it % 2]
                g_, b_ = gl[it % 2], bl[it % 2]
                o = yo[it % 2]
                self.dma(a[:], self.ytok_scr[0, g0:g0 + 128, :], [self.ytok_scr], [a])
                self.dma(b[:], self.ytok_scr[1, g0:g0 + 128, :], [self.ytok_scr], [b], eng="act")
                self.dma(g_[:], self.g_scr[:, :, g0:g0 + 128], [self.g_scr], [g_])
                self.dma(b_[:], self.bon_scr[:, :, g0:g0 + 128], [self.bon_scr], [b_], eng="act")
                a3 = a[:].rearrange("p (h v) -> p h v", v=64)
                self.tt("pool", a[:], a[:], b[:], ALU.add, [a, b], [a])
                self.S.op("dve", lambda e, a3=a3: e.tensor_reduce(out=mean[:], in_=a3, op=ALU.add, axis=mybir.AxisListType.X),
                          [a.b], [mean.b])
                self.tsc("dve", mean[:], mean[:], 1.0 / 64, ALU.mult, [mean], [mean])
                self.tt("dve", cen[:], a3, mean[:].unsqueeze(2).to_broadcast([128, 8, 64]), ALU.subtract, [a, mean], [cen])
                self.tt("pool", sqv[:], cen[:], cen[:], ALU.mult, [cen], [sqv])
                self.S.op("dve", lambda e: e.tensor_reduce(out=var[:], in_=sqv[:], op=ALU.add, axis=mybir.AxisListType.X),
                          [sqv.b], [var.b])
                self.act(var[:], var[:], AF.Sqrt, [var, self.epsT], [var], scale=1.0 / 64, bias=self.epsT[:, 1:2])
                self.S.op("dve", lambda e: e.reciprocal(out=var[:], in_=var[:]), [var.b], [var.b])
                self.tt("dve", cen[:], cen[:], var[:].unsqueeze(2).to_broadcast([128, 8, 64]), ALU.mult, [cen, var], [cen])
                p = self.ps()
                cen2 = cen[:].rearrange("p h v -> p (h v)")
                for j in range(4):
                    self.tr(p[:, j * 128:(j + 1) * 128], cen2[:, j * 128:(j + 1) * 128], ident[:], [cen, ident], [p])
                for j in range(4):
                    self.act(ynT[:, j, :], p[:, j * 128:(j + 1) * 128], AF.Identity, [p, prm], [ynT],
                             scale=prm[:, P_LNG + j:P_LNG + j + 1], bias=prm[:, P_LNB + j:P_LNB + j + 1])
                self.tt("pool", ynT[:], ynT[:], b_[:], ALU.add, [ynT, b_], [ynT])
                self.tt("dve", o[:], ynT[:], g_[:], ALU.mult, [ynT, g_], [o])
                self.dma(self.y_scr[:, 0:4, g0:g0 + 128], o[:], [o], [self.y_scr])
                blk = yield

    def phaseC(self):
        prm, cst = self.prm_t, self.cst_t
        ident, ones_bf = self.ident, self.ones_bf
        with contextlib.ExitStack() as st:
            pass
        import os
        kcc = int(os.environ.get("KCC", "99"))
        if kcc == 0:
            return
        with contextlib.ExitStack() as st:
            wout = self.sb(st, "wout", [128, 8, D], BF16)
            wsrc = self.w_out[:].rearrange("(kc p) n -> p kc n", p=128)
            with contextlib.ExitStack() as st2:
                stg = [self.sb(st2, "stgD%d" % i, [128, 8, 256], F32) for i in range(2)]
                for cb in range(4):
                    s = stg[cb % 2]
                    self.dma(s[:], wsrc[:, :, cb * 256:(cb + 1) * 256], [], [s])
                    self.cp("act", wout[:, :, cb * 256:(cb + 1) * 256], s[:], [s], [wout])
            self.S.barrier()
            yT = self.sb(st, "yT_c", [128, 8, TC], BF16)
            o1T = self.sb(st, "o1T", [128, 8, TC], F32)
            sq1 = self.sb(st, "sq1_c", [128, 8, TC], BF16)
            oT = self.sb(st, "oT", [128, 8, TC], F32)
            sq = self.sb(st, "sq_c", [128, 8, TC], BF16)
            xTs = [self.sb(st, "xT_c%d" % i, [128, 8, TC], F32) for i in range(2)]
            h2 = self.sb(st, "h2", [128, 8, TC], BF16)
            f = self.sb(st, "f_c", [128, 32, TC], BF16)
            otoks = [self.sb(st, "otok%d" % i, [128, 2, D], F32) for i in range(1)]
            rstd = self.sb(st, "rstd_c", [128, TC], F32)
            tmp = [self.sb(st, "tmpC%d" % i, [128, TC], F32) for i in range(2)]
            NW = 3
            w1r = [self.sb(st, "w1r%d" % i, [128, 8, 512], BF16) for i in range(NW)]
            w2r = [self.sb(st, "w2r%d" % i, [128, 32, 128], BF16) for i in range(2)]
            ntile = min(NTOK // TC, kcc)
            wseq = []
            for ti_ in range(ntile):
                wseq += [("w1", b_) for b_ in range(8)] + [("w2", b_) for b_ in range(8)]
            wstate = {"issued": 0, "w1": 0, "w2": 0}
            wbuf = {}

            def issue_upto(n):
                while wstate["issued"] < min(n, len(wseq)):
                    k_ = wstate["issued"]
                    kind, b_ = wseq[k_]
                    ring = w1r if kind == "w1" else w2r
                    buf = ring[wstate[kind] % len(ring)]
                    wstate[kind] += 1
                    scr = self.w1_scr if kind == "w1" else self.w2_scr
                    self.dma(buf[:], scr[b_], [scr], [buf], eng="sp" if k_ % 2 == 0 else "act")
                    wbuf[k_] = buf
                    wstate["issued"] += 1

            def rms(sqt, R):
                p = self.ps()
                for j in range(8):
                    self.mm(p[:], ones_bf[:], sqt[:, j, :], j == 0, j == 7, [ones_bf, sqt], [p])
                self.act(rstd[:], p[:], AF.Sqrt, [p, self.epsT], [rstd], scale=1.0 / D, bias=self.epsT[:, 0:1])
                self.S.op("dve", lambda e: e.reciprocal(out=rstd[:], in_=rstd[:]), [rstd.b], [rstd.b])

            def resid(gg, mc, oT, xT):
                for j in range(8):
                    t = tmp[j % 2]
                    self.tt("dve", t[:], oT[:, j, :], rstd[:], ALU.mult, [oT, rstd], [t])
                    self.stt(xT[:, j, :], t[:], gg[:, j, mc:mc + 1], xT[:, j, :], ALU.mult, ALU.add, [t, gg, xT], [xT])

            def head(tj):
                gj = tj * TC
                mcj = 0 if gj < TS else 1
                xT = xTs[tj % 2]
                self.dma(xT[:], self.xT_scr[:, :, gj:gj + TC], [self.xT_scr], [xT], eng="act")
                yield
                rms(sq1, None)
                yield
                for j in range(8):
                    t = tmp[j % 2]
                    self.tt("dve", t[:], o1T[:, j, :], rstd[:], ALU.mult, [o1T, rstd], [t])
                    self.stt(xT[:, j, :], t[:], self.gg1[:, j, mcj:mcj + 1], xT[:, j, :], ALU.mult, ALU.add, [t, self.gg1, xT], [xT])
                    self.act(sq1[:, j, :], xT[:, j, :], AF.Square, [xT], [sq1])
                    if j % 2 == 1:
                        yield
                rms(sq1, None)
                yield
                for j in range(8):
                    t = tmp[j % 2]
                    self.tt("dve", t[:], xT[:, j, :], rstd[:], ALU.mult, [xT, rstd], [t])
                    self.act(h2[:, j, :], t[:], AF.Identity, [t, self.gs2, self.modT], [h2],
                             scale=self.gs2[:, j, mcj:mcj + 1], bias=self.modT[:, 24 + j, mcj:mcj + 1])
                    if j % 2 == 1:
                        yield

            def wout_stage(tj):
                gj = tj * TC
                self.dma(yT[:], self.y_scr[:, :, gj:gj + TC], [self.y_scr], [yT])
                for m in range(8):
                    p = self.ps()
                    for kc in range(8):
                        self.mm(p[:], wout[:, kc, m * 128:(m + 1) * 128], yT[:, kc, :], kc == 0, kc == 7, [wout, yT], [p])
                    self.cp("act", o1T[:, m, :], p[:], [p], [o1T])
                    self.act(sq1[:, m, :], p[:], AF.Square, [p], [sq1])

            wi = 0
            for ti in range(NTOK // TC):
                if ti >= kcc:
                    break
                g0 = ti * TC
                mc = 0 if g0 < TS else 1
                xT = xTs[ti % 2]
                if ti == 0:
                    wout_stage(0)
                    for _ in head(0):
                        pass
                issue_upto(ti * 16 + 3)
                for blk in range(8):
                    issue_upto(ti * 16 + blk + 3)
                    w = wbuf[ti * 16 + blk]
                    for c4 in range(4):
                        fc = blk * 4 + c4
                        p = self.ps()
                        for kc in range(8):
                            self.mm(p[:], w[:, kc, c4 * 128:(c4 + 1) * 128], h2[:, kc, :], kc == 0, kc == 7, [w, h2], [p])
                        t = tmp[fc % 2]
                        self.act(t[:], p[:], AF.Relu, [p], [t])
                        self.tt("pool" if fc % 2 == 0 else "dve", f[:, fc, :], t[:], t[:], ALU.mult, [t], [f])
                hg = None
                if ti + 1 < ntile:
                    wout_stage(ti + 1)
                    hg = head(ti + 1)
                for m in range(8):
                    if hg is not None:
                        for _ in range(2):
                            try:
                                next(hg)
                            except StopIteration:
                                hg = None
                                break
                    issue_upto(ti * 16 + 8 + m + 2)
                    w = wbuf[ti * 16 + 8 + m]
                    p = self.ps()
                    for fc in range(32):
                        self.mm(p[:], w[:, fc, :], f[:, fc, :], fc == 0, fc == 31, [w, f], [p])
                    self.cp("act", oT[:, m, :], p[:], [p], [oT])
                    self.act(sq[:, m, :], p[:], AF.Square, [p], [sq])
                if hg is not None:
                    for _ in hg:
                        pass
                rms(sq, None)
                resid(self.gg2, mc, oT, xT)
                for hh in range(2):
                    otok = otoks[0]
                    for s2 in range(2):
                        s_ = hh * 2 + s2
                        for half in range(2):
                            p = self.ps()
                            for jj in range(4):
                                j = half * 4 + jj
                                self.tr(p[:, jj * 128:(jj + 1) * 128], xT[:, j, s_ * 128:(s_ + 1) * 128], ident[:], [xT, ident], [p])
                            self.cp("act" if half == 0 else "dve", otok[:, s2, half * 512:(half + 1) * 512], p[:], [p], [otok])
                    if g0 < TS:
                        dst = self.ys[g0 + hh * 256:g0 + (hh + 1) * 256, :].rearrange("(s p) f -> p s f", p=128)
                        self.dma(dst, otok[:], [otok], [self.ys])
                    else:
                        dst = self.yp[hh * 256:(hh + 1) * 256, :].rearrange("(s p) f -> p s f", p=128)
                        self.dma(dst, otok[:], [otok], [self.yp])


def _fm(v):
    v = np.asarray(v, np.float32).reshape(-1, 128)
    return np.ascontiguousarray(v.T)


def _pos_embed():
    def sincos(pos, dim):
        omega = (1.0 / (10000.0 ** (np.arange(dim // 2, dtype=np.float32) / np.float32(dim // 2)))).astype(np.float32)
        ang = pos.astype(np.float32)[:, None] * omega[None, :]
        return np.concatenate([np.sin(ang), np.cos(ang)], axis=-1).astype(np.float32)
    rows = TS // 64
    half = D // 2
    e_row = sincos(np.arange(rows), half)
    e_col = sincos(np.arange(64), half)
    emb = np.concatenate([np.broadcast_to(e_row[:, None, :], (rows, 64, half)),
                          np.broadcast_to(e_col[None, :, :], (rows, 64, half))], axis=-1)
    return np.ascontiguousarray(emb.reshape(rows * 64, D).astype(np.float32))


def _consts():
    c = np.zeros((128, NCST), np.float32)
    c[:, C_ID:C_ID + 128] = np.eye(128, dtype=np.float32)
    ob = np.zeros((128, 128), np.float32)
    ob[:64, :64] = 1.0
    ob[64:, 64:] = 1.0
    c[:, C_OB:C_OB + 128] = ob
    s = np.arange(64)[:, None]
    t = np.arange(64)[None, :]
    msi = np.zeros((128, 2, 64), np.float32)
    msi[:64, 0] = (s < t)
    msi[:64, 1] = (s <= t)
    msi[64:, 0] = (s > t)
    msi[64:, 1] = (s >= t)
    c[:, C_MSI:C_MSI + 128] = msi.reshape(128, 128)
    ml = np.zeros((128, 64), np.float32)
    ml[:64] = (t < s)
    ml[64:] = (t > s)
    c[:, C_ML:C_ML + 64] = ml
    ids = np.zeros((128, 64), np.float32)
    ids[:64] = np.eye(64)
    ids[64:] = np.eye(64)
    c[:, C_IDS:C_IDS + 64] = ids
    c[:64, C_MSI1:C_MSI1 + 128] = msi[64:].reshape(64, 128)
    c[:64, C_ML1:C_ML1 + 64] = ml[64:]
    tt_ = np.arange(TT)
    c[:, C_RMF:C_RMF + TT] = (tt_ % 64 != 0).astype(np.float32)[None, :]
    c[:, C_RMB:C_RMB + TT] = (tt_ % 64 != 63).astype(np.float32)[None, :]
    return c


_NC_CACHE = {}


def kernel(x_prompt, x_sample, c, state_rwkv, state_lru, c_ctx, w_mod, b_mod,
           g_pre_mix, g_post_mix, g_pre_mlp, g_post_mlp, w_in,
           rwkv_w0, rwkv_w_up, rwkv_a0, rwkv_a_up, rwkv_g_up, rwkv_k_k, rwkv_k_a, rwkv_r_k,
           rwkv_lnx_g, rwkv_lnx_b, lru_conv_w, lru_conv_b, lru_wa, lru_ba, lru_wx, lru_bx,
           lru_lambda, w_out, w_mlp1, w_mlp2, _debug=False):
    f = lambda a: np.ascontiguousarray(np.asarray(a, np.float32))
    x_prompt, x_sample, c, state_rwkv, state_lru, c_ctx = map(f, (x_prompt, x_sample, c, state_rwkv, state_lru, c_ctx))
    nc = K(debug=_debug).build()
    pe = _pos_embed()
    cst = _consts()
    shared = {
        "pe": pe, "cst": cst,
        "w_mod": f(w_mod[0]), "w_in": f(w_in[0]), "w_out": f(w_out[0]), "w1": f(w_mlp1[0]), "w2": f(w_mlp2[0]),
        "wup": f(rwkv_w_up[0]).reshape(128, 512), "aup": f(rwkv_a_up[0]).reshape(128, 512), "gup": f(rwkv_g_up[0]),
        "lwa": f(lru_wa[0]), "lwx": f(lru_wx[0]),
    }
    prm0 = np.zeros((128, NPRM), np.float32)
    prm0[:, P_GPRE:P_GPRE + 8] = _fm(g_pre_mix[0])
    prm0[:, P_GPOST:P_GPOST + 8] = _fm(g_post_mix[0])
    prm0[:, P_GPRE2:P_GPRE2 + 8] = _fm(g_pre_mlp[0])
    prm0[:, P_GPOST2:P_GPOST2 + 8] = _fm(g_post_mlp[0])
    prm0[:, P_BMOD:P_BMOD + 48] = _fm(b_mod[0])
    for d in range(2):
        prm0[:, P_W0 + 4 * d:P_W0 + 4 * d + 4] = _fm(rwkv_w0[0, d])
        prm0[:, P_A0 + 4 * d:P_A0 + 4 * d + 4] = _fm(rwkv_a0[0, d])
        prm0[:, P_BA + 4 * d:P_BA + 4 * d + 4] = _fm(lru_ba[0, d])
        prm0[:, P_BX + 4 * d:P_BX + 4 * d + 4] = _fm(lru_bx[0, d])
        prm0[:, P_LAM + 4 * d:P_LAM + 4 * d + 4] = _fm(lru_lambda[0, d])
    prm0[:, P_KK:P_KK + 4] = _fm(rwkv_k_k[0])
    prm0[:, P_KA:P_KA + 4] = _fm(rwkv_k_a[0])
    prm0[:, P_RK:P_RK + 4] = _fm(np.asarray(rwkv_r_k[0]).reshape(-1))
    prm0[:, P_LNG:P_LNG + 4] = _fm(rwkv_lnx_g[0])
    prm0[:, P_LNB:P_LNB + 4] = _fm(rwkv_lnx_b[0])
    for i in range(4):
        prm0[:, P_CW + 4 * i:P_CW + 4 * i + 4] = _fm(lru_conv_w[0, i])
    prm0[:, P_CB:P_CB + 4] = _fm(lru_conv_b[0])
    in_maps = []
    for i in range(8):
        prm = prm0.copy()
        for d in range(2):
            prm[:, P_H0 + 4 * d:P_H0 + 4 * d + 4] = _fm(state_lru[i, 0, d])
        cT = np.zeros((128, 8, 2), np.float32)
        cT[:, :, 0] = _fm(c[i])
        cT[:, :, 1] = _fm(c_ctx)
        h0 = np.ascontiguousarray(state_rwkv[i, 0].transpose(0, 3, 1, 2)).reshape(128, 512)
        m = dict(shared)
        m.update({"xs": x_sample[i], "xp": np.ascontiguousarray(x_prompt[2 * i:2 * i + 2].reshape(2 * TP, D)),
                  "cT": cT.reshape(128, 16), "h0r": h0, "prm": prm})
        in_maps.append(m)
    res = run_bass_kernel_spmd(nc, in_maps, core_ids=list(range(8)))
    R = res.results
    y_prompt = np.zeros((16, TP, D), np.float32)
    y_sample = np.zeros((8, TS, D), np.float32)
    st_r = np.zeros((16, 1, 2, 8, 64, 64), np.float32)
    st_l = np.zeros((16, 1, 2, 512), np.float32)
    for i in range(8):
        r = R[i]
        y_sample[i] = r["ys"]
        y_prompt[2 * i:2 * i + 2] = r["yp"].reshape(2, TP, D)
        so = r["str_o"].reshape(2, 2, 64, 8, 64)
        st_r[2 * i:2 * i + 2, 0] = so.transpose(0, 1, 3, 4, 2)
        sl = r["stl_o"].reshape(128, 4, 2, 2)
        st_l[2 * i:2 * i + 2, 0] = sl.transpose(2, 3, 1, 0).reshape(2, 2, 512)
    if _debug:
        return (y_prompt, y_sample, st_r, st_l), R
    return (y_prompt, y_sample, st_r, st_l)
```

```python
import contextlib
import numpy as np
import concourse.bass as bass
import concourse.mybir as mybir
from concourse.bass_utils import run_bass_kernel_spmd

F32 = mybir.dt.float32
BF16 = mybir.dt.bfloat16
F32R = mybir.dt.float32r
AF = mybir.ActivationFunctionType
ALU = mybir.AluOpType

D = 1024
TS = 2048
TP = 256
NTOK = TS + 2 * TP
NCH = NTOK // 64
DIN = 2944
DFF = 4096
LAM = float(np.exp(-0.5))
EPS = 1e-6
LNX_EPS = 64e-5
TT = 256
TC = 512
GELU_C = 1.5957691216057308

P_GPRE, P_GPOST, P_GPRE2, P_GPOST2 = 0, 8, 16, 24
P_BMOD = 32
P_W0, P_A0 = 80, 88
P_KK, P_KA, P_RK, P_LNG, P_LNB = 96, 100, 104, 108, 112
P_CW, P_CB = 116, 132
P_BA, P_BX, P_LAM, P_H0 = 136, 144, 152, 160
NPRM = 168
C_ID, C_OB, C_MSI, C_ML, C_IDS, C_RMF, C_RMB = 0, 128, 256, 384, 448, 512, 768
C_MSI1, C_ML1 = 1024, 1152
NCST = 1216


class Buf:
    __slots__ = ("name", "lw", "rd", "excl", "multi", "ws")

    def __init__(self, name=""):
        self.name = name
        self.lw = None
        self.rd = {}
        self.excl = False
        self.multi = False
        self.ws = {}


class TL:
    def __init__(self, t, name=""):
        self.t = t
        self.b = Buf(name)

    def __getitem__(self, k):
        return self.t[k]


class Sched:
    ENGS = ("pe", "act", "dve", "pool", "sp")

    def __init__(self, nc):
        self.nc = nc
        self.streams = {e: [] for e in self.ENGS}
        self.cnt = {}
        self.waited = {e: {} for e in self.ENGS}
        self.n_ops = 0
        self.dma_n = {e: 0 for e in self.ENGS}
        self.NSLOT = {"sp": 44, "act": 44, "pool": 4, "dve": 2, "pe": 2}

    def _deps(self, eng, reads, writes):
        need = {}
        for b in reads:
            if b.multi:
                for s, v in b.ws.items():
                    if need.get(s, 0) < v:
                        need[s] = v
                continue
            if b.lw is not None:
                s, v = b.lw
                if need.get(s, 0) < v:
                    need[s] = v
            if b.excl:
                for s, v in b.rd.items():
                    if s != eng and need.get(s, 0) < v:
                        need[s] = v
        for b in writes:
            if b.multi:
                continue
            if b.lw is not None:
                s, v = b.lw
                if need.get(s, 0) < v:
                    need[s] = v
            for s, v in b.rd.items():
                if need.get(s, 0) < v:
                    need[s] = v
        out = []
        w = self.waited[eng]
        for s, v in need.items():
            if s == "pe" and eng == "pe":
                continue
            if w.get(s, 0) >= v:
                continue
            w[s] = v
            out.append((s, v))
        return out

    def op(self, eng, fn, reads=(), writes=(), dma=False):
        reads = [r.b if isinstance(r, TL) else r for r in reads]
        writes = [r.b if isinstance(r, TL) else r for r in writes]
        waits = self._deps(eng, reads, writes)
        if dma:
            slot = self.dma_n[eng] % self.NSLOT[eng]
            self.dma_n[eng] += 1
            sem = "%s_d%d" % (eng, slot)
            prev = self.cnt.get(sem, 0)
            if prev > 0 and self.waited[eng].get(sem, 0) < prev:
                self.waited[eng][sem] = prev
                waits.append((sem, prev))
        else:
            sem = eng
        inc = 16 if dma else 1
        self.cnt[sem] = self.cnt.get(sem, 0) + inc
        val = self.cnt[sem]
        self.streams[eng].append((waits, fn, sem, inc))
        self.n_ops += 1
        for b in reads:
            if b.rd.get(sem, 0) < val:
                b.rd[sem] = val
        for b in writes:
            if b.multi:
                if b.ws.get(sem, 0) < val:
                    b.ws[sem] = val
                continue
            b.lw = (sem, val)
            b.rd = {}
        return val

    def barrier_on(self, tl):
        if tl.b.lw is None:
            return
        sname, v = tl.b.lw
        for e in ("sp", "act", "pool"):
            if self.waited[e].get(sname, 0) < v:
                self.waited[e][sname] = v
                self.streams[e].append(([(sname, v)], None, None, 0))

    def barrier(self):
        snap = dict(self.cnt)
        for e in self.ENGS:
            waits = []
            for s, v in snap.items():
                if s == "pe" and e == "pe":
                    continue
                if self.waited[e].get(s, 0) < v:
                    self.waited[e][s] = v
                    waits.append((s, v))
            if waits:
                self.streams[e].append((waits, None, None, 0))

    def emit(self):
        nc = self.nc
        sems = {}
        with contextlib.ExitStack() as st:
            for s in self.cnt:
                sems[s] = st.enter_context(nc.semaphore(s))
            block = st.enter_context(nc.Block())
            engmap = {"pe": block.tensor, "act": block.scalar, "dve": block.vector,
                      "pool": block.gpsimd, "sp": block.sync}
            for e in self.ENGS:
                stream = self.streams[e]
                if not stream:
                    continue

                def body(eng, stream=stream):
                    for waits, fn, sem, inc in stream:
                        for s, v in waits:
                            eng.wait_ge(sems[s], v)
                        if fn is not None:
                            fn(eng).then_inc(sems[sem], inc)
                engmap[e](body)


class K:
    def __init__(self, debug=False, stop_after=None):
        self.debug = debug
        self.stop_after = stop_after
        import os
        self.cutk = int(os.environ.get("KCUT", "0"))
        self.cutm = int(os.environ.get("KCUTM", "99"))
        self.ktiles = int(os.environ.get("KTILES", "99"))
        self.kskip = os.environ.get("KSKIP", "").split(",")
        self.nc = bass.Bass("TRN2", target_bir_lowering=False)
        self.S = Sched(self.nc)
        self.es = contextlib.ExitStack()
        self.psr = 0
        self.rr = {}

    def dram(self, name, shape, dt, kind="Internal"):
        t = TL(self.nc.dram_tensor(name, list(shape), dt, kind=kind).ap(), name)
        t.b.multi = True
        return t

    def sb(self, st, name, shape, dt):
        return TL(st.enter_context(self.nc.sbuf_tensor(name, list(shape), dt)), name)

    def sb2(self, st, name, shape, dt):
        t = st.enter_context(self.nc.sbuf_tensor(name, list(shape), dt))
        return [TL(t, name + "_lo"), TL(t, name + "_hi")]

    def ps(self):
        p = self.psum[self.psr % 8]
        self.psr += 1
        return p

    def mm(self, out, lhsT, rhs, start, stop, R, W):
        self.S.op("pe", lambda e: e.matmul(out, lhsT=lhsT, rhs=rhs, start=start, stop=stop), R, W)

    def tr(self, out, in_, ident, R, W):
        self.S.op("pe", lambda e: e.transpose(out, in_, ident), R, W)

    def act(self, out, in_, func, R, W, scale=1.0, bias=None, eng="act"):
        if bias is None:
            self.S.op("act", lambda e: e.activation(out=out, in_=in_, func=func, scale=scale), R, W)
        else:
            self.S.op("act", lambda e: e.activation(out=out, in_=in_, func=func, scale=scale, bias=bias), R, W)

    def tt(self, eng, out, in0, in1, op, R, W):
        self.S.op(eng, lambda e: e.tensor_tensor(out=out, in0=in0, in1=in1, op=op), R, W)

    def tsc(self, eng, out, in0, s1, op0, R, W, s2=None, op1=None):
        if op1 is None:
            self.S.op(eng, lambda e: e.tensor_scalar(out=out, in0=in0, scalar1=s1, scalar2=None, op0=op0), R, W)
        else:
            self.S.op(eng, lambda e: e.tensor_scalar(out=out, in0=in0, scalar1=s1, scalar2=s2, op0=op0, op1=op1), R, W)

    def stt(self, out, in0, scalar, in1, op0, op1, R, W):
        self.S.op("dve", lambda e: e.scalar_tensor_tensor(out=out, in0=in0, scalar=scalar, in1=in1, op0=op0, op1=op1), R, W)

    def cp(self, eng, out, in_, R, W):
        if eng == "act":
            self.S.op("act", lambda e: e.activation(out=out, in_=in_, func=AF.Copy), R, W)
        else:
            self.S.op(eng, lambda e: e.tensor_copy(out=out, in_=in_), R, W)

    def scan(self, out, d0, d1, init, R, W):
        self.S.op("dve", lambda e: e.tensor_tensor_scan(out=out, data0=d0, data1=d1, initial=init,
                                                        op0=ALU.mult, op1=ALU.add), R, W)

    def dma(self, out, in_, R, W, eng="sp"):
        self.S.op(eng, lambda e: e.dma_start(out=out, in_=in_), R, W, dma=True)

    def memset(self, eng, ap, val, W):
        self.S.op(eng, lambda e: e.memset(ap, val), (), W)

    def pick(self, key, engs):
        i = self.rr.get(key, 0)
        self.rr[key] = i + 1
        return engs[i % len(engs)]

    def build(self):
        nc = self.nc
        I = lambda n, s, dt=F32: self.dram(n, s, dt, "ExternalInput")
        O = lambda n, s, dt=F32: self.dram(n, s, dt, "ExternalOutput")
        self.xs = I("xs", [TS, D])
        self.xp = I("xp", [2 * TP, D])
        self.pe = I("pe", [TS, D])
        self.cT = I("cT", [128, 16])
        self.h0r = I("h0r", [128, 512])
        self.prm = I("prm", [128, NPRM])
        self.cst = I("cst", [128, NCST])
        self.w_mod = I("w_mod", [D, 6 * D])
        self.w_in = I("w_in", [D, DIN])
        self.w_out = I("w_out", [D, D])
        self.w1 = I("w1", [D, DFF])
        self.w2 = I("w2", [DFF, D])
        self.wup = I("wup", [128, 512])
        self.aup = I("aup", [128, 512])
        self.gup = I("gup", [128, 512])
        self.lwa = I("lwa", [2, 8, 64, 64])
        self.lwx = I("lwx", [2, 8, 64, 64])
        self.ys = O("ys", [TS, D])
        self.yp = O("yp", [2 * TP, D])
        self.str_o = O("str_o", [2, 128, 512])
        self.stl_o = O("stl_o", [128, 16])
        self.xT_scr = self.dram("xT_scr", [128, 8, NTOK], F32)
        self.xb_scr = self.dram("xb_scr", [128, 4, NTOK], F32)
        self.gate_scr = self.dram("gate_scr", [128, 4, NTOK], BF16)
        self.g_scr = self.dram("g_scr", [128, 4, NTOK], BF16)
        self.bon_scr = self.dram("bon_scr", [128, 4, NTOK], BF16)
        self.y_scr = self.dram("y_scr", [128, 8, NTOK], BF16)
        self.ytok_scr = self.dram("ytok_scr", [2, NTOK, 512], F32)
        for n in ("art", "rrt", "ttt", "akt", "mrbt", "mrkt", "bh", "kh"):
            setattr(self, n + "_scr", self.dram(n + "_scr", [NCH, 128, 512], BF16))
        self.vt_scr = self.dram("vt_scr", [NCH, 64, 512], BF16)
        self.pend_scr = self.dram("pend_scr", [NCH, 128, 512], F32)
        self.w1_scr = self.dram("w1_scr", [8, 128, 8, 512], BF16)
        self.w2_scr = self.dram("w2_scr", [8, 128, 32, 128], BF16)
        if self.debug:
            self.dbg = {}

        with self.es as st0:
            self.psum = [TL(st0.enter_context(nc.psum_tensor("ps%d" % i, [128, 512], F32)), "ps%d" % i)
                         for i in range(8)]
            for p_ in self.psum:
                p_.b.excl = True
            self.prm_t = self.sb(st0, "prm_t", [128, NPRM], F32)
            self.cst_t = self.sb(st0, "cst_t", [128, NCST], F32)
            self.modT = self.sb(st0, "modT", [128, 48, 2], F32)
            self.gs1 = self.sb(st0, "gs1", [128, 8, 2], F32)
            self.gs2 = self.sb(st0, "gs2", [128, 8, 2], F32)
            self.gg1 = self.sb(st0, "gg1", [128, 8, 2], F32)
            self.gg2 = self.sb(st0, "gg2", [128, 8, 2], F32)
            self.ident = self.sb(st0, "ident", [128, 128], F32)
            self.ones_bf = self.sb(st0, "ones_bf", [128, 128], BF16)
            self.oblk_bf = self.sb(st0, "oblk_bf", [128, 128], BF16)
            self.epsT = self.sb(st0, "epsT", [128, 2], F32)
            self.misc = self.sb(st0, "misc", [128, 32], F32)
            self.stl_t = self.sb(st0, "stl_t", [128, 16], F32)
            stop = False
            with contextlib.ExitStack() as stA:
                self.win = self.sb(stA, "win", [128, 8, DIN], BF16)
                self.win.b.multi = True
                self.wup_t = self.sb(stA, "wup_t", [128, 512], BF16)
                self.aup_t = self.sb(stA, "aup_t", [128, 512], BF16)
                self.gup_t = self.sb(stA, "gup_t", [128, 512], BF16)
                for nm, fn in (("p0", self.phase0), ("pA", self.phaseA)):
                    fn()
                    self.S.barrier()
                    if self.stop_after == nm:
                        stop = True
                        break
            if not stop:
                for nm, fn in (("pB", self.phaseB), ("pC", self.phaseC)):
                    fn()
                    self.S.barrier()
                    if self.stop_after == nm:
                        break
            self.S.emit()
        return nc

    def dump(self, name, src_ap, shape, dt, R):
        o = self.dram("dbg_" + name, shape, dt, "ExternalOutput")
        self.dma(o[:], src_ap, R, [o])

    def phase0(self):
        nc = self.nc
        prm, cst = self.prm_t, self.cst_t
        self.dma(prm[:], self.prm[:], [], [prm])
        self.dma(cst[:], self.cst[:], [], [cst])
        self.cp("dve", self.ident[:], cst[:, C_ID:C_ID + 128], [cst], [self.ident])
        self.cp("dve", self.oblk_bf[:], cst[:, C_OB:C_OB + 128], [cst], [self.oblk_bf])
        self.memset("dve", self.ones_bf[:], 1.0, [self.ones_bf])
        self.memset("dve", self.epsT[:, 0:1], EPS, [self.epsT])
        self.memset("dve", self.epsT[:, 1:2], LNX_EPS, [self.epsT])
        self.tsc("dve", self.misc[:, 0:4], prm[:, P_KA:P_KA + 4], -1.0, ALU.mult, [prm], [self.misc], 1.0, ALU.add)
        with contextlib.ExitStack() as st:
            scT = self.sb(st, "scT", [128, 16], F32)
            cT = self.sb(st, "cT_t", [128, 16], F32)
            wm = [self.sb(st, "wm%d" % i, [128, 8, 512], F32) for i in range(2)]
            tmp = self.sb(st, "lam_tmp", [128, 8], F32)
            self.dma(cT[:], self.cT[:], [], [cT])
            self.act(scT[:], cT[:], AF.Silu, [cT], [scT])
            self.act(tmp[:], prm[:, P_LAM:P_LAM + 8], AF.Exp, [prm], [tmp], scale=-1.0)
            self.act(tmp[:], tmp[:], AF.Ln, [tmp], [tmp], bias=1.0)
            self.tsc("dve", self.misc[:, 4:12], tmp[:], -8.0, ALU.mult, [tmp], [self.misc])
            self.tsc("dve", self.misc[:, 12:20], tmp[:], -16.0, ALU.mult, [tmp], [self.misc])
            wsrc = self.w_mod[:].rearrange("(kc p) n -> p kc n", p=128)
            scb = self.sb(st, "scb", [128, 16], BF16)
            self.cp("dve", scb[:], scT[:], [scT], [scb])
            sc3 = scb[:].rearrange("p (k c) -> p k c", c=2)
            wmb = [self.sb(st, "wmb%d" % i, [128, 8, 512], BF16) for i in range(2)]
            stgA = [self.sb(st, "stgA%d" % i, [128, 8, 256], F32) for i in range(2)]
            wisrc = self.w_in[:].rearrange("(kc p) n -> p kc n", p=128)
            wi_blocks = [(c0, min(256, DIN - c0)) for c0 in range(0, DIN, 256)]
            sm_list = [(self.wup, self.wup_t), (self.aup, self.aup_t), (self.gup, self.gup_t)]
            nb = [0]

            def win_step():
                if wi_blocks:
                    c0, cw = wi_blocks.pop(0)
                    s_ = stgA[nb[0] % 2]
                    self.dma(s_[:, :, 0:cw], wisrc[:, :, c0:c0 + cw], [], [s_], eng="act")
                    self.cp("act" if nb[0] % 2 == 0 else "pool", self.win[:, :, c0:c0 + cw], s_[:, :, 0:cw], [s_], [self.win])
                    nb[0] += 1
                elif sm_list:
                    src, dstt = sm_list.pop(0)
                    s_ = stgA[nb[0] % 2]
                    nb[0] += 1
                    s2 = s_[:].rearrange("p a b -> p (a b)")[:, 0:512]
                    self.dma(s2, src[:], [], [s_], eng="act")
                    self.cp("pool", dstt[:], s2, [s_], [dstt])

            for blk in range(12):
                w = wm[blk % 2]
                wb_ = wmb[blk % 2]
                self.dma(w[:], wsrc[:, :, blk * 512:(blk + 1) * 512], [], [w], eng="sp")
                self.cp("dve", wb_[:], w[:], [w], [wb_])
                win_step()
                p = self.ps()
                for m in range(4):
                    for kc in range(8):
                        self.mm(p[:, 2 * m:2 * m + 2], wb_[:, kc, m * 128:(m + 1) * 128], sc3[:, kc, :],
                                kc == 0, kc == 7, [wb_, scb], [p])
                for m in range(4):
                    mi = blk * 4 + m
                    self.tsc("dve", self.modT[:, mi, :], p[:, 2 * m:2 * m + 2], prm[:, P_BMOD + mi:P_BMOD + mi + 1],
                             ALU.add, [p, prm], [self.modT])
            while wi_blocks or sm_list:
                win_step()
            m3 = self.modT
            for (dst, sc_off, g_off, one) in ((self.gs1, 8, P_GPRE, 1.0), (self.gs2, 32, P_GPRE2, 1.0),
                                              (self.gg1, 16, P_GPOST, 0.0), (self.gg2, 40, P_GPOST2, 0.0)):
                for c in range(2):
                    self.tsc("dve", dst[:, :, c], m3[:, sc_off:sc_off + 8, c], one, ALU.add, [m3], [dst])
                    self.tt("dve", dst[:, :, c], dst[:, :, c], prm[:, g_off:g_off + 8], ALU.mult, [dst, prm], [dst])
            if self.debug:
                self.dump("modT", self.modT[:], [128, 48, 2], F32, [self.modT])
                self.dump("gs1", self.gs1[:], [128, 8, 2], F32, [self.gs1])

    def load_cast(self, st, dst_ap, dst_tl, src_ap, shape, tag):
        key = "stg_" + tag
        if not hasattr(self, key):
            setattr(self, key, [self.sb(st, "%s%d" % (key, i), shape, F32) for i in range(2)])
        ring = getattr(self, key)
        s = ring[self.rr.get(key, 0) % 2]
        self.rr[key] = self.rr.get(key, 0) + 1
        self.dma(s[:], src_ap, [], [s], eng="sp")
        eng = self.pick("castE", ["act", "pool"])
        self.cp(eng, dst_ap, s[:], [s], [dst_tl])

    def phaseA(self):
        nc = self.nc
        prm, cst = self.prm_t, self.cst_t
        with contextlib.ExitStack() as st:
            win, wup, aup, gup = self.win, self.wup_t, self.aup_t, self.gup_t
            mSI = cst[:, C_MSI:C_MSI + 128].rearrange("p (q t) -> p q t", q=2)
            mL = cst[:, C_ML:C_ML + 64]
            idS = cst[:, C_IDS:C_IDS + 64]

            xin = self.sb(st, "xin", [128, 2, D], F32)
            xT = self.sb(st, "xT", [128, 8, TT], F32)
            sq = self.sb(st, "sq", [128, 8, TT], BF16)
            hT = self.sb(st, "hT", [128, 8, TT], BF16)
            rstd = self.sb(st, "rstd", [128, TT], F32)
            tmpA = [self.sb(st, "tmpA%d" % i, [128, TT], F32) for i in range(2)]
            rT = self.sb(st, "rT", [128, 4, TT], F32)
            kT = self.sb(st, "kT", [128, 4, TT], F32)
            vT = self.sb(st, "vT", [128, 4, TT], F32)
            xw = self.sb(st, "xw", [128, TT], BF16)
            xa = self.sb(st, "xa", [128, TT], BF16)
            xg = self.sb(st, "xg", [128, TT], BF16)
            xbT = self.sb(st, "xbT", [128, 4, TT], F32)
            gtmp = [self.sb(st, "gtmp%d" % i, [128, TT], F32) for i in range(3)]
            gate = self.sb(st, "gate", [128, 4, TT], BF16)
            gT = self.sb(st, "gT", [128, 4, TT], BF16)
            kkn = self.sb(st, "kkn", [128, 4, TT], F32)
            ksum = self.sb(st, "ksum", [128, 4, TT], F32)
            bon = self.sb(st, "bon", [128, 4, TT], BF16)
            sg = self.sb(st, "sg", [128, 4, TT], F32)
            cs = self.sb(st, "cs", [128, 4, TT], F32)
            E1 = self.sb(st, "E1", [128, 4, TT], F32)
            ad = self.sb(st, "ad", [128, 4, TT], F32)
            wk1 = self.sb(st, "wk1", [128, 4, TT], F32)
            wk2 = self.sb(st, "wk2", [128, 4, TT], F32)
            NC4 = TT // 64
            AR = [self.sb(st, "AR%d" % d, [128, 4, NC4, 2, 64], BF16) for d in range(2)]
            BK = [self.sb(st, "BK%d" % d, [128, 4, NC4, 2, 64], BF16) for d in range(2)]
            PEb1 = self.sb(st, "PEb", [128, 4, NC4, 64], F32)
            PEb = [PEb1, PEb1]
            tokB1 = self.sb(st, "tokB", [128, 2, 8, 64], BF16)
            tokK1 = self.sb(st, "tokK", [128, 2, 8, 64], BF16)
            tokB, tokK = [tokB1, tokB1], [tokK1, tokK1]
            tokV = self.sb(st, "tokV", [128, 2, 8, 64], BF16)
            NSET = 2
            MRBs = [self.sb(st, "MRBs%d" % i, [64, 8, 64], BF16) for i in range(NSET)]
            Lt0s = [self.sb(st, "Lt0_%d" % i, [64, 8, 64], F32R) for i in range(NSET)]
            Tfins = [self.sb(st, "Tfin%d" % i, [64, 8, 64], BF16) for i in range(NSET)]
            SCk = self.sb(st, "SCk", [128, 2, 8, 64], BF16)
            Lms = [[self.sb(st, "Lm%d_%d" % (i, k), [64, 8, 64], F32R) for i in range(2)] for k in range(NSET)]
            Ltms = [[self.sb(st, "Ltm%d_%d" % (i, k), [64, 8, 64], F32R) for i in range(2)] for k in range(NSET)]
            ILms = [self.sb(st, "ILm_%d" % k, [64, 8, 64], F32R) for k in range(NSET)]
            Ttms = [[self.sb(st, "Ttm%d_%d" % (i, k), [64, 8, 64], F32R) for i in range(2)] for k in range(NSET)]

            ones_bf, oblk, ident = self.ones_bf, self.oblk_bf, self.ident
            tiles = [(0, t0, True, 0) for t0 in range(0, TS, TT)] + [(1, TS, False, 1), (2, TS + TP, False, 1)]
            mS1 = cst[0:64, C_MSI1:C_MSI1 + 128].rearrange("p (q t) -> p q t", q=2)
            mL1 = cst[0:64, C_ML1:C_ML1 + 64]
            id64 = idS[0:64]
            loaded = set()

            def load_x(g0, is_s):
                if g0 in loaded:
                    return
                loaded.add(g0)
                if is_s:
                    src = self.xs[g0:g0 + TT, :].rearrange("(s p) f -> p s f", p=128)
                else:
                    l0 = g0 - TS
                    src = self.xp[l0:l0 + TT, :].rearrange("(s p) f -> p s f", p=128)
                self.dma(xin[:], src, [], [xin])
                if is_s:
                    petv = xT[:].rearrange("p a b -> p (a b)").rearrange("p (s f) -> p s f", s=2)
                    self.dma(petv, self.pe[g0:g0 + TT, :].rearrange("(s p) f -> p s f", p=128), [], [xT], eng="act")

            def front(seq, g0, is_s, mc, nxt=None):
                load_x(g0, is_s)
                if is_s:
                    petv = xT[:].rearrange("p a b -> p (a b)").rearrange("p (s f) -> p s f", s=2)
                    self.tt("pool", xin[:], xin[:], petv, ALU.add, [xin, xT], [xin])
                for j in range(8):
                    p = self.ps()
                    for s in range(2):
                        self.tr(p[:, s * 128:(s + 1) * 128], xin[:, s, j * 128:(j + 1) * 128], ident[:], [xin, ident], [p])
                    self.cp("act", xT[:, j, :], p[:, 0:TT], [p], [xT])
                    self.tt("pool", sq[:, j, :], xT[:, j, :], xT[:, j, :], ALU.mult, [xT], [sq])
                    yield
                self.dma(self.xT_scr[:, :, g0:g0 + TT], xT[:], [xT], [self.xT_scr])
                p = self.ps()
                for j in range(8):
                    self.mm(p[:, 0:TT], ones_bf[:], sq[:, j, :], j == 0, j == 7, [ones_bf, sq], [p])
                self.act(rstd[:], p[:, 0:TT], AF.Sqrt, [p, self.epsT], [rstd], scale=1.0 / D, bias=self.epsT[:, 0:1])
                self.S.op("dve", lambda e: e.reciprocal(out=rstd[:], in_=rstd[:]), [rstd.b], [rstd.b])
                for j in range(8):
                    t = tmpA[j % 2]
                    self.tt("dve", t[:], xT[:, j, :], rstd[:], ALU.mult, [xT, rstd], [t])
                    self.act(hT[:, j, :], t[:], AF.Identity, [t, self.gs1, self.modT], [hT],
                             scale=self.gs1[:, j, mc:mc + 1], bias=self.modT[:, j, mc:mc + 1])
                    yield
                for m in range(23):
                    if m >= self.cutm:
                        break
                    p = self.ps()
                    for kc in range(8):
                        self.mm(p[:, 0:TT], win[:, kc, m * 128:(m + 1) * 128], hT[:, kc, :], kc == 0, kc == 7, [win, hT], [p])
                    pz = p[:, 0:TT]
                    if m < 4:
                        self.cp("act", rT[:, m, :], pz, [p], [rT])
                    elif m < 8:
                        self.cp("act", kT[:, m - 4, :], pz, [p], [kT])
                    elif m < 12:
                        self.cp("act", vT[:, m - 8, :], pz, [p], [vT])
                    elif m == 12:
                        self.act(xw[:], pz, AF.Tanh, [p], [xw])
                    elif m == 13:
                        self.cp("act", xa[:], pz, [p], [xa])
                    elif m == 14:
                        self.act(xg[:], pz, AF.Sigmoid, [p], [xg])
                    elif m < 19:
                        self.cp("act", xbT[:, m - 15, :], pz, [p], [xbT])
                    else:
                        j = m - 19
                        g0_, g1_, g2_ = gtmp
                        self.cp("act", g0_[:], pz, [p], [g0_])
                        self.tt("pool", g1_[:], g0_[:], g0_[:], ALU.mult, [g0_], [g1_])
                        self.tsc("dve", g1_[:], g1_[:], 0.044715, ALU.mult, [g1_], [g1_], 1.0, ALU.add)
                        self.tt("dve", g1_[:], g1_[:], g0_[:], ALU.mult, [g1_, g0_], [g1_])
                        self.act(g2_[:], g1_[:], AF.Sigmoid, [g1_], [g2_], scale=GELU_C)
                        self.tt("pool", gate[:, j, :], g0_[:], g2_[:], ALU.mult, [g0_, g2_], [gate])
                    yield
                self.dma(self.xb_scr[:, :, g0:g0 + TT], xbT[:], [xbT], [self.xb_scr])
                if self.debug and g0 == 0:
                    self.dump("hT", hT[:], [128, 8, TT], BF16, [hT])
                    self.dump("rT", rT[:], [128, 4, TT], F32, [rT])
                    self.dump("vT", vT[:], [128, 4, TT], F32, [vT])
                    self.dump("xbT", xbT[:], [128, 4, TT], F32, [xbT])
                    self.dump("gate", gate[:], [128, 4, TT], BF16, [gate])
                self.dma(self.gate_scr[:, :, g0:g0 + TT], gate[:], [gate], [self.gate_scr])
                for j in range(4):
                    p = self.ps()
                    self.mm(p[:, 0:TT], gup[:, j * 128:(j + 1) * 128], xg[:], True, True, [gup, xg], [p])
                    self.cp("act", gT[:, j, :], p[:, 0:TT], [p], [gT])
                    yield
                self.dma(self.g_scr[:, :, g0:g0 + TT], gT[:], [gT], [self.g_scr])
                for j in range(4):
                    self.tsc("dve", kkn[:, j, :], kT[:, j, :], prm[:, P_KK + j:P_KK + j + 1], ALU.mult, [kT, prm], [kkn])
                    self.tt("pool", sq[:, j, :], kkn[:, j, :], kkn[:, j, :], ALU.mult, [kkn], [sq])
                for j in range(4):
                    p = self.ps()
                    self.mm(p[:, 0:TT], oblk[:], sq[:, j, :], True, True, [oblk, sq], [p])
                    t = tmpA[j % 2]
                    self.act(t[:], p[:, 0:TT], AF.Sqrt, [p], [t])
                    self.tsc("dve", t[:], t[:], 1e-12, ALU.max, [t], [t])
                    self.S.op("dve", lambda e, t=t: e.reciprocal(out=t[:], in_=t[:]), [t.b], [t.b])
                    self.tt("dve", kkn[:, j, :], kkn[:, j, :], t[:], ALU.mult, [kkn, t], [kkn])
                    yield
                for s in range(2):
                    p = self.ps()
                    for j in range(4):
                        self.tr(p[:, j * 128:(j + 1) * 128], vT[:, j, s * 128:(s + 1) * 128], ident[:], [vT, ident], [p])
                    self.cp("act", tokV[:, s, :, :].rearrange("p h k -> p (h k)"), p[:], [p], [tokV])
                c0 = g0 // 64
                for s in range(2):
                    dst = self.vt_scr[c0 + 2 * s:c0 + 2 * s + 2, :, :].rearrange("c s f -> (c s) f")
                    self.dma(dst, tokV[:, s, :, :].rearrange("p h k -> p (h k)"), [tokV], [self.vt_scr])
                if nxt is not None:
                    load_x(nxt[1], nxt[2])
                yield
            def prep(g0, d):
                c0 = g0 // 64
                for j in range(4):
                    p = self.ps()
                    self.mm(p[:, 0:TT], wup[d * 64:(d + 1) * 64, j * 128:(j + 1) * 128], xw[d * 64:(d + 1) * 64, :],
                            True, True, [wup, xw], [p])
                    self.act(sg[:, j, :], p[:, 0:TT], AF.Sigmoid, [p, prm], [sg],
                             bias=prm[:, P_W0 + 4 * d + j:P_W0 + 4 * d + j + 1])
                    if d == 0:
                        self.scan(cs[:, j, :], cst[:, C_RMF:C_RMF + TT], sg[:, j, :], 0.0, [cst, sg], [cs])
                    else:
                        self.scan(cs[:, j, ::-1], cst[:, C_RMB:C_RMB + TT][:, ::-1], sg[:, j, ::-1], 0.0, [cst, sg], [cs])
                    yield
                for j in range(4):
                    p = self.ps()
                    self.mm(p[:, 0:TT], aup[d * 64:(d + 1) * 64, j * 128:(j + 1) * 128], xa[d * 64:(d + 1) * 64, :],
                            True, True, [aup, xa], [p])
                    self.act(ad[:, j, :], p[:, 0:TT], AF.Sigmoid, [p, prm], [ad],
                             bias=prm[:, P_A0 + 4 * d + j:P_A0 + 4 * d + j + 1])
                    yield
                self.tt("dve", sg[:], cs[:], sg[:], ALU.subtract, [cs, sg], [sg])
                self.act(E1[:], cs[:], AF.Exp, [cs], [E1], scale=-LAM)
                self.act(cs[:], cs[:], AF.Exp, [cs], [cs], scale=LAM)
                self.act(sg[:], sg[:], AF.Exp, [sg], [sg], scale=-LAM)
                E2, E3 = cs, sg
                ar5 = AR[d]
                bk5 = BK[d]
                v4 = lambda tl: tl[:].rearrange("p j (c t) -> p j c t", t=64)
                self.stt(ar5[:, :, :, 0, :], v4(kkn), -1.0, v4(E3), ALU.mult, ALU.mult, [kkn, E3], [ar5])
                self.tt("pool", ar5[:, :, :, 1, :], v4(rT), v4(E1), ALU.mult, [rT, E1], [ar5])
                yield
                self.tt("dve", wk1[:], kkn[:], ad[:], ALU.mult, [kkn, ad], [wk1])
                self.tt("dve", wk1[:], wk1[:], E2[:], ALU.mult, [wk1, E2], [wk1])
                self.cp("act", bk5[:, :, :, 0, :], v4(wk1), [wk1], [bk5])
                yield
                for j in range(4):
                    self.tsc("dve", wk2[:, j, :], ad[:, j, :], prm[:, P_KA + j:P_KA + j + 1], ALU.mult, [ad, prm, self.misc], [wk2],
                             self.misc[:, j:j + 1], ALU.add)
                self.tt("dve", wk2[:], wk2[:], kT[:], ALU.mult, [wk2, kT], [wk2])
                if d == 0:
                    self.cp("pool", ksum[:], wk2[:], [wk2], [ksum])
                else:
                    self.tt("pool", ksum[:], ksum[:], wk2[:], ALU.add, [ksum, wk2], [ksum])
                self.tt("dve", wk2[:], wk2[:], E2[:], ALU.mult, [wk2, E2], [wk2])
                self.cp("act", bk5[:, :, :, 1, :], v4(wk2), [wk2], [bk5])
                yield
                te = 63 if d == 0 else 0
                pend_b = v4(E1)[:, :, :, te:te + 1].to_broadcast([128, 4, NC4, 64])
                self.cp("act", PEb[d][:], pend_b, [E1], [PEb[d]])
                self.tt("dve", v4(wk1), v4(wk1), PEb[d][:], ALU.mult, [wk1, PEb[d]], [wk1])
                self.tt("dve", v4(wk2), v4(wk2), PEb[d][:], ALU.mult, [wk2, PEb[d]], [wk2])
                yield
                for (srcw, tokX, scr) in ((wk1, tokB[d], self.bh_scr), (wk2, tokK[d], self.kh_scr)):
                    for s in range(2):
                        p = self.ps()
                        for j in range(4):
                            self.tr(p[:, j * 128:(j + 1) * 128], srcw[:, j, s * 128:(s + 1) * 128], ident[:], [srcw, ident], [p])
                        self.cp("act", tokX[:, s, :, :].rearrange("p h k -> p (h k)"), p[:], [p], [tokX])
                        for cc in range(2):
                            self.dma(scr[c0 + 2 * s + cc, d * 64:(d + 1) * 64, :],
                                     tokX[cc * 64:(cc + 1) * 64, s, :, :].rearrange("p h k -> p (h k)"),
                                     [tokX], [scr])
                        yield
                for cl in range(NC4):
                    c = c0 + cl
                    for hp in range(2):
                        for (q, scr) in ((0, self.art_scr), (1, self.rrt_scr)):
                            dst = scr[c, d * 64:(d + 1) * 64, :].rearrange("k (j hp t) -> k j hp t", hp=2, t=64)[:, :, hp, :]
                            self.dma(dst, ar5[hp * 64:(hp + 1) * 64, :, cl, q, :], [ar5], [scr], eng="sp")
                        dst = self.pend_scr[c, d * 64:(d + 1) * 64, :].rearrange("k (j hp t) -> k j hp t", hp=2, t=64)[:, :, hp, :]
                        self.dma(dst, PEb[d][hp * 64:(hp + 1) * 64, :, cl, :], [PEb[d]], [self.pend_scr], eng="sp")
                yield
            def bonus(g0):
                for j in range(4):
                    self.stt(sq[:, j, :], rT[:, j, :], prm[:, P_RK + j:P_RK + j + 1], ksum[:, j, :], ALU.mult, ALU.mult,
                             [rT, prm, ksum], [sq])
                    p = self.ps()
                    self.mm(p[:, 0:TT], oblk[:], sq[:, j, :], True, True, [oblk, sq], [p])
                    self.tt("dve", bon[:, j, :], p[:, 0:TT], vT[:, j, :], ALU.mult, [p, vT], [bon])
                self.dma(self.bon_scr[:, :, g0:g0 + TT], bon[:], [bon], [self.bon_scr])
                yield
            def chunk_sck(g0):
                c0 = g0 // 64
                for cl in range(NC4):
                    c = c0 + cl
                    for hp in range(2):
                        p = self.ps()
                        for j in range(4):
                            for d in range(2):
                                self.mm(p[d * 64:(d + 1) * 64, j * 128:(j + 1) * 128],
                                        BK[d][hp * 64:(hp + 1) * 64, j, cl, 1, :],
                                        AR[d][hp * 64:(hp + 1) * 64, j, cl, :, :].rearrange("p q t -> p (q t)"),
                                        True, True, [BK[d], AR[d]], [p])
                        self.tt("dve", SCk[:, :, hp::2, :].rearrange("p q h t -> p h q t"),
                                p[:].rearrange("p (h q t) -> p h q t", q=2, t=64),
                                mSI.unsqueeze(1).to_broadcast([128, 4, 2, 64]), ALU.mult, [p, cst], [SCk])
                    self.dma(self.akt_scr[c], SCk[:, 0, :, :].rearrange("p h s -> p (h s)"), [SCk], [self.akt_scr], eng="act")
                    self.dma(self.mrkt_scr[c], SCk[:, 1, :, :].rearrange("p h s -> p (h s)"), [SCk], [self.mrkt_scr], eng="act")
                    yield
            def chunk_d(g0, d, cls, k):
                c0 = g0 // 64
                Lm, Ltm, ILm, Ttm, Lt0, Tfin = Lms[k], Ltms[k], ILms[k], Ttms[k], Lt0s[k], Tfins[k]
                MRB = MRBs[k]
                for cl in cls:
                    c = c0 + cl
                    msk = mSI[0:64] if d == 0 else mS1
                    mskL = mL[0:64] if d == 0 else mL1
                    L0 = Lm[0]
                    for hp in range(2):
                        p = self.ps()
                        for j in range(4):
                            self.mm(p[0:64, j * 128:(j + 1) * 128],
                                    BK[d][hp * 64:(hp + 1) * 64, j, cl, 0, :],
                                    AR[d][hp * 64:(hp + 1) * 64, j, cl, :, :].rearrange("p q t -> p (q t)"),
                                    True, True, [BK[d], AR[d]], [p])
                        p4 = p[0:64, :].rearrange("p (h q t) -> p h q t", q=2, t=64)
                        self.tt("dve", Lt0[:, hp::2, :], p4[:, :, 0, :], msk[:, 0, :].unsqueeze(1).to_broadcast([64, 4, 64]),
                                ALU.mult, [p, cst], [Lt0])
                        self.tt("dve", MRB[:, hp::2, :], p4[:, :, 1, :], msk[:, 1, :].unsqueeze(1).to_broadcast([64, 4, 64]),
                                ALU.mult, [p, cst], [MRB])
                        p2 = self.ps()
                        for j in range(4):
                            self.mm(p2[0:64, j * 64:(j + 1) * 64],
                                    AR[d][hp * 64:(hp + 1) * 64, j, cl, 0, :], BK[d][hp * 64:(hp + 1) * 64, j, cl, 0, :],
                                    True, True, [AR[d], BK[d]], [p2])
                        self.tt("dve", L0[:, hp::2, :], p2[0:64, 0:256].rearrange("p (h s) -> p h s", s=64),
                                mskL.unsqueeze(1).to_broadcast([64, 4, 64]), ALU.mult, [p2, cst], [L0])
                    self.dma(self.mrbt_scr[c, d * 64:(d + 1) * 64, :], MRB[:].rearrange("p h s -> p (h s)"),
                             [MRB], [self.mrbt_scr], eng="act")
                    yield
                    T0 = Ttm[0]
                    self.tt("pool", T0[:], Lt0[:].bitcast(F32), id64.unsqueeze(1).to_broadcast([64, 8, 64]), ALU.add,
                            [Lt0, cst], [T0])
                    L_prev, Tt_prev, Lt_prev = L0, T0, Lt0
                    for lev in range(1, 6):
                        L_new, Lt_new, Tt_new = Lm[lev % 2], Ltm[lev % 2], Ttm[lev % 2]
                        pA = self.ps()
                        for h in range(8):
                            self.mm(pA[0:64, h * 64:(h + 1) * 64], Lt_prev[:, h, :], L_prev[:, h, :], True, True,
                                    [Lt_prev, L_prev], [pA])
                        if lev < 5:
                            pB = self.ps()
                            for h in range(8):
                                self.mm(pB[0:64, h * 64:(h + 1) * 64], L_prev[:, h, :], Lt_prev[:, h, :], True, True,
                                        [Lt_prev, L_prev], [pB])
                        self.tt("dve", ILm[:], pA[0:64, :].rearrange("p (h s) -> p h s", s=64),
                                id64.unsqueeze(1).to_broadcast([64, 8, 64]), ALU.add, [pA, cst], [ILm])
                        if lev < 5:
                            self.cp("dve", L_new[:].rearrange("p h s -> p (h s)"), pA[0:64, :], [pA], [L_new])
                            self.cp("act", Lt_new[:].rearrange("p h s -> p (h s)"), pB[0:64, :], [pB], [Lt_new])
                        yield
                        pC = self.ps()
                        for h in range(8):
                            self.mm(pC[0:64, h * 64:(h + 1) * 64], ILm[:, h, :], Tt_prev[:, h, :], True, True,
                                    [ILm, Tt_prev], [pC])
                        if lev < 5:
                            self.cp("act", Tt_new[:].rearrange("p h s -> p (h s)"), pC[0:64, :], [pC], [Tt_new])
                        else:
                            self.cp("act", Tfin[:].rearrange("p h s -> p (h s)"), pC[0:64, :], [pC], [Tfin])
                        L_prev, Tt_prev, Lt_prev = L_new, Tt_new, Lt_new
                        yield
                    self.dma(self.ttt_scr[c, d * 64:(d + 1) * 64, :], Tfin[:].rearrange("p h s -> p (h s)"),
                             [Tfin], [self.ttt_scr], eng="act")


            def run_all(*gens):
                gens = list(gens)
                while gens:
                    for g in list(gens):
                        try:
                            next(g)
                        except StopIteration:
                            gens.remove(g)

            def seq_(*gens):
                for g in gens:
                    yield from g

            prev = None
            tl_ = tiles[:self.ktiles]
            for ti_, (seq, g0, is_s, mc) in enumerate(tl_):
                nxt = tl_[ti_ + 1] if ti_ + 1 < len(tl_) else None
                if prev is None:
                    run_all(front(seq, g0, is_s, mc, nxt))
                else:
                    run_all(seq_(chunk_d(prev, 1, [0, 1], 0), chunk_sck(prev)), chunk_d(prev, 1, [2, 3], 1), front(seq, g0, is_s, mc, nxt))
                run_all(prep(g0, 0))
                run_all(chunk_d(g0, 0, [0, 1], 0), chunk_d(g0, 0, [2, 3], 1), prep(g0, 1))
                run_all(bonus(g0))
                prev = g0
            run_all(seq_(chunk_d(prev, 1, [0, 1], 0), chunk_sck(prev)), chunk_d(prev, 1, [2, 3], 1))

    def phaseB(self):
        with contextlib.ExitStack() as st:
            side = [self.gen_c0(st), self.phaseB_lru(st)]
            post = self.phaseB_post(st)
            next(post)
            done = np.zeros((2, NCH), bool)
            posted = [False] * (NTOK // 128)
            si = 0
            for info in self.phaseB_chain(st):
                for (d, c) in info:
                    done[d, c] = True
                for _ in range(2):
                    if side:
                        g = side[si % len(side)]
                        si += 1
                        try:
                            next(g)
                        except StopIteration:
                            side.remove(g)
                for b in range(NTOK // 128):
                    if not posted[b] and done[:, 2 * b:2 * b + 2].all():
                        posted[b] = True
                        post.send(b)
            for g in side:
                for _ in g:
                    pass
            for b in range(NTOK // 128):
                if not posted[b]:
                    post.send(b)
            if self.debug:
                self.S.barrier()
                self.dump("yscr", self.y_scr[:], [128, 8, NTOK], BF16, [self.y_scr])

    def gen_c0(self, st):
        stg = [self.sb(st, "stgC%d" % i, [128, 8, 512], F32) for i in range(2)]
        wb = [self.sb(st, "wbC%d" % i, [128, 8, 512], BF16) for i in range(2)]
        w1src = self.w1[:].rearrange("(kc p) n -> p kc n", p=128)
        for blk in range(8):
            s, o = stg[blk % 2], wb[blk % 2]
            self.dma(s[:], w1src[:, :, blk * 512:(blk + 1) * 512], [], [s])
            self.cp("pool", o[:], s[:], [s], [o])
            self.dma(self.w1_scr[blk], o[:], [o], [self.w1_scr], eng="act")
            yield
        w2src = self.w2[:].rearrange("(fc p) n -> p fc n", p=128)
        for m in range(8):
            s, o = stg[m % 2], wb[m % 2]
            s4 = s[:].rearrange("p k (a b) -> p (k a) b", b=128)
            o4 = o[:].rearrange("p k (a b) -> p (k a) b", b=128)
            self.dma(s4, w2src[:, :, m * 128:(m + 1) * 128], [], [s])
            self.cp("pool", o[:], s[:], [s], [o])
            self.dma(self.w2_scr[m], o4, [o], [self.w2_scr], eng="act")
            yield

    def phaseB_lru(self, st):
        prm, cst, misc = self.prm_t, self.cst_t, self.misc
        if True:
            wbd32 = self.sb(st, "wbd32", [128, 16, 128], F32)
            wbd = self.sb(st, "wbd", [128, 16, 128], BF16)
            self.memset("pool", wbd32[:], 0.0, [wbd32])
            self.S.barrier_on(wbd32)
            wbd32.b.multi = True
            for gi, src in enumerate((self.lwa, self.lwx)):
                for d in range(2):
                    for j in range(4):
                        for hb in range(2):
                            self.dma(wbd32[hb * 64:(hb + 1) * 64, (gi * 2 + d) * 4 + j, hb * 64:(hb + 1) * 64],
                                     src[d, 2 * j + hb], [], [wbd32])
            self.cp("dve", wbd[:], wbd32[:], [wbd32], [wbd])
            TM = TS
            xbp = self.sb(st, "xbp", [128, TM + 4], F32)
            xc = self.sb(st, "xc", [128, TM], F32)
            xcb = self.sb(st, "xcb", [128, TM], BF16)
            gt = self.sb(st, "gt_l", [128, TM], BF16)
            a_t = self.sb(st, "a_t", [128, TM], F32)
            bx_t = self.sb(st, "bx_t", [128, TM], F32)
            s_t = self.sb(st, "s_t", [128, TM], F32)
            hs = [self.sb(st, "hs%d" % d, [128, TM], F32) for d in range(2)]
            yb = self.sb(st, "yb", [128, TM], BF16)
            for (seq, g0, T) in ((0, 0, TS), (1, TS, TP), (2, TS + TP, TP)):
                for j in range(4):
                    self.memset("pool", xbp[:, 0:2], 0.0, [xbp])
                    self.memset("pool", xbp[:, T + 2:T + 4], 0.0, [xbp])
                    self.dma(xbp[:, 2:T + 2], self.xb_scr[:, j, g0:g0 + T], [self.xb_scr], [xbp])
                    self.dma(gt[:, 0:T], self.gate_scr[:, j, g0:g0 + T], [self.gate_scr], [gt], eng="act")
                    cw = lambda i: prm[:, P_CW + 4 * i + j:P_CW + 4 * i + j + 1]
                    self.act(xc[:, 0:T], xbp[:, 0:T], AF.Identity, [xbp, prm], [xc], scale=cw(0), bias=prm[:, P_CB + j:P_CB + j + 1])
                    for i in range(1, 4):
                        self.stt(xc[:, 0:T], xbp[:, i:i + T], cw(i), xc[:, 0:T], ALU.mult, ALU.add, [xbp, prm, xc], [xc])
                    self.cp("pool", xcb[:, 0:T], xc[:, 0:T], [xc], [xcb])
                    yield
                    for d in range(2):
                        for t0 in range(0, T, 512):
                            tw = min(512, T - t0)
                            p = self.ps()
                            self.mm(p[:, 0:tw], wbd[:, (0 * 2 + d) * 4 + j, :], xcb[:, t0:t0 + tw], True, True, [wbd, xcb], [p])
                            self.act(s_t[:, t0:t0 + tw], p[:, 0:tw], AF.Sigmoid, [p, prm], [s_t],
                                     bias=prm[:, P_BA + 4 * d + j:P_BA + 4 * d + j + 1])
                            p2 = self.ps()
                            self.mm(p2[:, 0:tw], wbd[:, (1 * 2 + d) * 4 + j, :], xcb[:, t0:t0 + tw], True, True, [wbd, xcb], [p2])
                            self.act(bx_t[:, t0:t0 + tw], p2[:, 0:tw], AF.Sigmoid, [p2, prm], [bx_t],
                                     bias=prm[:, P_BX + 4 * d + j:P_BX + 4 * d + j + 1])
                        col = 4 + d * 4 + j
                        self.act(a_t[:, 0:T], s_t[:, 0:T], AF.Exp, [s_t, misc], [a_t], scale=misc[:, col:col + 1])
                        self.act(s_t[:, 0:T], s_t[:, 0:T], AF.Exp, [s_t, misc], [s_t], scale=misc[:, col + 8:col + 9])
                        self.act(s_t[:, 0:T], s_t[:, 0:T], AF.Sqrt, [s_t], [s_t], scale=-1.0, bias=1.0)
                        self.tt("pool", bx_t[:, 0:T], bx_t[:, 0:T], xc[:, 0:T], ALU.mult, [bx_t, xc], [bx_t])
                        self.tt("dve", bx_t[:, 0:T], bx_t[:, 0:T], s_t[:, 0:T], ALU.mult, [bx_t, s_t], [bx_t])
                        h = hs[d]
                        if seq == 0:
                            init = prm[:, P_H0 + 4 * d + j:P_H0 + 4 * d + j + 1]
                        else:
                            init = 0.0
                        if d == 0:
                            self.scan(h[:, 0:T], a_t[:, 0:T], bx_t[:, 0:T], init, [a_t, bx_t, prm], [h])
                        else:
                            self.scan(h[:, 0:T][:, ::-1], a_t[:, 0:T][:, ::-1], bx_t[:, 0:T][:, ::-1], init, [a_t, bx_t, prm], [h])
                        if seq > 0:
                            col_o = j * 4 + (seq - 1) * 2 + d
                            te = T - 1 if d == 0 else 0
                            self.cp("pool", self.stl_t[:, col_o:col_o + 1], h[:, te:te + 1], [h], [self.stl_t])
                        yield
                    self.tt("pool", hs[0][:, 0:T], hs[0][:, 0:T], hs[1][:, 0:T], ALU.add, [hs[0], hs[1]], [hs[0]])
                    self.tt("dve", yb[:, 0:T], hs[0][:, 0:T], gt[:, 0:T], ALU.mult, [hs[0], gt], [yb])
                    self.dma(self.y_scr[:, 4 + j, g0:g0 + T], yb[:, 0:T], [yb], [self.y_scr])
            self.dma(self.stl_o[:], self.stl_t[:], [self.stl_t], [self.stl_o])

    def phaseB_chain(self, st):
        if True:
            NB = 3
            def ring(name, dt=BF16):
                return [self.sb2(st, "%s%d" % (name, i), [128, 512], dt) for i in range(NB)]
            art, rrt, ttt, akt, mrbt, mrkt, bh, kh, vt = [ring(n) for n in
                                                          ("c_art", "c_rrt", "c_ttt", "c_akt", "c_mrbt", "c_mrkt", "c_bh", "c_kh", "c_vt")]
            pend = ring("c_pend", F32)
            Hf = self.sb2(st, "Hf", [128, 512], F32)
            Hb = self.sb2(st, "Hb", [128, 512], BF16)
            Zs = self.sb2(st, "Zs", [128, 512], BF16)
            Us = self.sb2(st, "Us", [128, 512], BF16)
            Yt = [self.sb2(st, "Yt%d" % i, [128, 512], F32) for i in range(2)]
            tmpH = self.sb2(st, "tmpH", [128, 512], F32)
            hs_ = lambda h: slice(h * 64, (h + 1) * 64)
            steps = []
            for (seq, cbase, n) in ((0, 0, 32), (1, 32, 4), (2, 36, 4)):
                for i in range(n):
                    steps.append((seq, cbase, n, i))

            def loads(k):
                seq, cbase, n, i = steps[k]
                r = k % NB
                for d in range(2):
                    sl = slice(d * 64, (d + 1) * 64)
                    c = cbase + i if d == 0 else cbase + n - 1 - i
                    for (tl, scr) in ((art, self.art_scr), (rrt, self.rrt_scr), (ttt, self.ttt_scr), (akt, self.akt_scr),
                                      (mrbt, self.mrbt_scr), (mrkt, self.mrkt_scr), (bh, self.bh_scr), (kh, self.kh_scr),
                                      (pend, self.pend_scr)):
                        self.dma(tl[r][d][sl, :], scr[c, sl, :], [scr], [tl[r][d]], eng="sp")
                    self.dma(vt[r][d][sl, :], self.vt_scr[c], [self.vt_scr], [vt[r][d]], eng="sp")

            loads(0)
            for k in range(len(steps)):
                seq, cbase, n, i = steps[k]
                step = k + 1
                r = k % NB
                if k + 1 < len(steps):
                    loads(k + 1)
                if i == 0:
                    for d in range(2):
                        sl = slice(d * 64, (d + 1) * 64)
                        if seq == 0:
                            self.dma(Hf[d][sl, :], self.h0r[sl, :], [], [Hf[d]], eng="act")
                        else:
                            self.memset("dve", Hf[d][sl, :], 0.0, [Hf[d]])
                        self.cp("dve", Hb[d][sl, :], Hf[d][sl, :], [Hf[d]], [Hb[d]])
                if True:
                    ctx = []
                    for d in range(2):
                        sl = slice(d * 64, (d + 1) * 64)
                        c = cbase + i if d == 0 else cbase + n - 1 - i
                        ops = tuple(x[r][d] for x in (art, rrt, ttt, akt, mrbt, mrkt, bh, kh, vt, pend))
                        ctx.append((d, sl, c, ops))
                    pZs = {}
                    for (d, sl, c, (A_, R_, T_, AK_, MRB_, MRK_, B_, K_, V_, PE_)) in ctx:
                        H_, Z_ = Hb[d], Zs[d]
                        pZ = self.ps()
                        for h in range(8):
                            self.mm(pZ[sl, hs_(h)], A_[sl, hs_(h)], H_[sl, hs_(h)], True, False, [A_, H_], [pZ])
                            self.mm(pZ[sl, hs_(h)], AK_[sl, hs_(h)], V_[sl, hs_(h)], False, True, [AK_, V_], [pZ])
                        pZs[d] = pZ
                    pYs = {}
                    for (d, sl, c, (A_, R_, T_, AK_, MRB_, MRK_, B_, K_, V_, PE_)) in ctx:
                        H_ = Hb[d]
                        pY = self.ps()
                        pYs[d] = pY
                    for (d, sl, c, ops) in ctx:
                        self.cp("act", Zs[d][sl, :], pZs[d][sl, :], [pZs[d]], [Zs[d]])
                    pUs = {}
                    for (d, sl, c, (A_, R_, T_, AK_, MRB_, MRK_, B_, K_, V_, PE_)) in ctx:
                        Z_ = Zs[d]
                        pU = self.ps()
                        for h in range(8):
                            self.mm(pU[sl, hs_(h)], T_[sl, hs_(h)], Z_[sl, hs_(h)], True, True, [T_, Z_], [pU])
                        pUs[d] = pU
                    for (d, sl, c, ops) in ctx:
                        self.cp("act", Us[d][sl, :], pUs[d][sl, :], [pUs[d]], [Us[d]])
                    pHs = {}
                    for (d, sl, c, (A_, R_, T_, AK_, MRB_, MRK_, B_, K_, V_, PE_)) in ctx:
                        U_ = Us[d]
                        self.tt("dve", tmpH[d][sl, :], Hf[d][sl, :], PE_[sl, :], ALU.mult, [Hf[d], PE_], [tmpH[d]])
                        pH = self.ps()
                        for h in range(8):
                            self.mm(pH[sl, hs_(h)], B_[sl, hs_(h)], U_[sl, hs_(h)], True, False, [B_, U_], [pH])
                            self.mm(pH[sl, hs_(h)], K_[sl, hs_(h)], V_[sl, hs_(h)], False, True, [K_, V_], [pH])
                        pHs[d] = pH
                    for (d, sl, c, (A_, R_, T_, AK_, MRB_, MRK_, B_, K_, V_, PE_)) in ctx:
                        H_, U_ = Hb[d], Us[d]
                        pY = pYs[d]
                        for h in range(8):
                            self.mm(pY[sl, hs_(h)], R_[sl, hs_(h)], H_[sl, hs_(h)], True, False, [R_, H_], [pY])
                            self.mm(pY[sl, hs_(h)], MRB_[sl, hs_(h)], U_[sl, hs_(h)], False, False, [MRB_, U_], [pY])
                            self.mm(pY[sl, hs_(h)], MRK_[sl, hs_(h)], V_[sl, hs_(h)], False, True, [MRK_, V_], [pY])
                    for (d, sl, c, ops) in ctx:
                        self.tt("dve", Hf[d][sl, :], tmpH[d][sl, :], pHs[d][sl, :], ALU.add, [tmpH[d], pHs[d]], [Hf[d]])
                        self.cp("dve", Hb[d][sl, :], Hf[d][sl, :], [Hf[d]], [Hb[d]])
                    for (d, sl, c, ops) in ctx:
                        y = Yt[step % 2][d]
                        self.cp("act", y[sl, :], pYs[d][sl, :], [pYs[d]], [y])
                        self.dma(self.ytok_scr[d, c * 64:(c + 1) * 64, :], y[sl, :], [y], [self.ytok_scr], eng="act")
                if seq > 0 and i == n - 1:
                    self.dma(self.str_o[seq - 1], Hf[0][:], [Hf[0], Hf[1]], [self.str_o], eng="act")
                yield [(0, cbase + i), (1, cbase + n - 1 - i)]

    def phaseB_post(self, st):
        prm, cst = self.prm_t, self.cst_t
        ident = self.ident
        if True:
            yf = [self.sb(st, "yf%d" % i, [128, 512], F32) for i in range(2)]
            yb2 = [self.sb(st, "yb2%d" % i, [128, 512], F32) for i in range(2)]
            cen = self.sb(st, "cen", [128, 8, 64], F32)
            sqv = self.sb(st, "sqv", [128, 8, 64], F32)
            mean = self.sb(st, "mean", [128, 8], F32)
            var = self.sb(st, "var", [128, 8], F32)
            gl = [self.sb(st, "gl%d" % i, [128, 4, 128], BF16) for i in range(2)]
            bl = [self.sb(st, "bl%d" % i, [128, 4, 128], BF16) for i in range(2)]
            ynT = self.sb(st, "ynT", [128, 4, 128], F32)
            yo = [self.sb(st, "yo%d" % i, [128, 4, 128], BF16) for i in range(2)]
            it = -1
            blk = yield
            while True:
                it += 1
                g0 = blk * 128
                a, b = yf[it % 2], yb2[it % 2]
                g_, b_ = gl[it % 2], bl[it % 2]
                o = yo[it % 2]
                self.dma(a[:], self.ytok_scr[0, g0:g0 + 128, :], [self.ytok_scr], [a])
                self.dma(b[:], self.ytok_scr[1, g0:g0 + 128, :], [self.ytok_scr], [b], eng="act")
                self.dma(g_[:], self.g_scr[:, :, g0:g0 + 128], [self.g_scr], [g_])
                self.dma(b_[:], self.bon_scr[:, :, g0:g0 + 128], [self.bon_scr], [b_], eng="act")
                a3 = a[:].rearrange("p (h v) -> p h v", v=64)
                self.tt("pool", a[:], a[:], b[:], ALU.add, [a, b], [a])
                self.S.op("dve", lambda e, a3=a3: e.tensor_reduce(out=mean[:], in_=a3, op=ALU.add, axis=mybir.AxisListType.X),
                          [a.b], [mean.b])
                self.tsc("dve", mean[:], mean[:], 1.0 / 64, ALU.mult, [mean], [mean])
                self.tt("dve", cen[:], a3, mean[:].unsqueeze(2).to_broadcast([128, 8, 64]), ALU.subtract, [a, mean], [cen])
                self.tt("pool", sqv[:], cen[:], cen[:], ALU.mult, [cen], [sqv])
                self.S.op("dve", lambda e: e.tensor_reduce(out=var[:], in_=sqv[:], op=ALU.add, axis=mybir.AxisListType.X),
                          [sqv.b], [var.b])
                self.act(var[:], var[:], AF.Sqrt, [var, self.epsT], [var], scale=1.0 / 64, bias=self.epsT[:, 1:2])
                self.S.op("dve", lambda e: e.reciprocal(out=var[:], in_=var[:]), [var.b], [var.b])
                self.tt("dve", cen[:], cen[:], var[:].unsqueeze(2).to_broadcast([128, 8, 64]), ALU.mult, [cen, var], [cen])
                p = self.ps()
                cen2 = cen[:].rearrange("p h v -> p (h v)")
                for j in range(4):
                    self.tr(p[:, j * 128:(j + 1) * 128], cen2[:, j * 128:(j + 1) * 128], ident[:], [cen, ident], [p])
                for j in range(4):
                    self.act(ynT[:, j, :], p[:, j * 128:(j + 1) * 128], AF.Identity, [p, prm], [ynT],
                             scale=prm[:, P_LNG + j:P_LNG + j + 1], bias=prm[:, P_LNB + j:P_LNB + j + 1])
                self.tt("pool", ynT[:], ynT[:], b_[:], ALU.add, [ynT, b_], [ynT])
                self.tt("dve", o[:], ynT[:], g_[:], ALU.mult, [ynT, g_], [o])
                self.dma(self.y_scr[:, 0:4, g0:g0 + 128], o[:], [o], [self.y_scr])
                blk = yield

    def phaseC(self):
        prm, cst = self.prm_t, self.cst_t
        ident, ones_bf = self.ident, self.ones_bf
        with contextlib.ExitStack() as st:
            pass
        import os
        kcc = int(os.environ.get("KCC", "99"))
        if kcc == 0:
            return
        with contextlib.ExitStack() as st:
            wout = self.sb(st, "wout", [128, 8, D], BF16)
            wsrc = self.w_out[:].rearrange("(kc p) n -> p kc n", p=128)
            with contextlib.ExitStack() as st2:
                stg = [self.sb(st2, "stgD%d" % i, [128, 8, 256], F32) for i in range(2)]
                for cb in range(4):
                    s = stg[cb % 2]
                    self.dma(s[:], wsrc[:, :, cb * 256:(cb + 1) * 256], [], [s])
                    self.cp("act", wout[:, :, cb * 256:(cb + 1) * 256], s[:], [s], [wout])
            self.S.barrier()
            yT = self.sb(st, "yT_c", [128, 8, TC], BF16)
            o1T = self.sb(st, "o1T", [128, 8, TC], F32)
            sq1 = self.sb(st, "sq1_c", [128, 8, TC], BF16)
            oT = self.sb(st, "oT", [128, 8, TC], F32)
            sq = self.sb(st, "sq_c", [128, 8, TC], BF16)
            xTs = [self.sb(st, "xT_c%d" % i, [128, 8, TC], F32) for i in range(2)]
            h2 = self.sb(st, "h2", [128, 8, TC], BF16)
            f = self.sb(st, "f_c", [128, 32, TC], BF16)
            otoks = [self.sb(st, "otok%d" % i, [128, 2, D], F32) for i in range(1)]
            rstd = self.sb(st, "rstd_c", [128, TC], F32)
            tmp = [self.sb(st, "tmpC%d" % i, [128, TC], F32) for i in range(2)]
            NW = 3
            w1r = [self.sb(st, "w1r%d" % i, [128, 8, 512], BF16) for i in range(NW)]
            w2r = [self.sb(st, "w2r%d" % i, [128, 32, 128], BF16) for i in range(2)]
            ntile = min(NTOK // TC, kcc)
            wseq = []
            for ti_ in range(ntile):
                wseq += [("w1", b_) for b_ in range(8)] + [("w2", b_) for b_ in range(8)]
            wstate = {"issued": 0, "w1": 0, "w2": 0}
            wbuf = {}

            def issue_upto(n):
                while wstate["issued"] < min(n, len(wseq)):
                    k_ = wstate["issued"]
                    kind, b_ = wseq[k_]
                    ring = w1r if kind == "w1" else w2r
                    buf = ring[wstate[kind] % len(ring)]
                    wstate[kind] += 1
                    scr = self.w1_scr if kind == "w1" else self.w2_scr
                    self.dma(buf[:], scr[b_], [scr], [buf], eng="sp" if k_ % 2 == 0 else "act")
                    wbuf[k_] = buf
                    wstate["issued"] += 1

            def rms(sqt, R):
                p = self.ps()
                for j in range(8):
                    self.mm(p[:], ones_bf[:], sqt[:, j, :], j == 0, j == 7, [ones_bf, sqt], [p])
                self.act(rstd[:], p[:], AF.Sqrt, [p, self.epsT], [rstd], scale=1.0 / D, bias=self.epsT[:, 0:1])
                self.S.op("dve", lambda e: e.reciprocal(out=rstd[:], in_=rstd[:]), [rstd.b], [rstd.b])

            def resid(gg, mc, oT, xT):
                for j in range(8):
                    t = tmp[j % 2]
                    self.tt("dve", t[:], oT[:, j, :], rstd[:], ALU.mult, [oT, rstd], [t])
                    self.stt(xT[:, j, :], t[:], gg[:, j, mc:mc + 1], xT[:, j, :], ALU.mult, ALU.add, [t, gg, xT], [xT])

            def head(tj):
                gj = tj * TC
                mcj = 0 if gj < TS else 1
                xT = xTs[tj % 2]
                self.dma(xT[:], self.xT_scr[:, :, gj:gj + TC], [self.xT_scr], [xT], eng="act")
                yield
                rms(sq1, None)
                yield
                for j in range(8):
                    t = tmp[j % 2]
                    self.tt("dve", t[:], o1T[:, j, :], rstd[:], ALU.mult, [o1T, rstd], [t])
                    self.stt(xT[:, j, :], t[:], self.gg1[:, j, mcj:mcj + 1], xT[:, j, :], ALU.mult, ALU.add, [t, self.gg1, xT], [xT])
                    self.act(sq1[:, j, :], xT[:, j, :], AF.Square, [xT], [sq1])
                    if j % 2 == 1:
                        yield
                rms(sq1, None)
                yield
                for j in range(8):
                    t = tmp[j % 2]
                    self.tt("dve", t[:], xT[:, j, :], rstd[:], ALU.mult, [xT, rstd], [t])
                    self.act(h2[:, j, :], t[:], AF.Identity, [t, self.gs2, self.modT], [h2],
                             scale=self.gs2[:, j, mcj:mcj + 1], bias=self.modT[:, 24 + j, mcj:mcj + 1])
                    if j % 2 == 1:
                        yield

            def wout_stage(tj):
                gj = tj * TC
                self.dma(yT[:], self.y_scr[:, :, gj:gj + TC], [self.y_scr], [yT])
                for m in range(8):
                    p = self.ps()
                    for kc in range(8):
                        self.mm(p[:], wout[:, kc, m * 128:(m + 1) * 128], yT[:, kc, :], kc == 0, kc == 7, [wout, yT], [p])
                    self.cp("act", o1T[:, m, :], p[:], [p], [o1T])
                    self.act(sq1[:, m, :], p[:], AF.Square, [p], [sq1])

            def tail(g0, mc, xT):
                rms(sq, None)
                yield
                resid(self.gg2, mc, oT, xT)
                yield
                for hh in range(2):
                    otok = otoks[0]
                    for s2 in range(2):
                        s_ = hh * 2 + s2
                        for half in range(2):
                            p = self.ps()
                            for jj in range(4):
                                j = half * 4 + jj
                                self.tr(p[:, jj * 128:(jj + 1) * 128], xT[:, j, s_ * 128:(s_ + 1) * 128], ident[:], [xT, ident], [p])
                            self.cp("act" if half == 0 else "dve", otok[:, s2, half * 512:(half + 1) * 512], p[:], [p], [otok])
                            yield
                    if g0 < TS:
                        dst = self.ys[g0 + hh * 256:g0 + (hh + 1) * 256, :].rearrange("(s p) f -> p s f", p=128)
                        self.dma(dst, otok[:], [otok], [self.ys])
                    else:
                        dst = self.yp[hh * 256:(hh + 1) * 256, :].rearrange("(s p) f -> p s f", p=128)
                        self.dma(dst, otok[:], [otok], [self.yp])

            tg = None
            wi = 0
            for ti in range(NTOK // TC):
                if ti >= kcc:
                    break
                g0 = ti * TC
                mc = 0 if g0 < TS else 1
                xT = xTs[ti % 2]
                if ti == 0:
                    wout_stage(0)
                    for _ in head(0):
                        pass
                issue_upto(ti * 16 + 3)
                for blk in range(8):
                    if tg is not None:
                        for _ in range(2):
                            try:
                                next(tg)
                            except StopIteration:
                                tg = None
                                break
                    issue_upto(ti * 16 + blk + 3)
                    w = wbuf[ti * 16 + blk]
                    for c4 in range(4):
                        fc = blk * 4 + c4
                        p = self.ps()
                        for kc in range(8):
                            self.mm(p[:], w[:, kc, c4 * 128:(c4 + 1) * 128], h2[:, kc, :], kc == 0, kc == 7, [w, h2], [p])
                        t = tmp[fc % 2]
                        self.act(t[:], p[:], AF.Relu, [p], [t])
                        self.tt("pool" if fc % 2 == 0 else "dve", f[:, fc, :], t[:], t[:], ALU.mult, [t], [f])
                if tg is not None:
                    for _ in tg:
                        pass
                    tg = None
                hg = None
                if ti + 1 < ntile:
                    wout_stage(ti + 1)
                    hg = head(ti + 1)
                for m in range(8):
                    if hg is not None:
                        for _ in range(2):
                            try:
                                next(hg)
                            except StopIteration:
                                hg = None
                                break
                    issue_upto(ti * 16 + 8 + m + 2)
                    w = wbuf[ti * 16 + 8 + m]
                    p = self.ps()
                    for fc in range(32):
                        self.mm(p[:], w[:, fc, :], f[:, fc, :], fc == 0, fc == 31, [w, f], [p])
                    self.cp("act", oT[:, m, :], p[:], [p], [oT])
                    self.act(sq[:, m, :], p[:], AF.Square, [p], [sq])
                if hg is not None:
                    for _ in hg:
                        pass
                tg = tail(g0, mc, xT)
                if ti + 1 >= ntile:
                    for _ in tg:
                        pass
                    tg = None


def _fm(v):
    v = np.asarray(v, np.float32).reshape(-1, 128)
    return np.ascontiguousarray(v.T)


def _pos_embed():
    def sincos(pos, dim):
        omega = (1.0 / (10000.0 ** (np.arange(dim // 2, dtype=np.float32) / np.float32(dim // 2)))).astype(np.float32)
        ang = pos.astype(np.float32)[:, None] * omega[None, :]
        return np.concatenate([np.sin(ang), np.cos(ang)], axis=-1).astype(np.float32)
    rows = TS // 64
    half = D // 2
    e_row = sincos(np.arange(rows), half)
    e_col = sincos(np.arange(64), half)
    emb = np.concatenate([np.broadcast_to(e_row[:, None, :], (rows, 64, half)),
                          np.broadcast_to(e_col[None, :, :], (rows, 64, half))], axis=-1)
    return np.ascontiguousarray(emb.reshape(rows * 64, D).astype(np.float32))


def _consts():
    c = np.zeros((128, NCST), np.float32)
    c[:, C_ID:C_ID + 128] = np.eye(128, dtype=np.float32)
    ob = np.zeros((128, 128), np.float32)
    ob[:64, :64] = 1.0
    ob[64:, 64:] = 1.0
    c[:, C_OB:C_OB + 128] = ob
    s = np.arange(64)[:, None]
    t = np.arange(64)[None, :]
    msi = np.zeros((128, 2, 64), np.float32)
    msi[:64, 0] = (s < t)
    msi[:64, 1] = (s <= t)
    msi[64:, 0] = (s > t)
    msi[64:, 1] = (s >= t)
    c[:, C_MSI:C_MSI + 128] = msi.reshape(128, 128)
    ml = np.zeros((128, 64), np.float32)
    ml[:64] = (t < s)
    ml[64:] = (t > s)
    c[:, C_ML:C_ML + 64] = ml
    ids = np.zeros((128, 64), np.float32)
    ids[:64] = np.eye(64)
    ids[64:] = np.eye(64)
    c[:, C_IDS:C_IDS + 64] = ids
    c[:64, C_MSI1:C_MSI1 + 128] = msi[64:].reshape(64, 128)
    c[:64, C_ML1:C_ML1 + 64] = ml[64:]
    tt_ = np.arange(TT)
    c[:, C_RMF:C_RMF + TT] = (tt_ % 64 != 0).astype(np.float32)[None, :]
    c[:, C_RMB:C_RMB + TT] = (tt_ % 64 != 63).astype(np.float32)[None, :]
    return c


_NC_CACHE = {}


def kernel(x_prompt, x_sample, c, state_rwkv, state_lru, c_ctx, w_mod, b_mod,
           g_pre_mix, g_post_mix, g_pre_mlp, g_post_mlp, w_in,
           rwkv_w0, rwkv_w_up, rwkv_a0, rwkv_a_up, rwkv_g_up, rwkv_k_k, rwkv_k_a, rwkv_r_k,
           rwkv_lnx_g, rwkv_lnx_b, lru_conv_w, lru_conv_b, lru_wa, lru_ba, lru_wx, lru_bx,
           lru_lambda, w_out, w_mlp1, w_mlp2, _debug=False):
    f = lambda a: np.ascontiguousarray(np.asarray(a, np.float32))
    x_prompt, x_sample, c, state_rwkv, state_lru, c_ctx = map(f, (x_prompt, x_sample, c, state_rwkv, state_lru, c_ctx))
    nc = K(debug=_debug).build()
    pe = _pos_embed()
    cst = _consts()
    shared = {
        "pe": pe, "cst": cst,
        "w_mod": f(w_mod[0]), "w_in": f(w_in[0]), "w_out": f(w_out[0]), "w1": f(w_mlp1[0]), "w2": f(w_mlp2[0]),
        "wup": f(rwkv_w_up[0]).reshape(128, 512), "aup": f(rwkv_a_up[0]).reshape(128, 512), "gup": f(rwkv_g_up[0]),
        "lwa": f(lru_wa[0]), "lwx": f(lru_wx[0]),
    }
    prm0 = np.zeros((128, NPRM), np.float32)
    prm0[:, P_GPRE:P_GPRE + 8] = _fm(g_pre_mix[0])
    prm0[:, P_GPOST:P_GPOST + 8] = _fm(g_post_mix[0])
    prm0[:, P_GPRE2:P_GPRE2 + 8] = _fm(g_pre_mlp[0])
    prm0[:, P_GPOST2:P_GPOST2 + 8] = _fm(g_post_mlp[0])
    prm0[:, P_BMOD:P_BMOD + 48] = _fm(b_mod[0])
    for d in range(2):
        prm0[:, P_W0 + 4 * d:P_W0 + 4 * d + 4] = _fm(rwkv_w0[0, d])
        prm0[:, P_A0 + 4 * d:P_A0 + 4 * d + 4] = _fm(rwkv_a0[0, d])
        prm0[:, P_BA + 4 * d:P_BA + 4 * d + 4] = _fm(lru_ba[0, d])
        prm0[:, P_BX + 4 * d:P_BX + 4 * d + 4] = _fm(lru_bx[0, d])
        prm0[:, P_LAM + 4 * d:P_LAM + 4 * d + 4] = _fm(lru_lambda[0, d])
    prm0[:, P_KK:P_KK + 4] = _fm(rwkv_k_k[0])
    prm0[:, P_KA:P_KA + 4] = _fm(rwkv_k_a[0])
    prm0[:, P_RK:P_RK + 4] = _fm(np.asarray(rwkv_r_k[0]).reshape(-1))
    prm0[:, P_LNG:P_LNG + 4] = _fm(rwkv_lnx_g[0])
    prm0[:, P_LNB:P_LNB + 4] = _fm(rwkv_lnx_b[0])
    for i in range(4):
        prm0[:, P_CW + 4 * i:P_CW + 4 * i + 4] = _fm(lru_conv_w[0, i])
    prm0[:, P_CB:P_CB + 4] = _fm(lru_conv_b[0])
    in_maps = []
    for i in range(8):
        prm = prm0.copy()
        for d in range(2):
            prm[:, P_H0 + 4 * d:P_H0 + 4 * d + 4] = _fm(state_lru[i, 0, d])
        cT = np.zeros((128, 8, 2), np.float32)
        cT[:, :, 0] = _fm(c[i])
        cT[:, :, 1] = _fm(c_ctx)
        h0 = np.ascontiguousarray(state_rwkv[i, 0].transpose(0, 3, 1, 2)).reshape(128, 512)
        m = dict(shared)
        m.update({"xs": x_sample[i], "xp": np.ascontiguousarray(x_prompt[2 * i:2 * i + 2].reshape(2 * TP, D)),
                  "cT": cT.reshape(128, 16), "h0r": h0, "prm": prm})
        in_maps.append(m)
    res = run_bass_kernel_spmd(nc, in_maps, core_ids=list(range(8)))
    R = res.results
    y_prompt = np.zeros((16, TP, D), np.float32)
    y_sample = np.zeros((8, TS, D), np.float32)
    st_r = np.zeros((16, 1, 2, 8, 64, 64), np.float32)
    st_l = np.zeros((16, 1, 2, 512), np.float32)
    for i in range(8):
        r = R[i]
        y_sample[i] = r["ys"]
        y_prompt[2 * i:2 * i + 2] = r["yp"].reshape(2, TP, D)
        so = r["str_o"].reshape(2, 2, 64, 8, 64)
        st_r[2 * i:2 * i + 2, 0] = so.transpose(0, 1, 3, 4, 2)
        sl = r["stl_o"].reshape(128, 4, 2, 2)
        st_l[2 * i:2 * i + 2, 0] = sl.transpose(2, 3, 1, 0).reshape(2, 2, 512)
    if _debug:
        return (y_prompt, y_sample, st_r, st_l), R
    return (y_prompt, y_sample, st_r, st_l)
```

```python
import contextlib
import numpy as np
import concourse.bass as bass
import concourse.mybir as mybir
from concourse.bass_utils import run_bass_kernel_spmd

F32 = mybir.dt.float32
BF16 = mybir.dt.bfloat16
F32R = mybir.dt.float32r
AF = mybir.ActivationFunctionType
ALU = mybir.AluOpType

D = 1024
TS = 2048
TP = 256
NTOK = TS + 2 * TP
NCH = NTOK // 64
DIN = 2944
DFF = 4096
LAM = float(np.exp(-0.5))
EPS = 1e-6
LNX_EPS = 64e-5
TT = 256
TC = 512
GELU_C = 1.5957691216057308

P_GPRE, P_GPOST, P_GPRE2, P_GPOST2 = 0, 8, 16, 24
P_BMOD = 32
P_W0, P_A0 = 80, 88
P_KK, P_KA, P_RK, P_LNG, P_LNB = 96, 100, 104, 108, 112
P_CW, P_CB = 116, 132
P_BA, P_BX, P_LAM, P_H0 = 136, 144, 152, 160
NPRM = 168
C_ID, C_OB, C_MSI, C_ML, C_IDS, C_RMF, C_RMB = 0, 128, 256, 384, 448, 512, 768
C_MSI1, C_ML1 = 1024, 1152
NCST = 1216


class Buf:
    __slots__ = ("name", "lw", "rd", "excl", "multi", "ws")

    def __init__(self, name=""):
        self.name = name
        self.lw = None
        self.rd = {}
        self.excl = False
        self.multi = False
        self.ws = {}


class TL:
    def __init__(self, t, name=""):
        self.t = t
        self.b = Buf(name)

    def __getitem__(self, k):
        return self.t[k]


class Sched:
    ENGS = ("pe", "act", "dve", "pool", "sp")

    def __init__(self, nc):
        self.nc = nc
        self.streams = {e: [] for e in self.ENGS}
        self.cnt = {}
        self.waited = {e: {} for e in self.ENGS}
        self.n_ops = 0
        self.dma_n = {e: 0 for e in self.ENGS}
        self.NSLOT = {"sp": 44, "act": 44, "pool": 4, "dve": 2, "pe": 2}

    def _deps(self, eng, reads, writes):
        need = {}
        for b in reads:
            if b.multi:
                for s, v in b.ws.items():
                    if need.get(s, 0) < v:
                        need[s] = v
                continue
            if b.lw is not None:
                s, v = b.lw
                if need.get(s, 0) < v:
                    need[s] = v
            if b.excl:
                for s, v in b.rd.items():
                    if s != eng and need.get(s, 0) < v:
                        need[s] = v
        for b in writes:
            if b.multi:
                continue
            if b.lw is not None:
                s, v = b.lw
                if need.get(s, 0) < v:
                    need[s] = v
            for s, v in b.rd.items():
                if need.get(s, 0) < v:
                    need[s] = v
        out = []
        w = self.waited[eng]
        for s, v in need.items():
            if s == "pe" and eng == "pe":
                continue
            if w.get(s, 0) >= v:
                continue
            w[s] = v
            out.append((s, v))
        return out

    def op(self, eng, fn, reads=(), writes=(), dma=False):
        reads = [r.b if isinstance(r, TL) else r for r in reads]
        writes = [r.b if isinstance(r, TL) else r for r in writes]
        waits = self._deps(eng, reads, writes)
        if dma:
            slot = self.dma_n[eng] % self.NSLOT[eng]
            self.dma_n[eng] += 1
            sem = "%s_d%d" % (eng, slot)
            prev = self.cnt.get(sem, 0)
            if prev > 0 and self.waited[eng].get(sem, 0) < prev:
                self.waited[eng][sem] = prev
                waits.append((sem, prev))
        else:
            sem = eng
        inc = 16 if dma else 1
        self.cnt[sem] = self.cnt.get(sem, 0) + inc
        val = self.cnt[sem]
        self.streams[eng].append((waits, fn, sem, inc))
        self.n_ops += 1
        for b in reads:
            if b.rd.get(sem, 0) < val:
                b.rd[sem] = val
        for b in writes:
            if b.multi:
                if b.ws.get(sem, 0) < val:
                    b.ws[sem] = val
                continue
            b.lw = (sem, val)
            b.rd = {}
        return val

    def barrier_on(self, tl):
        if tl.b.lw is None:
            return
        sname, v = tl.b.lw
        for e in ("sp", "act", "pool"):
            if self.waited[e].get(sname, 0) < v:
                self.waited[e][sname] = v
                self.streams[e].append(([(sname, v)], None, None, 0))

    def barrier(self):
        snap = dict(self.cnt)
        for e in self.ENGS:
            waits = []
            for s, v in snap.items():
                if s == "pe" and e == "pe":
                    continue
                if self.waited[e].get(s, 0) < v:
                    self.waited[e][s] = v
                    waits.append((s, v))
            if waits:
                self.streams[e].append((waits, None, None, 0))

    def emit(self):
        nc = self.nc
        sems = {}
        with contextlib.ExitStack() as st:
            for s in self.cnt:
                sems[s] = st.enter_context(nc.semaphore(s))
            block = st.enter_context(nc.Block())
            engmap = {"pe": block.tensor, "act": block.scalar, "dve": block.vector,
                      "pool": block.gpsimd, "sp": block.sync}
            for e in self.ENGS:
                stream = self.streams[e]
                if not stream:
                    continue

                def body(eng, stream=stream):
                    for waits, fn, sem, inc in stream:
                        for s, v in waits:
                            eng.wait_ge(sems[s], v)
                        if fn is not None:
                            fn(eng).then_inc(sems[sem], inc)
                engmap[e](body)


class K:
    def __init__(self, debug=False, stop_after=None):
        self.debug = debug
        self.stop_after = stop_after
        import os
        self.cutk = int(os.environ.get("KCUT", "0"))
        self.cutm = int(os.environ.get("KCUTM", "99"))
        self.ktiles = int(os.environ.get("KTILES", "99"))
        self.kskip = os.environ.get("KSKIP", "").split(",")
        self.nc = bass.Bass("TRN2", target_bir_lowering=False)
        self.S = Sched(self.nc)
        self.es = contextlib.ExitStack()
        self.psr = 0
        self.rr = {}

    def dram(self, name, shape, dt, kind="Internal"):
        t = TL(self.nc.dram_tensor(name, list(shape), dt, kind=kind).ap(), name)
        t.b.multi = True
        return t

    def sb(self, st, name, shape, dt):
        return TL(st.enter_context(self.nc.sbuf_tensor(name, list(shape), dt)), name)

    def sb2(self, st, name, shape, dt):
        t = st.enter_context(self.nc.sbuf_tensor(name, list(shape), dt))
        return [TL(t, name + "_lo"), TL(t, name + "_hi")]

    def ps(self):
        p = self.psum[self.psr % 8]
        self.psr += 1
        return p

    def mm(self, out, lhsT, rhs, start, stop, R, W):
        self.S.op("pe", lambda e: e.matmul(out, lhsT=lhsT, rhs=rhs, start=start, stop=stop), R, W)

    def tr(self, out, in_, ident, R, W):
        self.S.op("pe", lambda e: e.transpose(out, in_, ident), R, W)

    def act(self, out, in_, func, R, W, scale=1.0, bias=None, eng="act"):
        if bias is None:
            self.S.op("act", lambda e: e.activation(out=out, in_=in_, func=func, scale=scale), R, W)
        else:
            self.S.op("act", lambda e: e.activation(out=out, in_=in_, func=func, scale=scale, bias=bias), R, W)

    def tt(self, eng, out, in0, in1, op, R, W):
        self.S.op(eng, lambda e: e.tensor_tensor(out=out, in0=in0, in1=in1, op=op), R, W)

    def tsc(self, eng, out, in0, s1, op0, R, W, s2=None, op1=None):
        if op1 is None:
            self.S.op(eng, lambda e: e.tensor_scalar(out=out, in0=in0, scalar1=s1, scalar2=None, op0=op0), R, W)
        else:
            self.S.op(eng, lambda e: e.tensor_scalar(out=out, in0=in0, scalar1=s1, scalar2=s2, op0=op0, op1=op1), R, W)

    def stt(self, out, in0, scalar, in1, op0, op1, R, W):
        self.S.op("dve", lambda e: e.scalar_tensor_tensor(out=out, in0=in0, scalar=scalar, in1=in1, op0=op0, op1=op1), R, W)

    def cp(self, eng, out, in_, R, W):
        if eng == "act":
            self.S.op("act", lambda e: e.activation(out=out, in_=in_, func=AF.Copy), R, W)
        else:
            self.S.op(eng, lambda e: e.tensor_copy(out=out, in_=in_), R, W)

    def scan(self, out, d0, d1, init, R, W):
        self.S.op("dve", lambda e: e.tensor_tensor_scan(out=out, data0=d0, data1=d1, initial=init,
                                                        op0=ALU.mult, op1=ALU.add), R, W)

    def dma(self, out, in_, R, W, eng="sp"):
        self.S.op(eng, lambda e: e.dma_start(out=out, in_=in_), R, W, dma=True)

    def memset(self, eng, ap, val, W):
        self.S.op(eng, lambda e: e.memset(ap, val), (), W)

    def pick(self, key, engs):
        i = self.rr.get(key, 0)
        self.rr[key] = i + 1
        return engs[i % len(engs)]

    def build(self):
        nc = self.nc
        I = lambda n, s, dt=F32: self.dram(n, s, dt, "ExternalInput")
        O = lambda n, s, dt=F32: self.dram(n, s, dt, "ExternalOutput")
        self.xs = I("xs", [TS, D])
        self.xp = I("xp", [2 * TP, D])
        self.pe = I("pe", [TS, D])
        self.cT = I("cT", [128, 16])
        self.h0r = I("h0r", [128, 512])
        self.prm = I("prm", [128, NPRM])
        self.cst = I("cst", [128, NCST])
        self.w_mod = I("w_mod", [D, 6 * D])
        self.w_in = I("w_in", [D, DIN])
        self.w_out = I("w_out", [D, D])
        self.w1 = I("w1", [D, DFF])
        self.w2 = I("w2", [DFF, D])
        self.wup = I("wup", [128, 512])
        self.aup = I("aup", [128, 512])
        self.gup = I("gup", [128, 512])
        self.lwa = I("lwa", [2, 8, 64, 64])
        self.lwx = I("lwx", [2, 8, 64, 64])
        self.ys = O("ys", [TS, D])
        self.yp = O("yp", [2 * TP, D])
        self.str_o = O("str_o", [2, 128, 512])
        self.stl_o = O("stl_o", [128, 16])
        self.xT_scr = self.dram("xT_scr", [128, 8, NTOK], F32)
        self.xb_scr = self.dram("xb_scr", [128, 4, NTOK], F32)
        self.gate_scr = self.dram("gate_scr", [128, 4, NTOK], BF16)
        self.g_scr = self.dram("g_scr", [128, 4, NTOK], BF16)
        self.bon_scr = self.dram("bon_scr", [128, 4, NTOK], BF16)
        self.y_scr = self.dram("y_scr", [128, 8, NTOK], BF16)
        self.ytok_scr = self.dram("ytok_scr", [2, NTOK, 512], F32)
        for n in ("art", "rrt", "ttt", "akt", "mrbt", "mrkt", "bh", "kh"):
            setattr(self, n + "_scr", self.dram(n + "_scr", [NCH, 128, 512], BF16))
        self.vt_scr = self.dram("vt_scr", [NCH, 64, 512], BF16)
        self.pend_scr = self.dram("pend_scr", [NCH, 128, 512], F32)
        self.w1_scr = self.dram("w1_scr", [8, 128, 8, 512], BF16)
        self.w2_scr = self.dram("w2_scr", [8, 128, 32, 128], BF16)
        if self.debug:
            self.dbg = {}

        with self.es as st0:
            self.psum = [TL(st0.enter_context(nc.psum_tensor("ps%d" % i, [128, 512], F32)), "ps%d" % i)
                         for i in range(8)]
            for p_ in self.psum:
                p_.b.excl = True
            self.prm_t = self.sb(st0, "prm_t", [128, NPRM], F32)
            self.cst_t = self.sb(st0, "cst_t", [128, NCST], F32)
            self.modT = self.sb(st0, "modT", [128, 48, 2], F32)
            self.gs1 = self.sb(st0, "gs1", [128, 8, 2], F32)
            self.gs2 = self.sb(st0, "gs2", [128, 8, 2], F32)
            self.gg1 = self.sb(st0, "gg1", [128, 8, 2], F32)
            self.gg2 = self.sb(st0, "gg2", [128, 8, 2], F32)
            self.ident = self.sb(st0, "ident", [128, 128], F32)
            self.ones_bf = self.sb(st0, "ones_bf", [128, 128], BF16)
            self.oblk_bf = self.sb(st0, "oblk_bf", [128, 128], BF16)
            self.epsT = self.sb(st0, "epsT", [128, 2], F32)
            self.misc = self.sb(st0, "misc", [128, 32], F32)
            self.stl_t = self.sb(st0, "stl_t", [128, 16], F32)
            stop = False
            with contextlib.ExitStack() as stA:
                self.win = self.sb(stA, "win", [128, 8, DIN], BF16)
                self.win.b.multi = True
                self.wup_t = self.sb(stA, "wup_t", [128, 512], BF16)
                self.aup_t = self.sb(stA, "aup_t", [128, 512], BF16)
                self.gup_t = self.sb(stA, "gup_t", [128, 512], BF16)
                for nm, fn in (("p0", self.phase0), ("pA", self.phaseA)):
                    fn()
                    self.S.barrier()
                    if self.stop_after == nm:
                        stop = True
                        break
            if not stop:
                for nm, fn in (("pB", self.phaseB), ("pC", self.phaseC)):
                    fn()
                    self.S.barrier()
                    if self.stop_after == nm:
                        break
            self.S.emit()
        return nc

    def dump(self, name, src_ap, shape, dt, R):
        o = self.dram("dbg_" + name, shape, dt, "ExternalOutput")
        self.dma(o[:], src_ap, R, [o])

    def phase0(self):
        nc = self.nc
        prm, cst = self.prm_t, self.cst_t
        self.dma(prm[:], self.prm[:], [], [prm])
        self.dma(cst[:], self.cst[:], [], [cst])
        self.cp("dve", self.ident[:], cst[:, C_ID:C_ID + 128], [cst], [self.ident])
        self.cp("dve", self.oblk_bf[:], cst[:, C_OB:C_OB + 128], [cst], [self.oblk_bf])
        self.memset("dve", self.ones_bf[:], 1.0, [self.ones_bf])
        self.memset("dve", self.epsT[:, 0:1], EPS, [self.epsT])
        self.memset("dve", self.epsT[:, 1:2], LNX_EPS, [self.epsT])
        self.tsc("dve", self.misc[:, 0:4], prm[:, P_KA:P_KA + 4], -1.0, ALU.mult, [prm], [self.misc], 1.0, ALU.add)
        with contextlib.ExitStack() as st:
            scT = self.sb(st, "scT", [128, 16], F32)
            cT = self.sb(st, "cT_t", [128, 16], F32)
            wm = [self.sb(st, "wm%d" % i, [128, 8, 512], F32) for i in range(2)]
            tmp = self.sb(st, "lam_tmp", [128, 8], F32)
            self.dma(cT[:], self.cT[:], [], [cT])
            self.act(scT[:], cT[:], AF.Silu, [cT], [scT])
            self.act(tmp[:], prm[:, P_LAM:P_LAM + 8], AF.Exp, [prm], [tmp], scale=-1.0)
            self.act(tmp[:], tmp[:], AF.Ln, [tmp], [tmp], bias=1.0)
            self.tsc("dve", self.misc[:, 4:12], tmp[:], -8.0, ALU.mult, [tmp], [self.misc])
            self.tsc("dve", self.misc[:, 12:20], tmp[:], -16.0, ALU.mult, [tmp], [self.misc])
            wsrc = self.w_mod[:].rearrange("(kc p) n -> p kc n", p=128)
            scb = self.sb(st, "scb", [128, 16], BF16)
            self.cp("dve", scb[:], scT[:], [scT], [scb])
            sc3 = scb[:].rearrange("p (k c) -> p k c", c=2)
            wmb = [self.sb(st, "wmb%d" % i, [128, 8, 512], BF16) for i in range(2)]
            stgA = [self.sb(st, "stgA%d" % i, [128, 8, 256], F32) for i in range(2)]
            wisrc = self.w_in[:].rearrange("(kc p) n -> p kc n", p=128)
            wi_blocks = [(c0, min(256, DIN - c0)) for c0 in range(0, DIN, 256)]
            sm_list = [(self.wup, self.wup_t), (self.aup, self.aup_t), (self.gup, self.gup_t)]
            nb = [0]

            def win_step():
                if wi_blocks:
                    c0, cw = wi_blocks.pop(0)
                    s_ = stgA[nb[0] % 2]
                    self.dma(s_[:, :, 0:cw], wisrc[:, :, c0:c0 + cw], [], [s_], eng="act")
                    self.cp("act" if nb[0] % 2 == 0 else "pool", self.win[:, :, c0:c0 + cw], s_[:, :, 0:cw], [s_], [self.win])
                    nb[0] += 1
                elif sm_list:
                    src, dstt = sm_list.pop(0)
                    s_ = stgA[nb[0] % 2]
                    nb[0] += 1
                    s2 = s_[:].rearrange("p a b -> p (a b)")[:, 0:512]
                    self.dma(s2, src[:], [], [s_], eng="act")
                    self.cp("pool", dstt[:], s2, [s_], [dstt])

            for blk in range(12):
                w = wm[blk % 2]
                wb_ = wmb[blk % 2]
                self.dma(w[:], wsrc[:, :, blk * 512:(blk + 1) * 512], [], [w], eng="sp")
                self.cp("dve", wb_[:], w[:], [w], [wb_])
                win_step()
                p = self.ps()
                for m in range(4):
                    for kc in range(8):
                        self.mm(p[:, 2 * m:2 * m + 2], wb_[:, kc, m * 128:(m + 1) * 128], sc3[:, kc, :],
                                kc == 0, kc == 7, [wb_, scb], [p])
                for m in range(4):
                    mi = blk * 4 + m
                    self.tsc("dve", self.modT[:, mi, :], p[:, 2 * m:2 * m + 2], prm[:, P_BMOD + mi:P_BMOD + mi + 1],
                             ALU.add, [p, prm], [self.modT])
            while wi_blocks or sm_list:
                win_step()
            m3 = self.modT
            for (dst, sc_off, g_off, one) in ((self.gs1, 8, P_GPRE, 1.0), (self.gs2, 32, P_GPRE2, 1.0),
                                              (self.gg1, 16, P_GPOST, 0.0), (self.gg2, 40, P_GPOST2, 0.0)):
                for c in range(2):
                    self.tsc("dve", dst[:, :, c], m3[:, sc_off:sc_off + 8, c], one, ALU.add, [m3], [dst])
                    self.tt("dve", dst[:, :, c], dst[:, :, c], prm[:, g_off:g_off + 8], ALU.mult, [dst, prm], [dst])
            if self.debug:
                self.dump("modT", self.modT[:], [128, 48, 2], F32, [self.modT])
                self.dump("gs1", self.gs1[:], [128, 8, 2], F32, [self.gs1])

    def load_cast(self, st, dst_ap, dst_tl, src_ap, shape, tag):
        key = "stg_" + tag
        if not hasattr(self, key):
            setattr(self, key, [self.sb(st, "%s%d" % (key, i), shape, F32) for i in range(2)])
        ring = getattr(self, key)
        s = ring[self.rr.get(key, 0) % 2]
        self.rr[key] = self.rr.get(key, 0) + 1
        self.dma(s[:], src_ap, [], [s], eng="sp")
        eng = self.pick("castE", ["act", "pool"])
        self.cp(eng, dst_ap, s[:], [s], [dst_tl])

    def phaseA(self):
        nc = self.nc
        prm, cst = self.prm_t, self.cst_t
        with contextlib.ExitStack() as st:
            win, wup, aup, gup = self.win, self.wup_t, self.aup_t, self.gup_t
            mSI = cst[:, C_MSI:C_MSI + 128].rearrange("p (q t) -> p q t", q=2)
            mL = cst[:, C_ML:C_ML + 64]
            idS = cst[:, C_IDS:C_IDS + 64]

            xin = self.sb(st, "xin", [128, 2, D], F32)
            xT = self.sb(st, "xT", [128, 8, TT], F32)
            sq = self.sb(st, "sq", [128, 8, TT], BF16)
            hT = self.sb(st, "hT", [128, 8, TT], BF16)
            rstd = self.sb(st, "rstd", [128, TT], F32)
            tmpA = [self.sb(st, "tmpA%d" % i, [128, TT], F32) for i in range(2)]
            rT = self.sb(st, "rT", [128, 4, TT], F32)
            kT = self.sb(st, "kT", [128, 4, TT], F32)
            vT = self.sb(st, "vT", [128, 4, TT], F32)
            xw = self.sb(st, "xw", [128, TT], BF16)
            xa = self.sb(st, "xa", [128, TT], BF16)
            xg = self.sb(st, "xg", [128, TT], BF16)
            xbT = self.sb(st, "xbT", [128, 4, TT], F32)
            gtmp = [self.sb(st, "gtmp%d" % i, [128, TT], F32) for i in range(3)]
            gate = self.sb(st, "gate", [128, 4, TT], BF16)
            gT = self.sb(st, "gT", [128, 4, TT], BF16)
            kkn = self.sb(st, "kkn", [128, 4, TT], F32)
            ksum = self.sb(st, "ksum", [128, 4, TT], F32)
            bon = self.sb(st, "bon", [128, 4, TT], BF16)
            sg = self.sb(st, "sg", [128, 4, TT], F32)
            cs = self.sb(st, "cs", [128, 4, TT], F32)
            E1 = self.sb(st, "E1", [128, 4, TT], F32)
            ad = self.sb(st, "ad", [128, 4, TT], F32)
            wk1 = self.sb(st, "wk1", [128, 4, TT], F32)
            wk2 = self.sb(st, "wk2", [128, 4, TT], F32)
            NC4 = TT // 64
            AR = [self.sb(st, "AR%d" % d, [128, 4, NC4, 2, 64], BF16) for d in range(2)]
            BK = [self.sb(st, "BK%d" % d, [128, 4, NC4, 2, 64], BF16) for d in range(2)]
            PEb1 = self.sb(st, "PEb", [128, 4, NC4, 64], F32)
            PEb = [PEb1, PEb1]
            tokB1 = self.sb(st, "tokB", [128, 2, 8, 64], BF16)
            tokK1 = self.sb(st, "tokK", [128, 2, 8, 64], BF16)
            tokB, tokK = [tokB1, tokB1], [tokK1, tokK1]
            tokV = self.sb(st, "tokV", [128, 2, 8, 64], BF16)
            NSET = 2
            MRBs = [self.sb(st, "MRBs%d" % i, [64, 8, 64], BF16) for i in range(NSET)]
            Lt0s = [self.sb(st, "Lt0_%d" % i, [64, 8, 64], F32R) for i in range(NSET)]
            Tfins = [self.sb(st, "Tfin%d" % i, [64, 8, 64], BF16) for i in range(NSET)]
            SCk = self.sb(st, "SCk", [128, 2, 8, 64], BF16)
            Lms = [[self.sb(st, "Lm%d_%d" % (i, k), [64, 8, 64], F32R) for i in range(2)] for k in range(NSET)]
            Ltms = [[self.sb(st, "Ltm%d_%d" % (i, k), [64, 8, 64], F32R) for i in range(2)] for k in range(NSET)]
            ILms = [self.sb(st, "ILm_%d" % k, [64, 8, 64], F32R) for k in range(NSET)]
            Ttms = [[self.sb(st, "Ttm%d_%d" % (i, k), [64, 8, 64], F32R) for i in range(2)] for k in range(NSET)]

            ones_bf, oblk, ident = self.ones_bf, self.oblk_bf, self.ident
            tiles = [(0, t0, True, 0) for t0 in range(0, TS, TT)] + [(1, TS, False, 1), (2, TS + TP, False, 1)]
            mS1 = cst[0:64, C_MSI1:C_MSI1 + 128].rearrange("p (q t) -> p q t", q=2)
            mL1 = cst[0:64, C_ML1:C_ML1 + 64]
            id64 = idS[0:64]
            loaded = set()

            def load_x(g0, is_s):
                if g0 in loaded:
                    return
                loaded.add(g0)
                if is_s:
                    src = self.xs[g0:g0 + TT, :].rearrange("(s p) f -> p s f", p=128)
                else:
                    l0 = g0 - TS
                    src = self.xp[l0:l0 + TT, :].rearrange("(s p) f -> p s f", p=128)
                self.dma(xin[:], src, [], [xin])
                if is_s:
                    petv = xT[:].rearrange("p a b -> p (a b)").rearrange("p (s f) -> p s f", s=2)
                    self.dma(petv, self.pe[g0:g0 + TT, :].rearrange("(s p) f -> p s f", p=128), [], [xT], eng="act")

            def front(seq, g0, is_s, mc, nxt=None):
                load_x(g0, is_s)
                if is_s:
                    petv = xT[:].rearrange("p a b -> p (a b)").rearrange("p (s f) -> p s f", s=2)
                    self.tt("pool", xin[:], xin[:], petv, ALU.add, [xin, xT], [xin])
                for j in range(8):
                    p = self.ps()
                    for s in range(2):
                        self.tr(p[:, s * 128:(s + 1) * 128], xin[:, s, j * 128:(j + 1) * 128], ident[:], [xin, ident], [p])
                    self.cp("act", xT[:, j, :], p[:, 0:TT], [p], [xT])
                    self.tt("pool", sq[:, j, :], xT[:, j, :], xT[:, j, :], ALU.mult, [xT], [sq])
                    yield
                self.dma(self.xT_scr[:, :, g0:g0 + TT], xT[:], [xT], [self.xT_scr])
                p = self.ps()
                for j in range(8):
                    self.mm(p[:, 0:TT], ones_bf[:], sq[:, j, :], j == 0, j == 7, [ones_bf, sq], [p])
                self.act(rstd[:], p[:, 0:TT], AF.Sqrt, [p, self.epsT], [rstd], scale=1.0 / D, bias=self.epsT[:, 0:1])
                self.S.op("dve", lambda e: e.reciprocal(out=rstd[:], in_=rstd[:]), [rstd.b], [rstd.b])
                for j in range(8):
                    t = tmpA[j % 2]
                    self.tt("dve", t[:], xT[:, j, :], rstd[:], ALU.mult, [xT, rstd], [t])
                    self.act(hT[:, j, :], t[:], AF.Identity, [t, self.gs1, self.modT], [hT],
                             scale=self.gs1[:, j, mc:mc + 1], bias=self.modT[:, j, mc:mc + 1])
                    yield
                for m in range(23):
                    if m >= self.cutm:
                        break
                    p = self.ps()
                    for kc in range(8):
                        self.mm(p[:, 0:TT], win[:, kc, m * 128:(m + 1) * 128], hT[:, kc, :], kc == 0, kc == 7, [win, hT], [p])
                    pz = p[:, 0:TT]
                    if m < 4:
                        self.cp("act", rT[:, m, :], pz, [p], [rT])
                    elif m < 8:
                        self.cp("act", kT[:, m - 4, :], pz, [p], [kT])
                    elif m < 12:
                        self.cp("act", vT[:, m - 8, :], pz, [p], [vT])
                    elif m == 12:
                        self.act(xw[:], pz, AF.Tanh, [p], [xw])
                    elif m == 13:
                        self.cp("act", xa[:], pz, [p], [xa])
                    elif m == 14:
                        self.act(xg[:], pz, AF.Sigmoid, [p], [xg])
                    elif m < 19:
                        self.cp("act", xbT[:, m - 15, :], pz, [p], [xbT])
                    else:
                        j = m - 19
                        g0_, g1_, g2_ = gtmp
                        self.cp("act", g0_[:], pz, [p], [g0_])
                        self.tt("pool", g1_[:], g0_[:], g0_[:], ALU.mult, [g0_], [g1_])
                        self.tsc("dve", g1_[:], g1_[:], 0.044715, ALU.mult, [g1_], [g1_], 1.0, ALU.add)
                        self.tt("dve", g1_[:], g1_[:], g0_[:], ALU.mult, [g1_, g0_], [g1_])
                        self.act(g2_[:], g1_[:], AF.Sigmoid, [g1_], [g2_], scale=GELU_C)
                        self.tt("pool", gate[:, j, :], g0_[:], g2_[:], ALU.mult, [g0_, g2_], [gate])
                    yield
                self.dma(self.xb_scr[:, :, g0:g0 + TT], xbT[:], [xbT], [self.xb_scr])
                if self.debug and g0 == 0:
                    self.dump("hT", hT[:], [128, 8, TT], BF16, [hT])
                    self.dump("rT", rT[:], [128, 4, TT], F32, [rT])
                    self.dump("vT", vT[:], [128, 4, TT], F32, [vT])
                    self.dump("xbT", xbT[:], [128, 4, TT], F32, [xbT])
                    self.dump("gate", gate[:], [128, 4, TT], BF16, [gate])
                self.dma(self.gate_scr[:, :, g0:g0 + TT], gate[:], [gate], [self.gate_scr])
                for j in range(4):
                    p = self.ps()
                    self.mm(p[:, 0:TT], gup[:, j * 128:(j + 1) * 128], xg[:], True, True, [gup, xg], [p])
                    self.cp("act", gT[:, j, :], p[:, 0:TT], [p], [gT])
                    yield
                self.dma(self.g_scr[:, :, g0:g0 + TT], gT[:], [gT], [self.g_scr])
                for j in range(4):
                    self.tsc("dve", kkn[:, j, :], kT[:, j, :], prm[:, P_KK + j:P_KK + j + 1], ALU.mult, [kT, prm], [kkn])
                    self.tt("pool", sq[:, j, :], kkn[:, j, :], kkn[:, j, :], ALU.mult, [kkn], [sq])
                for j in range(4):
                    p = self.ps()
                    self.mm(p[:, 0:TT], oblk[:], sq[:, j, :], True, True, [oblk, sq], [p])
                    t = tmpA[j % 2]
                    self.act(t[:], p[:, 0:TT], AF.Sqrt, [p], [t])
                    self.tsc("dve", t[:], t[:], 1e-12, ALU.max, [t], [t])
                    self.S.op("dve", lambda e, t=t: e.reciprocal(out=t[:], in_=t[:]), [t.b], [t.b])
                    self.tt("dve", kkn[:, j, :], kkn[:, j, :], t[:], ALU.mult, [kkn, t], [kkn])
                    yield
                for s in range(2):
                    p = self.ps()
                    for j in range(4):
                        self.tr(p[:, j * 128:(j + 1) * 128], vT[:, j, s * 128:(s + 1) * 128], ident[:], [vT, ident], [p])
                    self.cp("act", tokV[:, s, :, :].rearrange("p h k -> p (h k)"), p[:], [p], [tokV])
                c0 = g0 // 64
                for s in range(2):
                    dst = self.vt_scr[c0 + 2 * s:c0 + 2 * s + 2, :, :].rearrange("c s f -> (c s) f")
                    self.dma(dst, tokV[:, s, :, :].rearrange("p h k -> p (h k)"), [tokV], [self.vt_scr])
                if nxt is not None:
                    load_x(nxt[1], nxt[2])
                yield
            def prep(g0, d):
                c0 = g0 // 64
                for j in range(4):
                    p = self.ps()
                    self.mm(p[:, 0:TT], wup[d * 64:(d + 1) * 64, j * 128:(j + 1) * 128], xw[d * 64:(d + 1) * 64, :],
                            True, True, [wup, xw], [p])
                    self.act(sg[:, j, :], p[:, 0:TT], AF.Sigmoid, [p, prm], [sg],
                             bias=prm[:, P_W0 + 4 * d + j:P_W0 + 4 * d + j + 1])
                    if d == 0:
                        self.scan(cs[:, j, :], cst[:, C_RMF:C_RMF + TT], sg[:, j, :], 0.0, [cst, sg], [cs])
                    else:
                        self.scan(cs[:, j, ::-1], cst[:, C_RMB:C_RMB + TT][:, ::-1], sg[:, j, ::-1], 0.0, [cst, sg], [cs])
                    yield
                for j in range(4):
                    p = self.ps()
                    self.mm(p[:, 0:TT], aup[d * 64:(d + 1) * 64, j * 128:(j + 1) * 128], xa[d * 64:(d + 1) * 64, :],
                            True, True, [aup, xa], [p])
                    self.act(ad[:, j, :], p[:, 0:TT], AF.Sigmoid, [p, prm], [ad],
                             bias=prm[:, P_A0 + 4 * d + j:P_A0 + 4 * d + j + 1])
                    yield
                self.tt("dve", sg[:], cs[:], sg[:], ALU.subtract, [cs, sg], [sg])
                self.act(E1[:], cs[:], AF.Exp, [cs], [E1], scale=-LAM)
                self.act(cs[:], cs[:], AF.Exp, [cs], [cs], scale=LAM)
                self.act(sg[:], sg[:], AF.Exp, [sg], [sg], scale=-LAM)
                E2, E3 = cs, sg
                ar5 = AR[d]
                bk5 = BK[d]
                v4 = lambda tl: tl[:].rearrange("p j (c t) -> p j c t", t=64)
                self.stt(ar5[:, :, :, 0, :], v4(kkn), -1.0, v4(E3), ALU.mult, ALU.mult, [kkn, E3], [ar5])
                self.tt("pool", ar5[:, :, :, 1, :], v4(rT), v4(E1), ALU.mult, [rT, E1], [ar5])
                yield
                self.tt("dve", wk1[:], kkn[:], ad[:], ALU.mult, [kkn, ad], [wk1])
                self.tt("dve", wk1[:], wk1[:], E2[:], ALU.mult, [wk1, E2], [wk1])
                self.cp("act", bk5[:, :, :, 0, :], v4(wk1), [wk1], [bk5])
                yield
                for j in range(4):
                    self.tsc("dve", wk2[:, j, :], ad[:, j, :], prm[:, P_KA + j:P_KA + j + 1], ALU.mult, [ad, prm, self.misc], [wk2],
                             self.misc[:, j:j + 1], ALU.add)
                self.tt("dve", wk2[:], wk2[:], kT[:], ALU.mult, [wk2, kT], [wk2])
                if d == 0:
                    self.cp("pool", ksum[:], wk2[:], [wk2], [ksum])
                else:
                    self.tt("pool", ksum[:], ksum[:], wk2[:], ALU.add, [ksum, wk2], [ksum])
                self.tt("dve", wk2[:], wk2[:], E2[:], ALU.mult, [wk2, E2], [wk2])
                self.cp("act", bk5[:, :, :, 1, :], v4(wk2), [wk2], [bk5])
                yield
                te = 63 if d == 0 else 0
                pend_b = v4(E1)[:, :, :, te:te + 1].to_broadcast([128, 4, NC4, 64])
                self.cp("act", PEb[d][:], pend_b, [E1], [PEb[d]])
                self.tt("dve", v4(wk1), v4(wk1), PEb[d][:], ALU.mult, [wk1, PEb[d]], [wk1])
                self.tt("dve", v4(wk2), v4(wk2), PEb[d][:], ALU.mult, [wk2, PEb[d]], [wk2])
                yield
                for (srcw, tokX, scr) in ((wk1, tokB[d], self.bh_scr), (wk2, tokK[d], self.kh_scr)):
                    for s in range(2):
                        p = self.ps()
                        for j in range(4):
                            self.tr(p[:, j * 128:(j + 1) * 128], srcw[:, j, s * 128:(s + 1) * 128], ident[:], [srcw, ident], [p])
                        self.cp("act", tokX[:, s, :, :].rearrange("p h k -> p (h k)"), p[:], [p], [tokX])
                        for cc in range(2):
                            self.dma(scr[c0 + 2 * s + cc, d * 64:(d + 1) * 64, :],
                                     tokX[cc * 64:(cc + 1) * 64, s, :, :].rearrange("p h k -> p (h k)"),
                                     [tokX], [scr])
                        yield
                for cl in range(NC4):
                    c = c0 + cl
                    for hp in range(2):
                        for (q, scr) in ((0, self.art_scr), (1, self.rrt_scr)):
                            dst = scr[c, d * 64:(d + 1) * 64, :].rearrange("k (j hp t) -> k j hp t", hp=2, t=64)[:, :, hp, :]
                            self.dma(dst, ar5[hp * 64:(hp + 1) * 64, :, cl, q, :], [ar5], [scr], eng="sp")
                        dst = self.pend_scr[c, d * 64:(d + 1) * 64, :].rearrange("k (j hp t) -> k j hp t", hp=2, t=64)[:, :, hp, :]
                        self.dma(dst, PEb[d][hp * 64:(hp + 1) * 64, :, cl, :], [PEb[d]], [self.pend_scr], eng="sp")
                yield
            def bonus(g0):
                for j in range(4):
                    self.stt(sq[:, j, :], rT[:, j, :], prm[:, P_RK + j:P_RK + j + 1], ksum[:, j, :], ALU.mult, ALU.mult,
                             [rT, prm, ksum], [sq])
                    p = self.ps()
                    self.mm(p[:, 0:TT], oblk[:], sq[:, j, :], True, True, [oblk, sq], [p])
                    self.tt("dve", bon[:, j, :], p[:, 0:TT], vT[:, j, :], ALU.mult, [p, vT], [bon])
                self.dma(self.bon_scr[:, :, g0:g0 + TT], bon[:], [bon], [self.bon_scr])
                yield
            def chunk_sck(g0):
                c0 = g0 // 64
                for cl in range(NC4):
                    c = c0 + cl
                    for hp in range(2):
                        p = self.ps()
                        for j in range(4):
                            for d in range(2):
                                self.mm(p[d * 64:(d + 1) * 64, j * 128:(j + 1) * 128],
                                        BK[d][hp * 64:(hp + 1) * 64, j, cl, 1, :],
                                        AR[d][hp * 64:(hp + 1) * 64, j, cl, :, :].rearrange("p q t -> p (q t)"),
                                        True, True, [BK[d], AR[d]], [p])
                        self.tt("dve", SCk[:, :, hp::2, :].rearrange("p q h t -> p h q t"),
                                p[:].rearrange("p (h q t) -> p h q t", q=2, t=64),
                                mSI.unsqueeze(1).to_broadcast([128, 4, 2, 64]), ALU.mult, [p, cst], [SCk])
                    self.dma(self.akt_scr[c], SCk[:, 0, :, :].rearrange("p h s -> p (h s)"), [SCk], [self.akt_scr], eng="act")
                    self.dma(self.mrkt_scr[c], SCk[:, 1, :, :].rearrange("p h s -> p (h s)"), [SCk], [self.mrkt_scr], eng="act")
                    yield
            def chunk_d(g0, d, cls, k):
                c0 = g0 // 64
                Lm, Ltm, ILm, Ttm, Lt0, Tfin = Lms[k], Ltms[k], ILms[k], Ttms[k], Lt0s[k], Tfins[k]
                MRB = MRBs[k]
                for cl in cls:
                    c = c0 + cl
                    msk = mSI[0:64] if d == 0 else mS1
                    mskL = mL[0:64] if d == 0 else mL1
                    L0 = Lm[0]
                    for hp in range(2):
                        p = self.ps()
                        for j in range(4):
                            self.mm(p[0:64, j * 128:(j + 1) * 128],
                                    BK[d][hp * 64:(hp + 1) * 64, j, cl, 0, :],
                                    AR[d][hp * 64:(hp + 1) * 64, j, cl, :, :].rearrange("p q t -> p (q t)"),
                                    True, True, [BK[d], AR[d]], [p])
                        p4 = p[0:64, :].rearrange("p (h q t) -> p h q t", q=2, t=64)
                        self.tt("dve", Lt0[:, hp::2, :], p4[:, :, 0, :], msk[:, 0, :].unsqueeze(1).to_broadcast([64, 4, 64]),
                                ALU.mult, [p, cst], [Lt0])
                        self.tt("dve", MRB[:, hp::2, :], p4[:, :, 1, :], msk[:, 1, :].unsqueeze(1).to_broadcast([64, 4, 64]),
                                ALU.mult, [p, cst], [MRB])
                        p2 = self.ps()
                        for j in range(4):
                            self.mm(p2[0:64, j * 64:(j + 1) * 64],
                                    AR[d][hp * 64:(hp + 1) * 64, j, cl, 0, :], BK[d][hp * 64:(hp + 1) * 64, j, cl, 0, :],
                                    True, True, [AR[d], BK[d]], [p2])
                        self.tt("dve", L0[:, hp::2, :], p2[0:64, 0:256].rearrange("p (h s) -> p h s", s=64),
                                mskL.unsqueeze(1).to_broadcast([64, 4, 64]), ALU.mult, [p2, cst], [L0])
                    self.dma(self.mrbt_scr[c, d * 64:(d + 1) * 64, :], MRB[:].rearrange("p h s -> p (h s)"),
                             [MRB], [self.mrbt_scr], eng="act")
                    yield
                    T0 = Ttm[0]
                    self.tt("pool", T0[:], Lt0[:].bitcast(F32), id64.unsqueeze(1).to_broadcast([64, 8, 64]), ALU.add,
                            [Lt0, cst], [T0])
                    L_prev, Tt_prev, Lt_prev = L0, T0, Lt0
                    for lev in range(1, 6):
                        L_new, Lt_new, Tt_new = Lm[lev % 2], Ltm[lev % 2], Ttm[lev % 2]
                        pA = self.ps()
                        for h in range(8):
                            self.mm(pA[0:64, h * 64:(h + 1) * 64], Lt_prev[:, h, :], L_prev[:, h, :], True, True,
                                    [Lt_prev, L_prev], [pA])
                        if lev < 5:
                            pB = self.ps()
                            for h in range(8):
                                self.mm(pB[0:64, h * 64:(h + 1) * 64], L_prev[:, h, :], Lt_prev[:, h, :], True, True,
                                        [Lt_prev, L_prev], [pB])
                        self.tt("dve", ILm[:], pA[0:64, :].rearrange("p (h s) -> p h s", s=64),
                                id64.unsqueeze(1).to_broadcast([64, 8, 64]), ALU.add, [pA, cst], [ILm])
                        if lev < 5:
                            self.cp("dve", L_new[:].rearrange("p h s -> p (h s)"), pA[0:64, :], [pA], [L_new])
                            self.cp("act", Lt_new[:].rearrange("p h s -> p (h s)"), pB[0:64, :], [pB], [Lt_new])
                        yield
                        pC = self.ps()
                        for h in range(8):
                            self.mm(pC[0:64, h * 64:(h + 1) * 64], ILm[:, h, :], Tt_prev[:, h, :], True, True,
                                    [ILm, Tt_prev], [pC])
                        if lev < 5:
                            self.cp("act", Tt_new[:].rearrange("p h s -> p (h s)"), pC[0:64, :], [pC], [Tt_new])
                        else:
                            self.cp("act", Tfin[:].rearrange("p h s -> p (h s)"), pC[0:64, :], [pC], [Tfin])
                        L_prev, Tt_prev, Lt_prev = L_new, Tt_new, Lt_new
                        yield
                    self.dma(self.ttt_scr[c, d * 64:(d + 1) * 64, :], Tfin[:].rearrange("p h s -> p (h s)"),
                             [Tfin], [self.ttt_scr], eng="act")


            def run_all(*gens, w=None):
                gens = list(gens)
                wts = {id(g): (w[i] if w else 1) for i, g in enumerate(gens)}
                while gens:
                    for g in list(gens):
                        for _ in range(wts[id(g)]):
                            try:
                                next(g)
                            except StopIteration:
                                gens.remove(g)
                                break

            def seq_(*gens):
                for g in gens:
                    yield from g

            prev = None
            tl_ = tiles[:self.ktiles]
            for ti_, (seq, g0, is_s, mc) in enumerate(tl_):
                nxt = tl_[ti_ + 1] if ti_ + 1 < len(tl_) else None
                if prev is None:
                    run_all(front(seq, g0, is_s, mc, nxt))
                else:
                    run_all(seq_(chunk_d(prev, 1, [0, 1], 0), chunk_sck(prev)), chunk_d(prev, 1, [2, 3], 1), front(seq, g0, is_s, mc, nxt), w=[1, 1, 2])
                run_all(prep(g0, 0))
                run_all(chunk_d(g0, 0, [0, 1], 0), chunk_d(g0, 0, [2, 3], 1), prep(g0, 1))
                run_all(bonus(g0))
                prev = g0
            run_all(seq_(chunk_d(prev, 1, [0, 1], 0), chunk_sck(prev)), chunk_d(prev, 1, [2, 3], 1))

    def phaseB(self):
        with contextlib.ExitStack() as st:
            side = [self.gen_c0(st), self.phaseB_lru(st)]
            post = self.phaseB_post(st)
            next(post)
            done = np.zeros((2, NCH), bool)
            posted = [False] * (NTOK // 128)
            si = 0
            for info in self.phaseB_chain(st):
                for (d, c) in info:
                    done[d, c] = True
                for _ in range(2):
                    if side:
                        g = side[si % len(side)]
                        si += 1
                        try:
                            next(g)
                        except StopIteration:
                            side.remove(g)
                for b in range(NTOK // 128):
                    if not posted[b] and done[:, 2 * b:2 * b + 2].all():
                        posted[b] = True
                        post.send(b)
            for g in side:
                for _ in g:
                    pass
            for b in range(NTOK // 128):
                if not posted[b]:
                    post.send(b)
            if self.debug:
                self.S.barrier()
                self.dump("yscr", self.y_scr[:], [128, 8, NTOK], BF16, [self.y_scr])

    def gen_c0(self, st):
        stg = [self.sb(st, "stgC%d" % i, [128, 8, 512], F32) for i in range(2)]
        wb = [self.sb(st, "wbC%d" % i, [128, 8, 512], BF16) for i in range(2)]
        w1src = self.w1[:].rearrange("(kc p) n -> p kc n", p=128)
        for blk in range(8):
            s, o = stg[blk % 2], wb[blk % 2]
            self.dma(s[:], w1src[:, :, blk * 512:(blk + 1) * 512], [], [s])
            self.cp("pool", o[:], s[:], [s], [o])
            self.dma(self.w1_scr[blk], o[:], [o], [self.w1_scr], eng="act")
            yield
        w2src = self.w2[:].rearrange("(fc p) n -> p fc n", p=128)
        for m in range(8):
            s, o = stg[m % 2], wb[m % 2]
            s4 = s[:].rearrange("p k (a b) -> p (k a) b", b=128)
            o4 = o[:].rearrange("p k (a b) -> p (k a) b", b=128)
            self.dma(s4, w2src[:, :, m * 128:(m + 1) * 128], [], [s])
            self.cp("pool", o[:], s[:], [s], [o])
            self.dma(self.w2_scr[m], o4, [o], [self.w2_scr], eng="act")
            yield

    def phaseB_lru(self, st):
        prm, cst, misc = self.prm_t, self.cst_t, self.misc
        if True:
            wbd32 = self.sb(st, "wbd32", [128, 16, 128], F32)
            wbd = self.sb(st, "wbd", [128, 16, 128], BF16)
            self.memset("pool", wbd32[:], 0.0, [wbd32])
            self.S.barrier_on(wbd32)
            wbd32.b.multi = True
            for gi, src in enumerate((self.lwa, self.lwx)):
                for d in range(2):
                    for j in range(4):
                        for hb in range(2):
                            self.dma(wbd32[hb * 64:(hb + 1) * 64, (gi * 2 + d) * 4 + j, hb * 64:(hb + 1) * 64],
                                     src[d, 2 * j + hb], [], [wbd32])
            self.cp("dve", wbd[:], wbd32[:], [wbd32], [wbd])
            TM = TS
            xbp = self.sb(st, "xbp", [128, TM + 4], F32)
            xc = self.sb(st, "xc", [128, TM], F32)
            xcb = self.sb(st, "xcb", [128, TM], BF16)
            gt = self.sb(st, "gt_l", [128, TM], BF16)
            a_t = self.sb(st, "a_t", [128, TM], F32)
            bx_t = self.sb(st, "bx_t", [128, TM], F32)
            s_t = self.sb(st, "s_t", [128, TM], F32)
            hs = [self.sb(st, "hs%d" % d, [128, TM], F32) for d in range(2)]
            yb = self.sb(st, "yb", [128, TM], BF16)
            for (seq, g0, T) in ((0, 0, TS), (1, TS, TP), (2, TS + TP, TP)):
                for j in range(4):
                    self.memset("pool", xbp[:, 0:2], 0.0, [xbp])
                    self.memset("pool", xbp[:, T + 2:T + 4], 0.0, [xbp])
                    self.dma(xbp[:, 2:T + 2], self.xb_scr[:, j, g0:g0 + T], [self.xb_scr], [xbp])
                    self.dma(gt[:, 0:T], self.gate_scr[:, j, g0:g0 + T], [self.gate_scr], [gt], eng="act")
                    cw = lambda i: prm[:, P_CW + 4 * i + j:P_CW + 4 * i + j + 1]
                    self.act(xc[:, 0:T], xbp[:, 0:T], AF.Identity, [xbp, prm], [xc], scale=cw(0), bias=prm[:, P_CB + j:P_CB + j + 1])
                    for i in range(1, 4):
                        self.stt(xc[:, 0:T], xbp[:, i:i + T], cw(i), xc[:, 0:T], ALU.mult, ALU.add, [xbp, prm, xc], [xc])
                    self.cp("pool", xcb[:, 0:T], xc[:, 0:T], [xc], [xcb])
                    yield
                    for d in range(2):
                        for t0 in range(0, T, 512):
                            tw = min(512, T - t0)
                            p = self.ps()
                            self.mm(p[:, 0:tw], wbd[:, (0 * 2 + d) * 4 + j, :], xcb[:, t0:t0 + tw], True, True, [wbd, xcb], [p])
                            self.act(s_t[:, t0:t0 + tw], p[:, 0:tw], AF.Sigmoid, [p, prm], [s_t],
                                     bias=prm[:, P_BA + 4 * d + j:P_BA + 4 * d + j + 1])
                            p2 = self.ps()
                            self.mm(p2[:, 0:tw], wbd[:, (1 * 2 + d) * 4 + j, :], xcb[:, t0:t0 + tw], True, True, [wbd, xcb], [p2])
                            self.act(bx_t[:, t0:t0 + tw], p2[:, 0:tw], AF.Sigmoid, [p2, prm], [bx_t],
                                     bias=prm[:, P_BX + 4 * d + j:P_BX + 4 * d + j + 1])
                        col = 4 + d * 4 + j
                        self.act(a_t[:, 0:T], s_t[:, 0:T], AF.Exp, [s_t, misc], [a_t], scale=misc[:, col:col + 1])
                        self.act(s_t[:, 0:T], s_t[:, 0:T], AF.Exp, [s_t, misc], [s_t], scale=misc[:, col + 8:col + 9])
                        self.act(s_t[:, 0:T], s_t[:, 0:T], AF.Sqrt, [s_t], [s_t], scale=-1.0, bias=1.0)
                        self.tt("pool", bx_t[:, 0:T], bx_t[:, 0:T], xc[:, 0:T], ALU.mult, [bx_t, xc], [bx_t])
                        self.tt("dve", bx_t[:, 0:T], bx_t[:, 0:T], s_t[:, 0:T], ALU.mult, [bx_t, s_t], [bx_t])
                        h = hs[d]
                        if seq == 0:
                            init = prm[:, P_H0 + 4 * d + j:P_H0 + 4 * d + j + 1]
                        else:
                            init = 0.0
                        if d == 0:
                            self.scan(h[:, 0:T], a_t[:, 0:T], bx_t[:, 0:T], init, [a_t, bx_t, prm], [h])
                        else:
                            self.scan(h[:, 0:T][:, ::-1], a_t[:, 0:T][:, ::-1], bx_t[:, 0:T][:, ::-1], init, [a_t, bx_t, prm], [h])
                        if seq > 0:
                            col_o = j * 4 + (seq - 1) * 2 + d
                            te = T - 1 if d == 0 else 0
                            self.cp("pool", self.stl_t[:, col_o:col_o + 1], h[:, te:te + 1], [h], [self.stl_t])
                        yield
                    self.tt("pool", hs[0][:, 0:T], hs[0][:, 0:T], hs[1][:, 0:T], ALU.add, [hs[0], hs[1]], [hs[0]])
                    self.tt("dve", yb[:, 0:T], hs[0][:, 0:T], gt[:, 0:T], ALU.mult, [hs[0], gt], [yb])
                    self.dma(self.y_scr[:, 4 + j, g0:g0 + T], yb[:, 0:T], [yb], [self.y_scr])
            self.dma(self.stl_o[:], self.stl_t[:], [self.stl_t], [self.stl_o])

    def phaseB_chain(self, st):
        if True:
            NB = 3
            def ring(name, dt=BF16):
                return [self.sb2(st, "%s%d" % (name, i), [128, 512], dt) for i in range(NB)]
            art, rrt, ttt, akt, mrbt, mrkt, bh, kh, vt = [ring(n) for n in
                                                          ("c_art", "c_rrt", "c_ttt", "c_akt", "c_mrbt", "c_mrkt", "c_bh", "c_kh", "c_vt")]
            pend = ring("c_pend", F32)
            Hf = self.sb2(st, "Hf", [128, 512], F32)
            Hb = self.sb2(st, "Hb", [128, 512], BF16)
            Zs = self.sb2(st, "Zs", [128, 512], BF16)
            Us = self.sb2(st, "Us", [128, 512], BF16)
            Yt = [self.sb2(st, "Yt%d" % i, [128, 512], F32) for i in range(2)]
            tmpH = self.sb2(st, "tmpH", [128, 512], F32)
            hs_ = lambda h: slice(h * 64, (h + 1) * 64)
            steps = []
            for (seq, cbase, n) in ((0, 0, 32), (1, 32, 4), (2, 36, 4)):
                for i in range(n):
                    steps.append((seq, cbase, n, i))

            def loads(k):
                seq, cbase, n, i = steps[k]
                r = k % NB
                for d in range(2):
                    sl = slice(d * 64, (d + 1) * 64)
                    c = cbase + i if d == 0 else cbase + n - 1 - i
                    for (tl, scr) in ((art, self.art_scr), (rrt, self.rrt_scr), (ttt, self.ttt_scr), (akt, self.akt_scr),
                                      (mrbt, self.mrbt_scr), (mrkt, self.mrkt_scr), (bh, self.bh_scr), (kh, self.kh_scr),
                                      (pend, self.pend_scr)):
                        self.dma(tl[r][d][sl, :], scr[c, sl, :], [scr], [tl[r][d]], eng="sp")
                    self.dma(vt[r][d][sl, :], self.vt_scr[c], [self.vt_scr], [vt[r][d]], eng="sp")

            loads(0)
            for k in range(len(steps)):
                seq, cbase, n, i = steps[k]
                step = k + 1
                r = k % NB
                if k + 1 < len(steps):
                    loads(k + 1)
                if i == 0:
                    for d in range(2):
                        sl = slice(d * 64, (d + 1) * 64)
                        if seq == 0:
                            self.dma(Hf[d][sl, :], self.h0r[sl, :], [], [Hf[d]], eng="act")
                        else:
                            self.memset("dve", Hf[d][sl, :], 0.0, [Hf[d]])
                        self.cp("dve", Hb[d][sl, :], Hf[d][sl, :], [Hf[d]], [Hb[d]])
                if True:
                    ctx = []
                    for d in range(2):
                        sl = slice(d * 64, (d + 1) * 64)
                        c = cbase + i if d == 0 else cbase + n - 1 - i
                        ops = tuple(x[r][d] for x in (art, rrt, ttt, akt, mrbt, mrkt, bh, kh, vt, pend))
                        ctx.append((d, sl, c, ops))
                    pZs = {}
                    for (d, sl, c, (A_, R_, T_, AK_, MRB_, MRK_, B_, K_, V_, PE_)) in ctx:
                        H_, Z_ = Hb[d], Zs[d]
                        pZ = self.ps()
                        for h in range(8):
                            self.mm(pZ[sl, hs_(h)], A_[sl, hs_(h)], H_[sl, hs_(h)], True, False, [A_, H_], [pZ])
                            self.mm(pZ[sl, hs_(h)], AK_[sl, hs_(h)], V_[sl, hs_(h)], False, True, [AK_, V_], [pZ])
                        pZs[d] = pZ
                    pYs = {}
                    for (d, sl, c, (A_, R_, T_, AK_, MRB_, MRK_, B_, K_, V_, PE_)) in ctx:
                        H_ = Hb[d]
                        pY = self.ps()
                        pYs[d] = pY
                    for (d, sl, c, ops) in ctx:
                        self.cp("act", Zs[d][sl, :], pZs[d][sl, :], [pZs[d]], [Zs[d]])
                    pUs = {}
                    for (d, sl, c, (A_, R_, T_, AK_, MRB_, MRK_, B_, K_, V_, PE_)) in ctx:
                        Z_ = Zs[d]
                        pU = self.ps()
                        for h in range(8):
                            self.mm(pU[sl, hs_(h)], T_[sl, hs_(h)], Z_[sl, hs_(h)], True, True, [T_, Z_], [pU])
                        pUs[d] = pU
                    for (d, sl, c, ops) in ctx:
                        self.cp("act", Us[d][sl, :], pUs[d][sl, :], [pUs[d]], [Us[d]])
                    pHs = {}
                    for (d, sl, c, (A_, R_, T_, AK_, MRB_, MRK_, B_, K_, V_, PE_)) in ctx:
                        U_ = Us[d]
                        self.tt("dve", tmpH[d][sl, :], Hf[d][sl, :], PE_[sl, :], ALU.mult, [Hf[d], PE_], [tmpH[d]])
                        pH = self.ps()
                        for h in range(8):
                            self.mm(pH[sl, hs_(h)], B_[sl, hs_(h)], U_[sl, hs_(h)], True, False, [B_, U_], [pH])
                            self.mm(pH[sl, hs_(h)], K_[sl, hs_(h)], V_[sl, hs_(h)], False, True, [K_, V_], [pH])
                        pHs[d] = pH
                    for (d, sl, c, (A_, R_, T_, AK_, MRB_, MRK_, B_, K_, V_, PE_)) in ctx:
                        H_, U_ = Hb[d], Us[d]
                        pY = pYs[d]
                        for h in range(8):
                            self.mm(pY[sl, hs_(h)], R_[sl, hs_(h)], H_[sl, hs_(h)], True, False, [R_, H_], [pY])
                            self.mm(pY[sl, hs_(h)], MRB_[sl, hs_(h)], U_[sl, hs_(h)], False, False, [MRB_, U_], [pY])
                            self.mm(pY[sl, hs_(h)], MRK_[sl, hs_(h)], V_[sl, hs_(h)], False, True, [MRK_, V_], [pY])
                    for (d, sl, c, ops) in ctx:
                        self.tt("dve", Hf[d][sl, :], tmpH[d][sl, :], pHs[d][sl, :], ALU.add, [tmpH[d], pHs[d]], [Hf[d]])
                        self.cp("dve", Hb[d][sl, :], Hf[d][sl, :], [Hf[d]], [Hb[d]])
                    for (d, sl, c, ops) in ctx:
                        y = Yt[step % 2][d]
                        self.cp("act", y[sl, :], pYs[d][sl, :], [pYs[d]], [y])
                        self.dma(self.ytok_scr[d, c * 64:(c + 1) * 64, :], y[sl, :], [y], [self.ytok_scr], eng="act")
                if seq > 0 and i == n - 1:
                    self.dma(self.str_o[seq - 1], Hf[0][:], [Hf[0], Hf[1]], [self.str_o], eng="act")
                yield [(0, cbase + i), (1, cbase + n - 1 - i)]

    def phaseB_post(self, st):
        prm, cst = self.prm_t, self.cst_t
        ident = self.ident
        if True:
            yf = [self.sb(st, "yf%d" % i, [128, 512], F32) for i in range(2)]
            yb2 = [self.sb(st, "yb2%d" % i, [128, 512], F32) for i in range(2)]
            cen = self.sb(st, "cen", [128, 8, 64], F32)
            sqv = self.sb(st, "sqv", [128, 8, 64], F32)
            mean = self.sb(st, "mean", [128, 8], F32)
            var = self.sb(st, "var", [128, 8], F32)
            gl = [self.sb(st, "gl%d" % i, [128, 4, 128], BF16) for i in range(2)]
            bl = [self.sb(st, "bl%d" % i, [128, 4, 128], BF16) for i in range(2)]
            ynT = self.sb(st, "ynT", [128, 4, 128], F32)
            yo = [self.sb(st, "yo%d" % i, [128, 4, 128], BF16) for i in range(2)]
            it = -1
            blk = yield
            while True:
                it += 1
                g0 = blk * 128
                a, b = yf[it % 2], yb2[it % 2]
                g_, b_ = gl[it % 2], bl[it % 2]
                o = yo[it % 2]
                self.dma(a[:], self.ytok_scr[0, g0:g0 + 128, :], [self.ytok_scr], [a])
                self.dma(b[:], self.ytok_scr[1, g0:g0 + 128, :], [self.ytok_scr], [b], eng="act")
                self.dma(g_[:], self.g_scr[:, :, g0:g0 + 128], [self.g_scr], [g_])
                self.dma(b_[:], self.bon_scr[:, :, g0:g0 + 128], [self.bon_scr], [b_], eng="act")
                a3 = a[:].rearrange("p (h v) -> p h v", v=64)
                self.tt("pool", a[:], a[:], b[:], ALU.add, [a, b], [a])
                self.S.op("dve", lambda e, a3=a3: e.tensor_reduce(out=mean[:], in_=a3, op=ALU.add, axis=mybir.AxisListType.X),
                          [a.b], [mean.b])
                self.tsc("dve", mean[:], mean[:], 1.0 / 64, ALU.mult, [mean], [mean])
                self.tt("dve", cen[:], a3, mean[:].unsqueeze(2).to_broadcast([128, 8, 64]), ALU.subtract, [a, mean], [cen])
                self.tt("pool", sqv[:], cen[:], cen[:], ALU.mult, [cen], [sqv])
                self.S.op("dve", lambda e: e.tensor_reduce(out=var[:], in_=sqv[:], op=ALU.add, axis=mybir.AxisListType.X),
                          [sqv.b], [var.b])
                self.act(var[:], var[:], AF.Sqrt, [var, self.epsT], [var], scale=1.0 / 64, bias=self.epsT[:, 1:2])
                self.S.op("dve", lambda e: e.reciprocal(out=var[:], in_=var[:]), [var.b], [var.b])
                self.tt("dve", cen[:], cen[:], var[:].unsqueeze(2).to_broadcast([128, 8, 64]), ALU.mult, [cen, var], [cen])
                p = self.ps()
                cen2 = cen[:].rearrange("p h v -> p (h v)")
                for j in range(4):
                    self.tr(p[:, j * 128:(j + 1) * 128], cen2[:, j * 128:(j + 1) * 128], ident[:], [cen, ident], [p])
                for j in range(4):
                    self.act(ynT[:, j, :], p[:, j * 128:(j + 1) * 128], AF.Identity, [p, prm], [ynT],
                             scale=prm[:, P_LNG + j:P_LNG + j + 1], bias=prm[:, P_LNB + j:P_LNB + j + 1])
                self.tt("pool", ynT[:], ynT[:], b_[:], ALU.add, [ynT, b_], [ynT])
                self.tt("dve", o[:], ynT[:], g_[:], ALU.mult, [ynT, g_], [o])
                self.dma(self.y_scr[:, 0:4, g0:g0 + 128], o[:], [o], [self.y_scr])
                blk = yield

    def phaseC(self):
        prm, cst = self.prm_t, self.cst_t
        ident, ones_bf = self.ident, self.ones_bf
        with contextlib.ExitStack() as st:
            pass
        import os
        kcc = int(os.environ.get("KCC", "99"))
        if kcc == 0:
            return
        with contextlib.ExitStack() as st:
            wout = self.sb(st, "wout", [128, 8, D], BF16)
            wsrc = self.w_out[:].rearrange("(kc p) n -> p kc n", p=128)
            with contextlib.ExitStack() as st2:
                stg = [self.sb(st2, "stgD%d" % i, [128, 8, 256], F32) for i in range(2)]
                for cb in range(4):
                    s = stg[cb % 2]
                    self.dma(s[:], wsrc[:, :, cb * 256:(cb + 1) * 256], [], [s])
                    self.cp("act", wout[:, :, cb * 256:(cb + 1) * 256], s[:], [s], [wout])
            self.S.barrier()
            yT = self.sb(st, "yT_c", [128, 8, TC], BF16)
            o1T = self.sb(st, "o1T", [128, 8, TC], F32)
            sq1 = self.sb(st, "sq1_c", [128, 8, TC], BF16)
            oT = self.sb(st, "oT", [128, 8, TC], F32)
            sq = self.sb(st, "sq_c", [128, 8, TC], BF16)
            xTs = [self.sb(st, "xT_c%d" % i, [128, 8, TC], F32) for i in range(2)]
            h2 = self.sb(st, "h2", [128, 8, TC], BF16)
            f = self.sb(st, "f_c", [128, 32, TC], BF16)
            otoks = [self.sb(st, "otok%d" % i, [128, 2, D], F32) for i in range(1)]
            rstd = self.sb(st, "rstd_c", [128, TC], F32)
            tmp = [self.sb(st, "tmpC%d" % i, [128, TC], F32) for i in range(2)]
            NW = 3
            w1r = [self.sb(st, "w1r%d" % i, [128, 8, 512], BF16) for i in range(NW)]
            w2r = [self.sb(st, "w2r%d" % i, [128, 32, 128], BF16) for i in range(2)]
            ntile = min(NTOK // TC, kcc)
            wseq = []
            for ti_ in range(ntile):
                wseq += [("w1", b_) for b_ in range(8)] + [("w2", b_) for b_ in range(8)]
            wstate = {"issued": 0, "w1": 0, "w2": 0}
            wbuf = {}

            def issue_upto(n):
                while wstate["issued"] < min(n, len(wseq)):
                    k_ = wstate["issued"]
                    kind, b_ = wseq[k_]
                    ring = w1r if kind == "w1" else w2r
                    buf = ring[wstate[kind] % len(ring)]
                    wstate[kind] += 1
                    scr = self.w1_scr if kind == "w1" else self.w2_scr
                    self.dma(buf[:], scr[b_], [scr], [buf], eng="sp" if k_ % 2 == 0 else "act")
                    wbuf[k_] = buf
                    wstate["issued"] += 1

            def rms(sqt, R):
                p = self.ps()
                for j in range(8):
                    self.mm(p[:], ones_bf[:], sqt[:, j, :], j == 0, j == 7, [ones_bf, sqt], [p])
                self.act(rstd[:], p[:], AF.Sqrt, [p, self.epsT], [rstd], scale=1.0 / D, bias=self.epsT[:, 0:1])
                self.S.op("dve", lambda e: e.reciprocal(out=rstd[:], in_=rstd[:]), [rstd.b], [rstd.b])

            def resid(gg, mc, oT, xT):
                for j in range(8):
                    t = tmp[j % 2]
                    self.tt("dve", t[:], oT[:, j, :], rstd[:], ALU.mult, [oT, rstd], [t])
                    self.stt(xT[:, j, :], t[:], gg[:, j, mc:mc + 1], xT[:, j, :], ALU.mult, ALU.add, [t, gg, xT], [xT])

            def head(tj):
                gj = tj * TC
                mcj = 0 if gj < TS else 1
                xT = xTs[tj % 2]
                self.dma(xT[:], self.xT_scr[:, :, gj:gj + TC], [self.xT_scr], [xT], eng="act")
                yield
                rms(sq1, None)
                yield
                for j in range(8):
                    t = tmp[j % 2]
                    self.tt("dve", t[:], o1T[:, j, :], rstd[:], ALU.mult, [o1T, rstd], [t])
                    self.stt(xT[:, j, :], t[:], self.gg1[:, j, mcj:mcj + 1], xT[:, j, :], ALU.mult, ALU.add, [t, self.gg1, xT], [xT])
                    self.act(sq1[:, j, :], xT[:, j, :], AF.Square, [xT], [sq1])
                    if j % 2 == 1:
                        yield
                rms(sq1, None)
                yield
                for j in range(8):
                    t = tmp[j % 2]
                    self.tt("dve", t[:], xT[:, j, :], rstd[:], ALU.mult, [xT, rstd], [t])
                    self.act(h2[:, j, :], t[:], AF.Identity, [t, self.gs2, self.modT], [h2],
                             scale=self.gs2[:, j, mcj:mcj + 1], bias=self.modT[:, 24 + j, mcj:mcj + 1])
                    if j % 2 == 1:
                        yield

            def wout_stage(tj):
                gj = tj * TC
                self.dma(yT[:], self.y_scr[:, :, gj:gj + TC], [self.y_scr], [yT])
                for m in range(8):
                    p = self.ps()
                    for kc in range(8):
                        self.mm(p[:], wout[:, kc, m * 128:(m + 1) * 128], yT[:, kc, :], kc == 0, kc == 7, [wout, yT], [p])
                    self.cp("act", o1T[:, m, :], p[:], [p], [o1T])
                    self.act(sq1[:, m, :], p[:], AF.Square, [p], [sq1])

            def tail(g0, mc, xT):
                rms(sq, None)
                yield
                resid(self.gg2, mc, oT, xT)
                yield
                for hh in range(2):
                    otok = otoks[0]
                    for s2 in range(2):
                        s_ = hh * 2 + s2
                        for half in range(2):
                            p = self.ps()
                            for jj in range(4):
                                j = half * 4 + jj
                                self.tr(p[:, jj * 128:(jj + 1) * 128], xT[:, j, s_ * 128:(s_ + 1) * 128], ident[:], [xT, ident], [p])
                            self.cp("act" if half == 0 else "dve", otok[:, s2, half * 512:(half + 1) * 512], p[:], [p], [otok])
                            yield
                    if g0 < TS:
                        dst = self.ys[g0 + hh * 256:g0 + (hh + 1) * 256, :].rearrange("(s p) f -> p s f", p=128)
                        self.dma(dst, otok[:], [otok], [self.ys])
                    else:
                        dst = self.yp[hh * 256:(hh + 1) * 256, :].rearrange("(s p) f -> p s f", p=128)
                        self.dma(dst, otok[:], [otok], [self.yp])

            tg = None
            wi = 0
            for ti in range(NTOK // TC):
                if ti >= kcc:
                    break
                g0 = ti * TC
                mc = 0 if g0 < TS else 1
                xT = xTs[ti % 2]
                if ti == 0:
                    wout_stage(0)
                    for _ in head(0):
                        pass
                issue_upto(ti * 16 + 3)
                for blk in range(8):
                    if tg is not None:
                        for _ in range(2):
                            try:
                                next(tg)
                            except StopIteration:
                                tg = None
                                break
                    issue_upto(ti * 16 + blk + 3)
                    w = wbuf[ti * 16 + blk]
                    for c4 in range(4):
                        fc = blk * 4 + c4
                        p = self.ps()
                        for kc in range(8):
                            self.mm(p[:], w[:, kc, c4 * 128:(c4 + 1) * 128], h2[:, kc, :], kc == 0, kc == 7, [w, h2], [p])
                        t = tmp[fc % 2]
                        self.act(t[:], p[:], AF.Relu, [p], [t])
                        self.tt("pool" if fc % 2 == 0 else "dve", f[:, fc, :], t[:], t[:], ALU.mult, [t], [f])
                if tg is not None:
                    for _ in tg:
                        pass
                    tg = None
                hg = None
                if ti + 1 < ntile:
                    wout_stage(ti + 1)
                    hg = head(ti + 1)
                for m in range(8):
                    if hg is not None:
                        for _ in range(2):
                            try:
                                next(hg)
                            except StopIteration:
                                hg = None
                                break
                    issue_upto(ti * 16 + 8 + m + 2)
                    w = wbuf[ti * 16 + 8 + m]
                    p = self.ps()
                    for fc in range(32):
                        self.mm(p[:], w[:, fc, :], f[:, fc, :], fc == 0, fc == 31, [w, f], [p])
                    self.cp("act", oT[:, m, :], p[:], [p], [oT])
                    self.act(sq[:, m, :], p[:], AF.Square, [p], [sq])
                if hg is not None:
                    for _ in hg:
                        pass
                tg = tail(g0, mc, xT)
                if ti + 1 >= ntile:
                    for _ in tg:
                        pass
                    tg = None


def _fm(v):
    v = np.asarray(v, np.float32).reshape(-1, 128)
    return np.ascontiguousarray(v.T)


def _pos_embed():
    def sincos(pos, dim):
        omega = (1.0 / (10000.0 ** (np.arange(dim // 2, dtype=np.float32) / np.float32(dim // 2)))).astype(np.float32)
        ang = pos.astype(np.float32)[:, None] * omega[None, :]
        return np.concatenate([np.sin(ang), np.cos(ang)], axis=-1).astype(np.float32)
    rows = TS // 64
    half = D // 2
    e_row = sincos(np.arange(rows), half)
    e_col = sincos(np.arange(64), half)
    emb = np.concatenate([np.broadcast_to(e_row[:, None, :], (rows, 64, half)),
                          np.broadcast_to(e_col[None, :, :], (rows, 64, half))], axis=-1)
    return np.ascontiguousarray(emb.reshape(rows * 64, D).astype(np.float32))


def _consts():
    c = np.zeros((128, NCST), np.float32)
    c[:, C_ID:C_ID + 128] = np.eye(128, dtype=np.float32)
    ob = np.zeros((128, 128), np.float32)
    ob[:64, :64] = 1.0
    ob[64:, 64:] = 1.0
    c[:, C_OB:C_OB + 128] = ob
    s = np.arange(64)[:, None]
    t = np.arange(64)[None, :]
    msi = np.zeros((128, 2, 64), np.float32)
    msi[:64, 0] = (s < t)
    msi[:64, 1] = (s <= t)
    msi[64:, 0] = (s > t)
    msi[64:, 1] = (s >= t)
    c[:, C_MSI:C_MSI + 128] = msi.reshape(128, 128)
    ml = np.zeros((128, 64), np.float32)
    ml[:64] = (t < s)
    ml[64:] = (t > s)
    c[:, C_ML:C_ML + 64] = ml
    ids = np.zeros((128, 64), np.float32)
    ids[:64] = np.eye(64)
    ids[64:] = np.eye(64)
    c[:, C_IDS:C_IDS + 64] = ids
    c[:64, C_MSI1:C_MSI1 + 128] = msi[64:].reshape(64, 128)
    c[:64, C_ML1:C_ML1 + 64] = ml[64:]
    tt_ = np.arange(TT)
    c[:, C_RMF:C_RMF + TT] = (tt_ % 64 != 0).astype(np.float32)[None, :]
    c[:, C_RMB:C_RMB + TT] = (tt_ % 64 != 63).astype(np.float32)[None, :]
    return c


_NC_CACHE = {}


def kernel(x_prompt, x_sample, c, state_rwkv, state_lru, c_ctx, w_mod, b_mod,
           g_pre_mix, g_post_mix, g_pre_mlp, g_post_mlp, w_in,
           rwkv_w0, rwkv_w_up, rwkv_a0, rwkv_a_up, rwkv_g_up, rwkv_k_k, rwkv_k_a, rwkv_r_k,
           rwkv_lnx_g, rwkv_lnx_b, lru_conv_w, lru_conv_b, lru_wa, lru_ba, lru_wx, lru_bx,
           lru_lambda, w_out, w_mlp1, w_mlp2, _debug=False):
    f = lambda a: np.ascontiguousarray(np.asarray(a, np.float32))
    x_prompt, x_sample, c, state_rwkv, state_lru, c_ctx = map(f, (x_prompt, x_sample, c, state_rwkv, state_lru, c_ctx))
    nc = K(debug=_debug).build()
    pe = _pos_embed()
    cst = _consts()
    shared = {
        "pe": pe, "cst": cst,
        "w_mod": f(w_mod[0]), "w_in": f(w_in[0]), "w_out": f(w_out[0]), "w1": f(w_mlp1[0]), "w2": f(w_mlp2[0]),
        "wup": f(rwkv_w_up[0]).reshape(128, 512), "aup": f(rwkv_a_up[0]).reshape(128, 512), "gup": f(rwkv_g_up[0]),
        "lwa": f(lru_wa[0]), "lwx": f(lru_wx[0]),
    }
    prm0 = np.zeros((128, NPRM), np.float32)
    prm0[:, P_GPRE:P_GPRE + 8] = _fm(g_pre_mix[0])
    prm0[:, P_GPOST:P_GPOST + 8] = _fm(g_post_mix[0])
    prm0[:, P_GPRE2:P_GPRE2 + 8] = _fm(g_pre_mlp[0])
    prm0[:, P_GPOST2:P_GPOST2 + 8] = _fm(g_post_mlp[0])
    prm0[:, P_BMOD:P_BMOD + 48] = _fm(b_mod[0])
    for d in range(2):
        prm0[:, P_W0 + 4 * d:P_W0 + 4 * d + 4] = _fm(rwkv_w0[0, d])
        prm0[:, P_A0 + 4 * d:P_A0 + 4 * d + 4] = _fm(rwkv_a0[0, d])
        prm0[:, P_BA + 4 * d:P_BA + 4 * d + 4] = _fm(lru_ba[0, d])
        prm0[:, P_BX + 4 * d:P_BX + 4 * d + 4] = _fm(lru_bx[0, d])
        prm0[:, P_LAM + 4 * d:P_LAM + 4 * d + 4] = _fm(lru_lambda[0, d])
    prm0[:, P_KK:P_KK + 4] = _fm(rwkv_k_k[0])
    prm0[:, P_KA:P_KA + 4] = _fm(rwkv_k_a[0])
    prm0[:, P_RK:P_RK + 4] = _fm(np.asarray(rwkv_r_k[0]).reshape(-1))
    prm0[:, P_LNG:P_LNG + 4] = _fm(rwkv_lnx_g[0])
    prm0[:, P_LNB:P_LNB + 4] = _fm(rwkv_lnx_b[0])
    for i in range(4):
        prm0[:, P_CW + 4 * i:P_CW + 4 * i + 4] = _fm(lru_conv_w[0, i])
    prm0[:, P_CB:P_CB + 4] = _fm(lru_conv_b[0])
    in_maps = []
    for i in range(8):
        prm = prm0.copy()
        for d in range(2):
            prm[:, P_H0 + 4 * d:P_H0 + 4 * d + 4] = _fm(state_lru[i, 0, d])
        cT = np.zeros((128, 8, 2), np.float32)
        cT[:, :, 0] = _fm(c[i])
        cT[:, :, 1] = _fm(c_ctx)
        h0 = np.ascontiguousarray(state_rwkv[i, 0].transpose(0, 3, 1, 2)).reshape(128, 512)
        m = dict(shared)
        m.update({"xs": x_sample[i], "xp": np.ascontiguousarray(x_prompt[2 * i:2 * i + 2].reshape(2 * TP, D)),
                  "cT": cT.reshape(128, 16), "h0r": h0, "prm": prm})
        in_maps.append(m)
    res = run_bass_kernel_spmd(nc, in_maps, core_ids=list(range(8)))
    R = res.results
    y_prompt = np.zeros((16, TP, D), np.float32)
    y_sample = np.zeros((8, TS, D), np.float32)
    st_r = np.zeros((16, 1, 2, 8, 64, 64), np.float32)
    st_l = np.zeros((16, 1, 2, 512), np.float32)
    for i in range(8):
        r = R[i]
        y_sample[i] = r["ys"]
        y_prompt[2 * i:2 * i + 2] = r["yp"].reshape(2, TP, D)
        so = r["str_o"].reshape(2, 2, 64, 8, 64)
        st_r[2 * i:2 * i + 2, 0] = so.transpose(0, 1, 3, 4, 2)
        sl = r["stl_o"].reshape(128, 4, 2, 2)
        st_l[2 * i:2 * i + 2, 0] = sl.transpose(2, 3, 1, 0).reshape(2, 2, 512)
    if _debug:
        return (y_prompt, y_sample, st_r, st_l), R
    return (y_prompt, y_sample, st_r, st_l)
```

```python
import contextlib
import numpy as np
import concourse.bass as bass
import concourse.mybir as mybir
from concourse.bass_utils import run_bass_kernel_spmd

F32 = mybir.dt.float32
BF16 = mybir.dt.bfloat16
F32R = mybir.dt.float32r
AF = mybir.ActivationFunctionType
ALU = mybir.AluOpType

D = 1024
TS = 2048
TP = 256
NTOK = TS + 2 * TP
NCH = NTOK // 64
DIN = 2944
DFF = 4096
LAM = float(np.exp(-0.5))
EPS = 1e-6
LNX_EPS = 64e-5
TT = 256
TC = 512
GELU_C = 1.5957691216057308

P_GPRE, P_GPOST, P_GPRE2, P_GPOST2 = 0, 8, 16, 24
P_BMOD = 32
P_W0, P_A0 = 80, 88
P_KK, P_KA, P_RK, P_LNG, P_LNB = 96, 100, 104, 108, 112
P_CW, P_CB = 116, 132
P_BA, P_BX, P_LAM, P_H0 = 136, 144, 152, 160
NPRM = 168
C_ID, C_OB, C_MSI, C_ML, C_IDS, C_RMF, C_RMB = 0, 128, 256, 384, 448, 512, 768
C_MSI1, C_ML1 = 1024, 1152
NCST = 1216


class Buf:
    __slots__ = ("name", "lw", "rd", "excl", "multi", "ws")

    def __init__(self, name=""):
        self.name = name
        self.lw = None
        self.rd = {}
        self.excl = False
        self.multi = False
        self.ws = {}


class TL:
    def __init__(self, t, name=""):
        self.t = t
        self.b = Buf(name)

    def __getitem__(self, k):
        return self.t[k]


class Sched:
    ENGS = ("pe", "act", "dve", "pool", "sp")

    def __init__(self, nc):
        self.nc = nc
        self.streams = {e: [] for e in self.ENGS}
        self.cnt = {}
        self.waited = {e: {} for e in self.ENGS}
        self.n_ops = 0
        self.dma_n = {e: 0 for e in self.ENGS}
        self.NSLOT = {"sp": 44, "act": 44, "pool": 4, "dve": 2, "pe": 2}

    def _deps(self, eng, reads, writes):
        need = {}
        for b in reads:
            if b.multi:
                for s, v in b.ws.items():
                    if need.get(s, 0) < v:
                        need[s] = v
                continue
            if b.lw is not None:
                s, v = b.lw
                if need.get(s, 0) < v:
                    need[s] = v
            if b.excl:
                for s, v in b.rd.items():
                    if s != eng and need.get(s, 0) < v:
                        need[s] = v
        for b in writes:
            if b.multi:
                continue
            if b.lw is not None:
                s, v = b.lw
                if need.get(s, 0) < v:
                    need[s] = v
            for s, v in b.rd.items():
                if need.get(s, 0) < v:
                    need[s] = v
        out = []
        w = self.waited[eng]
        for s, v in need.items():
            if s == "pe" and eng == "pe":
                continue
            if w.get(s, 0) >= v:
                continue
            w[s] = v
            out.append((s, v))
        return out

    def op(self, eng, fn, reads=(), writes=(), dma=False):
        reads = [r.b if isinstance(r, TL) else r for r in reads]
        writes = [r.b if isinstance(r, TL) else r for r in writes]
        waits = self._deps(eng, reads, writes)
        if dma:
            slot = self.dma_n[eng] % self.NSLOT[eng]
            self.dma_n[eng] += 1
            sem = "%s_d%d" % (eng, slot)
            prev = self.cnt.get(sem, 0)
            if prev > 0 and self.waited[eng].get(sem, 0) < prev:
                self.waited[eng][sem] = prev
                waits.append((sem, prev))
        else:
            sem = eng
        inc = 16 if dma else 1
        self.cnt[sem] = self.cnt.get(sem, 0) + inc
        val = self.cnt[sem]
        self.streams[eng].append((waits, fn, sem, inc))
        self.n_ops += 1
        for b in reads:
            if b.rd.get(sem, 0) < val:
                b.rd[sem] = val
        for b in writes:
            if b.multi:
                if b.ws.get(sem, 0) < val:
                    b.ws[sem] = val
                continue
            b.lw = (sem, val)
            b.rd = {}
        return val

    def barrier_on(self, tl):
        if tl.b.lw is None:
            return
        sname, v = tl.b.lw
        for e in ("sp", "act", "pool"):
            if self.waited[e].get(sname, 0) < v:
                self.waited[e][sname] = v
                self.streams[e].append(([(sname, v)], None, None, 0))

    def barrier(self):
        snap = dict(self.cnt)
        for e in self.ENGS:
            waits = []
            for s, v in snap.items():
                if s == "pe" and e == "pe":
                    continue
                if self.waited[e].get(s, 0) < v:
                    self.waited[e][s] = v
                    waits.append((s, v))
            if waits:
                self.streams[e].append((waits, None, None, 0))

    def emit(self):
        nc = self.nc
        sems = {}
        with contextlib.ExitStack() as st:
            for s in self.cnt:
                sems[s] = st.enter_context(nc.semaphore(s))
            block = st.enter_context(nc.Block())
            engmap = {"pe": block.tensor, "act": block.scalar, "dve": block.vector,
                      "pool": block.gpsimd, "sp": block.sync}
            for e in self.ENGS:
                stream = self.streams[e]
                if not stream:
                    continue

                def body(eng, stream=stream):
                    for waits, fn, sem, inc in stream:
                        for s, v in waits:
                            eng.wait_ge(sems[s], v)
                        if fn is not None:
                            fn(eng).then_inc(sems[sem], inc)
                engmap[e](body)


class K:
    def __init__(self, debug=False, stop_after=None):
        self.debug = debug
        self.stop_after = stop_after
        import os
        self.cutk = int(os.environ.get("KCUT", "0"))
        self.cutm = int(os.environ.get("KCUTM", "99"))
        self.ktiles = int(os.environ.get("KTILES", "99"))
        self.kskip = os.environ.get("KSKIP", "").split(",")
        self.nc = bass.Bass("TRN2", target_bir_lowering=False)
        self.S = Sched(self.nc)
        self.es = contextlib.ExitStack()
        self.psr = 0
        self.rr = {}

    def dram(self, name, shape, dt, kind="Internal"):
        t = TL(self.nc.dram_tensor(name, list(shape), dt, kind=kind).ap(), name)
        t.b.multi = True
        return t

    def sb(self, st, name, shape, dt):
        return TL(st.enter_context(self.nc.sbuf_tensor(name, list(shape), dt)), name)

    def sb2(self, st, name, shape, dt):
        t = st.enter_context(self.nc.sbuf_tensor(name, list(shape), dt))
        return [TL(t, name + "_lo"), TL(t, name + "_hi")]

    def ps(self):
        p = self.psum[self.psr % 8]
        self.psr += 1
        return p

    def mm(self, out, lhsT, rhs, start, stop, R, W):
        self.S.op("pe", lambda e: e.matmul(out, lhsT=lhsT, rhs=rhs, start=start, stop=stop), R, W)

    def tr(self, out, in_, ident, R, W):
        self.S.op("pe", lambda e: e.transpose(out, in_, ident), R, W)

    def act(self, out, in_, func, R, W, scale=1.0, bias=None, eng="act"):
        if bias is None:
            self.S.op("act", lambda e: e.activation(out=out, in_=in_, func=func, scale=scale), R, W)
        else:
            self.S.op("act", lambda e: e.activation(out=out, in_=in_, func=func, scale=scale, bias=bias), R, W)

    def tt(self, eng, out, in0, in1, op, R, W):
        self.S.op(eng, lambda e: e.tensor_tensor(out=out, in0=in0, in1=in1, op=op), R, W)

    def tsc(self, eng, out, in0, s1, op0, R, W, s2=None, op1=None):
        if op1 is None:
            self.S.op(eng, lambda e: e.tensor_scalar(out=out, in0=in0, scalar1=s1, scalar2=None, op0=op0), R, W)
        else:
            self.S.op(eng, lambda e: e.tensor_scalar(out=out, in0=in0, scalar1=s1, scalar2=s2, op0=op0, op1=op1), R, W)

    def stt(self, out, in0, scalar, in1, op0, op1, R, W):
        self.S.op("dve", lambda e: e.scalar_tensor_tensor(out=out, in0=in0, scalar=scalar, in1=in1, op0=op0, op1=op1), R, W)

    def cp(self, eng, out, in_, R, W):
        if eng == "act":
            self.S.op("act", lambda e: e.activation(out=out, in_=in_, func=AF.Copy), R, W)
        else:
            self.S.op(eng, lambda e: e.tensor_copy(out=out, in_=in_), R, W)

    def scan(self, out, d0, d1, init, R, W):
        self.S.op("dve", lambda e: e.tensor_tensor_scan(out=out, data0=d0, data1=d1, initial=init,
                                                        op0=ALU.mult, op1=ALU.add), R, W)

    def dma(self, out, in_, R, W, eng="sp"):
        self.S.op(eng, lambda e: e.dma_start(out=out, in_=in_), R, W, dma=True)

    def memset(self, eng, ap, val, W):
        self.S.op(eng, lambda e: e.memset(ap, val), (), W)

    def pick(self, key, engs):
        i = self.rr.get(key, 0)
        self.rr[key] = i + 1
        return engs[i % len(engs)]

    def build(self):
        nc = self.nc
        I = lambda n, s, dt=F32: self.dram(n, s, dt, "ExternalInput")
        O = lambda n, s, dt=F32: self.dram(n, s, dt, "ExternalOutput")
        self.xs = I("xs", [TS, D])
        self.xp = I("xp", [2 * TP, D])
        self.pe = I("pe", [TS, D])
        self.cT = I("cT", [128, 16])
        self.h0r = I("h0r", [128, 512])
        self.prm = I("prm", [128, NPRM])
        self.cst = I("cst", [128, NCST])
        self.w_mod = I("w_mod", [D, 6 * D])
        self.w_in = I("w_in", [D, DIN])
        self.w_out = I("w_out", [D, D])
        self.w1 = I("w1", [D, DFF])
        self.w2 = I("w2", [DFF, D])
        self.wup = I("wup", [128, 512])
        self.aup = I("aup", [128, 512])
        self.gup = I("gup", [128, 512])
        self.lwa = I("lwa", [2, 8, 64, 64])
        self.lwx = I("lwx", [2, 8, 64, 64])
        self.ys = O("ys", [TS, D])
        self.yp = O("yp", [2 * TP, D])
        self.str_o = O("str_o", [2, 128, 512])
        self.stl_o = O("stl_o", [128, 16])
        self.xT_scr = self.dram("xT_scr", [128, 8, NTOK], F32)
        self.xb_scr = self.dram("xb_scr", [128, 4, NTOK], F32)
        self.gate_scr = self.dram("gate_scr", [128, 4, NTOK], BF16)
        self.g_scr = self.dram("g_scr", [128, 4, NTOK], BF16)
        self.bon_scr = self.dram("bon_scr", [128, 4, NTOK], BF16)
        self.y_scr = self.dram("y_scr", [128, 8, NTOK], BF16)
        self.ytok_scr = self.dram("ytok_scr", [2, NTOK, 512], F32)
        for n in ("art", "rrt", "ttt", "akt", "mrbt", "mrkt", "bh", "kh"):
            setattr(self, n + "_scr", self.dram(n + "_scr", [NCH, 128, 512], BF16))
        self.vt_scr = self.dram("vt_scr", [NCH, 64, 512], BF16)
        self.pend_scr = self.dram("pend_scr", [NCH, 128, 512], F32)
        self.w1_scr = self.dram("w1_scr", [8, 128, 8, 512], BF16)
        self.w2_scr = self.dram("w2_scr", [8, 128, 32, 128], BF16)
        if self.debug:
            self.dbg = {}

        with self.es as st0:
            self.psum = [TL(st0.enter_context(nc.psum_tensor("ps%d" % i, [128, 512], F32)), "ps%d" % i)
                         for i in range(8)]
            for p_ in self.psum:
                p_.b.excl = True
            self.prm_t = self.sb(st0, "prm_t", [128, NPRM], F32)
            self.cst_t = self.sb(st0, "cst_t", [128, NCST], F32)
            self.modT = self.sb(st0, "modT", [128, 48, 2], F32)
            self.gs1 = self.sb(st0, "gs1", [128, 8, 2], F32)
            self.gs2 = self.sb(st0, "gs2", [128, 8, 2], F32)
            self.gg1 = self.sb(st0, "gg1", [128, 8, 2], F32)
            self.gg2 = self.sb(st0, "gg2", [128, 8, 2], F32)
            self.ident = self.sb(st0, "ident", [128, 128], F32)
            self.ones_bf = self.sb(st0, "ones_bf", [128, 128], BF16)
            self.oblk_bf = self.sb(st0, "oblk_bf", [128, 128], BF16)
            self.epsT = self.sb(st0, "epsT", [128, 2], F32)
            self.misc = self.sb(st0, "misc", [128, 32], F32)
            self.stl_t = self.sb(st0, "stl_t", [128, 16], F32)
            stop = False
            with contextlib.ExitStack() as stA:
                self.win = self.sb(stA, "win", [128, 8, DIN], BF16)
                self.win.b.multi = True
                self.wup_t = self.sb(stA, "wup_t", [128, 512], BF16)
                self.aup_t = self.sb(stA, "aup_t", [128, 512], BF16)
                self.gup_t = self.sb(stA, "gup_t", [128, 512], BF16)
                for nm, fn in (("p0", self.phase0), ("pA", self.phaseA)):
                    fn()
                    self.S.barrier()
                    if self.stop_after == nm:
                        stop = True
                        break
            if not stop:
                for nm, fn in (("pB", self.phaseB), ("pC", self.phaseC)):
                    fn()
                    self.S.barrier()
                    if self.stop_after == nm:
                        break
            self.S.emit()
        return nc

    def dump(self, name, src_ap, shape, dt, R):
        o = self.dram("dbg_" + name, shape, dt, "ExternalOutput")
        self.dma(o[:], src_ap, R, [o])

    def phase0(self):
        nc = self.nc
        prm, cst = self.prm_t, self.cst_t
        self.dma(prm[:], self.prm[:], [], [prm])
        self.dma(cst[:], self.cst[:], [], [cst])
        self.cp("dve", self.ident[:], cst[:, C_ID:C_ID + 128], [cst], [self.ident])
        self.cp("dve", self.oblk_bf[:], cst[:, C_OB:C_OB + 128], [cst], [self.oblk_bf])
        self.memset("dve", self.ones_bf[:], 1.0, [self.ones_bf])
        self.memset("dve", self.epsT[:, 0:1], EPS, [self.epsT])
        self.memset("dve", self.epsT[:, 1:2], LNX_EPS, [self.epsT])
        self.tsc("dve", self.misc[:, 0:4], prm[:, P_KA:P_KA + 4], -1.0, ALU.mult, [prm], [self.misc], 1.0, ALU.add)
        with contextlib.ExitStack() as st:
            scT = self.sb(st, "scT", [128, 16], F32)
            cT = self.sb(st, "cT_t", [128, 16], F32)
            wm = [self.sb(st, "wm%d" % i, [128, 8, 512], F32) for i in range(2)]
            tmp = self.sb(st, "lam_tmp", [128, 8], F32)
            self.dma(cT[:], self.cT[:], [], [cT])
            self.act(scT[:], cT[:], AF.Silu, [cT], [scT])
            self.act(tmp[:], prm[:, P_LAM:P_LAM + 8], AF.Exp, [prm], [tmp], scale=-1.0)
            self.act(tmp[:], tmp[:], AF.Ln, [tmp], [tmp], bias=1.0)
            self.tsc("dve", self.misc[:, 4:12], tmp[:], -8.0, ALU.mult, [tmp], [self.misc])
            self.tsc("dve", self.misc[:, 12:20], tmp[:], -16.0, ALU.mult, [tmp], [self.misc])
            wsrc = self.w_mod[:].rearrange("(kc p) n -> p kc n", p=128)
            scb = self.sb(st, "scb", [128, 16], BF16)
            self.cp("dve", scb[:], scT[:], [scT], [scb])
            sc3 = scb[:].rearrange("p (k c) -> p k c", c=2)
            wmb = [self.sb(st, "wmb%d" % i, [128, 8, 512], BF16) for i in range(2)]
            stgA = [self.sb(st, "stgA%d" % i, [128, 8, 256], F32) for i in range(2)]
            wisrc = self.w_in[:].rearrange("(kc p) n -> p kc n", p=128)
            wi_blocks = [(c0, min(256, DIN - c0)) for c0 in range(0, DIN, 256)]
            sm_list = [(self.wup, self.wup_t), (self.aup, self.aup_t), (self.gup, self.gup_t)]
            nb = [0]

            def win_step():
                if wi_blocks:
                    c0, cw = wi_blocks.pop(0)
                    s_ = stgA[nb[0] % 2]
                    self.dma(s_[:, :, 0:cw], wisrc[:, :, c0:c0 + cw], [], [s_], eng="act")
                    self.cp("act" if nb[0] % 2 == 0 else "pool", self.win[:, :, c0:c0 + cw], s_[:, :, 0:cw], [s_], [self.win])
                    nb[0] += 1
                elif sm_list:
                    src, dstt = sm_list.pop(0)
                    s_ = stgA[nb[0] % 2]
                    nb[0] += 1
                    s2 = s_[:].rearrange("p a b -> p (a b)")[:, 0:512]
                    self.dma(s2, src[:], [], [s_], eng="act")
                    self.cp("pool", dstt[:], s2, [s_], [dstt])

            for blk in range(12):
                w = wm[blk % 2]
                wb_ = wmb[blk % 2]
                self.dma(w[:], wsrc[:, :, blk * 512:(blk + 1) * 512], [], [w], eng="sp")
                self.cp("dve", wb_[:], w[:], [w], [wb_])
                win_step()
                p = self.ps()
                for m in range(4):
                    for kc in range(8):
                        self.mm(p[:, 2 * m:2 * m + 2], wb_[:, kc, m * 128:(m + 1) * 128], sc3[:, kc, :],
                                kc == 0, kc == 7, [wb_, scb], [p])
                for m in range(4):
                    mi = blk * 4 + m
                    self.tsc("dve", self.modT[:, mi, :], p[:, 2 * m:2 * m + 2], prm[:, P_BMOD + mi:P_BMOD + mi + 1],
                             ALU.add, [p, prm], [self.modT])
            while wi_blocks or sm_list:
                win_step()
            m3 = self.modT
            for (dst, sc_off, g_off, one) in ((self.gs1, 8, P_GPRE, 1.0), (self.gs2, 32, P_GPRE2, 1.0),
                                              (self.gg1, 16, P_GPOST, 0.0), (self.gg2, 40, P_GPOST2, 0.0)):
                for c in range(2):
                    self.tsc("dve", dst[:, :, c], m3[:, sc_off:sc_off + 8, c], one, ALU.add, [m3], [dst])
                    self.tt("dve", dst[:, :, c], dst[:, :, c], prm[:, g_off:g_off + 8], ALU.mult, [dst, prm], [dst])
            if self.debug:
                self.dump("modT", self.modT[:], [128, 48, 2], F32, [self.modT])
                self.dump("gs1", self.gs1[:], [128, 8, 2], F32, [self.gs1])

    def load_cast(self, st, dst_ap, dst_tl, src_ap, shape, tag):
        key = "stg_" + tag
        if not hasattr(self, key):
            setattr(self, key, [self.sb(st, "%s%d" % (key, i), shape, F32) for i in range(2)])
        ring = getattr(self, key)
        s = ring[self.rr.get(key, 0) % 2]
        self.rr[key] = self.rr.get(key, 0) + 1
        self.dma(s[:], src_ap, [], [s], eng="sp")
        eng = self.pick("castE", ["act", "pool"])
        self.cp(eng, dst_ap, s[:], [s], [dst_tl])

    def phaseA(self):
        nc = self.nc
        prm, cst = self.prm_t, self.cst_t
        with contextlib.ExitStack() as st:
            win, wup, aup, gup = self.win, self.wup_t, self.aup_t, self.gup_t
            mSI = cst[:, C_MSI:C_MSI + 128].rearrange("p (q t) -> p q t", q=2)
            mL = cst[:, C_ML:C_ML + 64]
            idS = cst[:, C_IDS:C_IDS + 64]

            xin = self.sb(st, "xin", [128, 2, D], F32)
            xT = self.sb(st, "xT", [128, 8, TT], F32)
            sq = self.sb(st, "sq", [128, 8, TT], BF16)
            hT = self.sb(st, "hT", [128, 8, TT], BF16)
            rstd = self.sb(st, "rstd", [128, TT], F32)
            tmpA = [self.sb(st, "tmpA%d" % i, [128, TT], F32) for i in range(2)]
            rT = self.sb(st, "rT", [128, 4, TT], F32)
            kT = self.sb(st, "kT", [128, 4, TT], F32)
            vT = self.sb(st, "vT", [128, 4, TT], F32)
            xw = self.sb(st, "xw", [128, TT], BF16)
            xa = self.sb(st, "xa", [128, TT], BF16)
            xg = self.sb(st, "xg", [128, TT], BF16)
            xbT = self.sb(st, "xbT", [128, 4, TT], F32)
            gtmp = [self.sb(st, "gtmp%d" % i, [128, TT], F32) for i in range(3)]
            gate = self.sb(st, "gate", [128, 4, TT], BF16)
            gT = self.sb(st, "gT", [128, 4, TT], BF16)
            kkn = self.sb(st, "kkn", [128, 4, TT], F32)
            ksum = self.sb(st, "ksum", [128, 4, TT], F32)
            bon = self.sb(st, "bon", [128, 4, TT], BF16)
            sg = self.sb(st, "sg", [128, 4, TT], F32)
            cs = self.sb(st, "cs", [128, 4, TT], F32)
            E1 = self.sb(st, "E1", [128, 4, TT], F32)
            ad = self.sb(st, "ad", [128, 4, TT], F32)
            wk1 = self.sb(st, "wk1", [128, 4, TT], F32)
            wk2 = self.sb(st, "wk2", [128, 4, TT], F32)
            NC4 = TT // 64
            AR = [self.sb(st, "AR%d" % d, [128, 4, NC4, 2, 64], BF16) for d in range(2)]
            BK = [self.sb(st, "BK%d" % d, [128, 4, NC4, 2, 64], BF16) for d in range(2)]
            PEb1 = self.sb(st, "PEb", [128, 4, NC4, 64], F32)
            PEb = [PEb1, PEb1]
            tokB1 = self.sb(st, "tokB", [128, 2, 8, 64], BF16)
            tokK1 = self.sb(st, "tokK", [128, 2, 8, 64], BF16)
            tokB, tokK = [tokB1, tokB1], [tokK1, tokK1]
            tokV = self.sb(st, "tokV", [128, 2, 8, 64], BF16)
            NSET = 2
            MRBs = [self.sb(st, "MRBs%d" % i, [64, 8, 64], BF16) for i in range(NSET)]
            Lt0s = [self.sb(st, "Lt0_%d" % i, [64, 8, 64], F32R) for i in range(NSET)]
            Tfins = [self.sb(st, "Tfin%d" % i, [64, 8, 64], BF16) for i in range(NSET)]
            SCk = self.sb(st, "SCk", [128, 2, 8, 64], BF16)
            Lms = [[self.sb(st, "Lm%d_%d" % (i, k), [64, 8, 64], F32R) for i in range(2)] for k in range(NSET)]
            Ltms = [[self.sb(st, "Ltm%d_%d" % (i, k), [64, 8, 64], F32R) for i in range(2)] for k in range(NSET)]
            ILms = [self.sb(st, "ILm_%d" % k, [64, 8, 64], F32R) for k in range(NSET)]
            Ttms = [[self.sb(st, "Ttm%d_%d" % (i, k), [64, 8, 64], F32R) for i in range(2)] for k in range(NSET)]

            ones_bf, oblk, ident = self.ones_bf, self.oblk_bf, self.ident
            tiles = [(0, t0, True, 0) for t0 in range(0, TS, TT)] + [(1, TS, False, 1), (2, TS + TP, False, 1)]
            mS1 = cst[0:64, C_MSI1:C_MSI1 + 128].rearrange("p (q t) -> p q t", q=2)
            mL1 = cst[0:64, C_ML1:C_ML1 + 64]
            id64 = idS[0:64]
            loaded = set()

            def load_x(g0, is_s):
                if g0 in loaded:
                    return
                loaded.add(g0)
                if is_s:
                    src = self.xs[g0:g0 + TT, :].rearrange("(s p) f -> p s f", p=128)
                else:
                    l0 = g0 - TS
                    src = self.xp[l0:l0 + TT, :].rearrange("(s p) f -> p s f", p=128)
                self.dma(xin[:], src, [], [xin])
                if is_s:
                    petv = xT[:].rearrange("p a b -> p (a b)").rearrange("p (s f) -> p s f", s=2)
                    self.dma(petv, self.pe[g0:g0 + TT, :].rearrange("(s p) f -> p s f", p=128), [], [xT], eng="act")

            def front(seq, g0, is_s, mc, nxt=None):
                load_x(g0, is_s)
                if is_s:
                    petv = xT[:].rearrange("p a b -> p (a b)").rearrange("p (s f) -> p s f", s=2)
                    self.tt("pool", xin[:], xin[:], petv, ALU.add, [xin, xT], [xin])
                for j in range(8):
                    p = self.ps()
                    for s in range(2):
                        self.tr(p[:, s * 128:(s + 1) * 128], xin[:, s, j * 128:(j + 1) * 128], ident[:], [xin, ident], [p])
                    self.cp("act", xT[:, j, :], p[:, 0:TT], [p], [xT])
                    self.tt("pool", sq[:, j, :], xT[:, j, :], xT[:, j, :], ALU.mult, [xT], [sq])
                    yield
                self.dma(self.xT_scr[:, :, g0:g0 + TT], xT[:], [xT], [self.xT_scr])
                p = self.ps()
                for j in range(8):
                    self.mm(p[:, 0:TT], ones_bf[:], sq[:, j, :], j == 0, j == 7, [ones_bf, sq], [p])
                self.act(rstd[:], p[:, 0:TT], AF.Sqrt, [p, self.epsT], [rstd], scale=1.0 / D, bias=self.epsT[:, 0:1])
                self.S.op("dve", lambda e: e.reciprocal(out=rstd[:], in_=rstd[:]), [rstd.b], [rstd.b])
                for j in range(8):
                    t = tmpA[j % 2]
                    self.tt("dve", t[:], xT[:, j, :], rstd[:], ALU.mult, [xT, rstd], [t])
                    self.act(hT[:, j, :], t[:], AF.Identity, [t, self.gs1, self.modT], [hT],
                             scale=self.gs1[:, j, mc:mc + 1], bias=self.modT[:, j, mc:mc + 1])
                    yield
                for m in range(23):
                    if m >= self.cutm:
                        break
                    p = self.ps()
                    for kc in range(8):
                        self.mm(p[:, 0:TT], win[:, kc, m * 128:(m + 1) * 128], hT[:, kc, :], kc == 0, kc == 7, [win, hT], [p])
                    pz = p[:, 0:TT]
                    if m < 4:
                        self.cp("act", rT[:, m, :], pz, [p], [rT])
                    elif m < 8:
                        self.cp("act", kT[:, m - 4, :], pz, [p], [kT])
                    elif m < 12:
                        self.cp("act", vT[:, m - 8, :], pz, [p], [vT])
                    elif m == 12:
                        self.act(xw[:], pz, AF.Tanh, [p], [xw])
                    elif m == 13:
                        self.cp("act", xa[:], pz, [p], [xa])
                    elif m == 14:
                        self.act(xg[:], pz, AF.Sigmoid, [p], [xg])
                    elif m < 19:
                        self.cp("act", xbT[:, m - 15, :], pz, [p], [xbT])
                    else:
                        j = m - 19
                        g0_, g1_, g2_ = gtmp
                        self.cp("act", g0_[:], pz, [p], [g0_])
                        self.tt("pool", g1_[:], g0_[:], g0_[:], ALU.mult, [g0_], [g1_])
                        self.tsc("dve", g1_[:], g1_[:], 0.044715, ALU.mult, [g1_], [g1_], 1.0, ALU.add)
                        self.tt("dve", g1_[:], g1_[:], g0_[:], ALU.mult, [g1_, g0_], [g1_])
                        self.act(g2_[:], g1_[:], AF.Sigmoid, [g1_], [g2_], scale=GELU_C)
                        self.tt("pool", gate[:, j, :], g0_[:], g2_[:], ALU.mult, [g0_, g2_], [gate])
                    yield
                self.dma(self.xb_scr[:, :, g0:g0 + TT], xbT[:], [xbT], [self.xb_scr])
                if self.debug and g0 == 0:
                    self.dump("hT", hT[:], [128, 8, TT], BF16, [hT])
                    self.dump("rT", rT[:], [128, 4, TT], F32, [rT])
                    self.dump("vT", vT[:], [128, 4, TT], F32, [vT])
                    self.dump("xbT", xbT[:], [128, 4, TT], F32, [xbT])
                    self.dump("gate", gate[:], [128, 4, TT], BF16, [gate])
                self.dma(self.gate_scr[:, :, g0:g0 + TT], gate[:], [gate], [self.gate_scr])
                for j in range(4):
                    p = self.ps()
                    self.mm(p[:, 0:TT], gup[:, j * 128:(j + 1) * 128], xg[:], True, True, [gup, xg], [p])
                    self.cp("act", gT[:, j, :], p[:, 0:TT], [p], [gT])
                    yield
                self.dma(self.g_scr[:, :, g0:g0 + TT], gT[:], [gT], [self.g_scr])
                for j in range(4):
                    self.tsc("dve", kkn[:, j, :], kT[:, j, :], prm[:, P_KK + j:P_KK + j + 1], ALU.mult, [kT, prm], [kkn])
                    self.tt("pool", sq[:, j, :], kkn[:, j, :], kkn[:, j, :], ALU.mult, [kkn], [sq])
                for j in range(4):
                    p = self.ps()
                    self.mm(p[:, 0:TT], oblk[:], sq[:, j, :], True, True, [oblk, sq], [p])
                    t = tmpA[j % 2]
                    self.act(t[:], p[:, 0:TT], AF.Sqrt, [p], [t])
                    self.tsc("dve", t[:], t[:], 1e-12, ALU.max, [t], [t])
                    self.S.op("dve", lambda e, t=t: e.reciprocal(out=t[:], in_=t[:]), [t.b], [t.b])
                    self.tt("dve", kkn[:, j, :], kkn[:, j, :], t[:], ALU.mult, [kkn, t], [kkn])
                    yield
                for s in range(2):
                    p = self.ps()
                    for j in range(4):
                        self.tr(p[:, j * 128:(j + 1) * 128], vT[:, j, s * 128:(s + 1) * 128], ident[:], [vT, ident], [p])
                    self.cp("act", tokV[:, s, :, :].rearrange("p h k -> p (h k)"), p[:], [p], [tokV])
                c0 = g0 // 64
                for s in range(2):
                    dst = self.vt_scr[c0 + 2 * s:c0 + 2 * s + 2, :, :].rearrange("c s f -> (c s) f")
                    self.dma(dst, tokV[:, s, :, :].rearrange("p h k -> p (h k)"), [tokV], [self.vt_scr])
                if nxt is not None:
                    load_x(nxt[1], nxt[2])
                yield
            def prep(g0, d):
                c0 = g0 // 64
                for j in range(4):
                    p = self.ps()
                    self.mm(p[:, 0:TT], wup[d * 64:(d + 1) * 64, j * 128:(j + 1) * 128], xw[d * 64:(d + 1) * 64, :],
                            True, True, [wup, xw], [p])
                    self.act(sg[:, j, :], p[:, 0:TT], AF.Sigmoid, [p, prm], [sg],
                             bias=prm[:, P_W0 + 4 * d + j:P_W0 + 4 * d + j + 1])
                    if d == 0:
                        self.scan(cs[:, j, :], cst[:, C_RMF:C_RMF + TT], sg[:, j, :], 0.0, [cst, sg], [cs])
                    else:
                        self.scan(cs[:, j, ::-1], cst[:, C_RMB:C_RMB + TT][:, ::-1], sg[:, j, ::-1], 0.0, [cst, sg], [cs])
                    yield
                for j in range(4):
                    p = self.ps()
                    self.mm(p[:, 0:TT], aup[d * 64:(d + 1) * 64, j * 128:(j + 1) * 128], xa[d * 64:(d + 1) * 64, :],
                            True, True, [aup, xa], [p])
                    self.act(ad[:, j, :], p[:, 0:TT], AF.Sigmoid, [p, prm], [ad],
                             bias=prm[:, P_A0 + 4 * d + j:P_A0 + 4 * d + j + 1])
                    yield
                self.tt("dve", sg[:], cs[:], sg[:], ALU.subtract, [cs, sg], [sg])
                self.act(E1[:], cs[:], AF.Exp, [cs], [E1], scale=-LAM)
                self.act(cs[:], cs[:], AF.Exp, [cs], [cs], scale=LAM)
                self.act(sg[:], sg[:], AF.Exp, [sg], [sg], scale=-LAM)
                E2, E3 = cs, sg
                ar5 = AR[d]
                bk5 = BK[d]
                v4 = lambda tl: tl[:].rearrange("p j (c t) -> p j c t", t=64)
                self.stt(ar5[:, :, :, 0, :], v4(kkn), -1.0, v4(E3), ALU.mult, ALU.mult, [kkn, E3], [ar5])
                self.tt("pool", ar5[:, :, :, 1, :], v4(rT), v4(E1), ALU.mult, [rT, E1], [ar5])
                yield
                self.tt("dve", wk1[:], kkn[:], ad[:], ALU.mult, [kkn, ad], [wk1])
                self.tt("dve", wk1[:], wk1[:], E2[:], ALU.mult, [wk1, E2], [wk1])
                self.cp("act", bk5[:, :, :, 0, :], v4(wk1), [wk1], [bk5])
                yield
                for j in range(4):
                    self.tsc("dve", wk2[:, j, :], ad[:, j, :], prm[:, P_KA + j:P_KA + j + 1], ALU.mult, [ad, prm, self.misc], [wk2],
                             self.misc[:, j:j + 1], ALU.add)
                self.tt("dve", wk2[:], wk2[:], kT[:], ALU.mult, [wk2, kT], [wk2])
                if d == 0:
                    self.cp("pool", ksum[:], wk2[:], [wk2], [ksum])
                else:
                    self.tt("pool", ksum[:], ksum[:], wk2[:], ALU.add, [ksum, wk2], [ksum])
                self.tt("dve", wk2[:], wk2[:], E2[:], ALU.mult, [wk2, E2], [wk2])
                self.cp("act", bk5[:, :, :, 1, :], v4(wk2), [wk2], [bk5])
                yield
                te = 63 if d == 0 else 0
                pend_b = v4(E1)[:, :, :, te:te + 1].to_broadcast([128, 4, NC4, 64])
                self.cp("act", PEb[d][:], pend_b, [E1], [PEb[d]])
                self.tt("dve", v4(wk1), v4(wk1), PEb[d][:], ALU.mult, [wk1, PEb[d]], [wk1])
                self.tt("dve", v4(wk2), v4(wk2), PEb[d][:], ALU.mult, [wk2, PEb[d]], [wk2])
                yield
                for (srcw, tokX, scr) in ((wk1, tokB[d], self.bh_scr), (wk2, tokK[d], self.kh_scr)):
                    for s in range(2):
                        p = self.ps()
                        for j in range(4):
                            self.tr(p[:, j * 128:(j + 1) * 128], srcw[:, j, s * 128:(s + 1) * 128], ident[:], [srcw, ident], [p])
                        self.cp("act", tokX[:, s, :, :].rearrange("p h k -> p (h k)"), p[:], [p], [tokX])
                        for cc in range(2):
                            self.dma(scr[c0 + 2 * s + cc, d * 64:(d + 1) * 64, :],
                                     tokX[cc * 64:(cc + 1) * 64, s, :, :].rearrange("p h k -> p (h k)"),
                                     [tokX], [scr])
                        yield
                for cl in range(NC4):
                    c = c0 + cl
                    for hp in range(2):
                        for (q, scr) in ((0, self.art_scr), (1, self.rrt_scr)):
                            dst = scr[c, d * 64:(d + 1) * 64, :].rearrange("k (j hp t) -> k j hp t", hp=2, t=64)[:, :, hp, :]
                            self.dma(dst, ar5[hp * 64:(hp + 1) * 64, :, cl, q, :], [ar5], [scr], eng="sp")
                        dst = self.pend_scr[c, d * 64:(d + 1) * 64, :].rearrange("k (j hp t) -> k j hp t", hp=2, t=64)[:, :, hp, :]
                        self.dma(dst, PEb[d][hp * 64:(hp + 1) * 64, :, cl, :], [PEb[d]], [self.pend_scr], eng="sp")
                yield
            def bonus(g0):
                for j in range(4):
                    self.stt(sq[:, j, :], rT[:, j, :], prm[:, P_RK + j:P_RK + j + 1], ksum[:, j, :], ALU.mult, ALU.mult,
                             [rT, prm, ksum], [sq])
                    p = self.ps()
                    self.mm(p[:, 0:TT], oblk[:], sq[:, j, :], True, True, [oblk, sq], [p])
                    self.tt("dve", bon[:, j, :], p[:, 0:TT], vT[:, j, :], ALU.mult, [p, vT], [bon])
                self.dma(self.bon_scr[:, :, g0:g0 + TT], bon[:], [bon], [self.bon_scr])
                yield
            def chunk_sck(g0):
                c0 = g0 // 64
                for cl in range(NC4):
                    c = c0 + cl
                    for hp in range(2):
                        p = self.ps()
                        for j in range(4):
                            for d in range(2):
                                self.mm(p[d * 64:(d + 1) * 64, j * 128:(j + 1) * 128],
                                        BK[d][hp * 64:(hp + 1) * 64, j, cl, 1, :],
                                        AR[d][hp * 64:(hp + 1) * 64, j, cl, :, :].rearrange("p q t -> p (q t)"),
                                        True, True, [BK[d], AR[d]], [p])
                        self.tt("dve", SCk[:, :, hp::2, :].rearrange("p q h t -> p h q t"),
                                p[:].rearrange("p (h q t) -> p h q t", q=2, t=64),
                                mSI.unsqueeze(1).to_broadcast([128, 4, 2, 64]), ALU.mult, [p, cst], [SCk])
                    self.dma(self.akt_scr[c], SCk[:, 0, :, :].rearrange("p h s -> p (h s)"), [SCk], [self.akt_scr], eng="sp")
                    self.dma(self.mrkt_scr[c], SCk[:, 1, :, :].rearrange("p h s -> p (h s)"), [SCk], [self.mrkt_scr], eng="sp")
                    yield
            def chunk_d(g0, d, cls, k):
                c0 = g0 // 64
                Lm, Ltm, ILm, Ttm, Lt0, Tfin = Lms[k], Ltms[k], ILms[k], Ttms[k], Lt0s[k], Tfins[k]
                MRB = MRBs[k]
                for cl in cls:
                    c = c0 + cl
                    msk = mSI[0:64] if d == 0 else mS1
                    mskL = mL[0:64] if d == 0 else mL1
                    L0 = Lm[0]
                    for hp in range(2):
                        p = self.ps()
                        for j in range(4):
                            self.mm(p[0:64, j * 128:(j + 1) * 128],
                                    BK[d][hp * 64:(hp + 1) * 64, j, cl, 0, :],
                                    AR[d][hp * 64:(hp + 1) * 64, j, cl, :, :].rearrange("p q t -> p (q t)"),
                                    True, True, [BK[d], AR[d]], [p])
                        p4 = p[0:64, :].rearrange("p (h q t) -> p h q t", q=2, t=64)
                        self.tt("dve", Lt0[:, hp::2, :], p4[:, :, 0, :], msk[:, 0, :].unsqueeze(1).to_broadcast([64, 4, 64]),
                                ALU.mult, [p, cst], [Lt0])
                        self.tt("dve", MRB[:, hp::2, :], p4[:, :, 1, :], msk[:, 1, :].unsqueeze(1).to_broadcast([64, 4, 64]),
                                ALU.mult, [p, cst], [MRB])
                        p2 = self.ps()
                        for j in range(4):
                            self.mm(p2[0:64, j * 64:(j + 1) * 64],
                                    AR[d][hp * 64:(hp + 1) * 64, j, cl, 0, :], BK[d][hp * 64:(hp + 1) * 64, j, cl, 0, :],
                                    True, True, [AR[d], BK[d]], [p2])
                        self.tt("dve", L0[:, hp::2, :], p2[0:64, 0:256].rearrange("p (h s) -> p h s", s=64),
                                mskL.unsqueeze(1).to_broadcast([64, 4, 64]), ALU.mult, [p2, cst], [L0])
                    self.dma(self.mrbt_scr[c, d * 64:(d + 1) * 64, :], MRB[:].rearrange("p h s -> p (h s)"),
                             [MRB], [self.mrbt_scr], eng="sp")
                    yield
                    T0 = Ttm[0]
                    self.tt("pool", T0[:], Lt0[:].bitcast(F32), id64.unsqueeze(1).to_broadcast([64, 8, 64]), ALU.add,
                            [Lt0, cst], [T0])
                    L_prev, Tt_prev, Lt_prev = L0, T0, Lt0
                    for lev in range(1, 6):
                        L_new, Lt_new, Tt_new = Lm[lev % 2], Ltm[lev % 2], Ttm[lev % 2]
                        pA = self.ps()
                        for h in range(8):
                            self.mm(pA[0:64, h * 64:(h + 1) * 64], Lt_prev[:, h, :], L_prev[:, h, :], True, True,
                                    [Lt_prev, L_prev], [pA])
                        if lev < 5:
                            pB = self.ps()
                            for h in range(8):
                                self.mm(pB[0:64, h * 64:(h + 1) * 64], L_prev[:, h, :], Lt_prev[:, h, :], True, True,
                                        [Lt_prev, L_prev], [pB])
                        self.tt("dve", ILm[:], pA[0:64, :].rearrange("p (h s) -> p h s", s=64),
                                id64.unsqueeze(1).to_broadcast([64, 8, 64]), ALU.add, [pA, cst], [ILm])
                        if lev < 5:
                            self.cp("dve", L_new[:].rearrange("p h s -> p (h s)"), pA[0:64, :], [pA], [L_new])
                            self.cp("act", Lt_new[:].rearrange("p h s -> p (h s)"), pB[0:64, :], [pB], [Lt_new])
                        yield
                        pC = self.ps()
                        for h in range(8):
                            self.mm(pC[0:64, h * 64:(h + 1) * 64], ILm[:, h, :], Tt_prev[:, h, :], True, True,
                                    [ILm, Tt_prev], [pC])
                        if lev < 5:
                            self.cp("act", Tt_new[:].rearrange("p h s -> p (h s)"), pC[0:64, :], [pC], [Tt_new])
                        else:
                            self.cp("act", Tfin[:].rearrange("p h s -> p (h s)"), pC[0:64, :], [pC], [Tfin])
                        L_prev, Tt_prev, Lt_prev = L_new, Tt_new, Lt_new
                        yield
                    self.dma(self.ttt_scr[c, d * 64:(d + 1) * 64, :], Tfin[:].rearrange("p h s -> p (h s)"),
                             [Tfin], [self.ttt_scr], eng="sp")


            def run_all(*gens, w=None):
                gens = list(gens)
                wts = {id(g): (w[i] if w else 1) for i, g in enumerate(gens)}
                while gens:
                    for g in list(gens):
                        for _ in range(wts[id(g)]):
                            try:
                                next(g)
                            except StopIteration:
                                gens.remove(g)
                                break

            def seq_(*gens):
                for g in gens:
                    yield from g

            prev = None
            tl_ = tiles[:self.ktiles]
            for ti_, (seq, g0, is_s, mc) in enumerate(tl_):
                nxt = tl_[ti_ + 1] if ti_ + 1 < len(tl_) else None
                if prev is None:
                    run_all(front(seq, g0, is_s, mc, nxt))
                else:
                    run_all(seq_(chunk_d(prev, 1, [0, 1], 0), chunk_sck(prev)), chunk_d(prev, 1, [2, 3], 1), front(seq, g0, is_s, mc, nxt), w=[1, 1, 2])
                run_all(prep(g0, 0))
                run_all(chunk_d(g0, 0, [0, 1], 0), chunk_d(g0, 0, [2, 3], 1), prep(g0, 1))
                run_all(bonus(g0))
                prev = g0
            run_all(seq_(chunk_d(prev, 1, [0, 1], 0), chunk_sck(prev)), chunk_d(prev, 1, [2, 3], 1))

    def phaseB(self):
        with contextlib.ExitStack() as st:
            side = [self.gen_c0(st), self.phaseB_lru(st)]
            post = self.phaseB_post(st)
            next(post)
            done = np.zeros((2, NCH), bool)
            posted = [False] * (NTOK // 128)
            si = 0
            for info in self.phaseB_chain(st):
                for (d, c) in info:
                    done[d, c] = True
                for _ in range(2):
                    if side:
                        g = side[si % len(side)]
                        si += 1
                        try:
                            next(g)
                        except StopIteration:
                            side.remove(g)
                for b in range(NTOK // 128):
                    if not posted[b] and done[:, 2 * b:2 * b + 2].all():
                        posted[b] = True
                        post.send(b)
            for g in side:
                for _ in g:
                    pass
            for b in range(NTOK // 128):
                if not posted[b]:
                    post.send(b)
            if self.debug:
                self.S.barrier()
                self.dump("yscr", self.y_scr[:], [128, 8, NTOK], BF16, [self.y_scr])

    def gen_c0(self, st):
        stg = [self.sb(st, "stgC%d" % i, [128, 8, 512], F32) for i in range(2)]
        wb = [self.sb(st, "wbC%d" % i, [128, 8, 512], BF16) for i in range(2)]
        w1src = self.w1[:].rearrange("(kc p) n -> p kc n", p=128)
        for blk in range(8):
            s, o = stg[blk % 2], wb[blk % 2]
            self.dma(s[:], w1src[:, :, blk * 512:(blk + 1) * 512], [], [s])
            self.cp("pool", o[:], s[:], [s], [o])
            self.dma(self.w1_scr[blk], o[:], [o], [self.w1_scr], eng="act")
            yield
        w2src = self.w2[:].rearrange("(fc p) n -> p fc n", p=128)
        for m in range(8):
            s, o = stg[m % 2], wb[m % 2]
            s4 = s[:].rearrange("p k (a b) -> p (k a) b", b=128)
            o4 = o[:].rearrange("p k (a b) -> p (k a) b", b=128)
            self.dma(s4, w2src[:, :, m * 128:(m + 1) * 128], [], [s])
            self.cp("pool", o[:], s[:], [s], [o])
            self.dma(self.w2_scr[m], o4, [o], [self.w2_scr], eng="act")
            yield

    def phaseB_lru(self, st):
        prm, cst, misc = self.prm_t, self.cst_t, self.misc
        if True:
            wbd32 = self.sb(st, "wbd32", [128, 16, 128], F32)
            wbd = self.sb(st, "wbd", [128, 16, 128], BF16)
            self.memset("pool", wbd32[:], 0.0, [wbd32])
            self.S.barrier_on(wbd32)
            wbd32.b.multi = True
            for gi, src in enumerate((self.lwa, self.lwx)):
                for d in range(2):
                    for j in range(4):
                        for hb in range(2):
                            self.dma(wbd32[hb * 64:(hb + 1) * 64, (gi * 2 + d) * 4 + j, hb * 64:(hb + 1) * 64],
                                     src[d, 2 * j + hb], [], [wbd32])
            self.cp("dve", wbd[:], wbd32[:], [wbd32], [wbd])
            TM = TS
            xbp = self.sb(st, "xbp", [128, TM + 4], F32)
            xc = self.sb(st, "xc", [128, TM], F32)
            xcb = self.sb(st, "xcb", [128, TM], BF16)
            gt = self.sb(st, "gt_l", [128, TM], BF16)
            a_t = self.sb(st, "a_t", [128, TM], F32)
            bx_t = self.sb(st, "bx_t", [128, TM], F32)
            s_t = self.sb(st, "s_t", [128, TM], F32)
            hs = [self.sb(st, "hs%d" % d, [128, TM], F32) for d in range(2)]
            yb = self.sb(st, "yb", [128, TM], BF16)
            for (seq, g0, T) in ((0, 0, TS), (1, TS, TP), (2, TS + TP, TP)):
                for j in range(4):
                    self.memset("pool", xbp[:, 0:2], 0.0, [xbp])
                    self.memset("pool", xbp[:, T + 2:T + 4], 0.0, [xbp])
                    self.dma(xbp[:, 2:T + 2], self.xb_scr[:, j, g0:g0 + T], [self.xb_scr], [xbp])
                    self.dma(gt[:, 0:T], self.gate_scr[:, j, g0:g0 + T], [self.gate_scr], [gt], eng="act")
                    cw = lambda i: prm[:, P_CW + 4 * i + j:P_CW + 4 * i + j + 1]
                    self.act(xc[:, 0:T], xbp[:, 0:T], AF.Identity, [xbp, prm], [xc], scale=cw(0), bias=prm[:, P_CB + j:P_CB + j + 1])
                    for i in range(1, 4):
                        self.stt(xc[:, 0:T], xbp[:, i:i + T], cw(i), xc[:, 0:T], ALU.mult, ALU.add, [xbp, prm, xc], [xc])
                    self.cp("pool", xcb[:, 0:T], xc[:, 0:T], [xc], [xcb])
                    yield
                    for d in range(2):
                        for t0 in range(0, T, 512):
                            tw = min(512, T - t0)
                            p = self.ps()
                            self.mm(p[:, 0:tw], wbd[:, (0 * 2 + d) * 4 + j, :], xcb[:, t0:t0 + tw], True, True, [wbd, xcb], [p])
                            self.act(s_t[:, t0:t0 + tw], p[:, 0:tw], AF.Sigmoid, [p, prm], [s_t],
                                     bias=prm[:, P_BA + 4 * d + j:P_BA + 4 * d + j + 1])
                            p2 = self.ps()
                            self.mm(p2[:, 0:tw], wbd[:, (1 * 2 + d) * 4 + j, :], xcb[:, t0:t0 + tw], True, True, [wbd, xcb], [p2])
                            self.act(bx_t[:, t0:t0 + tw], p2[:, 0:tw], AF.Sigmoid, [p2, prm], [bx_t],
                                     bias=prm[:, P_BX + 4 * d + j:P_BX + 4 * d + j + 1])
                        col = 4 + d * 4 + j
                        self.act(a_t[:, 0:T], s_t[:, 0:T], AF.Exp, [s_t, misc], [a_t], scale=misc[:, col:col + 1])
                        self.act(s_t[:, 0:T], s_t[:, 0:T], AF.Exp, [s_t, misc], [s_t], scale=misc[:, col + 8:col + 9])
                        self.act(s_t[:, 0:T], s_t[:, 0:T], AF.Sqrt, [s_t], [s_t], scale=-1.0, bias=1.0)
                        self.tt("pool", bx_t[:, 0:T], bx_t[:, 0:T], xc[:, 0:T], ALU.mult, [bx_t, xc], [bx_t])
                        self.tt("dve", bx_t[:, 0:T], bx_t[:, 0:T], s_t[:, 0:T], ALU.mult, [bx_t, s_t], [bx_t])
                        h = hs[d]
                        if seq == 0:
                            init = prm[:, P_H0 + 4 * d + j:P_H0 + 4 * d + j + 1]
                        else:
                            init = 0.0
                        if d == 0:
                            self.scan(h[:, 0:T], a_t[:, 0:T], bx_t[:, 0:T], init, [a_t, bx_t, prm], [h])
                        else:
                            self.scan(h[:, 0:T][:, ::-1], a_t[:, 0:T][:, ::-1], bx_t[:, 0:T][:, ::-1], init, [a_t, bx_t, prm], [h])
                        if seq > 0:
                            col_o = j * 4 + (seq - 1) * 2 + d
                            te = T - 1 if d == 0 else 0
                            self.cp("pool", self.stl_t[:, col_o:col_o + 1], h[:, te:te + 1], [h], [self.stl_t])
                        yield
                    self.tt("pool", hs[0][:, 0:T], hs[0][:, 0:T], hs[1][:, 0:T], ALU.add, [hs[0], hs[1]], [hs[0]])
                    self.tt("dve", yb[:, 0:T], hs[0][:, 0:T], gt[:, 0:T], ALU.mult, [hs[0], gt], [yb])
                    self.dma(self.y_scr[:, 4 + j, g0:g0 + T], yb[:, 0:T], [yb], [self.y_scr])
            self.dma(self.stl_o[:], self.stl_t[:], [self.stl_t], [self.stl_o])

    def phaseB_chain(self, st):
        if True:
            NB = 3
            def ring(name, dt=BF16):
                return [self.sb2(st, "%s%d" % (name, i), [128, 512], dt) for i in range(NB)]
            art, rrt, ttt, akt, mrbt, mrkt, bh, kh, vt = [ring(n) for n in
                                                          ("c_art", "c_rrt", "c_ttt", "c_akt", "c_mrbt", "c_mrkt", "c_bh", "c_kh", "c_vt")]
            pend = ring("c_pend", F32)
            Hf = self.sb2(st, "Hf", [128, 512], F32)
            Hb = self.sb2(st, "Hb", [128, 512], BF16)
            Zs = self.sb2(st, "Zs", [128, 512], BF16)
            Us = self.sb2(st, "Us", [128, 512], BF16)
            Yt = [self.sb2(st, "Yt%d" % i, [128, 512], F32) for i in range(2)]
            tmpH = self.sb2(st, "tmpH", [128, 512], F32)
            hs_ = lambda h: slice(h * 64, (h + 1) * 64)
            steps = []
            for (seq, cbase, n) in ((0, 0, 32), (1, 32, 4), (2, 36, 4)):
                for i in range(n):
                    steps.append((seq, cbase, n, i))

            def loads(k):
                seq, cbase, n, i = steps[k]
                r = k % NB
                for d in range(2):
                    sl = slice(d * 64, (d + 1) * 64)
                    c = cbase + i if d == 0 else cbase + n - 1 - i
                    for (tl, scr) in ((art, self.art_scr), (rrt, self.rrt_scr), (ttt, self.ttt_scr), (akt, self.akt_scr),
                                      (mrbt, self.mrbt_scr), (mrkt, self.mrkt_scr), (bh, self.bh_scr), (kh, self.kh_scr),
                                      (pend, self.pend_scr)):
                        self.dma(tl[r][d][sl, :], scr[c, sl, :], [scr], [tl[r][d]], eng="sp")
                    self.dma(vt[r][d][sl, :], self.vt_scr[c], [self.vt_scr], [vt[r][d]], eng="sp")

            loads(0)
            for k in range(len(steps)):
                seq, cbase, n, i = steps[k]
                step = k + 1
                r = k % NB
                if k + 1 < len(steps):
                    loads(k + 1)
                if i == 0:
                    for d in range(2):
                        sl = slice(d * 64, (d + 1) * 64)
                        if seq == 0:
                            self.dma(Hf[d][sl, :], self.h0r[sl, :], [], [Hf[d]], eng="act")
                        else:
                            self.memset("dve", Hf[d][sl, :], 0.0, [Hf[d]])
                        self.cp("dve", Hb[d][sl, :], Hf[d][sl, :], [Hf[d]], [Hb[d]])
                if True:
                    ctx = []
                    for d in range(2):
                        sl = slice(d * 64, (d + 1) * 64)
                        c = cbase + i if d == 0 else cbase + n - 1 - i
                        ops = tuple(x[r][d] for x in (art, rrt, ttt, akt, mrbt, mrkt, bh, kh, vt, pend))
                        ctx.append((d, sl, c, ops))
                    pZs = {}
                    for (d, sl, c, (A_, R_, T_, AK_, MRB_, MRK_, B_, K_, V_, PE_)) in ctx:
                        H_, Z_ = Hb[d], Zs[d]
                        pZ = self.ps()
                        for h in range(8):
                            self.mm(pZ[sl, hs_(h)], A_[sl, hs_(h)], H_[sl, hs_(h)], True, False, [A_, H_], [pZ])
                            self.mm(pZ[sl, hs_(h)], AK_[sl, hs_(h)], V_[sl, hs_(h)], False, True, [AK_, V_], [pZ])
                        pZs[d] = pZ
                    pYs = {}
                    for (d, sl, c, (A_, R_, T_, AK_, MRB_, MRK_, B_, K_, V_, PE_)) in ctx:
                        H_ = Hb[d]
                        pY = self.ps()
                        pYs[d] = pY
                    for (d, sl, c, ops) in ctx:
                        self.cp("act", Zs[d][sl, :], pZs[d][sl, :], [pZs[d]], [Zs[d]])
                    pUs = {}
                    for (d, sl, c, (A_, R_, T_, AK_, MRB_, MRK_, B_, K_, V_, PE_)) in ctx:
                        Z_ = Zs[d]
                        pU = self.ps()
                        for h in range(8):
                            self.mm(pU[sl, hs_(h)], T_[sl, hs_(h)], Z_[sl, hs_(h)], True, True, [T_, Z_], [pU])
                        pUs[d] = pU
                    for (d, sl, c, ops) in ctx:
                        self.cp("act", Us[d][sl, :], pUs[d][sl, :], [pUs[d]], [Us[d]])
                    pHs = {}
                    for (d, sl, c, (A_, R_, T_, AK_, MRB_, MRK_, B_, K_, V_, PE_)) in ctx:
                        U_ = Us[d]
                        self.tt("dve", tmpH[d][sl, :], Hf[d][sl, :], PE_[sl, :], ALU.mult, [Hf[d], PE_], [tmpH[d]])
                        pH = self.ps()
                        for h in range(8):
                            self.mm(pH[sl, hs_(h)], B_[sl, hs_(h)], U_[sl, hs_(h)], True, False, [B_, U_], [pH])
                            self.mm(pH[sl, hs_(h)], K_[sl, hs_(h)], V_[sl, hs_(h)], False, True, [K_, V_], [pH])
                        pHs[d] = pH
                    for (d, sl, c, (A_, R_, T_, AK_, MRB_, MRK_, B_, K_, V_, PE_)) in ctx:
                        H_, U_ = Hb[d], Us[d]
                        pY = pYs[d]
                        for h in range(8):
                            self.mm(pY[sl, hs_(h)], R_[sl, hs_(h)], H_[sl, hs_(h)], True, False, [R_, H_], [pY])
                            self.mm(pY[sl, hs_(h)], MRB_[sl, hs_(h)], U_[sl, hs_(h)], False, False, [MRB_, U_], [pY])
                            self.mm(pY[sl, hs_(h)], MRK_[sl, hs_(h)], V_[sl, hs_(h)], False, True, [MRK_, V_], [pY])
                    for (d, sl, c, ops) in ctx:
                        self.tt("dve", Hf[d][sl, :], tmpH[d][sl, :], pHs[d][sl, :], ALU.add, [tmpH[d], pHs[d]], [Hf[d]])
                        self.cp("dve", Hb[d][sl, :], Hf[d][sl, :], [Hf[d]], [Hb[d]])
                    for (d, sl, c, ops) in ctx:
                        y = Yt[step % 2][d]
                        self.cp("act", y[sl, :], pYs[d][sl, :], [pYs[d]], [y])
                        self.dma(self.ytok_scr[d, c * 64:(c + 1) * 64, :], y[sl, :], [y], [self.ytok_scr], eng="act")
                if seq > 0 and i == n - 1:
                    self.dma(self.str_o[seq - 1], Hf[0][:], [Hf[0], Hf[1]], [self.str_o], eng="act")
                yield [(0, cbase + i), (1, cbase + n - 1 - i)]

    def phaseB_post(self, st):
        prm, cst = self.prm_t, self.cst_t
        ident = self.ident
        if True:
            yf = [self.sb(st, "yf%d" % i, [128, 512], F32) for i in range(2)]
            yb2 = [self.sb(st, "yb2%d" % i, [128, 512], F32) for i in range(2)]
            cen = self.sb(st, "cen", [128, 8, 64], F32)
            sqv = self.sb(st, "sqv", [128, 8, 64], F32)
            mean = self.sb(st, "mean", [128, 8], F32)
            var = self.sb(st, "var", [128, 8], F32)
            gl = [self.sb(st, "gl%d" % i, [128, 4, 128], BF16) for i in range(2)]
            bl = [self.sb(st, "bl%d" % i, [128, 4, 128], BF16) for i in range(2)]
            ynT = self.sb(st, "ynT", [128, 4, 128], F32)
            yo = [self.sb(st, "yo%d" % i, [128, 4, 128], BF16) for i in range(2)]
            it = -1
            blk = yield
            while True:
                it += 1
                g0 = blk * 128
                a, b = yf[it % 2], yb2[it % 2]
                g_, b_ = gl[it % 2], bl[it % 2]
                o = yo[it % 2]
                self.dma(a[:], self.ytok_scr[0, g0:g0 + 128, :], [self.ytok_scr], [a])
                self.dma(b[:], self.ytok_scr[1, g0:g0 + 128, :], [self.ytok_scr], [b], eng="act")
                self.dma(g_[:], self.g_scr[:, :, g0:g0 + 128], [self.g_scr], [g_])
                self.dma(b_[:], self.bon_scr[:, :, g0:g0 + 128], [self.bon_scr], [b_], eng="act")
                a3 = a[:].rearrange("p (h v) -> p h v", v=64)
                self.tt("pool", a[:], a[:], b[:], ALU.add, [a, b], [a])
                self.S.op("dve", lambda e, a3=a3: e.tensor_reduce(out=mean[:], in_=a3, op=ALU.add, axis=mybir.AxisListType.X),
                          [a.b], [mean.b])
                self.tsc("dve", mean[:], mean[:], 1.0 / 64, ALU.mult, [mean], [mean])
                self.tt("dve", cen[:], a3, mean[:].unsqueeze(2).to_broadcast([128, 8, 64]), ALU.subtract, [a, mean], [cen])
                self.tt("pool", sqv[:], cen[:], cen[:], ALU.mult, [cen], [sqv])
                self.S.op("dve", lambda e: e.tensor_reduce(out=var[:], in_=sqv[:], op=ALU.add, axis=mybir.AxisListType.X),
                          [sqv.b], [var.b])
                self.act(var[:], var[:], AF.Sqrt, [var, self.epsT], [var], scale=1.0 / 64, bias=self.epsT[:, 1:2])
                self.S.op("dve", lambda e: e.reciprocal(out=var[:], in_=var[:]), [var.b], [var.b])
                self.tt("dve", cen[:], cen[:], var[:].unsqueeze(2).to_broadcast([128, 8, 64]), ALU.mult, [cen, var], [cen])
                p = self.ps()
                cen2 = cen[:].rearrange("p h v -> p (h v)")
                for j in range(4):
                    self.tr(p[:, j * 128:(j + 1) * 128], cen2[:, j * 128:(j + 1) * 128], ident[:], [cen, ident], [p])
                for j in range(4):
                    self.act(ynT[:, j, :], p[:, j * 128:(j + 1) * 128], AF.Identity, [p, prm], [ynT],
                             scale=prm[:, P_LNG + j:P_LNG + j + 1], bias=prm[:, P_LNB + j:P_LNB + j + 1])
                self.tt("pool", ynT[:], ynT[:], b_[:], ALU.add, [ynT, b_], [ynT])
                self.tt("dve", o[:], ynT[:], g_[:], ALU.mult, [ynT, g_], [o])
                self.dma(self.y_scr[:, 0:4, g0:g0 + 128], o[:], [o], [self.y_scr])
                blk = yield

    def phaseC(self):
        prm, cst = self.prm_t, self.cst_t
        ident, ones_bf = self.ident, self.ones_bf
        with contextlib.ExitStack() as st:
            pass
        import os
        kcc = int(os.environ.get("KCC", "99"))
        if kcc == 0:
            return
        with contextlib.ExitStack() as st:
            wout = self.sb(st, "wout", [128, 8, D], BF16)
            wsrc = self.w_out[:].rearrange("(kc p) n -> p kc n", p=128)
            with contextlib.ExitStack() as st2:
                stg = [self.sb(st2, "stgD%d" % i, [128, 8, 256], F32) for i in range(2)]
                for cb in range(4):
                    s = stg[cb % 2]
                    self.dma(s[:], wsrc[:, :, cb * 256:(cb + 1) * 256], [], [s])
                    self.cp("act", wout[:, :, cb * 256:(cb + 1) * 256], s[:], [s], [wout])
            self.S.barrier()
            yT = self.sb(st, "yT_c", [128, 8, TC], BF16)
            o1T = self.sb(st, "o1T", [128, 8, TC], F32)
            sq1 = self.sb(st, "sq1_c", [128, 8, TC], BF16)
            oT = self.sb(st, "oT", [128, 8, TC], F32)
            sq = self.sb(st, "sq_c", [128, 8, TC], BF16)
            xTs = [self.sb(st, "xT_c%d" % i, [128, 8, TC], F32) for i in range(2)]
            h2 = self.sb(st, "h2", [128, 8, TC], BF16)
            f = self.sb(st, "f_c", [128, 32, TC], BF16)
            otoks = [self.sb(st, "otok%d" % i, [128, 2, D], F32) for i in range(1)]
            rstd = self.sb(st, "rstd_c", [128, TC], F32)
            tmp = [self.sb(st, "tmpC%d" % i, [128, TC], F32) for i in range(2)]
            NW = 3
            w1r = [self.sb(st, "w1r%d" % i, [128, 8, 512], BF16) for i in range(NW)]
            w2r = [self.sb(st, "w2r%d" % i, [128, 32, 128], BF16) for i in range(2)]
            ntile = min(NTOK // TC, kcc)
            wseq = []
            for ti_ in range(ntile):
                wseq += [("w1", b_) for b_ in range(8)] + [("w2", b_) for b_ in range(8)]
            wstate = {"issued": 0, "w1": 0, "w2": 0}
            wbuf = {}

            def issue_upto(n):
                while wstate["issued"] < min(n, len(wseq)):
                    k_ = wstate["issued"]
                    kind, b_ = wseq[k_]
                    ring = w1r if kind == "w1" else w2r
                    buf = ring[wstate[kind] % len(ring)]
                    wstate[kind] += 1
                    scr = self.w1_scr if kind == "w1" else self.w2_scr
                    self.dma(buf[:], scr[b_], [scr], [buf], eng="sp" if k_ % 2 == 0 else "act")
                    wbuf[k_] = buf
                    wstate["issued"] += 1

            def rms(sqt, R):
                p = self.ps()
                for j in range(8):
                    self.mm(p[:], ones_bf[:], sqt[:, j, :], j == 0, j == 7, [ones_bf, sqt], [p])
                self.act(rstd[:], p[:], AF.Sqrt, [p, self.epsT], [rstd], scale=1.0 / D, bias=self.epsT[:, 0:1])
                self.S.op("dve", lambda e: e.reciprocal(out=rstd[:], in_=rstd[:]), [rstd.b], [rstd.b])

            def resid(gg, mc, oT, xT):
                for j in range(8):
                    t = tmp[j % 2]
                    self.tt("dve", t[:], oT[:, j, :], rstd[:], ALU.mult, [oT, rstd], [t])
                    self.stt(xT[:, j, :], t[:], gg[:, j, mc:mc + 1], xT[:, j, :], ALU.mult, ALU.add, [t, gg, xT], [xT])

            def head(tj):
                gj = tj * TC
                mcj = 0 if gj < TS else 1
                xT = xTs[tj % 2]
                self.dma(xT[:], self.xT_scr[:, :, gj:gj + TC], [self.xT_scr], [xT], eng="act")
                yield
                rms(sq1, None)
                yield
                for j in range(8):
                    t = tmp[j % 2]
                    self.tt("dve", t[:], o1T[:, j, :], rstd[:], ALU.mult, [o1T, rstd], [t])
                    self.stt(xT[:, j, :], t[:], self.gg1[:, j, mcj:mcj + 1], xT[:, j, :], ALU.mult, ALU.add, [t, self.gg1, xT], [xT])
                    self.act(sq1[:, j, :], xT[:, j, :], AF.Square, [xT], [sq1])
                    if j % 2 == 1:
                        yield
                rms(sq1, None)
                yield
                for j in range(8):
                    t = tmp[j % 2]
                    self.tt("dve", t[:], xT[:, j, :], rstd[:], ALU.mult, [xT, rstd], [t])
                    self.act(h2[:, j, :], t[:], AF.Identity, [t, self.gs2, self.modT], [h2],
                             scale=self.gs2[:, j, mcj:mcj + 1], bias=self.modT[:, 24 + j, mcj:mcj + 1])
                    if j % 2 == 1:
                        yield

            def wout_stage(tj):
                gj = tj * TC
                self.dma(yT[:], self.y_scr[:, :, gj:gj + TC], [self.y_scr], [yT])
                for m in range(8):
                    p = self.ps()
                    for kc in range(8):
                        self.mm(p[:], wout[:, kc, m * 128:(m + 1) * 128], yT[:, kc, :], kc == 0, kc == 7, [wout, yT], [p])
                    self.cp("act", o1T[:, m, :], p[:], [p], [o1T])
                    self.act(sq1[:, m, :], p[:], AF.Square, [p], [sq1])

            def tail(g0, mc, xT):
                rms(sq, None)
                yield
                resid(self.gg2, mc, oT, xT)
                yield
                for hh in range(2):
                    otok = otoks[0]
                    for s2 in range(2):
                        s_ = hh * 2 + s2
                        for half in range(2):
                            p = self.ps()
                            for jj in range(4):
                                j = half * 4 + jj
                                self.tr(p[:, jj * 128:(jj + 1) * 128], xT[:, j, s_ * 128:(s_ + 1) * 128], ident[:], [xT, ident], [p])
                            self.cp("act" if half == 0 else "dve", otok[:, s2, half * 512:(half + 1) * 512], p[:], [p], [otok])
                            yield
                    if g0 < TS:
                        dst = self.ys[g0 + hh * 256:g0 + (hh + 1) * 256, :].rearrange("(s p) f -> p s f", p=128)
                        self.dma(dst, otok[:], [otok], [self.ys])
                    else:
                        dst = self.yp[hh * 256:(hh + 1) * 256, :].rearrange("(s p) f -> p s f", p=128)
                        self.dma(dst, otok[:], [otok], [self.yp])

            tg = None
            wi = 0
            for ti in range(NTOK // TC):
                if ti >= kcc:
                    break
                g0 = ti * TC
                mc = 0 if g0 < TS else 1
                xT = xTs[ti % 2]
                if ti == 0:
                    wout_stage(0)
                    for _ in head(0):
                        pass
                issue_upto(ti * 16 + 3)
                for blk in range(8):
                    if tg is not None:
                        for _ in range(2):
                            try:
                                next(tg)
                            except StopIteration:
                                tg = None
                                break
                    issue_upto(ti * 16 + blk + 3)
                    w = wbuf[ti * 16 + blk]
                    for c4 in range(4):
                        fc = blk * 4 + c4
                        p = self.ps()
                        for kc in range(8):
                            self.mm(p[:], w[:, kc, c4 * 128:(c4 + 1) * 128], h2[:, kc, :], kc == 0, kc == 7, [w, h2], [p])
                        t = tmp[fc % 2]
                        self.act(t[:], p[:], AF.Relu, [p], [t])
                        self.tt("pool" if fc % 2 == 0 else "dve", f[:, fc, :], t[:], t[:], ALU.mult, [t], [f])
                if tg is not None:
                    for _ in tg:
                        pass
                    tg = None
                hg = None
                if ti + 1 < ntile:
                    wout_stage(ti + 1)
                    hg = head(ti + 1)
                for m in range(8):
                    if hg is not None:
                        for _ in range(2):
                            try:
                                next(hg)
                            except StopIteration:
                                hg = None
                                break
                    issue_upto(ti * 16 + 8 + m + 2)
                    w = wbuf[ti * 16 + 8 + m]
                    p = self.ps()
                    for fc in range(32):
                        self.mm(p[:], w[:, fc, :], f[:, fc, :], fc == 0, fc == 31, [w, f], [p])
                    self.cp("act", oT[:, m, :], p[:], [p], [oT])
                    self.act(sq[:, m, :], p[:], AF.Square, [p], [sq])
                if hg is not None:
                    for _ in hg:
                        pass
                tg = tail(g0, mc, xT)
                if ti + 1 >= ntile:
                    for _ in tg:
                        pass
                    tg = None


def _fm(v):
    v = np.asarray(v, np.float32).reshape(-1, 128)
    return np.ascontiguousarray(v.T)


def _pos_embed():
    def sincos(pos, dim):
        omega = (1.0 / (10000.0 ** (np.arange(dim // 2, dtype=np.float32) / np.float32(dim // 2)))).astype(np.float32)
        ang = pos.astype(np.float32)[:, None] * omega[None, :]
        return np.concatenate([np.sin(ang), np.cos(ang)], axis=-1).astype(np.float32)
    rows = TS // 64
    half = D // 2
    e_row = sincos(np.arange(rows), half)
    e_col = sincos(np.arange(64), half)
    emb = np.concatenate([np.broadcast_to(e_row[:, None, :], (rows, 64, half)),
                          np.broadcast_to(e_col[None, :, :], (rows, 64, half))], axis=-1)
    return np.ascontiguousarray(emb.reshape(rows * 64, D).astype(np.float32))


def _consts():
    c = np.zeros((128, NCST), np.float32)
    c[:, C_ID:C_ID + 128] = np.eye(128, dtype=np.float32)
    ob = np.zeros((128, 128), np.float32)
    ob[:64, :64] = 1.0
    ob[64:, 64:] = 1.0
    c[:, C_OB:C_OB + 128] = ob
    s = np.arange(64)[:, None]
    t = np.arange(64)[None, :]
    msi = np.zeros((128, 2, 64), np.float32)
    msi[:64, 0] = (s < t)
    msi[:64, 1] = (s <= t)
    msi[64:, 0] = (s > t)
    msi[64:, 1] = (s >= t)
    c[:, C_MSI:C_MSI + 128] = msi.reshape(128, 128)
    ml = np.zeros((128, 64), np.float32)
    ml[:64] = (t < s)
    ml[64:] = (t > s)
    c[:, C_ML:C_ML + 64] = ml
    ids = np.zeros((128, 64), np.float32)
    ids[:64] = np.eye(64)
    ids[64:] = np.eye(64)
    c[:, C_IDS:C_IDS + 64] = ids
    c[:64, C_MSI1:C_MSI1 + 128] = msi[64:].reshape(64, 128)
    c[:64, C_ML1:C_ML1 + 64] = ml[64:]
    tt_ = np.arange(TT)
    c[:, C_RMF:C_RMF + TT] = (tt_ % 64 != 0).astype(np.float32)[None, :]
    c[:, C_RMB:C_RMB + TT] = (tt_ % 64 != 63).astype(np.float32)[None, :]
    return c


_NC_CACHE = {}


def kernel(x_prompt, x_sample, c, state_rwkv, state_lru, c_ctx, w_mod, b_mod,
           g_pre_mix, g_post_mix, g_pre_mlp, g_post_mlp, w_in,
           rwkv_w0, rwkv_w_up, rwkv_a0, rwkv_a_up, rwkv_g_up, rwkv_k_k, rwkv_k_a, rwkv_r_k,
           rwkv_lnx_g, rwkv_lnx_b, lru_conv_w, lru_conv_b, lru_wa, lru_ba, lru_wx, lru_bx,
           lru_lambda, w_out, w_mlp1, w_mlp2, _debug=False):
    f = lambda a: np.ascontiguousarray(np.asarray(a, np.float32))
    x_prompt, x_sample, c, state_rwkv, state_lru, c_ctx = map(f, (x_prompt, x_sample, c, state_rwkv, state_lru, c_ctx))
    nc = K(debug=_debug).build()
    pe = _pos_embed()
    cst = _consts()
    shared = {
        "pe": pe, "cst": cst,
        "w_mod": f(w_mod[0]), "w_in": f(w_in[0]), "w_out": f(w_out[0]), "w1": f(w_mlp1[0]), "w2": f(w_mlp2[0]),
        "wup": f(rwkv_w_up[0]).reshape(128, 512), "aup": f(rwkv_a_up[0]).reshape(128, 512), "gup": f(rwkv_g_up[0]),
        "lwa": f(lru_wa[0]), "lwx": f(lru_wx[0]),
    }
    prm0 = np.zeros((128, NPRM), np.float32)
    prm0[:, P_GPRE:P_GPRE + 8] = _fm(g_pre_mix[0])
    prm0[:, P_GPOST:P_GPOST + 8] = _fm(g_post_mix[0])
    prm0[:, P_GPRE2:P_GPRE2 + 8] = _fm(g_pre_mlp[0])
    prm0[:, P_GPOST2:P_GPOST2 + 8] = _fm(g_post_mlp[0])
    prm0[:, P_BMOD:P_BMOD + 48] = _fm(b_mod[0])
    for d in range(2):
        prm0[:, P_W0 + 4 * d:P_W0 + 4 * d + 4] = _fm(rwkv_w0[0, d])
        prm0[:, P_A0 + 4 * d:P_A0 + 4 * d + 4] = _fm(rwkv_a0[0, d])
        prm0[:, P_BA + 4 * d:P_BA + 4 * d + 4] = _fm(lru_ba[0, d])
        prm0[:, P_BX + 4 * d:P_BX + 4 * d + 4] = _fm(lru_bx[0, d])
        prm0[:, P_LAM + 4 * d:P_LAM + 4 * d + 4] = _fm(lru_lambda[0, d])
    prm0[:, P_KK:P_KK + 4] = _fm(rwkv_k_k[0])
    prm0[:, P_KA:P_KA + 4] = _fm(rwkv_k_a[0])
    prm0[:, P_RK:P_RK + 4] = _fm(np.asarray(rwkv_r_k[0]).reshape(-1))
    prm0[:, P_LNG:P_LNG + 4] = _fm(rwkv_lnx_g[0])
    prm0[:, P_LNB:P_LNB + 4] = _fm(rwkv_lnx_b[0])
    for i in range(4):
        prm0[:, P_CW + 4 * i:P_CW + 4 * i + 4] = _fm(lru_conv_w[0, i])
    prm0[:, P_CB:P_CB + 4] = _fm(lru_conv_b[0])
    in_maps = []
    for i in range(8):
        prm = prm0.copy()
        for d in range(2):
            prm[:, P_H0 + 4 * d:P_H0 + 4 * d + 4] = _fm(state_lru[i, 0, d])
        cT = np.zeros((128, 8, 2), np.float32)
        cT[:, :, 0] = _fm(c[i])
        cT[:, :, 1] = _fm(c_ctx)
        h0 = np.ascontiguousarray(state_rwkv[i, 0].transpose(0, 3, 1, 2)).reshape(128, 512)
        m = dict(shared)
        m.update({"xs": x_sample[i], "xp": np.ascontiguousarray(x_prompt[2 * i:2 * i + 2].reshape(2 * TP, D)),
                  "cT": cT.reshape(128, 16), "h0r": h0, "prm": prm})
        in_maps.append(m)
    res = run_bass_kernel_spmd(nc, in_maps, core_ids=list(range(8)))
    R = res.results
    y_prompt = np.zeros((16, TP, D), np.float32)
    y_sample = np.zeros((8, TS, D), np.float32)
    st_r = np.zeros((16, 1, 2, 8, 64, 64), np.float32)
    st_l = np.zeros((16, 1, 2, 512), np.float32)
    for i in range(8):
        r = R[i]
        y_sample[i] = r["ys"]
        y_prompt[2 * i:2 * i + 2] = r["yp"].reshape(2, TP, D)
        so = r["str_o"].reshape(2, 2, 64, 8, 64)
        st_r[2 * i:2 * i + 2, 0] = so.transpose(0, 1, 3, 4, 2)
        sl = r["stl_o"].reshape(128, 4, 2, 2)
        st_l[2 * i:2 * i + 2, 0] = sl.transpose(2, 3, 1, 0).reshape(2, 2, 512)
    if _debug:
        return (y_prompt, y_sample, st_r, st_l), R
    return (y_prompt, y_sample, st_r, st_l)
```

```python
import contextlib
import numpy as np
import concourse.bass as bass
import concourse.mybir as mybir
from concourse.bass_utils import run_bass_kernel_spmd

F32 = mybir.dt.float32
BF16 = mybir.dt.bfloat16
F32R = mybir.dt.float32r
AF = mybir.ActivationFunctionType
ALU = mybir.AluOpType

D = 1024
TS = 2048
TP = 256
NTOK = TS + 2 * TP
NCH = NTOK // 64
DIN = 2944
DFF = 4096
LAM = float(np.exp(-0.5))
EPS = 1e-6
LNX_EPS = 64e-5
TT = 256
TC = 512
GELU_C = 1.5957691216057308

P_GPRE, P_GPOST, P_GPRE2, P_GPOST2 = 0, 8, 16, 24
P_BMOD = 32
P_W0, P_A0 = 80, 88
P_KK, P_KA, P_RK, P_LNG, P_LNB = 96, 100, 104, 108, 112
P_CW, P_CB = 116, 132
P_BA, P_BX, P_LAM, P_H0 = 136, 144, 152, 160
NPRM = 168
C_ID, C_OB, C_MSI, C_ML, C_IDS, C_RMF, C_RMB = 0, 128, 256, 384, 448, 512, 768
C_MSI1, C_ML1 = 1024, 1152
NCST = 1216


class Buf:
    __slots__ = ("name", "lw", "rd", "excl", "multi", "ws")

    def __init__(self, name=""):
        self.name = name
        self.lw = None
        self.rd = {}
        self.excl = False
        self.multi = False
        self.ws = {}


class TL:
    def __init__(self, t, name=""):
        self.t = t
        self.b = Buf(name)

    def __getitem__(self, k):
        return self.t[k]


class Sched:
    ENGS = ("pe", "act", "dve", "pool", "sp")

    def __init__(self, nc):
        self.nc = nc
        self.streams = {e: [] for e in self.ENGS}
        self.cnt = {}
        self.waited = {e: {} for e in self.ENGS}
        self.n_ops = 0
        self.dma_n = {e: 0 for e in self.ENGS}
        self.NSLOT = {"sp": 44, "act": 44, "pool": 4, "dve": 2, "pe": 2}

    def _deps(self, eng, reads, writes):
        need = {}
        for b in reads:
            if b.multi:
                for s, v in b.ws.items():
                    if need.get(s, 0) < v:
                        need[s] = v
                continue
            if b.lw is not None:
                s, v = b.lw
                if need.get(s, 0) < v:
                    need[s] = v
            if b.excl:
                for s, v in b.rd.items():
                    if s != eng and need.get(s, 0) < v:
                        need[s] = v
        for b in writes:
            if b.multi:
                continue
            if b.lw is not None:
                s, v = b.lw
                if need.get(s, 0) < v:
                    need[s] = v
            for s, v in b.rd.items():
                if need.get(s, 0) < v:
                    need[s] = v
        out = []
        w = self.waited[eng]
        for s, v in need.items():
            if s == "pe" and eng == "pe":
                continue
            if w.get(s, 0) >= v:
                continue
            w[s] = v
            out.append((s, v))
        return out

    def op(self, eng, fn, reads=(), writes=(), dma=False):
        reads = [r.b if isinstance(r, TL) else r for r in reads]
        writes = [r.b if isinstance(r, TL) else r for r in writes]
        waits = self._deps(eng, reads, writes)
        if dma:
            slot = self.dma_n[eng] % self.NSLOT[eng]
            self.dma_n[eng] += 1
            sem = "%s_d%d" % (eng, slot)
            prev = self.cnt.get(sem, 0)
            if prev > 0 and self.waited[eng].get(sem, 0) < prev:
                self.waited[eng][sem] = prev
                waits.append((sem, prev))
        else:
            sem = eng
        inc = 16 if dma else 1
        self.cnt[sem] = self.cnt.get(sem, 0) + inc
        val = self.cnt[sem]
        self.streams[eng].append((waits, fn, sem, inc))
        self.n_ops += 1
        for b in reads:
            if b.rd.get(sem, 0) < val:
                b.rd[sem] = val
        for b in writes:
            if b.multi:
                if b.ws.get(sem, 0) < val:
                    b.ws[sem] = val
                continue
            b.lw = (sem, val)
            b.rd = {}
        return val

    def barrier_on(self, tl):
        if tl.b.lw is None:
            return
        sname, v = tl.b.lw
        for e in ("sp", "act", "pool"):
            if self.waited[e].get(sname, 0) < v:
                self.waited[e][sname] = v
                self.streams[e].append(([(sname, v)], None, None, 0))

    def barrier(self):
        snap = dict(self.cnt)
        for e in self.ENGS:
            waits = []
            for s, v in snap.items():
                if s == "pe" and e == "pe":
                    continue
                if self.waited[e].get(s, 0) < v:
                    self.waited[e][s] = v
                    waits.append((s, v))
            if waits:
                self.streams[e].append((waits, None, None, 0))

    def emit(self):
        nc = self.nc
        sems = {}
        with contextlib.ExitStack() as st:
            for s in self.cnt:
                sems[s] = st.enter_context(nc.semaphore(s))
            block = st.enter_context(nc.Block())
            engmap = {"pe": block.tensor, "act": block.scalar, "dve": block.vector,
                      "pool": block.gpsimd, "sp": block.sync}
            for e in self.ENGS:
                stream = self.streams[e]
                if not stream:
                    continue

                def body(eng, stream=stream):
                    for waits, fn, sem, inc in stream:
                        for s, v in waits:
                            eng.wait_ge(sems[s], v)
                        if fn is not None:
                            fn(eng).then_inc(sems[sem], inc)
                engmap[e](body)


class K:
    def __init__(self, debug=False, stop_after=None):
        self.debug = debug
        self.stop_after = stop_after
        import os
        self.cutk = int(os.environ.get("KCUT", "0"))
        self.cutm = int(os.environ.get("KCUTM", "99"))
        self.ktiles = int(os.environ.get("KTILES", "99"))
        self.kskip = os.environ.get("KSKIP", "").split(",")
        self.nc = bass.Bass("TRN2", target_bir_lowering=False)
        self.S = Sched(self.nc)
        self.es = contextlib.ExitStack()
        self.psr = 0
        self.rr = {}

    def dram(self, name, shape, dt, kind="Internal"):
        t = TL(self.nc.dram_tensor(name, list(shape), dt, kind=kind).ap(), name)
        t.b.multi = True
        return t

    def sb(self, st, name, shape, dt):
        return TL(st.enter_context(self.nc.sbuf_tensor(name, list(shape), dt)), name)

    def sb2(self, st, name, shape, dt):
        t = st.enter_context(self.nc.sbuf_tensor(name, list(shape), dt))
        return [TL(t, name + "_lo"), TL(t, name + "_hi")]

    def ps(self):
        p = self.psum[self.psr % 8]
        self.psr += 1
        return p

    def mm(self, out, lhsT, rhs, start, stop, R, W):
        self.S.op("pe", lambda e: e.matmul(out, lhsT=lhsT, rhs=rhs, start=start, stop=stop), R, W)

    def tr(self, out, in_, ident, R, W):
        self.S.op("pe", lambda e: e.transpose(out, in_, ident), R, W)

    def act(self, out, in_, func, R, W, scale=1.0, bias=None, eng="act"):
        if bias is None:
            self.S.op("act", lambda e: e.activation(out=out, in_=in_, func=func, scale=scale), R, W)
        else:
            self.S.op("act", lambda e: e.activation(out=out, in_=in_, func=func, scale=scale, bias=bias), R, W)

    def tt(self, eng, out, in0, in1, op, R, W):
        self.S.op(eng, lambda e: e.tensor_tensor(out=out, in0=in0, in1=in1, op=op), R, W)

    def tsc(self, eng, out, in0, s1, op0, R, W, s2=None, op1=None):
        if op1 is None:
            self.S.op(eng, lambda e: e.tensor_scalar(out=out, in0=in0, scalar1=s1, scalar2=None, op0=op0), R, W)
        else:
            self.S.op(eng, lambda e: e.tensor_scalar(out=out, in0=in0, scalar1=s1, scalar2=s2, op0=op0, op1=op1), R, W)

    def stt(self, out, in0, scalar, in1, op0, op1, R, W):
        self.S.op("dve", lambda e: e.scalar_tensor_tensor(out=out, in0=in0, scalar=scalar, in1=in1, op0=op0, op1=op1), R, W)

    def cp(self, eng, out, in_, R, W):
        if eng == "act":
            self.S.op("act", lambda e: e.activation(out=out, in_=in_, func=AF.Copy), R, W)
        else:
            self.S.op(eng, lambda e: e.tensor_copy(out=out, in_=in_), R, W)

    def scan(self, out, d0, d1, init, R, W):
        self.S.op("dve", lambda e: e.tensor_tensor_scan(out=out, data0=d0, data1=d1, initial=init,
                                                        op0=ALU.mult, op1=ALU.add), R, W)

    def dma(self, out, in_, R, W, eng="sp"):
        self.S.op(eng, lambda e: e.dma_start(out=out, in_=in_), R, W, dma=True)

    def memset(self, eng, ap, val, W):
        self.S.op(eng, lambda e: e.memset(ap, val), (), W)

    def pick(self, key, engs):
        i = self.rr.get(key, 0)
        self.rr[key] = i + 1
        return engs[i % len(engs)]

    def build(self):
        nc = self.nc
        I = lambda n, s, dt=F32: self.dram(n, s, dt, "ExternalInput")
        O = lambda n, s, dt=F32: self.dram(n, s, dt, "ExternalOutput")
        self.xs = I("xs", [TS, D])
        self.xp = I("xp", [2 * TP, D])
        self.pe = I("pe", [TS, D])
        self.cT = I("cT", [128, 16])
        self.h0r = I("h0r", [128, 512])
        self.prm = I("prm", [128, NPRM])
        self.cst = I("cst", [128, NCST])
        self.w_mod = I("w_mod", [D, 6 * D])
        self.w_in = I("w_in", [D, DIN])
        self.w_out = I("w_out", [D, D])
        self.w1 = I("w1", [D, DFF])
        self.w2 = I("w2", [DFF, D])
        self.wup = I("wup", [128, 512])
        self.aup = I("aup", [128, 512])
        self.gup = I("gup", [128, 512])
        self.lwa = I("lwa", [2, 8, 64, 64])
        self.lwx = I("lwx", [2, 8, 64, 64])
        self.ys = O("ys", [TS, D])
        self.yp = O("yp", [2 * TP, D])
        self.str_o = O("str_o", [2, 128, 512])
        self.stl_o = O("stl_o", [128, 16])
        self.xT_scr = self.dram("xT_scr", [128, 8, NTOK], F32)
        self.xb_scr = self.dram("xb_scr", [128, 4, NTOK], F32)
        self.gate_scr = self.dram("gate_scr", [128, 4, NTOK], BF16)
        self.g_scr = self.dram("g_scr", [128, 4, NTOK], BF16)
        self.bon_scr = self.dram("bon_scr", [128, 4, NTOK], BF16)
        self.y_scr = self.dram("y_scr", [128, 8, NTOK], BF16)
        self.ytok_scr = self.dram("ytok_scr", [2, NTOK, 512], F32)
        for n in ("art", "rrt", "ttt", "akt", "mrbt", "mrkt", "bh", "kh"):
            setattr(self, n + "_scr", self.dram(n + "_scr", [NCH, 128, 512], BF16))
        self.vt_scr = self.dram("vt_scr", [NCH, 64, 512], BF16)
        self.pend_scr = self.dram("pend_scr", [NCH, 128, 512], F32)
        self.w1_scr = self.dram("w1_scr", [8, 128, 8, 512], BF16)
        self.w2_scr = self.dram("w2_scr", [8, 128, 32, 128], BF16)
        if self.debug:
            self.dbg = {}

        with self.es as st0:
            self.psum = [TL(st0.enter_context(nc.psum_tensor("ps%d" % i, [128, 512], F32)), "ps%d" % i)
                         for i in range(8)]
            for p_ in self.psum:
                p_.b.excl = True
            self.prm_t = self.sb(st0, "prm_t", [128, NPRM], F32)
            self.cst_t = self.sb(st0, "cst_t", [128, NCST], F32)
            self.modT = self.sb(st0, "modT", [128, 48, 2], F32)
            self.gs1 = self.sb(st0, "gs1", [128, 8, 2], F32)
            self.gs2 = self.sb(st0, "gs2", [128, 8, 2], F32)
            self.gg1 = self.sb(st0, "gg1", [128, 8, 2], F32)
            self.gg2 = self.sb(st0, "gg2", [128, 8, 2], F32)
            self.ident = self.sb(st0, "ident", [128, 128], F32)
            self.ones_bf = self.sb(st0, "ones_bf", [128, 128], BF16)
            self.oblk_bf = self.sb(st0, "oblk_bf", [128, 128], BF16)
            self.epsT = self.sb(st0, "epsT", [128, 2], F32)
            self.misc = self.sb(st0, "misc", [128, 32], F32)
            self.stl_t = self.sb(st0, "stl_t", [128, 16], F32)
            stop = False
            with contextlib.ExitStack() as stA:
                self.win = self.sb(stA, "win", [128, 8, DIN], BF16)
                self.win.b.multi = True
                self.wup_t = self.sb(stA, "wup_t", [128, 512], BF16)
                self.aup_t = self.sb(stA, "aup_t", [128, 512], BF16)
                self.gup_t = self.sb(stA, "gup_t", [128, 512], BF16)
                for nm, fn in (("p0", self.phase0), ("pA", self.phaseA)):
                    fn()
                    self.S.barrier()
                    if self.stop_after == nm:
                        stop = True
                        break
            if not stop:
                for nm, fn in (("pB", self.phaseB), ("pC", self.phaseC)):
                    fn()
                    self.S.barrier()
                    if self.stop_after == nm:
                        break
            self.S.emit()
        return nc

    def dump(self, name, src_ap, shape, dt, R):
        o = self.dram("dbg_" + name, shape, dt, "ExternalOutput")
        self.dma(o[:], src_ap, R, [o])

    def phase0(self):
        nc = self.nc
        prm, cst = self.prm_t, self.cst_t
        self.dma(prm[:], self.prm[:], [], [prm])
        self.dma(cst[:], self.cst[:], [], [cst])
        self.cp("dve", self.ident[:], cst[:, C_ID:C_ID + 128], [cst], [self.ident])
        self.cp("dve", self.oblk_bf[:], cst[:, C_OB:C_OB + 128], [cst], [self.oblk_bf])
        self.memset("dve", self.ones_bf[:], 1.0, [self.ones_bf])
        self.memset("dve", self.epsT[:, 0:1], EPS, [self.epsT])
        self.memset("dve", self.epsT[:, 1:2], LNX_EPS, [self.epsT])
        self.tsc("dve", self.misc[:, 0:4], prm[:, P_KA:P_KA + 4], -1.0, ALU.mult, [prm], [self.misc], 1.0, ALU.add)
        with contextlib.ExitStack() as st:
            scT = self.sb(st, "scT", [128, 16], F32)
            cT = self.sb(st, "cT_t", [128, 16], F32)
            wm = [self.sb(st, "wm%d" % i, [128, 8, 512], F32) for i in range(2)]
            tmp = self.sb(st, "lam_tmp", [128, 8], F32)
            self.dma(cT[:], self.cT[:], [], [cT])
            self.act(scT[:], cT[:], AF.Silu, [cT], [scT])
            self.act(tmp[:], prm[:, P_LAM:P_LAM + 8], AF.Exp, [prm], [tmp], scale=-1.0)
            self.act(tmp[:], tmp[:], AF.Ln, [tmp], [tmp], bias=1.0)
            self.tsc("dve", self.misc[:, 4:12], tmp[:], -8.0, ALU.mult, [tmp], [self.misc])
            self.tsc("dve", self.misc[:, 12:20], tmp[:], -16.0, ALU.mult, [tmp], [self.misc])
            wsrc = self.w_mod[:].rearrange("(kc p) n -> p kc n", p=128)
            scb = self.sb(st, "scb", [128, 16], BF16)
            self.cp("dve", scb[:], scT[:], [scT], [scb])
            sc3 = scb[:].rearrange("p (k c) -> p k c", c=2)
            wmb = [self.sb(st, "wmb%d" % i, [128, 8, 512], BF16) for i in range(2)]
            stgA = [self.sb(st, "stgA%d" % i, [128, 8, 256], F32) for i in range(2)]
            wisrc = self.w_in[:].rearrange("(kc p) n -> p kc n", p=128)
            wi_blocks = [(c0, min(256, DIN - c0)) for c0 in range(0, DIN, 256)]
            sm_list = [(self.wup, self.wup_t), (self.aup, self.aup_t), (self.gup, self.gup_t)]
            nb = [0]

            def win_step():
                if wi_blocks:
                    c0, cw = wi_blocks.pop(0)
                    s_ = stgA[nb[0] % 2]
                    self.dma(s_[:, :, 0:cw], wisrc[:, :, c0:c0 + cw], [], [s_], eng="act")
                    self.cp("act" if nb[0] % 2 == 0 else "pool", self.win[:, :, c0:c0 + cw], s_[:, :, 0:cw], [s_], [self.win])
                    nb[0] += 1
                elif sm_list:
                    src, dstt = sm_list.pop(0)
                    s_ = stgA[nb[0] % 2]
                    nb[0] += 1
                    s2 = s_[:].rearrange("p a b -> p (a b)")[:, 0:512]
                    self.dma(s2, src[:], [], [s_], eng="act")
                    self.cp("pool", dstt[:], s2, [s_], [dstt])

            for blk in range(12):
                w = wm[blk % 2]
                wb_ = wmb[blk % 2]
                self.dma(w[:], wsrc[:, :, blk * 512:(blk + 1) * 512], [], [w], eng="sp")
                self.cp("dve", wb_[:], w[:], [w], [wb_])
                win_step()
                p = self.ps()
                for m in range(4):
                    for kc in range(8):
                        self.mm(p[:, 2 * m:2 * m + 2], wb_[:, kc, m * 128:(m + 1) * 128], sc3[:, kc, :],
                                kc == 0, kc == 7, [wb_, scb], [p])
                for m in range(4):
                    mi = blk * 4 + m
                    self.tsc("dve", self.modT[:, mi, :], p[:, 2 * m:2 * m + 2], prm[:, P_BMOD + mi:P_BMOD + mi + 1],
                             ALU.add, [p, prm], [self.modT])
            while wi_blocks or sm_list:
                win_step()
            m3 = self.modT
            for (dst, sc_off, g_off, one) in ((self.gs1, 8, P_GPRE, 1.0), (self.gs2, 32, P_GPRE2, 1.0),
                                              (self.gg1, 16, P_GPOST, 0.0), (self.gg2, 40, P_GPOST2, 0.0)):
                for c in range(2):
                    self.tsc("dve", dst[:, :, c], m3[:, sc_off:sc_off + 8, c], one, ALU.add, [m3], [dst])
                    self.tt("dve", dst[:, :, c], dst[:, :, c], prm[:, g_off:g_off + 8], ALU.mult, [dst, prm], [dst])
            if self.debug:
                self.dump("modT", self.modT[:], [128, 48, 2], F32, [self.modT])
                self.dump("gs1", self.gs1[:], [128, 8, 2], F32, [self.gs1])

    def load_cast(self, st, dst_ap, dst_tl, src_ap, shape, tag):
        key = "stg_" + tag
        if not hasattr(self, key):
            setattr(self, key, [self.sb(st, "%s%d" % (key, i), shape, F32) for i in range(2)])
        ring = getattr(self, key)
        s = ring[self.rr.get(key, 0) % 2]
        self.rr[key] = self.rr.get(key, 0) + 1
        self.dma(s[:], src_ap, [], [s], eng="sp")
        eng = self.pick("castE", ["act", "pool"])
        self.cp(eng, dst_ap, s[:], [s], [dst_tl])

    def phaseA(self):
        nc = self.nc
        prm, cst = self.prm_t, self.cst_t
        with contextlib.ExitStack() as st:
            win, wup, aup, gup = self.win, self.wup_t, self.aup_t, self.gup_t
            mSI = cst[:, C_MSI:C_MSI + 128].rearrange("p (q t) -> p q t", q=2)
            mL = cst[:, C_ML:C_ML + 64]
            idS = cst[:, C_IDS:C_IDS + 64]

            xin = self.sb(st, "xin", [128, 2, D], F32)
            xT = self.sb(st, "xT", [128, 8, TT], F32)
            sq = self.sb(st, "sq", [128, 8, TT], BF16)
            hT = self.sb(st, "hT", [128, 8, TT], BF16)
            rstd = self.sb(st, "rstd", [128, TT], F32)
            tmpA = [self.sb(st, "tmpA%d" % i, [128, TT], F32) for i in range(2)]
            rT = self.sb(st, "rT", [128, 4, TT], F32)
            kT = self.sb(st, "kT", [128, 4, TT], F32)
            vT = self.sb(st, "vT", [128, 4, TT], F32)
            xw = self.sb(st, "xw", [128, TT], BF16)
            xa = self.sb(st, "xa", [128, TT], BF16)
            xg = self.sb(st, "xg", [128, TT], BF16)
            xbT = self.sb(st, "xbT", [128, 4, TT], F32)
            gtmp = [self.sb(st, "gtmp%d" % i, [128, TT], F32) for i in range(3)]
            gate = self.sb(st, "gate", [128, 4, TT], BF16)
            gT = self.sb(st, "gT", [128, 4, TT], BF16)
            kkn = self.sb(st, "kkn", [128, 4, TT], F32)
            ksum = self.sb(st, "ksum", [128, 4, TT], F32)
            bon = self.sb(st, "bon", [128, 4, TT], BF16)
            sg = self.sb(st, "sg", [128, 4, TT], F32)
            cs = self.sb(st, "cs", [128, 4, TT], F32)
            E1 = self.sb(st, "E1", [128, 4, TT], F32)
            ad = self.sb(st, "ad", [128, 4, TT], F32)
            wk1 = self.sb(st, "wk1", [128, 4, TT], F32)
            wk2 = self.sb(st, "wk2", [128, 4, TT], F32)
            NC4 = TT // 64
            AR = [self.sb(st, "AR%d" % d, [128, 4, NC4, 2, 64], BF16) for d in range(2)]
            BK = [self.sb(st, "BK%d" % d, [128, 4, NC4, 2, 64], BF16) for d in range(2)]
            PEb1 = self.sb(st, "PEb", [128, 4, NC4, 64], F32)
            PEb = [PEb1, PEb1]
            tokB1 = self.sb(st, "tokB", [128, 2, 8, 64], BF16)
            tokK1 = self.sb(st, "tokK", [128, 2, 8, 64], BF16)
            tokB, tokK = [tokB1, tokB1], [tokK1, tokK1]
            tokV = self.sb(st, "tokV", [128, 2, 8, 64], BF16)
            NSET = 2
            MRBs = [self.sb(st, "MRBs%d" % i, [64, 8, 64], BF16) for i in range(NSET)]
            Lt0s = [self.sb(st, "Lt0_%d" % i, [64, 8, 64], F32R) for i in range(NSET)]
            Tfins = [self.sb(st, "Tfin%d" % i, [64, 8, 64], BF16) for i in range(NSET)]
            SCk = self.sb(st, "SCk", [128, 2, 8, 64], BF16)
            Lms = [[self.sb(st, "Lm%d_%d" % (i, k), [64, 8, 64], F32R) for i in range(2)] for k in range(NSET)]
            Ltms = [[self.sb(st, "Ltm%d_%d" % (i, k), [64, 8, 64], F32R) for i in range(2)] for k in range(NSET)]
            ILms = [self.sb(st, "ILm_%d" % k, [64, 8, 64], F32R) for k in range(NSET)]
            Ttms = [[self.sb(st, "Ttm%d_%d" % (i, k), [64, 8, 64], F32R) for i in range(2)] for k in range(NSET)]

            ones_bf, oblk, ident = self.ones_bf, self.oblk_bf, self.ident
            tiles = [(0, t0, True, 0) for t0 in range(0, TS, TT)] + [(1, TS, False, 1), (2, TS + TP, False, 1)]
            mS1 = cst[0:64, C_MSI1:C_MSI1 + 128].rearrange("p (q t) -> p q t", q=2)
            mL1 = cst[0:64, C_ML1:C_ML1 + 64]
            id64 = idS[0:64]
            loaded = set()

            def load_x(g0, is_s):
                if g0 in loaded:
                    return
                loaded.add(g0)
                if is_s:
                    src = self.xs[g0:g0 + TT, :].rearrange("(s p) f -> p s f", p=128)
                else:
                    l0 = g0 - TS
                    src = self.xp[l0:l0 + TT, :].rearrange("(s p) f -> p s f", p=128)
                self.dma(xin[:], src, [], [xin])
                if is_s:
                    petv = xT[:].rearrange("p a b -> p (a b)").rearrange("p (s f) -> p s f", s=2)
                    self.dma(petv, self.pe[g0:g0 + TT, :].rearrange("(s p) f -> p s f", p=128), [], [xT], eng="act")

            def front(seq, g0, is_s, mc, nxt=None):
                load_x(g0, is_s)
                if is_s:
                    petv = xT[:].rearrange("p a b -> p (a b)").rearrange("p (s f) -> p s f", s=2)
                    self.tt("pool", xin[:], xin[:], petv, ALU.add, [xin, xT], [xin])
                for j in range(8):
                    p = self.ps()
                    for s in range(2):
                        self.tr(p[:, s * 128:(s + 1) * 128], xin[:, s, j * 128:(j + 1) * 128], ident[:], [xin, ident], [p])
                    self.cp("act", xT[:, j, :], p[:, 0:TT], [p], [xT])
                    self.tt("pool", sq[:, j, :], xT[:, j, :], xT[:, j, :], ALU.mult, [xT], [sq])
                    yield
                self.dma(self.xT_scr[:, :, g0:g0 + TT], xT[:], [xT], [self.xT_scr])
                p = self.ps()
                for j in range(8):
                    self.mm(p[:, 0:TT], ones_bf[:], sq[:, j, :], j == 0, j == 7, [ones_bf, sq], [p])
                self.act(rstd[:], p[:, 0:TT], AF.Sqrt, [p, self.epsT], [rstd], scale=1.0 / D, bias=self.epsT[:, 0:1])
                self.S.op("dve", lambda e: e.reciprocal(out=rstd[:], in_=rstd[:]), [rstd.b], [rstd.b])
                for j in range(8):
                    t = tmpA[j % 2]
                    self.tt("dve", t[:], xT[:, j, :], rstd[:], ALU.mult, [xT, rstd], [t])
                    self.act(hT[:, j, :], t[:], AF.Identity, [t, self.gs1, self.modT], [hT],
                             scale=self.gs1[:, j, mc:mc + 1], bias=self.modT[:, j, mc:mc + 1])
                    yield
                for m in range(23):
                    if m >= self.cutm:
                        break
                    p = self.ps()
                    for kc in range(8):
                        self.mm(p[:, 0:TT], win[:, kc, m * 128:(m + 1) * 128], hT[:, kc, :], kc == 0, kc == 7, [win, hT], [p])
                    pz = p[:, 0:TT]
                    if m < 4:
                        self.cp("act", rT[:, m, :], pz, [p], [rT])
                    elif m < 8:
                        self.cp("act", kT[:, m - 4, :], pz, [p], [kT])
                    elif m < 12:
                        self.cp("act", vT[:, m - 8, :], pz, [p], [vT])
                    elif m == 12:
                        self.act(xw[:], pz, AF.Tanh, [p], [xw])
                    elif m == 13:
                        self.cp("act", xa[:], pz, [p], [xa])
                    elif m == 14:
                        self.act(xg[:], pz, AF.Sigmoid, [p], [xg])
                    elif m < 19:
                        self.cp("act", xbT[:, m - 15, :], pz, [p], [xbT])
                    else:
                        j = m - 19
                        g0_, g1_, g2_ = gtmp
                        self.cp("act", g0_[:], pz, [p], [g0_])
                        self.tt("pool", g1_[:], g0_[:], g0_[:], ALU.mult, [g0_], [g1_])
                        self.tsc("dve", g1_[:], g1_[:], 0.044715, ALU.mult, [g1_], [g1_], 1.0, ALU.add)
                        self.tt("dve", g1_[:], g1_[:], g0_[:], ALU.mult, [g1_, g0_], [g1_])
                        self.act(g2_[:], g1_[:], AF.Sigmoid, [g1_], [g2_], scale=GELU_C)
                        self.tt("pool", gate[:, j, :], g0_[:], g2_[:], ALU.mult, [g0_, g2_], [gate])
                    yield
                self.dma(self.xb_scr[:, :, g0:g0 + TT], xbT[:], [xbT], [self.xb_scr])
                if self.debug and g0 == 0:
                    self.dump("hT", hT[:], [128, 8, TT], BF16, [hT])
                    self.dump("rT", rT[:], [128, 4, TT], F32, [rT])
                    self.dump("vT", vT[:], [128, 4, TT], F32, [vT])
                    self.dump("xbT", xbT[:], [128, 4, TT], F32, [xbT])
                    self.dump("gate", gate[:], [128, 4, TT], BF16, [gate])
                self.dma(self.gate_scr[:, :, g0:g0 + TT], gate[:], [gate], [self.gate_scr])
                for j in range(4):
                    p = self.ps()
                    self.mm(p[:, 0:TT], gup[:, j * 128:(j + 1) * 128], xg[:], True, True, [gup, xg], [p])
                    self.cp("act", gT[:, j, :], p[:, 0:TT], [p], [gT])
                    yield
                self.dma(self.g_scr[:, :, g0:g0 + TT], gT[:], [gT], [self.g_scr])
                for j in range(4):
                    self.tsc("dve", kkn[:, j, :], kT[:, j, :], prm[:, P_KK + j:P_KK + j + 1], ALU.mult, [kT, prm], [kkn])
                    self.tt("pool", sq[:, j, :], kkn[:, j, :], kkn[:, j, :], ALU.mult, [kkn], [sq])
                for j in range(4):
                    p = self.ps()
                    self.mm(p[:, 0:TT], oblk[:], sq[:, j, :], True, True, [oblk, sq], [p])
                    t = tmpA[j % 2]
                    self.act(t[:], p[:, 0:TT], AF.Sqrt, [p], [t])
                    self.tsc("dve", t[:], t[:], 1e-12, ALU.max, [t], [t])
                    self.S.op("dve", lambda e, t=t: e.reciprocal(out=t[:], in_=t[:]), [t.b], [t.b])
                    self.tt("dve", kkn[:, j, :], kkn[:, j, :], t[:], ALU.mult, [kkn, t], [kkn])
                    yield
                for s in range(2):
                    p = self.ps()
                    for j in range(4):
                        self.tr(p[:, j * 128:(j + 1) * 128], vT[:, j, s * 128:(s + 1) * 128], ident[:], [vT, ident], [p])
                    self.cp("act", tokV[:, s, :, :].rearrange("p h k -> p (h k)"), p[:], [p], [tokV])
                c0 = g0 // 64
                for s in range(2):
                    dst = self.vt_scr[c0 + 2 * s:c0 + 2 * s + 2, :, :].rearrange("c s f -> (c s) f")
                    self.dma(dst, tokV[:, s, :, :].rearrange("p h k -> p (h k)"), [tokV], [self.vt_scr])
                if nxt is not None:
                    load_x(nxt[1], nxt[2])
                yield
            def prep(g0, d):
                c0 = g0 // 64
                for j in range(4):
                    p = self.ps()
                    self.mm(p[:, 0:TT], wup[d * 64:(d + 1) * 64, j * 128:(j + 1) * 128], xw[d * 64:(d + 1) * 64, :],
                            True, True, [wup, xw], [p])
                    self.act(sg[:, j, :], p[:, 0:TT], AF.Sigmoid, [p, prm], [sg],
                             bias=prm[:, P_W0 + 4 * d + j:P_W0 + 4 * d + j + 1])
                    if d == 0:
                        self.scan(cs[:, j, :], cst[:, C_RMF:C_RMF + TT], sg[:, j, :], 0.0, [cst, sg], [cs])
                    else:
                        self.scan(cs[:, j, ::-1], cst[:, C_RMB:C_RMB + TT][:, ::-1], sg[:, j, ::-1], 0.0, [cst, sg], [cs])
                    yield
                for j in range(4):
                    p = self.ps()
                    self.mm(p[:, 0:TT], aup[d * 64:(d + 1) * 64, j * 128:(j + 1) * 128], xa[d * 64:(d + 1) * 64, :],
                            True, True, [aup, xa], [p])
                    self.act(ad[:, j, :], p[:, 0:TT], AF.Sigmoid, [p, prm], [ad],
                             bias=prm[:, P_A0 + 4 * d + j:P_A0 + 4 * d + j + 1])
                    yield
                self.tt("dve", sg[:], cs[:], sg[:], ALU.subtract, [cs, sg], [sg])
                self.act(E1[:], cs[:], AF.Exp, [cs], [E1], scale=-LAM)
                self.act(cs[:], cs[:], AF.Exp, [cs], [cs], scale=LAM)
                self.act(sg[:], sg[:], AF.Exp, [sg], [sg], scale=-LAM)
                E2, E3 = cs, sg
                ar5 = AR[d]
                bk5 = BK[d]
                v4 = lambda tl: tl[:].rearrange("p j (c t) -> p j c t", t=64)
                self.stt(ar5[:, :, :, 0, :], v4(kkn), -1.0, v4(E3), ALU.mult, ALU.mult, [kkn, E3], [ar5])
                self.tt("pool", ar5[:, :, :, 1, :], v4(rT), v4(E1), ALU.mult, [rT, E1], [ar5])
                yield
                self.tt("dve", wk1[:], kkn[:], ad[:], ALU.mult, [kkn, ad], [wk1])
                self.tt("dve", wk1[:], wk1[:], E2[:], ALU.mult, [wk1, E2], [wk1])
                self.cp("act", bk5[:, :, :, 0, :], v4(wk1), [wk1], [bk5])
                yield
                for j in range(4):
                    self.tsc("dve", wk2[:, j, :], ad[:, j, :], prm[:, P_KA + j:P_KA + j + 1], ALU.mult, [ad, prm, self.misc], [wk2],
                             self.misc[:, j:j + 1], ALU.add)
                self.tt("dve", wk2[:], wk2[:], kT[:], ALU.mult, [wk2, kT], [wk2])
                if d == 0:
                    self.cp("pool", ksum[:], wk2[:], [wk2], [ksum])
                else:
                    self.tt("pool", ksum[:], ksum[:], wk2[:], ALU.add, [ksum, wk2], [ksum])
                self.tt("dve", wk2[:], wk2[:], E2[:], ALU.mult, [wk2, E2], [wk2])
                self.cp("act", bk5[:, :, :, 1, :], v4(wk2), [wk2], [bk5])
                yield
                te = 63 if d == 0 else 0
                pend_b = v4(E1)[:, :, :, te:te + 1].to_broadcast([128, 4, NC4, 64])
                self.cp("act", PEb[d][:], pend_b, [E1], [PEb[d]])
                self.tt("dve", v4(wk1), v4(wk1), PEb[d][:], ALU.mult, [wk1, PEb[d]], [wk1])
                self.tt("dve", v4(wk2), v4(wk2), PEb[d][:], ALU.mult, [wk2, PEb[d]], [wk2])
                yield
                for (srcw, tokX, scr) in ((wk1, tokB[d], self.bh_scr), (wk2, tokK[d], self.kh_scr)):
                    for s in range(2):
                        p = self.ps()
                        for j in range(4):
                            self.tr(p[:, j * 128:(j + 1) * 128], srcw[:, j, s * 128:(s + 1) * 128], ident[:], [srcw, ident], [p])
                        self.cp("act", tokX[:, s, :, :].rearrange("p h k -> p (h k)"), p[:], [p], [tokX])
                        for cc in range(2):
                            self.dma(scr[c0 + 2 * s + cc, d * 64:(d + 1) * 64, :],
                                     tokX[cc * 64:(cc + 1) * 64, s, :, :].rearrange("p h k -> p (h k)"),
                                     [tokX], [scr])
                        yield
                for cl in range(NC4):
                    c = c0 + cl
                    for hp in range(2):
                        for (q, scr) in ((0, self.art_scr), (1, self.rrt_scr)):
                            dst = scr[c, d * 64:(d + 1) * 64, :].rearrange("k (j hp t) -> k j hp t", hp=2, t=64)[:, :, hp, :]
                            self.dma(dst, ar5[hp * 64:(hp + 1) * 64, :, cl, q, :], [ar5], [scr], eng="sp")
                        dst = self.pend_scr[c, d * 64:(d + 1) * 64, :].rearrange("k (j hp t) -> k j hp t", hp=2, t=64)[:, :, hp, :]
                        self.dma(dst, PEb[d][hp * 64:(hp + 1) * 64, :, cl, :], [PEb[d]], [self.pend_scr], eng="sp")
                yield
            def bonus(g0):
                for j in range(4):
                    self.stt(sq[:, j, :], rT[:, j, :], prm[:, P_RK + j:P_RK + j + 1], ksum[:, j, :], ALU.mult, ALU.mult,
                             [rT, prm, ksum], [sq])
                    p = self.ps()
                    self.mm(p[:, 0:TT], oblk[:], sq[:, j, :], True, True, [oblk, sq], [p])
                    self.tt("dve", bon[:, j, :], p[:, 0:TT], vT[:, j, :], ALU.mult, [p, vT], [bon])
                self.dma(self.bon_scr[:, :, g0:g0 + TT], bon[:], [bon], [self.bon_scr])
                yield
            def chunk_sck(g0):
                c0 = g0 // 64
                for cl in range(NC4):
                    c = c0 + cl
                    for hp in range(2):
                        p = self.ps()
                        for j in range(4):
                            for d in range(2):
                                self.mm(p[d * 64:(d + 1) * 64, j * 128:(j + 1) * 128],
                                        BK[d][hp * 64:(hp + 1) * 64, j, cl, 1, :],
                                        AR[d][hp * 64:(hp + 1) * 64, j, cl, :, :].rearrange("p q t -> p (q t)"),
                                        True, True, [BK[d], AR[d]], [p])
                        self.tt("dve", SCk[:, :, hp::2, :].rearrange("p q h t -> p h q t"),
                                p[:].rearrange("p (h q t) -> p h q t", q=2, t=64),
                                mSI.unsqueeze(1).to_broadcast([128, 4, 2, 64]), ALU.mult, [p, cst], [SCk])
                    self.dma(self.akt_scr[c], SCk[:, 0, :, :].rearrange("p h s -> p (h s)"), [SCk], [self.akt_scr], eng="sp")
                    self.dma(self.mrkt_scr[c], SCk[:, 1, :, :].rearrange("p h s -> p (h s)"), [SCk], [self.mrkt_scr], eng="sp")
                    yield
            def chunk_d(g0, d, cls, k):
                c0 = g0 // 64
                Lm, Ltm, ILm, Ttm, Lt0, Tfin = Lms[k], Ltms[k], ILms[k], Ttms[k], Lt0s[k], Tfins[k]
                MRB = MRBs[k]
                for cl in cls:
                    c = c0 + cl
                    msk = mSI[0:64] if d == 0 else mS1
                    mskL = mL[0:64] if d == 0 else mL1
                    L0 = Lm[0]
                    for hp in range(2):
                        p = self.ps()
                        for j in range(4):
                            self.mm(p[0:64, j * 128:(j + 1) * 128],
                                    BK[d][hp * 64:(hp + 1) * 64, j, cl, 0, :],
                                    AR[d][hp * 64:(hp + 1) * 64, j, cl, :, :].rearrange("p q t -> p (q t)"),
                                    True, True, [BK[d], AR[d]], [p])
                        p4 = p[0:64, :].rearrange("p (h q t) -> p h q t", q=2, t=64)
                        self.tt("dve", Lt0[:, hp::2, :], p4[:, :, 0, :], msk[:, 0, :].unsqueeze(1).to_broadcast([64, 4, 64]),
                                ALU.mult, [p, cst], [Lt0])
                        self.tt("dve", MRB[:, hp::2, :], p4[:, :, 1, :], msk[:, 1, :].unsqueeze(1).to_broadcast([64, 4, 64]),
                                ALU.mult, [p, cst], [MRB])
                        p2 = self.ps()
                        for j in range(4):
                            self.mm(p2[0:64, j * 64:(j + 1) * 64],
                                    AR[d][hp * 64:(hp + 1) * 64, j, cl, 0, :], BK[d][hp * 64:(hp + 1) * 64, j, cl, 0, :],
                                    True, True, [AR[d], BK[d]], [p2])
                        self.tt("dve", L0[:, hp::2, :], p2[0:64, 0:256].rearrange("p (h s) -> p h s", s=64),
                                mskL.unsqueeze(1).to_broadcast([64, 4, 64]), ALU.mult, [p2, cst], [L0])
                    self.dma(self.mrbt_scr[c, d * 64:(d + 1) * 64, :], MRB[:].rearrange("p h s -> p (h s)"),
                             [MRB], [self.mrbt_scr], eng="sp")
                    yield
                    T0 = Ttm[0]
                    self.tt("pool", T0[:], Lt0[:].bitcast(F32), id64.unsqueeze(1).to_broadcast([64, 8, 64]), ALU.add,
                            [Lt0, cst], [T0])
                    L_prev, Tt_prev, Lt_prev = L0, T0, Lt0
                    for lev in range(1, 6):
                        L_new, Lt_new, Tt_new = Lm[lev % 2], Ltm[lev % 2], Ttm[lev % 2]
                        pA = self.ps()
                        for h in range(8):
                            self.mm(pA[0:64, h * 64:(h + 1) * 64], Lt_prev[:, h, :], L_prev[:, h, :], True, True,
                                    [Lt_prev, L_prev], [pA])
                        if lev < 5:
                            pB = self.ps()
                            for h in range(8):
                                self.mm(pB[0:64, h * 64:(h + 1) * 64], L_prev[:, h, :], Lt_prev[:, h, :], True, True,
                                        [Lt_prev, L_prev], [pB])
                        self.tt("dve", ILm[:], pA[0:64, :].rearrange("p (h s) -> p h s", s=64),
                                id64.unsqueeze(1).to_broadcast([64, 8, 64]), ALU.add, [pA, cst], [ILm])
                        if lev < 5:
                            self.cp("dve", L_new[:].rearrange("p h s -> p (h s)"), pA[0:64, :], [pA], [L_new])
                            self.cp("act", Lt_new[:].rearrange("p h s -> p (h s)"), pB[0:64, :], [pB], [Lt_new])
                        yield
                        pC = self.ps()
                        for h in range(8):
                            self.mm(pC[0:64, h * 64:(h + 1) * 64], ILm[:, h, :], Tt_prev[:, h, :], True, True,
                                    [ILm, Tt_prev], [pC])
                        if lev < 5:
                            self.cp("act", Tt_new[:].rearrange("p h s -> p (h s)"), pC[0:64, :], [pC], [Tt_new])
                        else:
                            self.cp("act", Tfin[:].rearrange("p h s -> p (h s)"), pC[0:64, :], [pC], [Tfin])
                        L_prev, Tt_prev, Lt_prev = L_new, Tt_new, Lt_new
                        yield
                    self.dma(self.ttt_scr[c, d * 64:(d + 1) * 64, :], Tfin[:].rearrange("p h s -> p (h s)"),
                             [Tfin], [self.ttt_scr], eng="sp")


            def run_all(*gens, w=None):
                gens = list(gens)
                wts = {id(g): (w[i] if w else 1) for i, g in enumerate(gens)}
                while gens:
                    for g in list(gens):
                        for _ in range(wts[id(g)]):
                            try:
                                next(g)
                            except StopIteration:
                                gens.remove(g)
                                break

            def seq_(*gens):
                for g in gens:
                    yield from g

            prev = None
            tl_ = tiles[:self.ktiles]
            for ti_, (seq, g0, is_s, mc) in enumerate(tl_):
                nxt = tl_[ti_ + 1] if ti_ + 1 < len(tl_) else None
                if prev is None:
                    run_all(front(seq, g0, is_s, mc, nxt))
                else:
                    run_all(seq_(chunk_d(prev, 1, [0, 1], 0), chunk_sck(prev)), chunk_d(prev, 1, [2, 3], 1), front(seq, g0, is_s, mc, nxt), w=[1, 1, 2])
                run_all(prep(g0, 0))
                run_all(chunk_d(g0, 0, [0, 1], 0), chunk_d(g0, 0, [2, 3], 1), prep(g0, 1))
                run_all(bonus(g0))
                prev = g0
            run_all(seq_(chunk_d(prev, 1, [0, 1], 0), chunk_sck(prev)), chunk_d(prev, 1, [2, 3], 1))

    def phaseB(self):
        with contextlib.ExitStack() as st:
            side = [self.gen_c0(st), self.phaseB_lru(st)]
            post = self.phaseB_post(st)
            next(post)
            done = np.zeros((2, NCH), bool)
            posted = [False] * (NTOK // 128)
            si = 0
            for info in self.phaseB_chain(st):
                for (d, c) in info:
                    done[d, c] = True
                for _ in range(2):
                    if side:
                        g = side[si % len(side)]
                        si += 1
                        try:
                            next(g)
                        except StopIteration:
                            side.remove(g)
                for b in range(NTOK // 128):
                    if not posted[b] and done[:, 2 * b:2 * b + 2].all():
                        posted[b] = True
                        post.send(b)
            for g in side:
                for _ in g:
                    pass
            for b in range(NTOK // 128):
                if not posted[b]:
                    post.send(b)
            if self.debug:
                self.S.barrier()
                self.dump("yscr", self.y_scr[:], [128, 8, NTOK], BF16, [self.y_scr])

    def gen_c0(self, st):
        stg = [self.sb(st, "stgC%d" % i, [128, 8, 512], F32) for i in range(2)]
        wb = [self.sb(st, "wbC%d" % i, [128, 8, 512], BF16) for i in range(2)]
        w1src = self.w1[:].rearrange("(kc p) n -> p kc n", p=128)
        for blk in range(8):
            s, o = stg[blk % 2], wb[blk % 2]
            self.dma(s[:], w1src[:, :, blk * 512:(blk + 1) * 512], [], [s])
            self.cp("pool", o[:], s[:], [s], [o])
            self.dma(self.w1_scr[blk], o[:], [o], [self.w1_scr], eng="act")
            yield
        w2src = self.w2[:].rearrange("(fc p) n -> p fc n", p=128)
        for m in range(8):
            s, o = stg[m % 2], wb[m % 2]
            s4 = s[:].rearrange("p k (a b) -> p (k a) b", b=128)
            o4 = o[:].rearrange("p k (a b) -> p (k a) b", b=128)
            self.dma(s4, w2src[:, :, m * 128:(m + 1) * 128], [], [s])
            self.cp("pool", o[:], s[:], [s], [o])
            self.dma(self.w2_scr[m], o4, [o], [self.w2_scr], eng="act")
            yield

    def phaseB_lru(self, st):
        prm, cst, misc = self.prm_t, self.cst_t, self.misc
        if True:
            wbd32 = self.sb(st, "wbd32", [128, 16, 128], F32)
            wbd = self.sb(st, "wbd", [128, 16, 128], BF16)
            self.memset("pool", wbd32[:], 0.0, [wbd32])
            self.S.barrier_on(wbd32)
            wbd32.b.multi = True
            for gi, src in enumerate((self.lwa, self.lwx)):
                for d in range(2):
                    for j in range(4):
                        for hb in range(2):
                            self.dma(wbd32[hb * 64:(hb + 1) * 64, (gi * 2 + d) * 4 + j, hb * 64:(hb + 1) * 64],
                                     src[d, 2 * j + hb], [], [wbd32])
            self.cp("dve", wbd[:], wbd32[:], [wbd32], [wbd])
            TM = TS
            xbp = self.sb(st, "xbp", [128, TM + 4], F32)
            xc = self.sb(st, "xc", [128, TM], F32)
            xcb = self.sb(st, "xcb", [128, TM], BF16)
            gt = self.sb(st, "gt_l", [128, TM], BF16)
            a_t = self.sb(st, "a_t", [128, TM], F32)
            bx_t = self.sb(st, "bx_t", [128, TM], F32)
            s_t = self.sb(st, "s_t", [128, TM], F32)
            hs = [self.sb(st, "hs%d" % d, [128, TM], F32) for d in range(2)]
            yb = self.sb(st, "yb", [128, TM], BF16)
            for (seq, g0, T) in ((0, 0, TS), (1, TS, TP), (2, TS + TP, TP)):
                for j in range(4):
                    self.memset("pool", xbp[:, 0:2], 0.0, [xbp])
                    self.memset("pool", xbp[:, T + 2:T + 4], 0.0, [xbp])
                    self.dma(xbp[:, 2:T + 2], self.xb_scr[:, j, g0:g0 + T], [self.xb_scr], [xbp])
                    self.dma(gt[:, 0:T], self.gate_scr[:, j, g0:g0 + T], [self.gate_scr], [gt], eng="act")
                    cw = lambda i: prm[:, P_CW + 4 * i + j:P_CW + 4 * i + j + 1]
                    self.act(xc[:, 0:T], xbp[:, 0:T], AF.Identity, [xbp, prm], [xc], scale=cw(0), bias=prm[:, P_CB + j:P_CB + j + 1])
                    for i in range(1, 4):
                        self.stt(xc[:, 0:T], xbp[:, i:i + T], cw(i), xc[:, 0:T], ALU.mult, ALU.add, [xbp, prm, xc], [xc])
                    self.cp("pool", xcb[:, 0:T], xc[:, 0:T], [xc], [xcb])
                    yield
                    for d in range(2):
                        for t0 in range(0, T, 512):
                            tw = min(512, T - t0)
                            p = self.ps()
                            self.mm(p[:, 0:tw], wbd[:, (0 * 2 + d) * 4 + j, :], xcb[:, t0:t0 + tw], True, True, [wbd, xcb], [p])
                            self.act(s_t[:, t0:t0 + tw], p[:, 0:tw], AF.Sigmoid, [p, prm], [s_t],
                                     bias=prm[:, P_BA + 4 * d + j:P_BA + 4 * d + j + 1])
                            p2 = self.ps()
                            self.mm(p2[:, 0:tw], wbd[:, (1 * 2 + d) * 4 + j, :], xcb[:, t0:t0 + tw], True, True, [wbd, xcb], [p2])
                            self.act(bx_t[:, t0:t0 + tw], p2[:, 0:tw], AF.Sigmoid, [p2, prm], [bx_t],
                                     bias=prm[:, P_BX + 4 * d + j:P_BX + 4 * d + j + 1])
                        col = 4 + d * 4 + j
                        self.act(a_t[:, 0:T], s_t[:, 0:T], AF.Exp, [s_t, misc], [a_t], scale=misc[:, col:col + 1])
                        self.act(s_t[:, 0:T], s_t[:, 0:T], AF.Exp, [s_t, misc], [s_t], scale=misc[:, col + 8:col + 9])
                        self.act(s_t[:, 0:T], s_t[:, 0:T], AF.Sqrt, [s_t], [s_t], scale=-1.0, bias=1.0)
                        self.tt("pool", bx_t[:, 0:T], bx_t[:, 0:T], xc[:, 0:T], ALU.mult, [bx_t, xc], [bx_t])
                        self.tt("dve", bx_t[:, 0:T], bx_t[:, 0:T], s_t[:, 0:T], ALU.mult, [bx_t, s_t], [bx_t])
                        h = hs[d]
                        if seq == 0:
                            init = prm[:, P_H0 + 4 * d + j:P_H0 + 4 * d + j + 1]
                        else:
                            init = 0.0
                        if d == 0:
                            self.scan(h[:, 0:T], a_t[:, 0:T], bx_t[:, 0:T], init, [a_t, bx_t, prm], [h])
                        else:
                            self.scan(h[:, 0:T][:, ::-1], a_t[:, 0:T][:, ::-1], bx_t[:, 0:T][:, ::-1], init, [a_t, bx_t, prm], [h])
                        if seq > 0:
                            col_o = j * 4 + (seq - 1) * 2 + d
                            te = T - 1 if d == 0 else 0
                            self.cp("pool", self.stl_t[:, col_o:col_o + 1], h[:, te:te + 1], [h], [self.stl_t])
                        yield
                    self.tt("pool", hs[0][:, 0:T], hs[0][:, 0:T], hs[1][:, 0:T], ALU.add, [hs[0], hs[1]], [hs[0]])
                    self.tt("dve", yb[:, 0:T], hs[0][:, 0:T], gt[:, 0:T], ALU.mult, [hs[0], gt], [yb])
                    self.dma(self.y_scr[:, 4 + j, g0:g0 + T], yb[:, 0:T], [yb], [self.y_scr])
            self.dma(self.stl_o[:], self.stl_t[:], [self.stl_t], [self.stl_o])

    def phaseB_chain(self, st):
        if True:
            NB = 3
            def ring(name, dt=BF16):
                return [self.sb2(st, "%s%d" % (name, i), [128, 512], dt) for i in range(NB)]
            art, rrt, ttt, akt, mrbt, mrkt, bh, kh, vt = [ring(n) for n in
                                                          ("c_art", "c_rrt", "c_ttt", "c_akt", "c_mrbt", "c_mrkt", "c_bh", "c_kh", "c_vt")]
            pend = ring("c_pend", F32)
            Hf = self.sb2(st, "Hf", [128, 512], F32)
            Hb = self.sb2(st, "Hb", [128, 512], BF16)
            Zs = self.sb2(st, "Zs", [128, 512], BF16)
            Us = self.sb2(st, "Us", [128, 512], BF16)
            Yt = [self.sb2(st, "Yt%d" % i, [128, 512], F32) for i in range(2)]
            tmpH = self.sb2(st, "tmpH", [128, 512], F32)
            hs_ = lambda h: slice(h * 64, (h + 1) * 64)
            steps = []
            for (seq, cbase, n) in ((0, 0, 32), (1, 32, 4), (2, 36, 4)):
                for i in range(n):
                    steps.append((seq, cbase, n, i))

            def loads(k):
                seq, cbase, n, i = steps[k]
                r = k % NB
                for d in range(2):
                    sl = slice(d * 64, (d + 1) * 64)
                    c = cbase + i if d == 0 else cbase + n - 1 - i
                    for (tl, scr) in ((art, self.art_scr), (rrt, self.rrt_scr), (ttt, self.ttt_scr), (akt, self.akt_scr),
                                      (mrbt, self.mrbt_scr), (mrkt, self.mrkt_scr), (bh, self.bh_scr), (kh, self.kh_scr),
                                      (pend, self.pend_scr)):
                        self.dma(tl[r][d][sl, :], scr[c, sl, :], [scr], [tl[r][d]], eng="sp")
                    self.dma(vt[r][d][sl, :], self.vt_scr[c], [self.vt_scr], [vt[r][d]], eng="sp")

            loads(0)
            for k in range(len(steps)):
                seq, cbase, n, i = steps[k]
                step = k + 1
                r = k % NB
                if k + 1 < len(steps):
                    loads(k + 1)
                if i == 0:
                    for d in range(2):
                        sl = slice(d * 64, (d + 1) * 64)
                        if seq == 0:
                            self.dma(Hf[d][sl, :], self.h0r[sl, :], [], [Hf[d]], eng="act")
                        else:
                            self.memset("dve", Hf[d][sl, :], 0.0, [Hf[d]])
                        self.cp("dve", Hb[d][sl, :], Hf[d][sl, :], [Hf[d]], [Hb[d]])
                if True:
                    ctx = []
                    for d in range(2):
                        sl = slice(d * 64, (d + 1) * 64)
                        c = cbase + i if d == 0 else cbase + n - 1 - i
                        ops = tuple(x[r][d] for x in (art, rrt, ttt, akt, mrbt, mrkt, bh, kh, vt, pend))
                        ctx.append((d, sl, c, ops))
                    pZs = {}
                    for (d, sl, c, (A_, R_, T_, AK_, MRB_, MRK_, B_, K_, V_, PE_)) in ctx:
                        H_, Z_ = Hb[d], Zs[d]
                        pZ = self.ps()
                        for h in range(8):
                            self.mm(pZ[sl, hs_(h)], A_[sl, hs_(h)], H_[sl, hs_(h)], True, False, [A_, H_], [pZ])
                            self.mm(pZ[sl, hs_(h)], AK_[sl, hs_(h)], V_[sl, hs_(h)], False, True, [AK_, V_], [pZ])
                        pZs[d] = pZ
                    pYs = {}
                    for (d, sl, c, (A_, R_, T_, AK_, MRB_, MRK_, B_, K_, V_, PE_)) in ctx:
                        H_ = Hb[d]
                        pY = self.ps()
                        pYs[d] = pY
                    for (d, sl, c, ops) in ctx:
                        self.cp("act", Zs[d][sl, :], pZs[d][sl, :], [pZs[d]], [Zs[d]])
                    pUs = {}
                    for (d, sl, c, (A_, R_, T_, AK_, MRB_, MRK_, B_, K_, V_, PE_)) in ctx:
                        Z_ = Zs[d]
                        pU = self.ps()
                        for h in range(8):
                            self.mm(pU[sl, hs_(h)], T_[sl, hs_(h)], Z_[sl, hs_(h)], True, True, [T_, Z_], [pU])
                        pUs[d] = pU
                    for (d, sl, c, ops) in ctx:
                        self.cp("act", Us[d][sl, :], pUs[d][sl, :], [pUs[d]], [Us[d]])
                    pHs = {}
                    for (d, sl, c, (A_, R_, T_, AK_, MRB_, MRK_, B_, K_, V_, PE_)) in ctx:
                        U_ = Us[d]
                        self.tt("dve", tmpH[d][sl, :], Hf[d][sl, :], PE_[sl, :], ALU.mult, [Hf[d], PE_], [tmpH[d]])
                        pH = self.ps()
                        for h in range(8):
                            self.mm(pH[sl, hs_(h)], B_[sl, hs_(h)], U_[sl, hs_(h)], True, False, [B_, U_], [pH])
                            self.mm(pH[sl, hs_(h)], K_[sl, hs_(h)], V_[sl, hs_(h)], False, True, [K_, V_], [pH])
                        pHs[d] = pH
                    for (d, sl, c, (A_, R_, T_, AK_, MRB_, MRK_, B_, K_, V_, PE_)) in ctx:
                        H_, U_ = Hb[d], Us[d]
                        pY = pYs[d]
                        for h in range(8):
                            self.mm(pY[sl, hs_(h)], R_[sl, hs_(h)], H_[sl, hs_(h)], True, False, [R_, H_], [pY])
                            self.mm(pY[sl, hs_(h)], MRB_[sl, hs_(h)], U_[sl, hs_(h)], False, False, [MRB_, U_], [pY])
                            self.mm(pY[sl, hs_(h)], MRK_[sl, hs_(h)], V_[sl, hs_(h)], False, True, [MRK_, V_], [pY])
                    for (d, sl, c, ops) in ctx:
                        self.tt("dve", Hf[d][sl, :], tmpH[d][sl, :], pHs[d][sl, :], ALU.add, [tmpH[d], pHs[d]], [Hf[d]])
                        self.cp("dve", Hb[d][sl, :], Hf[d][sl, :], [Hf[d]], [Hb[d]])
                    for (d, sl, c, ops) in ctx:
                        y = Yt[step % 2][d]
                        self.cp("act", y[sl, :], pYs[d][sl, :], [pYs[d]], [y])
                        self.dma(self.ytok_scr[d, c * 64:(c + 1) * 64, :], y[sl, :], [y], [self.ytok_scr], eng="act")
                if seq > 0 and i == n - 1:
                    self.dma(self.str_o[seq - 1], Hf[0][:], [Hf[0], Hf[1]], [self.str_o], eng="act")
                yield [(0, cbase + i), (1, cbase + n - 1 - i)]

    def phaseB_post(self, st):
        prm, cst = self.prm_t, self.cst_t
        ident = self.ident
        if True:
            yf = [self.sb(st, "yf%d" % i, [128, 512], F32) for i in range(2)]
            yb2 = [self.sb(st, "yb2%d" % i, [128, 512], F32) for i in range(2)]
            cen = self.sb(st, "cen", [128, 8, 64], F32)
            sqv = self.sb(st, "sqv", [128, 8, 64], F32)
            mean = self.sb(st, "mean", [128, 8], F32)
            var = self.sb(st, "var", [128, 8], F32)
            gl = [self.sb(st, "gl%d" % i, [128, 4, 128], BF16) for i in range(2)]
            bl = [self.sb(st, "bl%d" % i, [128, 4, 128], BF16) for i in range(2)]
            ynT = self.sb(st, "ynT", [128, 4, 128], F32)
            yo = [self.sb(st, "yo%d" % i, [128, 4, 128], BF16) for i in range(2)]
            it = -1
            blk = yield
            while True:
                it += 1
                g0 = blk * 128
                a, b = yf[it % 2], yb2[it % 2]
                g_, b_ = gl[it % 2], bl[it % 2]
                o = yo[it % 2]
                self.dma(a[:], self.ytok_scr[0, g0:g0 + 128, :], [self.ytok_scr], [a])
                self.dma(b[:], self.ytok_scr[1, g0:g0 + 128, :], [self.ytok_scr], [b], eng="act")
                self.dma(g_[:], self.g_scr[:, :, g0:g0 + 128], [self.g_scr], [g_])
                self.dma(b_[:], self.bon_scr[:, :, g0:g0 + 128], [self.bon_scr], [b_], eng="act")
                a3 = a[:].rearrange("p (h v) -> p h v", v=64)
                self.tt("pool", a[:], a[:], b[:], ALU.add, [a, b], [a])
                self.S.op("dve", lambda e, a3=a3: e.tensor_reduce(out=mean[:], in_=a3, op=ALU.add, axis=mybir.AxisListType.X),
                          [a.b], [mean.b])
                self.tsc("dve", mean[:], mean[:], 1.0 / 64, ALU.mult, [mean], [mean])
                self.tt("dve", cen[:], a3, mean[:].unsqueeze(2).to_broadcast([128, 8, 64]), ALU.subtract, [a, mean], [cen])
                self.tt("pool", sqv[:], cen[:], cen[:], ALU.mult, [cen], [sqv])
                self.S.op("dve", lambda e: e.tensor_reduce(out=var[:], in_=sqv[:], op=ALU.add, axis=mybir.AxisListType.X),
                          [sqv.b], [var.b])
                self.act(var[:], var[:], AF.Sqrt, [var, self.epsT], [var], scale=1.0 / 64, bias=self.epsT[:, 1:2])
                self.S.op("dve", lambda e: e.reciprocal(out=var[:], in_=var[:]), [var.b], [var.b])
                self.tt("dve", cen[:], cen[:], var[:].unsqueeze(2).to_broadcast([128, 8, 64]), ALU.mult, [cen, var], [cen])
                p = self.ps()
                cen2 = cen[:].rearrange("p h v -> p (h v)")
                for j in range(4):
                    self.tr(p[:, j * 128:(j + 1) * 128], cen2[:, j * 128:(j + 1) * 128], ident[:], [cen, ident], [p])
                for j in range(4):
                    self.act(ynT[:, j, :], p[:, j * 128:(j + 1) * 128], AF.Identity, [p, prm], [ynT],
                             scale=prm[:, P_LNG + j:P_LNG + j + 1], bias=prm[:, P_LNB + j:P_LNB + j + 1])
                self.tt("pool", ynT[:], ynT[:], b_[:], ALU.add, [ynT, b_], [ynT])
                self.tt("dve", o[:], ynT[:], g_[:], ALU.mult, [ynT, g_], [o])
                self.dma(self.y_scr[:, 0:4, g0:g0 + 128], o[:], [o], [self.y_scr])
                blk = yield

    def phaseC(self):
        prm, cst = self.prm_t, self.cst_t
        ident, ones_bf = self.ident, self.ones_bf
        with contextlib.ExitStack() as st:
            pass
        import os
        kcc = int(os.environ.get("KCC", "99"))
        if kcc == 0:
            return
        with contextlib.ExitStack() as st:
            wout = self.sb(st, "wout", [128, 8, D], BF16)
            wsrc = self.w_out[:].rearrange("(kc p) n -> p kc n", p=128)
            with contextlib.ExitStack() as st2:
                stg = [self.sb(st2, "stgD%d" % i, [128, 8, 256], F32) for i in range(2)]
                for cb in range(4):
                    s = stg[cb % 2]
                    self.dma(s[:], wsrc[:, :, cb * 256:(cb + 1) * 256], [], [s])
                    self.cp("act", wout[:, :, cb * 256:(cb + 1) * 256], s[:], [s], [wout])
            self.S.barrier()
            yT = self.sb(st, "yT_c", [128, 8, TC], BF16)
            o1T = self.sb(st, "o1T", [128, 8, TC], F32)
            sq1 = self.sb(st, "sq1_c", [128, 8, TC], BF16)
            oT = self.sb(st, "oT", [128, 8, TC], F32)
            sq = self.sb(st, "sq_c", [128, 8, TC], BF16)
            xTs = [self.sb(st, "xT_c%d" % i, [128, 8, TC], F32) for i in range(2)]
            h2 = self.sb(st, "h2", [128, 8, TC], BF16)
            f = self.sb(st, "f_c", [128, 32, TC], BF16)
            otoks = [self.sb(st, "otok%d" % i, [128, 2, D], F32) for i in range(1)]
            rstd = self.sb(st, "rstd_c", [128, TC], F32)
            tmp = [self.sb(st, "tmpC%d" % i, [128, TC], F32) for i in range(2)]
            NW = 3
            w1r = [self.sb(st, "w1r%d" % i, [128, 8, 512], BF16) for i in range(NW)]
            w2r = [self.sb(st, "w2r%d" % i, [128, 32, 128], BF16) for i in range(2)]
            ntile = min(NTOK // TC, kcc)
            wseq = []
            for ti_ in range(ntile):
                wseq += [("w1", b_) for b_ in range(8)] + [("w2", b_) for b_ in range(8)]
            wstate = {"issued": 0, "w1": 0, "w2": 0}
            wbuf = {}

            def issue_upto(n):
                while wstate["issued"] < min(n, len(wseq)):
                    k_ = wstate["issued"]
                    kind, b_ = wseq[k_]
                    ring = w1r if kind == "w1" else w2r
                    buf = ring[wstate[kind] % len(ring)]
                    wstate[kind] += 1
                    scr = self.w1_scr if kind == "w1" else self.w2_scr
                    self.dma(buf[:], scr[b_], [scr], [buf], eng="sp")
                    wbuf[k_] = buf
                    wstate["issued"] += 1

            def rms(sqt, R):
                p = self.ps()
                for j in range(8):
                    self.mm(p[:], ones_bf[:], sqt[:, j, :], j == 0, j == 7, [ones_bf, sqt], [p])
                self.act(rstd[:], p[:], AF.Sqrt, [p, self.epsT], [rstd], scale=1.0 / D, bias=self.epsT[:, 0:1])
                self.S.op("dve", lambda e: e.reciprocal(out=rstd[:], in_=rstd[:]), [rstd.b], [rstd.b])

            def resid(gg, mc, oT, xT):
                for j in range(8):
                    t = tmp[j % 2]
                    self.tt("dve", t[:], oT[:, j, :], rstd[:], ALU.mult, [oT, rstd], [t])
                    self.stt(xT[:, j, :], t[:], gg[:, j, mc:mc + 1], xT[:, j, :], ALU.mult, ALU.add, [t, gg, xT], [xT])

            def head(tj):
                gj = tj * TC
                mcj = 0 if gj < TS else 1
                xT = xTs[tj % 2]
                self.dma(xT[:], self.xT_scr[:, :, gj:gj + TC], [self.xT_scr], [xT], eng="sp")
                yield
                rms(sq1, None)
                yield
                for j in range(8):
                    t = tmp[j % 2]
                    self.tt("dve", t[:], o1T[:, j, :], rstd[:], ALU.mult, [o1T, rstd], [t])
                    self.stt(xT[:, j, :], t[:], self.gg1[:, j, mcj:mcj + 1], xT[:, j, :], ALU.mult, ALU.add, [t, self.gg1, xT], [xT])
                    self.act(sq1[:, j, :], xT[:, j, :], AF.Square, [xT], [sq1])
                    if j % 2 == 1:
                        yield
                rms(sq1, None)
                yield
                for j in range(8):
                    t = tmp[j % 2]
                    self.tt("dve", t[:], xT[:, j, :], rstd[:], ALU.mult, [xT, rstd], [t])
                    self.act(h2[:, j, :], t[:], AF.Identity, [t, self.gs2, self.modT], [h2],
                             scale=self.gs2[:, j, mcj:mcj + 1], bias=self.modT[:, 24 + j, mcj:mcj + 1])
                    if j % 2 == 1:
                        yield

            def wout_stage(tj):
                gj = tj * TC
                self.dma(yT[:], self.y_scr[:, :, gj:gj + TC], [self.y_scr], [yT])
                for m in range(8):
                    p = self.ps()
                    for kc in range(8):
                        self.mm(p[:], wout[:, kc, m * 128:(m + 1) * 128], yT[:, kc, :], kc == 0, kc == 7, [wout, yT], [p])
                    self.cp("act", o1T[:, m, :], p[:], [p], [o1T])
                    self.act(sq1[:, m, :], p[:], AF.Square, [p], [sq1])

            def tail(g0, mc, xT):
                rms(sq, None)
                yield
                resid(self.gg2, mc, oT, xT)
                yield
                for hh in range(2):
                    otok = otoks[0]
                    for s2 in range(2):
                        s_ = hh * 2 + s2
                        for half in range(2):
                            p = self.ps()
                            for jj in range(4):
                                j = half * 4 + jj
                                self.tr(p[:, jj * 128:(jj + 1) * 128], xT[:, j, s_ * 128:(s_ + 1) * 128], ident[:], [xT, ident], [p])
                            self.cp("act" if half == 0 else "dve", otok[:, s2, half * 512:(half + 1) * 512], p[:], [p], [otok])
                            yield
                    if g0 < TS:
                        dst = self.ys[g0 + hh * 256:g0 + (hh + 1) * 256, :].rearrange("(s p) f -> p s f", p=128)
                        self.dma(dst, otok[:], [otok], [self.ys])
                    else:
                        dst = self.yp[hh * 256:(hh + 1) * 256, :].rearrange("(s p) f -> p s f", p=128)
                        self.dma(dst, otok[:], [otok], [self.yp])

            tg = None
            wi = 0
            for ti in range(NTOK // TC):
                if ti >= kcc:
                    break
                g0 = ti * TC
                mc = 0 if g0 < TS else 1
                xT = xTs[ti % 2]
                if ti == 0:
                    wout_stage(0)
                    for _ in head(0):
                        pass
                issue_upto(ti * 16 + 3)
                for blk in range(8):
                    if tg is not None:
                        for _ in range(2):
                            try:
                                next(tg)
                            except StopIteration:
                                tg = None
                                break
                    issue_upto(ti * 16 + blk + 3)
                    w = wbuf[ti * 16 + blk]
                    for c4 in range(4):
                        fc = blk * 4 + c4
                        p = self.ps()
                        for kc in range(8):
                            self.mm(p[:], w[:, kc, c4 * 128:(c4 + 1) * 128], h2[:, kc, :], kc == 0, kc == 7, [w, h2], [p])
                        t = tmp[fc % 2]
                        self.act(t[:], p[:], AF.Relu, [p], [t])
                        self.tt("pool" if fc % 2 == 0 else "dve", f[:, fc, :], t[:], t[:], ALU.mult, [t], [f])
                if tg is not None:
                    for _ in tg:
                        pass
                    tg = None
                hg = None
                if ti + 1 < ntile:
                    wout_stage(ti + 1)
                    hg = head(ti + 1)
                for m in range(8):
                    if hg is not None:
                        for _ in range(2):
                            try:
                                next(hg)
                            except StopIteration:
                                hg = None
                                break
                    issue_upto(ti * 16 + 8 + m + 2)
                    w = wbuf[ti * 16 + 8 + m]
                    p = self.ps()
                    for fc in range(32):
                        self.mm(p[:], w[:, fc, :], f[:, fc, :], fc == 0, fc == 31, [w, f], [p])
                    self.cp("act", oT[:, m, :], p[:], [p], [oT])
                    self.act(sq[:, m, :], p[:], AF.Square, [p], [sq])
                if hg is not None:
                    for _ in hg:
                        pass
                tg = tail(g0, mc, xT)
                if ti + 1 >= ntile:
                    for _ in tg:
                        pass
                    tg = None


def _fm(v):
    v = np.asarray(v, np.float32).reshape(-1, 128)
    return np.ascontiguousarray(v.T)


def _pos_embed():
    def sincos(pos, dim):
        omega = (1.0 / (10000.0 ** (np.arange(dim // 2, dtype=np.float32) / np.float32(dim // 2)))).astype(np.float32)
        ang = pos.astype(np.float32)[:, None] * omega[None, :]
        return np.concatenate([np.sin(ang), np.cos(ang)], axis=-1).astype(np.float32)
    rows = TS // 64
    half = D // 2
    e_row = sincos(np.arange(rows), half)
    e_col = sincos(np.arange(64), half)
    emb = np.concatenate([np.broadcast_to(e_row[:, None, :], (rows, 64, half)),
                          np.broadcast_to(e_col[None, :, :], (rows, 64, half))], axis=-1)
    return np.ascontiguousarray(emb.reshape(rows * 64, D).astype(np.float32))


def _consts():
    c = np.zeros((128, NCST), np.float32)
    c[:, C_ID:C_ID + 128] = np.eye(128, dtype=np.float32)
    ob = np.zeros((128, 128), np.float32)
    ob[:64, :64] = 1.0
    ob[64:, 64:] = 1.0
    c[:, C_OB:C_OB + 128] = ob
    s = np.arange(64)[:, None]
    t = np.arange(64)[None, :]
    msi = np.zeros((128, 2, 64), np.float32)
    msi[:64, 0] = (s < t)
    msi[:64, 1] = (s <= t)
    msi[64:, 0] = (s > t)
    msi[64:, 1] = (s >= t)
    c[:, C_MSI:C_MSI + 128] = msi.reshape(128, 128)
    ml = np.zeros((128, 64), np.float32)
    ml[:64] = (t < s)
    ml[64:] = (t > s)
    c[:, C_ML:C_ML + 64] = ml
    ids = np.zeros((128, 64), np.float32)
    ids[:64] = np.eye(64)
    ids[64:] = np.eye(64)
    c[:, C_IDS:C_IDS + 64] = ids
    c[:64, C_MSI1:C_MSI1 + 128] = msi[64:].reshape(64, 128)
    c[:64, C_ML1:C_ML1 + 64] = ml[64:]
    tt_ = np.arange(TT)
    c[:, C_RMF:C_RMF + TT] = (tt_ % 64 != 0).astype(np.float32)[None, :]
    c[:, C_RMB:C_RMB + TT] = (tt_ % 64 != 63).astype(np.float32)[None, :]
    return c


_NC_CACHE = {}


def kernel(x_prompt, x_sample, c, state_rwkv, state_lru, c_ctx, w_mod, b_mod,
           g_pre_mix, g_post_mix, g_pre_mlp, g_post_mlp, w_in,
           rwkv_w0, rwkv_w_up, rwkv_a0, rwkv_a_up, rwkv_g_up, rwkv_k_k, rwkv_k_a, rwkv_r_k,
           rwkv_lnx_g, rwkv_lnx_b, lru_conv_w, lru_conv_b, lru_wa, lru_ba, lru_wx, lru_bx,
           lru_lambda, w_out, w_mlp1, w_mlp2, _debug=False):
    f = lambda a: np.ascontiguousarray(np.asarray(a, np.float32))
    x_prompt, x_sample, c, state_rwkv, state_lru, c_ctx = map(f, (x_prompt, x_sample, c, state_rwkv, state_lru, c_ctx))
    nc = K(debug=_debug).build()
    pe = _pos_embed()
    cst = _consts()
    shared = {
        "pe": pe, "cst": cst,
        "w_mod": f(w_mod[0]), "w_in": f(w_in[0]), "w_out": f(w_out[0]), "w1": f(w_mlp1[0]), "w2": f(w_mlp2[0]),
        "wup": f(rwkv_w_up[0]).reshape(128, 512), "aup": f(rwkv_a_up[0]).reshape(128, 512), "gup": f(rwkv_g_up[0]),
        "lwa": f(lru_wa[0]), "lwx": f(lru_wx[0]),
    }
    prm0 = np.zeros((128, NPRM), np.float32)
    prm0[:, P_GPRE:P_GPRE + 8] = _fm(g_pre_mix[0])
    prm0[:, P_GPOST:P_GPOST + 8] = _fm(g_post_mix[0])
    prm0[:, P_GPRE2:P_GPRE2 + 8] = _fm(g_pre_mlp[0])
    prm0[:, P_GPOST2:P_GPOST2 + 8] = _fm(g_post_mlp[0])
    prm0[:, P_BMOD:P_BMOD + 48] = _fm(b_mod[0])
    for d in range(2):
        prm0[:, P_W0 + 4 * d:P_W0 + 4 * d + 4] = _fm(rwkv_w0[0, d])
        prm0[:, P_A0 + 4 * d:P_A0 + 4 * d + 4] = _fm(rwkv_a0[0, d])
        prm0[:, P_BA + 4 * d:P_BA + 4 * d + 4] = _fm(lru_ba[0, d])
        prm0[:, P_BX + 4 * d:P_BX + 4 * d + 4] = _fm(lru_bx[0, d])
        prm0[:, P_LAM + 4 * d:P_LAM + 4 * d + 4] = _fm(lru_lambda[0, d])
    prm0[:, P_KK:P_KK + 4] = _fm(rwkv_k_k[0])
    prm0[:, P_KA:P_KA + 4] = _fm(rwkv_k_a[0])
    prm0[:, P_RK:P_RK + 4] = _fm(np.asarray(rwkv_r_k[0]).reshape(-1))
    prm0[:, P_LNG:P_LNG + 4] = _fm(rwkv_lnx_g[0])
    prm0[:, P_LNB:P_LNB + 4] = _fm(rwkv_lnx_b[0])
    for i in range(4):
        prm0[:, P_CW + 4 * i:P_CW + 4 * i + 4] = _fm(lru_conv_w[0, i])
    prm0[:, P_CB:P_CB + 4] = _fm(lru_conv_b[0])
    in_maps = []
    for i in range(8):
        prm = prm0.copy()
        for d in range(2):
            prm[:, P_H0 + 4 * d:P_H0 + 4 * d + 4] = _fm(state_lru[i, 0, d])
        cT = np.zeros((128, 8, 2), np.float32)
        cT[:, :, 0] = _fm(c[i])
        cT[:, :, 1] = _fm(c_ctx)
        h0 = np.ascontiguousarray(state_rwkv[i, 0].transpose(0, 3, 1, 2)).reshape(128, 512)
        m = dict(shared)
        m.update({"xs": x_sample[i], "xp": np.ascontiguousarray(x_prompt[2 * i:2 * i + 2].reshape(2 * TP, D)),
                  "cT": cT.reshape(128, 16), "h0r": h0, "prm": prm})
        in_maps.append(m)
    res = run_bass_kernel_spmd(nc, in_maps, core_ids=list(range(8)))
    R = res.results
    y_prompt = np.zeros((16, TP, D), np.float32)
    y_sample = np.zeros((8, TS, D), np.float32)
    st_r = np.zeros((16, 1, 2, 8, 64, 64), np.float32)
    st_l = np.zeros((16, 1, 2, 512), np.float32)
    for i in range(8):
        r = R[i]
        y_sample[i] = r["ys"]
        y_prompt[2 * i:2 * i + 2] = r["yp"].reshape(2, TP, D)
        so = r["str_o"].reshape(2, 2, 64, 8, 64)
        st_r[2 * i:2 * i + 2, 0] = so.transpose(0, 1, 3, 4, 2)
        sl = r["stl_o"].reshape(128, 4, 2, 2)
        st_l[2 * i:2 * i + 2, 0] = sl.transpose(2, 3, 1, 0).reshape(2, 2, 512)
    if _debug:
        return (y_prompt, y_sample, st_r, st_l), R
    return (y_prompt, y_sample, st_r, st_l)
```
